# Optimizing a Trainium2 kernel written in Bass

```python
import math
import jax
import jax.numpy as jnp
from jax import lax
import numpy as np

D_MODEL = 2048
BATCH = 2
SEQ = 4096
DEPTH = 2
DEC_BATCH = 8
DEC_SEQ = 8
PAST_LEN = 16384
PAGE_SIZE = 128

N_ATTN_LAYERS = (DEPTH + 1) // 2
N_SSM_LAYERS = DEPTH // 2
ATTN_WIDTH = D_MODEL // 2
CONV_WIDTH = D_MODEL // 2
HEAD_DIM = 128
N_HEADS = ATTN_WIDTH // HEAD_DIM
DILATED_GROUPS = ((128, 1), (512, 4), (2048, 16))
MAX_WINDOW = 2048
BLK = 128
CONV_K = 3
SSM_WIDTH = D_MODEL
SSM_GROUP = 16
SSM_GROUPS = SSM_WIDTH // SSM_GROUP
SSM_STATE = 64
SSM_CHUNK = 128
ROPE_THETA = 10000.0
RMS_EPS = 1e-6

kernel_name = "dilated_attn_shortconv_s5_hybrid_step"


def rms_norm(x, g):
    xf = x.astype(jnp.float32)
    y = xf * lax.rsqrt(jnp.mean(xf * xf, axis=-1, keepdims=True) + RMS_EPS)
    return (y * g.astype(jnp.float32)).astype(x.dtype)


def rotary(x, pos):
    half = HEAD_DIM // 2
    inv = ROPE_THETA ** (-jnp.arange(half, dtype=jnp.float32) / half)
    ang = pos.astype(jnp.float32)[:, None] * inv[None, :]
    cos = jnp.cos(ang)[None, :, None, :]
    sin = jnp.sin(ang)[None, :, None, :]
    xf = x.astype(jnp.float32)
    x1, x2 = xf[..., :half], xf[..., half:]
    return jnp.concatenate([x1 * cos - x2 * sin, x2 * cos + x1 * sin], axis=-1).astype(x.dtype)


def banded_window_attn(q, k, v, n_back):
    n, L, h, dh = q.shape
    nb = -(-L // BLK)
    pad_end = nb * BLK - L
    qb = jnp.pad(q, ((0, 0), (0, pad_end), (0, 0), (0, 0))).reshape(n, nb, BLK, h, dh)

    def kv_blocks(t):
        tp = jnp.pad(t, ((0, 0), (BLK, pad_end), (0, 0), (0, 0))).reshape(n, nb + 1, BLK, h, dh)
        return jnp.concatenate([tp[:, :-1], tp[:, 1:]], axis=2)

    kb, vb = kv_blocks(k), kv_blocks(v)
    s = jnp.einsum('nbqhd,nbkhd->nbhqk', qb, kb, preferred_element_type=jnp.float32) * (dh ** -0.5)
    qi = jnp.arange(BLK)[:, None]
    kj = jnp.arange(2 * BLK)[None, :]
    dist = BLK + qi - kj
    key_pos = jnp.arange(nb)[:, None, None] * BLK + kj[None] - BLK
    valid = ((dist >= 0) & (dist <= n_back))[None] & (key_pos >= 0)
    s = jnp.where(valid[None, :, None], s, -jnp.inf)
    m = jnp.max(s, axis=-1, keepdims=True)
    p = jnp.exp(s - m)
    l = jnp.sum(p, axis=-1, keepdims=True)
    o = jnp.einsum('nbhqk,nbkhd->nbqhd', p, vb.astype(jnp.float32))
    o = o / jnp.swapaxes(l, 2, 3)
    lse = jnp.swapaxes((m + jnp.log(l))[..., 0], 2, 3)
    o = o.reshape(n, nb * BLK, h, dh)[:, :L]
    lse = lse.reshape(n, nb * BLK, h)[:, :L]
    return o, lse


def merge_dilations(outs, lses):
    w = jax.nn.softmax(jnp.stack(lses, axis=0), axis=0)
    return sum(w[g][..., None] * outs[g] for g in range(len(outs)))


def dilated_attn_prompt(q, k, v):
    b, t, h, dh = q.shape
    outs, lses = [], []
    for window, d in DILATED_GROUPS:
        L = t // d

        def split(x):
            return x.reshape(b, L, d, h, dh).transpose(0, 2, 1, 3, 4).reshape(b * d, L, h, dh)

        o, lse = banded_window_attn(split(q), split(k), split(v), window // d)
        outs.append(o.reshape(b, d, L, h, dh).transpose(0, 2, 1, 3, 4).reshape(b, t, h, dh))
        lses.append(lse.reshape(b, d, L, h).transpose(0, 2, 1, 3).reshape(b, t, h))
    return merge_dilations(outs, lses)


def dilated_attn_sample(q, k_all, v_all, n_buf):
    s_len = q.shape[1]
    outs, lses = [], []
    for window, d in DILATED_GROUPS:
        j = jnp.arange(window // d + 1)
        idx = n_buf + jnp.arange(s_len)[:, None] - d * j[None, :]
        valid = idx >= 0
        idxc = jnp.maximum(idx, 0)
        kg = k_all[:, idxc]
        vg = v_all[:, idxc]
        sc = jnp.einsum('bshd,bskhd->bshk', q, kg, preferred_element_type=jnp.float32) * (HEAD_DIM ** -0.5)
        sc = jnp.where(valid[None, :, None, :], sc, -jnp.inf)
        m = jnp.max(sc, axis=-1, keepdims=True)
        p = jnp.exp(sc - m)
        l = jnp.sum(p, axis=-1, keepdims=True)
        o = jnp.einsum('bshk,bskhd->bshd', p, vg.astype(jnp.float32)) / l
        outs.append(o)
        lses.append((m + jnp.log(l))[..., 0])
    return merge_dilations(outs, lses)


def causal_short_conv(u, buf, w):
    up = jnp.concatenate([buf, u], axis=1)
    t = u.shape[1]
    y = sum(up[:, i:i + t] * w[i] for i in range(CONV_K))
    return y, up[:, -(CONV_K - 1):]


def mixer_ab_layer(x, pos, kv_prev, conv_prev, g, w_in, conv_w, w_out):
    b, t, _ = x.shape
    hn = rms_norm(x, g)
    proj = hn @ w_in
    q, k, v, z_a, gate_b, gate_c, h_in, z_b = jnp.split(proj, 8, axis=-1)
    q = rotary(q.reshape(b, t, N_HEADS, HEAD_DIM), pos)
    k = rotary(k.reshape(b, t, N_HEADS, HEAD_DIM), pos)
    v = v.reshape(b, t, N_HEADS, HEAD_DIM)
    if kv_prev is None:
        o_a = dilated_attn_prompt(q, k, v)
        n_keep = min(MAX_WINDOW, t)
        k_state, v_state = k[:, t - n_keep:], v[:, t - n_keep:]
    else:
        k_buf, v_buf = kv_prev
        o_a = dilated_attn_sample(q, jnp.concatenate([k_buf, k], axis=1),
                                  jnp.concatenate([v_buf, v], axis=1), k_buf.shape[1])
        k_state, v_state = k, v
    o_a = o_a.reshape(b, t, ATTN_WIDTH).astype(x.dtype) * jax.nn.silu(z_a)
    conv_out, conv_state = causal_short_conv(gate_c * h_in, conv_prev, conv_w)
    o_b = gate_b * conv_out * jax.nn.silu(z_b)
    y = jnp.concatenate([o_a, o_b], axis=-1) @ w_out
    return x + y, k_state, v_state, conv_state


def s5_discretise(lam_re, lam_im, log_step, b_re, b_im):
    f32 = jnp.float32
    lr, li = lam_re.astype(f32), lam_im.astype(f32)
    step = jnp.exp(log_step.astype(f32))[:, None]
    mag = jnp.exp(lr * step)
    abar_re, abar_im = mag * jnp.cos(li * step), mag * jnp.sin(li * step)
    nr, ni = abar_re - 1.0, abar_im
    den = lr * lr + li * li
    cr = (nr * lr + ni * li) / den
    ci = (ni * lr - nr * li) / den
    br, bi = b_re.astype(f32), b_im.astype(f32)
    bbar_re = cr[..., None] * br - ci[..., None] * bi
    bbar_im = cr[..., None] * bi + ci[..., None] * br
    return abar_re, abar_im, bbar_re, bbar_im


def _ssm_combine(e1, e2):
    a1r, a1i, b1r, b1i = e1
    a2r, a2i, b2r, b2i = e2
    return (a2r * a1r - a2i * a1i, a2r * a1i + a2i * a1r,
            a2r * b1r - a2i * b1i + b2r, a2r * b1i + a2i * b1r + b2i)


def s5_scan(u, h0_re, h0_im, abar_re, abar_im, bbar_re, bbar_im, c_re, c_im, d_skip):
    b, t, _ = u.shape
    f32 = jnp.float32
    chunk = math.gcd(t, SSM_CHUNK)
    n_chunks = t // chunk
    ug = u.astype(f32).reshape(b, n_chunks, chunk, SSM_GROUPS, SSM_GROUP).transpose(1, 0, 2, 3, 4)
    cr, ci = c_re.astype(f32), c_im.astype(f32)

    def step(carry, u_c):
        hr, hi = carry
        bur = jnp.einsum('bcgk,gpk->bcgp', u_c, bbar_re)
        bui = jnp.einsum('bcgk,gpk->bcgp', u_c, bbar_im)
        bur = bur.at[:, 0].add(abar_re * hr - abar_im * hi)
        bui = bui.at[:, 0].add(abar_re * hi + abar_im * hr)
        ar = jnp.broadcast_to(abar_re, bur.shape)
        ai = jnp.broadcast_to(abar_im, bur.shape)
        _, _, sr, si = lax.associative_scan(_ssm_combine, (ar, ai, bur, bui), axis=1)
        y = jnp.einsum('bcgp,gkp->bcgk', sr, cr) - jnp.einsum('bcgp,gkp->bcgk', si, ci)
        return (sr[:, -1], si[:, -1]), y

    (hr, hi), y = lax.scan(step, (h0_re.astype(f32), h0_im.astype(f32)), ug)
    y = y.transpose(1, 0, 2, 3, 4).reshape(b, t, SSM_WIDTH) + d_skip.astype(f32) * u.astype(f32)
    return y, hr, hi


def mixer_c_layer(x, h0_re, h0_im, g, w_in, lam_re, lam_im, log_step, b_re, b_im,
                  c_re, c_im, d_skip, w_glu, b_glu, w_out):
    hn = rms_norm(x, g)
    u, z = jnp.split(hn @ w_in, 2, axis=-1)
    abar_re, abar_im, bbar_re, bbar_im = s5_discretise(lam_re, lam_im, log_step, b_re, b_im)
    y, hr, hi = s5_scan(u, h0_re, h0_im, abar_re, abar_im, bbar_re, bbar_im, c_re, c_im, d_skip)
    y = jax.nn.gelu(y)
    y = y * jax.nn.sigmoid(y @ w_glu.astype(jnp.float32) + b_glu.astype(jnp.float32))
    y = y.astype(x.dtype) * jax.nn.silu(z)
    return x + y @ w_out, hr, hi


def setup_inputs(seed: int = 0) -> dict:
    key = jax.random.key(seed)
    ks = jax.random.split(key, 25)
    f32 = jnp.float32
    n_buf = min(MAX_WINDOW, PAST_LEN)
    nrm = lambda k, shape, s: jax.random.normal(k, shape, f32) * s
    lam_im_init = jnp.pi * jnp.arange(SSM_STATE, dtype=f32)
    return {
        'x_prompt': nrm(ks[0], (BATCH, SEQ, D_MODEL), 1.0),
        'x_sample': nrm(ks[1], (DEC_BATCH, DEC_SEQ, D_MODEL), 1.0),
        'cache_win_k': nrm(ks[2], (N_ATTN_LAYERS, DEC_BATCH, n_buf, N_HEADS, HEAD_DIM), 1.0),
        'cache_win_v': nrm(ks[3], (N_ATTN_LAYERS, DEC_BATCH, n_buf, N_HEADS, HEAD_DIM), 1.0),
        'state_conv': nrm(ks[4], (N_ATTN_LAYERS, DEC_BATCH, CONV_K - 1, CONV_WIDTH), 1.0),
        'state_ssm_re': nrm(ks[5], (N_SSM_LAYERS, DEC_BATCH, SSM_GROUPS, SSM_STATE), 0.1),
        'state_ssm_im': nrm(ks[6], (N_SSM_LAYERS, DEC_BATCH, SSM_GROUPS, SSM_STATE), 0.1),
        'attn_norm': 1.0 + nrm(ks[7], (N_ATTN_LAYERS, D_MODEL), 0.02),
        'w_in_ab': nrm(ks[8], (N_ATTN_LAYERS, D_MODEL, 4 * ATTN_WIDTH + 4 * CONV_WIDTH), D_MODEL ** -0.5),
        'conv_w': nrm(ks[9], (N_ATTN_LAYERS, CONV_K, CONV_WIDTH), CONV_K ** -0.5),
        'w_out_ab': nrm(ks[10], (N_ATTN_LAYERS, ATTN_WIDTH + CONV_WIDTH, D_MODEL), (ATTN_WIDTH + CONV_WIDTH) ** -0.5),
        'ssm_norm': 1.0 + nrm(ks[11], (N_SSM_LAYERS, D_MODEL), 0.02),
        'w_in_c': nrm(ks[12], (N_SSM_LAYERS, D_MODEL, 2 * SSM_WIDTH), D_MODEL ** -0.5),
        'lam_re': -0.5 + nrm(ks[13], (N_SSM_LAYERS, SSM_GROUPS, SSM_STATE), 0.01),
        'lam_im': lam_im_init + nrm(ks[14], (N_SSM_LAYERS, SSM_GROUPS, SSM_STATE), 0.01),
        'log_step': jax.random.uniform(ks[15], (N_SSM_LAYERS, SSM_GROUPS), f32,
                                       minval=math.log(1e-3), maxval=math.log(1e-1)),
        'b_re': nrm(ks[16], (N_SSM_LAYERS, SSM_GROUPS, SSM_STATE, SSM_GROUP), (2 * SSM_GROUP) ** -0.5),
        'b_im': nrm(ks[17], (N_SSM_LAYERS, SSM_GROUPS, SSM_STATE, SSM_GROUP), (2 * SSM_GROUP) ** -0.5),
        'c_re': nrm(ks[18], (N_SSM_LAYERS, SSM_GROUPS, SSM_GROUP, SSM_STATE), SSM_STATE ** -0.5),
        'c_im': nrm(ks[19], (N_SSM_LAYERS, SSM_GROUPS, SSM_GROUP, SSM_STATE), SSM_STATE ** -0.5),
        'd_skip': nrm(ks[20], (N_SSM_LAYERS, SSM_WIDTH), 1.0),
        'w_glu': nrm(ks[21], (N_SSM_LAYERS, SSM_WIDTH, SSM_WIDTH), SSM_WIDTH ** -0.5),
        'b_glu': nrm(ks[22], (N_SSM_LAYERS, SSM_WIDTH), 0.01),
        'w_out_c': nrm(ks[23], (N_SSM_LAYERS, SSM_WIDTH, D_MODEL), SSM_WIDTH ** -0.5),
        'final_norm': 1.0 + nrm(ks[24], (D_MODEL,), 0.02),
    }


def reference(x_prompt, x_sample, cache_win_k, cache_win_v, state_conv, state_ssm_re, state_ssm_im,
              attn_norm, w_in_ab, conv_w, w_out_ab, ssm_norm, w_in_c, lam_re, lam_im, log_step,
              b_re, b_im, c_re, c_im, d_skip, w_glu, b_glu, w_out_c, final_norm):
    bp, tp, _ = x_prompt.shape
    ts = x_sample.shape[1]
    pos_p = jnp.arange(tp)
    pos_s = PAST_LEN + jnp.arange(ts)
    hp, hs = x_prompt, x_sample
    kp_l, vp_l, cp_l, ks_l, vs_l, cs_l = [], [], [], [], [], []
    srp_l, sip_l, srs_l, sis_l = [], [], [], []
    for layer in range(DEPTH):
        i = layer // 2
        if layer % 2 == 0:
            conv0 = jnp.zeros((bp, CONV_K - 1, CONV_WIDTH), hp.dtype)
            hp, kp, vp, cp = mixer_ab_layer(hp, pos_p, None, conv0, attn_norm[i], w_in_ab[i],
                                            conv_w[i], w_out_ab[i])
            hs, ksn, vsn, csn = mixer_ab_layer(hs, pos_s, (cache_win_k[i], cache_win_v[i]), state_conv[i],
                                               attn_norm[i], w_in_ab[i], conv_w[i], w_out_ab[i])
            kp_l.append(kp); vp_l.append(vp); cp_l.append(cp)
            ks_l.append(ksn); vs_l.append(vsn); cs_l.append(csn)
        else:
            h0 = jnp.zeros((bp, SSM_GROUPS, SSM_STATE), jnp.float32)
            hp, rp, ip = mixer_c_layer(hp, h0, h0, ssm_norm[i], w_in_c[i], lam_re[i], lam_im[i], log_step[i],
                                       b_re[i], b_im[i], c_re[i], c_im[i], d_skip[i], w_glu[i], b_glu[i], w_out_c[i])
            hs, rs, is_ = mixer_c_layer(hs, state_ssm_re[i], state_ssm_im[i], ssm_norm[i], w_in_c[i], lam_re[i],
                                        lam_im[i], log_step[i], b_re[i], b_im[i], c_re[i], c_im[i], d_skip[i],
                                        w_glu[i], b_glu[i], w_out_c[i])
            srp_l.append(rp); sip_l.append(ip); srs_l.append(rs); sis_l.append(is_)
    y_prompt = rms_norm(hp, final_norm)
    y_sample = rms_norm(hs, final_norm)
    return (y_prompt, y_sample,
            jnp.stack(kp_l), jnp.stack(vp_l), jnp.stack(cp_l), jnp.stack(srp_l), jnp.stack(sip_l),
            jnp.stack(ks_l), jnp.stack(vs_l), jnp.stack(cs_l), jnp.stack(srs_l), jnp.stack(sis_l))
```

```python
import math
import os
STOP = os.environ.get('MK_STOP', '')
from contextlib import ExitStack

import numpy as np
import concourse.bass as bass
import concourse.mybir as mybir
from concourse.bass_utils import run_bass_kernel_spmd

F32 = mybir.dt.float32
BF16 = mybir.dt.bfloat16
ALU = mybir.AluOpType
AF = mybir.ActivationFunctionType
AX = mybir.AxisListType

ENGS = ["pe", "act", "dve", "pool", "sp"]
D = 2048
KT = 16
NOWN = 1024
NHALO = 2048
NTP = NOWN + NHALO
NTILE_P = NTP // 128
NO = NOWN + 128
SEQ = 4096
PAST = 16384
NCH = 1032
TWO_PI = 2.0 * math.pi


class Buf:
    def __init__(self, t, name):
        self.t = t
        self.name = name
        self.w = {}
        self.r = {}
        self.dsem = None
        self.dcnt = 0


class Prog:
    def __init__(self, nc, stack):
        self.nc = nc
        self.stack = stack
        self.q = {e: [] for e in ENGS}
        self.cnt = {e: 0 for e in ENGS}
        self.seen = {e: {} for e in ENGS}
        self.sems = {}
        self.semval = {}
        for e in ["pe", "act", "dve", "pool"]:
            self.sems[e] = stack.enter_context(nc.semaphore("s_" + e))
        self.off = 16512
        self.free = []
        self.dval = {}
        self.phase_bufs = []

    def sb(self, name, shape, dt, at=None):
        nbytes = int(np.prod(shape[1:])) * (2 if dt == BF16 else 4)
        if at is None:
            at = self.off
            self.off = (at + nbytes + 63) // 64 * 64
        assert at + nbytes <= 229300, (name, at, nbytes)
        t = self.nc.alloc_sbuf_tensor_at(name, list(shape), dt, offset=at)
        b = Buf(t, name)
        b.at = at
        b.nbytes = nbytes
        return b

    def ps(self, name, shape, dt=F32):
        t = self.stack.enter_context(self.nc.psum_tensor(name, list(shape), dt))
        return Buf(t, name)

    def dram(self, name, shape, dt, kind="Internal"):
        t = self.nc.dram_tensor(name, list(shape), dt, kind=kind)
        return Buf(t, name)

    def _need(self, eng, k, v, waits):
        if self.seen[eng].get(k, 0) >= v:
            return
        waits[k] = max(waits.get(k, 0), v)

    def _deps(self, eng, reads, writes):
        waits = {}
        for b in reads:
            for k, v in b.w.items():
                self._need(eng, k, v, waits)
        for b in writes:
            for k, v in b.w.items():
                self._need(eng, k, v, waits)
            for k, v in b.r.items():
                self._need(eng, k, v, waits)
        for k, v in waits.items():
            self.seen[eng][k] = v
        return [(self.sems[k], v) for k, v in waits.items()]

    def _commit(self, k, v, reads, writes):
        self.semval[k] = v
        for b in reads:
            b.r[k] = max(b.r.get(k, 0), v)
        for b in writes:
            b.w[k] = max(b.w.get(k, 0), v)
            b.r = {}

    def op(self, eng, fn, reads=(), writes=()):
        reads = [b for b in reads if b is not None]
        writes = [b for b in writes if b is not None]
        wl = self._deps(eng, reads, writes)
        self.cnt[eng] += 1
        sem = self.sems[eng]

        def emit(h, fn=fn, wl=wl, sem=sem):
            for s, v in wl:
                h.wait_ge(s, v)
            fn(h).then_inc(sem, 1)

        self.q[eng].append(emit)
        self._commit(eng, self.cnt[eng], reads, writes)

    def mm(self, fns, reads, writes):
        eng = "pe"
        wl = self._deps(eng, reads, writes)
        self.cnt[eng] += 1
        sem = self.sems[eng]

        def emit(h, fns=fns, wl=wl, sem=sem):
            for s, v in wl:
                h.wait_ge(s, v)
            for f in fns[:-1]:
                f(h)
            fns[-1](h).then_inc(sem, 1)

        self.q[eng].append(emit)
        self._commit(eng, self.cnt[eng], reads, writes)

    def dma(self, eng, out, in_, reads, writes, semb, **kw):
        reads = [b for b in reads if b is not None]
        writes = [b for b in writes if b is not None]
        if semb.dsem is None:
            if self.free:
                key = self.free.pop()
            else:
                key = "d%d" % len(self.sems)
                self.sems[key] = self.stack.enter_context(self.nc.semaphore(key))
            semb.dsem = key
            semb.dcnt = self.dval.get(key, 0)
            self.phase_bufs.append(semb)
        wl = self._deps(eng, reads, writes)
        semb.dcnt += 16
        self.dval[semb.dsem] = semb.dcnt
        sem = self.sems[semb.dsem]

        def emit(h, wl=wl, sem=sem, out=out, in_=in_, kw=kw):
            for s, v in wl:
                h.wait_ge(s, v)
            h.dma_start(out=out, in_=in_, **kw).then_inc(sem, 16)

        self.q[eng].append(emit)
        self._commit(semb.dsem, semb.dcnt, reads, writes)

    def coll(self, src, dst, groups):
        key = "c%d" % len(self.sems)
        self.sems[key] = self.stack.enter_context(self.nc.semaphore(key))
        wl = self._deps("pool", [src], [dst])
        sem = self.sems[key]

        def emit(h, wl=wl, sem=sem):
            for s, v in wl:
                h.wait_ge(s, v)
            h.collective_compute("AllGather", ALU.bypass, replica_groups=groups,
                                 ins=[src.t.ap()], outs=[dst.t.ap()]).then_inc(sem)
            h.wait_ge(sem, 1)

        self.q["pool"].append(emit)
        self.cnt["pool"] += 1
        s2 = self.sems["pool"]
        self.q["pool"].append(lambda h, s2=s2: h.engine_nop().then_inc(s2, 1))
        self._commit("pool", self.cnt["pool"], [src], [dst])

    def barrier(self):
        for b in self.phase_bufs:
            self.free.append(b.dsem)
            b.dsem = None
        self.phase_bufs = []
        items = list(self.semval.items())
        for e in ENGS:
            wl = []
            for k, v in items:
                if self.seen[e].get(k, 0) < v:
                    self.seen[e][k] = v
                    wl.append((self.sems[k], v))

            def emit(h, wl=wl):
                for s, v in wl:
                    h.wait_ge(s, v)

            if wl:
                self.q[e].append(emit)

    def run(self):
        nc = self.nc
        with nc.Block() as block:
            @block.tensor
            def _(h):
                for f in self.q["pe"]:
                    f(h)

            @block.scalar
            def _(h):
                for f in self.q["act"]:
                    f(h)

            @block.vector
            def _(h):
                for f in self.q["dve"]:
                    f(h)

            @block.gpsimd
            def _(h):
                for f in self.q["pool"]:
                    f(h)

            @block.sync
            def _(h):
                for f in self.q["sp"]:
                    f(h)


def mult_of(d):
    d = np.asarray(d)
    m = ((d >= 0) & (d <= 128)).astype(np.float32)
    m += ((d >= 0) & (d <= 512) & (d % 4 == 0))
    m += ((d >= 0) & (d <= 2048) & (d % 16 == 0))
    return m.astype(np.float32)


IN_SPECS = [
    ("xh", [NTP, D]), ("xs", [128, D]), ("cs", [128, 25, 64]), ("sn", [128, 25, 64]),
    ("valid", [128, 24]), ("ck", [2048, 1024]), ("cv", [2048, 1024]), ("sconv", [128, 8, 2]),
    ("g_attn", [128, D]), ("g_ssm", [128, D]), ("g_fin", [128, D]),
    ("w_in_ab", [D, 8192]), ("cw", [128, 8, 3]), ("w_out_ab", [D, D]), ("w_in_c", [D, 4096]),
    ("w_glu", [D, D]), ("w_out_c", [D, D]), ("bglu", [128, 16]), ("dsk", [128, 4]),
    ("maskp", [128, 20, 512]), ("masks", [128, 17, 128]),
    ("lre_s", [128, 16]), ("lim_s", [128, 16]), ("lst_s", [128, 16]),
    ("lre_r", [128, 4, 64]), ("lim_r", [128, 4, 64]), ("lst_r", [128, 4, 64]),
    ("bre_r", [128, 4, 64]), ("bim_r", [128, 4, 64]),
    ("cre_s", [128, 16, 16]), ("cim_s", [128, 16, 16]),
    ("rmask", [128, 8]), ("smask", [128, 2]), ("sel", [128, 4]),
    ("sre0", [128, 4, 16]), ("sim0", [128, 4, 16]), ("iota", [128, 1024]),
]
OUT_SPECS = [
    ("yp", [NOWN, D]), ("ys", [128, D]), ("kp", [NOWN, 1024]), ("vp", [NOWN, 1024]),
    ("convp", [128, 8, 2]), ("ssm_p", [128, 2, 16]), ("ks", [128, 1024]), ("vs", [128, 1024]),
    ("convs", [128, 8, 2]), ("ssm_s", [128, 4, 2, 16]),
]


def build_nc():
    nc = bass.Bass("TRN2", target_bir_lowering=False)
    IN = {}
    for n, s in IN_SPECS:
        IN[n] = nc.dram_tensor(n, s, F32, kind="ExternalInput")
    OUT = {}
    for n, s in OUT_SPECS:
        OUT[n] = nc.dram_tensor(n, s, F32, kind="ExternalOutput")
    st = ExitStack()
    with st:
        P = Prog(nc, st)
        build_program(nc, P, IN, OUT)
        P.run()
    return nc


def build_program(nc, P, IN, OUT):
    GROUPS = [[0, 1, 2, 3], [4, 5, 6, 7]]
    outbufs = {n: Buf(OUT[n], n) for n in OUT}
    kT_scr = P.dram("kT_scr", [8, 128, NTP], BF16)
    v_scr = P.dram("v_scr", [NTP, 1024], BF16)
    kTs_scr = P.dram("kTs_scr", [8, 128, 2176], BF16)
    vs_scr = P.dram("vs_scr", [2176, 1024], BF16)
    qT_scr = P.dram("qT_scr", [8, 128, NO], BF16)
    h1_scr = P.dram("h1_scr", [NO, D], F32)
    hn_src = [P.dram("hn_src%d" % j, [256, NCH], BF16) for j in range(8)]
    hn_dst = [P.dram("hn_dst%d" % j, [4 * 256, NCH], BF16) for j in range(8)]
    y_src = [P.dram("y_src%d" % j, [64, 4 * NCH], BF16) for j in range(8)]
    y_dst = [P.dram("y_dst%d" % j, [4 * 64, 4 * NCH], BF16) for j in range(8)]

    pf = [P.ps("pf%d" % i, [128, 512], F32) for i in range(6)]
    pb = [P.ps("pb%d" % i, [128, 8, 128], BF16) for i in range(2)]
    pfi = [0]
    pbi = [0]

    def next_pf():
        pfi[0] = (pfi[0] + 1) % 4
        return pf[pfi[0]]

    def next_pb():
        pbi[0] = (pbi[0] + 1) % 2
        return pb[pbi[0]]

    ident = P.sb("ident", [128, 128], BF16)
    P.op("pool", lambda h: h.memset(ident.t[:], 1.0), [], [ident])
    P.op("pool", lambda h: h.affine_select(ident.t[:], ident.t[:], [[-1, 128]], ALU.is_equal, 0.0,
                                            base=0, channel_multiplier=1), [ident], [ident])
    ones_bf = P.sb("ones_bf", [128, 128], BF16)
    P.op("pool", lambda h: h.memset(ones_bf.t[:], 1.0), [], [ones_bf])
    gt = P.sb("gt", [128, D], F32)
    cs = P.sb("cs", [128, 25, 64], F32)
    sn = P.sb("sn", [128, 25, 64], F32)
    valid = P.sb("valid", [128, 24], F32)
    validB = P.sb("validB", [128, 24, 128], BF16)
    cw = P.sb("cw", [128, 8, 3], F32)
    bglu = P.sb("bglu", [128, 16], F32)
    dsk = P.sb("dsk", [128, 4], F32)
    sel = P.sb("sel", [128, 4], F32)
    eps_t = P.sb("eps_t", [128, 1], F32)
    P.op("pool", lambda h: h.memset(eps_t.t[:], 1e-6), [], [eps_t])
    for b_, n in [(cs, "cs"), (sn, "sn"), (valid, "valid"), (cw, "cw"), (bglu, "bglu"), (dsk, "dsk"), (sel, "sel")]:
        P.dma("sp", b_.t[:], IN[n].ap(), [], [b_], b_)
    P.op("dve", lambda h: h.tensor_copy(validB.t[:], valid.t[:].unsqueeze(2).to_broadcast([128, 24, 128])),
         [valid], [validB])
    pospi = P.sb("pospi", [128, 1], F32)
    P.op("pool", lambda h: h.memset(pospi.t[:], math.pi), [], [pospi])
    CONST_END = P.off
    hnT_o = P.sb("hnT_o", [128, KT, NO], BF16)

    def load_g(name):
        P.dma("sp", gt.t[:], IN[name].ap(), [], [gt], gt)

    def rmsnorm_rows(xt, xn, ss, junk):
        P.op("act", lambda h: h.activation(junk.t[:], xt.t[:], AF.Square, accum_out=ss.t[:, 0:1]), [xt], [junk, ss])
        P.op("act", lambda h: h.activation(ss.t[:, 1:2], ss.t[:, 0:1], AF.Sqrt, bias=eps_t.t[:, 0:1], scale=1.0 / D), [ss, eps_t], [ss])
        P.op("dve", lambda h: h.reciprocal(ss.t[:, 2:3], ss.t[:, 1:2]), [ss], [ss])
        P.op("dve", lambda h: h.scalar_tensor_tensor(xn.t[:], xt.t[:], ss.t[:, 2:3], gt.t[:], ALU.mult, ALU.mult),
             [xt, ss, gt], [xn])

    def transpose_rows(xn, dst, dst_ap_fn):
        for half in range(2):
            p = next_pb()
            fns = []
            for j in range(8):
                kt = half * 8 + j
                fns.append(lambda h, p=p, j=j, kt=kt: h.transpose(p.t[:, j, :], xn.t[:, kt * 128:(kt + 1) * 128], ident.t[:]))
            P.mm(fns, [xn, ident], [p])
            P.op("act", lambda h, p=p, half=half: h.activation(dst_ap_fn(half), p.t[:], AF.Identity), [p], [dst])

    A0 = P.off
    wkv = P.sb("wkv", [128, KT, 2048], BF16)
    xts = [P.sb("xt%d" % i, [128, D], F32) for i in range(2)]
    xns = [P.sb("xn%d" % i, [128, D], BF16) for i in range(2)]
    hts = [P.sb("ht%d" % i, [128, KT, 128], BF16) for i in range(2)]
    junk = P.sb("junk", [128, D], F32)
    ss = P.sb("ss", [128, 4], F32)
    kr = P.sb("kr", [128, 1024], F32)
    vf = P.sb("vf", [128, 1024], F32)
    t1 = P.sb("t1", [128, 256], F32)
    t2 = P.sb("t2", [128, 256], F32)
    krb = P.sb("krb", [128, 1024], BF16)
    vb = P.sb("vb", [128, 1024], BF16)
    kTt = P.sb("kTt", [128, 8, 128], BF16)
    hprev2 = P.sb("hprev2", [128, KT, 2], BF16)
    A1_END = P.off

    load_g("g_attn")
    for half in range(2):
        P.dma("pool", wkv.t[:, :, half * 1024:(half + 1) * 1024],
              IN["w_in_ab"].ap()[:, 1024 + half * 1024:2048 + half * 1024].rearrange("(kt p) c -> p kt c", p=128),
              [], [wkv], wkv)

    def rotary(pk, ti, dst, c0):
        v = pk.t[:].rearrange("p (h two d) -> p h two d", h=4, two=2)
        o = dst.t[:, c0:c0 + 512].rearrange("p (h two d) -> p h two d", h=4, two=2)
        cb = cs.t[:, ti, :].unsqueeze(1).to_broadcast([128, 4, 64])
        sb_ = sn.t[:, ti, :].unsqueeze(1).to_broadcast([128, 4, 64])
        a = t1.t[:].rearrange("p (h d) -> p h d", h=4)
        b = t2.t[:].rearrange("p (h d) -> p h d", h=4)
        P.op("dve", lambda h: h.tensor_tensor(a, v[:, :, 0, :], cb, ALU.mult), [pk, cs], [t1])
        P.op("dve", lambda h: h.tensor_tensor(b, v[:, :, 1, :], sb_, ALU.mult), [pk, sn], [t2])
        P.op("dve", lambda h: h.tensor_tensor(o[:, :, 0, :], a, b, ALU.subtract), [t1, t2], [dst])
        P.op("dve", lambda h: h.tensor_tensor(a, v[:, :, 1, :], cb, ALU.mult), [pk, cs], [t1])
        P.op("dve", lambda h: h.tensor_tensor(b, v[:, :, 0, :], sb_, ALU.mult), [pk, sn], [t2])
        P.op("dve", lambda h: h.tensor_tensor(o[:, :, 1, :], a, b, ALU.add), [t1, t2], [dst])

    def store_kT(src_bf, scr, col0):
        p = next_pb()
        kb = kTt
        fns = [(lambda h, p=p, j=j: h.transpose(p.t[:, j, :], src_bf.t[:, j * 128:(j + 1) * 128], ident.t[:])) for j in range(8)]
        P.mm(fns, [src_bf, ident], [p])
        P.op("act", lambda h, p=p, kb=kb: h.activation(kb.t[:], p.t[:], AF.Identity), [p], [kb])
        P.dma("sp", scr.t.ap()[:, :, col0:col0 + 128].rearrange("h d t -> d h t"), kb.t[:], [kb], [scr], kb)

    for ti in range(25):
        xt = xts[ti % 2]
        xn = xns[ti % 2]
        src = IN["xh"].ap()[ti * 128:(ti + 1) * 128, :] if ti < 24 else IN["xs"].ap()
        P.dma("sp", xt.t[:], src, [], [xt], xt)
        rmsnorm_rows(xt, xn, ss, junk)
        if ti < 16:
            ht = hts[ti % 2]
            transpose_rows(xn, ht, lambda half, ht=ht: ht.t[:, half * 8:(half + 1) * 8, :])
            lhs = lambda kt, ht=ht: ht.t[:, kt, :]
            hb = ht
            if ti == 15:
                P.op("dve", lambda h, ht=ht: h.tensor_copy(hprev2.t[:], ht.t[:, :, 126:128]), [ht], [hprev2])
        else:
            o0 = (ti - 16) * 128
            transpose_rows(xn, hnT_o, lambda half, o0=o0: hnT_o.t[:, half * 8:(half + 1) * 8, o0:o0 + 128])
            lhs = lambda kt, o0=o0: hnT_o.t[:, kt, o0:o0 + 128]
            hb = hnT_o
        for g4 in range(4):
            pk = next_pf()
            fns = [(lambda h, pk=pk, kt=kt, g4=g4, lhs=lhs: h.matmul(pk.t[:], lhs(kt), wkv.t[:, kt, g4 * 512:(g4 + 1) * 512],
                                                                    start=(kt == 0), stop=(kt == KT - 1))) for kt in range(KT)]
            P.mm(fns, [hb, wkv], [pk])
            if g4 < 2:
                rotary(pk, ti, kr, g4 * 512)
            else:
                c0 = (g4 - 2) * 512
                P.op("act", lambda h, pk=pk, c0=c0: h.activation(vf.t[:, c0:c0 + 512], pk.t[:], AF.Identity), [pk], [vf])
        P.op("act", lambda h: h.activation(krb.t[:], kr.t[:], AF.Identity), [kr], [krb])
        if ti < 24:
            P.op("dve", lambda h, ti=ti: h.tensor_scalar(vb.t[:], vf.t[:], valid.t[:, ti:ti + 1], None, ALU.mult), [vf, valid], [vb])
            store_kT(krb, kT_scr, ti * 128)
            P.dma("sp", v_scr.t.ap()[ti * 128:(ti + 1) * 128, :], vb.t[:], [vb], [v_scr], vb)
            if ti >= 16:
                r0 = (ti - 16) * 128
                P.dma("sp", OUT["kp"].ap()[r0:r0 + 128, :], kr.t[:], [kr], [outbufs["kp"]], kr)
                P.dma("sp", OUT["vp"].ap()[r0:r0 + 128, :], vf.t[:], [vf], [outbufs["vp"]], vf)
        else:
            P.op("dve", lambda h: h.tensor_copy(vb.t[:], vf.t[:]), [vf], [vb])
            store_kT(krb, kTs_scr, 2048)
            P.dma("sp", vs_scr.t.ap()[2048:2176, :], vb.t[:], [vb], [vs_scr], vb)
            P.dma("sp", OUT["ks"].ap(), kr.t[:], [kr], [outbufs["ks"]], kr)
            P.dma("sp", OUT["vs"].ap(), vf.t[:], [vf], [outbufs["vs"]], vf)
    for ti in range(16):
        xt = xts[ti % 2]
        P.dma("sp", xt.t[:, 0:1024], IN["ck"].ap()[ti * 128:(ti + 1) * 128, :], [], [xt], xt)
        P.dma("sp", xt.t[:, 1024:2048], IN["cv"].ap()[ti * 128:(ti + 1) * 128, :], [], [xt], xt)
        P.op("act", lambda h, xt=xt: h.activation(krb.t[:], xt.t[:, 0:1024], AF.Identity), [xt], [krb])
        P.op("dve", lambda h, xt=xt: h.tensor_copy(vb.t[:], xt.t[:, 1024:2048]), [xt], [vb])
        store_kT(krb, kTs_scr, ti * 128)
        P.dma("sp", vs_scr.t.ap()[ti * 128:(ti + 1) * 128, :], vb.t[:], [vb], [vs_scr], vb)

    if STOP == 'A1':
        P.barrier()
        return
    P.barrier()
    P.off = A0
    wq = P.sb("wq", [128, KT, 1024], BF16)
    hprev2b = P.sb("hprev2b", [128, KT, 2], BF16)
    qf = P.sb("qf", [128, 1024], F32)
    qb = P.sb("qb", [128, 1024], BF16)
    t1 = P.sb("t1b", [128, 256], F32)
    t2 = P.sb("t2b", [128, 256], F32)
    kTt = P.sb("kTtb", [128, 8, 128], BF16)
    hprev2k = P.sb("hprev2k", [128, KT, 2], BF16, at=hprev2.at)
    hprev2k.w = dict(hprev2.w)
    P.dma("pool", wq.t[:], IN["w_in_ab"].ap()[:, 0:1024].rearrange("(kt p) c -> p kt c", p=128), [], [wq], wq)
    for tj in range(9):
        ti = 16 + tj
        o0 = tj * 128
        for g2_ in range(2):
            pk = next_pf()
            fns = [(lambda h, pk=pk, kt=kt, g2_=g2_, o0=o0: h.matmul(pk.t[:], hnT_o.t[:, kt, o0:o0 + 128],
                                                                      wq.t[:, kt, g2_ * 512:(g2_ + 1) * 512],
                                                                      start=(kt == 0), stop=(kt == KT - 1))) for kt in range(KT)]
            P.mm(fns, [hnT_o, wq], [pk])
            rotary(pk, ti, qf, g2_ * 512)
        P.op("act", lambda h: h.activation(qb.t[:], qf.t[:], AF.Identity), [qf], [qb])
        store_kT(qb, qT_scr, o0)

    if STOP == 'A2':
        P.barrier()
        return
    P.barrier()
    P.off = A0
    ocat = P.sb("ocat", [128, KT, NO], BF16)
    hp2 = P.sb("hp2", [128, KT, 2], BF16)
    B0 = P.off
    P.op("dve", lambda h: h.tensor_copy(hp2.t[:], hprev2k.t[:]), [hprev2k], [hp2])
    P.barrier()
    maskp = P.sb("maskp", [128, 20, 512], BF16)
    masks_ = P.sb("masks_", [128, 17, 128], BF16)
    P.dma("pool", maskp.t[:], IN["maskp"].ap(), [], [maskp], maskp)
    P.dma("pool", masks_.t[:], IN["masks"].ap(), [], [masks_], masks_)
    kTh = [P.sb("kTh%d" % i, [128, NTP], BF16) for i in range(1)] * 2
    vh = [P.sb("vh%d" % i, [128, 24, 128], BF16) for i in range(1)] * 2
    kTsh = [P.sb("kTsh%d" % i, [128, 2176], BF16) for i in range(1)] * 2
    vsh = [P.sb("vsh%d" % i, [128, 17, 128], BF16) for i in range(1)] * 2
    qTh = [P.sb("qTh%d" % i, [128, NO], BF16) for i in range(1)] * 2
    wt = [P.sb("wt%d" % i, [128, KT, 128], BF16) for i in range(4)]
    pts = [P.sb("pt%d" % i, [128, 512], BF16) for i in range(3)]
    ptm = [P.sb("ptm%d" % i, [128, 512], BF16) for i in range(3)]
    za = P.sb("za", [128, NO], F32)
    rl = P.sb("rl", [128, 512], F32)
    of = P.sb("of", [128, 512], F32)
    sg = P.sb("sg", [128, 512], F32)

    def silu_evac(pk, dstb, dst_ap, n):
        P.op("act", lambda h: h.activation(sg.t[:, 0:n], pk.t[:, 0:n], AF.Exp, scale=-1.0), [pk], [sg])
        P.op("dve", lambda h: h.tensor_scalar(sg.t[:, 0:n], sg.t[:, 0:n], 1.0, None, ALU.add), [sg], [sg])
        P.op("dve", lambda h: h.reciprocal(sg.t[:, 0:n], sg.t[:, 0:n]), [sg], [sg])
        P.op("dve", lambda h: h.tensor_tensor(dst_ap, pk.t[:, 0:n], sg.t[:, 0:n], ALU.mult), [pk, sg], [dstb])
    fb = [P.sb("fb%d" % i, [128, NO + 2], F32) for i in range(4)]
    convo_p = P.sb("convo_p", [128, 8, 2], F32)
    convo_s = P.sb("convo_s", [128, 8, 2], F32)
    sconv = P.sb("sconv", [128, 8, 2], F32)
    P.dma("sp", sconv.t[:], IN["sconv"].ap(), [], [sconv], sconv)
    scale = 128.0 ** -0.5

    def load_wt(i, c0):
        P.dma("pool", wt[i].t[:], IN["w_in_ab"].ap()[:, c0:c0 + 128].rearrange("(kt p) c -> p kt c", p=128), [], [wt[i]], wt[i])

    def proj_feat(wb, dst_ap_fn, evac, rhs_buf, rhs_fn, n):
        pk = next_pf()
        fns = [(lambda h, pk=pk, kt=kt: h.matmul(pk.t[:, 0:n], wb.t[:, kt, :], rhs_fn(kt), start=(kt == 0), stop=(kt == KT - 1)))
               for kt in range(KT)]
        P.mm(fns, [wb, rhs_buf], [pk])
        evac(pk)

    def attention(hh, qT, q0, nq, kT, vt, ktiles, mask_fn, vB_fn, o_dst_fn, zcol0):
        po = pf[4]
        pl = pf[5]
        nk = len(ktiles)
        for i, kt_ in enumerate(ktiles):
            ps_ = next_pf()
            P.mm([lambda h, ps_=ps_, kt_=kt_: h.matmul(ps_.t[:, 0:nq], kT.t[:, kt_ * 128:(kt_ + 1) * 128], qT.t[:, q0:q0 + nq],
                                                        start=True, stop=True)], [kT, qT], [ps_])
            pe_ = pts[i % 3]
            pm_ = ptm[i % 3]
            P.op("act", lambda h, ps_=ps_, pe_=pe_: h.activation(pe_.t[:, 0:nq], ps_.t[:, 0:nq], AF.Exp, scale=scale), [ps_], [pe_])
            mk, mb = mask_fn(i)
            eng = "dve" if i % 2 == 0 else "pool"
            P.op(eng, lambda h, pe_=pe_, pm_=pm_, mk=mk: h.tensor_tensor(pm_.t[:, 0:nq], pe_.t[:, 0:nq], mk, ALU.mult), [pe_, mb], [pm_])
            vB, vBb = vB_fn(i)
            P.mm([lambda h, po=po, pm_=pm_, kt_=kt_, i=i: h.matmul(po.t[:, 0:nq], vt.t[:, kt_, :], pm_.t[:, 0:nq], start=(i == 0), stop=(i == nk - 1)),
                  lambda h, pl=pl, pm_=pm_, vB=vB, i=i: h.matmul(pl.t[:, 0:nq], vB, pm_.t[:, 0:nq], start=(i == 0), stop=(i == nk - 1))],
                 [vt, pm_, vBb], [po, pl])
        P.op("dve", lambda h: h.reciprocal(rl.t[:, 0:nq], pl.t[:, 0:nq]), [pl], [rl])
        P.op("dve", lambda h: h.tensor_tensor(of.t[:, 0:nq], po.t[:, 0:nq], rl.t[:, 0:nq], ALU.mult), [po, rl], [of])
        P.op("dve", lambda h: h.tensor_tensor(o_dst_fn(), of.t[:, 0:nq], za.t[:, zcol0:zcol0 + nq], ALU.mult), [of, za], [ocat])

    for hh in range(8):
        b2 = hh % 2
        P.dma("sp", kTh[b2].t[:], kT_scr.t.ap()[hh], [kT_scr], [kTh[b2]], kTh[b2])
        P.dma("sp", vh[b2].t[:], v_scr.t.ap()[:, hh * 128:(hh + 1) * 128].rearrange("(t p) d -> p t d", p=128), [v_scr], [vh[b2]], vh[b2])
        P.dma("sp", kTsh[b2].t[:], kTs_scr.t.ap()[hh], [kTs_scr], [kTsh[b2]], kTsh[b2])
        P.dma("sp", vsh[b2].t[:], vs_scr.t.ap()[:, hh * 128:(hh + 1) * 128].rearrange("(t p) d -> p t d", p=128), [vs_scr], [vsh[b2]], vsh[b2])
        P.dma("sp", qTh[b2].t[:], qT_scr.t.ap()[hh], [qT_scr], [qTh[b2]], qTh[b2])
        load_wt(0, 3072 + hh * 128)
        for (c0, n) in [(0, 512), (512, 512), (1024, 128)]:
            proj_feat(wt[0], None, lambda pk, c0=c0, n=n: silu_evac(pk, za, za.t[:, c0:c0 + n], n),
                      hnT_o, lambda kt, c0=c0, n=n: hnT_o.t[:, kt, c0:c0 + n], n)
        for qc in range(2):
            kts = list(range(4 * qc, 4 * qc + 20))
            attention(hh, qTh[b2], qc * 512, 512, kTh[b2], vh[b2], kts,
                      lambda i: (maskp.t[:, i, :], maskp),
                      lambda i, kts=kts: (validB.t[:, kts[i], :], validB),
                      lambda qc=qc, hh=hh: ocat.t[:, hh, qc * 512:(qc + 1) * 512], qc * 512)
        attention(hh, qTh[b2], 1024, 128, kTsh[b2], vsh[b2], list(range(17)),
                  lambda i: (masks_.t[:, i, :], masks_),
                  lambda i: (ones_bf.t[:], ones_bf),
                  lambda hh=hh: ocat.t[:, hh, 1024:1152], 1024)

    for cc in range(8):
        for j, base in enumerate([4096, 5120, 6144, 7168]):
            load_wt(j, base + cc * 128)
        bb, cb_, hb_, zb = fb
        for j, dstb in enumerate(fb):
            for (c0, n) in [(0, 512), (512, 512), (1024, 128)]:
                if j == 3:
                    ev = lambda pk, c0=c0, n=n, dstb=dstb: silu_evac(pk, dstb, dstb.t[:, 2 + c0:2 + c0 + n], n)
                else:
                    ev = lambda pk, c0=c0, n=n, dstb=dstb: P.op("act", lambda h: h.activation(dstb.t[:, 2 + c0:2 + c0 + n], pk.t[:, 0:n], AF.Identity), [pk], [dstb])
                proj_feat(wt[j], None, ev, hnT_o, lambda kt, c0=c0, n=n: hnT_o.t[:, kt, c0:c0 + n], n)
            if j in (1, 2):
                proj_feat(wt[j], None, lambda pk, dstb=dstb: P.op("act", lambda h: h.activation(dstb.t[:, 0:2], pk.t[:, 0:2], AF.Identity), [pk], [dstb]),
                          hp2, lambda kt: hp2.t[:, kt, :], 2)
        P.op("dve", lambda h: h.tensor_tensor(cb_.t[:], cb_.t[:], hb_.t[:], ALU.mult), [cb_, hb_], [cb_])
        w0 = cw.t[:, cc, 0:1]
        w1 = cw.t[:, cc, 1:2]
        w2 = cw.t[:, cc, 2:3]
        P.op("dve", lambda h, w2=w2: h.tensor_scalar(hb_.t[:, 2:1026], cb_.t[:, 2:1026], w2, None, ALU.mult), [cb_, cw], [hb_])
        P.op("dve", lambda h, w1=w1: h.scalar_tensor_tensor(hb_.t[:, 2:1026], cb_.t[:, 1:1025], w1, hb_.t[:, 2:1026], ALU.mult, ALU.add), [cb_, cw, hb_], [hb_])
        P.op("dve", lambda h, w0=w0: h.scalar_tensor_tensor(hb_.t[:, 2:1026], cb_.t[:, 0:1024], w0, hb_.t[:, 2:1026], ALU.mult, ALU.add), [cb_, cw, hb_], [hb_])
        P.op("dve", lambda h, cc=cc: h.tensor_copy(convo_p.t[:, cc, :], cb_.t[:, 1024:1026]), [cb_], [convo_p])
        P.op("dve", lambda h, cc=cc: h.tensor_copy(cb_.t[:, 1024:1026], sconv.t[:, cc, :]), [sconv], [cb_])
        P.op("dve", lambda h, w2=w2: h.tensor_scalar(hb_.t[:, 1026:1034], cb_.t[:, 1026:1034], w2, None, ALU.mult), [cb_, cw], [hb_])
        P.op("dve", lambda h, w1=w1: h.scalar_tensor_tensor(hb_.t[:, 1026:1034], cb_.t[:, 1025:1033], w1, hb_.t[:, 1026:1034], ALU.mult, ALU.add), [cb_, cw, hb_], [hb_])
        P.op("dve", lambda h, w0=w0: h.scalar_tensor_tensor(hb_.t[:, 1026:1034], cb_.t[:, 1024:1032], w0, hb_.t[:, 1026:1034], ALU.mult, ALU.add), [cb_, cw, hb_], [hb_])
        P.op("dve", lambda h, cc=cc: h.tensor_copy(convo_s.t[:, cc, :], cb_.t[:, 1032:1034]), [cb_], [convo_s])
        P.op("dve", lambda h: h.tensor_tensor(hb_.t[:, 2:1034], hb_.t[:, 2:1034], bb.t[:, 2:1034], ALU.mult), [hb_, bb], [hb_])
        P.op("dve", lambda h, cc=cc: h.tensor_tensor(ocat.t[:, 8 + cc, 0:1032], hb_.t[:, 2:1034], zb.t[:, 2:1034], ALU.mult), [hb_, zb], [ocat])
        P.op("dve", lambda h, cc=cc: h.memset(ocat.t[:, 8 + cc, 1032:1152], 0.0), [], [ocat])
    P.dma("sp", OUT["convp"].ap(), convo_p.t[:], [convo_p], [outbufs["convp"]], convo_p)
    P.dma("sp", OUT["convs"].ap(), convo_s.t[:], [convo_s], [outbufs["convs"]], convo_s)

    if STOP == 'B':
        P.barrier()
        return
    P.barrier()
    P.off = B0
    wg = P.sb("wg", [128, KT, 512], BF16)
    h1t = [P.sb("h1t%d" % i, [128, D], F32) for i in range(2)]
    xn1 = [P.sb("xn1%d" % i, [128, D], BF16) for i in range(2)]
    junk = P.sb("junk2", [128, D], F32)
    ss = P.sb("ss2", [128, 4], F32)
    hn1T = P.sb("hn1T", [128, KT, NCH], BF16)
    load_g("g_ssm")
    wgs = [wg]
    for tj in range(9):
        ht_ = h1t[tj % 2]
        o0 = tj * 128
        src = IN["xh"].ap()[NHALO + o0:NHALO + o0 + 128, :] if tj < 8 else IN["xs"].ap()
        P.dma("sp", ht_.t[:], src, [], [ht_], ht_)
        for g4 in range(4):
            P.dma("pool", wg.t[:], IN["w_out_ab"].ap()[:, g4 * 512:(g4 + 1) * 512].rearrange("(kt p) c -> p kt c", p=128), [], [wg], wg)
            pk = next_pf()
            fns = [(lambda h, pk=pk, kt=kt, o0=o0: h.matmul(pk.t[:], ocat.t[:, kt, o0:o0 + 128], wg.t[:, kt, :],
                                                              start=(kt == 0), stop=(kt == KT - 1))) for kt in range(KT)]
            P.mm(fns, [ocat, wg], [pk])
            P.op("dve", lambda h, pk=pk, ht_=ht_, g4=g4: h.tensor_tensor(ht_.t[:, g4 * 512:(g4 + 1) * 512], ht_.t[:, g4 * 512:(g4 + 1) * 512], pk.t[:], ALU.add),
                 [pk, ht_], [ht_])
        P.dma("sp", h1_scr.t.ap()[o0:o0 + 128, :], ht_.t[:], [ht_], [h1_scr], ht_)
        if STOP == 'C1':
            if tj < 8:
                P.dma("sp", OUT["yp"].ap()[o0:o0 + 128, :], ht_.t[:], [ht_], [outbufs["yp"]], ht_)
            else:
                P.dma("sp", OUT["ys"].ap(), ht_.t[:], [ht_], [outbufs["ys"]], ht_)
        xn = xn1[tj % 2]
        rmsnorm_rows(ht_, xn, ss, junk)
        if tj < 8:
            transpose_rows(xn, hn1T, lambda half, o0=o0: hn1T.t[:, half * 8:(half + 1) * 8, o0:o0 + 128])
        else:
            tmpT = hts_s = P.sb("tmpT", [128, KT, 128], BF16) if not hasattr(P, "_tmpT") else P._tmpT
            P._tmpT = tmpT
            transpose_rows(xn, tmpT, lambda half: tmpT.t[:, half * 8:(half + 1) * 8, :])
            P.op("dve", lambda h: h.tensor_copy(hn1T.t[:, :, 1024:1032], tmpT.t[:, :, 0:8]), [tmpT], [hn1T])
    for j in range(8):
        P.dma("sp", hn_src[j].t.ap().rearrange("(k p) t -> p k t", p=128), hn1T.t[:, 2 * j:2 * j + 2, :], [hn1T], [hn_src[j]], hn1T)
        P.coll(hn_src[j], hn_dst[j], GROUPS)

    if STOP == 'C1':
        P.barrier()
        return
    P.barrier()
    P.off = CONST_END
    uT = P.sb("uT", [128, 4, 4 * NCH], BF16)
    ygT = P.sb("ygT", [128, 4, 4 * NCH], BF16)
    L1 = P.off
    wu = P.sb("wu", [128, KT, 512], BF16)
    hch = [P.sb("hch%d" % i, [128, KT, 516], BF16) for i in range(2)]
    P.dma("pool", wu.t[:], IN["w_in_c"].ap()[:, 0:512].rearrange("(kt p) c -> p kt c", p=128), [], [wu], wu)
    ci = 0
    for r in range(4):
        for hf in range(2):
            hc = hch[ci % 2]
            ci += 1
            c0 = hf * 516
            for j in range(8):
                P.dma("sp", hc.t[:, 2 * j:2 * j + 2, :], hn_dst[j].t.ap()[r * 256:(r + 1) * 256, c0:c0 + 516].rearrange("(k p) t -> p k t", p=128),
                      [hn_dst[j]], [hc], hc)
            for ft in range(4):
                pk = next_pf()
                fns = [(lambda h, pk=pk, kt=kt, ft=ft, hc=hc: h.matmul(pk.t[:, 0:512], wu.t[:, kt, ft * 128:(ft + 1) * 128], hc.t[:, kt, 0:512],
                                                                      start=(kt == 0), stop=(kt == KT - 1))) for kt in range(KT)]
                P.mm(fns, [wu, hc], [pk])
                P.op("act", lambda h, pk=pk, ft=ft, r=r, c0=c0: h.activation(uT.t[:, ft, r * NCH + c0:r * NCH + c0 + 512], pk.t[:, 0:512], AF.Identity), [pk], [uT])
                pk2 = next_pf()
                fns2 = [(lambda h, pk2=pk2, kt=kt, ft=ft, hc=hc: h.matmul(pk2.t[:, 0:4], wu.t[:, kt, ft * 128:(ft + 1) * 128], hc.t[:, kt, 512:516],
                                                                         start=(kt == 0), stop=(kt == KT - 1))) for kt in range(KT)]
                P.mm(fns2, [wu, hc], [pk2])
                P.op("act", lambda h, pk2=pk2, ft=ft, r=r, c0=c0: h.activation(uT.t[:, ft, r * NCH + c0 + 512:r * NCH + c0 + 516], pk2.t[:, 0:4], AF.Identity), [pk2], [uT])

    if STOP == 'U':
        P.barrier()
        return
    P.barrier()
    P.off = L1
    def small(name, shape, dt=F32):
        return P.sb(name, shape, dt)
    lre_s = small("lre_s", [128, 16]); lim_s = small("lim_s", [128, 16]); lst_s = small("lst_s", [128, 16])
    lre_r = small("lre_r", [128, 256]); lim_r = small("lim_r", [128, 256]); lst_r = small("lst_r", [128, 256])
    bre_r = small("bre_r", [128, 256]); bim_r = small("bim_r", [128, 256])
    cre_s = small("cre_s", [128, 16, 16]); cim_s = small("cim_s", [128, 16, 16])
    rmask = small("rmask", [128, 8]); smask = small("smask", [128, 2])
    sre0 = small("sre0", [128, 4, 16]); sim0 = small("sim0", [128, 4, 16])
    iota = small("iota", [128, 1024])
    for b_, n in [(lre_s, "lre_s"), (lim_s, "lim_s"), (lst_s, "lst_s"), (cre_s, "cre_s"), (cim_s, "cim_s"),
                  (rmask, "rmask"), (smask, "smask"), (sre0, "sre0"), (sim0, "sim0"), (iota, "iota")]:
        P.dma("sp", b_.t[:], IN[n].ap(), [], [b_], b_)
    for b_, n in [(lre_r, "lre_r"), (lim_r, "lim_r"), (lst_r, "lst_r"), (bre_r, "bre_r"), (bim_r, "bim_r")]:
        P.dma("sp", b_.t[:], IN[n].ap().rearrange("p a b -> p (a b)"), [], [b_], b_)
    negpi = small("negpi", [128, 1])
    P.op("dve", lambda h: h.memset(negpi.t[:], -math.pi), [], [negpi])

    I32 = mybir.dt.int32
    tq = small("tq", [128, 1024]); tiq = small("tiq", [128, 1024], I32)
    halfpi = small("halfpi", [128, 1]); zero_t = small("zero_t", [128, 1])
    P.op("dve", lambda h: h.memset(halfpi.t[:], 0.5 * math.pi), [], [halfpi])
    P.op("dve", lambda h: h.memset(zero_t.t[:], 0.0), [], [zero_t])

    def sincos(ang_ap, n, s_ap, c_ap, rd, wr):
        for (dst, addc, bt, lo, hi) in [(s_ap, 0.0, zero_t, -math.pi, math.pi), (c_ap, 0.25, halfpi, -1.5 * math.pi, 0.5 * math.pi)]:
            P.op("dve", lambda h, addc=addc: h.tensor_scalar(tq.t[:, 0:n], ang_ap, 1.0 / TWO_PI, addc, ALU.mult, ALU.add), rd, [tq])
            P.op("dve", lambda h: h.tensor_copy(tiq.t[:, 0:n], tq.t[:, 0:n]), [tq], [tiq])
            P.op("dve", lambda h: h.tensor_copy(tq.t[:, 0:n], tiq.t[:, 0:n]), [tiq], [tq])
            P.op("dve", lambda h: h.scalar_tensor_tensor(tq.t[:, 0:n], tq.t[:, 0:n], -TWO_PI, ang_ap, ALU.mult, ALU.add), [tq] + rd, [tq])
            P.op("dve", lambda h, lo=lo, hi=hi: h.tensor_scalar(tq.t[:, 0:n], tq.t[:, 0:n], lo, hi, ALU.max, ALU.min), [tq], [tq])
            P.op("act", lambda h, dst=dst, bt=bt: h.activation(dst, tq.t[:, 0:n], AF.Sin, bias=bt.t[:, 0:1], scale=1.0), [tq, bt], wr)

    def disc(lre, lim, lst, n, pref):
        o = {}
        for nm in ["step", "mag", "th", "c", "s", "tmp", "nr", "den", "cr", "ci", "a", "b"]:
            o[nm] = small(pref + nm, [128, n])
        al = [o[k] for k in o] + [lre, lim, lst]
        P.op("act", lambda h: h.activation(o["step"].t[:], lst.t[:, 0:n], AF.Exp), al, al)
        P.op("dve", lambda h: h.tensor_tensor(o["th"].t[:], lim.t[:, 0:n], o["step"].t[:], ALU.mult), al, al)
        P.op("dve", lambda h: h.tensor_tensor(o["a"].t[:], lre.t[:, 0:n], o["step"].t[:], ALU.mult), al, al)
        P.op("act", lambda h: h.activation(o["mag"].t[:], o["a"].t[:], AF.Exp), al, al)
        sincos(o["th"].t[:], n, o["s"].t[:], o["c"].t[:], al, al)
        P.op("dve", lambda h: h.tensor_tensor(o["a"].t[:], o["mag"].t[:], o["c"].t[:], ALU.mult), al, al)
        P.op("dve", lambda h: h.tensor_scalar(o["nr"].t[:], o["a"].t[:], 1.0, -1.0, ALU.mult, ALU.add), al, al)
        P.op("dve", lambda h: h.tensor_tensor(o["b"].t[:], o["mag"].t[:], o["s"].t[:], ALU.mult), al, al)
        P.op("dve", lambda h: h.tensor_tensor(o["den"].t[:], lre.t[:, 0:n], lre.t[:, 0:n], ALU.mult), al, al)
        P.op("dve", lambda h: h.tensor_tensor(o["tmp"].t[:], lim.t[:, 0:n], lim.t[:, 0:n], ALU.mult), al, al)
        P.op("dve", lambda h: h.tensor_tensor(o["den"].t[:], o["den"].t[:], o["tmp"].t[:], ALU.add), al, al)
        P.op("dve", lambda h: h.reciprocal(o["den"].t[:], o["den"].t[:]), al, al)
        P.op("dve", lambda h: h.tensor_tensor(o["cr"].t[:], o["nr"].t[:], lre.t[:, 0:n], ALU.mult), al, al)
        P.op("dve", lambda h: h.tensor_tensor(o["tmp"].t[:], o["b"].t[:], lim.t[:, 0:n], ALU.mult), al, al)
        P.op("dve", lambda h: h.tensor_tensor(o["cr"].t[:], o["cr"].t[:], o["tmp"].t[:], ALU.add), al, al)
        P.op("dve", lambda h: h.tensor_tensor(o["cr"].t[:], o["cr"].t[:], o["den"].t[:], ALU.mult), al, al)
        P.op("dve", lambda h: h.tensor_tensor(o["ci"].t[:], o["b"].t[:], lre.t[:, 0:n], ALU.mult), al, al)
        P.op("dve", lambda h: h.tensor_tensor(o["tmp"].t[:], o["nr"].t[:], lim.t[:, 0:n], ALU.mult), al, al)
        P.op("dve", lambda h: h.tensor_tensor(o["ci"].t[:], o["ci"].t[:], o["tmp"].t[:], ALU.subtract), al, al)
        P.op("dve", lambda h: h.tensor_tensor(o["ci"].t[:], o["ci"].t[:], o["den"].t[:], ALU.mult), al, al)
        return o, al

    ds_, als = disc(lre_s, lim_s, lst_s, 16, "ds_")
    dr_, alr = disc(lre_r, lim_r, lst_r, 256, "dr_")
    bbr = small("bbr", [128, 256]); bbi = small("bbi", [128, 256]); tmpr = small("tmpr", [128, 256])
    alr2 = alr + [bbr, bbi, tmpr, bre_r, bim_r]
    P.op("dve", lambda h: h.tensor_tensor(bbr.t[:], dr_["cr"].t[:], bre_r.t[:], ALU.mult), alr2, alr2)
    P.op("dve", lambda h: h.tensor_tensor(tmpr.t[:], dr_["ci"].t[:], bim_r.t[:], ALU.mult), alr2, alr2)
    P.op("dve", lambda h: h.tensor_tensor(bbr.t[:], bbr.t[:], tmpr.t[:], ALU.subtract), alr2, alr2)
    P.op("dve", lambda h: h.tensor_tensor(bbi.t[:], dr_["cr"].t[:], bim_r.t[:], ALU.mult), alr2, alr2)
    P.op("dve", lambda h: h.tensor_tensor(tmpr.t[:], dr_["ci"].t[:], bre_r.t[:], ALU.mult), alr2, alr2)
    P.op("dve", lambda h: h.tensor_tensor(bbi.t[:], bbi.t[:], tmpr.t[:], ALU.add), alr2, alr2)
    BbT = [small("BbT%d" % ri, [128, 16, 128], BF16) for ri in range(2)]
    for ri, src in enumerate([bbr, bbi]):
        for qq in range(4):
            for g2 in range(2):
                m = rmask.t[:, qq * 2 + g2:qq * 2 + g2 + 1]
                o_ap = BbT[ri].t[:].rearrange("p (ft q) c -> p ft q c", q=4)[:, :, qq, g2 * 64:(g2 + 1) * 64]
                i_ap = src.t[:].rearrange("p (ft d) -> p ft d", ft=4)
                P.op("dve", lambda h, o_ap=o_ap, i_ap=i_ap, m=m: h.tensor_scalar(o_ap, i_ap, m, None, ALU.mult), alr2 + [rmask], [BbT[ri]])
    CT = [small("CT%d" % ri, [128, 16, 128], BF16) for ri in range(2)]
    for ri in range(2):
        P.op("dve", lambda h, ri=ri: h.memset(CT[ri].t[:], 0.0), [], [CT[ri]])
    for ri, (src, sgn) in enumerate([(cre_s, 1.0), (cim_s, -1.0)]):
        for pair in range(16):
            qq = pair % 4
            for g2 in range(2):
                m = smask.t[:, g2:g2 + 1]
                col = qq * 32 + g2 * 16
                P.op("dve", lambda h, ri=ri, pair=pair, col=col, m=m, src=src, sgn=sgn: h.tensor_scalar(
                    CT[ri].t[:, pair, col:col + 16], src.t[:, pair, :], m, sgn, ALU.mult, ALU.mult), [src, smask], [CT[ri]])
    rr = ds_["mag"]; th = ds_["th"]
    cth = small("cth", [128, 16]); sth = small("sth", [128, 16])
    P.op("dve", lambda h: h.tensor_copy(cth.t[:], ds_["c"].t[:]), als, [cth])
    P.op("dve", lambda h: h.tensor_copy(sth.t[:], ds_["s"].t[:]), als, [sth])

    tabc = [small("tabc%d" % i, [128, 1024]) for i in range(4)]
    tabs = [small("tabs%d" % i, [128, 1024]) for i in range(4)]
    gr = small("gr", [128, 1024]); gi = small("gi", [128, 1024])
    ttmp = gi; ang = gr
    yr = small("yr", [128, 1024]); yi = small("yi", [128, 1024])
    hrb = small("hrb", [128, 1024], BF16); hib = small("hib", [128, 1024], BF16)
    ytmp = small("ytmp", [128, 512]); ysq = small("ysq", [128, 512])
    stp = small("stp", [128, 16, 2])
    sts = small("sts", [128, 4, 16, 2])
    gin = small("gin", [128, 2]); hend = small("hend", [128, 2])
    P.op("dve", lambda h: h.memset(stp.t[:], 0.0), [], [stp])
    GC = 1.5957691216057308

    def run_seq(pair, qq, ft, col0, T, tc_, ts_, init_re, init_im, init_bufs, out_re, out_im, out_buf, ypsum, yp_off):
        for h0 in range(0, T, 512):
            n = min(512, T - h0)
            pxr = next_pf(); pxi = next_pf()
            P.mm([lambda h, pxr=pxr, n=n, h0=h0: h.matmul(pxr.t[:, 0:n], BbT[0].t[:, pair, :], uT.t[:, ft, col0 + h0:col0 + h0 + n], start=True, stop=True)], [BbT[0], uT], [pxr])
            P.mm([lambda h, pxi=pxi, n=n, h0=h0: h.matmul(pxi.t[:, 0:n], BbT[1].t[:, pair, :], uT.t[:, ft, col0 + h0:col0 + h0 + n], start=True, stop=True)], [BbT[1], uT], [pxi])
            c_ = tc_.t[:, h0:h0 + n]; s_ = ts_.t[:, h0:h0 + n]
            P.op("dve", lambda h, pxr=pxr, c_=c_, n=n, h0=h0: h.tensor_tensor(yr.t[:, h0:h0 + n], pxr.t[:, 0:n], c_, ALU.mult), [pxr, tc_], [yr])
            P.op("dve", lambda h, pxi=pxi, s_=s_, n=n, h0=h0: h.tensor_tensor(gr.t[:, h0:h0 + n], pxi.t[:, 0:n], s_, ALU.mult), [pxi, ts_], [gr])
            P.op("pool", lambda h, n=n, h0=h0: h.tensor_tensor(yr.t[:, h0:h0 + n], yr.t[:, h0:h0 + n], gr.t[:, h0:h0 + n], ALU.add), [yr, gr], [yr])
            P.op("dve", lambda h, pxi=pxi, c_=c_, n=n, h0=h0: h.tensor_tensor(yi.t[:, h0:h0 + n], pxi.t[:, 0:n], c_, ALU.mult), [pxi, tc_], [yi])
            P.op("dve", lambda h, pxr=pxr, s_=s_, n=n, h0=h0: h.tensor_tensor(gi.t[:, h0:h0 + n], pxr.t[:, 0:n], s_, ALU.mult), [pxr, ts_], [gi])
            P.op("pool", lambda h, n=n, h0=h0: h.tensor_tensor(yi.t[:, h0:h0 + n], yi.t[:, h0:h0 + n], gi.t[:, h0:h0 + n], ALU.subtract), [yi, gi], [yi])
        ct = cth.t[:, pair:pair + 1]; st_ = sth.t[:, pair:pair + 1]
        P.op("dve", lambda h: h.tensor_scalar(gin.t[:, 0:1], init_re, ct, None, ALU.mult), init_bufs + [cth], [gin])
        P.op("dve", lambda h: h.scalar_tensor_tensor(gin.t[:, 0:1], init_im, st_, gin.t[:, 0:1], ALU.mult, ALU.subtract), init_bufs + [sth, gin], [gin])
        P.op("dve", lambda h: h.tensor_scalar(gin.t[:, 0:1], gin.t[:, 0:1], -1.0, None, ALU.mult), [gin], [gin])
        P.op("dve", lambda h: h.tensor_scalar(gin.t[:, 1:2], init_re, st_, None, ALU.mult), init_bufs + [sth], [gin])
        P.op("dve", lambda h: h.scalar_tensor_tensor(gin.t[:, 1:2], init_im, ct, gin.t[:, 1:2], ALU.mult, ALU.add), init_bufs + [cth, gin], [gin])
        rb = rr.t[:, pair:pair + 1].to_broadcast([128, T])
        P.op("dve", lambda h: h.tensor_tensor_scan(gr.t[:, 0:T], rb, yr.t[:, 0:T], gin.t[:, 0:1], ALU.mult, ALU.add), [yr, gin] + als, [gr])
        P.op("dve", lambda h: h.tensor_tensor_scan(gi.t[:, 0:T], rb, yi.t[:, 0:T], gin.t[:, 1:2], ALU.mult, ALU.add), [yi, gin] + als, [gi])
        c_ = tc_.t[:, 0:T]; s_ = ts_.t[:, 0:T]
        P.op("dve", lambda h: h.tensor_tensor(yr.t[:, 0:T], gr.t[:, 0:T], c_, ALU.mult), [gr, tc_], [yr])
        P.op("pool", lambda h: h.tensor_tensor(yi.t[:, 0:T], gi.t[:, 0:T], s_, ALU.mult), [gi, ts_], [yi])
        P.op("dve", lambda h: h.tensor_tensor(hrb.t[:, 0:T], yr.t[:, 0:T], yi.t[:, 0:T], ALU.subtract), [yr, yi], [hrb])
        P.op("dve", lambda h: h.tensor_tensor(hend.t[:, 0:1], yr.t[:, T - 1:T], yi.t[:, T - 1:T], ALU.subtract), [yr, yi], [hend])
        P.op("pool", lambda h: h.tensor_tensor(yr.t[:, 0:T], gr.t[:, 0:T], s_, ALU.mult), [gr, ts_, hrb, hend], [yr])
        P.op("dve", lambda h: h.tensor_tensor(yi.t[:, 0:T], gi.t[:, 0:T], c_, ALU.mult), [gi, tc_, hrb, hend], [yi])
        P.op("dve", lambda h: h.tensor_tensor(hib.t[:, 0:T], yr.t[:, 0:T], yi.t[:, 0:T], ALU.add), [yr, yi], [hib])
        P.op("dve", lambda h: h.tensor_tensor(hend.t[:, 1:2], yr.t[:, T - 1:T], yi.t[:, T - 1:T], ALU.add), [yr, yi], [hend])
        P.op("dve", lambda h: h.tensor_copy(out_re, hend.t[:, 0:1]), [hend], [out_buf])
        P.op("dve", lambda h: h.tensor_copy(out_im, hend.t[:, 1:2]), [hend], [out_buf])
        for h0 in range(0, T, 512):
            n = min(512, T - h0)
            yp_ = ypsum[(yp_off + h0) // 512]
            P.mm([lambda h, yp_=yp_, n=n, h0=h0: h.matmul(yp_.t[:, 0:n], CT[0].t[:, pair, :], hrb.t[:, h0:h0 + n], start=(qq == 0), stop=False),
                  lambda h, yp_=yp_, n=n, h0=h0: h.matmul(yp_.t[:, 0:n], CT[1].t[:, pair, :], hib.t[:, h0:h0 + n], start=False, stop=(qq == 3))],
                 [CT[0], CT[1], hrb, hib], [yp_])

    def y_evac(yp_, ft, col0, n):
        P.op("dve", lambda h: h.scalar_tensor_tensor(ytmp.t[:, 0:n], uT.t[:, ft, col0:col0 + n], dsk.t[:, ft:ft + 1], yp_.t[:, 0:n], ALU.mult, ALU.add),
             [uT, dsk, yp_], [ytmp])
        P.op("dve", lambda h: h.tensor_tensor(ysq.t[:, 0:n], ytmp.t[:, 0:n], ytmp.t[:, 0:n], ALU.mult), [ytmp], [ysq])
        P.op("dve", lambda h: h.tensor_scalar(ysq.t[:, 0:n], ysq.t[:, 0:n], 0.044715, 1.0, ALU.mult, ALU.add), [ysq], [ysq])
        P.op("dve", lambda h: h.tensor_tensor(ysq.t[:, 0:n], ysq.t[:, 0:n], ytmp.t[:, 0:n], ALU.mult), [ysq, ytmp], [ysq])
        P.op("act", lambda h: h.activation(ysq.t[:, 0:n], ysq.t[:, 0:n], AF.Sigmoid, scale=GC), [ysq], [ysq])
        P.op("dve", lambda h: h.tensor_tensor(ygT.t[:, ft, col0:col0 + n], ysq.t[:, 0:n], ytmp.t[:, 0:n], ALU.mult), [ysq, ytmp], [ygT])

    ypb = [pf[4], pf[5]]
    for ft in range(4):
        for qq in range(4):
            pair = ft * 4 + qq
            P.op("dve", lambda h, pair=pair: h.tensor_scalar(ang.t[:], iota.t[:], th.t[:, pair:pair + 1], None, ALU.mult), [iota] + als, [ang])
            sincos(ang.t[:], 1024, tabs[qq].t[:], tabc[qq].t[:], [ang], [tabs[qq], tabc[qq]])
        for seg in range(4):
            for qq in range(4):
                pair = ft * 4 + qq
                run_seq(pair, qq, ft, seg * NCH, 1024, tabc[qq], tabs[qq],
                        stp.t[:, pair, 0:1], stp.t[:, pair, 1:2], [stp],
                        stp.t[:, pair, 0:1], stp.t[:, pair, 1:2], stp, ypb, 0)
            for hf in range(2):
                y_evac(ypb[hf], ft, seg * NCH + hf * 512, 512)
        for sb_i in range(4):
            for qq in range(4):
                pair = ft * 4 + qq
                run_seq(pair, qq, ft, sb_i * NCH + 1024, 8, tabc[qq], tabs[qq],
                        sre0.t[:, sb_i, pair:pair + 1], sim0.t[:, sb_i, pair:pair + 1], [sre0, sim0],
                        sts.t[:, sb_i, pair, 0:1], sts.t[:, sb_i, pair, 1:2], sts, ypb, 0)
            y_evac(ypb[0], ft, sb_i * NCH + 1024, 8)
    stp2 = small("stp2", [128, 2, 16]); sts2 = small("sts2", [128, 4, 2, 16])
    if STOP == 'SSM':
        for r_ in range(4):
            P.dma("pool", OUT["yp"].ap().rearrange("(p f x) c -> p f (x c)", p=128, f=4)[:, :, r_ * 1024:(r_ + 1) * 1024],
                  ygT.t[:, :, r_ * NCH:r_ * NCH + 1024], [ygT], [outbufs["yp"]], ygT)
        P.dma("pool", OUT["ys"].ap()[:, 0:128].rearrange("p (f r s) -> p f r s", f=4, r=4),
              ygT.t[:].rearrange("p f (r c) -> p f r c", r=4)[:, :, :, 1024:1032], [ygT], [outbufs["ys"]], ygT)
    P.op("dve", lambda h: h.tensor_copy(stp2.t[:], stp.t[:].rearrange("p a b -> p b a")), [stp], [stp2])
    P.op("dve", lambda h: h.tensor_copy(sts2.t[:], sts.t[:].rearrange("p s a b -> p s b a")), [sts], [sts2])
    P.dma("sp", OUT["ssm_p"].ap(), stp2.t[:], [stp2], [outbufs["ssm_p"]], stp2)
    P.dma("sp", OUT["ssm_s"].ap(), sts2.t[:], [sts2], [outbufs["ssm_s"]], sts2)
    for j in range(8):
        P.dma("sp", y_src[j].t.ap(), ygT.t[(j % 2) * 64:(j % 2) * 64 + 64, j // 2, :], [ygT], [y_src[j]], ygT)
        P.coll(y_src[j], y_dst[j], GROUPS)

    if STOP == 'SSM':
        P.barrier()
        return
    P.barrier()
    P.off = CONST_END
    ygo = P.sb("ygo", [128, KT, NCH], BF16)
    hn1o = P.sb("hn1o", [128, KT, NCH], BF16)
    y2T = P.sb("y2T", [128, KT, NO], BF16)
    ych = [P.sb("ych%d" % i, [128, 4, NCH], BF16) for i in range(1)] * 2
    wt2 = [P.sb("wt2_%d" % i, [128, KT, 128], BF16) for i in range(2)]
    gl = P.sb("gl", [128, NCH], F32); zz = P.sb("zz", [128, NCH], F32)
    wg2 = P.sb("wg2", [128, KT, 512], BF16)
    h1t = [P.sb("h1u%d" % i, [128, D], F32) for i in range(1)] * 2
    junk = P.sb("junk3", [128, D], F32); ss = P.sb("ss3", [128, 4], F32)
    yo = [P.sb("yo%d" % i, [128, D], F32) for i in range(1)] * 2
    for j in range(8):
        P.dma("sp", hn1o.t[:, 2 * j:2 * j + 2, :], hn_src[j].t.ap().rearrange("(k p) t -> p k t", p=128), [hn_src[j]], [hn1o], hn1o)
    P.op("pool", lambda h: h.memset(y2T.t[:, :, NCH:NO], 0.0), [], [y2T])
    ci = 0
    for rf in range(4):
        for r in range(4):
            yc = ych[ci % 2]; ci += 1
            for j in range(8):
                P.dma("sp", yc.t[(j % 2) * 64:(j % 2) * 64 + 64, j // 2, :], y_dst[j].t.ap()[rf * 64:(rf + 1) * 64, r * NCH:(r + 1) * NCH],
                      [y_dst[j]], [yc], yc)
            dst = ygo.t[:, rf * 4:(rf + 1) * 4, :]
            if r == 0:
                P.op("dve", lambda h, yc=yc, dst=dst: h.tensor_scalar(dst, yc.t[:], sel.t[:, 0:1], None, ALU.mult), [yc, sel], [ygo])
            else:
                P.op("dve", lambda h, yc=yc, dst=dst, r=r: h.scalar_tensor_tensor(dst, yc.t[:], sel.t[:, r:r + 1], dst, ALU.mult, ALU.add), [yc, sel, ygo], [ygo])
    if STOP == 'G1':
        P.barrier()
        return
    for nt in range(int(os.environ.get('MK_NT', '16'))):
        P.dma("pool", wt2[0].t[:], IN["w_glu"].ap()[:, nt * 128:(nt + 1) * 128].rearrange("(kt p) c -> p kt c", p=128), [], [wt2[0]], wt2[0])
        P.dma("pool", wt2[1].t[:], IN["w_in_c"].ap()[:, 2048 + nt * 128:2048 + (nt + 1) * 128].rearrange("(kt p) c -> p kt c", p=128), [], [wt2[1]], wt2[1])
        for (c0, n) in [(0, 512), (512, 512), (1024, 8)]:
            proj_feat(wt2[0], None, lambda pk, c0=c0, n=n, nt=nt: P.op("act", lambda h: h.activation(gl.t[:, c0:c0 + n], pk.t[:, 0:n], AF.Sigmoid, bias=bglu.t[:, nt:nt + 1], scale=1.0), [pk, bglu], [gl]),
                      ygo, lambda kt, c0=c0, n=n: ygo.t[:, kt, c0:c0 + n], n)
            def zev(pk, c0=c0, n=n):
                P.op("act", lambda h: h.activation(zz.t[:, c0:c0 + n], pk.t[:, 0:n], AF.Sigmoid), [pk], [zz])
                P.op("dve", lambda h: h.tensor_tensor(zz.t[:, c0:c0 + n], zz.t[:, c0:c0 + n], pk.t[:, 0:n], ALU.mult), [zz, pk], [zz])
            proj_feat(wt2[1], None, zev, hn1o, lambda kt, c0=c0, n=n: hn1o.t[:, kt, c0:c0 + n], n)
        P.op("dve", lambda h, nt=nt: h.tensor_tensor(gl.t[:], gl.t[:], ygo.t[:, nt, :], ALU.mult), [gl, ygo], [gl])
        P.op("dve", lambda h, nt=nt: h.tensor_tensor(y2T.t[:, nt, 0:NCH], gl.t[:], zz.t[:], ALU.mult), [gl, zz], [y2T])
    if STOP == 'G2':
        P.barrier()
        return
    load_g("g_fin")
    for tj in range(9):
        ht_ = h1t[tj % 2]
        o0 = tj * 128
        P.dma("sp", ht_.t[:], h1_scr.t.ap()[o0:o0 + 128, :], [h1_scr], [ht_], ht_)
        for g4 in range(4):
            P.dma("pool", wg2.t[:], IN["w_out_c"].ap()[:, g4 * 512:(g4 + 1) * 512].rearrange("(kt p) c -> p kt c", p=128), [], [wg2], wg2)
            pk = next_pf()
            fns = [(lambda h, pk=pk, kt=kt, o0=o0: h.matmul(pk.t[:], y2T.t[:, kt, o0:o0 + 128], wg2.t[:, kt, :],
                                                              start=(kt == 0), stop=(kt == KT - 1))) for kt in range(KT)]
            P.mm(fns, [y2T, wg2], [pk])
            P.op("dve", lambda h, pk=pk, ht_=ht_, g4=g4: h.tensor_tensor(ht_.t[:, g4 * 512:(g4 + 1) * 512], ht_.t[:, g4 * 512:(g4 + 1) * 512], pk.t[:], ALU.add),
                 [pk, ht_], [ht_])
        yo_ = yo[tj % 2]
        P.op("act", lambda h, ht_=ht_: h.activation(junk.t[:], ht_.t[:], AF.Square, accum_out=ss.t[:, 0:1]), [ht_], [junk, ss])
        P.op("act", lambda h: h.activation(ss.t[:, 1:2], ss.t[:, 0:1], AF.Sqrt, bias=eps_t.t[:, 0:1], scale=1.0 / D), [ss, eps_t], [ss])
        P.op("dve", lambda h: h.reciprocal(ss.t[:, 2:3], ss.t[:, 1:2]), [ss], [ss])
        P.op("dve", lambda h, ht_=ht_, yo_=yo_: h.scalar_tensor_tensor(yo_.t[:], ht_.t[:], ss.t[:, 2:3], gt.t[:], ALU.mult, ALU.mult), [ht_, ss, gt], [yo_])
        if tj < 8:
            P.dma("sp", OUT["yp"].ap()[o0:o0 + 128, :], yo_.t[:], [yo_], [outbufs["yp"]], yo_)
        else:
            P.dma("sp", OUT["ys"].ap(), yo_.t[:], [yo_], [outbufs["ys"]], yo_)
    P.barrier()


_NC_CACHE = {}


def _rope_tables(pos):
    half = 64
    inv = (np.float32(10000.0) ** (-np.arange(half, dtype=np.float32) / np.float32(half))).astype(np.float32)
    ang = pos.astype(np.float32)[:, None] * inv[None, :]
    return np.cos(ang).astype(np.float32), np.sin(ang).astype(np.float32)


def kernel(x_prompt, x_sample, cache_win_k, cache_win_v, state_conv, state_ssm_re, state_ssm_im,
           attn_norm, w_in_ab, conv_w, w_out_ab, ssm_norm, w_in_c, lam_re, lam_im, log_step,
           b_re, b_im, c_re, c_im, d_skip, w_glu, b_glu, w_out_c, final_norm):
    f = lambda a: np.ascontiguousarray(np.asarray(a, dtype=np.float32))
    x_prompt, x_sample = f(x_prompt), f(x_sample)
    cache_win_k, cache_win_v, state_conv = f(cache_win_k), f(cache_win_v), f(state_conv)
    state_ssm_re, state_ssm_im = f(state_ssm_re), f(state_ssm_im)
    w_in_ab0, w_out_ab0, w_in_c0, w_glu0, w_out_c0 = f(w_in_ab)[0], f(w_out_ab)[0], f(w_in_c)[0], f(w_glu)[0], f(w_out_c)[0]
    lam_re, lam_im, log_step = f(lam_re)[0], f(lam_im)[0], f(log_step)[0]
    b_re, b_im, c_re, c_im = f(b_re)[0], f(b_im)[0], f(c_re)[0], f(c_im)[0]
    d_skip0, b_glu0 = f(d_skip)[0], f(b_glu)[0]
    if "nc" not in _NC_CACHE:
        _NC_CACHE["nc"] = build_nc()
    nc = _NC_CACHE["nc"]

    kk = np.arange(128)[:, None]
    qq_ = np.arange(512)[None, :]
    maskp = np.stack([mult_of(qq_ - ((i - 16) * 128 + kk)) for i in range(20)], 1)
    rows = np.arange(2176).reshape(17, 128)
    s_ = np.arange(128)[None, :]
    masks = np.zeros((128, 17, 128), np.float32)
    for i in range(17):
        row = rows[i][:, None]
        m = mult_of(2048 + s_ - row)
        m[:, 8:] = ((2048 + s_[:, 8:] - row) == 0)
        masks[:, i, :] = m
    iota = np.broadcast_to(np.arange(1024, dtype=np.float32)[None, :], (128, 1024)).copy()
    rmask = np.zeros((128, 8), np.float32)
    for p in range(128):
        rmask[p, (p // 32) * 2 + (p % 32) // 16] = 1.0
    smask = np.zeros((128, 2), np.float32)
    smask[:64, 0] = 1.0
    smask[64:, 1] = 1.0
    bc = lambda v: np.ascontiguousarray(np.broadcast_to(v[None, :], (128, v.shape[0])))

    in_maps = []
    for c in range(8):
        b, r = c // 4, c % 4
        T0 = r * NOWN
        xh = np.zeros((NTP, D), np.float32)
        lo = T0 - NHALO
        src_lo = max(lo, 0)
        xh[src_lo - lo:] = x_prompt[b, src_lo:T0 + NOWN]
        pos = np.concatenate([np.arange(lo, T0 + NOWN), PAST + np.arange(128)]).astype(np.float32)
        valid = (pos[:NTP] >= 0).astype(np.float32)
        cosv, sinv = _rope_tables(np.maximum(pos, 0))
        xs = np.zeros((128, D), np.float32)
        xs[:8] = x_sample[c]
        g0 = 32 * r
        gs = slice(g0, g0 + 32)
        st_lay = lambda a: np.ascontiguousarray(a.reshape(16, 2, 64).transpose(1, 2, 0).reshape(128, 16))
        def row_lay_rep(a):
            t = a.reshape(4, 4, 2, 64)
            t = np.broadcast_to(t[:, :, :, None, :], (4, 4, 2, 16, 64))
            return np.ascontiguousarray(t.transpose(1, 2, 3, 0, 4).reshape(128, 4, 64))
        def row_lay_b(a):
            t = a.reshape(4, 4, 2, 64, 16)
            return np.ascontiguousarray(t.transpose(1, 2, 4, 0, 3).reshape(128, 4, 64))
        def st_lay_c(a):
            t = a.reshape(16, 2, 16, 64)
            return np.ascontiguousarray(t.transpose(1, 3, 0, 2).reshape(128, 16, 16))
        lst32 = np.broadcast_to(log_step[gs][:, None], (32, 64))
        sel = np.zeros((128, 4), np.float32)
        sel[:, r] = 1.0
        sre0 = np.stack([st_lay(state_ssm_re[0, 4 * b + i, gs]) for i in range(4)], 1)
        sim0 = np.stack([st_lay(state_ssm_im[0, 4 * b + i, gs]) for i in range(4)], 1)
        w_in_c_rolled = np.concatenate([w_in_c0[:, 512 * r:512 * (r + 1)], w_in_c0[:, 512:2048], w_in_c0[:, 2048:]], 1)
        m = {
            "xh": xh, "xs": xs,
            "cs": np.ascontiguousarray(cosv.reshape(25, 128, 64).transpose(1, 0, 2)),
            "sn": np.ascontiguousarray(sinv.reshape(25, 128, 64).transpose(1, 0, 2)),
            "valid": np.ascontiguousarray(valid.reshape(24, 128).T),
            "ck": np.ascontiguousarray(cache_win_k[0, c].reshape(2048, 1024)),
            "cv": np.ascontiguousarray(cache_win_v[0, c].reshape(2048, 1024)),
            "sconv": np.ascontiguousarray(state_conv[0, c].reshape(2, 8, 128).transpose(2, 1, 0)),
            "g_attn": bc(f(attn_norm)[0]), "g_ssm": bc(f(ssm_norm)[0]), "g_fin": bc(f(final_norm)),
            "w_in_ab": w_in_ab0, "cw": np.ascontiguousarray(f(conv_w)[0].reshape(3, 8, 128).transpose(2, 1, 0)),
            "w_out_ab": w_out_ab0, "w_in_c": np.ascontiguousarray(w_in_c_rolled),
            "w_glu": w_glu0, "w_out_c": w_out_c0,
            "bglu": np.ascontiguousarray(b_glu0.reshape(16, 128).T),
            "dsk": np.ascontiguousarray(d_skip0[512 * r:512 * (r + 1)].reshape(4, 128).T),
            "maskp": maskp, "masks": masks,
            "lre_s": st_lay(lam_re[gs]), "lim_s": st_lay(lam_im[gs]), "lst_s": st_lay(lst32),
            "lre_r": row_lay_rep(lam_re[gs]), "lim_r": row_lay_rep(lam_im[gs]), "lst_r": row_lay_rep(np.ascontiguousarray(lst32)),
            "bre_r": row_lay_b(b_re[gs]), "bim_r": row_lay_b(b_im[gs]),
            "cre_s": st_lay_c(c_re[gs]), "cim_s": st_lay_c(c_im[gs]),
            "rmask": rmask, "smask": smask, "sel": sel, "sre0": sre0, "sim0": sim0, "iota": iota,
        }
        in_maps.append({k: np.ascontiguousarray(v, dtype=np.float32) for k, v in m.items()})

    res = run_bass_kernel_spmd(nc, in_maps, core_ids=list(range(8)))
    R = res.results
    _NC_CACHE['raw'] = R
    y_prompt = np.zeros((2, SEQ, D), np.float32)
    y_sample = np.zeros((8, 8, D), np.float32)
    kp = np.zeros((1, 2, 2048, 8, 128), np.float32)
    vp = np.zeros((1, 2, 2048, 8, 128), np.float32)
    convp = np.zeros((1, 2, 2, 1024), np.float32)
    srp = np.zeros((1, 2, 128, 64), np.float32)
    sip = np.zeros((1, 2, 128, 64), np.float32)
    ks = np.zeros((1, 8, 8, 8, 128), np.float32)
    vs = np.zeros((1, 8, 8, 8, 128), np.float32)
    convs = np.zeros((1, 8, 2, 1024), np.float32)
    srs = np.zeros((1, 8, 128, 64), np.float32)
    sis = np.zeros((1, 8, 128, 64), np.float32)
    unst = lambda a: a.reshape(2, 64, 16).transpose(2, 0, 1).reshape(32, 64)
    for c in range(8):
        b, r = c // 4, c % 4
        o = R[c]
        y_prompt[b, r * NOWN:(r + 1) * NOWN] = o["yp"]
        y_sample[c] = o["ys"][:8]
        if r >= 2:
            kp[0, b, (r - 2) * NOWN:(r - 1) * NOWN] = o["kp"].reshape(NOWN, 8, 128)
            vp[0, b, (r - 2) * NOWN:(r - 1) * NOWN] = o["vp"].reshape(NOWN, 8, 128)
        if r == 3:
            convp[0, b] = o["convp"].transpose(2, 1, 0).reshape(2, 1024)
        ks[0, c] = o["ks"][:8].reshape(8, 8, 128)
        vs[0, c] = o["vs"][:8].reshape(8, 8, 128)
        convs[0, c] = o["convs"].transpose(2, 1, 0).reshape(2, 1024)
        srp[0, b, 32 * r:32 * (r + 1)] = unst(o["ssm_p"][:, 0, :])
        sip[0, b, 32 * r:32 * (r + 1)] = unst(o["ssm_p"][:, 1, :])
        for i in range(4):
            srs[0, 4 * b + i, 32 * r:32 * (r + 1)] = unst(o["ssm_s"][:, i, 0, :])
            sis[0, 4 * b + i, 32 * r:32 * (r + 1)] = unst(o["ssm_s"][:, i, 1, :])
    return (y_prompt, y_sample, kp, vp, convp, srp, sip, ks, vs, convs, srs, sis)
```

```python
import math
import os
STOP = os.environ.get('MK_STOP', '')
from contextlib import ExitStack

import numpy as np
import concourse.bass as bass
import concourse.mybir as mybir
from concourse.bass_utils import run_bass_kernel_spmd

F32 = mybir.dt.float32
BF16 = mybir.dt.bfloat16
ALU = mybir.AluOpType
AF = mybir.ActivationFunctionType
AX = mybir.AxisListType

ENGS = ["pe", "act", "dve", "pool", "sp"]
D = 2048
KT = 16
NOWN = 1024
NHALO = 2048
NTP = NOWN + NHALO
NTILE_P = NTP // 128
NO = NOWN + 128
SEQ = 4096
PAST = 16384
NCH = 1032
TWO_PI = 2.0 * math.pi


class Buf:
    def __init__(self, t, name):
        self.t = t
        self.name = name
        self.w = {}
        self.r = {}
        self.dsem = None
        self.dcnt = 0


class Prog:
    def __init__(self, nc, stack):
        self.nc = nc
        self.stack = stack
        self.q = {e: [] for e in ENGS}
        self.cnt = {e: 0 for e in ENGS}
        self.seen = {e: {} for e in ENGS}
        self.sems = {}
        self.semval = {}
        for e in ["pe", "act", "dve", "pool"]:
            self.sems[e] = stack.enter_context(nc.semaphore("s_" + e))
        self.off = 16512
        self.free = []
        self.dval = {}
        self.phase_bufs = []

    def sb(self, name, shape, dt, at=None):
        nbytes = int(np.prod(shape[1:])) * (2 if dt == BF16 else 4)
        if at is None:
            at = self.off
            self.off = (at + nbytes + 63) // 64 * 64
        assert at + nbytes <= 229300, (name, at, nbytes)
        t = self.nc.alloc_sbuf_tensor_at(name, list(shape), dt, offset=at)
        b = Buf(t, name)
        b.at = at
        b.nbytes = nbytes
        return b

    def ps(self, name, shape, dt=F32):
        t = self.stack.enter_context(self.nc.psum_tensor(name, list(shape), dt))
        return Buf(t, name)

    def dram(self, name, shape, dt, kind="Internal"):
        t = self.nc.dram_tensor(name, list(shape), dt, kind=kind)
        return Buf(t, name)

    def _need(self, eng, k, v, waits):
        if self.seen[eng].get(k, 0) >= v:
            return
        waits[k] = max(waits.get(k, 0), v)

    def _deps(self, eng, reads, writes):
        waits = {}
        for b in reads:
            for k, v in b.w.items():
                self._need(eng, k, v, waits)
        for b in writes:
            for k, v in b.w.items():
                self._need(eng, k, v, waits)
            for k, v in b.r.items():
                self._need(eng, k, v, waits)
        for k, v in waits.items():
            self.seen[eng][k] = v
        return [(self.sems[k], v) for k, v in waits.items()]

    def _commit(self, k, v, reads, writes):
        self.semval[k] = v
        for b in reads:
            b.r[k] = max(b.r.get(k, 0), v)
        for b in writes:
            b.w[k] = max(b.w.get(k, 0), v)
            b.r = {}

    def op(self, eng, fn, reads=(), writes=()):
        reads = [b for b in reads if b is not None]
        writes = [b for b in writes if b is not None]
        wl = self._deps(eng, reads, writes)
        self.cnt[eng] += 1
        sem = self.sems[eng]

        def emit(h, fn=fn, wl=wl, sem=sem):
            for s, v in wl:
                h.wait_ge(s, v)
            fn(h).then_inc(sem, 1)

        self.q[eng].append(emit)
        self._commit(eng, self.cnt[eng], reads, writes)

    def mm(self, fns, reads, writes):
        eng = "pe"
        wl = self._deps(eng, reads, writes)
        self.cnt[eng] += 1
        sem = self.sems[eng]

        def emit(h, fns=fns, wl=wl, sem=sem):
            for s, v in wl:
                h.wait_ge(s, v)
            for f in fns[:-1]:
                f(h)
            fns[-1](h).then_inc(sem, 1)

        self.q[eng].append(emit)
        self._commit(eng, self.cnt[eng], reads, writes)

    def dma(self, eng, out, in_, reads, writes, semb, **kw):
        reads = [b for b in reads if b is not None]
        writes = [b for b in writes if b is not None]
        if semb.dsem is None:
            if self.free:
                key = self.free.pop()
            else:
                key = "d%d" % len(self.sems)
                self.sems[key] = self.stack.enter_context(self.nc.semaphore(key))
            semb.dsem = key
            semb.dcnt = self.dval.get(key, 0)
            self.phase_bufs.append(semb)
        wl = self._deps(eng, reads, writes)
        semb.dcnt += 16
        self.dval[semb.dsem] = semb.dcnt
        sem = self.sems[semb.dsem]

        def emit(h, wl=wl, sem=sem, out=out, in_=in_, kw=kw):
            for s, v in wl:
                h.wait_ge(s, v)
            h.dma_start(out=out, in_=in_, **kw).then_inc(sem, 16)

        self.q[eng].append(emit)
        self._commit(semb.dsem, semb.dcnt, reads, writes)

    def coll(self, src, dst, groups):
        key = "c%d" % len(self.sems)
        self.sems[key] = self.stack.enter_context(self.nc.semaphore(key))
        wl = self._deps("pool", [src], [dst])
        sem = self.sems[key]

        def emit(h, wl=wl, sem=sem):
            for s, v in wl:
                h.wait_ge(s, v)
            h.collective_compute("AllGather", ALU.bypass, replica_groups=groups,
                                 ins=[src.t.ap()], outs=[dst.t.ap()]).then_inc(sem)
            h.wait_ge(sem, 1)

        self.q["pool"].append(emit)
        self.cnt["pool"] += 1
        s2 = self.sems["pool"]
        self.q["pool"].append(lambda h, s2=s2: h.engine_nop().then_inc(s2, 1))
        self._commit("pool", self.cnt["pool"], [src], [dst])

    def barrier(self):
        for b in self.phase_bufs:
            self.free.append(b.dsem)
            b.dsem = None
        self.phase_bufs = []
        items = list(self.semval.items())
        for e in ENGS:
            wl = []
            for k, v in items:
                if self.seen[e].get(k, 0) < v:
                    self.seen[e][k] = v
                    wl.append((self.sems[k], v))

            def emit(h, wl=wl):
                for s, v in wl:
                    h.wait_ge(s, v)

            if wl:
                self.q[e].append(emit)

    def run(self):
        nc = self.nc
        with nc.Block() as block:
            @block.tensor
            def _(h):
                for f in self.q["pe"]:
                    f(h)

            @block.scalar
            def _(h):
                for f in self.q["act"]:
                    f(h)

            @block.vector
            def _(h):
                for f in self.q["dve"]:
                    f(h)

            @block.gpsimd
            def _(h):
                for f in self.q["pool"]:
                    f(h)

            @block.sync
            def _(h):
                for f in self.q["sp"]:
                    f(h)


def mult_of(d):
    d = np.asarray(d)
    m = ((d >= 0) & (d <= 128)).astype(np.float32)
    m += ((d >= 0) & (d <= 512) & (d % 4 == 0))
    m += ((d >= 0) & (d <= 2048) & (d % 16 == 0))
    return m.astype(np.float32)


IN_SPECS = [
    ("xh", [NTP, D]), ("xs", [128, D]), ("cs", [128, 25, 64]), ("sn", [128, 25, 64]),
    ("valid", [128, 24]), ("ck", [2048, 1024]), ("cv", [2048, 1024]), ("sconv", [128, 8, 2]),
    ("g_attn", [128, D]), ("g_ssm", [128, D]), ("g_fin", [128, D]),
    ("w_in_ab", [D, 8192]), ("cw", [128, 8, 3]), ("w_out_ab", [D, D]), ("w_in_c", [D, 4096]),
    ("w_glu", [D, D]), ("w_out_c", [D, D]), ("bglu", [128, 16]), ("dsk", [128, 4]),
    ("maskp", [128, 20, 512]), ("masks", [128, 17, 128]),
    ("lre_s", [128, 16]), ("lim_s", [128, 16]), ("lst_s", [128, 16]),
    ("lre_r", [128, 4, 64]), ("lim_r", [128, 4, 64]), ("lst_r", [128, 4, 64]),
    ("bre_r", [128, 4, 64]), ("bim_r", [128, 4, 64]),
    ("cre_s", [128, 16, 16]), ("cim_s", [128, 16, 16]),
    ("rmask", [128, 8]), ("smask", [128, 2]), ("sel", [128, 4]),
    ("sre0", [128, 4, 16]), ("sim0", [128, 4, 16]), ("iota", [128, 1024]),
]
OUT_SPECS = [
    ("yp", [NOWN, D]), ("ys", [128, D]), ("kp", [NOWN, 1024]), ("vp", [NOWN, 1024]),
    ("convp", [128, 8, 2]), ("ssm_p", [128, 2, 16]), ("ks", [128, 1024]), ("vs", [128, 1024]),
    ("convs", [128, 8, 2]), ("ssm_s", [128, 4, 2, 16]),
]


def build_nc():
    nc = bass.Bass("TRN2", target_bir_lowering=False)
    IN = {}
    for n, s in IN_SPECS:
        IN[n] = nc.dram_tensor(n, s, F32, kind="ExternalInput")
    OUT = {}
    for n, s in OUT_SPECS:
        OUT[n] = nc.dram_tensor(n, s, F32, kind="ExternalOutput")
    st = ExitStack()
    with st:
        P = Prog(nc, st)
        build_program(nc, P, IN, OUT)
        P.run()
    return nc


def build_program(nc, P, IN, OUT):
    GROUPS = [[0, 1, 2, 3], [4, 5, 6, 7]]
    outbufs = {n: Buf(OUT[n], n) for n in OUT}
    kT_scr = P.dram("kT_scr", [8, 128, NTP], BF16)
    v_scr = P.dram("v_scr", [NTP, 1024], BF16)
    kTs_scr = P.dram("kTs_scr", [8, 128, 2176], BF16)
    vs_scr = P.dram("vs_scr", [2176, 1024], BF16)
    qT_scr = P.dram("qT_scr", [8, 128, NO], BF16)
    h1_scr = P.dram("h1_scr", [NO, D], F32)
    hn_src = [P.dram("hn_src%d" % j, [256, NCH], BF16) for j in range(8)]
    hn_dst = [P.dram("hn_dst%d" % j, [4 * 256, NCH], BF16) for j in range(8)]
    y_src = [P.dram("y_src%d" % j, [64, 4 * NCH], BF16) for j in range(8)]
    y_dst = [P.dram("y_dst%d" % j, [4 * 64, 4 * NCH], BF16) for j in range(8)]

    pf = [P.ps("pf%d" % i, [128, 512], F32) for i in range(6)]
    pb = [P.ps("pb%d" % i, [128, 8, 128], BF16) for i in range(2)]
    pfi = [0]
    pbi = [0]

    def next_pf():
        pfi[0] = (pfi[0] + 1) % 4
        return pf[pfi[0]]

    def next_pb():
        pbi[0] = (pbi[0] + 1) % 2
        return pb[pbi[0]]

    ident = P.sb("ident", [128, 128], BF16)
    P.op("pool", lambda h: h.memset(ident.t[:], 1.0), [], [ident])
    P.op("pool", lambda h: h.affine_select(ident.t[:], ident.t[:], [[-1, 128]], ALU.is_equal, 0.0,
                                            base=0, channel_multiplier=1), [ident], [ident])
    ones_bf = P.sb("ones_bf", [128, 128], BF16)
    P.op("pool", lambda h: h.memset(ones_bf.t[:], 1.0), [], [ones_bf])
    gt = P.sb("gt", [128, D], F32)
    cs = P.sb("cs", [128, 25, 64], F32)
    sn = P.sb("sn", [128, 25, 64], F32)
    valid = P.sb("valid", [128, 24], F32)
    validB = P.sb("validB", [128, 24, 128], BF16)
    cw = P.sb("cw", [128, 8, 3], F32)
    bglu = P.sb("bglu", [128, 16], F32)
    dsk = P.sb("dsk", [128, 4], F32)
    sel = P.sb("sel", [128, 4], F32)
    eps_t = P.sb("eps_t", [128, 1], F32)
    P.op("pool", lambda h: h.memset(eps_t.t[:], 1e-6), [], [eps_t])
    for b_, n in [(cs, "cs"), (sn, "sn"), (valid, "valid"), (cw, "cw"), (bglu, "bglu"), (dsk, "dsk"), (sel, "sel")]:
        P.dma("sp", b_.t[:], IN[n].ap(), [], [b_], b_)
    P.op("dve", lambda h: h.tensor_copy(validB.t[:], valid.t[:].unsqueeze(2).to_broadcast([128, 24, 128])),
         [valid], [validB])
    pospi = P.sb("pospi", [128, 1], F32)
    P.op("pool", lambda h: h.memset(pospi.t[:], math.pi), [], [pospi])
    CONST_END = P.off
    hnT_o = P.sb("hnT_o", [128, KT, NO], BF16)

    def load_g(name):
        P.dma("sp", gt.t[:], IN[name].ap(), [], [gt], gt)

    def rmsnorm_rows(xt, xn, ss, junk):
        P.op("act", lambda h: h.activation(junk.t[:], xt.t[:], AF.Square, accum_out=ss.t[:, 0:1]), [xt], [junk, ss])
        P.op("act", lambda h: h.activation(ss.t[:, 1:2], ss.t[:, 0:1], AF.Sqrt, bias=eps_t.t[:, 0:1], scale=1.0 / D), [ss, eps_t], [ss])
        P.op("dve", lambda h: h.reciprocal(ss.t[:, 2:3], ss.t[:, 1:2]), [ss], [ss])
        P.op("dve", lambda h: h.scalar_tensor_tensor(xn.t[:], xt.t[:], ss.t[:, 2:3], gt.t[:], ALU.mult, ALU.mult),
             [xt, ss, gt], [xn])

    def transpose_rows(xn, dst, dst_ap_fn):
        for half in range(2):
            p = next_pb()
            fns = []
            for j in range(8):
                kt = half * 8 + j
                fns.append(lambda h, p=p, j=j, kt=kt: h.transpose(p.t[:, j, :], xn.t[:, kt * 128:(kt + 1) * 128], ident.t[:]))
            P.mm(fns, [xn, ident], [p])
            P.op("act", lambda h, p=p, half=half: h.activation(dst_ap_fn(half), p.t[:], AF.Identity), [p], [dst])

    A0 = P.off
    wkv = P.sb("wkv", [128, KT, 2048], BF16)
    xts = [P.sb("xt%d" % i, [128, D], F32) for i in range(2)]
    xns = [P.sb("xn%d" % i, [128, D], BF16) for i in range(2)]
    hts = [P.sb("ht%d" % i, [128, KT, 128], BF16) for i in range(2)]
    junk = P.sb("junk", [128, D], F32)
    ss = P.sb("ss", [128, 4], F32)
    kr = P.sb("kr", [128, 1024], F32)
    vf = P.sb("vf", [128, 1024], F32)
    t1 = P.sb("t1", [128, 256], F32)
    t2 = P.sb("t2", [128, 256], F32)
    krb = P.sb("krb", [128, 1024], BF16)
    vb = P.sb("vb", [128, 1024], BF16)
    kTt = P.sb("kTt", [128, 8, 128], BF16)
    hprev2 = P.sb("hprev2", [128, KT, 2], BF16)
    A1_END = P.off

    load_g("g_attn")
    for half in range(2):
        P.dma("pool", wkv.t[:, :, half * 1024:(half + 1) * 1024],
              IN["w_in_ab"].ap()[:, 1024 + half * 1024:2048 + half * 1024].rearrange("(kt p) c -> p kt c", p=128),
              [], [wkv], wkv)

    def rotary(pk, ti, dst, c0):
        v = pk.t[:].rearrange("p (h two d) -> p h two d", h=4, two=2)
        o = dst.t[:, c0:c0 + 512].rearrange("p (h two d) -> p h two d", h=4, two=2)
        cb = cs.t[:, ti, :].unsqueeze(1).to_broadcast([128, 4, 64])
        sb_ = sn.t[:, ti, :].unsqueeze(1).to_broadcast([128, 4, 64])
        a = t1.t[:].rearrange("p (h d) -> p h d", h=4)
        b = t2.t[:].rearrange("p (h d) -> p h d", h=4)
        P.op("dve", lambda h: h.tensor_tensor(a, v[:, :, 0, :], cb, ALU.mult), [pk, cs], [t1])
        P.op("dve", lambda h: h.tensor_tensor(b, v[:, :, 1, :], sb_, ALU.mult), [pk, sn], [t2])
        P.op("dve", lambda h: h.tensor_tensor(o[:, :, 0, :], a, b, ALU.subtract), [t1, t2], [dst])
        P.op("dve", lambda h: h.tensor_tensor(a, v[:, :, 1, :], cb, ALU.mult), [pk, cs], [t1])
        P.op("dve", lambda h: h.tensor_tensor(b, v[:, :, 0, :], sb_, ALU.mult), [pk, sn], [t2])
        P.op("dve", lambda h: h.tensor_tensor(o[:, :, 1, :], a, b, ALU.add), [t1, t2], [dst])

    def store_kT(src_bf, scr, col0):
        p = next_pb()
        kb = kTt
        fns = [(lambda h, p=p, j=j: h.transpose(p.t[:, j, :], src_bf.t[:, j * 128:(j + 1) * 128], ident.t[:])) for j in range(8)]
        P.mm(fns, [src_bf, ident], [p])
        P.op("act", lambda h, p=p, kb=kb: h.activation(kb.t[:], p.t[:], AF.Identity), [p], [kb])
        P.dma("sp", scr.t.ap()[:, :, col0:col0 + 128].rearrange("h d t -> d h t"), kb.t[:], [kb], [scr], kb)

    for ti in range(25):
        xt = xts[ti % 2]
        xn = xns[ti % 2]
        src = IN["xh"].ap()[ti * 128:(ti + 1) * 128, :] if ti < 24 else IN["xs"].ap()
        P.dma("sp", xt.t[:], src, [], [xt], xt)
        rmsnorm_rows(xt, xn, ss, junk)
        if ti < 16:
            ht = hts[ti % 2]
            transpose_rows(xn, ht, lambda half, ht=ht: ht.t[:, half * 8:(half + 1) * 8, :])
            lhs = lambda kt, ht=ht: ht.t[:, kt, :]
            hb = ht
            if ti == 15:
                P.op("dve", lambda h, ht=ht: h.tensor_copy(hprev2.t[:], ht.t[:, :, 126:128]), [ht], [hprev2])
        else:
            o0 = (ti - 16) * 128
            transpose_rows(xn, hnT_o, lambda half, o0=o0: hnT_o.t[:, half * 8:(half + 1) * 8, o0:o0 + 128])
            lhs = lambda kt, o0=o0: hnT_o.t[:, kt, o0:o0 + 128]
            hb = hnT_o
        for g4 in range(4):
            pk = next_pf()
            fns = [(lambda h, pk=pk, kt=kt, g4=g4, lhs=lhs: h.matmul(pk.t[:], lhs(kt), wkv.t[:, kt, g4 * 512:(g4 + 1) * 512],
                                                                    start=(kt == 0), stop=(kt == KT - 1))) for kt in range(KT)]
            P.mm(fns, [hb, wkv], [pk])
            if g4 < 2:
                rotary(pk, ti, kr, g4 * 512)
            else:
                c0 = (g4 - 2) * 512
                P.op("act", lambda h, pk=pk, c0=c0: h.activation(vf.t[:, c0:c0 + 512], pk.t[:], AF.Identity), [pk], [vf])
        P.op("act", lambda h: h.activation(krb.t[:], kr.t[:], AF.Identity), [kr], [krb])
        if ti < 24:
            P.op("dve", lambda h, ti=ti: h.tensor_scalar(vb.t[:], vf.t[:], valid.t[:, ti:ti + 1], None, ALU.mult), [vf, valid], [vb])
            store_kT(krb, kT_scr, ti * 128)
            P.dma("sp", v_scr.t.ap()[ti * 128:(ti + 1) * 128, :], vb.t[:], [vb], [v_scr], vb)
            if ti >= 16:
                r0 = (ti - 16) * 128
                P.dma("sp", OUT["kp"].ap()[r0:r0 + 128, :], kr.t[:], [kr], [outbufs["kp"]], kr)
                P.dma("sp", OUT["vp"].ap()[r0:r0 + 128, :], vf.t[:], [vf], [outbufs["vp"]], vf)
        else:
            P.op("dve", lambda h: h.tensor_copy(vb.t[:], vf.t[:]), [vf], [vb])
            store_kT(krb, kTs_scr, 2048)
            P.dma("sp", vs_scr.t.ap()[2048:2176, :], vb.t[:], [vb], [vs_scr], vb)
            P.dma("sp", OUT["ks"].ap(), kr.t[:], [kr], [outbufs["ks"]], kr)
            P.dma("sp", OUT["vs"].ap(), vf.t[:], [vf], [outbufs["vs"]], vf)
    for ti in range(16):
        xt = xts[ti % 2]
        P.dma("sp", xt.t[:, 0:1024], IN["ck"].ap()[ti * 128:(ti + 1) * 128, :], [], [xt], xt)
        P.dma("sp", xt.t[:, 1024:2048], IN["cv"].ap()[ti * 128:(ti + 1) * 128, :], [], [xt], xt)
        P.op("act", lambda h, xt=xt: h.activation(krb.t[:], xt.t[:, 0:1024], AF.Identity), [xt], [krb])
        P.op("dve", lambda h, xt=xt: h.tensor_copy(vb.t[:], xt.t[:, 1024:2048]), [xt], [vb])
        store_kT(krb, kTs_scr, ti * 128)
        P.dma("sp", vs_scr.t.ap()[ti * 128:(ti + 1) * 128, :], vb.t[:], [vb], [vs_scr], vb)

    if STOP == 'A1':
        P.barrier()
        return
    P.barrier()
    P.off = A0
    wq = P.sb("wq", [128, KT, 1024], BF16)
    hprev2b = P.sb("hprev2b", [128, KT, 2], BF16)
    qf = P.sb("qf", [128, 1024], F32)
    qb = P.sb("qb", [128, 1024], BF16)
    t1 = P.sb("t1b", [128, 256], F32)
    t2 = P.sb("t2b", [128, 256], F32)
    kTt = P.sb("kTtb", [128, 8, 128], BF16)
    hprev2k = P.sb("hprev2k", [128, KT, 2], BF16, at=hprev2.at)
    hprev2k.w = dict(hprev2.w)
    P.dma("pool", wq.t[:], IN["w_in_ab"].ap()[:, 0:1024].rearrange("(kt p) c -> p kt c", p=128), [], [wq], wq)
    for tj in range(9):
        ti = 16 + tj
        o0 = tj * 128
        for g2_ in range(2):
            pk = next_pf()
            fns = [(lambda h, pk=pk, kt=kt, g2_=g2_, o0=o0: h.matmul(pk.t[:], hnT_o.t[:, kt, o0:o0 + 128],
                                                                      wq.t[:, kt, g2_ * 512:(g2_ + 1) * 512],
                                                                      start=(kt == 0), stop=(kt == KT - 1))) for kt in range(KT)]
            P.mm(fns, [hnT_o, wq], [pk])
            rotary(pk, ti, qf, g2_ * 512)
        P.op("act", lambda h: h.activation(qb.t[:], qf.t[:], AF.Identity), [qf], [qb])
        store_kT(qb, qT_scr, o0)

    if STOP == 'A2':
        P.barrier()
        return
    P.barrier()
    P.off = A0
    ocat = P.sb("ocat", [128, KT, NO], BF16)
    hp2 = P.sb("hp2", [128, KT, 2], BF16)
    B0 = P.off
    P.op("dve", lambda h: h.tensor_copy(hp2.t[:], hprev2k.t[:]), [hprev2k], [hp2])
    P.barrier()
    maskp = P.sb("maskp", [128, 20, 512], BF16)
    masks_ = P.sb("masks_", [128, 17, 128], BF16)
    P.dma("pool", maskp.t[:], IN["maskp"].ap(), [], [maskp], maskp)
    P.dma("pool", masks_.t[:], IN["masks"].ap(), [], [masks_], masks_)
    kTh = [P.sb("kTh%d" % i, [128, NTP], BF16) for i in range(1)] * 2
    vh = [P.sb("vh%d" % i, [128, 24, 128], BF16) for i in range(1)] * 2
    kTsh = [P.sb("kTsh%d" % i, [128, 2176], BF16) for i in range(1)] * 2
    vsh = [P.sb("vsh%d" % i, [128, 17, 128], BF16) for i in range(1)] * 2
    qTh = [P.sb("qTh%d" % i, [128, NO], BF16) for i in range(1)] * 2
    wt = [P.sb("wt%d" % i, [128, KT, 128], BF16) for i in range(4)]
    pts = [P.sb("pt%d" % i, [128, 512], BF16) for i in range(3)]
    ptm = [P.sb("ptm%d" % i, [128, 512], BF16) for i in range(3)]
    za = P.sb("za", [128, NO], F32)
    rl = P.sb("rl", [128, 512], F32)
    of = P.sb("of", [128, 512], F32)
    sg = P.sb("sg", [128, 512], F32)

    def silu_evac(pk, dstb, dst_ap, n):
        P.op("act", lambda h: h.activation(sg.t[:, 0:n], pk.t[:, 0:n], AF.Exp, scale=-1.0), [pk], [sg])
        P.op("dve", lambda h: h.tensor_scalar(sg.t[:, 0:n], sg.t[:, 0:n], 1.0, None, ALU.add), [sg], [sg])
        P.op("dve", lambda h: h.reciprocal(sg.t[:, 0:n], sg.t[:, 0:n]), [sg], [sg])
        P.op("dve", lambda h: h.tensor_tensor(dst_ap, pk.t[:, 0:n], sg.t[:, 0:n], ALU.mult), [pk, sg], [dstb])
    fb = [P.sb("fb%d" % i, [128, NO + 2], F32) for i in range(4)]
    convo_p = P.sb("convo_p", [128, 8, 2], F32)
    convo_s = P.sb("convo_s", [128, 8, 2], F32)
    sconv = P.sb("sconv", [128, 8, 2], F32)
    P.dma("sp", sconv.t[:], IN["sconv"].ap(), [], [sconv], sconv)
    scale = 128.0 ** -0.5

    def load_wt(i, c0):
        P.dma("pool", wt[i].t[:], IN["w_in_ab"].ap()[:, c0:c0 + 128].rearrange("(kt p) c -> p kt c", p=128), [], [wt[i]], wt[i])

    def proj_feat(wb, dst_ap_fn, evac, rhs_buf, rhs_fn, n):
        pk = next_pf()
        fns = [(lambda h, pk=pk, kt=kt: h.matmul(pk.t[:, 0:n], wb.t[:, kt, :], rhs_fn(kt), start=(kt == 0), stop=(kt == KT - 1)))
               for kt in range(KT)]
        P.mm(fns, [wb, rhs_buf], [pk])
        evac(pk)

    def attention(hh, qT, q0, nq, kT, vt, ktiles, mask_fn, vB_fn, o_dst_fn, zcol0):
        po = pf[4]
        pl = pf[5]
        nk = len(ktiles)
        for i, kt_ in enumerate(ktiles):
            ps_ = next_pf()
            P.mm([lambda h, ps_=ps_, kt_=kt_: h.matmul(ps_.t[:, 0:nq], kT.t[:, kt_ * 128:(kt_ + 1) * 128], qT.t[:, q0:q0 + nq],
                                                        start=True, stop=True)], [kT, qT], [ps_])
            pe_ = pts[i % 3]
            pm_ = ptm[i % 3]
            P.op("act", lambda h, ps_=ps_, pe_=pe_: h.activation(pe_.t[:, 0:nq], ps_.t[:, 0:nq], AF.Exp, scale=scale), [ps_], [pe_])
            mk, mb = mask_fn(i)
            eng = "dve" if i % 2 == 0 else "pool"
            P.op(eng, lambda h, pe_=pe_, pm_=pm_, mk=mk: h.tensor_tensor(pm_.t[:, 0:nq], pe_.t[:, 0:nq], mk, ALU.mult), [pe_, mb], [pm_])
            vB, vBb = vB_fn(i)
            P.mm([lambda h, po=po, pm_=pm_, kt_=kt_, i=i: h.matmul(po.t[:, 0:nq], vt.t[:, kt_, :], pm_.t[:, 0:nq], start=(i == 0), stop=(i == nk - 1)),
                  lambda h, pl=pl, pm_=pm_, vB=vB, i=i: h.matmul(pl.t[:, 0:nq], vB, pm_.t[:, 0:nq], start=(i == 0), stop=(i == nk - 1))],
                 [vt, pm_, vBb], [po, pl])
        P.op("dve", lambda h: h.reciprocal(rl.t[:, 0:nq], pl.t[:, 0:nq]), [pl], [rl])
        P.op("dve", lambda h: h.tensor_tensor(of.t[:, 0:nq], po.t[:, 0:nq], rl.t[:, 0:nq], ALU.mult), [po, rl], [of])
        P.op("dve", lambda h: h.tensor_tensor(o_dst_fn(), of.t[:, 0:nq], za.t[:, zcol0:zcol0 + nq], ALU.mult), [of, za], [ocat])

    for hh in range(8):
        b2 = hh % 2
        P.dma("sp", kTh[b2].t[:], kT_scr.t.ap()[hh], [kT_scr], [kTh[b2]], kTh[b2])
        P.dma("sp", vh[b2].t[:], v_scr.t.ap()[:, hh * 128:(hh + 1) * 128].rearrange("(t p) d -> p t d", p=128), [v_scr], [vh[b2]], vh[b2])
        P.dma("sp", kTsh[b2].t[:], kTs_scr.t.ap()[hh], [kTs_scr], [kTsh[b2]], kTsh[b2])
        P.dma("sp", vsh[b2].t[:], vs_scr.t.ap()[:, hh * 128:(hh + 1) * 128].rearrange("(t p) d -> p t d", p=128), [vs_scr], [vsh[b2]], vsh[b2])
        P.dma("sp", qTh[b2].t[:], qT_scr.t.ap()[hh], [qT_scr], [qTh[b2]], qTh[b2])
        load_wt(0, 3072 + hh * 128)
        for (c0, n) in [(0, 512), (512, 512), (1024, 128)]:
            proj_feat(wt[0], None, lambda pk, c0=c0, n=n: silu_evac(pk, za, za.t[:, c0:c0 + n], n),
                      hnT_o, lambda kt, c0=c0, n=n: hnT_o.t[:, kt, c0:c0 + n], n)
        for qc in range(2):
            kts = list(range(4 * qc, 4 * qc + 20))
            attention(hh, qTh[b2], qc * 512, 512, kTh[b2], vh[b2], kts,
                      lambda i: (maskp.t[:, i, :], maskp),
                      lambda i, kts=kts: (validB.t[:, kts[i], :], validB),
                      lambda qc=qc, hh=hh: ocat.t[:, hh, qc * 512:(qc + 1) * 512], qc * 512)
        attention(hh, qTh[b2], 1024, 128, kTsh[b2], vsh[b2], list(range(17)),
                  lambda i: (masks_.t[:, i, :], masks_),
                  lambda i: (ones_bf.t[:], ones_bf),
                  lambda hh=hh: ocat.t[:, hh, 1024:1152], 1024)

    for cc in range(8):
        for j, base in enumerate([4096, 5120, 6144, 7168]):
            load_wt(j, base + cc * 128)
        bb, cb_, hb_, zb = fb
        for j, dstb in enumerate(fb):
            for (c0, n) in [(0, 512), (512, 512), (1024, 128)]:
                if j == 3:
                    ev = lambda pk, c0=c0, n=n, dstb=dstb: silu_evac(pk, dstb, dstb.t[:, 2 + c0:2 + c0 + n], n)
                else:
                    ev = lambda pk, c0=c0, n=n, dstb=dstb: P.op("act", lambda h: h.activation(dstb.t[:, 2 + c0:2 + c0 + n], pk.t[:, 0:n], AF.Identity), [pk], [dstb])
                proj_feat(wt[j], None, ev, hnT_o, lambda kt, c0=c0, n=n: hnT_o.t[:, kt, c0:c0 + n], n)
            if j in (1, 2):
                proj_feat(wt[j], None, lambda pk, dstb=dstb: P.op("act", lambda h: h.activation(dstb.t[:, 0:2], pk.t[:, 0:2], AF.Identity), [pk], [dstb]),
                          hp2, lambda kt: hp2.t[:, kt, :], 2)
        P.op("dve", lambda h: h.tensor_tensor(cb_.t[:], cb_.t[:], hb_.t[:], ALU.mult), [cb_, hb_], [cb_])
        w0 = cw.t[:, cc, 0:1]
        w1 = cw.t[:, cc, 1:2]
        w2 = cw.t[:, cc, 2:3]
        P.op("dve", lambda h, w2=w2: h.tensor_scalar(hb_.t[:, 2:1026], cb_.t[:, 2:1026], w2, None, ALU.mult), [cb_, cw], [hb_])
        P.op("dve", lambda h, w1=w1: h.scalar_tensor_tensor(hb_.t[:, 2:1026], cb_.t[:, 1:1025], w1, hb_.t[:, 2:1026], ALU.mult, ALU.add), [cb_, cw, hb_], [hb_])
        P.op("dve", lambda h, w0=w0: h.scalar_tensor_tensor(hb_.t[:, 2:1026], cb_.t[:, 0:1024], w0, hb_.t[:, 2:1026], ALU.mult, ALU.add), [cb_, cw, hb_], [hb_])
        P.op("dve", lambda h, cc=cc: h.tensor_copy(convo_p.t[:, cc, :], cb_.t[:, 1024:1026]), [cb_], [convo_p])
        P.op("dve", lambda h, cc=cc: h.tensor_copy(cb_.t[:, 1024:1026], sconv.t[:, cc, :]), [sconv], [cb_])
        P.op("dve", lambda h, w2=w2: h.tensor_scalar(hb_.t[:, 1026:1034], cb_.t[:, 1026:1034], w2, None, ALU.mult), [cb_, cw], [hb_])
        P.op("dve", lambda h, w1=w1: h.scalar_tensor_tensor(hb_.t[:, 1026:1034], cb_.t[:, 1025:1033], w1, hb_.t[:, 1026:1034], ALU.mult, ALU.add), [cb_, cw, hb_], [hb_])
        P.op("dve", lambda h, w0=w0: h.scalar_tensor_tensor(hb_.t[:, 1026:1034], cb_.t[:, 1024:1032], w0, hb_.t[:, 1026:1034], ALU.mult, ALU.add), [cb_, cw, hb_], [hb_])
        P.op("dve", lambda h, cc=cc: h.tensor_copy(convo_s.t[:, cc, :], cb_.t[:, 1032:1034]), [cb_], [convo_s])
        P.op("dve", lambda h: h.tensor_tensor(hb_.t[:, 2:1034], hb_.t[:, 2:1034], bb.t[:, 2:1034], ALU.mult), [hb_, bb], [hb_])
        P.op("dve", lambda h, cc=cc: h.tensor_tensor(ocat.t[:, 8 + cc, 0:1032], hb_.t[:, 2:1034], zb.t[:, 2:1034], ALU.mult), [hb_, zb], [ocat])
        P.op("dve", lambda h, cc=cc: h.memset(ocat.t[:, 8 + cc, 1032:1152], 0.0), [], [ocat])
    P.dma("sp", OUT["convp"].ap(), convo_p.t[:], [convo_p], [outbufs["convp"]], convo_p)
    P.dma("sp", OUT["convs"].ap(), convo_s.t[:], [convo_s], [outbufs["convs"]], convo_s)

    if STOP == 'B':
        P.barrier()
        return
    P.barrier()
    hn1T = P.sb("hn1T", [128, KT, NCH], BF16, at=hnT_o.at)
    P.off = B0
    wgb = [P.sb("wg%d" % i, [128, KT, 512], BF16) for i in range(2)]
    h1s = [P.sb("h1s%d" % i, [128, D], F32) for i in range(5)]
    xn1 = [P.sb("xn1%d" % i, [128, D], BF16) for i in range(2)]
    junk = P.sb("junk2", [128, D], F32)
    ss = P.sb("ss2", [128, 4], F32)
    tmpT = P.sb("tmpT", [128, KT, 128], BF16)
    load_g("g_ssm")
    wi = 0
    for tiles in [list(range(0, 5)), list(range(5, 9))]:
        for si, tj in enumerate(tiles):
            o0 = tj * 128
            src = IN["xh"].ap()[NHALO + o0:NHALO + o0 + 128, :] if tj < 8 else IN["xs"].ap()
            P.dma("sp", h1s[si].t[:], src, [], [h1s[si]], h1s[si])
        for g4 in range(4):
            wg = wgb[wi % 2]
            wi += 1
            P.dma("pool", wg.t[:], IN["w_out_ab"].ap()[:, g4 * 512:(g4 + 1) * 512].rearrange("(kt p) c -> p kt c", p=128), [], [wg], wg)
            for si, tj in enumerate(tiles):
                o0 = tj * 128
                ht_ = h1s[si]
                pk = next_pf()
                fns = [(lambda h, pk=pk, kt=kt, o0=o0, wg=wg: h.matmul(pk.t[:], ocat.t[:, kt, o0:o0 + 128], wg.t[:, kt, :],
                                                                         start=(kt == 0), stop=(kt == KT - 1))) for kt in range(KT)]
                P.mm(fns, [ocat, wg], [pk])
                P.op("dve", lambda h, pk=pk, ht_=ht_, g4=g4: h.tensor_tensor(ht_.t[:, g4 * 512:(g4 + 1) * 512], ht_.t[:, g4 * 512:(g4 + 1) * 512], pk.t[:], ALU.add),
                     [pk, ht_], [ht_])
        for si, tj in enumerate(tiles):
            o0 = tj * 128
            ht_ = h1s[si]
            P.dma("sp", h1_scr.t.ap()[o0:o0 + 128, :], ht_.t[:], [ht_], [h1_scr], ht_)
            if STOP == 'C1':
                if tj < 8:
                    P.dma("sp", OUT["yp"].ap()[o0:o0 + 128, :], ht_.t[:], [ht_], [outbufs["yp"]], ht_)
                else:
                    P.dma("sp", OUT["ys"].ap(), ht_.t[:], [ht_], [outbufs["ys"]], ht_)
            xn = xn1[tj % 2]
            rmsnorm_rows(ht_, xn, ss, junk)
            if tj < 8:
                transpose_rows(xn, hn1T, lambda half, o0=o0: hn1T.t[:, half * 8:(half + 1) * 8, o0:o0 + 128])
            else:
                transpose_rows(xn, tmpT, lambda half: tmpT.t[:, half * 8:(half + 1) * 8, :])
                P.op("dve", lambda h: h.tensor_copy(hn1T.t[:, :, 1024:1032], tmpT.t[:, :, 0:8]), [tmpT], [hn1T])
    for j in range(8):
        P.dma("sp", hn_src[j].t.ap().rearrange("(k p) t -> p k t", p=128), hn1T.t[:, 2 * j:2 * j + 2, :], [hn1T], [hn_src[j]], hn1T)
        P.coll(hn_src[j], hn_dst[j], GROUPS)

    if STOP == 'C1':
        P.barrier()
        return
    P.barrier()
    P.off = CONST_END
    uT = P.sb("uT", [128, 4, 4 * NCH], BF16)
    ygT = P.sb("ygT", [128, 4, 4 * NCH], BF16)
    L1 = P.off
    wu = P.sb("wu", [128, KT, 512], BF16)
    hch = [P.sb("hch%d" % i, [128, KT, 516], BF16) for i in range(2)]
    P.dma("pool", wu.t[:], IN["w_in_c"].ap()[:, 0:512].rearrange("(kt p) c -> p kt c", p=128), [], [wu], wu)
    ci = 0
    for r in range(4):
        for hf in range(2):
            hc = hch[ci % 2]
            ci += 1
            c0 = hf * 516
            for j in range(8):
                P.dma("sp", hc.t[:, 2 * j:2 * j + 2, :], hn_dst[j].t.ap()[r * 256:(r + 1) * 256, c0:c0 + 516].rearrange("(k p) t -> p k t", p=128),
                      [hn_dst[j]], [hc], hc)
            for ft in range(4):
                pk = next_pf()
                fns = [(lambda h, pk=pk, kt=kt, ft=ft, hc=hc: h.matmul(pk.t[:, 0:512], wu.t[:, kt, ft * 128:(ft + 1) * 128], hc.t[:, kt, 0:512],
                                                                      start=(kt == 0), stop=(kt == KT - 1))) for kt in range(KT)]
                P.mm(fns, [wu, hc], [pk])
                P.op("act", lambda h, pk=pk, ft=ft, r=r, c0=c0: h.activation(uT.t[:, ft, r * NCH + c0:r * NCH + c0 + 512], pk.t[:, 0:512], AF.Identity), [pk], [uT])
                pk2 = next_pf()
                fns2 = [(lambda h, pk2=pk2, kt=kt, ft=ft, hc=hc: h.matmul(pk2.t[:, 0:4], wu.t[:, kt, ft * 128:(ft + 1) * 128], hc.t[:, kt, 512:516],
                                                                         start=(kt == 0), stop=(kt == KT - 1))) for kt in range(KT)]
                P.mm(fns2, [wu, hc], [pk2])
                P.op("act", lambda h, pk2=pk2, ft=ft, r=r, c0=c0: h.activation(uT.t[:, ft, r * NCH + c0 + 512:r * NCH + c0 + 516], pk2.t[:, 0:4], AF.Identity), [pk2], [uT])

    if STOP == 'U':
        P.barrier()
        return
    P.barrier()
    P.off = L1
    def small(name, shape, dt=F32):
        return P.sb(name, shape, dt)
    lre_s = small("lre_s", [128, 16]); lim_s = small("lim_s", [128, 16]); lst_s = small("lst_s", [128, 16])
    lre_r = small("lre_r", [128, 256]); lim_r = small("lim_r", [128, 256]); lst_r = small("lst_r", [128, 256])
    bre_r = small("bre_r", [128, 256]); bim_r = small("bim_r", [128, 256])
    cre_s = small("cre_s", [128, 16, 16]); cim_s = small("cim_s", [128, 16, 16])
    rmask = small("rmask", [128, 8]); smask = small("smask", [128, 2])
    sre0 = small("sre0", [128, 4, 16]); sim0 = small("sim0", [128, 4, 16])
    iota = small("iota", [128, 1024])
    for b_, n in [(lre_s, "lre_s"), (lim_s, "lim_s"), (lst_s, "lst_s"), (cre_s, "cre_s"), (cim_s, "cim_s"),
                  (rmask, "rmask"), (smask, "smask"), (sre0, "sre0"), (sim0, "sim0"), (iota, "iota")]:
        P.dma("sp", b_.t[:], IN[n].ap(), [], [b_], b_)
    for b_, n in [(lre_r, "lre_r"), (lim_r, "lim_r"), (lst_r, "lst_r"), (bre_r, "bre_r"), (bim_r, "bim_r")]:
        P.dma("sp", b_.t[:], IN[n].ap().rearrange("p a b -> p (a b)"), [], [b_], b_)
    negpi = small("negpi", [128, 1])
    P.op("dve", lambda h: h.memset(negpi.t[:], -math.pi), [], [negpi])

    I32 = mybir.dt.int32
    tq = small("tq", [128, 1024]); tiq = small("tiq", [128, 1024], I32)
    halfpi = small("halfpi", [128, 1]); zero_t = small("zero_t", [128, 1])
    P.op("dve", lambda h: h.memset(halfpi.t[:], 0.5 * math.pi), [], [halfpi])
    P.op("dve", lambda h: h.memset(zero_t.t[:], 0.0), [], [zero_t])

    def sincos(ang_ap, n, s_ap, c_ap, rd, wr):
        for (dst, addc, bt, lo, hi) in [(s_ap, 0.0, zero_t, -math.pi, math.pi), (c_ap, 0.25, halfpi, -1.5 * math.pi, 0.5 * math.pi)]:
            P.op("dve", lambda h, addc=addc: h.tensor_scalar(tq.t[:, 0:n], ang_ap, 1.0 / TWO_PI, addc, ALU.mult, ALU.add), rd, [tq])
            P.op("dve", lambda h: h.tensor_copy(tiq.t[:, 0:n], tq.t[:, 0:n]), [tq], [tiq])
            P.op("dve", lambda h: h.tensor_copy(tq.t[:, 0:n], tiq.t[:, 0:n]), [tiq], [tq])
            P.op("dve", lambda h: h.scalar_tensor_tensor(tq.t[:, 0:n], tq.t[:, 0:n], -TWO_PI, ang_ap, ALU.mult, ALU.add), [tq] + rd, [tq])
            P.op("dve", lambda h, lo=lo, hi=hi: h.tensor_scalar(tq.t[:, 0:n], tq.t[:, 0:n], lo, hi, ALU.max, ALU.min), [tq], [tq])
            P.op("act", lambda h, dst=dst, bt=bt: h.activation(dst, tq.t[:, 0:n], AF.Sin, bias=bt.t[:, 0:1], scale=1.0), [tq, bt], wr)

    def disc(lre, lim, lst, n, pref):
        o = {}
        for nm in ["step", "mag", "th", "c", "s", "tmp", "nr", "den", "cr", "ci", "a", "b"]:
            o[nm] = small(pref + nm, [128, n])
        al = [o[k] for k in o] + [lre, lim, lst]
        P.op("act", lambda h: h.activation(o["step"].t[:], lst.t[:, 0:n], AF.Exp), al, al)
        P.op("dve", lambda h: h.tensor_tensor(o["th"].t[:], lim.t[:, 0:n], o["step"].t[:], ALU.mult), al, al)
        P.op("dve", lambda h: h.tensor_tensor(o["a"].t[:], lre.t[:, 0:n], o["step"].t[:], ALU.mult), al, al)
        P.op("act", lambda h: h.activation(o["mag"].t[:], o["a"].t[:], AF.Exp), al, al)
        sincos(o["th"].t[:], n, o["s"].t[:], o["c"].t[:], al, al)
        P.op("dve", lambda h: h.tensor_tensor(o["a"].t[:], o["mag"].t[:], o["c"].t[:], ALU.mult), al, al)
        P.op("dve", lambda h: h.tensor_scalar(o["nr"].t[:], o["a"].t[:], 1.0, -1.0, ALU.mult, ALU.add), al, al)
        P.op("dve", lambda h: h.tensor_tensor(o["b"].t[:], o["mag"].t[:], o["s"].t[:], ALU.mult), al, al)
        P.op("dve", lambda h: h.tensor_tensor(o["den"].t[:], lre.t[:, 0:n], lre.t[:, 0:n], ALU.mult), al, al)
        P.op("dve", lambda h: h.tensor_tensor(o["tmp"].t[:], lim.t[:, 0:n], lim.t[:, 0:n], ALU.mult), al, al)
        P.op("dve", lambda h: h.tensor_tensor(o["den"].t[:], o["den"].t[:], o["tmp"].t[:], ALU.add), al, al)
        P.op("dve", lambda h: h.reciprocal(o["den"].t[:], o["den"].t[:]), al, al)
        P.op("dve", lambda h: h.tensor_tensor(o["cr"].t[:], o["nr"].t[:], lre.t[:, 0:n], ALU.mult), al, al)
        P.op("dve", lambda h: h.tensor_tensor(o["tmp"].t[:], o["b"].t[:], lim.t[:, 0:n], ALU.mult), al, al)
        P.op("dve", lambda h: h.tensor_tensor(o["cr"].t[:], o["cr"].t[:], o["tmp"].t[:], ALU.add), al, al)
        P.op("dve", lambda h: h.tensor_tensor(o["cr"].t[:], o["cr"].t[:], o["den"].t[:], ALU.mult), al, al)
        P.op("dve", lambda h: h.tensor_tensor(o["ci"].t[:], o["b"].t[:], lre.t[:, 0:n], ALU.mult), al, al)
        P.op("dve", lambda h: h.tensor_tensor(o["tmp"].t[:], o["nr"].t[:], lim.t[:, 0:n], ALU.mult), al, al)
        P.op("dve", lambda h: h.tensor_tensor(o["ci"].t[:], o["ci"].t[:], o["tmp"].t[:], ALU.subtract), al, al)
        P.op("dve", lambda h: h.tensor_tensor(o["ci"].t[:], o["ci"].t[:], o["den"].t[:], ALU.mult), al, al)
        return o, al

    ds_, als = disc(lre_s, lim_s, lst_s, 16, "ds_")
    dr_, alr = disc(lre_r, lim_r, lst_r, 256, "dr_")
    bbr = small("bbr", [128, 256]); bbi = small("bbi", [128, 256]); tmpr = small("tmpr", [128, 256])
    alr2 = alr + [bbr, bbi, tmpr, bre_r, bim_r]
    P.op("dve", lambda h: h.tensor_tensor(bbr.t[:], dr_["cr"].t[:], bre_r.t[:], ALU.mult), alr2, alr2)
    P.op("dve", lambda h: h.tensor_tensor(tmpr.t[:], dr_["ci"].t[:], bim_r.t[:], ALU.mult), alr2, alr2)
    P.op("dve", lambda h: h.tensor_tensor(bbr.t[:], bbr.t[:], tmpr.t[:], ALU.subtract), alr2, alr2)
    P.op("dve", lambda h: h.tensor_tensor(bbi.t[:], dr_["cr"].t[:], bim_r.t[:], ALU.mult), alr2, alr2)
    P.op("dve", lambda h: h.tensor_tensor(tmpr.t[:], dr_["ci"].t[:], bre_r.t[:], ALU.mult), alr2, alr2)
    P.op("dve", lambda h: h.tensor_tensor(bbi.t[:], bbi.t[:], tmpr.t[:], ALU.add), alr2, alr2)
    BbT = [small("BbT%d" % ri, [128, 16, 128], BF16) for ri in range(2)]
    for ri, src in enumerate([bbr, bbi]):
        for qq in range(4):
            for g2 in range(2):
                m = rmask.t[:, qq * 2 + g2:qq * 2 + g2 + 1]
                o_ap = BbT[ri].t[:].rearrange("p (ft q) c -> p ft q c", q=4)[:, :, qq, g2 * 64:(g2 + 1) * 64]
                i_ap = src.t[:].rearrange("p (ft d) -> p ft d", ft=4)
                P.op("dve", lambda h, o_ap=o_ap, i_ap=i_ap, m=m: h.tensor_scalar(o_ap, i_ap, m, None, ALU.mult), alr2 + [rmask], [BbT[ri]])
    CT = [small("CT%d" % ri, [128, 16, 128], BF16) for ri in range(2)]
    for ri in range(2):
        P.op("dve", lambda h, ri=ri: h.memset(CT[ri].t[:], 0.0), [], [CT[ri]])
    for ri, (src, sgn) in enumerate([(cre_s, 1.0), (cim_s, -1.0)]):
        for pair in range(16):
            qq = pair % 4
            for g2 in range(2):
                m = smask.t[:, g2:g2 + 1]
                col = qq * 32 + g2 * 16
                P.op("dve", lambda h, ri=ri, pair=pair, col=col, m=m, src=src, sgn=sgn: h.tensor_scalar(
                    CT[ri].t[:, pair, col:col + 16], src.t[:, pair, :], m, sgn, ALU.mult, ALU.mult), [src, smask], [CT[ri]])
    rr = ds_["mag"]; th = ds_["th"]
    cth = small("cth", [128, 16]); sth = small("sth", [128, 16])
    P.op("dve", lambda h: h.tensor_copy(cth.t[:], ds_["c"].t[:]), als, [cth])
    P.op("dve", lambda h: h.tensor_copy(sth.t[:], ds_["s"].t[:]), als, [sth])

    tabc = [small("tabc%d" % i, [128, 1024]) for i in range(4)]
    tabs = [small("tabs%d" % i, [128, 1024]) for i in range(4)]
    gr = small("gr", [128, 1024]); gi = small("gi", [128, 1024])
    ttmp = gi; ang = gr
    yr = small("yr", [128, 1024]); yi = small("yi", [128, 1024])
    hrb = small("hrb", [128, 1024], BF16); hib = small("hib", [128, 1024], BF16)
    ytmp = small("ytmp", [128, 512]); ysq = small("ysq", [128, 512])
    stp = small("stp", [128, 16, 2])
    sts = small("sts", [128, 4, 16, 2])
    gin = small("gin", [128, 2]); hend = small("hend", [128, 2])
    P.op("dve", lambda h: h.memset(stp.t[:], 0.0), [], [stp])
    GC = 1.5957691216057308

    def run_seq(pair, qq, ft, col0, T, tc_, ts_, init_re, init_im, init_bufs, out_re, out_im, out_buf, ypsum, yp_off):
        for h0 in range(0, T, 512):
            n = min(512, T - h0)
            pxr = next_pf(); pxi = next_pf()
            P.mm([lambda h, pxr=pxr, n=n, h0=h0: h.matmul(pxr.t[:, 0:n], BbT[0].t[:, pair, :], uT.t[:, ft, col0 + h0:col0 + h0 + n], start=True, stop=True)], [BbT[0], uT], [pxr])
            P.mm([lambda h, pxi=pxi, n=n, h0=h0: h.matmul(pxi.t[:, 0:n], BbT[1].t[:, pair, :], uT.t[:, ft, col0 + h0:col0 + h0 + n], start=True, stop=True)], [BbT[1], uT], [pxi])
            c_ = tc_.t[:, h0:h0 + n]; s_ = ts_.t[:, h0:h0 + n]
            P.op("dve", lambda h, pxr=pxr, c_=c_, n=n, h0=h0: h.tensor_tensor(yr.t[:, h0:h0 + n], pxr.t[:, 0:n], c_, ALU.mult), [pxr, tc_], [yr])
            P.op("dve", lambda h, pxi=pxi, s_=s_, n=n, h0=h0: h.tensor_tensor(gr.t[:, h0:h0 + n], pxi.t[:, 0:n], s_, ALU.mult), [pxi, ts_], [gr])
            P.op("pool", lambda h, n=n, h0=h0: h.tensor_tensor(yr.t[:, h0:h0 + n], yr.t[:, h0:h0 + n], gr.t[:, h0:h0 + n], ALU.add), [yr, gr], [yr])
            P.op("dve", lambda h, pxi=pxi, c_=c_, n=n, h0=h0: h.tensor_tensor(yi.t[:, h0:h0 + n], pxi.t[:, 0:n], c_, ALU.mult), [pxi, tc_], [yi])
            P.op("dve", lambda h, pxr=pxr, s_=s_, n=n, h0=h0: h.tensor_tensor(gi.t[:, h0:h0 + n], pxr.t[:, 0:n], s_, ALU.mult), [pxr, ts_], [gi])
            P.op("pool", lambda h, n=n, h0=h0: h.tensor_tensor(yi.t[:, h0:h0 + n], yi.t[:, h0:h0 + n], gi.t[:, h0:h0 + n], ALU.subtract), [yi, gi], [yi])
        ct = cth.t[:, pair:pair + 1]; st_ = sth.t[:, pair:pair + 1]
        P.op("dve", lambda h: h.tensor_scalar(gin.t[:, 0:1], init_re, ct, None, ALU.mult), init_bufs + [cth], [gin])
        P.op("dve", lambda h: h.scalar_tensor_tensor(gin.t[:, 0:1], init_im, st_, gin.t[:, 0:1], ALU.mult, ALU.subtract), init_bufs + [sth, gin], [gin])
        P.op("dve", lambda h: h.tensor_scalar(gin.t[:, 0:1], gin.t[:, 0:1], -1.0, None, ALU.mult), [gin], [gin])
        P.op("dve", lambda h: h.tensor_scalar(gin.t[:, 1:2], init_re, st_, None, ALU.mult), init_bufs + [sth], [gin])
        P.op("dve", lambda h: h.scalar_tensor_tensor(gin.t[:, 1:2], init_im, ct, gin.t[:, 1:2], ALU.mult, ALU.add), init_bufs + [cth, gin], [gin])
        rb = rr.t[:, pair:pair + 1].to_broadcast([128, T])
        P.op("dve", lambda h: h.tensor_tensor_scan(gr.t[:, 0:T], rb, yr.t[:, 0:T], gin.t[:, 0:1], ALU.mult, ALU.add), [yr, gin] + als, [gr])
        P.op("dve", lambda h: h.tensor_tensor_scan(gi.t[:, 0:T], rb, yi.t[:, 0:T], gin.t[:, 1:2], ALU.mult, ALU.add), [yi, gin] + als, [gi])
        c_ = tc_.t[:, 0:T]; s_ = ts_.t[:, 0:T]
        P.op("dve", lambda h: h.tensor_tensor(yr.t[:, 0:T], gr.t[:, 0:T], c_, ALU.mult), [gr, tc_], [yr])
        P.op("pool", lambda h: h.tensor_tensor(yi.t[:, 0:T], gi.t[:, 0:T], s_, ALU.mult), [gi, ts_], [yi])
        P.op("dve", lambda h: h.tensor_tensor(hrb.t[:, 0:T], yr.t[:, 0:T], yi.t[:, 0:T], ALU.subtract), [yr, yi], [hrb])
        P.op("dve", lambda h: h.tensor_tensor(hend.t[:, 0:1], yr.t[:, T - 1:T], yi.t[:, T - 1:T], ALU.subtract), [yr, yi], [hend])
        P.op("pool", lambda h: h.tensor_tensor(yr.t[:, 0:T], gr.t[:, 0:T], s_, ALU.mult), [gr, ts_, hrb, hend], [yr])
        P.op("dve", lambda h: h.tensor_tensor(yi.t[:, 0:T], gi.t[:, 0:T], c_, ALU.mult), [gi, tc_, hrb, hend], [yi])
        P.op("dve", lambda h: h.tensor_tensor(hib.t[:, 0:T], yr.t[:, 0:T], yi.t[:, 0:T], ALU.add), [yr, yi], [hib])
        P.op("dve", lambda h: h.tensor_tensor(hend.t[:, 1:2], yr.t[:, T - 1:T], yi.t[:, T - 1:T], ALU.add), [yr, yi], [hend])
        P.op("dve", lambda h: h.tensor_copy(out_re, hend.t[:, 0:1]), [hend], [out_buf])
        P.op("dve", lambda h: h.tensor_copy(out_im, hend.t[:, 1:2]), [hend], [out_buf])
        for h0 in range(0, T, 512):
            n = min(512, T - h0)
            yp_ = ypsum[(yp_off + h0) // 512]
            P.mm([lambda h, yp_=yp_, n=n, h0=h0: h.matmul(yp_.t[:, 0:n], CT[0].t[:, pair, :], hrb.t[:, h0:h0 + n], start=(qq == 0), stop=False),
                  lambda h, yp_=yp_, n=n, h0=h0: h.matmul(yp_.t[:, 0:n], CT[1].t[:, pair, :], hib.t[:, h0:h0 + n], start=False, stop=(qq == 3))],
                 [CT[0], CT[1], hrb, hib], [yp_])

    def y_evac(yp_, ft, col0, n):
        P.op("dve", lambda h: h.scalar_tensor_tensor(ytmp.t[:, 0:n], uT.t[:, ft, col0:col0 + n], dsk.t[:, ft:ft + 1], yp_.t[:, 0:n], ALU.mult, ALU.add),
             [uT, dsk, yp_], [ytmp])
        P.op("dve", lambda h: h.tensor_tensor(ysq.t[:, 0:n], ytmp.t[:, 0:n], ytmp.t[:, 0:n], ALU.mult), [ytmp], [ysq])
        P.op("dve", lambda h: h.tensor_scalar(ysq.t[:, 0:n], ysq.t[:, 0:n], 0.044715, 1.0, ALU.mult, ALU.add), [ysq], [ysq])
        P.op("dve", lambda h: h.tensor_tensor(ysq.t[:, 0:n], ysq.t[:, 0:n], ytmp.t[:, 0:n], ALU.mult), [ysq, ytmp], [ysq])
        P.op("act", lambda h: h.activation(ysq.t[:, 0:n], ysq.t[:, 0:n], AF.Sigmoid, scale=GC), [ysq], [ysq])
        P.op("dve", lambda h: h.tensor_tensor(ygT.t[:, ft, col0:col0 + n], ysq.t[:, 0:n], ytmp.t[:, 0:n], ALU.mult), [ysq, ytmp], [ygT])

    ypb = [pf[4], pf[5]]
    for ft in range(4):
        for qq in range(4):
            pair = ft * 4 + qq
            P.op("dve", lambda h, pair=pair: h.tensor_scalar(ang.t[:], iota.t[:], th.t[:, pair:pair + 1], None, ALU.mult), [iota] + als, [ang])
            sincos(ang.t[:], 1024, tabs[qq].t[:], tabc[qq].t[:], [ang], [tabs[qq], tabc[qq]])
        for seg in range(4):
            for qq in range(4):
                pair = ft * 4 + qq
                run_seq(pair, qq, ft, seg * NCH, 1024, tabc[qq], tabs[qq],
                        stp.t[:, pair, 0:1], stp.t[:, pair, 1:2], [stp],
                        stp.t[:, pair, 0:1], stp.t[:, pair, 1:2], stp, ypb, 0)
            for hf in range(2):
                y_evac(ypb[hf], ft, seg * NCH + hf * 512, 512)
        for sb_i in range(4):
            for qq in range(4):
                pair = ft * 4 + qq
                run_seq(pair, qq, ft, sb_i * NCH + 1024, 8, tabc[qq], tabs[qq],
                        sre0.t[:, sb_i, pair:pair + 1], sim0.t[:, sb_i, pair:pair + 1], [sre0, sim0],
                        sts.t[:, sb_i, pair, 0:1], sts.t[:, sb_i, pair, 1:2], sts, ypb, 0)
            y_evac(ypb[0], ft, sb_i * NCH + 1024, 8)
    stp2 = small("stp2", [128, 2, 16]); sts2 = small("sts2", [128, 4, 2, 16])
    if STOP == 'SSM':
        for r_ in range(4):
            P.dma("pool", OUT["yp"].ap().rearrange("(p f x) c -> p f (x c)", p=128, f=4)[:, :, r_ * 1024:(r_ + 1) * 1024],
                  ygT.t[:, :, r_ * NCH:r_ * NCH + 1024], [ygT], [outbufs["yp"]], ygT)
        P.dma("pool", OUT["ys"].ap()[:, 0:128].rearrange("p (f r s) -> p f r s", f=4, r=4),
              ygT.t[:].rearrange("p f (r c) -> p f r c", r=4)[:, :, :, 1024:1032], [ygT], [outbufs["ys"]], ygT)
    P.op("dve", lambda h: h.tensor_copy(stp2.t[:], stp.t[:].rearrange("p a b -> p b a")), [stp], [stp2])
    P.op("dve", lambda h: h.tensor_copy(sts2.t[:], sts.t[:].rearrange("p s a b -> p s b a")), [sts], [sts2])
    P.dma("sp", OUT["ssm_p"].ap(), stp2.t[:], [stp2], [outbufs["ssm_p"]], stp2)
    P.dma("sp", OUT["ssm_s"].ap(), sts2.t[:], [sts2], [outbufs["ssm_s"]], sts2)
    for j in range(8):
        P.dma("sp", y_src[j].t.ap(), ygT.t[(j % 2) * 64:(j % 2) * 64 + 64, j // 2, :], [ygT], [y_src[j]], ygT)
        P.coll(y_src[j], y_dst[j], GROUPS)

    if STOP == 'SSM':
        P.barrier()
        return
    P.barrier()
    P.off = CONST_END
    y2T = P.sb("y2T", [128, KT, NO], BF16)
    F0 = P.off
    ygo = P.sb("ygo", [128, KT, NCH], BF16)
    hn1o = P.sb("hn1o", [128, KT, NCH], BF16)
    ych = [P.sb("ych%d" % i, [128, 4, NCH], BF16) for i in range(2)]
    wt2 = [P.sb("wt2_%d" % i, [128, KT, 128], BF16) for i in range(4)]
    gl = P.sb("gl", [128, NCH], F32); zz = P.sb("zz", [128, NCH], F32)
    for j in range(8):
        P.dma("sp", hn1o.t[:, 2 * j:2 * j + 2, :], hn_src[j].t.ap().rearrange("(k p) t -> p k t", p=128), [hn_src[j]], [hn1o], hn1o)
    P.op("pool", lambda h: h.memset(y2T.t[:, :, NCH:NO], 0.0), [], [y2T])
    ci = 0
    for rf in range(4):
        for r in range(4):
            yc = ych[ci % 2]; ci += 1
            for j in range(8):
                P.dma("sp", yc.t[(j % 2) * 64:(j % 2) * 64 + 64, j // 2, :], y_dst[j].t.ap()[rf * 64:(rf + 1) * 64, r * NCH:(r + 1) * NCH],
                      [y_dst[j]], [yc], yc)
            dst = ygo.t[:, rf * 4:(rf + 1) * 4, :]
            if r == 0:
                P.op("dve", lambda h, yc=yc, dst=dst: h.tensor_scalar(dst, yc.t[:], sel.t[:, 0:1], None, ALU.mult), [yc, sel], [ygo])
            else:
                P.op("dve", lambda h, yc=yc, dst=dst, r=r: h.scalar_tensor_tensor(dst, yc.t[:], sel.t[:, r:r + 1], dst, ALU.mult, ALU.add), [yc, sel, ygo], [ygo])
    if STOP == 'G1':
        P.barrier()
        return
    for nt in range(16):
        wa = wt2[2 * (nt % 2)]
        wb_ = wt2[2 * (nt % 2) + 1]
        P.dma("pool", wa.t[:], IN["w_glu"].ap()[:, nt * 128:(nt + 1) * 128].rearrange("(kt p) c -> p kt c", p=128), [], [wa], wa)
        P.dma("pool", wb_.t[:], IN["w_in_c"].ap()[:, 2048 + nt * 128:2048 + (nt + 1) * 128].rearrange("(kt p) c -> p kt c", p=128), [], [wb_], wb_)
        for (c0, n) in [(0, 512), (512, 512), (1024, 8)]:
            proj_feat(wa, None, lambda pk, c0=c0, n=n, nt=nt: P.op("act", lambda h: h.activation(gl.t[:, c0:c0 + n], pk.t[:, 0:n], AF.Sigmoid, bias=bglu.t[:, nt:nt + 1], scale=1.0), [pk, bglu], [gl]),
                      ygo, lambda kt, c0=c0, n=n: ygo.t[:, kt, c0:c0 + n], n)
            def zev(pk, c0=c0, n=n):
                P.op("act", lambda h: h.activation(zz.t[:, c0:c0 + n], pk.t[:, 0:n], AF.Sigmoid), [pk], [zz])
                P.op("dve", lambda h: h.tensor_tensor(zz.t[:, c0:c0 + n], zz.t[:, c0:c0 + n], pk.t[:, 0:n], ALU.mult), [zz, pk], [zz])
            proj_feat(wb_, None, zev, hn1o, lambda kt, c0=c0, n=n: hn1o.t[:, kt, c0:c0 + n], n)
        P.op("dve", lambda h, nt=nt: h.tensor_tensor(gl.t[:], gl.t[:], ygo.t[:, nt, :], ALU.mult), [gl, ygo], [gl])
        P.op("dve", lambda h, nt=nt: h.tensor_tensor(y2T.t[:, nt, 0:NCH], gl.t[:], zz.t[:], ALU.mult), [gl, zz], [y2T])
    if STOP == 'G2':
        P.barrier()
        return
    P.barrier()
    P.off = F0
    wgb2 = [P.sb("wgc%d" % i, [128, KT, 512], BF16) for i in range(2)]
    h1s = [P.sb("h1f%d" % i, [128, D], F32) for i in range(5)]
    junk = P.sb("junk3", [128, D], F32); ss = P.sb("ss3", [128, 4], F32)
    yo = [P.sb("yo%d" % i, [128, D], F32) for i in range(2)]
    load_g("g_fin")
    wi = 0
    for tiles in [list(range(0, 5)), list(range(5, 9))]:
        for si, tj in enumerate(tiles):
            o0 = tj * 128
            P.dma("sp", h1s[si].t[:], h1_scr.t.ap()[o0:o0 + 128, :], [h1_scr], [h1s[si]], h1s[si])
        for g4 in range(4):
            wg = wgb2[wi % 2]
            wi += 1
            P.dma("pool", wg.t[:], IN["w_out_c"].ap()[:, g4 * 512:(g4 + 1) * 512].rearrange("(kt p) c -> p kt c", p=128), [], [wg], wg)
            for si, tj in enumerate(tiles):
                o0 = tj * 128
                ht_ = h1s[si]
                pk = next_pf()
                fns = [(lambda h, pk=pk, kt=kt, o0=o0, wg=wg: h.matmul(pk.t[:], y2T.t[:, kt, o0:o0 + 128], wg.t[:, kt, :],
                                                                         start=(kt == 0), stop=(kt == KT - 1))) for kt in range(KT)]
                P.mm(fns, [y2T, wg], [pk])
                P.op("dve", lambda h, pk=pk, ht_=ht_, g4=g4: h.tensor_tensor(ht_.t[:, g4 * 512:(g4 + 1) * 512], ht_.t[:, g4 * 512:(g4 + 1) * 512], pk.t[:], ALU.add),
                     [pk, ht_], [ht_])
        for si, tj in enumerate(tiles):
            o0 = tj * 128
            ht_ = h1s[si]
            yo_ = yo[tj % 2]
            rmsnorm_rows(ht_, yo_, ss, junk)
            if tj < 8:
                P.dma("sp", OUT["yp"].ap()[o0:o0 + 128, :], yo_.t[:], [yo_], [outbufs["yp"]], yo_)
            else:
                P.dma("sp", OUT["ys"].ap(), yo_.t[:], [yo_], [outbufs["ys"]], yo_)
    P.barrier()


_NC_CACHE = {}


def _rope_tables(pos):
    half = 64
    inv = (np.float32(10000.0) ** (-np.arange(half, dtype=np.float32) / np.float32(half))).astype(np.float32)
    ang = pos.astype(np.float32)[:, None] * inv[None, :]
    return np.cos(ang).astype(np.float32), np.sin(ang).astype(np.float32)


def kernel(x_prompt, x_sample, cache_win_k, cache_win_v, state_conv, state_ssm_re, state_ssm_im,
           attn_norm, w_in_ab, conv_w, w_out_ab, ssm_norm, w_in_c, lam_re, lam_im, log_step,
           b_re, b_im, c_re, c_im, d_skip, w_glu, b_glu, w_out_c, final_norm):
    f = lambda a: np.ascontiguousarray(np.asarray(a, dtype=np.float32))
    x_prompt, x_sample = f(x_prompt), f(x_sample)
    cache_win_k, cache_win_v, state_conv = f(cache_win_k), f(cache_win_v), f(state_conv)
    state_ssm_re, state_ssm_im = f(state_ssm_re), f(state_ssm_im)
    w_in_ab0, w_out_ab0, w_in_c0, w_glu0, w_out_c0 = f(w_in_ab)[0], f(w_out_ab)[0], f(w_in_c)[0], f(w_glu)[0], f(w_out_c)[0]
    lam_re, lam_im, log_step = f(lam_re)[0], f(lam_im)[0], f(log_step)[0]
    b_re, b_im, c_re, c_im = f(b_re)[0], f(b_im)[0], f(c_re)[0], f(c_im)[0]
    d_skip0, b_glu0 = f(d_skip)[0], f(b_glu)[0]
    if "nc" not in _NC_CACHE:
        _NC_CACHE["nc"] = build_nc()
    nc = _NC_CACHE["nc"]

    kk = np.arange(128)[:, None]
    qq_ = np.arange(512)[None, :]
    maskp = np.stack([mult_of(qq_ - ((i - 16) * 128 + kk)) for i in range(20)], 1)
    rows = np.arange(2176).reshape(17, 128)
    s_ = np.arange(128)[None, :]
    masks = np.zeros((128, 17, 128), np.float32)
    for i in range(17):
        row = rows[i][:, None]
        m = mult_of(2048 + s_ - row)
        m[:, 8:] = ((2048 + s_[:, 8:] - row) == 0)
        masks[:, i, :] = m
    iota = np.broadcast_to(np.arange(1024, dtype=np.float32)[None, :], (128, 1024)).copy()
    rmask = np.zeros((128, 8), np.float32)
    for p in range(128):
        rmask[p, (p // 32) * 2 + (p % 32) // 16] = 1.0
    smask = np.zeros((128, 2), np.float32)
    smask[:64, 0] = 1.0
    smask[64:, 1] = 1.0
    bc = lambda v: np.ascontiguousarray(np.broadcast_to(v[None, :], (128, v.shape[0])))

    in_maps = []
    for c in range(8):
        b, r = c // 4, c % 4
        T0 = r * NOWN
        xh = np.zeros((NTP, D), np.float32)
        lo = T0 - NHALO
        src_lo = max(lo, 0)
        xh[src_lo - lo:] = x_prompt[b, src_lo:T0 + NOWN]
        pos = np.concatenate([np.arange(lo, T0 + NOWN), PAST + np.arange(128)]).astype(np.float32)
        valid = (pos[:NTP] >= 0).astype(np.float32)
        cosv, sinv = _rope_tables(np.maximum(pos, 0))
        xs = np.zeros((128, D), np.float32)
        xs[:8] = x_sample[c]
        g0 = 32 * r
        gs = slice(g0, g0 + 32)
        st_lay = lambda a: np.ascontiguousarray(a.reshape(16, 2, 64).transpose(1, 2, 0).reshape(128, 16))
        def row_lay_rep(a):
            t = a.reshape(4, 4, 2, 64)
            t = np.broadcast_to(t[:, :, :, None, :], (4, 4, 2, 16, 64))
            return np.ascontiguousarray(t.transpose(1, 2, 3, 0, 4).reshape(128, 4, 64))
        def row_lay_b(a):
            t = a.reshape(4, 4, 2, 64, 16)
            return np.ascontiguousarray(t.transpose(1, 2, 4, 0, 3).reshape(128, 4, 64))
        def st_lay_c(a):
            t = a.reshape(16, 2, 16, 64)
            return np.ascontiguousarray(t.transpose(1, 3, 0, 2).reshape(128, 16, 16))
        lst32 = np.broadcast_to(log_step[gs][:, None], (32, 64))
        sel = np.zeros((128, 4), np.float32)
        sel[:, r] = 1.0
        sre0 = np.stack([st_lay(state_ssm_re[0, 4 * b + i, gs]) for i in range(4)], 1)
        sim0 = np.stack([st_lay(state_ssm_im[0, 4 * b + i, gs]) for i in range(4)], 1)
        w_in_c_rolled = np.concatenate([w_in_c0[:, 512 * r:512 * (r + 1)], w_in_c0[:, 512:2048], w_in_c0[:, 2048:]], 1)
        m = {
            "xh": xh, "xs": xs,
            "cs": np.ascontiguousarray(cosv.reshape(25, 128, 64).transpose(1, 0, 2)),
            "sn": np.ascontiguousarray(sinv.reshape(25, 128, 64).transpose(1, 0, 2)),
            "valid": np.ascontiguousarray(valid.reshape(24, 128).T),
            "ck": np.ascontiguousarray(cache_win_k[0, c].reshape(2048, 1024)),
            "cv": np.ascontiguousarray(cache_win_v[0, c].reshape(2048, 1024)),
            "sconv": np.ascontiguousarray(state_conv[0, c].reshape(2, 8, 128).transpose(2, 1, 0)),
            "g_attn": bc(f(attn_norm)[0]), "g_ssm": bc(f(ssm_norm)[0]), "g_fin": bc(f(final_norm)),
            "w_in_ab": w_in_ab0, "cw": np.ascontiguousarray(f(conv_w)[0].reshape(3, 8, 128).transpose(2, 1, 0)),
            "w_out_ab": w_out_ab0, "w_in_c": np.ascontiguousarray(w_in_c_rolled),
            "w_glu": w_glu0, "w_out_c": w_out_c0,
            "bglu": np.ascontiguousarray(b_glu0.reshape(16, 128).T),
            "dsk": np.ascontiguousarray(d_skip0[512 * r:512 * (r + 1)].reshape(4, 128).T),
            "maskp": maskp, "masks": masks,
            "lre_s": st_lay(lam_re[gs]), "lim_s": st_lay(lam_im[gs]), "lst_s": st_lay(lst32),
            "lre_r": row_lay_rep(lam_re[gs]), "lim_r": row_lay_rep(lam_im[gs]), "lst_r": row_lay_rep(np.ascontiguousarray(lst32)),
            "bre_r": row_lay_b(b_re[gs]), "bim_r": row_lay_b(b_im[gs]),
            "cre_s": st_lay_c(c_re[gs]), "cim_s": st_lay_c(c_im[gs]),
            "rmask": rmask, "smask": smask, "sel": sel, "sre0": sre0, "sim0": sim0, "iota": iota,
        }
        in_maps.append({k: np.ascontiguousarray(v, dtype=np.float32) for k, v in m.items()})

    res = run_bass_kernel_spmd(nc, in_maps, core_ids=list(range(8)))
    R = res.results
    _NC_CACHE['raw'] = R
    y_prompt = np.zeros((2, SEQ, D), np.float32)
    y_sample = np.zeros((8, 8, D), np.float32)
    kp = np.zeros((1, 2, 2048, 8, 128), np.float32)
    vp = np.zeros((1, 2, 2048, 8, 128), np.float32)
    convp = np.zeros((1, 2, 2, 1024), np.float32)
    srp = np.zeros((1, 2, 128, 64), np.float32)
    sip = np.zeros((1, 2, 128, 64), np.float32)
    ks = np.zeros((1, 8, 8, 8, 128), np.float32)
    vs = np.zeros((1, 8, 8, 8, 128), np.float32)
    convs = np.zeros((1, 8, 2, 1024), np.float32)
    srs = np.zeros((1, 8, 128, 64), np.float32)
    sis = np.zeros((1, 8, 128, 64), np.float32)
    unst = lambda a: a.reshape(2, 64, 16).transpose(2, 0, 1).reshape(32, 64)
    for c in range(8):
        b, r = c // 4, c % 4
        o = R[c]
        y_prompt[b, r * NOWN:(r + 1) * NOWN] = o["yp"]
        y_sample[c] = o["ys"][:8]
        if r >= 2:
            kp[0, b, (r - 2) * NOWN:(r - 1) * NOWN] = o["kp"].reshape(NOWN, 8, 128)
            vp[0, b, (r - 2) * NOWN:(r - 1) * NOWN] = o["vp"].reshape(NOWN, 8, 128)
        if r == 3:
            convp[0, b] = o["convp"].transpose(2, 1, 0).reshape(2, 1024)
        ks[0, c] = o["ks"][:8].reshape(8, 8, 128)
        vs[0, c] = o["vs"][:8].reshape(8, 8, 128)
        convs[0, c] = o["convs"].transpose(2, 1, 0).reshape(2, 1024)
        srp[0, b, 32 * r:32 * (r + 1)] = unst(o["ssm_p"][:, 0, :])
        sip[0, b, 32 * r:32 * (r + 1)] = unst(o["ssm_p"][:, 1, :])
        for i in range(4):
            srs[0, 4 * b + i, 32 * r:32 * (r + 1)] = unst(o["ssm_s"][:, i, 0, :])
            sis[0, 4 * b + i, 32 * r:32 * (r + 1)] = unst(o["ssm_s"][:, i, 1, :])
    return (y_prompt, y_sample, kp, vp, convp, srp, sip, ks, vs, convs, srs, sis)
```

```python
import math
import os
STOP = os.environ.get('MK_STOP', '')
from contextlib import ExitStack

import numpy as np
import concourse.bass as bass
import concourse.mybir as mybir
from concourse.bass_utils import run_bass_kernel_spmd

F32 = mybir.dt.float32
BF16 = mybir.dt.bfloat16
ALU = mybir.AluOpType
AF = mybir.ActivationFunctionType
AX = mybir.AxisListType

ENGS = ["pe", "act", "dve", "pool", "sp"]
D = 2048
KT = 16
NOWN = 1024
NHALO = 2048
NTP = NOWN + NHALO
NTILE_P = NTP // 128
NO = NOWN + 128
SEQ = 4096
PAST = 16384
NCH = 1032
TWO_PI = 2.0 * math.pi


class Buf:
    def __init__(self, t, name):
        self.t = t
        self.name = name
        self.w = {}
        self.r = {}
        self.dsem = None
        self.dcnt = 0


class Prog:
    def __init__(self, nc, stack):
        self.nc = nc
        self.stack = stack
        self.q = {e: [] for e in ENGS}
        self.cnt = {e: 0 for e in ENGS}
        self.seen = {e: {} for e in ENGS}
        self.sems = {}
        self.semval = {}
        for e in ["pe", "act", "dve", "pool"]:
            self.sems[e] = stack.enter_context(nc.semaphore("s_" + e))
        self.off = 16512
        self.free = []
        self.dval = {}
        self.phase_bufs = []

    def sb(self, name, shape, dt, at=None):
        nbytes = int(np.prod(shape[1:])) * (2 if dt == BF16 else 4)
        if at is None:
            at = self.off
            self.off = (at + nbytes + 63) // 64 * 64
        assert at + nbytes <= 229300, (name, at, nbytes)
        t = self.nc.alloc_sbuf_tensor_at(name, list(shape), dt, offset=at)
        b = Buf(t, name)
        b.at = at
        b.nbytes = nbytes
        return b

    def ps(self, name, shape, dt=F32):
        t = self.stack.enter_context(self.nc.psum_tensor(name, list(shape), dt))
        return Buf(t, name)

    def dram(self, name, shape, dt, kind="Internal"):
        t = self.nc.dram_tensor(name, list(shape), dt, kind=kind)
        return Buf(t, name)

    def _need(self, eng, k, v, waits):
        if self.seen[eng].get(k, 0) >= v:
            return
        waits[k] = max(waits.get(k, 0), v)

    def _deps(self, eng, reads, writes):
        waits = {}
        for b in reads:
            for k, v in b.w.items():
                self._need(eng, k, v, waits)
        for b in writes:
            for k, v in b.w.items():
                self._need(eng, k, v, waits)
            for k, v in b.r.items():
                self._need(eng, k, v, waits)
        for k, v in waits.items():
            self.seen[eng][k] = v
        return [(self.sems[k], v) for k, v in waits.items()]

    def _commit(self, k, v, reads, writes):
        self.semval[k] = v
        for b in reads:
            b.r[k] = max(b.r.get(k, 0), v)
        for b in writes:
            b.w[k] = max(b.w.get(k, 0), v)
            b.r = {}

    def op(self, eng, fn, reads=(), writes=()):
        reads = [b for b in reads if b is not None]
        writes = [b for b in writes if b is not None]
        wl = self._deps(eng, reads, writes)
        self.cnt[eng] += 1
        sem = self.sems[eng]

        def emit(h, fn=fn, wl=wl, sem=sem):
            for s, v in wl:
                h.wait_ge(s, v)
            fn(h).then_inc(sem, 1)

        self.q[eng].append(emit)
        self._commit(eng, self.cnt[eng], reads, writes)

    def mm(self, fns, reads, writes):
        eng = "pe"
        wl = self._deps(eng, reads, writes)
        self.cnt[eng] += 1
        sem = self.sems[eng]

        def emit(h, fns=fns, wl=wl, sem=sem):
            for s, v in wl:
                h.wait_ge(s, v)
            for f in fns[:-1]:
                f(h)
            fns[-1](h).then_inc(sem, 1)

        self.q[eng].append(emit)
        self._commit(eng, self.cnt[eng], reads, writes)

    def dma(self, eng, out, in_, reads, writes, semb, **kw):
        reads = [b for b in reads if b is not None]
        writes = [b for b in writes if b is not None]
        if semb.dsem is None:
            if self.free:
                key = self.free.pop()
            else:
                key = "d%d" % len(self.sems)
                self.sems[key] = self.stack.enter_context(self.nc.semaphore(key))
            semb.dsem = key
            semb.dcnt = self.dval.get(key, 0)
            self.phase_bufs.append(semb)
        wl = self._deps(eng, reads, writes)
        semb.dcnt += 16
        self.dval[semb.dsem] = semb.dcnt
        sem = self.sems[semb.dsem]

        def emit(h, wl=wl, sem=sem, out=out, in_=in_, kw=kw):
            for s, v in wl:
                h.wait_ge(s, v)
            h.dma_start(out=out, in_=in_, **kw).then_inc(sem, 16)

        self.q[eng].append(emit)
        self._commit(semb.dsem, semb.dcnt, reads, writes)

    def coll(self, src, dst, groups):
        key = "c%d" % len(self.sems)
        self.sems[key] = self.stack.enter_context(self.nc.semaphore(key))
        wl = self._deps("pool", [src], [dst])
        sem = self.sems[key]

        def emit(h, wl=wl, sem=sem):
            for s, v in wl:
                h.wait_ge(s, v)
            h.collective_compute("AllGather", ALU.bypass, replica_groups=groups,
                                 ins=[src.t.ap()], outs=[dst.t.ap()]).then_inc(sem)
            h.wait_ge(sem, 1)

        self.q["pool"].append(emit)
        self.cnt["pool"] += 1
        s2 = self.sems["pool"]
        self.q["pool"].append(lambda h, s2=s2: h.engine_nop().then_inc(s2, 1))
        self._commit("pool", self.cnt["pool"], [src], [dst])

    def barrier(self):
        for b in self.phase_bufs:
            self.free.append(b.dsem)
            b.dsem = None
        self.phase_bufs = []
        items = list(self.semval.items())
        for e in ENGS:
            wl = []
            for k, v in items:
                if self.seen[e].get(k, 0) < v:
                    self.seen[e][k] = v
                    wl.append((self.sems[k], v))

            def emit(h, wl=wl):
                for s, v in wl:
                    h.wait_ge(s, v)

            if wl:
                self.q[e].append(emit)

    def run(self):
        nc = self.nc
        with nc.Block() as block:
            @block.tensor
            def _(h):
                for f in self.q["pe"]:
                    f(h)

            @block.scalar
            def _(h):
                for f in self.q["act"]:
                    f(h)

            @block.vector
            def _(h):
                for f in self.q["dve"]:
                    f(h)

            @block.gpsimd
            def _(h):
                for f in self.q["pool"]:
                    f(h)

            @block.sync
            def _(h):
                for f in self.q["sp"]:
                    f(h)


def mult_of(d):
    d = np.asarray(d)
    m = ((d >= 0) & (d <= 128)).astype(np.float32)
    m += ((d >= 0) & (d <= 512) & (d % 4 == 0))
    m += ((d >= 0) & (d <= 2048) & (d % 16 == 0))
    return m.astype(np.float32)


IN_SPECS = [
    ("xh", [NTP, D]), ("xs", [128, D]), ("cs", [128, 25, 64]), ("sn", [128, 25, 64]),
    ("valid", [128, 24]), ("ck", [2048, 1024]), ("cv", [2048, 1024]), ("sconv", [128, 8, 2]),
    ("g_attn", [128, D]), ("g_ssm", [128, D]), ("g_fin", [128, D]),
    ("w_in_ab", [D, 8192]), ("cw", [128, 8, 3]), ("w_out_ab", [D, D]), ("w_in_c", [D, 4096]),
    ("w_glu", [D, D]), ("w_out_c", [D, D]), ("bglu", [128, 16]), ("dsk", [128, 4]),
    ("maskp", [128, 20, 512]), ("masks", [128, 17, 128]),
    ("lre_s", [128, 16]), ("lim_s", [128, 16]), ("lst_s", [128, 16]),
    ("lre_r", [128, 4, 64]), ("lim_r", [128, 4, 64]), ("lst_r", [128, 4, 64]),
    ("bre_r", [128, 4, 64]), ("bim_r", [128, 4, 64]),
    ("cre_s", [128, 16, 16]), ("cim_s", [128, 16, 16]),
    ("rmask", [128, 8]), ("smask", [128, 2]), ("sel", [128, 4]),
    ("sre0", [128, 4, 16]), ("sim0", [128, 4, 16]), ("iota", [128, 1024]),
]
OUT_SPECS = [
    ("yp", [NOWN, D]), ("ys", [128, D]), ("kp", [NOWN, 1024]), ("vp", [NOWN, 1024]),
    ("convp", [128, 8, 2]), ("ssm_p", [128, 2, 16]), ("ks", [128, 1024]), ("vs", [128, 1024]),
    ("convs", [128, 8, 2]), ("ssm_s", [128, 4, 2, 16]),
]


def build_nc():
    nc = bass.Bass("TRN2", target_bir_lowering=False)
    IN = {}
    for n, s in IN_SPECS:
        IN[n] = nc.dram_tensor(n, s, F32, kind="ExternalInput")
    OUT = {}
    for n, s in OUT_SPECS:
        OUT[n] = nc.dram_tensor(n, s, F32, kind="ExternalOutput")
    st = ExitStack()
    with st:
        P = Prog(nc, st)
        build_program(nc, P, IN, OUT)
        P.run()
    return nc


def build_program(nc, P, IN, OUT):
    GROUPS = [[0, 1, 2, 3], [4, 5, 6, 7]]
    outbufs = {n: Buf(OUT[n], n) for n in OUT}
    kT_scr = P.dram("kT_scr", [8, 128, NTP], BF16)
    v_scr = P.dram("v_scr", [NTP, 1024], BF16)
    kTs_scr = P.dram("kTs_scr", [8, 128, 2176], BF16)
    vs_scr = P.dram("vs_scr", [2176, 1024], BF16)
    qT_scr = P.dram("qT_scr", [8, 128, NO], BF16)
    h1_scr = P.dram("h1_scr", [NO, D], F32)
    hn_src = [P.dram("hn_src%d" % j, [256, NCH], BF16) for j in range(8)]
    hn_dst = [P.dram("hn_dst%d" % j, [4 * 256, NCH], BF16) for j in range(8)]
    y_src = [P.dram("y_src%d" % j, [64, 4 * NCH], BF16) for j in range(8)]
    y_dst = [P.dram("y_dst%d" % j, [4 * 64, 4 * NCH], BF16) for j in range(8)]

    pf = [P.ps("pf%d" % i, [128, 512], F32) for i in range(6)]
    pb = [P.ps("pb%d" % i, [128, 8, 128], BF16) for i in range(2)]
    pfi = [0]
    pbi = [0]

    def next_pf():
        pfi[0] = (pfi[0] + 1) % 4
        return pf[pfi[0]]

    def next_pb():
        pbi[0] = (pbi[0] + 1) % 2
        return pb[pbi[0]]

    ident = P.sb("ident", [128, 128], BF16)
    P.op("pool", lambda h: h.memset(ident.t[:], 1.0), [], [ident])
    P.op("pool", lambda h: h.affine_select(ident.t[:], ident.t[:], [[-1, 128]], ALU.is_equal, 0.0,
                                            base=0, channel_multiplier=1), [ident], [ident])
    ones_bf = P.sb("ones_bf", [128, 128], BF16)
    P.op("pool", lambda h: h.memset(ones_bf.t[:], 1.0), [], [ones_bf])
    gt = P.sb("gt", [128, D], F32)
    cs = P.sb("cs", [128, 25, 64], F32)
    sn = P.sb("sn", [128, 25, 64], F32)
    valid = P.sb("valid", [128, 24], F32)
    validB = P.sb("validB", [128, 24, 128], BF16)
    cw = P.sb("cw", [128, 8, 3], F32)
    bglu = P.sb("bglu", [128, 16], F32)
    dsk = P.sb("dsk", [128, 4], F32)
    sel = P.sb("sel", [128, 4], F32)
    eps_t = P.sb("eps_t", [128, 1], F32)
    P.op("pool", lambda h: h.memset(eps_t.t[:], 1e-6), [], [eps_t])
    for b_, n in [(cs, "cs"), (sn, "sn"), (valid, "valid"), (cw, "cw"), (bglu, "bglu"), (dsk, "dsk"), (sel, "sel")]:
        P.dma("sp", b_.t[:], IN[n].ap(), [], [b_], b_)
    P.op("dve", lambda h: h.tensor_copy(validB.t[:], valid.t[:].unsqueeze(2).to_broadcast([128, 24, 128])),
         [valid], [validB])
    pospi = P.sb("pospi", [128, 1], F32)
    P.op("pool", lambda h: h.memset(pospi.t[:], math.pi), [], [pospi])
    CONST_END = P.off
    hnT_o = P.sb("hnT_o", [128, KT, NO], BF16)

    def load_g(name):
        P.dma("sp", gt.t[:], IN[name].ap(), [], [gt], gt)

    def rmsnorm_rows(xt, xn, ss, junk):
        P.op("act", lambda h: h.activation(junk.t[:], xt.t[:], AF.Square, accum_out=ss.t[:, 0:1]), [xt], [junk, ss])
        P.op("act", lambda h: h.activation(ss.t[:, 1:2], ss.t[:, 0:1], AF.Sqrt, bias=eps_t.t[:, 0:1], scale=1.0 / D), [ss, eps_t], [ss])
        P.op("dve", lambda h: h.reciprocal(ss.t[:, 2:3], ss.t[:, 1:2]), [ss], [ss])
        P.op("dve", lambda h: h.scalar_tensor_tensor(xn.t[:], xt.t[:], ss.t[:, 2:3], gt.t[:], ALU.mult, ALU.mult),
             [xt, ss, gt], [xn])

    def transpose_rows(xn, dst, dst_ap_fn):
        for half in range(2):
            p = next_pb()
            fns = []
            for j in range(8):
                kt = half * 8 + j
                fns.append(lambda h, p=p, j=j, kt=kt: h.transpose(p.t[:, j, :], xn.t[:, kt * 128:(kt + 1) * 128], ident.t[:]))
            P.mm(fns, [xn, ident], [p])
            P.op("act", lambda h, p=p, half=half: h.activation(dst_ap_fn(half), p.t[:], AF.Identity), [p], [dst])

    A0 = P.off
    wkv = P.sb("wkv", [128, KT, 2048], BF16)
    xts = [P.sb("xt%d" % i, [128, D], F32) for i in range(2)]
    xns = [P.sb("xn%d" % i, [128, D], BF16) for i in range(2)]
    hts = [P.sb("ht%d" % i, [128, KT, 128], BF16) for i in range(2)]
    junk = P.sb("junk", [128, D], F32)
    ss = P.sb("ss", [128, 4], F32)
    krs = [P.sb("kr%d" % i, [128, 1024], F32) for i in range(2)]
    vfs = [P.sb("vf%d" % i, [128, 1024], F32) for i in range(2)]
    t1 = P.sb("t1", [128, 256], F32)
    t2 = P.sb("t2", [128, 256], F32)
    krbs = [P.sb("krb%d" % i, [128, 1024], BF16) for i in range(2)]
    vbs = [P.sb("vb%d" % i, [128, 1024], BF16) for i in range(2)]
    kTts = [P.sb("kTt%d" % i, [128, 8, 128], BF16) for i in range(2)]
    kTt = kTts[0]
    hprev2 = P.sb("hprev2", [128, KT, 2], BF16)
    A1_END = P.off

    load_g("g_attn")
    for half in range(2):
        P.dma("pool", wkv.t[:, :, half * 1024:(half + 1) * 1024],
              IN["w_in_ab"].ap()[:, 1024 + half * 1024:2048 + half * 1024].rearrange("(kt p) c -> p kt c", p=128),
              [], [wkv], wkv)

    def rotary(pk, ti, dst, c0):
        v = pk.t[:].rearrange("p (h two d) -> p h two d", h=4, two=2)
        o = dst.t[:, c0:c0 + 512].rearrange("p (h two d) -> p h two d", h=4, two=2)
        cb = cs.t[:, ti, :].unsqueeze(1).to_broadcast([128, 4, 64])
        sb_ = sn.t[:, ti, :].unsqueeze(1).to_broadcast([128, 4, 64])
        a = t1.t[:].rearrange("p (h d) -> p h d", h=4)
        b = t2.t[:].rearrange("p (h d) -> p h d", h=4)
        P.op("dve", lambda h: h.tensor_tensor(a, v[:, :, 0, :], cb, ALU.mult), [pk, cs], [t1])
        P.op("dve", lambda h: h.tensor_tensor(b, v[:, :, 1, :], sb_, ALU.mult), [pk, sn], [t2])
        P.op("dve", lambda h: h.tensor_tensor(o[:, :, 0, :], a, b, ALU.subtract), [t1, t2], [dst])
        P.op("dve", lambda h: h.tensor_tensor(a, v[:, :, 1, :], cb, ALU.mult), [pk, cs], [t1])
        P.op("dve", lambda h: h.tensor_tensor(b, v[:, :, 0, :], sb_, ALU.mult), [pk, sn], [t2])
        P.op("dve", lambda h: h.tensor_tensor(o[:, :, 1, :], a, b, ALU.add), [t1, t2], [dst])

    def store_kT(src_bf, scr, col0, kb=None):
        p = next_pb()
        if kb is None:
            kb = kTt
        fns = [(lambda h, p=p, j=j: h.transpose(p.t[:, j, :], src_bf.t[:, j * 128:(j + 1) * 128], ident.t[:])) for j in range(8)]
        P.mm(fns, [src_bf, ident], [p])
        P.op("act", lambda h, p=p, kb=kb: h.activation(kb.t[:], p.t[:], AF.Identity), [p], [kb])
        P.dma("sp", scr.t.ap()[:, :, col0:col0 + 128].rearrange("h d t -> d h t"), kb.t[:], [kb], [scr], kb)

    def stageA(ti):
        xt = xts[ti % 2]
        xn = xns[ti % 2]
        src = IN["xh"].ap()[ti * 128:(ti + 1) * 128, :] if ti < 24 else IN["xs"].ap()
        P.dma("sp", xt.t[:], src, [], [xt], xt)
        rmsnorm_rows(xt, xn, ss, junk)
        if ti < 16:
            ht = hts[ti % 2]
            transpose_rows(xn, ht, lambda half, ht=ht: ht.t[:, half * 8:(half + 1) * 8, :])
            if ti == 15:
                P.op("dve", lambda h, ht=ht: h.tensor_copy(hprev2.t[:], ht.t[:, :, 126:128]), [ht], [hprev2])
            return (lambda kt, ht=ht: ht.t[:, kt, :]), ht
        o0 = (ti - 16) * 128
        transpose_rows(xn, hnT_o, lambda half, o0=o0: hnT_o.t[:, half * 8:(half + 1) * 8, o0:o0 + 128])
        return (lambda kt, o0=o0: hnT_o.t[:, kt, o0:o0 + 128]), hnT_o

    def stageM(ti, lhs, hb):
        kr = krs[ti % 2]
        vf = vfs[ti % 2]
        for g4 in range(4):
            pk = next_pf()
            fns = [(lambda h, pk=pk, kt=kt, g4=g4, lhs=lhs: h.matmul(pk.t[:], lhs(kt), wkv.t[:, kt, g4 * 512:(g4 + 1) * 512],
                                                                    start=(kt == 0), stop=(kt == KT - 1))) for kt in range(KT)]
            P.mm(fns, [hb, wkv], [pk])
            if g4 < 2:
                rotary(pk, ti, kr, g4 * 512)
            else:
                c0 = (g4 - 2) * 512
                P.op("act", lambda h, pk=pk, c0=c0, vf=vf: h.activation(vf.t[:, c0:c0 + 512], pk.t[:], AF.Identity), [pk], [vf])

    def stageK(ti):
        kr = krs[ti % 2]
        vf = vfs[ti % 2]
        krb = krbs[ti % 2]
        vb = vbs[ti % 2]
        kb = kTts[ti % 2]
        P.op("act", lambda h: h.activation(krb.t[:], kr.t[:], AF.Identity), [kr], [krb])
        if ti < 24:
            P.op("dve", lambda h: h.tensor_scalar(vb.t[:], vf.t[:], valid.t[:, ti:ti + 1], None, ALU.mult), [vf, valid], [vb])
            store_kT(krb, kT_scr, ti * 128, kb)
            P.dma("sp", v_scr.t.ap()[ti * 128:(ti + 1) * 128, :], vb.t[:], [vb], [v_scr], vb)
            if ti >= 16:
                r0 = (ti - 16) * 128
                P.dma("sp", OUT["kp"].ap()[r0:r0 + 128, :], kr.t[:], [kr], [outbufs["kp"]], kr)
                P.dma("sp", OUT["vp"].ap()[r0:r0 + 128, :], vf.t[:], [vf], [outbufs["vp"]], vf)
        else:
            P.op("dve", lambda h: h.tensor_copy(vb.t[:], vf.t[:]), [vf], [vb])
            store_kT(krb, kTs_scr, 2048, kb)
            P.dma("sp", vs_scr.t.ap()[2048:2176, :], vb.t[:], [vb], [vs_scr], vb)
            P.dma("sp", OUT["ks"].ap(), kr.t[:], [kr], [outbufs["ks"]], kr)
            P.dma("sp", OUT["vs"].ap(), vf.t[:], [vf], [outbufs["vs"]], vf)

    infoA = {0: stageA(0)}
    for ti in range(25):
        if ti + 1 < 25:
            infoA[ti + 1] = stageA(ti + 1)
        stageM(ti, *infoA[ti])
        if ti >= 1:
            stageK(ti - 1)
    stageK(24)
    for ti in range(16):
        xt = xts[ti % 2]
        krb = krbs[ti % 2]
        vb = vbs[ti % 2]
        P.dma("sp", xt.t[:, 0:1024], IN["ck"].ap()[ti * 128:(ti + 1) * 128, :], [], [xt], xt)
        P.dma("sp", xt.t[:, 1024:2048], IN["cv"].ap()[ti * 128:(ti + 1) * 128, :], [], [xt], xt)
        P.op("act", lambda h, xt=xt, krb=krb: h.activation(krb.t[:], xt.t[:, 0:1024], AF.Identity), [xt], [krb])
        P.op("dve", lambda h, xt=xt, vb=vb: h.tensor_copy(vb.t[:], xt.t[:, 1024:2048]), [xt], [vb])
        store_kT(krb, kTs_scr, ti * 128, kTts[ti % 2])
        P.dma("sp", vs_scr.t.ap()[ti * 128:(ti + 1) * 128, :], vb.t[:], [vb], [vs_scr], vb)

    if STOP == 'A1':
        P.barrier()
        return
    P.barrier()
    P.off = A0
    wq = P.sb("wq", [128, KT, 1024], BF16)
    hprev2b = P.sb("hprev2b", [128, KT, 2], BF16)
    qf = P.sb("qf", [128, 1024], F32)
    qb = P.sb("qb", [128, 1024], BF16)
    t1 = P.sb("t1b", [128, 256], F32)
    t2 = P.sb("t2b", [128, 256], F32)
    kTt = P.sb("kTtb", [128, 8, 128], BF16)
    hprev2k = P.sb("hprev2k", [128, KT, 2], BF16, at=hprev2.at)
    hprev2k.w = dict(hprev2.w)
    P.dma("pool", wq.t[:], IN["w_in_ab"].ap()[:, 0:1024].rearrange("(kt p) c -> p kt c", p=128), [], [wq], wq)
    for tj in range(9):
        ti = 16 + tj
        o0 = tj * 128
        for g2_ in range(2):
            pk = next_pf()
            fns = [(lambda h, pk=pk, kt=kt, g2_=g2_, o0=o0: h.matmul(pk.t[:], hnT_o.t[:, kt, o0:o0 + 128],
                                                                      wq.t[:, kt, g2_ * 512:(g2_ + 1) * 512],
                                                                      start=(kt == 0), stop=(kt == KT - 1))) for kt in range(KT)]
            P.mm(fns, [hnT_o, wq], [pk])
            rotary(pk, ti, qf, g2_ * 512)
        P.op("act", lambda h: h.activation(qb.t[:], qf.t[:], AF.Identity), [qf], [qb])
        store_kT(qb, qT_scr, o0)

    if STOP == 'A2':
        P.barrier()
        return
    P.barrier()
    P.off = A0
    ocat = P.sb("ocat", [128, KT, NO], BF16)
    hp2 = P.sb("hp2", [128, KT, 2], BF16)
    B0 = P.off
    P.op("dve", lambda h: h.tensor_copy(hp2.t[:], hprev2k.t[:]), [hprev2k], [hp2])
    P.barrier()
    maskp = P.sb("maskp", [128, 20, 512], BF16)
    masks_ = P.sb("masks_", [128, 17, 128], BF16)
    P.dma("pool", maskp.t[:], IN["maskp"].ap(), [], [maskp], maskp)
    P.dma("pool", masks_.t[:], IN["masks"].ap(), [], [masks_], masks_)
    kTh = [P.sb("kTh%d" % i, [128, NTP], BF16) for i in range(1)] * 2
    vh = [P.sb("vh%d" % i, [128, 24, 128], BF16) for i in range(1)] * 2
    kTsh = [P.sb("kTsh%d" % i, [128, 2176], BF16) for i in range(1)] * 2
    vsh = [P.sb("vsh%d" % i, [128, 17, 128], BF16) for i in range(1)] * 2
    qTh = [P.sb("qTh%d" % i, [128, NO], BF16) for i in range(1)] * 2
    wt = [P.sb("wt%d" % i, [128, KT, 128], BF16) for i in range(4)]
    pts = [P.sb("pt%d" % i, [128, 512], BF16) for i in range(4)]
    ptm = [P.sb("ptm%d" % i, [128, 512], BF16) for i in range(4)]
    za = P.sb("za", [128, NO], F32)
    rl = P.sb("rl", [128, 512], F32)
    of = P.sb("of", [128, 512], F32)
    sg = P.sb("sg", [128, 512], F32)

    def silu_evac(pk, dstb, dst_ap, n):
        P.op("act", lambda h: h.activation(sg.t[:, 0:n], pk.t[:, 0:n], AF.Exp, scale=-1.0), [pk], [sg])
        P.op("dve", lambda h: h.tensor_scalar(sg.t[:, 0:n], sg.t[:, 0:n], 1.0, None, ALU.add), [sg], [sg])
        P.op("dve", lambda h: h.reciprocal(sg.t[:, 0:n], sg.t[:, 0:n]), [sg], [sg])
        P.op("dve", lambda h: h.tensor_tensor(dst_ap, pk.t[:, 0:n], sg.t[:, 0:n], ALU.mult), [pk, sg], [dstb])
    fb = [P.sb("fb%d" % i, [128, NO + 2], F32) for i in range(4)]
    convo_p = P.sb("convo_p", [128, 8, 2], F32)
    convo_s = P.sb("convo_s", [128, 8, 2], F32)
    sconv = P.sb("sconv", [128, 8, 2], F32)
    P.dma("sp", sconv.t[:], IN["sconv"].ap(), [], [sconv], sconv)
    scale = 128.0 ** -0.5

    def load_wt(i, c0):
        P.dma("pool", wt[i].t[:], IN["w_in_ab"].ap()[:, c0:c0 + 128].rearrange("(kt p) c -> p kt c", p=128), [], [wt[i]], wt[i])

    def proj_feat(wb, dst_ap_fn, evac, rhs_buf, rhs_fn, n):
        pk = next_pf()
        fns = [(lambda h, pk=pk, kt=kt: h.matmul(pk.t[:, 0:n], wb.t[:, kt, :], rhs_fn(kt), start=(kt == 0), stop=(kt == KT - 1)))
               for kt in range(KT)]
        P.mm(fns, [wb, rhs_buf], [pk])
        evac(pk)

    def attention(hh, qT, q0, nq, kT, vt, ktiles, mask_fn, vB_fn, o_dst_fn, zcol0):
        po = pf[4]
        pl = pf[5]
        nk = len(ktiles)
        LA = 3
        pms = {}

        def issue_S(i):
            kt_ = ktiles[i]
            ps_ = next_pf()
            P.mm([lambda h, ps_=ps_, kt_=kt_: h.matmul(ps_.t[:, 0:nq], kT.t[:, kt_ * 128:(kt_ + 1) * 128], qT.t[:, q0:q0 + nq],
                                                        start=True, stop=True)], [kT, qT], [ps_])
            pe_ = pts[i % 4]
            pm_ = ptm[i % 4]
            P.op("act", lambda h, ps_=ps_, pe_=pe_: h.activation(pe_.t[:, 0:nq], ps_.t[:, 0:nq], AF.Exp, scale=scale), [ps_], [pe_])
            mk, mb = mask_fn(i)
            eng = "pool" if i % 4 == 3 else "dve"
            P.op(eng, lambda h, pe_=pe_, pm_=pm_, mk=mk: h.tensor_tensor(pm_.t[:, 0:nq], pe_.t[:, 0:nq], mk, ALU.mult), [pe_, mb], [pm_])
            pms[i] = pm_

        def issue_PV(i):
            kt_ = ktiles[i]
            pm_ = pms[i]
            vB, vBb = vB_fn(i)
            P.mm([lambda h, pm_=pm_, kt_=kt_, i=i: h.matmul(po.t[:, 0:nq], vt.t[:, kt_, :], pm_.t[:, 0:nq], start=(i == 0), stop=(i == nk - 1)),
                  lambda h, pm_=pm_, vB=vB, i=i: h.matmul(pl.t[:, 0:nq], vB, pm_.t[:, 0:nq], start=(i == 0), stop=(i == nk - 1))],
                 [vt, pm_, vBb], [po, pl])

        for i in range(min(LA, nk)):
            issue_S(i)
        for i in range(nk):
            if i + LA < nk:
                issue_S(i + LA)
            issue_PV(i)
        P.op("dve", lambda h: h.reciprocal(rl.t[:, 0:nq], pl.t[:, 0:nq]), [pl], [rl])
        P.op("dve", lambda h: h.tensor_tensor(of.t[:, 0:nq], po.t[:, 0:nq], rl.t[:, 0:nq], ALU.mult), [po, rl], [of])
        P.op("dve", lambda h: h.tensor_tensor(o_dst_fn(), of.t[:, 0:nq], za.t[:, zcol0:zcol0 + nq], ALU.mult), [of, za], [ocat])

    for hh in range(8):
        b2 = hh % 2
        P.dma("sp", kTh[b2].t[:], kT_scr.t.ap()[hh], [kT_scr], [kTh[b2]], kTh[b2])
        P.dma("sp", vh[b2].t[:], v_scr.t.ap()[:, hh * 128:(hh + 1) * 128].rearrange("(t p) d -> p t d", p=128), [v_scr], [vh[b2]], vh[b2])
        P.dma("sp", kTsh[b2].t[:], kTs_scr.t.ap()[hh], [kTs_scr], [kTsh[b2]], kTsh[b2])
        P.dma("sp", vsh[b2].t[:], vs_scr.t.ap()[:, hh * 128:(hh + 1) * 128].rearrange("(t p) d -> p t d", p=128), [vs_scr], [vsh[b2]], vsh[b2])
        P.dma("sp", qTh[b2].t[:], qT_scr.t.ap()[hh], [qT_scr], [qTh[b2]], qTh[b2])
        load_wt(0, 3072 + hh * 128)
        for (c0, n) in [(0, 512), (512, 512), (1024, 128)]:
            proj_feat(wt[0], None, lambda pk, c0=c0, n=n: silu_evac(pk, za, za.t[:, c0:c0 + n], n),
                      hnT_o, lambda kt, c0=c0, n=n: hnT_o.t[:, kt, c0:c0 + n], n)
        for qc in range(2):
            kts = list(range(4 * qc, 4 * qc + 20))
            attention(hh, qTh[b2], qc * 512, 512, kTh[b2], vh[b2], kts,
                      lambda i: (maskp.t[:, i, :], maskp),
                      lambda i, kts=kts: (validB.t[:, kts[i], :], validB),
                      lambda qc=qc, hh=hh: ocat.t[:, hh, qc * 512:(qc + 1) * 512], qc * 512)
        attention(hh, qTh[b2], 1024, 128, kTsh[b2], vsh[b2], list(range(17)),
                  lambda i: (masks_.t[:, i, :], masks_),
                  lambda i: (ones_bf.t[:], ones_bf),
                  lambda hh=hh: ocat.t[:, hh, 1024:1152], 1024)

    for cc in range(8):
        for j, base in enumerate([4096, 5120, 6144, 7168]):
            load_wt(j, base + cc * 128)
        bb, cb_, hb_, zb = fb
        for j, dstb in enumerate(fb):
            for (c0, n) in [(0, 512), (512, 512), (1024, 128)]:
                if j == 3:
                    ev = lambda pk, c0=c0, n=n, dstb=dstb: silu_evac(pk, dstb, dstb.t[:, 2 + c0:2 + c0 + n], n)
                else:
                    ev = lambda pk, c0=c0, n=n, dstb=dstb: P.op("act", lambda h: h.activation(dstb.t[:, 2 + c0:2 + c0 + n], pk.t[:, 0:n], AF.Identity), [pk], [dstb])
                proj_feat(wt[j], None, ev, hnT_o, lambda kt, c0=c0, n=n: hnT_o.t[:, kt, c0:c0 + n], n)
            if j in (1, 2):
                proj_feat(wt[j], None, lambda pk, dstb=dstb: P.op("act", lambda h: h.activation(dstb.t[:, 0:2], pk.t[:, 0:2], AF.Identity), [pk], [dstb]),
                          hp2, lambda kt: hp2.t[:, kt, :], 2)
        P.op("dve", lambda h: h.tensor_tensor(cb_.t[:], cb_.t[:], hb_.t[:], ALU.mult), [cb_, hb_], [cb_])
        w0 = cw.t[:, cc, 0:1]
        w1 = cw.t[:, cc, 1:2]
        w2 = cw.t[:, cc, 2:3]
        P.op("dve", lambda h, w2=w2: h.tensor_scalar(hb_.t[:, 2:1026], cb_.t[:, 2:1026], w2, None, ALU.mult), [cb_, cw], [hb_])
        P.op("dve", lambda h, w1=w1: h.scalar_tensor_tensor(hb_.t[:, 2:1026], cb_.t[:, 1:1025], w1, hb_.t[:, 2:1026], ALU.mult, ALU.add), [cb_, cw, hb_], [hb_])
        P.op("dve", lambda h, w0=w0: h.scalar_tensor_tensor(hb_.t[:, 2:1026], cb_.t[:, 0:1024], w0, hb_.t[:, 2:1026], ALU.mult, ALU.add), [cb_, cw, hb_], [hb_])
        P.op("dve", lambda h, cc=cc: h.tensor_copy(convo_p.t[:, cc, :], cb_.t[:, 1024:1026]), [cb_], [convo_p])
        P.op("dve", lambda h, cc=cc: h.tensor_copy(cb_.t[:, 1024:1026], sconv.t[:, cc, :]), [sconv], [cb_])
        P.op("dve", lambda h, w2=w2: h.tensor_scalar(hb_.t[:, 1026:1034], cb_.t[:, 1026:1034], w2, None, ALU.mult), [cb_, cw], [hb_])
        P.op("dve", lambda h, w1=w1: h.scalar_tensor_tensor(hb_.t[:, 1026:1034], cb_.t[:, 1025:1033], w1, hb_.t[:, 1026:1034], ALU.mult, ALU.add), [cb_, cw, hb_], [hb_])
        P.op("dve", lambda h, w0=w0: h.scalar_tensor_tensor(hb_.t[:, 1026:1034], cb_.t[:, 1024:1032], w0, hb_.t[:, 1026:1034], ALU.mult, ALU.add), [cb_, cw, hb_], [hb_])
        P.op("dve", lambda h, cc=cc: h.tensor_copy(convo_s.t[:, cc, :], cb_.t[:, 1032:1034]), [cb_], [convo_s])
        P.op("dve", lambda h: h.tensor_tensor(hb_.t[:, 2:1034], hb_.t[:, 2:1034], bb.t[:, 2:1034], ALU.mult), [hb_, bb], [hb_])
        P.op("dve", lambda h, cc=cc: h.tensor_tensor(ocat.t[:, 8 + cc, 0:1032], hb_.t[:, 2:1034], zb.t[:, 2:1034], ALU.mult), [hb_, zb], [ocat])
        P.op("dve", lambda h, cc=cc: h.memset(ocat.t[:, 8 + cc, 1032:1152], 0.0), [], [ocat])
    P.dma("sp", OUT["convp"].ap(), convo_p.t[:], [convo_p], [outbufs["convp"]], convo_p)
    P.dma("sp", OUT["convs"].ap(), convo_s.t[:], [convo_s], [outbufs["convs"]], convo_s)

    if STOP == 'B':
        P.barrier()
        return
    P.barrier()
    hn1T = P.sb("hn1T", [128, KT, NCH], BF16, at=hnT_o.at)
    P.off = B0
    wgb = [P.sb("wg%d" % i, [128, KT, 512], BF16) for i in range(2)]
    h1s = [P.sb("h1s%d" % i, [128, D], F32) for i in range(5)]
    xn1 = [P.sb("xn1%d" % i, [128, D], BF16) for i in range(2)]
    junk = P.sb("junk2", [128, D], F32)
    ss = P.sb("ss2", [128, 4], F32)
    tmpT = P.sb("tmpT", [128, KT, 128], BF16)
    load_g("g_ssm")
    wi = 0
    for tiles in [list(range(0, 5)), list(range(5, 9))]:
        for si, tj in enumerate(tiles):
            o0 = tj * 128
            src = IN["xh"].ap()[NHALO + o0:NHALO + o0 + 128, :] if tj < 8 else IN["xs"].ap()
            P.dma("sp", h1s[si].t[:], src, [], [h1s[si]], h1s[si])
        for g4 in range(4):
            wg = wgb[wi % 2]
            wi += 1
            P.dma("pool", wg.t[:], IN["w_out_ab"].ap()[:, g4 * 512:(g4 + 1) * 512].rearrange("(kt p) c -> p kt c", p=128), [], [wg], wg)
            for si, tj in enumerate(tiles):
                o0 = tj * 128
                ht_ = h1s[si]
                pk = next_pf()
                fns = [(lambda h, pk=pk, kt=kt, o0=o0, wg=wg: h.matmul(pk.t[:], ocat.t[:, kt, o0:o0 + 128], wg.t[:, kt, :],
                                                                         start=(kt == 0), stop=(kt == KT - 1))) for kt in range(KT)]
                P.mm(fns, [ocat, wg], [pk])
                P.op("dve", lambda h, pk=pk, ht_=ht_, g4=g4: h.tensor_tensor(ht_.t[:, g4 * 512:(g4 + 1) * 512], ht_.t[:, g4 * 512:(g4 + 1) * 512], pk.t[:], ALU.add),
                     [pk, ht_], [ht_])
        for si, tj in enumerate(tiles):
            o0 = tj * 128
            ht_ = h1s[si]
            P.dma("sp", h1_scr.t.ap()[o0:o0 + 128, :], ht_.t[:], [ht_], [h1_scr], ht_)
            if STOP == 'C1':
                if tj < 8:
                    P.dma("sp", OUT["yp"].ap()[o0:o0 + 128, :], ht_.t[:], [ht_], [outbufs["yp"]], ht_)
                else:
                    P.dma("sp", OUT["ys"].ap(), ht_.t[:], [ht_], [outbufs["ys"]], ht_)
            xn = xn1[tj % 2]
            rmsnorm_rows(ht_, xn, ss, junk)
            if tj < 8:
                transpose_rows(xn, hn1T, lambda half, o0=o0: hn1T.t[:, half * 8:(half + 1) * 8, o0:o0 + 128])
            else:
                transpose_rows(xn, tmpT, lambda half: tmpT.t[:, half * 8:(half + 1) * 8, :])
                P.op("dve", lambda h: h.tensor_copy(hn1T.t[:, :, 1024:1032], tmpT.t[:, :, 0:8]), [tmpT], [hn1T])
    for j in range(8):
        P.dma("sp", hn_src[j].t.ap().rearrange("(k p) t -> p k t", p=128), hn1T.t[:, 2 * j:2 * j + 2, :], [hn1T], [hn_src[j]], hn1T)
        P.coll(hn_src[j], hn_dst[j], GROUPS)

    if STOP == 'C1':
        P.barrier()
        return
    P.barrier()
    P.off = CONST_END
    uT = P.sb("uT", [128, 4, 4 * NCH], BF16)
    ygT = P.sb("ygT", [128, 4, 4 * NCH], BF16)
    L1 = P.off
    wu = P.sb("wu", [128, KT, 512], BF16)
    hch = [P.sb("hch%d" % i, [128, KT, 516], BF16) for i in range(2)]
    P.dma("pool", wu.t[:], IN["w_in_c"].ap()[:, 0:512].rearrange("(kt p) c -> p kt c", p=128), [], [wu], wu)
    ci = 0
    for r in range(4):
        for hf in range(2):
            hc = hch[ci % 2]
            ci += 1
            c0 = hf * 516
            for j in range(8):
                P.dma("sp", hc.t[:, 2 * j:2 * j + 2, :], hn_dst[j].t.ap()[r * 256:(r + 1) * 256, c0:c0 + 516].rearrange("(k p) t -> p k t", p=128),
                      [hn_dst[j]], [hc], hc)
            for ft in range(4):
                pk = next_pf()
                fns = [(lambda h, pk=pk, kt=kt, ft=ft, hc=hc: h.matmul(pk.t[:, 0:512], wu.t[:, kt, ft * 128:(ft + 1) * 128], hc.t[:, kt, 0:512],
                                                                      start=(kt == 0), stop=(kt == KT - 1))) for kt in range(KT)]
                P.mm(fns, [wu, hc], [pk])
                P.op("act", lambda h, pk=pk, ft=ft, r=r, c0=c0: h.activation(uT.t[:, ft, r * NCH + c0:r * NCH + c0 + 512], pk.t[:, 0:512], AF.Identity), [pk], [uT])
                pk2 = next_pf()
                fns2 = [(lambda h, pk2=pk2, kt=kt, ft=ft, hc=hc: h.matmul(pk2.t[:, 0:4], wu.t[:, kt, ft * 128:(ft + 1) * 128], hc.t[:, kt, 512:516],
                                                                         start=(kt == 0), stop=(kt == KT - 1))) for kt in range(KT)]
                P.mm(fns2, [wu, hc], [pk2])
                P.op("act", lambda h, pk2=pk2, ft=ft, r=r, c0=c0: h.activation(uT.t[:, ft, r * NCH + c0 + 512:r * NCH + c0 + 516], pk2.t[:, 0:4], AF.Identity), [pk2], [uT])

    if STOP == 'U':
        P.barrier()
        return
    P.barrier()
    P.off = L1
    def small(name, shape, dt=F32):
        return P.sb(name, shape, dt)
    lre_s = small("lre_s", [128, 16]); lim_s = small("lim_s", [128, 16]); lst_s = small("lst_s", [128, 16])
    lre_r = small("lre_r", [128, 256]); lim_r = small("lim_r", [128, 256]); lst_r = small("lst_r", [128, 256])
    bre_r = small("bre_r", [128, 256]); bim_r = small("bim_r", [128, 256])
    cre_s = small("cre_s", [128, 16, 16]); cim_s = small("cim_s", [128, 16, 16])
    rmask = small("rmask", [128, 8]); smask = small("smask", [128, 2])
    sre0 = small("sre0", [128, 4, 16]); sim0 = small("sim0", [128, 4, 16])
    iota = small("iota", [128, 1024])
    for b_, n in [(lre_s, "lre_s"), (lim_s, "lim_s"), (lst_s, "lst_s"), (cre_s, "cre_s"), (cim_s, "cim_s"),
                  (rmask, "rmask"), (smask, "smask"), (sre0, "sre0"), (sim0, "sim0"), (iota, "iota")]:
        P.dma("sp", b_.t[:], IN[n].ap(), [], [b_], b_)
    for b_, n in [(lre_r, "lre_r"), (lim_r, "lim_r"), (lst_r, "lst_r"), (bre_r, "bre_r"), (bim_r, "bim_r")]:
        P.dma("sp", b_.t[:], IN[n].ap().rearrange("p a b -> p (a b)"), [], [b_], b_)
    negpi = small("negpi", [128, 1])
    P.op("dve", lambda h: h.memset(negpi.t[:], -math.pi), [], [negpi])

    I32 = mybir.dt.int32
    tq = small("tq", [128, 1024]); tiq = small("tiq", [128, 1024], I32)
    halfpi = small("halfpi", [128, 1]); zero_t = small("zero_t", [128, 1])
    P.op("dve", lambda h: h.memset(halfpi.t[:], 0.5 * math.pi), [], [halfpi])
    P.op("dve", lambda h: h.memset(zero_t.t[:], 0.0), [], [zero_t])

    def sincos(ang_ap, n, s_ap, c_ap, rd, wr):
        for (dst, addc, bt, lo, hi) in [(s_ap, 0.0, zero_t, -math.pi, math.pi), (c_ap, 0.25, halfpi, -1.5 * math.pi, 0.5 * math.pi)]:
            P.op("dve", lambda h, addc=addc: h.tensor_scalar(tq.t[:, 0:n], ang_ap, 1.0 / TWO_PI, addc, ALU.mult, ALU.add), rd, [tq])
            P.op("dve", lambda h: h.tensor_copy(tiq.t[:, 0:n], tq.t[:, 0:n]), [tq], [tiq])
            P.op("dve", lambda h: h.tensor_copy(tq.t[:, 0:n], tiq.t[:, 0:n]), [tiq], [tq])
            P.op("dve", lambda h: h.scalar_tensor_tensor(tq.t[:, 0:n], tq.t[:, 0:n], -TWO_PI, ang_ap, ALU.mult, ALU.add), [tq] + rd, [tq])
            P.op("dve", lambda h, lo=lo, hi=hi: h.tensor_scalar(tq.t[:, 0:n], tq.t[:, 0:n], lo, hi, ALU.max, ALU.min), [tq], [tq])
            P.op("act", lambda h, dst=dst, bt=bt: h.activation(dst, tq.t[:, 0:n], AF.Sin, bias=bt.t[:, 0:1], scale=1.0), [tq, bt], wr)

    def disc(lre, lim, lst, n, pref):
        o = {}
        for nm in ["step", "mag", "th", "c", "s", "tmp", "nr", "den", "cr", "ci", "a", "b"]:
            o[nm] = small(pref + nm, [128, n])
        al = [o[k] for k in o] + [lre, lim, lst]
        P.op("act", lambda h: h.activation(o["step"].t[:], lst.t[:, 0:n], AF.Exp), al, al)
        P.op("dve", lambda h: h.tensor_tensor(o["th"].t[:], lim.t[:, 0:n], o["step"].t[:], ALU.mult), al, al)
        P.op("dve", lambda h: h.tensor_tensor(o["a"].t[:], lre.t[:, 0:n], o["step"].t[:], ALU.mult), al, al)
        P.op("act", lambda h: h.activation(o["mag"].t[:], o["a"].t[:], AF.Exp), al, al)
        sincos(o["th"].t[:], n, o["s"].t[:], o["c"].t[:], al, al)
        P.op("dve", lambda h: h.tensor_tensor(o["a"].t[:], o["mag"].t[:], o["c"].t[:], ALU.mult), al, al)
        P.op("dve", lambda h: h.tensor_scalar(o["nr"].t[:], o["a"].t[:], 1.0, -1.0, ALU.mult, ALU.add), al, al)
        P.op("dve", lambda h: h.tensor_tensor(o["b"].t[:], o["mag"].t[:], o["s"].t[:], ALU.mult), al, al)
        P.op("dve", lambda h: h.tensor_tensor(o["den"].t[:], lre.t[:, 0:n], lre.t[:, 0:n], ALU.mult), al, al)
        P.op("dve", lambda h: h.tensor_tensor(o["tmp"].t[:], lim.t[:, 0:n], lim.t[:, 0:n], ALU.mult), al, al)
        P.op("dve", lambda h: h.tensor_tensor(o["den"].t[:], o["den"].t[:], o["tmp"].t[:], ALU.add), al, al)
        P.op("dve", lambda h: h.reciprocal(o["den"].t[:], o["den"].t[:]), al, al)
        P.op("dve", lambda h: h.tensor_tensor(o["cr"].t[:], o["nr"].t[:], lre.t[:, 0:n], ALU.mult), al, al)
        P.op("dve", lambda h: h.tensor_tensor(o["tmp"].t[:], o["b"].t[:], lim.t[:, 0:n], ALU.mult), al, al)
        P.op("dve", lambda h: h.tensor_tensor(o["cr"].t[:], o["cr"].t[:], o["tmp"].t[:], ALU.add), al, al)
        P.op("dve", lambda h: h.tensor_tensor(o["cr"].t[:], o["cr"].t[:], o["den"].t[:], ALU.mult), al, al)
        P.op("dve", lambda h: h.tensor_tensor(o["ci"].t[:], o["b"].t[:], lre.t[:, 0:n], ALU.mult), al, al)
        P.op("dve", lambda h: h.tensor_tensor(o["tmp"].t[:], o["nr"].t[:], lim.t[:, 0:n], ALU.mult), al, al)
        P.op("dve", lambda h: h.tensor_tensor(o["ci"].t[:], o["ci"].t[:], o["tmp"].t[:], ALU.subtract), al, al)
        P.op("dve", lambda h: h.tensor_tensor(o["ci"].t[:], o["ci"].t[:], o["den"].t[:], ALU.mult), al, al)
        return o, al

    ds_, als = disc(lre_s, lim_s, lst_s, 16, "ds_")
    dr_, alr = disc(lre_r, lim_r, lst_r, 256, "dr_")
    bbr = small("bbr", [128, 256]); bbi = small("bbi", [128, 256]); tmpr = small("tmpr", [128, 256])
    alr2 = alr + [bbr, bbi, tmpr, bre_r, bim_r]
    P.op("dve", lambda h: h.tensor_tensor(bbr.t[:], dr_["cr"].t[:], bre_r.t[:], ALU.mult), alr2, alr2)
    P.op("dve", lambda h: h.tensor_tensor(tmpr.t[:], dr_["ci"].t[:], bim_r.t[:], ALU.mult), alr2, alr2)
    P.op("dve", lambda h: h.tensor_tensor(bbr.t[:], bbr.t[:], tmpr.t[:], ALU.subtract), alr2, alr2)
    P.op("dve", lambda h: h.tensor_tensor(bbi.t[:], dr_["cr"].t[:], bim_r.t[:], ALU.mult), alr2, alr2)
    P.op("dve", lambda h: h.tensor_tensor(tmpr.t[:], dr_["ci"].t[:], bre_r.t[:], ALU.mult), alr2, alr2)
    P.op("dve", lambda h: h.tensor_tensor(bbi.t[:], bbi.t[:], tmpr.t[:], ALU.add), alr2, alr2)
    BbT = [small("BbT%d" % ri, [128, 16, 128], BF16) for ri in range(2)]
    for ri, src in enumerate([bbr, bbi]):
        for qq in range(4):
            for g2 in range(2):
                m = rmask.t[:, qq * 2 + g2:qq * 2 + g2 + 1]
                o_ap = BbT[ri].t[:].rearrange("p (ft q) c -> p ft q c", q=4)[:, :, qq, g2 * 64:(g2 + 1) * 64]
                i_ap = src.t[:].rearrange("p (ft d) -> p ft d", ft=4)
                P.op("dve", lambda h, o_ap=o_ap, i_ap=i_ap, m=m: h.tensor_scalar(o_ap, i_ap, m, None, ALU.mult), alr2 + [rmask], [BbT[ri]])
    CT = [small("CT%d" % ri, [128, 16, 128], BF16) for ri in range(2)]
    for ri in range(2):
        P.op("dve", lambda h, ri=ri: h.memset(CT[ri].t[:], 0.0), [], [CT[ri]])
    for ri, (src, sgn) in enumerate([(cre_s, 1.0), (cim_s, -1.0)]):
        for pair in range(16):
            qq = pair % 4
            for g2 in range(2):
                m = smask.t[:, g2:g2 + 1]
                col = qq * 32 + g2 * 16
                P.op("dve", lambda h, ri=ri, pair=pair, col=col, m=m, src=src, sgn=sgn: h.tensor_scalar(
                    CT[ri].t[:, pair, col:col + 16], src.t[:, pair, :], m, sgn, ALU.mult, ALU.mult), [src, smask], [CT[ri]])
    rr = ds_["mag"]; th = ds_["th"]
    cth = small("cth", [128, 16]); sth = small("sth", [128, 16])
    P.op("dve", lambda h: h.tensor_copy(cth.t[:], ds_["c"].t[:]), als, [cth])
    P.op("dve", lambda h: h.tensor_copy(sth.t[:], ds_["s"].t[:]), als, [sth])

    tabc = [small("tabc%d" % i, [128, 1024]) for i in range(4)]
    tabs = [small("tabs%d" % i, [128, 1024]) for i in range(4)]
    gr = small("gr", [128, 1024]); gi = small("gi", [128, 1024])
    ttmp = gi; ang = gr
    yr = small("yr", [128, 1024]); yi = small("yi", [128, 1024])
    hrb = small("hrb", [128, 1024], BF16); hib = small("hib", [128, 1024], BF16)
    ytmp = small("ytmp", [128, 512]); ysq = small("ysq", [128, 512])
    stp = small("stp", [128, 16, 2])
    sts = small("sts", [128, 4, 16, 2])
    gin = small("gin", [128, 2]); hend = small("hend", [128, 2])
    P.op("dve", lambda h: h.memset(stp.t[:], 0.0), [], [stp])
    GC = 1.5957691216057308

    def run_seq(pair, qq, ft, col0, T, tc_, ts_, init_re, init_im, init_bufs, out_re, out_im, out_buf, ypsum, yp_off):
        for h0 in range(0, T, 512):
            n = min(512, T - h0)
            pxr = next_pf(); pxi = next_pf()
            P.mm([lambda h, pxr=pxr, n=n, h0=h0: h.matmul(pxr.t[:, 0:n], BbT[0].t[:, pair, :], uT.t[:, ft, col0 + h0:col0 + h0 + n], start=True, stop=True)], [BbT[0], uT], [pxr])
            P.mm([lambda h, pxi=pxi, n=n, h0=h0: h.matmul(pxi.t[:, 0:n], BbT[1].t[:, pair, :], uT.t[:, ft, col0 + h0:col0 + h0 + n], start=True, stop=True)], [BbT[1], uT], [pxi])
            c_ = tc_.t[:, h0:h0 + n]; s_ = ts_.t[:, h0:h0 + n]
            P.op("dve", lambda h, pxr=pxr, c_=c_, n=n, h0=h0: h.tensor_tensor(yr.t[:, h0:h0 + n], pxr.t[:, 0:n], c_, ALU.mult), [pxr, tc_], [yr])
            P.op("dve", lambda h, pxi=pxi, s_=s_, n=n, h0=h0: h.tensor_tensor(gr.t[:, h0:h0 + n], pxi.t[:, 0:n], s_, ALU.mult), [pxi, ts_], [gr])
            P.op("pool", lambda h, n=n, h0=h0: h.tensor_tensor(yr.t[:, h0:h0 + n], yr.t[:, h0:h0 + n], gr.t[:, h0:h0 + n], ALU.add), [yr, gr], [yr])
            P.op("dve", lambda h, pxi=pxi, c_=c_, n=n, h0=h0: h.tensor_tensor(yi.t[:, h0:h0 + n], pxi.t[:, 0:n], c_, ALU.mult), [pxi, tc_], [yi])
            P.op("dve", lambda h, pxr=pxr, s_=s_, n=n, h0=h0: h.tensor_tensor(gi.t[:, h0:h0 + n], pxr.t[:, 0:n], s_, ALU.mult), [pxr, ts_], [gi])
            P.op("pool", lambda h, n=n, h0=h0: h.tensor_tensor(yi.t[:, h0:h0 + n], yi.t[:, h0:h0 + n], gi.t[:, h0:h0 + n], ALU.subtract), [yi, gi], [yi])
        ct = cth.t[:, pair:pair + 1]; st_ = sth.t[:, pair:pair + 1]
        P.op("dve", lambda h: h.tensor_scalar(gin.t[:, 0:1], init_re, ct, None, ALU.mult), init_bufs + [cth], [gin])
        P.op("dve", lambda h: h.scalar_tensor_tensor(gin.t[:, 0:1], init_im, st_, gin.t[:, 0:1], ALU.mult, ALU.subtract), init_bufs + [sth, gin], [gin])
        P.op("dve", lambda h: h.tensor_scalar(gin.t[:, 0:1], gin.t[:, 0:1], -1.0, None, ALU.mult), [gin], [gin])
        P.op("dve", lambda h: h.tensor_scalar(gin.t[:, 1:2], init_re, st_, None, ALU.mult), init_bufs + [sth], [gin])
        P.op("dve", lambda h: h.scalar_tensor_tensor(gin.t[:, 1:2], init_im, ct, gin.t[:, 1:2], ALU.mult, ALU.add), init_bufs + [cth, gin], [gin])
        rb = rr.t[:, pair:pair + 1].to_broadcast([128, T])
        P.op("dve", lambda h: h.tensor_tensor_scan(gr.t[:, 0:T], rb, yr.t[:, 0:T], gin.t[:, 0:1], ALU.mult, ALU.add), [yr, gin] + als, [gr])
        P.op("dve", lambda h: h.tensor_tensor_scan(gi.t[:, 0:T], rb, yi.t[:, 0:T], gin.t[:, 1:2], ALU.mult, ALU.add), [yi, gin] + als, [gi])
        c_ = tc_.t[:, 0:T]; s_ = ts_.t[:, 0:T]
        P.op("dve", lambda h: h.tensor_tensor(yr.t[:, 0:T], gr.t[:, 0:T], c_, ALU.mult), [gr, tc_], [yr])
        P.op("pool", lambda h: h.tensor_tensor(yi.t[:, 0:T], gi.t[:, 0:T], s_, ALU.mult), [gi, ts_], [yi])
        P.op("dve", lambda h: h.tensor_tensor(hrb.t[:, 0:T], yr.t[:, 0:T], yi.t[:, 0:T], ALU.subtract), [yr, yi], [hrb])
        P.op("dve", lambda h: h.tensor_tensor(hend.t[:, 0:1], yr.t[:, T - 1:T], yi.t[:, T - 1:T], ALU.subtract), [yr, yi], [hend])
        P.op("pool", lambda h: h.tensor_tensor(yr.t[:, 0:T], gr.t[:, 0:T], s_, ALU.mult), [gr, ts_, hrb, hend], [yr])
        P.op("dve", lambda h: h.tensor_tensor(yi.t[:, 0:T], gi.t[:, 0:T], c_, ALU.mult), [gi, tc_, hrb, hend], [yi])
        P.op("dve", lambda h: h.tensor_tensor(hib.t[:, 0:T], yr.t[:, 0:T], yi.t[:, 0:T], ALU.add), [yr, yi], [hib])
        P.op("dve", lambda h: h.tensor_tensor(hend.t[:, 1:2], yr.t[:, T - 1:T], yi.t[:, T - 1:T], ALU.add), [yr, yi], [hend])
        P.op("dve", lambda h: h.tensor_copy(out_re, hend.t[:, 0:1]), [hend], [out_buf])
        P.op("dve", lambda h: h.tensor_copy(out_im, hend.t[:, 1:2]), [hend], [out_buf])
        for h0 in range(0, T, 512):
            n = min(512, T - h0)
            yp_ = ypsum[(yp_off + h0) // 512]
            P.mm([lambda h, yp_=yp_, n=n, h0=h0: h.matmul(yp_.t[:, 0:n], CT[0].t[:, pair, :], hrb.t[:, h0:h0 + n], start=(qq == 0), stop=False),
                  lambda h, yp_=yp_, n=n, h0=h0: h.matmul(yp_.t[:, 0:n], CT[1].t[:, pair, :], hib.t[:, h0:h0 + n], start=False, stop=(qq == 3))],
                 [CT[0], CT[1], hrb, hib], [yp_])

    def y_evac(yp_, ft, col0, n):
        P.op("dve", lambda h: h.scalar_tensor_tensor(ytmp.t[:, 0:n], uT.t[:, ft, col0:col0 + n], dsk.t[:, ft:ft + 1], yp_.t[:, 0:n], ALU.mult, ALU.add),
             [uT, dsk, yp_], [ytmp])
        P.op("dve", lambda h: h.tensor_tensor(ysq.t[:, 0:n], ytmp.t[:, 0:n], ytmp.t[:, 0:n], ALU.mult), [ytmp], [ysq])
        P.op("dve", lambda h: h.tensor_scalar(ysq.t[:, 0:n], ysq.t[:, 0:n], 0.044715, 1.0, ALU.mult, ALU.add), [ysq], [ysq])
        P.op("dve", lambda h: h.tensor_tensor(ysq.t[:, 0:n], ysq.t[:, 0:n], ytmp.t[:, 0:n], ALU.mult), [ysq, ytmp], [ysq])
        P.op("act", lambda h: h.activation(ysq.t[:, 0:n], ysq.t[:, 0:n], AF.Sigmoid, scale=GC), [ysq], [ysq])
        P.op("dve", lambda h: h.tensor_tensor(ygT.t[:, ft, col0:col0 + n], ysq.t[:, 0:n], ytmp.t[:, 0:n], ALU.mult), [ysq, ytmp], [ygT])

    ypb = [pf[4], pf[5]]
    for ft in range(4):
        for qq in range(4):
            pair = ft * 4 + qq
            P.op("dve", lambda h, pair=pair: h.tensor_scalar(ang.t[:], iota.t[:], th.t[:, pair:pair + 1], None, ALU.mult), [iota] + als, [ang])
            sincos(ang.t[:], 1024, tabs[qq].t[:], tabc[qq].t[:], [ang], [tabs[qq], tabc[qq]])
        for seg in range(4):
            for qq in range(4):
                pair = ft * 4 + qq
                run_seq(pair, qq, ft, seg * NCH, 1024, tabc[qq], tabs[qq],
                        stp.t[:, pair, 0:1], stp.t[:, pair, 1:2], [stp],
                        stp.t[:, pair, 0:1], stp.t[:, pair, 1:2], stp, ypb, 0)
            for hf in range(2):
                y_evac(ypb[hf], ft, seg * NCH + hf * 512, 512)
        for sb_i in range(4):
            for qq in range(4):
                pair = ft * 4 + qq
                run_seq(pair, qq, ft, sb_i * NCH + 1024, 8, tabc[qq], tabs[qq],
                        sre0.t[:, sb_i, pair:pair + 1], sim0.t[:, sb_i, pair:pair + 1], [sre0, sim0],
                        sts.t[:, sb_i, pair, 0:1], sts.t[:, sb_i, pair, 1:2], sts, ypb, 0)
            y_evac(ypb[0], ft, sb_i * NCH + 1024, 8)
    stp2 = small("stp2", [128, 2, 16]); sts2 = small("sts2", [128, 4, 2, 16])
    if STOP == 'SSM':
        for r_ in range(4):
            P.dma("pool", OUT["yp"].ap().rearrange("(p f x) c -> p f (x c)", p=128, f=4)[:, :, r_ * 1024:(r_ + 1) * 1024],
                  ygT.t[:, :, r_ * NCH:r_ * NCH + 1024], [ygT], [outbufs["yp"]], ygT)
        P.dma("pool", OUT["ys"].ap()[:, 0:128].rearrange("p (f r s) -> p f r s", f=4, r=4),
              ygT.t[:].rearrange("p f (r c) -> p f r c", r=4)[:, :, :, 1024:1032], [ygT], [outbufs["ys"]], ygT)
    P.op("dve", lambda h: h.tensor_copy(stp2.t[:], stp.t[:].rearrange("p a b -> p b a")), [stp], [stp2])
    P.op("dve", lambda h: h.tensor_copy(sts2.t[:], sts.t[:].rearrange("p s a b -> p s b a")), [sts], [sts2])
    P.dma("sp", OUT["ssm_p"].ap(), stp2.t[:], [stp2], [outbufs["ssm_p"]], stp2)
    P.dma("sp", OUT["ssm_s"].ap(), sts2.t[:], [sts2], [outbufs["ssm_s"]], sts2)
    for j in range(8):
        P.dma("sp", y_src[j].t.ap(), ygT.t[(j % 2) * 64:(j % 2) * 64 + 64, j // 2, :], [ygT], [y_src[j]], ygT)
        P.coll(y_src[j], y_dst[j], GROUPS)

    if STOP == 'SSM':
        P.barrier()
        return
    P.barrier()
    P.off = CONST_END
    y2T = P.sb("y2T", [128, KT, NO], BF16)
    F0 = P.off
    ygo = P.sb("ygo", [128, KT, NCH], BF16)
    hn1o = P.sb("hn1o", [128, KT, NCH], BF16)
    ych = [P.sb("ych%d" % i, [128, 4, NCH], BF16) for i in range(2)]
    wt2 = [P.sb("wt2_%d" % i, [128, KT, 128], BF16) for i in range(4)]
    gl = P.sb("gl", [128, NCH], F32); zz = P.sb("zz", [128, NCH], F32)
    for j in range(8):
        P.dma("sp", hn1o.t[:, 2 * j:2 * j + 2, :], hn_src[j].t.ap().rearrange("(k p) t -> p k t", p=128), [hn_src[j]], [hn1o], hn1o)
    P.op("pool", lambda h: h.memset(y2T.t[:, :, NCH:NO], 0.0), [], [y2T])
    ci = 0
    for rf in range(4):
        for r in range(4):
            yc = ych[ci % 2]; ci += 1
            for j in range(8):
                P.dma("sp", yc.t[(j % 2) * 64:(j % 2) * 64 + 64, j // 2, :], y_dst[j].t.ap()[rf * 64:(rf + 1) * 64, r * NCH:(r + 1) * NCH],
                      [y_dst[j]], [yc], yc)
            dst = ygo.t[:, rf * 4:(rf + 1) * 4, :]
            if r == 0:
                P.op("dve", lambda h, yc=yc, dst=dst: h.tensor_scalar(dst, yc.t[:], sel.t[:, 0:1], None, ALU.mult), [yc, sel], [ygo])
            else:
                P.op("dve", lambda h, yc=yc, dst=dst, r=r: h.scalar_tensor_tensor(dst, yc.t[:], sel.t[:, r:r + 1], dst, ALU.mult, ALU.add), [yc, sel, ygo], [ygo])
    if STOP == 'G1':
        P.barrier()
        return
    for nt in range(16):
        wa = wt2[2 * (nt % 2)]
        wb_ = wt2[2 * (nt % 2) + 1]
        P.dma("pool", wa.t[:], IN["w_glu"].ap()[:, nt * 128:(nt + 1) * 128].rearrange("(kt p) c -> p kt c", p=128), [], [wa], wa)
        P.dma("pool", wb_.t[:], IN["w_in_c"].ap()[:, 2048 + nt * 128:2048 + (nt + 1) * 128].rearrange("(kt p) c -> p kt c", p=128), [], [wb_], wb_)
        for (c0, n) in [(0, 512), (512, 512), (1024, 8)]:
            proj_feat(wa, None, lambda pk, c0=c0, n=n, nt=nt: P.op("act", lambda h: h.activation(gl.t[:, c0:c0 + n], pk.t[:, 0:n], AF.Sigmoid, bias=bglu.t[:, nt:nt + 1], scale=1.0), [pk, bglu], [gl]),
                      ygo, lambda kt, c0=c0, n=n: ygo.t[:, kt, c0:c0 + n], n)
            def zev(pk, c0=c0, n=n):
                P.op("act", lambda h: h.activation(zz.t[:, c0:c0 + n], pk.t[:, 0:n], AF.Sigmoid), [pk], [zz])
                P.op("dve", lambda h: h.tensor_tensor(zz.t[:, c0:c0 + n], zz.t[:, c0:c0 + n], pk.t[:, 0:n], ALU.mult), [zz, pk], [zz])
            proj_feat(wb_, None, zev, hn1o, lambda kt, c0=c0, n=n: hn1o.t[:, kt, c0:c0 + n], n)
        P.op("dve", lambda h, nt=nt: h.tensor_tensor(gl.t[:], gl.t[:], ygo.t[:, nt, :], ALU.mult), [gl, ygo], [gl])
        P.op("dve", lambda h, nt=nt: h.tensor_tensor(y2T.t[:, nt, 0:NCH], gl.t[:], zz.t[:], ALU.mult), [gl, zz], [y2T])
    if STOP == 'G2':
        P.barrier()
        return
    P.barrier()
    P.off = F0
    wgb2 = [P.sb("wgc%d" % i, [128, KT, 512], BF16) for i in range(2)]
    h1s = [P.sb("h1f%d" % i, [128, D], F32) for i in range(5)]
    junk = P.sb("junk3", [128, D], F32); ss = P.sb("ss3", [128, 4], F32)
    yo = [P.sb("yo%d" % i, [128, D], F32) for i in range(2)]
    load_g("g_fin")
    wi = 0
    for tiles in [list(range(0, 5)), list(range(5, 9))]:
        for si, tj in enumerate(tiles):
            o0 = tj * 128
            P.dma("sp", h1s[si].t[:], h1_scr.t.ap()[o0:o0 + 128, :], [h1_scr], [h1s[si]], h1s[si])
        for g4 in range(4):
            wg = wgb2[wi % 2]
            wi += 1
            P.dma("pool", wg.t[:], IN["w_out_c"].ap()[:, g4 * 512:(g4 + 1) * 512].rearrange("(kt p) c -> p kt c", p=128), [], [wg], wg)
            for si, tj in enumerate(tiles):
                o0 = tj * 128
                ht_ = h1s[si]
                pk = next_pf()
                fns = [(lambda h, pk=pk, kt=kt, o0=o0, wg=wg: h.matmul(pk.t[:], y2T.t[:, kt, o0:o0 + 128], wg.t[:, kt, :],
                                                                         start=(kt == 0), stop=(kt == KT - 1))) for kt in range(KT)]
                P.mm(fns, [y2T, wg], [pk])
                P.op("dve", lambda h, pk=pk, ht_=ht_, g4=g4: h.tensor_tensor(ht_.t[:, g4 * 512:(g4 + 1) * 512], ht_.t[:, g4 * 512:(g4 + 1) * 512], pk.t[:], ALU.add),
                     [pk, ht_], [ht_])
        for si, tj in enumerate(tiles):
            o0 = tj * 128
            ht_ = h1s[si]
            yo_ = yo[tj % 2]
            rmsnorm_rows(ht_, yo_, ss, junk)
            if tj < 8:
                P.dma("sp", OUT["yp"].ap()[o0:o0 + 128, :], yo_.t[:], [yo_], [outbufs["yp"]], yo_)
            else:
                P.dma("sp", OUT["ys"].ap(), yo_.t[:], [yo_], [outbufs["ys"]], yo_)
    P.barrier()


_NC_CACHE = {}


def _rope_tables(pos):
    half = 64
    inv = (np.float32(10000.0) ** (-np.arange(half, dtype=np.float32) / np.float32(half))).astype(np.float32)
    ang = pos.astype(np.float32)[:, None] * inv[None, :]
    return np.cos(ang).astype(np.float32), np.sin(ang).astype(np.float32)


def kernel(x_prompt, x_sample, cache_win_k, cache_win_v, state_conv, state_ssm_re, state_ssm_im,
           attn_norm, w_in_ab, conv_w, w_out_ab, ssm_norm, w_in_c, lam_re, lam_im, log_step,
           b_re, b_im, c_re, c_im, d_skip, w_glu, b_glu, w_out_c, final_norm):
    f = lambda a: np.ascontiguousarray(np.asarray(a, dtype=np.float32))
    x_prompt, x_sample = f(x_prompt), f(x_sample)
    cache_win_k, cache_win_v, state_conv = f(cache_win_k), f(cache_win_v), f(state_conv)
    state_ssm_re, state_ssm_im = f(state_ssm_re), f(state_ssm_im)
    w_in_ab0, w_out_ab0, w_in_c0, w_glu0, w_out_c0 = f(w_in_ab)[0], f(w_out_ab)[0], f(w_in_c)[0], f(w_glu)[0], f(w_out_c)[0]
    lam_re, lam_im, log_step = f(lam_re)[0], f(lam_im)[0], f(log_step)[0]
    b_re, b_im, c_re, c_im = f(b_re)[0], f(b_im)[0], f(c_re)[0], f(c_im)[0]
    d_skip0, b_glu0 = f(d_skip)[0], f(b_glu)[0]
    if "nc" not in _NC_CACHE:
        _NC_CACHE["nc"] = build_nc()
    nc = _NC_CACHE["nc"]

    kk = np.arange(128)[:, None]
    qq_ = np.arange(512)[None, :]
    maskp = np.stack([mult_of(qq_ - ((i - 16) * 128 + kk)) for i in range(20)], 1)
    rows = np.arange(2176).reshape(17, 128)
    s_ = np.arange(128)[None, :]
    masks = np.zeros((128, 17, 128), np.float32)
    for i in range(17):
        row = rows[i][:, None]
        m = mult_of(2048 + s_ - row)
        m[:, 8:] = ((2048 + s_[:, 8:] - row) == 0)
        masks[:, i, :] = m
    iota = np.broadcast_to(np.arange(1024, dtype=np.float32)[None, :], (128, 1024)).copy()
    rmask = np.zeros((128, 8), np.float32)
    for p in range(128):
        rmask[p, (p // 32) * 2 + (p % 32) // 16] = 1.0
    smask = np.zeros((128, 2), np.float32)
    smask[:64, 0] = 1.0
    smask[64:, 1] = 1.0
    bc = lambda v: np.ascontiguousarray(np.broadcast_to(v[None, :], (128, v.shape[0])))

    in_maps = []
    for c in range(8):
        b, r = c // 4, c % 4
        T0 = r * NOWN
        xh = np.zeros((NTP, D), np.float32)
        lo = T0 - NHALO
        src_lo = max(lo, 0)
        xh[src_lo - lo:] = x_prompt[b, src_lo:T0 + NOWN]
        pos = np.concatenate([np.arange(lo, T0 + NOWN), PAST + np.arange(128)]).astype(np.float32)
        valid = (pos[:NTP] >= 0).astype(np.float32)
        cosv, sinv = _rope_tables(np.maximum(pos, 0))
        xs = np.zeros((128, D), np.float32)
        xs[:8] = x_sample[c]
        g0 = 32 * r
        gs = slice(g0, g0 + 32)
        st_lay = lambda a: np.ascontiguousarray(a.reshape(16, 2, 64).transpose(1, 2, 0).reshape(128, 16))
        def row_lay_rep(a):
            t = a.reshape(4, 4, 2, 64)
            t = np.broadcast_to(t[:, :, :, None, :], (4, 4, 2, 16, 64))
            return np.ascontiguousarray(t.transpose(1, 2, 3, 0, 4).reshape(128, 4, 64))
        def row_lay_b(a):
            t = a.reshape(4, 4, 2, 64, 16)
            return np.ascontiguousarray(t.transpose(1, 2, 4, 0, 3).reshape(128, 4, 64))
        def st_lay_c(a):
            t = a.reshape(16, 2, 16, 64)
            return np.ascontiguousarray(t.transpose(1, 3, 0, 2).reshape(128, 16, 16))
        lst32 = np.broadcast_to(log_step[gs][:, None], (32, 64))
        sel = np.zeros((128, 4), np.float32)
        sel[:, r] = 1.0
        sre0 = np.stack([st_lay(state_ssm_re[0, 4 * b + i, gs]) for i in range(4)], 1)
        sim0 = np.stack([st_lay(state_ssm_im[0, 4 * b + i, gs]) for i in range(4)], 1)
        w_in_c_rolled = np.concatenate([w_in_c0[:, 512 * r:512 * (r + 1)], w_in_c0[:, 512:2048], w_in_c0[:, 2048:]], 1)
        m = {
            "xh": xh, "xs": xs,
            "cs": np.ascontiguousarray(cosv.reshape(25, 128, 64).transpose(1, 0, 2)),
            "sn": np.ascontiguousarray(sinv.reshape(25, 128, 64).transpose(1, 0, 2)),
            "valid": np.ascontiguousarray(valid.reshape(24, 128).T),
            "ck": np.ascontiguousarray(cache_win_k[0, c].reshape(2048, 1024)),
            "cv": np.ascontiguousarray(cache_win_v[0, c].reshape(2048, 1024)),
            "sconv": np.ascontiguousarray(state_conv[0, c].reshape(2, 8, 128).transpose(2, 1, 0)),
            "g_attn": bc(f(attn_norm)[0]), "g_ssm": bc(f(ssm_norm)[0]), "g_fin": bc(f(final_norm)),
            "w_in_ab": w_in_ab0, "cw": np.ascontiguousarray(f(conv_w)[0].reshape(3, 8, 128).transpose(2, 1, 0)),
            "w_out_ab": w_out_ab0, "w_in_c": np.ascontiguousarray(w_in_c_rolled),
            "w_glu": w_glu0, "w_out_c": w_out_c0,
            "bglu": np.ascontiguousarray(b_glu0.reshape(16, 128).T),
            "dsk": np.ascontiguousarray(d_skip0[512 * r:512 * (r + 1)].reshape(4, 128).T),
            "maskp": maskp, "masks": masks,
            "lre_s": st_lay(lam_re[gs]), "lim_s": st_lay(lam_im[gs]), "lst_s": st_lay(lst32),
            "lre_r": row_lay_rep(lam_re[gs]), "lim_r": row_lay_rep(lam_im[gs]), "lst_r": row_lay_rep(np.ascontiguousarray(lst32)),
            "bre_r": row_lay_b(b_re[gs]), "bim_r": row_lay_b(b_im[gs]),
            "cre_s": st_lay_c(c_re[gs]), "cim_s": st_lay_c(c_im[gs]),
            "rmask": rmask, "smask": smask, "sel": sel, "sre0": sre0, "sim0": sim0, "iota": iota,
        }
        in_maps.append({k: np.ascontiguousarray(v, dtype=np.float32) for k, v in m.items()})

    res = run_bass_kernel_spmd(nc, in_maps, core_ids=list(range(8)))
    R = res.results
    _NC_CACHE['raw'] = R
    y_prompt = np.zeros((2, SEQ, D), np.float32)
    y_sample = np.zeros((8, 8, D), np.float32)
    kp = np.zeros((1, 2, 2048, 8, 128), np.float32)
    vp = np.zeros((1, 2, 2048, 8, 128), np.float32)
    convp = np.zeros((1, 2, 2, 1024), np.float32)
    srp = np.zeros((1, 2, 128, 64), np.float32)
    sip = np.zeros((1, 2, 128, 64), np.float32)
    ks = np.zeros((1, 8, 8, 8, 128), np.float32)
    vs = np.zeros((1, 8, 8, 8, 128), np.float32)
    convs = np.zeros((1, 8, 2, 1024), np.float32)
    srs = np.zeros((1, 8, 128, 64), np.float32)
    sis = np.zeros((1, 8, 128, 64), np.float32)
    unst = lambda a: a.reshape(2, 64, 16).transpose(2, 0, 1).reshape(32, 64)
    for c in range(8):
        b, r = c // 4, c % 4
        o = R[c]
        y_prompt[b, r * NOWN:(r + 1) * NOWN] = o["yp"]
        y_sample[c] = o["ys"][:8]
        if r >= 2:
            kp[0, b, (r - 2) * NOWN:(r - 1) * NOWN] = o["kp"].reshape(NOWN, 8, 128)
            vp[0, b, (r - 2) * NOWN:(r - 1) * NOWN] = o["vp"].reshape(NOWN, 8, 128)
        if r == 3:
            convp[0, b] = o["convp"].transpose(2, 1, 0).reshape(2, 1024)
        ks[0, c] = o["ks"][:8].reshape(8, 8, 128)
        vs[0, c] = o["vs"][:8].reshape(8, 8, 128)
        convs[0, c] = o["convs"].transpose(2, 1, 0).reshape(2, 1024)
        srp[0, b, 32 * r:32 * (r + 1)] = unst(o["ssm_p"][:, 0, :])
        sip[0, b, 32 * r:32 * (r + 1)] = unst(o["ssm_p"][:, 1, :])
        for i in range(4):
            srs[0, 4 * b + i, 32 * r:32 * (r + 1)] = unst(o["ssm_s"][:, i, 0, :])
            sis[0, 4 * b + i, 32 * r:32 * (r + 1)] = unst(o["ssm_s"][:, i, 1, :])
    return (y_prompt, y_sample, kp, vp, convp, srp, sip, ks, vs, convs, srs, sis)
```

```python
import math
import os
STOP = os.environ.get('MK_STOP', '')
from contextlib import ExitStack

import numpy as np
import concourse.bass as bass
import concourse.mybir as mybir
from concourse.bass_utils import run_bass_kernel_spmd

F32 = mybir.dt.float32
BF16 = mybir.dt.bfloat16
ALU = mybir.AluOpType
AF = mybir.ActivationFunctionType
AX = mybir.AxisListType

ENGS = ["pe", "act", "dve", "pool", "sp"]
D = 2048
KT = 16
NOWN = 1024
NHALO = 2048
NTP = NOWN + NHALO
NTILE_P = NTP // 128
NO = NOWN + 128
SEQ = 4096
PAST = 16384
NCH = 1032
TWO_PI = 2.0 * math.pi


class Buf:
    def __init__(self, t, name):
        self.t = t
        self.name = name
        self.w = {}
        self.r = {}
        self.dsem = None
        self.dcnt = 0


class Prog:
    def __init__(self, nc, stack):
        self.nc = nc
        self.stack = stack
        self.q = {e: [] for e in ENGS}
        self.cnt = {e: 0 for e in ENGS}
        self.seen = {e: {} for e in ENGS}
        self.sems = {}
        self.semval = {}
        for e in ["pe", "act", "dve", "pool"]:
            self.sems[e] = stack.enter_context(nc.semaphore("s_" + e))
        self.off = 16512
        self.free = []
        self.dval = {}
        self.phase_bufs = []

    def sb(self, name, shape, dt, at=None):
        nbytes = int(np.prod(shape[1:])) * (2 if dt == BF16 else 4)
        if at is None:
            at = self.off
            self.off = (at + nbytes + 63) // 64 * 64
        assert at + nbytes <= 229300, (name, at, nbytes)
        t = self.nc.alloc_sbuf_tensor_at(name, list(shape), dt, offset=at)
        b = Buf(t, name)
        b.at = at
        b.nbytes = nbytes
        return b

    def ps(self, name, shape, dt=F32):
        t = self.stack.enter_context(self.nc.psum_tensor(name, list(shape), dt))
        return Buf(t, name)

    def dram(self, name, shape, dt, kind="Internal"):
        t = self.nc.dram_tensor(name, list(shape), dt, kind=kind)
        return Buf(t, name)

    def _need(self, eng, k, v, waits):
        if self.seen[eng].get(k, 0) >= v:
            return
        waits[k] = max(waits.get(k, 0), v)

    def _deps(self, eng, reads, writes):
        waits = {}
        for b in reads:
            for k, v in b.w.items():
                self._need(eng, k, v, waits)
        for b in writes:
            for k, v in b.w.items():
                self._need(eng, k, v, waits)
            for k, v in b.r.items():
                self._need(eng, k, v, waits)
        for k, v in waits.items():
            self.seen[eng][k] = v
        return [(self.sems[k], v) for k, v in waits.items()]

    def _commit(self, k, v, reads, writes):
        self.semval[k] = v
        for b in reads:
            b.r[k] = max(b.r.get(k, 0), v)
        for b in writes:
            b.w[k] = max(b.w.get(k, 0), v)
            b.r = {}

    def op(self, eng, fn, reads=(), writes=()):
        reads = [b for b in reads if b is not None]
        writes = [b for b in writes if b is not None]
        wl = self._deps(eng, reads, writes)
        self.cnt[eng] += 1
        sem = self.sems[eng]

        def emit(h, fn=fn, wl=wl, sem=sem):
            for s, v in wl:
                h.wait_ge(s, v)
            fn(h).then_inc(sem, 1)

        self.q[eng].append(emit)
        self._commit(eng, self.cnt[eng], reads, writes)

    def mm(self, fns, reads, writes):
        eng = "pe"
        wl = self._deps(eng, reads, writes)
        self.cnt[eng] += 1
        sem = self.sems[eng]

        def emit(h, fns=fns, wl=wl, sem=sem):
            for s, v in wl:
                h.wait_ge(s, v)
            for f in fns[:-1]:
                f(h)
            fns[-1](h).then_inc(sem, 1)

        self.q[eng].append(emit)
        self._commit(eng, self.cnt[eng], reads, writes)

    def dma(self, eng, out, in_, reads, writes, semb, **kw):
        reads = [b for b in reads if b is not None]
        writes = [b for b in writes if b is not None]
        if semb.dsem is None:
            if self.free:
                key = self.free.pop()
            else:
                key = "d%d" % len(self.sems)
                self.sems[key] = self.stack.enter_context(self.nc.semaphore(key))
            semb.dsem = key
            semb.dcnt = self.dval.get(key, 0)
            self.phase_bufs.append(semb)
        wl = self._deps(eng, reads, writes)
        semb.dcnt += 16
        self.dval[semb.dsem] = semb.dcnt
        sem = self.sems[semb.dsem]

        def emit(h, wl=wl, sem=sem, out=out, in_=in_, kw=kw):
            for s, v in wl:
                h.wait_ge(s, v)
            h.dma_start(out=out, in_=in_, **kw).then_inc(sem, 16)

        self.q[eng].append(emit)
        self._commit(semb.dsem, semb.dcnt, reads, writes)

    def coll(self, src, dst, groups):
        key = "c%d" % len(self.sems)
        self.sems[key] = self.stack.enter_context(self.nc.semaphore(key))
        wl = self._deps("pool", [src], [dst])
        sem = self.sems[key]

        def emit(h, wl=wl, sem=sem):
            for s, v in wl:
                h.wait_ge(s, v)
            h.collective_compute("AllGather", ALU.bypass, replica_groups=groups,
                                 ins=[src.t.ap()], outs=[dst.t.ap()]).then_inc(sem)
            h.wait_ge(sem, 1)

        self.q["pool"].append(emit)
        self.cnt["pool"] += 1
        s2 = self.sems["pool"]
        self.q["pool"].append(lambda h, s2=s2: h.engine_nop().then_inc(s2, 1))
        self._commit("pool", self.cnt["pool"], [src], [dst])

    def barrier(self):
        for b in self.phase_bufs:
            self.free.append(b.dsem)
            b.dsem = None
        self.phase_bufs = []
        items = list(self.semval.items())
        for e in ENGS:
            wl = []
            for k, v in items:
                if self.seen[e].get(k, 0) < v:
                    self.seen[e][k] = v
                    wl.append((self.sems[k], v))

            def emit(h, wl=wl):
                for s, v in wl:
                    h.wait_ge(s, v)

            if wl:
                self.q[e].append(emit)

    def run(self):
        nc = self.nc
        with nc.Block() as block:
            @block.tensor
            def _(h):
                for f in self.q["pe"]:
                    f(h)

            @block.scalar
            def _(h):
                for f in self.q["act"]:
                    f(h)

            @block.vector
            def _(h):
                for f in self.q["dve"]:
                    f(h)

            @block.gpsimd
            def _(h):
                for f in self.q["pool"]:
                    f(h)

            @block.sync
            def _(h):
                for f in self.q["sp"]:
                    f(h)


def mult_of(d):
    d = np.asarray(d)
    m = ((d >= 0) & (d <= 128)).astype(np.float32)
    m += ((d >= 0) & (d <= 512) & (d % 4 == 0))
    m += ((d >= 0) & (d <= 2048) & (d % 16 == 0))
    return m.astype(np.float32)


IN_SPECS = [
    ("xh", [NTP, D]), ("xs", [128, D]), ("cs", [128, 25, 64]), ("sn", [128, 25, 64]),
    ("valid", [128, 24]), ("ck", [2048, 1024]), ("cv", [2048, 1024]), ("sconv", [128, 8, 2]),
    ("g_attn", [128, D]), ("g_ssm", [128, D]), ("g_fin", [128, D]),
    ("w_in_ab", [D, 8192]), ("cw", [128, 8, 3]), ("w_out_ab", [D, D]), ("w_in_c", [D, 4096]),
    ("w_glu", [D, D]), ("w_out_c", [D, D]), ("bglu", [128, 16]), ("dsk", [128, 4]),
    ("maskp", [128, 20, 512]), ("masks", [128, 17, 128]),
    ("lre_s", [128, 16]), ("lim_s", [128, 16]), ("lst_s", [128, 16]),
    ("lre_r", [128, 4, 64]), ("lim_r", [128, 4, 64]), ("lst_r", [128, 4, 64]),
    ("bre_r", [128, 4, 64]), ("bim_r", [128, 4, 64]),
    ("cre_s", [128, 16, 16]), ("cim_s", [128, 16, 16]),
    ("rmask", [128, 8]), ("smask", [128, 2]), ("sel", [128, 4]),
    ("sre0", [128, 4, 16]), ("sim0", [128, 4, 16]), ("iota", [128, 1024]),
]
OUT_SPECS = [
    ("yp", [NOWN, D]), ("ys", [128, D]), ("kp", [NOWN, 1024]), ("vp", [NOWN, 1024]),
    ("convp", [128, 8, 2]), ("ssm_p", [128, 2, 16]), ("ks", [128, 1024]), ("vs", [128, 1024]),
    ("convs", [128, 8, 2]), ("ssm_s", [128, 4, 2, 16]),
]


def build_nc():
    nc = bass.Bass("TRN2", target_bir_lowering=False)
    IN = {}
    for n, s in IN_SPECS:
        IN[n] = nc.dram_tensor(n, s, F32, kind="ExternalInput")
    OUT = {}
    for n, s in OUT_SPECS:
        OUT[n] = nc.dram_tensor(n, s, F32, kind="ExternalOutput")
    st = ExitStack()
    with st:
        P = Prog(nc, st)
        build_program(nc, P, IN, OUT)
        P.run()
    return nc


def build_program(nc, P, IN, OUT):
    GROUPS = [[0, 1, 2, 3], [4, 5, 6, 7]]
    outbufs = {n: Buf(OUT[n], n) for n in OUT}
    kT_scr = P.dram("kT_scr", [8, 128, NTP], BF16)
    v_scr = P.dram("v_scr", [NTP, 1024], BF16)
    kTs_scr = P.dram("kTs_scr", [8, 128, 2176], BF16)
    vs_scr = P.dram("vs_scr", [2176, 1024], BF16)
    qT_scr = P.dram("qT_scr", [8, 128, NO], BF16)
    h1_scr = P.dram("h1_scr", [NO, D], F32)
    hn_src = [P.dram("hn_src%d" % j, [256, NCH], BF16) for j in range(8)]
    hn_dst = [P.dram("hn_dst%d" % j, [4 * 256, NCH], BF16) for j in range(8)]
    y_src = [P.dram("y_src%d" % j, [64, 4 * NCH], BF16) for j in range(8)]
    y_dst = [P.dram("y_dst%d" % j, [4 * 64, 4 * NCH], BF16) for j in range(8)]

    pf = [P.ps("pf%d" % i, [128, 512], F32) for i in range(6)]
    pb = [P.ps("pb%d" % i, [128, 8, 128], BF16) for i in range(2)]
    pfi = [0]
    pbi = [0]

    def next_pf():
        pfi[0] = (pfi[0] + 1) % 4
        return pf[pfi[0]]

    def next_pb():
        pbi[0] = (pbi[0] + 1) % 2
        return pb[pbi[0]]

    ident = P.sb("ident", [128, 128], BF16)
    P.op("pool", lambda h: h.memset(ident.t[:], 1.0), [], [ident])
    P.op("pool", lambda h: h.affine_select(ident.t[:], ident.t[:], [[-1, 128]], ALU.is_equal, 0.0,
                                            base=0, channel_multiplier=1), [ident], [ident])
    ones_bf = P.sb("ones_bf", [128, 128], BF16)
    P.op("pool", lambda h: h.memset(ones_bf.t[:], 1.0), [], [ones_bf])
    gt = P.sb("gt", [128, D], F32)
    cs = P.sb("cs", [128, 25, 64], F32)
    sn = P.sb("sn", [128, 25, 64], F32)
    valid = P.sb("valid", [128, 24], F32)
    validB = P.sb("validB", [128, 24, 128], BF16)
    cw = P.sb("cw", [128, 8, 3], F32)
    bglu = P.sb("bglu", [128, 16], F32)
    dsk = P.sb("dsk", [128, 4], F32)
    sel = P.sb("sel", [128, 4], F32)
    eps_t = P.sb("eps_t", [128, 1], F32)
    P.op("pool", lambda h: h.memset(eps_t.t[:], 1e-6), [], [eps_t])
    for b_, n in [(cs, "cs"), (sn, "sn"), (valid, "valid"), (cw, "cw"), (bglu, "bglu"), (dsk, "dsk"), (sel, "sel")]:
        P.dma("sp", b_.t[:], IN[n].ap(), [], [b_], b_)
    P.op("dve", lambda h: h.tensor_copy(validB.t[:], valid.t[:].unsqueeze(2).to_broadcast([128, 24, 128])),
         [valid], [validB])
    pospi = P.sb("pospi", [128, 1], F32)
    P.op("pool", lambda h: h.memset(pospi.t[:], math.pi), [], [pospi])
    CONST_END = P.off
    hnT_o = P.sb("hnT_o", [128, KT, NO], BF16)

    def load_g(name):
        P.dma("sp", gt.t[:], IN[name].ap(), [], [gt], gt)

    def rmsnorm_rows(xt, xn, ss, junk):
        P.op("act", lambda h: h.activation(junk.t[:], xt.t[:], AF.Square, accum_out=ss.t[:, 0:1]), [xt], [junk, ss])
        P.op("act", lambda h: h.activation(ss.t[:, 1:2], ss.t[:, 0:1], AF.Sqrt, bias=eps_t.t[:, 0:1], scale=1.0 / D), [ss, eps_t], [ss])
        P.op("dve", lambda h: h.reciprocal(ss.t[:, 2:3], ss.t[:, 1:2]), [ss], [ss])
        P.op("dve", lambda h: h.scalar_tensor_tensor(xn.t[:], xt.t[:], ss.t[:, 2:3], gt.t[:], ALU.mult, ALU.mult),
             [xt, ss, gt], [xn])

    def transpose_rows(xn, dst, dst_ap_fn):
        for half in range(2):
            p = next_pb()
            fns = []
            for j in range(8):
                kt = half * 8 + j
                fns.append(lambda h, p=p, j=j, kt=kt: h.transpose(p.t[:, j, :], xn.t[:, kt * 128:(kt + 1) * 128], ident.t[:]))
            P.mm(fns, [xn, ident], [p])
            P.op("act", lambda h, p=p, half=half: h.activation(dst_ap_fn(half), p.t[:], AF.Identity), [p], [dst])

    A0 = P.off
    wkv = P.sb("wkv", [128, KT, 2048], BF16)
    xts = [P.sb("xt%d" % i, [128, D], F32) for i in range(3)]
    xns = [P.sb("xn%d" % i, [128, D], BF16) for i in range(3)]
    hts = [P.sb("ht%d" % i, [128, KT, 128], BF16) for i in range(3)]
    ss = P.sb("ss", [128, 4], F32)
    krs = [P.sb("kr%d" % i, [128, 1024], F32) for i in range(2)]
    vfs = [P.sb("vf%d" % i, [128, 1024], F32) for i in range(2)]
    t1 = P.sb("t1", [128, 256], F32)
    t2 = P.sb("t2", [128, 256], F32)
    krbs = [P.sb("krb%d" % i, [128, 1024], BF16) for i in range(2)]
    vbs = [P.sb("vb%d" % i, [128, 1024], BF16) for i in range(2)]
    kTts = [P.sb("kTt%d" % i, [128, 8, 128], BF16) for i in range(2)]
    kTt = kTts[0]
    hprev2 = P.sb("hprev2", [128, KT, 2], BF16)
    A1_END = P.off

    load_g("g_attn")
    for half in range(2):
        P.dma("pool", wkv.t[:, :, half * 1024:(half + 1) * 1024],
              IN["w_in_ab"].ap()[:, 1024 + half * 1024:2048 + half * 1024].rearrange("(kt p) c -> p kt c", p=128),
              [], [wkv], wkv)

    def rotary(pk, ti, dst, c0):
        v = pk.t[:].rearrange("p (h two d) -> p h two d", h=4, two=2)
        o = dst.t[:, c0:c0 + 512].rearrange("p (h two d) -> p h two d", h=4, two=2)
        cb = cs.t[:, ti, :].unsqueeze(1).to_broadcast([128, 4, 64])
        sb_ = sn.t[:, ti, :].unsqueeze(1).to_broadcast([128, 4, 64])
        a = t1.t[:].rearrange("p (h d) -> p h d", h=4)
        b = t2.t[:].rearrange("p (h d) -> p h d", h=4)
        P.op("dve", lambda h: h.tensor_tensor(a, v[:, :, 0, :], cb, ALU.mult), [pk, cs], [t1])
        P.op("dve", lambda h: h.tensor_tensor(b, v[:, :, 1, :], sb_, ALU.mult), [pk, sn], [t2])
        P.op("dve", lambda h: h.tensor_tensor(o[:, :, 0, :], a, b, ALU.subtract), [t1, t2], [dst])
        P.op("dve", lambda h: h.tensor_tensor(a, v[:, :, 1, :], cb, ALU.mult), [pk, cs], [t1])
        P.op("dve", lambda h: h.tensor_tensor(b, v[:, :, 0, :], sb_, ALU.mult), [pk, sn], [t2])
        P.op("dve", lambda h: h.tensor_tensor(o[:, :, 1, :], a, b, ALU.add), [t1, t2], [dst])

    def store_kT(src_bf, scr, col0, kb=None):
        p = next_pb()
        if kb is None:
            kb = kTt
        fns = [(lambda h, p=p, j=j: h.transpose(p.t[:, j, :], src_bf.t[:, j * 128:(j + 1) * 128], ident.t[:])) for j in range(8)]
        P.mm(fns, [src_bf, ident], [p])
        P.op("act", lambda h, p=p, kb=kb: h.activation(kb.t[:], p.t[:], AF.Identity), [p], [kb])
        P.dma("sp", scr.t.ap()[:, :, col0:col0 + 128].rearrange("h d t -> d h t"), kb.t[:], [kb], [scr], kb)

    def stageA(ti):
        xt = xts[ti % 3]
        xn = xns[ti % 3]
        src = IN["xh"].ap()[ti * 128:(ti + 1) * 128, :] if ti < 24 else IN["xs"].ap()
        P.dma("sp", xt.t[:], src, [], [xt], xt)
        rmsnorm_rows(xt, xn, ss, xn)
        if ti < 16:
            ht = hts[ti % 3]
            transpose_rows(xn, ht, lambda half, ht=ht: ht.t[:, half * 8:(half + 1) * 8, :])
            if ti == 15:
                P.op("dve", lambda h, ht=ht: h.tensor_copy(hprev2.t[:], ht.t[:, :, 126:128]), [ht], [hprev2])
            return (lambda kt, ht=ht: ht.t[:, kt, :]), ht
        o0 = (ti - 16) * 128
        transpose_rows(xn, hnT_o, lambda half, o0=o0: hnT_o.t[:, half * 8:(half + 1) * 8, o0:o0 + 128])
        return (lambda kt, o0=o0: hnT_o.t[:, kt, o0:o0 + 128]), hnT_o

    def stageM(ti, lhs, hb):
        kr = krs[ti % 2]
        vf = vfs[ti % 2]
        for g4 in range(4):
            pk = next_pf()
            fns = [(lambda h, pk=pk, kt=kt, g4=g4, lhs=lhs: h.matmul(pk.t[:], lhs(kt), wkv.t[:, kt, g4 * 512:(g4 + 1) * 512],
                                                                    start=(kt == 0), stop=(kt == KT - 1))) for kt in range(KT)]
            P.mm(fns, [hb, wkv], [pk])
            if g4 < 2:
                rotary(pk, ti, kr, g4 * 512)
            else:
                c0 = (g4 - 2) * 512
                P.op("act", lambda h, pk=pk, c0=c0, vf=vf: h.activation(vf.t[:, c0:c0 + 512], pk.t[:], AF.Identity), [pk], [vf])

    def stageK(ti):
        kr = krs[ti % 2]
        vf = vfs[ti % 2]
        krb = krbs[ti % 2]
        vb = vbs[ti % 2]
        kb = kTts[ti % 2]
        P.op("act", lambda h: h.activation(krb.t[:], kr.t[:], AF.Identity), [kr], [krb])
        if ti < 24:
            P.op("dve", lambda h: h.tensor_scalar(vb.t[:], vf.t[:], valid.t[:, ti:ti + 1], None, ALU.mult), [vf, valid], [vb])
            store_kT(krb, kT_scr, ti * 128, kb)
            P.dma("sp", v_scr.t.ap()[ti * 128:(ti + 1) * 128, :], vb.t[:], [vb], [v_scr], vb)
            if ti >= 16:
                r0 = (ti - 16) * 128
                P.dma("sp", OUT["kp"].ap()[r0:r0 + 128, :], kr.t[:], [kr], [outbufs["kp"]], kr)
                P.dma("sp", OUT["vp"].ap()[r0:r0 + 128, :], vf.t[:], [vf], [outbufs["vp"]], vf)
        else:
            P.op("dve", lambda h: h.tensor_copy(vb.t[:], vf.t[:]), [vf], [vb])
            store_kT(krb, kTs_scr, 2048, kb)
            P.dma("sp", vs_scr.t.ap()[2048:2176, :], vb.t[:], [vb], [vs_scr], vb)
            P.dma("sp", OUT["ks"].ap(), kr.t[:], [kr], [outbufs["ks"]], kr)
            P.dma("sp", OUT["vs"].ap(), vf.t[:], [vf], [outbufs["vs"]], vf)

    infoA = {0: stageA(0), 1: stageA(1)}
    for ti in range(25):
        if ti + 2 < 25:
            infoA[ti + 2] = stageA(ti + 2)
        stageM(ti, *infoA[ti])
        if ti >= 1:
            stageK(ti - 1)
    stageK(24)
    for ti in range(16):
        xt = xts[ti % 2]
        krb = krbs[ti % 2]
        vb = vbs[ti % 2]
        P.dma("sp", xt.t[:, 0:1024], IN["ck"].ap()[ti * 128:(ti + 1) * 128, :], [], [xt], xt)
        P.dma("sp", xt.t[:, 1024:2048], IN["cv"].ap()[ti * 128:(ti + 1) * 128, :], [], [xt], xt)
        P.op("act", lambda h, xt=xt, krb=krb: h.activation(krb.t[:], xt.t[:, 0:1024], AF.Identity), [xt], [krb])
        P.op("dve", lambda h, xt=xt, vb=vb: h.tensor_copy(vb.t[:], xt.t[:, 1024:2048]), [xt], [vb])
        store_kT(krb, kTs_scr, ti * 128, kTts[ti % 2])
        P.dma("sp", vs_scr.t.ap()[ti * 128:(ti + 1) * 128, :], vb.t[:], [vb], [vs_scr], vb)

    if STOP == 'A1':
        P.barrier()
        return
    P.barrier()
    P.off = A0
    wq = P.sb("wq", [128, KT, 1024], BF16)
    hprev2b = P.sb("hprev2b", [128, KT, 2], BF16)
    qf = P.sb("qf", [128, 1024], F32)
    qb = P.sb("qb", [128, 1024], BF16)
    t1 = P.sb("t1b", [128, 256], F32)
    t2 = P.sb("t2b", [128, 256], F32)
    kTt = P.sb("kTtb", [128, 8, 128], BF16)
    hprev2k = P.sb("hprev2k", [128, KT, 2], BF16, at=hprev2.at)
    hprev2k.w = dict(hprev2.w)
    P.dma("pool", wq.t[:], IN["w_in_ab"].ap()[:, 0:1024].rearrange("(kt p) c -> p kt c", p=128), [], [wq], wq)
    for tj in range(9):
        ti = 16 + tj
        o0 = tj * 128
        for g2_ in range(2):
            pk = next_pf()
            fns = [(lambda h, pk=pk, kt=kt, g2_=g2_, o0=o0: h.matmul(pk.t[:], hnT_o.t[:, kt, o0:o0 + 128],
                                                                      wq.t[:, kt, g2_ * 512:(g2_ + 1) * 512],
                                                                      start=(kt == 0), stop=(kt == KT - 1))) for kt in range(KT)]
            P.mm(fns, [hnT_o, wq], [pk])
            rotary(pk, ti, qf, g2_ * 512)
        P.op("act", lambda h: h.activation(qb.t[:], qf.t[:], AF.Identity), [qf], [qb])
        store_kT(qb, qT_scr, o0)

    if STOP == 'A2':
        P.barrier()
        return
    P.barrier()
    P.off = A0
    ocat = P.sb("ocat", [128, KT, NO], BF16)
    hp2 = P.sb("hp2", [128, KT, 2], BF16)
    B0 = P.off
    P.op("dve", lambda h: h.tensor_copy(hp2.t[:], hprev2k.t[:]), [hprev2k], [hp2])
    P.barrier()
    maskp = P.sb("maskp", [128, 20, 512], BF16)
    masks_ = P.sb("masks_", [128, 17, 128], BF16)
    P.dma("pool", maskp.t[:], IN["maskp"].ap(), [], [maskp], maskp)
    P.dma("pool", masks_.t[:], IN["masks"].ap(), [], [masks_], masks_)
    kTh = [P.sb("kTh%d" % i, [128, NTP], BF16) for i in range(1)] * 2
    vh = [P.sb("vh%d" % i, [128, 24, 128], BF16) for i in range(1)] * 2
    kTsh = [P.sb("kTsh%d" % i, [128, 2176], BF16) for i in range(1)] * 2
    vsh = [P.sb("vsh%d" % i, [128, 17, 128], BF16) for i in range(1)] * 2
    qTh = [P.sb("qTh%d" % i, [128, NO], BF16) for i in range(1)] * 2
    wt = [P.sb("wt%d" % i, [128, KT, 128], BF16) for i in range(4)]
    pts = [P.sb("pt%d" % i, [128, 512], BF16) for i in range(4)]
    ptm = [P.sb("ptm%d" % i, [128, 512], BF16) for i in range(4)]
    za = P.sb("za", [128, NO], F32)
    rl = P.sb("rl", [128, 512], F32)
    of = P.sb("of", [128, 512], F32)
    sg = P.sb("sg", [128, 512], F32)

    def silu_evac(pk, dstb, dst_ap, n):
        P.op("act", lambda h: h.activation(sg.t[:, 0:n], pk.t[:, 0:n], AF.Exp, scale=-1.0), [pk], [sg])
        P.op("dve", lambda h: h.tensor_scalar(sg.t[:, 0:n], sg.t[:, 0:n], 1.0, None, ALU.add), [sg], [sg])
        P.op("dve", lambda h: h.reciprocal(sg.t[:, 0:n], sg.t[:, 0:n]), [sg], [sg])
        P.op("dve", lambda h: h.tensor_tensor(dst_ap, pk.t[:, 0:n], sg.t[:, 0:n], ALU.mult), [pk, sg], [dstb])
    fb = [P.sb("fb%d" % i, [128, NO + 2], F32) for i in range(4)]
    convo_p = P.sb("convo_p", [128, 8, 2], F32)
    convo_s = P.sb("convo_s", [128, 8, 2], F32)
    sconv = P.sb("sconv", [128, 8, 2], F32)
    P.dma("sp", sconv.t[:], IN["sconv"].ap(), [], [sconv], sconv)
    scale = 128.0 ** -0.5

    def load_wt(i, c0):
        P.dma("pool", wt[i].t[:], IN["w_in_ab"].ap()[:, c0:c0 + 128].rearrange("(kt p) c -> p kt c", p=128), [], [wt[i]], wt[i])

    def proj_feat(wb, dst_ap_fn, evac, rhs_buf, rhs_fn, n):
        pk = next_pf()
        fns = [(lambda h, pk=pk, kt=kt: h.matmul(pk.t[:, 0:n], wb.t[:, kt, :], rhs_fn(kt), start=(kt == 0), stop=(kt == KT - 1)))
               for kt in range(KT)]
        P.mm(fns, [wb, rhs_buf], [pk])
        evac(pk)

    def attention(hh, qT, q0, nq, kT, vt, ktiles, mask_fn, vB_fn, o_dst_fn, zcol0):
        po = pf[4]
        pl = pf[5]
        nk = len(ktiles)
        LA = 3
        pms = {}

        def issue_S(i):
            kt_ = ktiles[i]
            ps_ = next_pf()
            P.mm([lambda h, ps_=ps_, kt_=kt_: h.matmul(ps_.t[:, 0:nq], kT.t[:, kt_ * 128:(kt_ + 1) * 128], qT.t[:, q0:q0 + nq],
                                                        start=True, stop=True)], [kT, qT], [ps_])
            pe_ = pts[i % 4]
            pm_ = ptm[i % 4]
            P.op("act", lambda h, ps_=ps_, pe_=pe_: h.activation(pe_.t[:, 0:nq], ps_.t[:, 0:nq], AF.Exp, scale=scale), [ps_], [pe_])
            mk, mb = mask_fn(i)
            eng = "pool" if i % 4 == 3 else "dve"
            P.op(eng, lambda h, pe_=pe_, pm_=pm_, mk=mk: h.tensor_tensor(pm_.t[:, 0:nq], pe_.t[:, 0:nq], mk, ALU.mult), [pe_, mb], [pm_])
            pms[i] = pm_

        def issue_PV(i):
            kt_ = ktiles[i]
            pm_ = pms[i]
            vB, vBb = vB_fn(i)
            P.mm([lambda h, pm_=pm_, kt_=kt_, i=i: h.matmul(po.t[:, 0:nq], vt.t[:, kt_, :], pm_.t[:, 0:nq], start=(i == 0), stop=(i == nk - 1)),
                  lambda h, pm_=pm_, vB=vB, i=i: h.matmul(pl.t[:, 0:nq], vB, pm_.t[:, 0:nq], start=(i == 0), stop=(i == nk - 1))],
                 [vt, pm_, vBb], [po, pl])

        for i in range(min(LA, nk)):
            issue_S(i)
        for i in range(nk):
            if i + LA < nk:
                issue_S(i + LA)
            issue_PV(i)
        P.op("dve", lambda h: h.reciprocal(rl.t[:, 0:nq], pl.t[:, 0:nq]), [pl], [rl])
        P.op("dve", lambda h: h.tensor_tensor(of.t[:, 0:nq], po.t[:, 0:nq], rl.t[:, 0:nq], ALU.mult), [po, rl], [of])
        P.op("dve", lambda h: h.tensor_tensor(o_dst_fn(), of.t[:, 0:nq], za.t[:, zcol0:zcol0 + nq], ALU.mult), [of, za], [ocat])

    for hh in range(8):
        b2 = hh % 2
        P.dma("sp", kTh[b2].t[:], kT_scr.t.ap()[hh], [kT_scr], [kTh[b2]], kTh[b2])
        P.dma("sp", vh[b2].t[:], v_scr.t.ap()[:, hh * 128:(hh + 1) * 128].rearrange("(t p) d -> p t d", p=128), [v_scr], [vh[b2]], vh[b2])
        P.dma("sp", kTsh[b2].t[:], kTs_scr.t.ap()[hh], [kTs_scr], [kTsh[b2]], kTsh[b2])
        P.dma("sp", vsh[b2].t[:], vs_scr.t.ap()[:, hh * 128:(hh + 1) * 128].rearrange("(t p) d -> p t d", p=128), [vs_scr], [vsh[b2]], vsh[b2])
        P.dma("sp", qTh[b2].t[:], qT_scr.t.ap()[hh], [qT_scr], [qTh[b2]], qTh[b2])
        load_wt(0, 3072 + hh * 128)
        for (c0, n) in [(0, 512), (512, 512), (1024, 128)]:
            proj_feat(wt[0], None, lambda pk, c0=c0, n=n: silu_evac(pk, za, za.t[:, c0:c0 + n], n),
                      hnT_o, lambda kt, c0=c0, n=n: hnT_o.t[:, kt, c0:c0 + n], n)
        for qc in range(2):
            kts = list(range(4 * qc, 4 * qc + 20))
            attention(hh, qTh[b2], qc * 512, 512, kTh[b2], vh[b2], kts,
                      lambda i: (maskp.t[:, i, :], maskp),
                      lambda i, kts=kts: (validB.t[:, kts[i], :], validB),
                      lambda qc=qc, hh=hh: ocat.t[:, hh, qc * 512:(qc + 1) * 512], qc * 512)
        attention(hh, qTh[b2], 1024, 128, kTsh[b2], vsh[b2], list(range(17)),
                  lambda i: (masks_.t[:, i, :], masks_),
                  lambda i: (ones_bf.t[:], ones_bf),
                  lambda hh=hh: ocat.t[:, hh, 1024:1152], 1024)

    for cc in range(8):
        for j, base in enumerate([4096, 5120, 6144, 7168]):
            load_wt(j, base + cc * 128)
        bb, cb_, hb_, zb = fb
        for j, dstb in enumerate(fb):
            for (c0, n) in [(0, 512), (512, 512), (1024, 128)]:
                if j == 3:
                    ev = lambda pk, c0=c0, n=n, dstb=dstb: silu_evac(pk, dstb, dstb.t[:, 2 + c0:2 + c0 + n], n)
                else:
                    ev = lambda pk, c0=c0, n=n, dstb=dstb: P.op("act", lambda h: h.activation(dstb.t[:, 2 + c0:2 + c0 + n], pk.t[:, 0:n], AF.Identity), [pk], [dstb])
                proj_feat(wt[j], None, ev, hnT_o, lambda kt, c0=c0, n=n: hnT_o.t[:, kt, c0:c0 + n], n)
            if j in (1, 2):
                proj_feat(wt[j], None, lambda pk, dstb=dstb: P.op("act", lambda h: h.activation(dstb.t[:, 0:2], pk.t[:, 0:2], AF.Identity), [pk], [dstb]),
                          hp2, lambda kt: hp2.t[:, kt, :], 2)
        P.op("dve", lambda h: h.tensor_tensor(cb_.t[:], cb_.t[:], hb_.t[:], ALU.mult), [cb_, hb_], [cb_])
        w0 = cw.t[:, cc, 0:1]
        w1 = cw.t[:, cc, 1:2]
        w2 = cw.t[:, cc, 2:3]
        P.op("dve", lambda h, w2=w2: h.tensor_scalar(hb_.t[:, 2:1026], cb_.t[:, 2:1026], w2, None, ALU.mult), [cb_, cw], [hb_])
        P.op("dve", lambda h, w1=w1: h.scalar_tensor_tensor(hb_.t[:, 2:1026], cb_.t[:, 1:1025], w1, hb_.t[:, 2:1026], ALU.mult, ALU.add), [cb_, cw, hb_], [hb_])
        P.op("dve", lambda h, w0=w0: h.scalar_tensor_tensor(hb_.t[:, 2:1026], cb_.t[:, 0:1024], w0, hb_.t[:, 2:1026], ALU.mult, ALU.add), [cb_, cw, hb_], [hb_])
        P.op("dve", lambda h, cc=cc: h.tensor_copy(convo_p.t[:, cc, :], cb_.t[:, 1024:1026]), [cb_], [convo_p])
        P.op("dve", lambda h, cc=cc: h.tensor_copy(cb_.t[:, 1024:1026], sconv.t[:, cc, :]), [sconv], [cb_])
        P.op("dve", lambda h, w2=w2: h.tensor_scalar(hb_.t[:, 1026:1034], cb_.t[:, 1026:1034], w2, None, ALU.mult), [cb_, cw], [hb_])
        P.op("dve", lambda h, w1=w1: h.scalar_tensor_tensor(hb_.t[:, 1026:1034], cb_.t[:, 1025:1033], w1, hb_.t[:, 1026:1034], ALU.mult, ALU.add), [cb_, cw, hb_], [hb_])
        P.op("dve", lambda h, w0=w0: h.scalar_tensor_tensor(hb_.t[:, 1026:1034], cb_.t[:, 1024:1032], w0, hb_.t[:, 1026:1034], ALU.mult, ALU.add), [cb_, cw, hb_], [hb_])
        P.op("dve", lambda h, cc=cc: h.tensor_copy(convo_s.t[:, cc, :], cb_.t[:, 1032:1034]), [cb_], [convo_s])
        P.op("dve", lambda h: h.tensor_tensor(hb_.t[:, 2:1034], hb_.t[:, 2:1034], bb.t[:, 2:1034], ALU.mult), [hb_, bb], [hb_])
        P.op("dve", lambda h, cc=cc: h.tensor_tensor(ocat.t[:, 8 + cc, 0:1032], hb_.t[:, 2:1034], zb.t[:, 2:1034], ALU.mult), [hb_, zb], [ocat])
        P.op("dve", lambda h, cc=cc: h.memset(ocat.t[:, 8 + cc, 1032:1152], 0.0), [], [ocat])
    P.dma("sp", OUT["convp"].ap(), convo_p.t[:], [convo_p], [outbufs["convp"]], convo_p)
    P.dma("sp", OUT["convs"].ap(), convo_s.t[:], [convo_s], [outbufs["convs"]], convo_s)

    if STOP == 'B':
        P.barrier()
        return
    P.barrier()
    hn1T = P.sb("hn1T", [128, KT, NCH], BF16, at=hnT_o.at)
    P.off = B0
    wgb = [P.sb("wg%d" % i, [128, KT, 512], BF16) for i in range(2)]
    h1s = [P.sb("h1s%d" % i, [128, D], F32) for i in range(5)]
    xn1 = [P.sb("xn1%d" % i, [128, D], BF16) for i in range(2)]
    junk = P.sb("junk2", [128, D], F32)
    ss = P.sb("ss2", [128, 4], F32)
    tmpT = P.sb("tmpT", [128, KT, 128], BF16)
    load_g("g_ssm")
    wi = 0
    for tiles in [list(range(0, 5)), list(range(5, 9))]:
        for si, tj in enumerate(tiles):
            o0 = tj * 128
            src = IN["xh"].ap()[NHALO + o0:NHALO + o0 + 128, :] if tj < 8 else IN["xs"].ap()
            P.dma("sp", h1s[si].t[:], src, [], [h1s[si]], h1s[si])
        for g4 in range(4):
            wg = wgb[wi % 2]
            wi += 1
            P.dma("pool", wg.t[:], IN["w_out_ab"].ap()[:, g4 * 512:(g4 + 1) * 512].rearrange("(kt p) c -> p kt c", p=128), [], [wg], wg)
            for si, tj in enumerate(tiles):
                o0 = tj * 128
                ht_ = h1s[si]
                pk = next_pf()
                fns = [(lambda h, pk=pk, kt=kt, o0=o0, wg=wg: h.matmul(pk.t[:], ocat.t[:, kt, o0:o0 + 128], wg.t[:, kt, :],
                                                                         start=(kt == 0), stop=(kt == KT - 1))) for kt in range(KT)]
                P.mm(fns, [ocat, wg], [pk])
                P.op("dve", lambda h, pk=pk, ht_=ht_, g4=g4: h.tensor_tensor(ht_.t[:, g4 * 512:(g4 + 1) * 512], ht_.t[:, g4 * 512:(g4 + 1) * 512], pk.t[:], ALU.add),
                     [pk, ht_], [ht_])
        for si, tj in enumerate(tiles):
            o0 = tj * 128
            ht_ = h1s[si]
            P.dma("sp", h1_scr.t.ap()[o0:o0 + 128, :], ht_.t[:], [ht_], [h1_scr], ht_)
            if STOP == 'C1':
                if tj < 8:
                    P.dma("sp", OUT["yp"].ap()[o0:o0 + 128, :], ht_.t[:], [ht_], [outbufs["yp"]], ht_)
                else:
                    P.dma("sp", OUT["ys"].ap(), ht_.t[:], [ht_], [outbufs["ys"]], ht_)
            xn = xn1[tj % 2]
            rmsnorm_rows(ht_, xn, ss, junk)
            if tj < 8:
                transpose_rows(xn, hn1T, lambda half, o0=o0: hn1T.t[:, half * 8:(half + 1) * 8, o0:o0 + 128])
            else:
                transpose_rows(xn, tmpT, lambda half: tmpT.t[:, half * 8:(half + 1) * 8, :])
                P.op("dve", lambda h: h.tensor_copy(hn1T.t[:, :, 1024:1032], tmpT.t[:, :, 0:8]), [tmpT], [hn1T])
    for j in range(8):
        P.dma("sp", hn_src[j].t.ap().rearrange("(k p) t -> p k t", p=128), hn1T.t[:, 2 * j:2 * j + 2, :], [hn1T], [hn_src[j]], hn1T)
        P.coll(hn_src[j], hn_dst[j], GROUPS)

    if STOP == 'C1':
        P.barrier()
        return
    P.barrier()
    P.off = CONST_END
    uT = P.sb("uT", [128, 4, 4 * NCH], BF16)
    ygT = P.sb("ygT", [128, 4, 4 * NCH], BF16)
    L1 = P.off
    wu = P.sb("wu", [128, KT, 512], BF16)
    hch = [P.sb("hch%d" % i, [128, KT, 516], BF16) for i in range(2)]
    P.dma("pool", wu.t[:], IN["w_in_c"].ap()[:, 0:512].rearrange("(kt p) c -> p kt c", p=128), [], [wu], wu)
    ci = 0
    for r in range(4):
        for hf in range(2):
            hc = hch[ci % 2]
            ci += 1
            c0 = hf * 516
            for j in range(8):
                P.dma("sp", hc.t[:, 2 * j:2 * j + 2, :], hn_dst[j].t.ap()[r * 256:(r + 1) * 256, c0:c0 + 516].rearrange("(k p) t -> p k t", p=128),
                      [hn_dst[j]], [hc], hc)
            for ft in range(4):
                pk = next_pf()
                fns = [(lambda h, pk=pk, kt=kt, ft=ft, hc=hc: h.matmul(pk.t[:, 0:512], wu.t[:, kt, ft * 128:(ft + 1) * 128], hc.t[:, kt, 0:512],
                                                                      start=(kt == 0), stop=(kt == KT - 1))) for kt in range(KT)]
                P.mm(fns, [wu, hc], [pk])
                P.op("act", lambda h, pk=pk, ft=ft, r=r, c0=c0: h.activation(uT.t[:, ft, r * NCH + c0:r * NCH + c0 + 512], pk.t[:, 0:512], AF.Identity), [pk], [uT])
                pk2 = next_pf()
                fns2 = [(lambda h, pk2=pk2, kt=kt, ft=ft, hc=hc: h.matmul(pk2.t[:, 0:4], wu.t[:, kt, ft * 128:(ft + 1) * 128], hc.t[:, kt, 512:516],
                                                                         start=(kt == 0), stop=(kt == KT - 1))) for kt in range(KT)]
                P.mm(fns2, [wu, hc], [pk2])
                P.op("act", lambda h, pk2=pk2, ft=ft, r=r, c0=c0: h.activation(uT.t[:, ft, r * NCH + c0 + 512:r * NCH + c0 + 516], pk2.t[:, 0:4], AF.Identity), [pk2], [uT])

    if STOP == 'U':
        P.barrier()
        return
    P.barrier()
    P.off = L1
    def small(name, shape, dt=F32):
        return P.sb(name, shape, dt)
    lre_s = small("lre_s", [128, 16]); lim_s = small("lim_s", [128, 16]); lst_s = small("lst_s", [128, 16])
    lre_r = small("lre_r", [128, 256]); lim_r = small("lim_r", [128, 256]); lst_r = small("lst_r", [128, 256])
    bre_r = small("bre_r", [128, 256]); bim_r = small("bim_r", [128, 256])
    cre_s = small("cre_s", [128, 16, 16]); cim_s = small("cim_s", [128, 16, 16])
    rmask = small("rmask", [128, 8]); smask = small("smask", [128, 2])
    sre0 = small("sre0", [128, 4, 16]); sim0 = small("sim0", [128, 4, 16])
    iota = small("iota", [128, 1024])
    for b_, n in [(lre_s, "lre_s"), (lim_s, "lim_s"), (lst_s, "lst_s"), (cre_s, "cre_s"), (cim_s, "cim_s"),
                  (rmask, "rmask"), (smask, "smask"), (sre0, "sre0"), (sim0, "sim0"), (iota, "iota")]:
        P.dma("sp", b_.t[:], IN[n].ap(), [], [b_], b_)
    for b_, n in [(lre_r, "lre_r"), (lim_r, "lim_r"), (lst_r, "lst_r"), (bre_r, "bre_r"), (bim_r, "bim_r")]:
        P.dma("sp", b_.t[:], IN[n].ap().rearrange("p a b -> p (a b)"), [], [b_], b_)
    negpi = small("negpi", [128, 1])
    P.op("dve", lambda h: h.memset(negpi.t[:], -math.pi), [], [negpi])

    I32 = mybir.dt.int32
    tq = small("tq", [128, 1024]); tiq = small("tiq", [128, 1024], I32)
    halfpi = small("halfpi", [128, 1]); zero_t = small("zero_t", [128, 1])
    P.op("dve", lambda h: h.memset(halfpi.t[:], 0.5 * math.pi), [], [halfpi])
    P.op("dve", lambda h: h.memset(zero_t.t[:], 0.0), [], [zero_t])

    def sincos(ang_ap, n, s_ap, c_ap, rd, wr):
        for (dst, addc, bt, lo, hi) in [(s_ap, 0.0, zero_t, -math.pi, math.pi), (c_ap, 0.25, halfpi, -1.5 * math.pi, 0.5 * math.pi)]:
            P.op("dve", lambda h, addc=addc: h.tensor_scalar(tq.t[:, 0:n], ang_ap, 1.0 / TWO_PI, addc, ALU.mult, ALU.add), rd, [tq])
            P.op("dve", lambda h: h.tensor_copy(tiq.t[:, 0:n], tq.t[:, 0:n]), [tq], [tiq])
            P.op("dve", lambda h: h.tensor_copy(tq.t[:, 0:n], tiq.t[:, 0:n]), [tiq], [tq])
            P.op("dve", lambda h: h.scalar_tensor_tensor(tq.t[:, 0:n], tq.t[:, 0:n], -TWO_PI, ang_ap, ALU.mult, ALU.add), [tq] + rd, [tq])
            P.op("dve", lambda h, lo=lo, hi=hi: h.tensor_scalar(tq.t[:, 0:n], tq.t[:, 0:n], lo, hi, ALU.max, ALU.min), [tq], [tq])
            P.op("act", lambda h, dst=dst, bt=bt: h.activation(dst, tq.t[:, 0:n], AF.Sin, bias=bt.t[:, 0:1], scale=1.0), [tq, bt], wr)

    def disc(lre, lim, lst, n, pref):
        o = {}
        for nm in ["step", "mag", "th", "c", "s", "tmp", "nr", "den", "cr", "ci", "a", "b"]:
            o[nm] = small(pref + nm, [128, n])
        al = [o[k] for k in o] + [lre, lim, lst]
        P.op("act", lambda h: h.activation(o["step"].t[:], lst.t[:, 0:n], AF.Exp), al, al)
        P.op("dve", lambda h: h.tensor_tensor(o["th"].t[:], lim.t[:, 0:n], o["step"].t[:], ALU.mult), al, al)
        P.op("dve", lambda h: h.tensor_tensor(o["a"].t[:], lre.t[:, 0:n], o["step"].t[:], ALU.mult), al, al)
        P.op("act", lambda h: h.activation(o["mag"].t[:], o["a"].t[:], AF.Exp), al, al)
        sincos(o["th"].t[:], n, o["s"].t[:], o["c"].t[:], al, al)
        P.op("dve", lambda h: h.tensor_tensor(o["a"].t[:], o["mag"].t[:], o["c"].t[:], ALU.mult), al, al)
        P.op("dve", lambda h: h.tensor_scalar(o["nr"].t[:], o["a"].t[:], 1.0, -1.0, ALU.mult, ALU.add), al, al)
        P.op("dve", lambda h: h.tensor_tensor(o["b"].t[:], o["mag"].t[:], o["s"].t[:], ALU.mult), al, al)
        P.op("dve", lambda h: h.tensor_tensor(o["den"].t[:], lre.t[:, 0:n], lre.t[:, 0:n], ALU.mult), al, al)
        P.op("dve", lambda h: h.tensor_tensor(o["tmp"].t[:], lim.t[:, 0:n], lim.t[:, 0:n], ALU.mult), al, al)
        P.op("dve", lambda h: h.tensor_tensor(o["den"].t[:], o["den"].t[:], o["tmp"].t[:], ALU.add), al, al)
        P.op("dve", lambda h: h.reciprocal(o["den"].t[:], o["den"].t[:]), al, al)
        P.op("dve", lambda h: h.tensor_tensor(o["cr"].t[:], o["nr"].t[:], lre.t[:, 0:n], ALU.mult), al, al)
        P.op("dve", lambda h: h.tensor_tensor(o["tmp"].t[:], o["b"].t[:], lim.t[:, 0:n], ALU.mult), al, al)
        P.op("dve", lambda h: h.tensor_tensor(o["cr"].t[:], o["cr"].t[:], o["tmp"].t[:], ALU.add), al, al)
        P.op("dve", lambda h: h.tensor_tensor(o["cr"].t[:], o["cr"].t[:], o["den"].t[:], ALU.mult), al, al)
        P.op("dve", lambda h: h.tensor_tensor(o["ci"].t[:], o["b"].t[:], lre.t[:, 0:n], ALU.mult), al, al)
        P.op("dve", lambda h: h.tensor_tensor(o["tmp"].t[:], o["nr"].t[:], lim.t[:, 0:n], ALU.mult), al, al)
        P.op("dve", lambda h: h.tensor_tensor(o["ci"].t[:], o["ci"].t[:], o["tmp"].t[:], ALU.subtract), al, al)
        P.op("dve", lambda h: h.tensor_tensor(o["ci"].t[:], o["ci"].t[:], o["den"].t[:], ALU.mult), al, al)
        return o, al

    ds_, als = disc(lre_s, lim_s, lst_s, 16, "ds_")
    dr_, alr = disc(lre_r, lim_r, lst_r, 256, "dr_")
    bbr = small("bbr", [128, 256]); bbi = small("bbi", [128, 256]); tmpr = small("tmpr", [128, 256])
    alr2 = alr + [bbr, bbi, tmpr, bre_r, bim_r]
    P.op("dve", lambda h: h.tensor_tensor(bbr.t[:], dr_["cr"].t[:], bre_r.t[:], ALU.mult), alr2, alr2)
    P.op("dve", lambda h: h.tensor_tensor(tmpr.t[:], dr_["ci"].t[:], bim_r.t[:], ALU.mult), alr2, alr2)
    P.op("dve", lambda h: h.tensor_tensor(bbr.t[:], bbr.t[:], tmpr.t[:], ALU.subtract), alr2, alr2)
    P.op("dve", lambda h: h.tensor_tensor(bbi.t[:], dr_["cr"].t[:], bim_r.t[:], ALU.mult), alr2, alr2)
    P.op("dve", lambda h: h.tensor_tensor(tmpr.t[:], dr_["ci"].t[:], bre_r.t[:], ALU.mult), alr2, alr2)
    P.op("dve", lambda h: h.tensor_tensor(bbi.t[:], bbi.t[:], tmpr.t[:], ALU.add), alr2, alr2)
    BbT = [small("BbT%d" % ri, [128, 16, 128], BF16) for ri in range(2)]
    for ri, src in enumerate([bbr, bbi]):
        for qq in range(4):
            for g2 in range(2):
                m = rmask.t[:, qq * 2 + g2:qq * 2 + g2 + 1]
                o_ap = BbT[ri].t[:].rearrange("p (ft q) c -> p ft q c", q=4)[:, :, qq, g2 * 64:(g2 + 1) * 64]
                i_ap = src.t[:].rearrange("p (ft d) -> p ft d", ft=4)
                P.op("dve", lambda h, o_ap=o_ap, i_ap=i_ap, m=m: h.tensor_scalar(o_ap, i_ap, m, None, ALU.mult), alr2 + [rmask], [BbT[ri]])
    CT = [small("CT%d" % ri, [128, 16, 128], BF16) for ri in range(2)]
    for ri in range(2):
        P.op("dve", lambda h, ri=ri: h.memset(CT[ri].t[:], 0.0), [], [CT[ri]])
    for ri, (src, sgn) in enumerate([(cre_s, 1.0), (cim_s, -1.0)]):
        for pair in range(16):
            qq = pair % 4
            for g2 in range(2):
                m = smask.t[:, g2:g2 + 1]
                col = qq * 32 + g2 * 16
                P.op("dve", lambda h, ri=ri, pair=pair, col=col, m=m, src=src, sgn=sgn: h.tensor_scalar(
                    CT[ri].t[:, pair, col:col + 16], src.t[:, pair, :], m, sgn, ALU.mult, ALU.mult), [src, smask], [CT[ri]])
    rr = ds_["mag"]; th = ds_["th"]
    cth = small("cth", [128, 16]); sth = small("sth", [128, 16])
    P.op("dve", lambda h: h.tensor_copy(cth.t[:], ds_["c"].t[:]), als, [cth])
    P.op("dve", lambda h: h.tensor_copy(sth.t[:], ds_["s"].t[:]), als, [sth])

    tabc = [small("tabc%d" % i, [128, 1024]) for i in range(4)]
    tabs = [small("tabs%d" % i, [128, 1024]) for i in range(4)]
    WS = [dict(gr=small("gr0", [128, 1024]), gi=small("gi0", [128, 1024]), yr=small("yr0", [128, 1024]), yi=small("yi0", [128, 1024]),
               hrb=small("hrb0", [128, 1024], BF16), hib=small("hib0", [128, 1024], BF16))]
    hib1 = small("hib1", [128, 1024], BF16)
    ytmp = small("ytmp", [128, 512]); ysq = small("ysq", [128, 512])
    stp_l = [small("stp%d" % p_, [128, 2]) for p_ in range(16)]
    sts_l = [[small("sts%d_%d" % (b_, p_), [128, 2]) for p_ in range(16)] for b_ in range(4)]
    for w_ in WS:
        w_["gin"] = small("gin0", [128, 2]); w_["hend"] = small("hend0", [128, 2])
    for p_ in range(16):
        P.op("dve", lambda h, p_=p_: h.memset(stp_l[p_].t[:], 0.0), [], [stp_l[p_]])
    P.barrier()
    blkA = lre_r.at
    blkB = dr_["step"].at
    WS.append(dict(gr=P.sb("gr1", [128, 1024], F32, at=blkB), gi=P.sb("gi1", [128, 1024], F32, at=blkB + 4096),
                   yr=P.sb("yr1", [128, 1024], F32, at=blkB + 8192), yi=P.sb("yi1", [128, 1024], F32, at=blkA),
                   hrb=P.sb("hrb1", [128, 1024], BF16, at=blkB + 12288), hib=hib1,
                   gin=small("gin1", [128, 2]), hend=small("hend1", [128, 2])))
    ang = WS[0]["gr"]
    GC = 1.5957691216057308
    kcount = [0]

    def stageX(pair, qq, ft, col0, T, tc_, ts_, init_re, init_im, init_bufs, out_b):
        w = WS[kcount[0] % 2]
        kcount[0] += 1
        gr, gi, yr, yi, hrb, hib, gin, hend = w["gr"], w["gi"], w["yr"], w["yi"], w["hrb"], w["hib"], w["gin"], w["hend"]
        for h0 in range(0, T, 512):
            n = min(512, T - h0)
            pxr = next_pf(); pxi = next_pf()
            P.mm([lambda h, pxr=pxr, n=n, h0=h0: h.matmul(pxr.t[:, 0:n], BbT[0].t[:, pair, :], uT.t[:, ft, col0 + h0:col0 + h0 + n], start=True, stop=True)], [BbT[0], uT], [pxr])
            P.mm([lambda h, pxi=pxi, n=n, h0=h0: h.matmul(pxi.t[:, 0:n], BbT[1].t[:, pair, :], uT.t[:, ft, col0 + h0:col0 + h0 + n], start=True, stop=True)], [BbT[1], uT], [pxi])
            c_ = tc_.t[:, h0:h0 + n]; s_ = ts_.t[:, h0:h0 + n]
            P.op("dve", lambda h, pxr=pxr, c_=c_, n=n, h0=h0: h.tensor_tensor(yr.t[:, h0:h0 + n], pxr.t[:, 0:n], c_, ALU.mult), [pxr, tc_], [yr])
            P.op("dve", lambda h, pxi=pxi, s_=s_, n=n, h0=h0: h.tensor_tensor(gr.t[:, h0:h0 + n], pxi.t[:, 0:n], s_, ALU.mult), [pxi, ts_], [gr])
            P.op("pool", lambda h, n=n, h0=h0: h.tensor_tensor(yr.t[:, h0:h0 + n], yr.t[:, h0:h0 + n], gr.t[:, h0:h0 + n], ALU.add), [yr, gr], [yr])
            P.op("dve", lambda h, pxi=pxi, c_=c_, n=n, h0=h0: h.tensor_tensor(yi.t[:, h0:h0 + n], pxi.t[:, 0:n], c_, ALU.mult), [pxi, tc_], [yi])
            P.op("dve", lambda h, pxr=pxr, s_=s_, n=n, h0=h0: h.tensor_tensor(gi.t[:, h0:h0 + n], pxr.t[:, 0:n], s_, ALU.mult), [pxr, ts_], [gi])
            P.op("pool", lambda h, n=n, h0=h0: h.tensor_tensor(yi.t[:, h0:h0 + n], yi.t[:, h0:h0 + n], gi.t[:, h0:h0 + n], ALU.subtract), [yi, gi], [yi])
        ct = cth.t[:, pair:pair + 1]; st_ = sth.t[:, pair:pair + 1]
        P.op("dve", lambda h: h.tensor_scalar(gin.t[:, 0:1], init_re, ct, None, ALU.mult), init_bufs + [cth], [gin])
        P.op("dve", lambda h: h.scalar_tensor_tensor(gin.t[:, 0:1], init_im, st_, gin.t[:, 0:1], ALU.mult, ALU.subtract), init_bufs + [sth, gin], [gin])
        P.op("dve", lambda h: h.tensor_scalar(gin.t[:, 0:1], gin.t[:, 0:1], -1.0, None, ALU.mult), [gin], [gin])
        P.op("dve", lambda h: h.tensor_scalar(gin.t[:, 1:2], init_re, st_, None, ALU.mult), init_bufs + [sth], [gin])
        P.op("dve", lambda h: h.scalar_tensor_tensor(gin.t[:, 1:2], init_im, ct, gin.t[:, 1:2], ALU.mult, ALU.add), init_bufs + [cth, gin], [gin])
        rb = rr.t[:, pair:pair + 1].to_broadcast([128, T])
        P.op("dve", lambda h: h.tensor_tensor_scan(gr.t[:, 0:T], rb, yr.t[:, 0:T], gin.t[:, 0:1], ALU.mult, ALU.add), [yr, gin] + als, [gr])
        P.op("dve", lambda h: h.tensor_tensor_scan(gi.t[:, 0:T], rb, yi.t[:, 0:T], gin.t[:, 1:2], ALU.mult, ALU.add), [yi, gin] + als, [gi])
        c_ = tc_.t[:, 0:T]; s_ = ts_.t[:, 0:T]
        P.op("pool", lambda h: h.tensor_tensor(yr.t[:, 0:T], gr.t[:, 0:T], c_, ALU.mult), [gr, tc_], [yr])
        P.op("pool", lambda h: h.tensor_tensor(yi.t[:, 0:T], gi.t[:, 0:T], s_, ALU.mult), [gi, ts_], [yi])
        P.op("dve", lambda h: h.tensor_tensor(hrb.t[:, 0:T], yr.t[:, 0:T], yi.t[:, 0:T], ALU.subtract), [yr, yi], [hrb])
        P.op("dve", lambda h: h.tensor_tensor(hend.t[:, 0:1], yr.t[:, T - 1:T], yi.t[:, T - 1:T], ALU.subtract), [yr, yi], [hend])
        P.op("pool", lambda h: h.tensor_tensor(yr.t[:, 0:T], gr.t[:, 0:T], s_, ALU.mult), [gr, ts_, hrb, hend], [yr])
        P.op("dve", lambda h: h.tensor_tensor(yi.t[:, 0:T], gi.t[:, 0:T], c_, ALU.mult), [gi, tc_, hrb, hend], [yi])
        P.op("dve", lambda h: h.tensor_tensor(hib.t[:, 0:T], yr.t[:, 0:T], yi.t[:, 0:T], ALU.add), [yr, yi], [hib])
        P.op("dve", lambda h: h.tensor_tensor(hend.t[:, 1:2], yr.t[:, T - 1:T], yi.t[:, T - 1:T], ALU.add), [yr, yi], [hend])
        P.op("dve", lambda h: h.tensor_copy(out_b.t[:, 0:2], hend.t[:, 0:2]), [hend], [out_b])
        return dict(pair=pair, qq=qq, T=T, hrb=hrb, hib=hib, after=[])

    ypb = [pf[4], pf[5]]

    def stageY(c):
        pair, qq, T, hrb, hib = c["pair"], c["qq"], c["T"], c["hrb"], c["hib"]
        for h0 in range(0, T, 512):
            n = min(512, T - h0)
            yp_ = ypb[h0 // 512]
            P.mm([lambda h, yp_=yp_, n=n, h0=h0: h.matmul(yp_.t[:, 0:n], CT[0].t[:, pair, :], hrb.t[:, h0:h0 + n], start=(qq == 0), stop=False),
                  lambda h, yp_=yp_, n=n, h0=h0: h.matmul(yp_.t[:, 0:n], CT[1].t[:, pair, :], hib.t[:, h0:h0 + n], start=False, stop=(qq == 3))],
                 [CT[0], CT[1], hrb, hib], [yp_])
        for f in c["after"]:
            f()

    def y_evac(yp_, ft, col0, n):
        P.op("dve", lambda h: h.scalar_tensor_tensor(ytmp.t[:, 0:n], uT.t[:, ft, col0:col0 + n], dsk.t[:, ft:ft + 1], yp_.t[:, 0:n], ALU.mult, ALU.add),
             [uT, dsk, yp_], [ytmp])
        P.op("dve", lambda h: h.tensor_tensor(ysq.t[:, 0:n], ytmp.t[:, 0:n], ytmp.t[:, 0:n], ALU.mult), [ytmp], [ysq])
        P.op("dve", lambda h: h.tensor_scalar(ysq.t[:, 0:n], ysq.t[:, 0:n], 0.044715, 1.0, ALU.mult, ALU.add), [ysq], [ysq])
        P.op("dve", lambda h: h.tensor_tensor(ysq.t[:, 0:n], ysq.t[:, 0:n], ytmp.t[:, 0:n], ALU.mult), [ysq, ytmp], [ysq])
        P.op("act", lambda h: h.activation(ysq.t[:, 0:n], ysq.t[:, 0:n], AF.Sigmoid, scale=GC), [ysq], [ysq])
        P.op("dve", lambda h: h.tensor_tensor(ygT.t[:, ft, col0:col0 + n], ysq.t[:, 0:n], ytmp.t[:, 0:n], ALU.mult), [ysq, ytmp], [ygT])

    pending = [None]

    def push(ctx):
        if pending[0] is not None:
            stageY(pending[0])
        pending[0] = ctx

    for ft in range(4):
        for qq in range(4):
            pair = ft * 4 + qq
            P.op("dve", lambda h, pair=pair: h.tensor_scalar(ang.t[:], iota.t[:], th.t[:, pair:pair + 1], None, ALU.mult), [iota] + als, [ang])
            sincos(ang.t[:], 1024, tabs[qq].t[:], tabc[qq].t[:], [ang], [tabs[qq], tabc[qq]])
        for seg in range(4):
            for qq in range(4):
                pair = ft * 4 + qq
                ctx = stageX(pair, qq, ft, seg * NCH, 1024, tabc[qq], tabs[qq],
                             stp_l[pair].t[:, 0:1], stp_l[pair].t[:, 1:2], [stp_l[pair]], stp_l[pair])
                if qq == 3:
                    ctx["after"] = [(lambda ft=ft, seg=seg, hf=hf: y_evac(ypb[hf], ft, seg * NCH + hf * 512, 512)) for hf in range(2)]
                push(ctx)
        for sb_i in range(4):
            for qq in range(4):
                pair = ft * 4 + qq
                ctx = stageX(pair, qq, ft, sb_i * NCH + 1024, 8, tabc[qq], tabs[qq],
                             sre0.t[:, sb_i, pair:pair + 1], sim0.t[:, sb_i, pair:pair + 1], [sre0, sim0], sts_l[sb_i][pair])
                if qq == 3:
                    ctx["after"] = [(lambda ft=ft, sb_i=sb_i: y_evac(ypb[0], ft, sb_i * NCH + 1024, 8))]
                push(ctx)
    push(None)
    stp2 = P.sb("stp2", [128, 2, 16], F32, at=tq.at); sts2 = P.sb("sts2", [128, 4, 2, 16], F32, at=tq.at + 128)
    for p_ in range(16):
        P.op("dve", lambda h, p_=p_: h.tensor_copy(stp2.t[:, :, p_], stp_l[p_].t[:, 0:2]), [stp_l[p_]], [stp2])
        for b_ in range(4):
            P.op("dve", lambda h, p_=p_, b_=b_: h.tensor_copy(sts2.t[:, b_, :, p_], sts_l[b_][p_].t[:, 0:2]), [sts_l[b_][p_]], [sts2])
    if STOP == 'SSM':
        for r_ in range(4):
            P.dma("pool", OUT["yp"].ap().rearrange("(p f x) c -> p f (x c)", p=128, f=4)[:, :, r_ * 1024:(r_ + 1) * 1024],
                  ygT.t[:, :, r_ * NCH:r_ * NCH + 1024], [ygT], [outbufs["yp"]], ygT)
        P.dma("pool", OUT["ys"].ap()[:, 0:128].rearrange("p (f r s) -> p f r s", f=4, r=4),
              ygT.t[:].rearrange("p f (r c) -> p f r c", r=4)[:, :, :, 1024:1032], [ygT], [outbufs["ys"]], ygT)
    P.dma("sp", OUT["ssm_p"].ap(), stp2.t[:], [stp2], [outbufs["ssm_p"]], stp2)
    P.dma("sp", OUT["ssm_s"].ap(), sts2.t[:], [sts2], [outbufs["ssm_s"]], sts2)
    for j in range(8):
        P.dma("sp", y_src[j].t.ap(), ygT.t[(j % 2) * 64:(j % 2) * 64 + 64, j // 2, :], [ygT], [y_src[j]], ygT)
        P.coll(y_src[j], y_dst[j], GROUPS)

    if STOP == 'SSM':
        P.barrier()
        return
    P.barrier()
    P.off = CONST_END
    y2T = P.sb("y2T", [128, KT, NO], BF16)
    F0 = P.off
    ygo = P.sb("ygo", [128, KT, NCH], BF16)
    hn1o = P.sb("hn1o", [128, KT, NCH], BF16)
    ych = [P.sb("ych%d" % i, [128, 4, NCH], BF16) for i in range(2)]
    wt2 = [P.sb("wt2_%d" % i, [128, KT, 128], BF16) for i in range(4)]
    gl = P.sb("gl", [128, NCH], F32); zz = P.sb("zz", [128, NCH], F32)
    for j in range(8):
        P.dma("sp", hn1o.t[:, 2 * j:2 * j + 2, :], hn_src[j].t.ap().rearrange("(k p) t -> p k t", p=128), [hn_src[j]], [hn1o], hn1o)
    P.op("pool", lambda h: h.memset(y2T.t[:, :, NCH:NO], 0.0), [], [y2T])
    ci = 0
    for rf in range(4):
        for r in range(4):
            yc = ych[ci % 2]; ci += 1
            for j in range(8):
                P.dma("sp", yc.t[(j % 2) * 64:(j % 2) * 64 + 64, j // 2, :], y_dst[j].t.ap()[rf * 64:(rf + 1) * 64, r * NCH:(r + 1) * NCH],
                      [y_dst[j]], [yc], yc)
            dst = ygo.t[:, rf * 4:(rf + 1) * 4, :]
            if r == 0:
                P.op("dve", lambda h, yc=yc, dst=dst: h.tensor_scalar(dst, yc.t[:], sel.t[:, 0:1], None, ALU.mult), [yc, sel], [ygo])
            else:
                P.op("dve", lambda h, yc=yc, dst=dst, r=r: h.scalar_tensor_tensor(dst, yc.t[:], sel.t[:, r:r + 1], dst, ALU.mult, ALU.add), [yc, sel, ygo], [ygo])
    if STOP == 'G1':
        P.barrier()
        return
    for nt in range(16):
        wa = wt2[2 * (nt % 2)]
        wb_ = wt2[2 * (nt % 2) + 1]
        P.dma("pool", wa.t[:], IN["w_glu"].ap()[:, nt * 128:(nt + 1) * 128].rearrange("(kt p) c -> p kt c", p=128), [], [wa], wa)
        P.dma("pool", wb_.t[:], IN["w_in_c"].ap()[:, 2048 + nt * 128:2048 + (nt + 1) * 128].rearrange("(kt p) c -> p kt c", p=128), [], [wb_], wb_)
        for (c0, n) in [(0, 512), (512, 512), (1024, 8)]:
            proj_feat(wa, None, lambda pk, c0=c0, n=n, nt=nt: P.op("act", lambda h: h.activation(gl.t[:, c0:c0 + n], pk.t[:, 0:n], AF.Sigmoid, bias=bglu.t[:, nt:nt + 1], scale=1.0), [pk, bglu], [gl]),
                      ygo, lambda kt, c0=c0, n=n: ygo.t[:, kt, c0:c0 + n], n)
            def zev(pk, c0=c0, n=n):
                P.op("act", lambda h: h.activation(zz.t[:, c0:c0 + n], pk.t[:, 0:n], AF.Sigmoid), [pk], [zz])
                P.op("dve", lambda h: h.tensor_tensor(zz.t[:, c0:c0 + n], zz.t[:, c0:c0 + n], pk.t[:, 0:n], ALU.mult), [zz, pk], [zz])
            proj_feat(wb_, None, zev, hn1o, lambda kt, c0=c0, n=n: hn1o.t[:, kt, c0:c0 + n], n)
        P.op("dve", lambda h, nt=nt: h.tensor_tensor(gl.t[:], gl.t[:], ygo.t[:, nt, :], ALU.mult), [gl, ygo], [gl])
        P.op("dve", lambda h, nt=nt: h.tensor_tensor(y2T.t[:, nt, 0:NCH], gl.t[:], zz.t[:], ALU.mult), [gl, zz], [y2T])
    if STOP == 'G2':
        P.barrier()
        return
    P.barrier()
    P.off = F0
    wgb2 = [P.sb("wgc%d" % i, [128, KT, 512], BF16) for i in range(2)]
    h1s = [P.sb("h1f%d" % i, [128, D], F32) for i in range(5)]
    junk = P.sb("junk3", [128, D], F32); ss = P.sb("ss3", [128, 4], F32)
    yo = [P.sb("yo%d" % i, [128, D], F32) for i in range(2)]
    load_g("g_fin")
    wi = 0
    for tiles in [list(range(0, 5)), list(range(5, 9))]:
        for si, tj in enumerate(tiles):
            o0 = tj * 128
            P.dma("sp", h1s[si].t[:], h1_scr.t.ap()[o0:o0 + 128, :], [h1_scr], [h1s[si]], h1s[si])
        for g4 in range(4):
            wg = wgb2[wi % 2]
            wi += 1
            P.dma("pool", wg.t[:], IN["w_out_c"].ap()[:, g4 * 512:(g4 + 1) * 512].rearrange("(kt p) c -> p kt c", p=128), [], [wg], wg)
            for si, tj in enumerate(tiles):
                o0 = tj * 128
                ht_ = h1s[si]
                pk = next_pf()
                fns = [(lambda h, pk=pk, kt=kt, o0=o0, wg=wg: h.matmul(pk.t[:], y2T.t[:, kt, o0:o0 + 128], wg.t[:, kt, :],
                                                                         start=(kt == 0), stop=(kt == KT - 1))) for kt in range(KT)]
                P.mm(fns, [y2T, wg], [pk])
                P.op("dve", lambda h, pk=pk, ht_=ht_, g4=g4: h.tensor_tensor(ht_.t[:, g4 * 512:(g4 + 1) * 512], ht_.t[:, g4 * 512:(g4 + 1) * 512], pk.t[:], ALU.add),
                     [pk, ht_], [ht_])
        for si, tj in enumerate(tiles):
            o0 = tj * 128
            ht_ = h1s[si]
            yo_ = yo[tj % 2]
            rmsnorm_rows(ht_, yo_, ss, junk)
            if tj < 8:
                P.dma("sp", OUT["yp"].ap()[o0:o0 + 128, :], yo_.t[:], [yo_], [outbufs["yp"]], yo_)
            else:
                P.dma("sp", OUT["ys"].ap(), yo_.t[:], [yo_], [outbufs["ys"]], yo_)
    P.barrier()


_NC_CACHE = {}


def _rope_tables(pos):
    half = 64
    inv = (np.float32(10000.0) ** (-np.arange(half, dtype=np.float32) / np.float32(half))).astype(np.float32)
    ang = pos.astype(np.float32)[:, None] * inv[None, :]
    return np.cos(ang).astype(np.float32), np.sin(ang).astype(np.float32)


def kernel(x_prompt, x_sample, cache_win_k, cache_win_v, state_conv, state_ssm_re, state_ssm_im,
           attn_norm, w_in_ab, conv_w, w_out_ab, ssm_norm, w_in_c, lam_re, lam_im, log_step,
           b_re, b_im, c_re, c_im, d_skip, w_glu, b_glu, w_out_c, final_norm):
    f = lambda a: np.ascontiguousarray(np.asarray(a, dtype=np.float32))
    x_prompt, x_sample = f(x_prompt), f(x_sample)
    cache_win_k, cache_win_v, state_conv = f(cache_win_k), f(cache_win_v), f(state_conv)
    state_ssm_re, state_ssm_im = f(state_ssm_re), f(state_ssm_im)
    w_in_ab0, w_out_ab0, w_in_c0, w_glu0, w_out_c0 = f(w_in_ab)[0], f(w_out_ab)[0], f(w_in_c)[0], f(w_glu)[0], f(w_out_c)[0]
    lam_re, lam_im, log_step = f(lam_re)[0], f(lam_im)[0], f(log_step)[0]
    b_re, b_im, c_re, c_im = f(b_re)[0], f(b_im)[0], f(c_re)[0], f(c_im)[0]
    d_skip0, b_glu0 = f(d_skip)[0], f(b_glu)[0]
    if "nc" not in _NC_CACHE:
        _NC_CACHE["nc"] = build_nc()
    nc = _NC_CACHE["nc"]

    kk = np.arange(128)[:, None]
    qq_ = np.arange(512)[None, :]
    maskp = np.stack([mult_of(qq_ - ((i - 16) * 128 + kk)) for i in range(20)], 1)
    rows = np.arange(2176).reshape(17, 128)
    s_ = np.arange(128)[None, :]
    masks = np.zeros((128, 17, 128), np.float32)
    for i in range(17):
        row = rows[i][:, None]
        m = mult_of(2048 + s_ - row)
        m[:, 8:] = ((2048 + s_[:, 8:] - row) == 0)
        masks[:, i, :] = m
    iota = np.broadcast_to(np.arange(1024, dtype=np.float32)[None, :], (128, 1024)).copy()
    rmask = np.zeros((128, 8), np.float32)
    for p in range(128):
        rmask[p, (p // 32) * 2 + (p % 32) // 16] = 1.0
    smask = np.zeros((128, 2), np.float32)
    smask[:64, 0] = 1.0
    smask[64:, 1] = 1.0
    bc = lambda v: np.ascontiguousarray(np.broadcast_to(v[None, :], (128, v.shape[0])))

    in_maps = []
    for c in range(8):
        b, r = c // 4, c % 4
        T0 = r * NOWN
        xh = np.zeros((NTP, D), np.float32)
        lo = T0 - NHALO
        src_lo = max(lo, 0)
        xh[src_lo - lo:] = x_prompt[b, src_lo:T0 + NOWN]
        pos = np.concatenate([np.arange(lo, T0 + NOWN), PAST + np.arange(128)]).astype(np.float32)
        valid = (pos[:NTP] >= 0).astype(np.float32)
        cosv, sinv = _rope_tables(np.maximum(pos, 0))
        xs = np.zeros((128, D), np.float32)
        xs[:8] = x_sample[c]
        g0 = 32 * r
        gs = slice(g0, g0 + 32)
        st_lay = lambda a: np.ascontiguousarray(a.reshape(16, 2, 64).transpose(1, 2, 0).reshape(128, 16))
        def row_lay_rep(a):
            t = a.reshape(4, 4, 2, 64)
            t = np.broadcast_to(t[:, :, :, None, :], (4, 4, 2, 16, 64))
            return np.ascontiguousarray(t.transpose(1, 2, 3, 0, 4).reshape(128, 4, 64))
        def row_lay_b(a):
            t = a.reshape(4, 4, 2, 64, 16)
            return np.ascontiguousarray(t.transpose(1, 2, 4, 0, 3).reshape(128, 4, 64))
        def st_lay_c(a):
            t = a.reshape(16, 2, 16, 64)
            return np.ascontiguousarray(t.transpose(1, 3, 0, 2).reshape(128, 16, 16))
        lst32 = np.broadcast_to(log_step[gs][:, None], (32, 64))
        sel = np.zeros((128, 4), np.float32)
        sel[:, r] = 1.0
        sre0 = np.stack([st_lay(state_ssm_re[0, 4 * b + i, gs]) for i in range(4)], 1)
        sim0 = np.stack([st_lay(state_ssm_im[0, 4 * b + i, gs]) for i in range(4)], 1)
        w_in_c_rolled = np.concatenate([w_in_c0[:, 512 * r:512 * (r + 1)], w_in_c0[:, 512:2048], w_in_c0[:, 2048:]], 1)
        m = {
            "xh": xh, "xs": xs,
            "cs": np.ascontiguousarray(cosv.reshape(25, 128, 64).transpose(1, 0, 2)),
            "sn": np.ascontiguousarray(sinv.reshape(25, 128, 64).transpose(1, 0, 2)),
            "valid": np.ascontiguousarray(valid.reshape(24, 128).T),
            "ck": np.ascontiguousarray(cache_win_k[0, c].reshape(2048, 1024)),
            "cv": np.ascontiguousarray(cache_win_v[0, c].reshape(2048, 1024)),
            "sconv": np.ascontiguousarray(state_conv[0, c].reshape(2, 8, 128).transpose(2, 1, 0)),
            "g_attn": bc(f(attn_norm)[0]), "g_ssm": bc(f(ssm_norm)[0]), "g_fin": bc(f(final_norm)),
            "w_in_ab": w_in_ab0, "cw": np.ascontiguousarray(f(conv_w)[0].reshape(3, 8, 128).transpose(2, 1, 0)),
            "w_out_ab": w_out_ab0, "w_in_c": np.ascontiguousarray(w_in_c_rolled),
            "w_glu": w_glu0, "w_out_c": w_out_c0,
            "bglu": np.ascontiguousarray(b_glu0.reshape(16, 128).T),
            "dsk": np.ascontiguousarray(d_skip0[512 * r:512 * (r + 1)].reshape(4, 128).T),
            "maskp": maskp, "masks": masks,
            "lre_s": st_lay(lam_re[gs]), "lim_s": st_lay(lam_im[gs]), "lst_s": st_lay(lst32),
            "lre_r": row_lay_rep(lam_re[gs]), "lim_r": row_lay_rep(lam_im[gs]), "lst_r": row_lay_rep(np.ascontiguousarray(lst32)),
            "bre_r": row_lay_b(b_re[gs]), "bim_r": row_lay_b(b_im[gs]),
            "cre_s": st_lay_c(c_re[gs]), "cim_s": st_lay_c(c_im[gs]),
            "rmask": rmask, "smask": smask, "sel": sel, "sre0": sre0, "sim0": sim0, "iota": iota,
        }
        in_maps.append({k: np.ascontiguousarray(v, dtype=np.float32) for k, v in m.items()})

    res = run_bass_kernel_spmd(nc, in_maps, core_ids=list(range(8)))
    R = res.results
    _NC_CACHE['raw'] = R
    y_prompt = np.zeros((2, SEQ, D), np.float32)
    y_sample = np.zeros((8, 8, D), np.float32)
    kp = np.zeros((1, 2, 2048, 8, 128), np.float32)
    vp = np.zeros((1, 2, 2048, 8, 128), np.float32)
    convp = np.zeros((1, 2, 2, 1024), np.float32)
    srp = np.zeros((1, 2, 128, 64), np.float32)
    sip = np.zeros((1, 2, 128, 64), np.float32)
    ks = np.zeros((1, 8, 8, 8, 128), np.float32)
    vs = np.zeros((1, 8, 8, 8, 128), np.float32)
    convs = np.zeros((1, 8, 2, 1024), np.float32)
    srs = np.zeros((1, 8, 128, 64), np.float32)
    sis = np.zeros((1, 8, 128, 64), np.float32)
    unst = lambda a: a.reshape(2, 64, 16).transpose(2, 0, 1).reshape(32, 64)
    for c in range(8):
        b, r = c // 4, c % 4
        o = R[c]
        y_prompt[b, r * NOWN:(r + 1) * NOWN] = o["yp"]
        y_sample[c] = o["ys"][:8]
        if r >= 2:
            kp[0, b, (r - 2) * NOWN:(r - 1) * NOWN] = o["kp"].reshape(NOWN, 8, 128)
            vp[0, b, (r - 2) * NOWN:(r - 1) * NOWN] = o["vp"].reshape(NOWN, 8, 128)
        if r == 3:
            convp[0, b] = o["convp"].transpose(2, 1, 0).reshape(2, 1024)
        ks[0, c] = o["ks"][:8].reshape(8, 8, 128)
        vs[0, c] = o["vs"][:8].reshape(8, 8, 128)
        convs[0, c] = o["convs"].transpose(2, 1, 0).reshape(2, 1024)
        srp[0, b, 32 * r:32 * (r + 1)] = unst(o["ssm_p"][:, 0, :])
        sip[0, b, 32 * r:32 * (r + 1)] = unst(o["ssm_p"][:, 1, :])
        for i in range(4):
            srs[0, 4 * b + i, 32 * r:32 * (r + 1)] = unst(o["ssm_s"][:, i, 0, :])
            sis[0, 4 * b + i, 32 * r:32 * (r + 1)] = unst(o["ssm_s"][:, i, 1, :])
    return (y_prompt, y_sample, kp, vp, convp, srp, sip, ks, vs, convs, srs, sis)
```

```python
import math
import os
STOP = os.environ.get('MK_STOP', '')
from contextlib import ExitStack

import numpy as np
import concourse.bass as bass
import concourse.mybir as mybir
from concourse.bass_utils import run_bass_kernel_spmd

F32 = mybir.dt.float32
BF16 = mybir.dt.bfloat16
ALU = mybir.AluOpType
AF = mybir.ActivationFunctionType
AX = mybir.AxisListType

ENGS = ["pe", "act", "dve", "pool", "sp"]
D = 2048
KT = 16
NOWN = 1024
NHALO = 2048
NTP = NOWN + NHALO
NTILE_P = NTP // 128
NO = NOWN + 128
SEQ = 4096
PAST = 16384
NCH = 1032
TWO_PI = 2.0 * math.pi


class Buf:
    def __init__(self, t, name):
        self.t = t
        self.name = name
        self.w = {}
        self.r = {}
        self.dsem = None
        self.dcnt = 0


class Prog:
    def __init__(self, nc, stack):
        self.nc = nc
        self.stack = stack
        self.q = {e: [] for e in ENGS}
        self.cnt = {e: 0 for e in ENGS}
        self.seen = {e: {} for e in ENGS}
        self.sems = {}
        self.semval = {}
        for e in ["pe", "act", "dve", "pool"]:
            self.sems[e] = stack.enter_context(nc.semaphore("s_" + e))
        self.off = 16512
        self.free = []
        self.dval = {}
        self.phase_bufs = []

    def sb(self, name, shape, dt, at=None):
        nbytes = int(np.prod(shape[1:])) * (2 if dt == BF16 else 4)
        if at is None:
            at = self.off
            self.off = (at + nbytes + 63) // 64 * 64
        assert at + nbytes <= 229300, (name, at, nbytes)
        t = self.nc.alloc_sbuf_tensor_at(name, list(shape), dt, offset=at)
        b = Buf(t, name)
        b.at = at
        b.nbytes = nbytes
        return b

    def ps(self, name, shape, dt=F32):
        t = self.stack.enter_context(self.nc.psum_tensor(name, list(shape), dt))
        return Buf(t, name)

    def dram(self, name, shape, dt, kind="Internal"):
        t = self.nc.dram_tensor(name, list(shape), dt, kind=kind)
        return Buf(t, name)

    def _need(self, eng, k, v, waits):
        if self.seen[eng].get(k, 0) >= v:
            return
        waits[k] = max(waits.get(k, 0), v)

    def _deps(self, eng, reads, writes):
        waits = {}
        for b in reads:
            for k, v in b.w.items():
                self._need(eng, k, v, waits)
        for b in writes:
            for k, v in b.w.items():
                self._need(eng, k, v, waits)
            for k, v in b.r.items():
                self._need(eng, k, v, waits)
        for k, v in waits.items():
            self.seen[eng][k] = v
        return [(self.sems[k], v) for k, v in waits.items()]

    def _commit(self, k, v, reads, writes):
        self.semval[k] = v
        for b in reads:
            b.r[k] = max(b.r.get(k, 0), v)
        for b in writes:
            b.w[k] = max(b.w.get(k, 0), v)
            b.r = {}

    def op(self, eng, fn, reads=(), writes=()):
        reads = [b for b in reads if b is not None]
        writes = [b for b in writes if b is not None]
        wl = self._deps(eng, reads, writes)
        self.cnt[eng] += 1
        sem = self.sems[eng]

        def emit(h, fn=fn, wl=wl, sem=sem):
            for s, v in wl:
                h.wait_ge(s, v)
            fn(h).then_inc(sem, 1)

        self.q[eng].append(emit)
        self._commit(eng, self.cnt[eng], reads, writes)

    def mm(self, fns, reads, writes):
        eng = "pe"
        wl = self._deps(eng, reads, writes)
        self.cnt[eng] += 1
        sem = self.sems[eng]

        def emit(h, fns=fns, wl=wl, sem=sem):
            for s, v in wl:
                h.wait_ge(s, v)
            for f in fns[:-1]:
                f(h)
            fns[-1](h).then_inc(sem, 1)

        self.q[eng].append(emit)
        self._commit(eng, self.cnt[eng], reads, writes)

    def dma(self, eng, out, in_, reads, writes, semb, **kw):
        reads = [b for b in reads if b is not None]
        writes = [b for b in writes if b is not None]
        if semb.dsem is None:
            if self.free:
                key = self.free.pop()
            else:
                key = "d%d" % len(self.sems)
                self.sems[key] = self.stack.enter_context(self.nc.semaphore(key))
            semb.dsem = key
            semb.dcnt = self.dval.get(key, 0)
            self.phase_bufs.append(semb)
        wl = self._deps(eng, reads, writes)
        semb.dcnt += 16
        self.dval[semb.dsem] = semb.dcnt
        sem = self.sems[semb.dsem]

        def emit(h, wl=wl, sem=sem, out=out, in_=in_, kw=kw):
            for s, v in wl:
                h.wait_ge(s, v)
            h.dma_start(out=out, in_=in_, **kw).then_inc(sem, 16)

        self.q[eng].append(emit)
        self._commit(semb.dsem, semb.dcnt, reads, writes)

    def coll(self, src, dst, groups):
        key = "c%d" % len(self.sems)
        self.sems[key] = self.stack.enter_context(self.nc.semaphore(key))
        wl = self._deps("pool", [src], [dst])
        sem = self.sems[key]

        def emit(h, wl=wl, sem=sem):
            for s, v in wl:
                h.wait_ge(s, v)
            h.collective_compute("AllGather", ALU.bypass, replica_groups=groups,
                                 ins=[src.t.ap()], outs=[dst.t.ap()]).then_inc(sem)
            h.wait_ge(sem, 1)

        self.q["pool"].append(emit)
        self.cnt["pool"] += 1
        s2 = self.sems["pool"]
        self.q["pool"].append(lambda h, s2=s2: h.engine_nop().then_inc(s2, 1))
        self._commit("pool", self.cnt["pool"], [src], [dst])

    def barrier(self):
        for b in self.phase_bufs:
            self.free.append(b.dsem)
            b.dsem = None
        self.phase_bufs = []
        items = list(self.semval.items())
        for e in ENGS:
            wl = []
            for k, v in items:
                if self.seen[e].get(k, 0) < v:
                    self.seen[e][k] = v
                    wl.append((self.sems[k], v))

            def emit(h, wl=wl):
                for s, v in wl:
                    h.wait_ge(s, v)

            if wl:
                self.q[e].append(emit)

    def run(self):
        nc = self.nc
        with nc.Block() as block:
            @block.tensor
            def _(h):
                for f in self.q["pe"]:
                    f(h)

            @block.scalar
            def _(h):
                for f in self.q["act"]:
                    f(h)

            @block.vector
            def _(h):
                for f in self.q["dve"]:
                    f(h)

            @block.gpsimd
            def _(h):
                for f in self.q["pool"]:
                    f(h)

            @block.sync
            def _(h):
                for f in self.q["sp"]:
                    f(h)


def mult_of(d):
    d = np.asarray(d)
    m = ((d >= 0) & (d <= 128)).astype(np.float32)
    m += ((d >= 0) & (d <= 512) & (d % 4 == 0))
    m += ((d >= 0) & (d <= 2048) & (d % 16 == 0))
    return m.astype(np.float32)


IN_SPECS = [
    ("xh", [NTP, D]), ("xs", [128, D]), ("cs", [128, 25, 64]), ("sn", [128, 25, 64]),
    ("valid", [128, 24]), ("ck", [2048, 1024]), ("cv", [2048, 1024]), ("sconv", [128, 8, 2]),
    ("g_attn", [128, D]), ("g_ssm", [128, D]), ("g_fin", [128, D]),
    ("w_in_ab", [D, 8192]), ("cw", [128, 8, 3]), ("w_out_ab", [D, D]), ("w_in_c", [D, 4096]),
    ("w_glu", [D, D]), ("w_out_c", [D, D]), ("bglu", [128, 16]), ("dsk", [128, 4]),
    ("maskp", [128, 20, 512]), ("masks", [128, 17, 128]),
    ("lre_s", [128, 16]), ("lim_s", [128, 16]), ("lst_s", [128, 16]),
    ("lre_r", [128, 4, 64]), ("lim_r", [128, 4, 64]), ("lst_r", [128, 4, 64]),
    ("bre_r", [128, 4, 64]), ("bim_r", [128, 4, 64]),
    ("cre_s", [128, 16, 16]), ("cim_s", [128, 16, 16]),
    ("rmask", [128, 8]), ("smask", [128, 2]), ("sel", [128, 4]),
    ("sre0", [128, 4, 16]), ("sim0", [128, 4, 16]), ("iota", [128, 1024]),
]
OUT_SPECS = [
    ("yp", [NOWN, D]), ("ys", [128, D]), ("kp", [NOWN, 1024]), ("vp", [NOWN, 1024]),
    ("convp", [128, 8, 2]), ("ssm_p", [128, 2, 16]), ("ks", [128, 1024]), ("vs", [128, 1024]),
    ("convs", [128, 8, 2]), ("ssm_s", [128, 4, 2, 16]),
]


def build_nc():
    nc = bass.Bass("TRN2", target_bir_lowering=False)
    IN = {}
    for n, s in IN_SPECS:
        IN[n] = nc.dram_tensor(n, s, F32, kind="ExternalInput")
    OUT = {}
    for n, s in OUT_SPECS:
        OUT[n] = nc.dram_tensor(n, s, F32, kind="ExternalOutput")
    st = ExitStack()
    with st:
        P = Prog(nc, st)
        build_program(nc, P, IN, OUT)
        P.run()
    return nc


def build_program(nc, P, IN, OUT):
    GROUPS = [[0, 1, 2, 3], [4, 5, 6, 7]]
    outbufs = {n: Buf(OUT[n], n) for n in OUT}
    kT_scr = P.dram("kT_scr", [8, 128, NTP], BF16)
    v_scr = P.dram("v_scr", [NTP, 1024], BF16)
    kTs_scr = P.dram("kTs_scr", [8, 128, 2176], BF16)
    vs_scr = P.dram("vs_scr", [2176, 1024], BF16)
    qT_scr = P.dram("qT_scr", [8, 128, NO], BF16)
    h1_scr = P.dram("h1_scr", [NO, D], F32)
    hn_src = [P.dram("hn_src%d" % j, [256, NCH], BF16) for j in range(8)]
    hn_dst = [P.dram("hn_dst%d" % j, [4 * 256, NCH], BF16) for j in range(8)]
    y_src = [P.dram("y_src%d" % j, [64, 4 * NCH], BF16) for j in range(8)]
    y_dst = [P.dram("y_dst%d" % j, [4 * 64, 4 * NCH], BF16) for j in range(8)]

    pf = [P.ps("pf%d" % i, [128, 512], F32) for i in range(6)]
    pb = [P.ps("pb%d" % i, [128, 8, 128], BF16) for i in range(2)]
    pfi = [0]
    pbi = [0]

    def next_pf():
        pfi[0] = (pfi[0] + 1) % 4
        return pf[pfi[0]]

    def next_pb():
        pbi[0] = (pbi[0] + 1) % 2
        return pb[pbi[0]]

    ident = P.sb("ident", [128, 128], BF16)
    P.op("pool", lambda h: h.memset(ident.t[:], 1.0), [], [ident])
    P.op("pool", lambda h: h.affine_select(ident.t[:], ident.t[:], [[-1, 128]], ALU.is_equal, 0.0,
                                            base=0, channel_multiplier=1), [ident], [ident])
    ones_bf = P.sb("ones_bf", [128, 128], BF16)
    P.op("pool", lambda h: h.memset(ones_bf.t[:], 1.0), [], [ones_bf])
    gt = P.sb("gt", [128, D], F32)
    cs = P.sb("cs", [128, 25, 64], F32)
    sn = P.sb("sn", [128, 25, 64], F32)
    valid = P.sb("valid", [128, 24], F32)
    validB = P.sb("validB", [128, 24, 128], BF16)
    cw = P.sb("cw", [128, 8, 3], F32)
    bglu = P.sb("bglu", [128, 16], F32)
    dsk = P.sb("dsk", [128, 4], F32)
    sel = P.sb("sel", [128, 4], F32)
    eps_t = P.sb("eps_t", [128, 1], F32)
    P.op("pool", lambda h: h.memset(eps_t.t[:], 1e-6), [], [eps_t])
    for b_, n in [(cs, "cs"), (sn, "sn"), (valid, "valid"), (cw, "cw"), (bglu, "bglu"), (dsk, "dsk"), (sel, "sel")]:
        P.dma("sp", b_.t[:], IN[n].ap(), [], [b_], b_)
    P.op("dve", lambda h: h.tensor_copy(validB.t[:], valid.t[:].unsqueeze(2).to_broadcast([128, 24, 128])),
         [valid], [validB])
    pospi = P.sb("pospi", [128, 1], F32)
    P.op("pool", lambda h: h.memset(pospi.t[:], math.pi), [], [pospi])
    CONST_END = P.off
    hnT_o = P.sb("hnT_o", [128, KT, NO], BF16)

    def load_g(name):
        P.dma("sp", gt.t[:], IN[name].ap(), [], [gt], gt)

    def rmsnorm_rows(xt, xn, ss, junk):
        P.op("act", lambda h: h.activation(junk.t[:], xt.t[:], AF.Square, accum_out=ss.t[:, 0:1]), [xt], [junk, ss])
        P.op("act", lambda h: h.activation(ss.t[:, 1:2], ss.t[:, 0:1], AF.Sqrt, bias=eps_t.t[:, 0:1], scale=1.0 / D), [ss, eps_t], [ss])
        P.op("dve", lambda h: h.reciprocal(ss.t[:, 2:3], ss.t[:, 1:2]), [ss], [ss])
        P.op("dve", lambda h: h.scalar_tensor_tensor(xn.t[:], xt.t[:], ss.t[:, 2:3], gt.t[:], ALU.mult, ALU.mult),
             [xt, ss, gt], [xn])

    def transpose_rows(xn, dst, dst_ap_fn):
        for half in range(2):
            p = next_pb()
            fns = []
            for j in range(8):
                kt = half * 8 + j
                fns.append(lambda h, p=p, j=j, kt=kt: h.transpose(p.t[:, j, :], xn.t[:, kt * 128:(kt + 1) * 128], ident.t[:]))
            P.mm(fns, [xn, ident], [p])
            P.op("act", lambda h, p=p, half=half: h.activation(dst_ap_fn(half), p.t[:], AF.Identity), [p], [dst])

    A0 = P.off
    wkv = P.sb("wkv", [128, KT, 2048], BF16)
    xts = [P.sb("xt%d" % i, [128, D], F32) for i in range(3)]
    xns = [P.sb("xn%d" % i, [128, D], BF16) for i in range(3)]
    hts = [P.sb("ht%d" % i, [128, KT, 128], BF16) for i in range(3)]
    ss = P.sb("ss", [128, 4], F32)
    krs = [P.sb("kr%d" % i, [128, 1024], F32) for i in range(2)]
    vfs = [P.sb("vf%d" % i, [128, 1024], F32) for i in range(2)]
    t1 = P.sb("t1", [128, 256], F32)
    t2 = P.sb("t2", [128, 256], F32)
    krbs = [P.sb("krb%d" % i, [128, 1024], BF16) for i in range(2)]
    vbs = [P.sb("vb%d" % i, [128, 1024], BF16) for i in range(2)]
    kTts = [P.sb("kTt%d" % i, [128, 8, 128], BF16) for i in range(2)]
    kTt = kTts[0]
    hprev2 = P.sb("hprev2", [128, KT, 2], BF16)
    A1_END = P.off

    load_g("g_attn")
    for half in range(2):
        P.dma("pool", wkv.t[:, :, half * 1024:(half + 1) * 1024],
              IN["w_in_ab"].ap()[:, 1024 + half * 1024:2048 + half * 1024].rearrange("(kt p) c -> p kt c", p=128),
              [], [wkv], wkv)

    def rotary(pk, ti, dst, c0):
        v = pk.t[:].rearrange("p (h two d) -> p h two d", h=4, two=2)
        o = dst.t[:, c0:c0 + 512].rearrange("p (h two d) -> p h two d", h=4, two=2)
        cb = cs.t[:, ti, :].unsqueeze(1).to_broadcast([128, 4, 64])
        sb_ = sn.t[:, ti, :].unsqueeze(1).to_broadcast([128, 4, 64])
        a = t1.t[:].rearrange("p (h d) -> p h d", h=4)
        b = t2.t[:].rearrange("p (h d) -> p h d", h=4)
        P.op("dve", lambda h: h.tensor_tensor(a, v[:, :, 0, :], cb, ALU.mult), [pk, cs], [t1])
        P.op("dve", lambda h: h.tensor_tensor(b, v[:, :, 1, :], sb_, ALU.mult), [pk, sn], [t2])
        P.op("dve", lambda h: h.tensor_tensor(o[:, :, 0, :], a, b, ALU.subtract), [t1, t2], [dst])
        P.op("dve", lambda h: h.tensor_tensor(a, v[:, :, 1, :], cb, ALU.mult), [pk, cs], [t1])
        P.op("dve", lambda h: h.tensor_tensor(b, v[:, :, 0, :], sb_, ALU.mult), [pk, sn], [t2])
        P.op("dve", lambda h: h.tensor_tensor(o[:, :, 1, :], a, b, ALU.add), [t1, t2], [dst])

    def store_kT(src_bf, scr, col0, kb=None):
        p = next_pb()
        if kb is None:
            kb = kTt
        fns = [(lambda h, p=p, j=j: h.transpose(p.t[:, j, :], src_bf.t[:, j * 128:(j + 1) * 128], ident.t[:])) for j in range(8)]
        P.mm(fns, [src_bf, ident], [p])
        P.op("act", lambda h, p=p, kb=kb: h.activation(kb.t[:], p.t[:], AF.Identity), [p], [kb])
        P.dma("sp", scr.t.ap()[:, :, col0:col0 + 128].rearrange("h d t -> d h t"), kb.t[:], [kb], [scr], kb)

    def stageA(ti):
        xt = xts[ti % 3]
        xn = xns[ti % 3]
        src = IN["xh"].ap()[ti * 128:(ti + 1) * 128, :] if ti < 24 else IN["xs"].ap()
        P.dma("sp", xt.t[:], src, [], [xt], xt)
        rmsnorm_rows(xt, xn, ss, xn)
        if ti < 16:
            ht = hts[ti % 3]
            transpose_rows(xn, ht, lambda half, ht=ht: ht.t[:, half * 8:(half + 1) * 8, :])
            if ti == 15:
                P.op("dve", lambda h, ht=ht: h.tensor_copy(hprev2.t[:], ht.t[:, :, 126:128]), [ht], [hprev2])
            return (lambda kt, ht=ht: ht.t[:, kt, :]), ht
        o0 = (ti - 16) * 128
        transpose_rows(xn, hnT_o, lambda half, o0=o0: hnT_o.t[:, half * 8:(half + 1) * 8, o0:o0 + 128])
        return (lambda kt, o0=o0: hnT_o.t[:, kt, o0:o0 + 128]), hnT_o

    def stageM(ti, lhs, hb):
        kr = krs[ti % 2]
        vf = vfs[ti % 2]
        for g4 in range(4):
            pk = next_pf()
            fns = [(lambda h, pk=pk, kt=kt, g4=g4, lhs=lhs: h.matmul(pk.t[:], lhs(kt), wkv.t[:, kt, g4 * 512:(g4 + 1) * 512],
                                                                    start=(kt == 0), stop=(kt == KT - 1))) for kt in range(KT)]
            P.mm(fns, [hb, wkv], [pk])
            if g4 < 2:
                rotary(pk, ti, kr, g4 * 512)
            else:
                c0 = (g4 - 2) * 512
                P.op("act", lambda h, pk=pk, c0=c0, vf=vf: h.activation(vf.t[:, c0:c0 + 512], pk.t[:], AF.Identity), [pk], [vf])

    def stageK(ti):
        kr = krs[ti % 2]
        vf = vfs[ti % 2]
        krb = krbs[ti % 2]
        vb = vbs[ti % 2]
        kb = kTts[ti % 2]
        P.op("act", lambda h: h.activation(krb.t[:], kr.t[:], AF.Identity), [kr], [krb])
        if ti < 24:
            P.op("dve", lambda h: h.tensor_scalar(vb.t[:], vf.t[:], valid.t[:, ti:ti + 1], None, ALU.mult), [vf, valid], [vb])
            store_kT(krb, kT_scr, ti * 128, kb)
            P.dma("sp", v_scr.t.ap()[ti * 128:(ti + 1) * 128, :], vb.t[:], [vb], [v_scr], vb)
            if ti >= 16:
                r0 = (ti - 16) * 128
                P.dma("sp", OUT["kp"].ap()[r0:r0 + 128, :], kr.t[:], [kr], [outbufs["kp"]], kr)
                P.dma("sp", OUT["vp"].ap()[r0:r0 + 128, :], vf.t[:], [vf], [outbufs["vp"]], vf)
        else:
            P.op("dve", lambda h: h.tensor_copy(vb.t[:], vf.t[:]), [vf], [vb])
            store_kT(krb, kTs_scr, 2048, kb)
            P.dma("sp", vs_scr.t.ap()[2048:2176, :], vb.t[:], [vb], [vs_scr], vb)
            P.dma("sp", OUT["ks"].ap(), kr.t[:], [kr], [outbufs["ks"]], kr)
            P.dma("sp", OUT["vs"].ap(), vf.t[:], [vf], [outbufs["vs"]], vf)

    infoA = {0: stageA(0), 1: stageA(1)}
    for ti in range(25):
        if ti + 2 < 25:
            infoA[ti + 2] = stageA(ti + 2)
        stageM(ti, *infoA[ti])
        if ti >= 1:
            stageK(ti - 1)
    stageK(24)
    for ti in range(16):
        xt = xts[ti % 2]
        krb = krbs[ti % 2]
        vb = vbs[ti % 2]
        P.dma("sp", xt.t[:, 0:1024], IN["ck"].ap()[ti * 128:(ti + 1) * 128, :], [], [xt], xt)
        P.dma("sp", xt.t[:, 1024:2048], IN["cv"].ap()[ti * 128:(ti + 1) * 128, :], [], [xt], xt)
        P.op("act", lambda h, xt=xt, krb=krb: h.activation(krb.t[:], xt.t[:, 0:1024], AF.Identity), [xt], [krb])
        P.op("dve", lambda h, xt=xt, vb=vb: h.tensor_copy(vb.t[:], xt.t[:, 1024:2048]), [xt], [vb])
        store_kT(krb, kTs_scr, ti * 128, kTts[ti % 2])
        P.dma("sp", vs_scr.t.ap()[ti * 128:(ti + 1) * 128, :], vb.t[:], [vb], [vs_scr], vb)

    if STOP == 'A1':
        P.barrier()
        return
    P.barrier()
    P.off = A0
    wq = P.sb("wq", [128, KT, 1024], BF16)
    hprev2b = P.sb("hprev2b", [128, KT, 2], BF16)
    qf = P.sb("qf", [128, 1024], F32)
    qb = P.sb("qb", [128, 1024], BF16)
    t1 = P.sb("t1b", [128, 256], F32)
    t2 = P.sb("t2b", [128, 256], F32)
    kTt = P.sb("kTtb", [128, 8, 128], BF16)
    hprev2k = P.sb("hprev2k", [128, KT, 2], BF16, at=hprev2.at)
    hprev2k.w = dict(hprev2.w)
    P.dma("pool", wq.t[:], IN["w_in_ab"].ap()[:, 0:1024].rearrange("(kt p) c -> p kt c", p=128), [], [wq], wq)
    for tj in range(9):
        ti = 16 + tj
        o0 = tj * 128
        for g2_ in range(2):
            pk = next_pf()
            fns = [(lambda h, pk=pk, kt=kt, g2_=g2_, o0=o0: h.matmul(pk.t[:], hnT_o.t[:, kt, o0:o0 + 128],
                                                                      wq.t[:, kt, g2_ * 512:(g2_ + 1) * 512],
                                                                      start=(kt == 0), stop=(kt == KT - 1))) for kt in range(KT)]
            P.mm(fns, [hnT_o, wq], [pk])
            rotary(pk, ti, qf, g2_ * 512)
        P.op("act", lambda h: h.activation(qb.t[:], qf.t[:], AF.Identity), [qf], [qb])
        store_kT(qb, qT_scr, o0)

    if STOP == 'A2':
        P.barrier()
        return
    P.barrier()
    P.off = A0
    ocat = P.sb("ocat", [128, KT, NO], BF16)
    hp2 = P.sb("hp2", [128, KT, 2], BF16)
    B0 = P.off
    P.op("dve", lambda h: h.tensor_copy(hp2.t[:], hprev2k.t[:]), [hprev2k], [hp2])
    P.barrier()
    maskp = P.sb("maskp", [128, 20, 512], BF16)
    masks_ = P.sb("masks_", [128, 17, 128], BF16)
    P.dma("pool", maskp.t[:], IN["maskp"].ap(), [], [maskp], maskp)
    P.dma("pool", masks_.t[:], IN["masks"].ap(), [], [masks_], masks_)
    kTh = [P.sb("kTh%d" % i, [128, NTP], BF16) for i in range(1)] * 2
    vh = [P.sb("vh%d" % i, [128, 24, 128], BF16) for i in range(1)] * 2
    kTsh = [P.sb("kTsh%d" % i, [128, 2176], BF16) for i in range(1)] * 2
    vsh = [P.sb("vsh%d" % i, [128, 17, 128], BF16) for i in range(1)] * 2
    qTh = [P.sb("qTh%d" % i, [128, NO], BF16) for i in range(1)] * 2
    wt = [P.sb("wt%d" % i, [128, KT, 128], BF16) for i in range(4)]
    pts = [P.sb("pt%d" % i, [128, 512], BF16) for i in range(4)]
    ptm = [P.sb("ptm%d" % i, [128, 512], BF16) for i in range(4)]
    za = P.sb("za", [128, NO], F32)
    rl = P.sb("rl", [128, 512], F32)
    of = P.sb("of", [128, 512], F32)
    sg = P.sb("sg", [128, 512], F32)

    def silu_evac(pk, dstb, dst_ap, n):
        P.op("act", lambda h: h.activation(sg.t[:, 0:n], pk.t[:, 0:n], AF.Exp, scale=-1.0), [pk], [sg])
        P.op("dve", lambda h: h.tensor_scalar(sg.t[:, 0:n], sg.t[:, 0:n], 1.0, None, ALU.add), [sg], [sg])
        P.op("dve", lambda h: h.reciprocal(sg.t[:, 0:n], sg.t[:, 0:n]), [sg], [sg])
        P.op("dve", lambda h: h.tensor_tensor(dst_ap, pk.t[:, 0:n], sg.t[:, 0:n], ALU.mult), [pk, sg], [dstb])
    fb = [P.sb("fb%d" % i, [128, NO + 2], F32) for i in range(4)]
    convo_p = P.sb("convo_p", [128, 8, 2], F32)
    convo_s = P.sb("convo_s", [128, 8, 2], F32)
    sconv = P.sb("sconv", [128, 8, 2], F32)
    P.dma("sp", sconv.t[:], IN["sconv"].ap(), [], [sconv], sconv)
    scale = 128.0 ** -0.5

    def load_wt(i, c0):
        P.dma("pool", wt[i].t[:], IN["w_in_ab"].ap()[:, c0:c0 + 128].rearrange("(kt p) c -> p kt c", p=128), [], [wt[i]], wt[i])

    def proj_feat(wb, dst_ap_fn, evac, rhs_buf, rhs_fn, n):
        pk = next_pf()
        fns = [(lambda h, pk=pk, kt=kt: h.matmul(pk.t[:, 0:n], wb.t[:, kt, :], rhs_fn(kt), start=(kt == 0), stop=(kt == KT - 1)))
               for kt in range(KT)]
        P.mm(fns, [wb, rhs_buf], [pk])
        evac(pk)

    def attention(hh, qT, q0, nq, kT, vt, ktiles, mask_fn, vB_fn, o_dst_fn, zcol0):
        po = pf[4]
        pl = pf[5]
        nk = len(ktiles)
        LA = 3
        pms = {}

        def issue_S(i):
            kt_ = ktiles[i]
            ps_ = next_pf()
            P.mm([lambda h, ps_=ps_, kt_=kt_: h.matmul(ps_.t[:, 0:nq], kT.t[:, kt_ * 128:(kt_ + 1) * 128], qT.t[:, q0:q0 + nq],
                                                        start=True, stop=True)], [kT, qT], [ps_])
            pe_ = pts[i % 4]
            pm_ = ptm[i % 4]
            P.op("act", lambda h, ps_=ps_, pe_=pe_: h.activation(pe_.t[:, 0:nq], ps_.t[:, 0:nq], AF.Exp, scale=scale), [ps_], [pe_])
            mk, mb = mask_fn(i)
            eng = "dve"
            P.op(eng, lambda h, pe_=pe_, pm_=pm_, mk=mk: h.tensor_tensor(pm_.t[:, 0:nq], pe_.t[:, 0:nq], mk, ALU.mult), [pe_, mb], [pm_])
            pms[i] = pm_

        def issue_PV(i):
            kt_ = ktiles[i]
            pm_ = pms[i]
            vB, vBb = vB_fn(i)
            P.mm([lambda h, pm_=pm_, kt_=kt_, i=i: h.matmul(po.t[:, 0:nq], vt.t[:, kt_, :], pm_.t[:, 0:nq], start=(i == 0), stop=(i == nk - 1)),
                  lambda h, pm_=pm_, vB=vB, i=i: h.matmul(pl.t[:, 0:nq], vB, pm_.t[:, 0:nq], start=(i == 0), stop=(i == nk - 1))],
                 [vt, pm_, vBb], [po, pl])

        for i in range(min(LA, nk)):
            issue_S(i)
        for i in range(nk):
            if i + LA < nk:
                issue_S(i + LA)
            issue_PV(i)
        P.op("dve", lambda h: h.reciprocal(rl.t[:, 0:nq], pl.t[:, 0:nq]), [pl], [rl])
        P.op("dve", lambda h: h.tensor_tensor(of.t[:, 0:nq], po.t[:, 0:nq], rl.t[:, 0:nq], ALU.mult), [po, rl], [of])
        P.op("dve", lambda h: h.tensor_tensor(o_dst_fn(), of.t[:, 0:nq], za.t[:, zcol0:zcol0 + nq], ALU.mult), [of, za], [ocat])

    for hh in range(8):
        b2 = hh % 2
        P.dma("sp", kTh[b2].t[:], kT_scr.t.ap()[hh], [kT_scr], [kTh[b2]], kTh[b2])
        P.dma("sp", vh[b2].t[:], v_scr.t.ap()[:, hh * 128:(hh + 1) * 128].rearrange("(t p) d -> p t d", p=128), [v_scr], [vh[b2]], vh[b2])
        P.dma("sp", kTsh[b2].t[:], kTs_scr.t.ap()[hh], [kTs_scr], [kTsh[b2]], kTsh[b2])
        P.dma("sp", vsh[b2].t[:], vs_scr.t.ap()[:, hh * 128:(hh + 1) * 128].rearrange("(t p) d -> p t d", p=128), [vs_scr], [vsh[b2]], vsh[b2])
        P.dma("sp", qTh[b2].t[:], qT_scr.t.ap()[hh], [qT_scr], [qTh[b2]], qTh[b2])
        load_wt(0, 3072 + hh * 128)
        for (c0, n) in [(0, 512), (512, 512), (1024, 128)]:
            proj_feat(wt[0], None, lambda pk, c0=c0, n=n: silu_evac(pk, za, za.t[:, c0:c0 + n], n),
                      hnT_o, lambda kt, c0=c0, n=n: hnT_o.t[:, kt, c0:c0 + n], n)
        for qc in range(2):
            kts = list(range(4 * qc, 4 * qc + 20))
            attention(hh, qTh[b2], qc * 512, 512, kTh[b2], vh[b2], kts,
                      lambda i: (maskp.t[:, i, :], maskp),
                      lambda i, kts=kts: (validB.t[:, kts[i], :], validB),
                      lambda qc=qc, hh=hh: ocat.t[:, hh, qc * 512:(qc + 1) * 512], qc * 512)
        attention(hh, qTh[b2], 1024, 128, kTsh[b2], vsh[b2], list(range(17)),
                  lambda i: (masks_.t[:, i, :], masks_),
                  lambda i: (ones_bf.t[:], ones_bf),
                  lambda hh=hh: ocat.t[:, hh, 1024:1152], 1024)

    for cc in range(8):
        for j, base in enumerate([4096, 5120, 6144, 7168]):
            load_wt(j, base + cc * 128)
        bb, cb_, hb_, zb = fb
        for j, dstb in enumerate(fb):
            for (c0, n) in [(0, 512), (512, 512), (1024, 128)]:
                if j == 3:
                    ev = lambda pk, c0=c0, n=n, dstb=dstb: silu_evac(pk, dstb, dstb.t[:, 2 + c0:2 + c0 + n], n)
                else:
                    ev = lambda pk, c0=c0, n=n, dstb=dstb: P.op("act", lambda h: h.activation(dstb.t[:, 2 + c0:2 + c0 + n], pk.t[:, 0:n], AF.Identity), [pk], [dstb])
                proj_feat(wt[j], None, ev, hnT_o, lambda kt, c0=c0, n=n: hnT_o.t[:, kt, c0:c0 + n], n)
            if j in (1, 2):
                proj_feat(wt[j], None, lambda pk, dstb=dstb: P.op("act", lambda h: h.activation(dstb.t[:, 0:2], pk.t[:, 0:2], AF.Identity), [pk], [dstb]),
                          hp2, lambda kt: hp2.t[:, kt, :], 2)
        P.op("dve", lambda h: h.tensor_tensor(cb_.t[:], cb_.t[:], hb_.t[:], ALU.mult), [cb_, hb_], [cb_])
        w0 = cw.t[:, cc, 0:1]
        w1 = cw.t[:, cc, 1:2]
        w2 = cw.t[:, cc, 2:3]
        P.op("dve", lambda h, w2=w2: h.tensor_scalar(hb_.t[:, 2:1026], cb_.t[:, 2:1026], w2, None, ALU.mult), [cb_, cw], [hb_])
        P.op("dve", lambda h, w1=w1: h.scalar_tensor_tensor(hb_.t[:, 2:1026], cb_.t[:, 1:1025], w1, hb_.t[:, 2:1026], ALU.mult, ALU.add), [cb_, cw, hb_], [hb_])
        P.op("dve", lambda h, w0=w0: h.scalar_tensor_tensor(hb_.t[:, 2:1026], cb_.t[:, 0:1024], w0, hb_.t[:, 2:1026], ALU.mult, ALU.add), [cb_, cw, hb_], [hb_])
        P.op("dve", lambda h, cc=cc: h.tensor_copy(convo_p.t[:, cc, :], cb_.t[:, 1024:1026]), [cb_], [convo_p])
        P.op("dve", lambda h, cc=cc: h.tensor_copy(cb_.t[:, 1024:1026], sconv.t[:, cc, :]), [sconv], [cb_])
        P.op("dve", lambda h, w2=w2: h.tensor_scalar(hb_.t[:, 1026:1034], cb_.t[:, 1026:1034], w2, None, ALU.mult), [cb_, cw], [hb_])
        P.op("dve", lambda h, w1=w1: h.scalar_tensor_tensor(hb_.t[:, 1026:1034], cb_.t[:, 1025:1033], w1, hb_.t[:, 1026:1034], ALU.mult, ALU.add), [cb_, cw, hb_], [hb_])
        P.op("dve", lambda h, w0=w0: h.scalar_tensor_tensor(hb_.t[:, 1026:1034], cb_.t[:, 1024:1032], w0, hb_.t[:, 1026:1034], ALU.mult, ALU.add), [cb_, cw, hb_], [hb_])
        P.op("dve", lambda h, cc=cc: h.tensor_copy(convo_s.t[:, cc, :], cb_.t[:, 1032:1034]), [cb_], [convo_s])
        P.op("dve", lambda h: h.tensor_tensor(hb_.t[:, 2:1034], hb_.t[:, 2:1034], bb.t[:, 2:1034], ALU.mult), [hb_, bb], [hb_])
        P.op("dve", lambda h, cc=cc: h.tensor_tensor(ocat.t[:, 8 + cc, 0:1032], hb_.t[:, 2:1034], zb.t[:, 2:1034], ALU.mult), [hb_, zb], [ocat])
        P.op("dve", lambda h, cc=cc: h.memset(ocat.t[:, 8 + cc, 1032:1152], 0.0), [], [ocat])
    P.dma("sp", OUT["convp"].ap(), convo_p.t[:], [convo_p], [outbufs["convp"]], convo_p)
    P.dma("sp", OUT["convs"].ap(), convo_s.t[:], [convo_s], [outbufs["convs"]], convo_s)

    if STOP == 'B':
        P.barrier()
        return
    P.barrier()
    hn1T = P.sb("hn1T", [128, KT, NCH], BF16, at=hnT_o.at)
    P.off = B0
    wgb = [P.sb("wg%d" % i, [128, KT, 512], BF16) for i in range(2)]
    h1s = [P.sb("h1s%d" % i, [128, D], F32) for i in range(5)]
    xn1 = [P.sb("xn1%d" % i, [128, D], BF16) for i in range(2)]
    junk = P.sb("junk2", [128, D], F32)
    ss = P.sb("ss2", [128, 4], F32)
    tmpT = P.sb("tmpT", [128, KT, 128], BF16)
    load_g("g_ssm")
    wi = 0
    for tiles in [list(range(0, 5)), list(range(5, 9))]:
        for si, tj in enumerate(tiles):
            o0 = tj * 128
            src = IN["xh"].ap()[NHALO + o0:NHALO + o0 + 128, :] if tj < 8 else IN["xs"].ap()
            P.dma("sp", h1s[si].t[:], src, [], [h1s[si]], h1s[si])
        for g4 in range(4):
            wg = wgb[wi % 2]
            wi += 1
            P.dma("pool", wg.t[:], IN["w_out_ab"].ap()[:, g4 * 512:(g4 + 1) * 512].rearrange("(kt p) c -> p kt c", p=128), [], [wg], wg)
            for si, tj in enumerate(tiles):
                o0 = tj * 128
                ht_ = h1s[si]
                pk = next_pf()
                fns = [(lambda h, pk=pk, kt=kt, o0=o0, wg=wg: h.matmul(pk.t[:], ocat.t[:, kt, o0:o0 + 128], wg.t[:, kt, :],
                                                                         start=(kt == 0), stop=(kt == KT - 1))) for kt in range(KT)]
                P.mm(fns, [ocat, wg], [pk])
                P.op("dve", lambda h, pk=pk, ht_=ht_, g4=g4: h.tensor_tensor(ht_.t[:, g4 * 512:(g4 + 1) * 512], ht_.t[:, g4 * 512:(g4 + 1) * 512], pk.t[:], ALU.add),
                     [pk, ht_], [ht_])
        for si, tj in enumerate(tiles):
            o0 = tj * 128
            ht_ = h1s[si]
            P.dma("sp", h1_scr.t.ap()[o0:o0 + 128, :], ht_.t[:], [ht_], [h1_scr], ht_)
            if STOP == 'C1':
                if tj < 8:
                    P.dma("sp", OUT["yp"].ap()[o0:o0 + 128, :], ht_.t[:], [ht_], [outbufs["yp"]], ht_)
                else:
                    P.dma("sp", OUT["ys"].ap(), ht_.t[:], [ht_], [outbufs["ys"]], ht_)
            xn = xn1[tj % 2]
            rmsnorm_rows(ht_, xn, ss, junk)
            if tj < 8:
                transpose_rows(xn, hn1T, lambda half, o0=o0: hn1T.t[:, half * 8:(half + 1) * 8, o0:o0 + 128])
            else:
                transpose_rows(xn, tmpT, lambda half: tmpT.t[:, half * 8:(half + 1) * 8, :])
                P.op("dve", lambda h: h.tensor_copy(hn1T.t[:, :, 1024:1032], tmpT.t[:, :, 0:8]), [tmpT], [hn1T])
    for j in range(8):
        P.dma("sp", hn_src[j].t.ap().rearrange("(k p) t -> p k t", p=128), hn1T.t[:, 2 * j:2 * j + 2, :], [hn1T], [hn_src[j]], hn1T)
        P.coll(hn_src[j], hn_dst[j], GROUPS)

    if STOP == 'C1':
        P.barrier()
        return
    P.barrier()
    P.off = CONST_END
    uT = P.sb("uT", [128, 4, 4 * NCH], BF16)
    ygT = P.sb("ygT", [128, 4, 4 * NCH], BF16)
    L1 = P.off
    wu = P.sb("wu", [128, KT, 512], BF16)
    hch = [P.sb("hch%d" % i, [128, KT, 516], BF16) for i in range(2)]
    P.dma("pool", wu.t[:], IN["w_in_c"].ap()[:, 0:512].rearrange("(kt p) c -> p kt c", p=128), [], [wu], wu)
    ci = 0
    for r in range(4):
        for hf in range(2):
            hc = hch[ci % 2]
            ci += 1
            c0 = hf * 516
            for j in range(8):
                P.dma("sp", hc.t[:, 2 * j:2 * j + 2, :], hn_dst[j].t.ap()[r * 256:(r + 1) * 256, c0:c0 + 516].rearrange("(k p) t -> p k t", p=128),
                      [hn_dst[j]], [hc], hc)
            for ft in range(4):
                pk = next_pf()
                fns = [(lambda h, pk=pk, kt=kt, ft=ft, hc=hc: h.matmul(pk.t[:, 0:512], wu.t[:, kt, ft * 128:(ft + 1) * 128], hc.t[:, kt, 0:512],
                                                                      start=(kt == 0), stop=(kt == KT - 1))) for kt in range(KT)]
                P.mm(fns, [wu, hc], [pk])
                P.op("act", lambda h, pk=pk, ft=ft, r=r, c0=c0: h.activation(uT.t[:, ft, r * NCH + c0:r * NCH + c0 + 512], pk.t[:, 0:512], AF.Identity), [pk], [uT])
                pk2 = next_pf()
                fns2 = [(lambda h, pk2=pk2, kt=kt, ft=ft, hc=hc: h.matmul(pk2.t[:, 0:4], wu.t[:, kt, ft * 128:(ft + 1) * 128], hc.t[:, kt, 512:516],
                                                                         start=(kt == 0), stop=(kt == KT - 1))) for kt in range(KT)]
                P.mm(fns2, [wu, hc], [pk2])
                P.op("act", lambda h, pk2=pk2, ft=ft, r=r, c0=c0: h.activation(uT.t[:, ft, r * NCH + c0 + 512:r * NCH + c0 + 516], pk2.t[:, 0:4], AF.Identity), [pk2], [uT])

    if STOP == 'U':
        P.barrier()
        return
    P.barrier()
    P.off = L1
    def small(name, shape, dt=F32):
        return P.sb(name, shape, dt)
    lre_s = small("lre_s", [128, 16]); lim_s = small("lim_s", [128, 16]); lst_s = small("lst_s", [128, 16])
    lre_r = small("lre_r", [128, 256]); lim_r = small("lim_r", [128, 256]); lst_r = small("lst_r", [128, 256])
    bre_r = small("bre_r", [128, 256]); bim_r = small("bim_r", [128, 256])
    cre_s = small("cre_s", [128, 16, 16]); cim_s = small("cim_s", [128, 16, 16])
    rmask = small("rmask", [128, 8]); smask = small("smask", [128, 2])
    sre0 = small("sre0", [128, 4, 16]); sim0 = small("sim0", [128, 4, 16])
    iota = small("iota", [128, 1024])
    for b_, n in [(lre_s, "lre_s"), (lim_s, "lim_s"), (lst_s, "lst_s"), (cre_s, "cre_s"), (cim_s, "cim_s"),
                  (rmask, "rmask"), (smask, "smask"), (sre0, "sre0"), (sim0, "sim0"), (iota, "iota")]:
        P.dma("sp", b_.t[:], IN[n].ap(), [], [b_], b_)
    for b_, n in [(lre_r, "lre_r"), (lim_r, "lim_r"), (lst_r, "lst_r"), (bre_r, "bre_r"), (bim_r, "bim_r")]:
        P.dma("sp", b_.t[:], IN[n].ap().rearrange("p a b -> p (a b)"), [], [b_], b_)
    negpi = small("negpi", [128, 1])
    P.op("dve", lambda h: h.memset(negpi.t[:], -math.pi), [], [negpi])

    I32 = mybir.dt.int32
    tq = small("tq", [128, 1024]); tiq = small("tiq", [128, 1024], I32)
    halfpi = small("halfpi", [128, 1]); zero_t = small("zero_t", [128, 1])
    P.op("dve", lambda h: h.memset(halfpi.t[:], 0.5 * math.pi), [], [halfpi])
    P.op("dve", lambda h: h.memset(zero_t.t[:], 0.0), [], [zero_t])

    def sincos(ang_ap, n, s_ap, c_ap, rd, wr):
        for (dst, addc, bt, lo, hi) in [(s_ap, 0.0, zero_t, -math.pi, math.pi), (c_ap, 0.25, halfpi, -1.5 * math.pi, 0.5 * math.pi)]:
            P.op("dve", lambda h, addc=addc: h.tensor_scalar(tq.t[:, 0:n], ang_ap, 1.0 / TWO_PI, addc, ALU.mult, ALU.add), rd, [tq])
            P.op("dve", lambda h: h.tensor_copy(tiq.t[:, 0:n], tq.t[:, 0:n]), [tq], [tiq])
            P.op("dve", lambda h: h.tensor_copy(tq.t[:, 0:n], tiq.t[:, 0:n]), [tiq], [tq])
            P.op("dve", lambda h: h.scalar_tensor_tensor(tq.t[:, 0:n], tq.t[:, 0:n], -TWO_PI, ang_ap, ALU.mult, ALU.add), [tq] + rd, [tq])
            P.op("dve", lambda h, lo=lo, hi=hi: h.tensor_scalar(tq.t[:, 0:n], tq.t[:, 0:n], lo, hi, ALU.max, ALU.min), [tq], [tq])
            P.op("act", lambda h, dst=dst, bt=bt: h.activation(dst, tq.t[:, 0:n], AF.Sin, bias=bt.t[:, 0:1], scale=1.0), [tq, bt], wr)

    def disc(lre, lim, lst, n, pref):
        o = {}
        for nm in ["step", "mag", "th", "c", "s", "tmp", "nr", "den", "cr", "ci", "a", "b"]:
            o[nm] = small(pref + nm, [128, n])
        al = [o[k] for k in o] + [lre, lim, lst]
        P.op("act", lambda h: h.activation(o["step"].t[:], lst.t[:, 0:n], AF.Exp), al, al)
        P.op("dve", lambda h: h.tensor_tensor(o["th"].t[:], lim.t[:, 0:n], o["step"].t[:], ALU.mult), al, al)
        P.op("dve", lambda h: h.tensor_tensor(o["a"].t[:], lre.t[:, 0:n], o["step"].t[:], ALU.mult), al, al)
        P.op("act", lambda h: h.activation(o["mag"].t[:], o["a"].t[:], AF.Exp), al, al)
        sincos(o["th"].t[:], n, o["s"].t[:], o["c"].t[:], al, al)
        P.op("dve", lambda h: h.tensor_tensor(o["a"].t[:], o["mag"].t[:], o["c"].t[:], ALU.mult), al, al)
        P.op("dve", lambda h: h.tensor_scalar(o["nr"].t[:], o["a"].t[:], 1.0, -1.0, ALU.mult, ALU.add), al, al)
        P.op("dve", lambda h: h.tensor_tensor(o["b"].t[:], o["mag"].t[:], o["s"].t[:], ALU.mult), al, al)
        P.op("dve", lambda h: h.tensor_tensor(o["den"].t[:], lre.t[:, 0:n], lre.t[:, 0:n], ALU.mult), al, al)
        P.op("dve", lambda h: h.tensor_tensor(o["tmp"].t[:], lim.t[:, 0:n], lim.t[:, 0:n], ALU.mult), al, al)
        P.op("dve", lambda h: h.tensor_tensor(o["den"].t[:], o["den"].t[:], o["tmp"].t[:], ALU.add), al, al)
        P.op("dve", lambda h: h.reciprocal(o["den"].t[:], o["den"].t[:]), al, al)
        P.op("dve", lambda h: h.tensor_tensor(o["cr"].t[:], o["nr"].t[:], lre.t[:, 0:n], ALU.mult), al, al)
        P.op("dve", lambda h: h.tensor_tensor(o["tmp"].t[:], o["b"].t[:], lim.t[:, 0:n], ALU.mult), al, al)
        P.op("dve", lambda h: h.tensor_tensor(o["cr"].t[:], o["cr"].t[:], o["tmp"].t[:], ALU.add), al, al)
        P.op("dve", lambda h: h.tensor_tensor(o["cr"].t[:], o["cr"].t[:], o["den"].t[:], ALU.mult), al, al)
        P.op("dve", lambda h: h.tensor_tensor(o["ci"].t[:], o["b"].t[:], lre.t[:, 0:n], ALU.mult), al, al)
        P.op("dve", lambda h: h.tensor_tensor(o["tmp"].t[:], o["nr"].t[:], lim.t[:, 0:n], ALU.mult), al, al)
        P.op("dve", lambda h: h.tensor_tensor(o["ci"].t[:], o["ci"].t[:], o["tmp"].t[:], ALU.subtract), al, al)
        P.op("dve", lambda h: h.tensor_tensor(o["ci"].t[:], o["ci"].t[:], o["den"].t[:], ALU.mult), al, al)
        return o, al

    ds_, als = disc(lre_s, lim_s, lst_s, 16, "ds_")
    dr_, alr = disc(lre_r, lim_r, lst_r, 256, "dr_")
    bbr = small("bbr", [128, 256]); bbi = small("bbi", [128, 256]); tmpr = small("tmpr", [128, 256])
    alr2 = alr + [bbr, bbi, tmpr, bre_r, bim_r]
    P.op("dve", lambda h: h.tensor_tensor(bbr.t[:], dr_["cr"].t[:], bre_r.t[:], ALU.mult), alr2, alr2)
    P.op("dve", lambda h: h.tensor_tensor(tmpr.t[:], dr_["ci"].t[:], bim_r.t[:], ALU.mult), alr2, alr2)
    P.op("dve", lambda h: h.tensor_tensor(bbr.t[:], bbr.t[:], tmpr.t[:], ALU.subtract), alr2, alr2)
    P.op("dve", lambda h: h.tensor_tensor(bbi.t[:], dr_["cr"].t[:], bim_r.t[:], ALU.mult), alr2, alr2)
    P.op("dve", lambda h: h.tensor_tensor(tmpr.t[:], dr_["ci"].t[:], bre_r.t[:], ALU.mult), alr2, alr2)
    P.op("dve", lambda h: h.tensor_tensor(bbi.t[:], bbi.t[:], tmpr.t[:], ALU.add), alr2, alr2)
    BbT = [small("BbT%d" % ri, [128, 16, 128], BF16) for ri in range(2)]
    for ri, src in enumerate([bbr, bbi]):
        for qq in range(4):
            for g2 in range(2):
                m = rmask.t[:, qq * 2 + g2:qq * 2 + g2 + 1]
                o_ap = BbT[ri].t[:].rearrange("p (ft q) c -> p ft q c", q=4)[:, :, qq, g2 * 64:(g2 + 1) * 64]
                i_ap = src.t[:].rearrange("p (ft d) -> p ft d", ft=4)
                P.op("dve", lambda h, o_ap=o_ap, i_ap=i_ap, m=m: h.tensor_scalar(o_ap, i_ap, m, None, ALU.mult), alr2 + [rmask], [BbT[ri]])
    CT = [small("CT%d" % ri, [128, 16, 128], BF16) for ri in range(2)]
    for ri in range(2):
        P.op("dve", lambda h, ri=ri: h.memset(CT[ri].t[:], 0.0), [], [CT[ri]])
    for ri, (src, sgn) in enumerate([(cre_s, 1.0), (cim_s, -1.0)]):
        for pair in range(16):
            qq = pair % 4
            for g2 in range(2):
                m = smask.t[:, g2:g2 + 1]
                col = qq * 32 + g2 * 16
                P.op("dve", lambda h, ri=ri, pair=pair, col=col, m=m, src=src, sgn=sgn: h.tensor_scalar(
                    CT[ri].t[:, pair, col:col + 16], src.t[:, pair, :], m, sgn, ALU.mult, ALU.mult), [src, smask], [CT[ri]])
    rr = ds_["mag"]; th = ds_["th"]
    cth = small("cth", [128, 16]); sth = small("sth", [128, 16])
    P.op("dve", lambda h: h.tensor_copy(cth.t[:], ds_["c"].t[:]), als, [cth])
    P.op("dve", lambda h: h.tensor_copy(sth.t[:], ds_["s"].t[:]), als, [sth])

    tabc = [small("tabc%d" % i, [128, 1024]) for i in range(4)]
    tabs = [small("tabs%d" % i, [128, 1024]) for i in range(4)]
    WS = [dict(gr=small("gr0", [128, 1024]), gi=small("gi0", [128, 1024]), yr=small("yr0", [128, 1024]), yi=small("yi0", [128, 1024]),
               hrb=small("hrb0", [128, 1024], BF16), hib=small("hib0", [128, 1024], BF16))]
    hib1 = small("hib1", [128, 1024], BF16)
    ytmp = small("ytmp", [128, 512]); ysq = small("ysq", [128, 512])
    stp_l = [small("stp%d" % p_, [128, 2]) for p_ in range(16)]
    sts_l = [[small("sts%d_%d" % (b_, p_), [128, 2]) for p_ in range(16)] for b_ in range(4)]
    for w_ in WS:
        w_["gin"] = small("gin0", [128, 2]); w_["hend"] = small("hend0", [128, 2])
    for p_ in range(16):
        P.op("dve", lambda h, p_=p_: h.memset(stp_l[p_].t[:], 0.0), [], [stp_l[p_]])
    P.barrier()
    blkA = lre_r.at
    blkB = dr_["step"].at
    WS.append(dict(gr=P.sb("gr1", [128, 1024], F32, at=blkB), gi=P.sb("gi1", [128, 1024], F32, at=blkB + 4096),
                   yr=P.sb("yr1", [128, 1024], F32, at=blkB + 8192), yi=P.sb("yi1", [128, 1024], F32, at=blkA),
                   hrb=P.sb("hrb1", [128, 1024], BF16, at=blkB + 12288), hib=hib1,
                   gin=small("gin1", [128, 2]), hend=small("hend1", [128, 2])))
    ang = WS[0]["gr"]
    GC = 1.5957691216057308
    kcount = [0]

    def stageX(pair, qq, ft, col0, T, tc_, ts_, init_re, init_im, init_bufs, out_b):
        w = WS[kcount[0] % 2]
        kcount[0] += 1
        gr, gi, yr, yi, hrb, hib, gin, hend = w["gr"], w["gi"], w["yr"], w["yi"], w["hrb"], w["hib"], w["gin"], w["hend"]
        for h0 in range(0, T, 512):
            n = min(512, T - h0)
            pxr = next_pf(); pxi = next_pf()
            P.mm([lambda h, pxr=pxr, n=n, h0=h0: h.matmul(pxr.t[:, 0:n], BbT[0].t[:, pair, :], uT.t[:, ft, col0 + h0:col0 + h0 + n], start=True, stop=True)], [BbT[0], uT], [pxr])
            P.mm([lambda h, pxi=pxi, n=n, h0=h0: h.matmul(pxi.t[:, 0:n], BbT[1].t[:, pair, :], uT.t[:, ft, col0 + h0:col0 + h0 + n], start=True, stop=True)], [BbT[1], uT], [pxi])
            c_ = tc_.t[:, h0:h0 + n]; s_ = ts_.t[:, h0:h0 + n]
            P.op("dve", lambda h, pxr=pxr, c_=c_, n=n, h0=h0: h.tensor_tensor(yr.t[:, h0:h0 + n], pxr.t[:, 0:n], c_, ALU.mult), [pxr, tc_], [yr])
            P.op("dve", lambda h, pxi=pxi, s_=s_, n=n, h0=h0: h.tensor_tensor(gr.t[:, h0:h0 + n], pxi.t[:, 0:n], s_, ALU.mult), [pxi, ts_], [gr])
            P.op("dve", lambda h, n=n, h0=h0: h.tensor_tensor(yr.t[:, h0:h0 + n], yr.t[:, h0:h0 + n], gr.t[:, h0:h0 + n], ALU.add), [yr, gr], [yr])
            P.op("dve", lambda h, pxi=pxi, c_=c_, n=n, h0=h0: h.tensor_tensor(yi.t[:, h0:h0 + n], pxi.t[:, 0:n], c_, ALU.mult), [pxi, tc_], [yi])
            P.op("dve", lambda h, pxr=pxr, s_=s_, n=n, h0=h0: h.tensor_tensor(gi.t[:, h0:h0 + n], pxr.t[:, 0:n], s_, ALU.mult), [pxr, ts_], [gi])
            P.op("dve", lambda h, n=n, h0=h0: h.tensor_tensor(yi.t[:, h0:h0 + n], yi.t[:, h0:h0 + n], gi.t[:, h0:h0 + n], ALU.subtract), [yi, gi], [yi])
        ct = cth.t[:, pair:pair + 1]; st_ = sth.t[:, pair:pair + 1]
        P.op("dve", lambda h: h.tensor_scalar(gin.t[:, 0:1], init_re, ct, None, ALU.mult), init_bufs + [cth], [gin])
        P.op("dve", lambda h: h.scalar_tensor_tensor(gin.t[:, 0:1], init_im, st_, gin.t[:, 0:1], ALU.mult, ALU.subtract), init_bufs + [sth, gin], [gin])
        P.op("dve", lambda h: h.tensor_scalar(gin.t[:, 0:1], gin.t[:, 0:1], -1.0, None, ALU.mult), [gin], [gin])
        P.op("dve", lambda h: h.tensor_scalar(gin.t[:, 1:2], init_re, st_, None, ALU.mult), init_bufs + [sth], [gin])
        P.op("dve", lambda h: h.scalar_tensor_tensor(gin.t[:, 1:2], init_im, ct, gin.t[:, 1:2], ALU.mult, ALU.add), init_bufs + [cth, gin], [gin])
        rb = rr.t[:, pair:pair + 1].to_broadcast([128, T])
        P.op("dve", lambda h: h.tensor_tensor_scan(gr.t[:, 0:T], rb, yr.t[:, 0:T], gin.t[:, 0:1], ALU.mult, ALU.add), [yr, gin] + als, [gr])
        P.op("dve", lambda h: h.tensor_tensor_scan(gi.t[:, 0:T], rb, yi.t[:, 0:T], gin.t[:, 1:2], ALU.mult, ALU.add), [yi, gin] + als, [gi])
        c_ = tc_.t[:, 0:T]; s_ = ts_.t[:, 0:T]
        P.op("dve", lambda h: h.tensor_tensor(yr.t[:, 0:T], gr.t[:, 0:T], c_, ALU.mult), [gr, tc_], [yr])
        P.op("dve", lambda h: h.tensor_tensor(yi.t[:, 0:T], gi.t[:, 0:T], s_, ALU.mult), [gi, ts_], [yi])
        P.op("dve", lambda h: h.tensor_tensor(hrb.t[:, 0:T], yr.t[:, 0:T], yi.t[:, 0:T], ALU.subtract), [yr, yi], [hrb])
        P.op("dve", lambda h: h.tensor_tensor(hend.t[:, 0:1], yr.t[:, T - 1:T], yi.t[:, T - 1:T], ALU.subtract), [yr, yi], [hend])
        P.op("dve", lambda h: h.tensor_tensor(yr.t[:, 0:T], gr.t[:, 0:T], s_, ALU.mult), [gr, ts_, hrb, hend], [yr])
        P.op("dve", lambda h: h.tensor_tensor(yi.t[:, 0:T], gi.t[:, 0:T], c_, ALU.mult), [gi, tc_, hrb, hend], [yi])
        P.op("dve", lambda h: h.tensor_tensor(hib.t[:, 0:T], yr.t[:, 0:T], yi.t[:, 0:T], ALU.add), [yr, yi], [hib])
        P.op("dve", lambda h: h.tensor_tensor(hend.t[:, 1:2], yr.t[:, T - 1:T], yi.t[:, T - 1:T], ALU.add), [yr, yi], [hend])
        P.op("dve", lambda h: h.tensor_copy(out_b.t[:, 0:2], hend.t[:, 0:2]), [hend], [out_b])
        return dict(pair=pair, qq=qq, T=T, hrb=hrb, hib=hib, after=[])

    ypb = [pf[4], pf[5]]

    def stageY(c):
        pair, qq, T, hrb, hib = c["pair"], c["qq"], c["T"], c["hrb"], c["hib"]
        for h0 in range(0, T, 512):
            n = min(512, T - h0)
            yp_ = ypb[h0 // 512]
            P.mm([lambda h, yp_=yp_, n=n, h0=h0: h.matmul(yp_.t[:, 0:n], CT[0].t[:, pair, :], hrb.t[:, h0:h0 + n], start=(qq == 0), stop=False),
                  lambda h, yp_=yp_, n=n, h0=h0: h.matmul(yp_.t[:, 0:n], CT[1].t[:, pair, :], hib.t[:, h0:h0 + n], start=False, stop=(qq == 3))],
                 [CT[0], CT[1], hrb, hib], [yp_])
        for f in c["after"]:
            f()

    def y_evac(yp_, ft, col0, n):
        P.op("dve", lambda h: h.scalar_tensor_tensor(ytmp.t[:, 0:n], uT.t[:, ft, col0:col0 + n], dsk.t[:, ft:ft + 1], yp_.t[:, 0:n], ALU.mult, ALU.add),
             [uT, dsk, yp_], [ytmp])
        P.op("dve", lambda h: h.tensor_tensor(ysq.t[:, 0:n], ytmp.t[:, 0:n], ytmp.t[:, 0:n], ALU.mult), [ytmp], [ysq])
        P.op("dve", lambda h: h.tensor_scalar(ysq.t[:, 0:n], ysq.t[:, 0:n], 0.044715, 1.0, ALU.mult, ALU.add), [ysq], [ysq])
        P.op("dve", lambda h: h.tensor_tensor(ysq.t[:, 0:n], ysq.t[:, 0:n], ytmp.t[:, 0:n], ALU.mult), [ysq, ytmp], [ysq])
        P.op("act", lambda h: h.activation(ysq.t[:, 0:n], ysq.t[:, 0:n], AF.Sigmoid, scale=GC), [ysq], [ysq])
        P.op("dve", lambda h: h.tensor_tensor(ygT.t[:, ft, col0:col0 + n], ysq.t[:, 0:n], ytmp.t[:, 0:n], ALU.mult), [ysq, ytmp], [ygT])

    pending = [None]

    def push(ctx):
        if pending[0] is not None:
            stageY(pending[0])
        pending[0] = ctx

    for ft in range(4):
        for qq in range(4):
            pair = ft * 4 + qq
            P.op("dve", lambda h, pair=pair: h.tensor_scalar(ang.t[:], iota.t[:], th.t[:, pair:pair + 1], None, ALU.mult), [iota] + als, [ang])
            sincos(ang.t[:], 1024, tabs[qq].t[:], tabc[qq].t[:], [ang], [tabs[qq], tabc[qq]])
        for seg in range(4):
            for qq in range(4):
                pair = ft * 4 + qq
                ctx = stageX(pair, qq, ft, seg * NCH, 1024, tabc[qq], tabs[qq],
                             stp_l[pair].t[:, 0:1], stp_l[pair].t[:, 1:2], [stp_l[pair]], stp_l[pair])
                if qq == 3:
                    ctx["after"] = [(lambda ft=ft, seg=seg, hf=hf: y_evac(ypb[hf], ft, seg * NCH + hf * 512, 512)) for hf in range(2)]
                push(ctx)
        for sb_i in range(4):
            for qq in range(4):
                pair = ft * 4 + qq
                ctx = stageX(pair, qq, ft, sb_i * NCH + 1024, 8, tabc[qq], tabs[qq],
                             sre0.t[:, sb_i, pair:pair + 1], sim0.t[:, sb_i, pair:pair + 1], [sre0, sim0], sts_l[sb_i][pair])
                if qq == 3:
                    ctx["after"] = [(lambda ft=ft, sb_i=sb_i: y_evac(ypb[0], ft, sb_i * NCH + 1024, 8))]
                push(ctx)
    push(None)
    stp2 = P.sb("stp2", [128, 2, 16], F32, at=tq.at); sts2 = P.sb("sts2", [128, 4, 2, 16], F32, at=tq.at + 128)
    for p_ in range(16):
        P.op("dve", lambda h, p_=p_: h.tensor_copy(stp2.t[:, :, p_], stp_l[p_].t[:, 0:2]), [stp_l[p_]], [stp2])
        for b_ in range(4):
            P.op("dve", lambda h, p_=p_, b_=b_: h.tensor_copy(sts2.t[:, b_, :, p_], sts_l[b_][p_].t[:, 0:2]), [sts_l[b_][p_]], [sts2])
    if STOP == 'SSM':
        for r_ in range(4):
            P.dma("pool", OUT["yp"].ap().rearrange("(p f x) c -> p f (x c)", p=128, f=4)[:, :, r_ * 1024:(r_ + 1) * 1024],
                  ygT.t[:, :, r_ * NCH:r_ * NCH + 1024], [ygT], [outbufs["yp"]], ygT)
        P.dma("pool", OUT["ys"].ap()[:, 0:128].rearrange("p (f r s) -> p f r s", f=4, r=4),
              ygT.t[:].rearrange("p f (r c) -> p f r c", r=4)[:, :, :, 1024:1032], [ygT], [outbufs["ys"]], ygT)
    P.dma("sp", OUT["ssm_p"].ap(), stp2.t[:], [stp2], [outbufs["ssm_p"]], stp2)
    P.dma("sp", OUT["ssm_s"].ap(), sts2.t[:], [sts2], [outbufs["ssm_s"]], sts2)
    for j in range(8):
        P.dma("sp", y_src[j].t.ap(), ygT.t[(j % 2) * 64:(j % 2) * 64 + 64, j // 2, :], [ygT], [y_src[j]], ygT)
        P.coll(y_src[j], y_dst[j], GROUPS)

    if STOP == 'SSM':
        P.barrier()
        return
    P.barrier()
    P.off = CONST_END
    y2T = P.sb("y2T", [128, KT, NO], BF16)
    F0 = P.off
    ygo = P.sb("ygo", [128, KT, NCH], BF16)
    hn1o = P.sb("hn1o", [128, KT, NCH], BF16)
    ych = [P.sb("ych%d" % i, [128, 4, NCH], BF16) for i in range(2)]
    wt2 = [P.sb("wt2_%d" % i, [128, KT, 128], BF16) for i in range(4)]
    gl = P.sb("gl", [128, NCH], F32); zz = P.sb("zz", [128, NCH], F32)
    for j in range(8):
        P.dma("sp", hn1o.t[:, 2 * j:2 * j + 2, :], hn_src[j].t.ap().rearrange("(k p) t -> p k t", p=128), [hn_src[j]], [hn1o], hn1o)
    P.op("pool", lambda h: h.memset(y2T.t[:, :, NCH:NO], 0.0), [], [y2T])
    ci = 0
    for rf in range(4):
        for r in range(4):
            yc = ych[ci % 2]; ci += 1
            for j in range(8):
                P.dma("sp", yc.t[(j % 2) * 64:(j % 2) * 64 + 64, j // 2, :], y_dst[j].t.ap()[rf * 64:(rf + 1) * 64, r * NCH:(r + 1) * NCH],
                      [y_dst[j]], [yc], yc)
            dst = ygo.t[:, rf * 4:(rf + 1) * 4, :]
            if r == 0:
                P.op("dve", lambda h, yc=yc, dst=dst: h.tensor_scalar(dst, yc.t[:], sel.t[:, 0:1], None, ALU.mult), [yc, sel], [ygo])
            else:
                P.op("dve", lambda h, yc=yc, dst=dst, r=r: h.scalar_tensor_tensor(dst, yc.t[:], sel.t[:, r:r + 1], dst, ALU.mult, ALU.add), [yc, sel, ygo], [ygo])
    if STOP == 'G1':
        P.barrier()
        return
    for nt in range(16):
        wa = wt2[2 * (nt % 2)]
        wb_ = wt2[2 * (nt % 2) + 1]
        P.dma("pool", wa.t[:], IN["w_glu"].ap()[:, nt * 128:(nt + 1) * 128].rearrange("(kt p) c -> p kt c", p=128), [], [wa], wa)
        P.dma("pool", wb_.t[:], IN["w_in_c"].ap()[:, 2048 + nt * 128:2048 + (nt + 1) * 128].rearrange("(kt p) c -> p kt c", p=128), [], [wb_], wb_)
        for (c0, n) in [(0, 512), (512, 512), (1024, 8)]:
            proj_feat(wa, None, lambda pk, c0=c0, n=n, nt=nt: P.op("act", lambda h: h.activation(gl.t[:, c0:c0 + n], pk.t[:, 0:n], AF.Sigmoid, bias=bglu.t[:, nt:nt + 1], scale=1.0), [pk, bglu], [gl]),
                      ygo, lambda kt, c0=c0, n=n: ygo.t[:, kt, c0:c0 + n], n)
            def zev(pk, c0=c0, n=n):
                P.op("act", lambda h: h.activation(zz.t[:, c0:c0 + n], pk.t[:, 0:n], AF.Sigmoid), [pk], [zz])
                P.op("dve", lambda h: h.tensor_tensor(zz.t[:, c0:c0 + n], zz.t[:, c0:c0 + n], pk.t[:, 0:n], ALU.mult), [zz, pk], [zz])
            proj_feat(wb_, None, zev, hn1o, lambda kt, c0=c0, n=n: hn1o.t[:, kt, c0:c0 + n], n)
        P.op("dve", lambda h, nt=nt: h.tensor_tensor(gl.t[:], gl.t[:], ygo.t[:, nt, :], ALU.mult), [gl, ygo], [gl])
        P.op("dve", lambda h, nt=nt: h.tensor_tensor(y2T.t[:, nt, 0:NCH], gl.t[:], zz.t[:], ALU.mult), [gl, zz], [y2T])
    if STOP == 'G2':
        P.barrier()
        return
    P.barrier()
    P.off = F0
    wgb2 = [P.sb("wgc%d" % i, [128, KT, 512], BF16) for i in range(2)]
    h1s = [P.sb("h1f%d" % i, [128, D], F32) for i in range(5)]
    junk = P.sb("junk3", [128, D], F32); ss = P.sb("ss3", [128, 4], F32)
    yo = [P.sb("yo%d" % i, [128, D], F32) for i in range(2)]
    load_g("g_fin")
    wi = 0
    for tiles in [list(range(0, 5)), list(range(5, 9))]:
        for si, tj in enumerate(tiles):
            o0 = tj * 128
            P.dma("sp", h1s[si].t[:], h1_scr.t.ap()[o0:o0 + 128, :], [h1_scr], [h1s[si]], h1s[si])
        for g4 in range(4):
            wg = wgb2[wi % 2]
            wi += 1
            P.dma("pool", wg.t[:], IN["w_out_c"].ap()[:, g4 * 512:(g4 + 1) * 512].rearrange("(kt p) c -> p kt c", p=128), [], [wg], wg)
            for si, tj in enumerate(tiles):
                o0 = tj * 128
                ht_ = h1s[si]
                pk = next_pf()
                fns = [(lambda h, pk=pk, kt=kt, o0=o0, wg=wg: h.matmul(pk.t[:], y2T.t[:, kt, o0:o0 + 128], wg.t[:, kt, :],
                                                                         start=(kt == 0), stop=(kt == KT - 1))) for kt in range(KT)]
                P.mm(fns, [y2T, wg], [pk])
                P.op("dve", lambda h, pk=pk, ht_=ht_, g4=g4: h.tensor_tensor(ht_.t[:, g4 * 512:(g4 + 1) * 512], ht_.t[:, g4 * 512:(g4 + 1) * 512], pk.t[:], ALU.add),
                     [pk, ht_], [ht_])
        for si, tj in enumerate(tiles):
            o0 = tj * 128
            ht_ = h1s[si]
            yo_ = yo[tj % 2]
            rmsnorm_rows(ht_, yo_, ss, junk)
            if tj < 8:
                P.dma("sp", OUT["yp"].ap()[o0:o0 + 128, :], yo_.t[:], [yo_], [outbufs["yp"]], yo_)
            else:
                P.dma("sp", OUT["ys"].ap(), yo_.t[:], [yo_], [outbufs["ys"]], yo_)
    P.barrier()


_NC_CACHE = {}


def _rope_tables(pos):
    half = 64
    inv = (np.float32(10000.0) ** (-np.arange(half, dtype=np.float32) / np.float32(half))).astype(np.float32)
    ang = pos.astype(np.float32)[:, None] * inv[None, :]
    return np.cos(ang).astype(np.float32), np.sin(ang).astype(np.float32)


def kernel(x_prompt, x_sample, cache_win_k, cache_win_v, state_conv, state_ssm_re, state_ssm_im,
           attn_norm, w_in_ab, conv_w, w_out_ab, ssm_norm, w_in_c, lam_re, lam_im, log_step,
           b_re, b_im, c_re, c_im, d_skip, w_glu, b_glu, w_out_c, final_norm):
    f = lambda a: np.ascontiguousarray(np.asarray(a, dtype=np.float32))
    x_prompt, x_sample = f(x_prompt), f(x_sample)
    cache_win_k, cache_win_v, state_conv = f(cache_win_k), f(cache_win_v), f(state_conv)
    state_ssm_re, state_ssm_im = f(state_ssm_re), f(state_ssm_im)
    w_in_ab0, w_out_ab0, w_in_c0, w_glu0, w_out_c0 = f(w_in_ab)[0], f(w_out_ab)[0], f(w_in_c)[0], f(w_glu)[0], f(w_out_c)[0]
    lam_re, lam_im, log_step = f(lam_re)[0], f(lam_im)[0], f(log_step)[0]
    b_re, b_im, c_re, c_im = f(b_re)[0], f(b_im)[0], f(c_re)[0], f(c_im)[0]
    d_skip0, b_glu0 = f(d_skip)[0], f(b_glu)[0]
    if "nc" not in _NC_CACHE:
        _NC_CACHE["nc"] = build_nc()
    nc = _NC_CACHE["nc"]

    kk = np.arange(128)[:, None]
    qq_ = np.arange(512)[None, :]
    maskp = np.stack([mult_of(qq_ - ((i - 16) * 128 + kk)) for i in range(20)], 1)
    rows = np.arange(2176).reshape(17, 128)
    s_ = np.arange(128)[None, :]
    masks = np.zeros((128, 17, 128), np.float32)
    for i in range(17):
        row = rows[i][:, None]
        m = mult_of(2048 + s_ - row)
        m[:, 8:] = ((2048 + s_[:, 8:] - row) == 0)
        masks[:, i, :] = m
    iota = np.broadcast_to(np.arange(1024, dtype=np.float32)[None, :], (128, 1024)).copy()
    rmask = np.zeros((128, 8), np.float32)
    for p in range(128):
        rmask[p, (p // 32) * 2 + (p % 32) // 16] = 1.0
    smask = np.zeros((128, 2), np.float32)
    smask[:64, 0] = 1.0
    smask[64:, 1] = 1.0
    bc = lambda v: np.ascontiguousarray(np.broadcast_to(v[None, :], (128, v.shape[0])))

    in_maps = []
    for c in range(8):
        b, r = c // 4, c % 4
        T0 = r * NOWN
        xh = np.zeros((NTP, D), np.float32)
        lo = T0 - NHALO
        src_lo = max(lo, 0)
        xh[src_lo - lo:] = x_prompt[b, src_lo:T0 + NOWN]
        pos = np.concatenate([np.arange(lo, T0 + NOWN), PAST + np.arange(128)]).astype(np.float32)
        valid = (pos[:NTP] >= 0).astype(np.float32)
        cosv, sinv = _rope_tables(np.maximum(pos, 0))
        xs = np.zeros((128, D), np.float32)
        xs[:8] = x_sample[c]
        g0 = 32 * r
        gs = slice(g0, g0 + 32)
        st_lay = lambda a: np.ascontiguousarray(a.reshape(16, 2, 64).transpose(1, 2, 0).reshape(128, 16))
        def row_lay_rep(a):
            t = a.reshape(4, 4, 2, 64)
            t = np.broadcast_to(t[:, :, :, None, :], (4, 4, 2, 16, 64))
            return np.ascontiguousarray(t.transpose(1, 2, 3, 0, 4).reshape(128, 4, 64))
        def row_lay_b(a):
            t = a.reshape(4, 4, 2, 64, 16)
            return np.ascontiguousarray(t.transpose(1, 2, 4, 0, 3).reshape(128, 4, 64))
        def st_lay_c(a):
            t = a.reshape(16, 2, 16, 64)
            return np.ascontiguousarray(t.transpose(1, 3, 0, 2).reshape(128, 16, 16))
        lst32 = np.broadcast_to(log_step[gs][:, None], (32, 64))
        sel = np.zeros((128, 4), np.float32)
        sel[:, r] = 1.0
        sre0 = np.stack([st_lay(state_ssm_re[0, 4 * b + i, gs]) for i in range(4)], 1)
        sim0 = np.stack([st_lay(state_ssm_im[0, 4 * b + i, gs]) for i in range(4)], 1)
        w_in_c_rolled = np.concatenate([w_in_c0[:, 512 * r:512 * (r + 1)], w_in_c0[:, 512:2048], w_in_c0[:, 2048:]], 1)
        m = {
            "xh": xh, "xs": xs,
            "cs": np.ascontiguousarray(cosv.reshape(25, 128, 64).transpose(1, 0, 2)),
            "sn": np.ascontiguousarray(sinv.reshape(25, 128, 64).transpose(1, 0, 2)),
            "valid": np.ascontiguousarray(valid.reshape(24, 128).T),
            "ck": np.ascontiguousarray(cache_win_k[0, c].reshape(2048, 1024)),
            "cv": np.ascontiguousarray(cache_win_v[0, c].reshape(2048, 1024)),
            "sconv": np.ascontiguousarray(state_conv[0, c].reshape(2, 8, 128).transpose(2, 1, 0)),
            "g_attn": bc(f(attn_norm)[0]), "g_ssm": bc(f(ssm_norm)[0]), "g_fin": bc(f(final_norm)),
            "w_in_ab": w_in_ab0, "cw": np.ascontiguousarray(f(conv_w)[0].reshape(3, 8, 128).transpose(2, 1, 0)),
            "w_out_ab": w_out_ab0, "w_in_c": np.ascontiguousarray(w_in_c_rolled),
            "w_glu": w_glu0, "w_out_c": w_out_c0,
            "bglu": np.ascontiguousarray(b_glu0.reshape(16, 128).T),
            "dsk": np.ascontiguousarray(d_skip0[512 * r:512 * (r + 1)].reshape(4, 128).T),
            "maskp": maskp, "masks": masks,
            "lre_s": st_lay(lam_re[gs]), "lim_s": st_lay(lam_im[gs]), "lst_s": st_lay(lst32),
            "lre_r": row_lay_rep(lam_re[gs]), "lim_r": row_lay_rep(lam_im[gs]), "lst_r": row_lay_rep(np.ascontiguousarray(lst32)),
            "bre_r": row_lay_b(b_re[gs]), "bim_r": row_lay_b(b_im[gs]),
            "cre_s": st_lay_c(c_re[gs]), "cim_s": st_lay_c(c_im[gs]),
            "rmask": rmask, "smask": smask, "sel": sel, "sre0": sre0, "sim0": sim0, "iota": iota,
        }
        in_maps.append({k: np.ascontiguousarray(v, dtype=np.float32) for k, v in m.items()})

    res = run_bass_kernel_spmd(nc, in_maps, core_ids=list(range(8)))
    R = res.results
    _NC_CACHE['raw'] = R
    y_prompt = np.zeros((2, SEQ, D), np.float32)
    y_sample = np.zeros((8, 8, D), np.float32)
    kp = np.zeros((1, 2, 2048, 8, 128), np.float32)
    vp = np.zeros((1, 2, 2048, 8, 128), np.float32)
    convp = np.zeros((1, 2, 2, 1024), np.float32)
    srp = np.zeros((1, 2, 128, 64), np.float32)
    sip = np.zeros((1, 2, 128, 64), np.float32)
    ks = np.zeros((1, 8, 8, 8, 128), np.float32)
    vs = np.zeros((1, 8, 8, 8, 128), np.float32)
    convs = np.zeros((1, 8, 2, 1024), np.float32)
    srs = np.zeros((1, 8, 128, 64), np.float32)
    sis = np.zeros((1, 8, 128, 64), np.float32)
    unst = lambda a: a.reshape(2, 64, 16).transpose(2, 0, 1).reshape(32, 64)
    for c in range(8):
        b, r = c // 4, c % 4
        o = R[c]
        y_prompt[b, r * NOWN:(r + 1) * NOWN] = o["yp"]
        y_sample[c] = o["ys"][:8]
        if r >= 2:
            kp[0, b, (r - 2) * NOWN:(r - 1) * NOWN] = o["kp"].reshape(NOWN, 8, 128)
            vp[0, b, (r - 2) * NOWN:(r - 1) * NOWN] = o["vp"].reshape(NOWN, 8, 128)
        if r == 3:
            convp[0, b] = o["convp"].transpose(2, 1, 0).reshape(2, 1024)
        ks[0, c] = o["ks"][:8].reshape(8, 8, 128)
        vs[0, c] = o["vs"][:8].reshape(8, 8, 128)
        convs[0, c] = o["convs"].transpose(2, 1, 0).reshape(2, 1024)
        srp[0, b, 32 * r:32 * (r + 1)] = unst(o["ssm_p"][:, 0, :])
        sip[0, b, 32 * r:32 * (r + 1)] = unst(o["ssm_p"][:, 1, :])
        for i in range(4):
            srs[0, 4 * b + i, 32 * r:32 * (r + 1)] = unst(o["ssm_s"][:, i, 0, :])
            sis[0, 4 * b + i, 32 * r:32 * (r + 1)] = unst(o["ssm_s"][:, i, 1, :])
    return (y_prompt, y_sample, kp, vp, convp, srp, sip, ks, vs, convs, srs, sis)
```

```python
import math
import os
STOP = os.environ.get('MK_STOP', '')
from contextlib import ExitStack

import numpy as np
import concourse.bass as bass
import concourse.mybir as mybir
from concourse.bass_utils import run_bass_kernel_spmd

F32 = mybir.dt.float32
BF16 = mybir.dt.bfloat16
ALU = mybir.AluOpType
AF = mybir.ActivationFunctionType
AX = mybir.AxisListType

ENGS = ["pe", "act", "dve", "pool", "sp"]
D = 2048
KT = 16
NOWN = 1024
NHALO = 2048
NTP = NOWN + NHALO
NTILE_P = NTP // 128
NO = NOWN + 128
SEQ = 4096
PAST = 16384
NCH = 1032
TWO_PI = 2.0 * math.pi


class Buf:
    def __init__(self, t, name):
        self.t = t
        self.name = name
        self.w = {}
        self.r = {}
        self.dsem = None
        self.dcnt = 0


class Prog:
    def __init__(self, nc, stack):
        self.nc = nc
        self.stack = stack
        self.q = {e: [] for e in ENGS}
        self.cnt = {e: 0 for e in ENGS}
        self.seen = {e: {} for e in ENGS}
        self.sems = {}
        self.semval = {}
        for e in ["pe", "act", "dve", "pool"]:
            self.sems[e] = stack.enter_context(nc.semaphore("s_" + e))
        self.off = 16512
        self.free = []
        self.dval = {}
        self.phase_bufs = []

    def sb(self, name, shape, dt, at=None):
        nbytes = int(np.prod(shape[1:])) * (2 if dt == BF16 else 4)
        if at is None:
            at = self.off
            self.off = (at + nbytes + 63) // 64 * 64
        assert at + nbytes <= 229300, (name, at, nbytes)
        t = self.nc.alloc_sbuf_tensor_at(name, list(shape), dt, offset=at)
        b = Buf(t, name)
        b.at = at
        b.nbytes = nbytes
        return b

    def ps(self, name, shape, dt=F32):
        t = self.stack.enter_context(self.nc.psum_tensor(name, list(shape), dt))
        return Buf(t, name)

    def dram(self, name, shape, dt, kind="Internal"):
        t = self.nc.dram_tensor(name, list(shape), dt, kind=kind)
        return Buf(t, name)

    def _need(self, eng, k, v, waits):
        if self.seen[eng].get(k, 0) >= v:
            return
        waits[k] = max(waits.get(k, 0), v)

    def _deps(self, eng, reads, writes):
        waits = {}
        for b in reads:
            for k, v in b.w.items():
                self._need(eng, k, v, waits)
        for b in writes:
            for k, v in b.w.items():
                self._need(eng, k, v, waits)
            for k, v in b.r.items():
                self._need(eng, k, v, waits)
        for k, v in waits.items():
            self.seen[eng][k] = v
        return [(self.sems[k], v) for k, v in waits.items()]

    def _commit(self, k, v, reads, writes):
        self.semval[k] = v
        for b in reads:
            b.r[k] = max(b.r.get(k, 0), v)
        for b in writes:
            b.w[k] = max(b.w.get(k, 0), v)
            b.r = {}

    def op(self, eng, fn, reads=(), writes=()):
        reads = [b for b in reads if b is not None]
        writes = [b for b in writes if b is not None]
        wl = self._deps(eng, reads, writes)
        self.cnt[eng] += 1
        sem = self.sems[eng]

        def emit(h, fn=fn, wl=wl, sem=sem):
            for s, v in wl:
                h.wait_ge(s, v)
            fn(h).then_inc(sem, 1)

        self.q[eng].append(emit)
        self._commit(eng, self.cnt[eng], reads, writes)

    def mm(self, fns, reads, writes):
        eng = "pe"
        wl = self._deps(eng, reads, writes)
        self.cnt[eng] += 1
        sem = self.sems[eng]

        def emit(h, fns=fns, wl=wl, sem=sem):
            for s, v in wl:
                h.wait_ge(s, v)
            for f in fns[:-1]:
                f(h)
            fns[-1](h).then_inc(sem, 1)

        self.q[eng].append(emit)
        self._commit(eng, self.cnt[eng], reads, writes)

    def dma(self, eng, out, in_, reads, writes, semb, **kw):
        reads = [b for b in reads if b is not None]
        writes = [b for b in writes if b is not None]
        if semb.dsem is None:
            if self.free:
                key = self.free.pop()
            else:
                key = "d%d" % len(self.sems)
                self.sems[key] = self.stack.enter_context(self.nc.semaphore(key))
            semb.dsem = key
            semb.dcnt = self.dval.get(key, 0)
            self.phase_bufs.append(semb)
        wl = self._deps(eng, reads, writes)
        semb.dcnt += 16
        self.dval[semb.dsem] = semb.dcnt
        sem = self.sems[semb.dsem]

        def emit(h, wl=wl, sem=sem, out=out, in_=in_, kw=kw):
            for s, v in wl:
                h.wait_ge(s, v)
            h.dma_start(out=out, in_=in_, **kw).then_inc(sem, 16)

        self.q[eng].append(emit)
        self._commit(semb.dsem, semb.dcnt, reads, writes)

    def coll(self, src, dst, groups):
        key = "c%d" % len(self.sems)
        self.sems[key] = self.stack.enter_context(self.nc.semaphore(key))
        wl = self._deps("pool", [src], [dst])
        sem = self.sems[key]

        def emit(h, wl=wl, sem=sem):
            for s, v in wl:
                h.wait_ge(s, v)
            h.collective_compute("AllGather", ALU.bypass, replica_groups=groups,
                                 ins=[src.t.ap()], outs=[dst.t.ap()]).then_inc(sem)
            h.wait_ge(sem, 1)

        self.q["pool"].append(emit)
        self.cnt["pool"] += 1
        s2 = self.sems["pool"]
        self.q["pool"].append(lambda h, s2=s2: h.engine_nop().then_inc(s2, 1))
        self._commit("pool", self.cnt["pool"], [src], [dst])

    def barrier(self):
        for b in self.phase_bufs:
            self.free.append(b.dsem)
            b.dsem = None
        self.phase_bufs = []
        items = list(self.semval.items())
        for e in ENGS:
            wl = []
            for k, v in items:
                if self.seen[e].get(k, 0) < v:
                    self.seen[e][k] = v
                    wl.append((self.sems[k], v))

            def emit(h, wl=wl):
                for s, v in wl:
                    h.wait_ge(s, v)

            if wl:
                self.q[e].append(emit)

    def run(self):
        nc = self.nc
        with nc.Block() as block:
            @block.tensor
            def _(h):
                for f in self.q["pe"]:
                    f(h)

            @block.scalar
            def _(h):
                for f in self.q["act"]:
                    f(h)

            @block.vector
            def _(h):
                for f in self.q["dve"]:
                    f(h)

            @block.gpsimd
            def _(h):
                for f in self.q["pool"]:
                    f(h)

            @block.sync
            def _(h):
                for f in self.q["sp"]:
                    f(h)


def mult_of(d):
    d = np.asarray(d)
    m = ((d >= 0) & (d <= 128)).astype(np.float32)
    m += ((d >= 0) & (d <= 512) & (d % 4 == 0))
    m += ((d >= 0) & (d <= 2048) & (d % 16 == 0))
    return m.astype(np.float32)


IN_SPECS = [
    ("xh", [NTP, D]), ("xs", [128, D]), ("cs", [128, 25, 64]), ("sn", [128, 25, 64]),
    ("valid", [128, 24]), ("ck", [2048, 1024]), ("cv", [2048, 1024]), ("sconv", [128, 8, 2]),
    ("g_attn", [128, D]), ("g_ssm", [128, D]), ("g_fin", [128, D]),
    ("w_in_ab", [D, 8192]), ("cw", [128, 8, 3]), ("w_out_ab", [D, D]), ("w_in_c", [D, 4096]),
    ("w_glu", [D, D]), ("w_out_c", [D, D]), ("bglu", [128, 16]), ("dsk", [128, 4]),
    ("maskp", [128, 20, 512]), ("masks", [128, 17, 128]),
    ("lre_s", [128, 16]), ("lim_s", [128, 16]), ("lst_s", [128, 16]),
    ("lre_r", [128, 4, 64]), ("lim_r", [128, 4, 64]), ("lst_r", [128, 4, 64]),
    ("bre_r", [128, 4, 64]), ("bim_r", [128, 4, 64]),
    ("cre_s", [128, 16, 16]), ("cim_s", [128, 16, 16]),
    ("rmask", [128, 8]), ("smask", [128, 2]), ("sel", [128, 4]),
    ("sre0", [128, 4, 16]), ("sim0", [128, 4, 16]), ("iota", [128, 1024]),
]
OUT_SPECS = [
    ("yp", [NOWN, D]), ("ys", [128, D]), ("kp", [NOWN, 1024]), ("vp", [NOWN, 1024]),
    ("convp", [128, 8, 2]), ("ssm_p", [128, 2, 16]), ("ks", [128, 1024]), ("vs", [128, 1024]),
    ("convs", [128, 8, 2]), ("ssm_s", [128, 4, 2, 16]),
]


def build_nc():
    nc = bass.Bass("TRN2", target_bir_lowering=False)
    IN = {}
    for n, s in IN_SPECS:
        IN[n] = nc.dram_tensor(n, s, F32, kind="ExternalInput")
    OUT = {}
    for n, s in OUT_SPECS:
        OUT[n] = nc.dram_tensor(n, s, F32, kind="ExternalOutput")
    st = ExitStack()
    with st:
        P = Prog(nc, st)
        build_program(nc, P, IN, OUT)
        P.run()
    return nc


def build_program(nc, P, IN, OUT):
    GROUPS = [[0, 1, 2, 3], [4, 5, 6, 7]]
    outbufs = {n: Buf(OUT[n], n) for n in OUT}
    kT_scr = P.dram("kT_scr", [8, 128, NTP], BF16)
    v_scr = P.dram("v_scr", [NTP, 1024], BF16)
    kTs_scr = P.dram("kTs_scr", [8, 128, 2176], BF16)
    vs_scr = P.dram("vs_scr", [2176, 1024], BF16)
    qT_scr = P.dram("qT_scr", [8, 128, NO], BF16)
    h1_scr = P.dram("h1_scr", [NO, D], F32)
    hn_src = [P.dram("hn_src%d" % j, [256, NCH], BF16) for j in range(8)]
    hn_dst = [P.dram("hn_dst%d" % j, [4 * 256, NCH], BF16) for j in range(8)]
    y_src = [P.dram("y_src%d" % j, [512, 516], BF16) for j in range(8)]
    y_dst = [P.dram("y_dst%d" % j, [4 * 512, 516], BF16) for j in range(8)]

    pf = [P.ps("pf%d" % i, [128, 512], F32) for i in range(6)]
    pb = [P.ps("pb%d" % i, [128, 8, 128], BF16) for i in range(2)]
    pfi = [0]
    pbi = [0]

    def next_pf():
        pfi[0] = (pfi[0] + 1) % 4
        return pf[pfi[0]]

    def next_pb():
        pbi[0] = (pbi[0] + 1) % 2
        return pb[pbi[0]]

    ident = P.sb("ident", [128, 128], BF16)
    P.op("pool", lambda h: h.memset(ident.t[:], 1.0), [], [ident])
    P.op("pool", lambda h: h.affine_select(ident.t[:], ident.t[:], [[-1, 128]], ALU.is_equal, 0.0,
                                            base=0, channel_multiplier=1), [ident], [ident])
    ones_bf = P.sb("ones_bf", [128, 128], BF16)
    P.op("pool", lambda h: h.memset(ones_bf.t[:], 1.0), [], [ones_bf])
    gt = P.sb("gt", [128, D], F32)
    cs = P.sb("cs", [128, 25, 64], F32)
    sn = P.sb("sn", [128, 25, 64], F32)
    valid = P.sb("valid", [128, 24], F32)
    validB = P.sb("validB", [128, 24, 128], BF16)
    cw = P.sb("cw", [128, 8, 3], F32)
    bglu = P.sb("bglu", [128, 16], F32)
    dsk = P.sb("dsk", [128, 4], F32)
    sel = P.sb("sel", [128, 4], F32)
    eps_t = P.sb("eps_t", [128, 1], F32)
    P.op("pool", lambda h: h.memset(eps_t.t[:], 1e-6), [], [eps_t])
    for b_, n in [(cs, "cs"), (sn, "sn"), (valid, "valid"), (cw, "cw"), (bglu, "bglu"), (dsk, "dsk"), (sel, "sel")]:
        P.dma("sp", b_.t[:], IN[n].ap(), [], [b_], b_)
    P.op("dve", lambda h: h.tensor_copy(validB.t[:], valid.t[:].unsqueeze(2).to_broadcast([128, 24, 128])),
         [valid], [validB])
    pospi = P.sb("pospi", [128, 1], F32)
    P.op("pool", lambda h: h.memset(pospi.t[:], math.pi), [], [pospi])
    CONST_END = P.off
    hnT_o = P.sb("hnT_o", [128, KT, NO], BF16)

    def load_g(name):
        P.dma("sp", gt.t[:], IN[name].ap(), [], [gt], gt)

    def rmsnorm_rows(xt, xn, ss, junk):
        P.op("act", lambda h: h.activation(junk.t[:], xt.t[:], AF.Square, accum_out=ss.t[:, 0:1]), [xt], [junk, ss])
        P.op("act", lambda h: h.activation(ss.t[:, 1:2], ss.t[:, 0:1], AF.Sqrt, bias=eps_t.t[:, 0:1], scale=1.0 / D), [ss, eps_t], [ss])
        P.op("dve", lambda h: h.reciprocal(ss.t[:, 2:3], ss.t[:, 1:2]), [ss], [ss])
        P.op("dve", lambda h: h.scalar_tensor_tensor(xn.t[:], xt.t[:], ss.t[:, 2:3], gt.t[:], ALU.mult, ALU.mult),
             [xt, ss, gt], [xn])

    def transpose_rows(xn, dst, dst_ap_fn):
        for half in range(2):
            p = next_pb()
            fns = []
            for j in range(8):
                kt = half * 8 + j
                fns.append(lambda h, p=p, j=j, kt=kt: h.transpose(p.t[:, j, :], xn.t[:, kt * 128:(kt + 1) * 128], ident.t[:]))
            P.mm(fns, [xn, ident], [p])
            P.op("act", lambda h, p=p, half=half: h.activation(dst_ap_fn(half), p.t[:], AF.Identity), [p], [dst])

    A0 = P.off
    wkv = P.sb("wkv", [128, KT, 2048], BF16)
    xts = [P.sb("xt%d" % i, [128, D], F32) for i in range(3)]
    xns = [P.sb("xn%d" % i, [128, D], BF16) for i in range(3)]
    hts = [P.sb("ht%d" % i, [128, KT, 128], BF16) for i in range(3)]
    ss = P.sb("ss", [128, 4], F32)
    krs = [P.sb("kr%d" % i, [128, 1024], F32) for i in range(2)]
    vfs = [P.sb("vf%d" % i, [128, 1024], F32) for i in range(2)]
    t1 = P.sb("t1", [128, 256], F32)
    t2 = P.sb("t2", [128, 256], F32)
    krbs = [P.sb("krb%d" % i, [128, 1024], BF16) for i in range(2)]
    vbs = [P.sb("vb%d" % i, [128, 1024], BF16) for i in range(2)]
    kTts = [P.sb("kTt%d" % i, [128, 8, 128], BF16) for i in range(2)]
    kTt = kTts[0]
    hprev2 = P.sb("hprev2", [128, KT, 2], BF16)
    A1_END = P.off

    load_g("g_attn")
    for half in range(2):
        P.dma("pool", wkv.t[:, :, half * 1024:(half + 1) * 1024],
              IN["w_in_ab"].ap()[:, 1024 + half * 1024:2048 + half * 1024].rearrange("(kt p) c -> p kt c", p=128),
              [], [wkv], wkv)

    def rotary(pk, ti, dst, c0):
        v = pk.t[:].rearrange("p (h two d) -> p h two d", h=4, two=2)
        o = dst.t[:, c0:c0 + 512].rearrange("p (h two d) -> p h two d", h=4, two=2)
        cb = cs.t[:, ti, :].unsqueeze(1).to_broadcast([128, 4, 64])
        sb_ = sn.t[:, ti, :].unsqueeze(1).to_broadcast([128, 4, 64])
        a = t1.t[:].rearrange("p (h d) -> p h d", h=4)
        b = t2.t[:].rearrange("p (h d) -> p h d", h=4)
        P.op("dve", lambda h: h.tensor_tensor(a, v[:, :, 0, :], cb, ALU.mult), [pk, cs], [t1])
        P.op("dve", lambda h: h.tensor_tensor(b, v[:, :, 1, :], sb_, ALU.mult), [pk, sn], [t2])
        P.op("dve", lambda h: h.tensor_tensor(o[:, :, 0, :], a, b, ALU.subtract), [t1, t2], [dst])
        P.op("dve", lambda h: h.tensor_tensor(a, v[:, :, 1, :], cb, ALU.mult), [pk, cs], [t1])
        P.op("dve", lambda h: h.tensor_tensor(b, v[:, :, 0, :], sb_, ALU.mult), [pk, sn], [t2])
        P.op("dve", lambda h: h.tensor_tensor(o[:, :, 1, :], a, b, ALU.add), [t1, t2], [dst])

    def store_kT(src_bf, scr, col0, kb=None):
        p = next_pb()
        if kb is None:
            kb = kTt
        fns = [(lambda h, p=p, j=j: h.transpose(p.t[:, j, :], src_bf.t[:, j * 128:(j + 1) * 128], ident.t[:])) for j in range(8)]
        P.mm(fns, [src_bf, ident], [p])
        P.op("act", lambda h, p=p, kb=kb: h.activation(kb.t[:], p.t[:], AF.Identity), [p], [kb])
        P.dma("sp", scr.t.ap()[:, :, col0:col0 + 128].rearrange("h d t -> d h t"), kb.t[:], [kb], [scr], kb)

    def stageL(ti):
        xt = xts[ti % 3]
        src = IN["xh"].ap()[ti * 128:(ti + 1) * 128, :] if ti < 24 else IN["xs"].ap()
        P.dma("sp", xt.t[:], src, [], [xt], xt)

    def stageN(ti):
        rmsnorm_rows(xts[ti % 3], xns[ti % 3], ss, xns[ti % 3])

    def stageA2(ti):
        xn = xns[ti % 3]
        if ti < 16:
            ht = hts[ti % 3]
            transpose_rows(xn, ht, lambda half, ht=ht: ht.t[:, half * 8:(half + 1) * 8, :])
            if ti == 15:
                P.op("dve", lambda h, ht=ht: h.tensor_copy(hprev2.t[:], ht.t[:, :, 126:128]), [ht], [hprev2])
            return (lambda kt, ht=ht: ht.t[:, kt, :]), ht
        o0 = (ti - 16) * 128
        transpose_rows(xn, hnT_o, lambda half, o0=o0: hnT_o.t[:, half * 8:(half + 1) * 8, o0:o0 + 128])
        return (lambda kt, o0=o0: hnT_o.t[:, kt, o0:o0 + 128]), hnT_o

    def stageM(ti, lhs, hb):
        kr = krs[ti % 2]
        vf = vfs[ti % 2]
        for g4 in range(4):
            pk = next_pf()
            fns = [(lambda h, pk=pk, kt=kt, g4=g4, lhs=lhs: h.matmul(pk.t[:], lhs(kt), wkv.t[:, kt, g4 * 512:(g4 + 1) * 512],
                                                                    start=(kt == 0), stop=(kt == KT - 1))) for kt in range(KT)]
            P.mm(fns, [hb, wkv], [pk])
            if g4 < 2:
                rotary(pk, ti, kr, g4 * 512)
            else:
                c0 = (g4 - 2) * 512
                P.op("act", lambda h, pk=pk, c0=c0, vf=vf: h.activation(vf.t[:, c0:c0 + 512], pk.t[:], AF.Identity), [pk], [vf])

    def stageKpre(ti):
        kr = krs[ti % 2]
        vf = vfs[ti % 2]
        krb = krbs[ti % 2]
        vb = vbs[ti % 2]
        P.op("act", lambda h: h.activation(krb.t[:], kr.t[:], AF.Identity), [kr], [krb])
        if ti < 24:
            P.op("dve", lambda h: h.tensor_scalar(vb.t[:], vf.t[:], valid.t[:, ti:ti + 1], None, ALU.mult), [vf, valid], [vb])
        else:
            P.op("dve", lambda h: h.tensor_copy(vb.t[:], vf.t[:]), [vf], [vb])

    def stageKpost(ti):
        kr = krs[ti % 2]
        vf = vfs[ti % 2]
        krb = krbs[ti % 2]
        vb = vbs[ti % 2]
        kb = kTts[ti % 2]
        if ti < 24:
            store_kT(krb, kT_scr, ti * 128, kb)
            P.dma("sp", v_scr.t.ap()[ti * 128:(ti + 1) * 128, :], vb.t[:], [vb], [v_scr], vb)
            if ti >= 16:
                r0 = (ti - 16) * 128
                P.dma("sp", OUT["kp"].ap()[r0:r0 + 128, :], kr.t[:], [kr], [outbufs["kp"]], kr)
                P.dma("sp", OUT["vp"].ap()[r0:r0 + 128, :], vf.t[:], [vf], [outbufs["vp"]], vf)
        else:
            store_kT(krb, kTs_scr, 2048, kb)
            P.dma("sp", vs_scr.t.ap()[2048:2176, :], vb.t[:], [vb], [vs_scr], vb)
            P.dma("sp", OUT["ks"].ap(), kr.t[:], [kr], [outbufs["ks"]], kr)
            P.dma("sp", OUT["vs"].ap(), vf.t[:], [vf], [outbufs["vs"]], vf)

    for t_ in range(3):
        stageL(t_)
    stageN(0)
    stageN(1)
    infoA = {0: stageA2(0)}
    for ti in range(25):
        if ti + 3 < 25:
            stageL(ti + 3)
        if ti + 2 < 25:
            stageN(ti + 2)
        if ti >= 1:
            stageKpre(ti - 1)
        if ti + 1 < 25:
            infoA[ti + 1] = stageA2(ti + 1)
        stageM(ti, *infoA[ti])
        if ti >= 1:
            stageKpost(ti - 1)
    stageKpre(24)
    stageKpost(24)
    def cacheL(ti):
        xt = xts[ti % 3]
        P.dma("sp", xt.t[:, 0:1024], IN["ck"].ap()[ti * 128:(ti + 1) * 128, :], [], [xt], xt)
        P.dma("sp", xt.t[:, 1024:2048], IN["cv"].ap()[ti * 128:(ti + 1) * 128, :], [], [xt], xt)

    cacheL(0)
    cacheL(1)
    for ti in range(16):
        if ti + 2 < 16:
            cacheL(ti + 2)
        xt = xts[ti % 3]
        krb = krbs[ti % 2]
        vb = vbs[ti % 2]
        P.op("act", lambda h, xt=xt, krb=krb: h.activation(krb.t[:], xt.t[:, 0:1024], AF.Identity), [xt], [krb])
        P.op("dve", lambda h, xt=xt, vb=vb: h.tensor_copy(vb.t[:], xt.t[:, 1024:2048]), [xt], [vb])
        store_kT(krb, kTs_scr, ti * 128, kTts[ti % 2])
        P.dma("sp", vs_scr.t.ap()[ti * 128:(ti + 1) * 128, :], vb.t[:], [vb], [vs_scr], vb)

    if STOP == 'A1':
        P.barrier()
        return
    P.barrier()
    P.off = A0
    wq = P.sb("wq", [128, KT, 1024], BF16)
    hprev2b = P.sb("hprev2b", [128, KT, 2], BF16)
    qf = P.sb("qf", [128, 1024], F32)
    qb = P.sb("qb", [128, 1024], BF16)
    t1 = P.sb("t1b", [128, 256], F32)
    t2 = P.sb("t2b", [128, 256], F32)
    kTt = P.sb("kTtb", [128, 8, 128], BF16)
    hprev2k = P.sb("hprev2k", [128, KT, 2], BF16, at=hprev2.at)
    hprev2k.w = dict(hprev2.w)
    P.dma("pool", wq.t[:], IN["w_in_ab"].ap()[:, 0:1024].rearrange("(kt p) c -> p kt c", p=128), [], [wq], wq)
    for tj in range(9):
        ti = 16 + tj
        o0 = tj * 128
        for g2_ in range(2):
            pk = next_pf()
            fns = [(lambda h, pk=pk, kt=kt, g2_=g2_, o0=o0: h.matmul(pk.t[:], hnT_o.t[:, kt, o0:o0 + 128],
                                                                      wq.t[:, kt, g2_ * 512:(g2_ + 1) * 512],
                                                                      start=(kt == 0), stop=(kt == KT - 1))) for kt in range(KT)]
            P.mm(fns, [hnT_o, wq], [pk])
            rotary(pk, ti, qf, g2_ * 512)
        P.op("act", lambda h: h.activation(qb.t[:], qf.t[:], AF.Identity), [qf], [qb])
        store_kT(qb, qT_scr, o0)

    if STOP == 'A2':
        P.barrier()
        return
    P.barrier()
    P.off = A0
    ocat = P.sb("ocat", [128, KT, NO], BF16)
    hp2 = P.sb("hp2", [128, KT, 2], BF16)
    B0 = P.off
    P.op("dve", lambda h: h.tensor_copy(hp2.t[:], hprev2k.t[:]), [hprev2k], [hp2])
    P.barrier()
    maskp = P.sb("maskp", [128, 20, 512], BF16)
    masks_ = P.sb("masks_", [128, 17, 128], BF16)
    P.dma("pool", maskp.t[:], IN["maskp"].ap(), [], [maskp], maskp)
    P.dma("pool", masks_.t[:], IN["masks"].ap(), [], [masks_], masks_)
    kTh = [P.sb("kTh%d" % i, [128, NTP], BF16) for i in range(1)] * 2
    vh = [P.sb("vh%d" % i, [128, 24, 128], BF16) for i in range(1)] * 2
    kTsh = [P.sb("kTsh%d" % i, [128, 2176], BF16) for i in range(1)] * 2
    vsh = [P.sb("vsh%d" % i, [128, 17, 128], BF16) for i in range(1)] * 2
    qTh = [P.sb("qTh%d" % i, [128, NO], BF16) for i in range(1)] * 2
    wt = [P.sb("wt%d" % i, [128, KT, 128], BF16) for i in range(4)]
    pts = [P.sb("pt%d" % i, [128, 512], BF16) for i in range(4)]
    ptm = [P.sb("ptm%d" % i, [128, 512], BF16) for i in range(4)]
    za = P.sb("za", [128, NO], F32)
    rl = P.sb("rl", [128, 512], F32)
    of = P.sb("of", [128, 512], F32)
    sg = P.sb("sg", [128, 512], F32)

    def silu_evac(pk, dstb, dst_ap, n):
        P.op("act", lambda h: h.activation(sg.t[:, 0:n], pk.t[:, 0:n], AF.Exp, scale=-1.0), [pk], [sg])
        P.op("dve", lambda h: h.tensor_scalar(sg.t[:, 0:n], sg.t[:, 0:n], 1.0, None, ALU.add), [sg], [sg])
        P.op("dve", lambda h: h.reciprocal(sg.t[:, 0:n], sg.t[:, 0:n]), [sg], [sg])
        P.op("dve", lambda h: h.tensor_tensor(dst_ap, pk.t[:, 0:n], sg.t[:, 0:n], ALU.mult), [pk, sg], [dstb])
    fb = [P.sb("fb%d" % i, [128, NO + 2], F32) for i in range(4)]
    convo_p = P.sb("convo_p", [128, 8, 2], F32)
    convo_s = P.sb("convo_s", [128, 8, 2], F32)
    sconv = P.sb("sconv", [128, 8, 2], F32)
    P.dma("sp", sconv.t[:], IN["sconv"].ap(), [], [sconv], sconv)
    scale = 128.0 ** -0.5

    def load_wt(i, c0):
        P.dma("pool", wt[i].t[:], IN["w_in_ab"].ap()[:, c0:c0 + 128].rearrange("(kt p) c -> p kt c", p=128), [], [wt[i]], wt[i])

    def proj_feat(wb, dst_ap_fn, evac, rhs_buf, rhs_fn, n):
        pk = next_pf()
        fns = [(lambda h, pk=pk, kt=kt: h.matmul(pk.t[:, 0:n], wb.t[:, kt, :], rhs_fn(kt), start=(kt == 0), stop=(kt == KT - 1)))
               for kt in range(KT)]
        P.mm(fns, [wb, rhs_buf], [pk])
        evac(pk)

    def attention(hh, qT, q0, nq, kT, vt, ktiles, mask_fn, vB_fn, o_dst_fn, zcol0):
        po = pf[4]
        pl = pf[5]
        nk = len(ktiles)
        LA = 3
        pms = {}

        def issue_S(i):
            kt_ = ktiles[i]
            ps_ = next_pf()
            P.mm([lambda h, ps_=ps_, kt_=kt_: h.matmul(ps_.t[:, 0:nq], kT.t[:, kt_ * 128:(kt_ + 1) * 128], qT.t[:, q0:q0 + nq],
                                                        start=True, stop=True)], [kT, qT], [ps_])
            pe_ = pts[i % 4]
            pm_ = ptm[i % 4]
            P.op("act", lambda h, ps_=ps_, pe_=pe_: h.activation(pe_.t[:, 0:nq], ps_.t[:, 0:nq], AF.Exp, scale=scale), [ps_], [pe_])
            mk, mb = mask_fn(i)
            eng = "dve"
            P.op(eng, lambda h, pe_=pe_, pm_=pm_, mk=mk: h.tensor_tensor(pm_.t[:, 0:nq], pe_.t[:, 0:nq], mk, ALU.mult), [pe_, mb], [pm_])
            pms[i] = pm_

        def issue_PV(i):
            kt_ = ktiles[i]
            pm_ = pms[i]
            vB, vBb = vB_fn(i)
            P.mm([lambda h, pm_=pm_, kt_=kt_, i=i: h.matmul(po.t[:, 0:nq], vt.t[:, kt_, :], pm_.t[:, 0:nq], start=(i == 0), stop=(i == nk - 1)),
                  lambda h, pm_=pm_, vB=vB, i=i: h.matmul(pl.t[:, 0:nq], vB, pm_.t[:, 0:nq], start=(i == 0), stop=(i == nk - 1))],
                 [vt, pm_, vBb], [po, pl])

        for i in range(min(LA, nk)):
            issue_S(i)
        for i in range(nk):
            if i + LA < nk:
                issue_S(i + LA)
            issue_PV(i)
        P.op("dve", lambda h: h.reciprocal(rl.t[:, 0:nq], pl.t[:, 0:nq]), [pl], [rl])
        P.op("dve", lambda h: h.tensor_tensor(of.t[:, 0:nq], po.t[:, 0:nq], rl.t[:, 0:nq], ALU.mult), [po, rl], [of])
        P.op("dve", lambda h: h.tensor_tensor(o_dst_fn(), of.t[:, 0:nq], za.t[:, zcol0:zcol0 + nq], ALU.mult), [of, za], [ocat])

    for hh in range(8):
        b2 = hh % 2
        P.dma("sp", kTh[b2].t[:], kT_scr.t.ap()[hh], [kT_scr], [kTh[b2]], kTh[b2])
        P.dma("sp", vh[b2].t[:], v_scr.t.ap()[:, hh * 128:(hh + 1) * 128].rearrange("(t p) d -> p t d", p=128), [v_scr], [vh[b2]], vh[b2])
        P.dma("sp", kTsh[b2].t[:], kTs_scr.t.ap()[hh], [kTs_scr], [kTsh[b2]], kTsh[b2])
        P.dma("sp", vsh[b2].t[:], vs_scr.t.ap()[:, hh * 128:(hh + 1) * 128].rearrange("(t p) d -> p t d", p=128), [vs_scr], [vsh[b2]], vsh[b2])
        P.dma("sp", qTh[b2].t[:], qT_scr.t.ap()[hh], [qT_scr], [qTh[b2]], qTh[b2])
        load_wt(0, 3072 + hh * 128)
        for (c0, n) in [(0, 512), (512, 512), (1024, 128)]:
            proj_feat(wt[0], None, lambda pk, c0=c0, n=n: silu_evac(pk, za, za.t[:, c0:c0 + n], n),
                      hnT_o, lambda kt, c0=c0, n=n: hnT_o.t[:, kt, c0:c0 + n], n)
        for qc in range(2):
            kts = list(range(4 * qc, 4 * qc + 20))
            attention(hh, qTh[b2], qc * 512, 512, kTh[b2], vh[b2], kts,
                      lambda i: (maskp.t[:, i, :], maskp),
                      lambda i, kts=kts: (validB.t[:, kts[i], :], validB),
                      lambda qc=qc, hh=hh: ocat.t[:, hh, qc * 512:(qc + 1) * 512], qc * 512)
        attention(hh, qTh[b2], 1024, 128, kTsh[b2], vsh[b2], list(range(17)),
                  lambda i: (masks_.t[:, i, :], masks_),
                  lambda i: (ones_bf.t[:], ones_bf),
                  lambda hh=hh: ocat.t[:, hh, 1024:1152], 1024)

    for cc in range(8):
        for j, base in enumerate([4096, 5120, 6144, 7168]):
            load_wt(j, base + cc * 128)
        bb, cb_, hb_, zb = fb
        for j, dstb in enumerate(fb):
            for (c0, n) in [(0, 512), (512, 512), (1024, 128)]:
                if j == 3:
                    ev = lambda pk, c0=c0, n=n, dstb=dstb: silu_evac(pk, dstb, dstb.t[:, 2 + c0:2 + c0 + n], n)
                else:
                    ev = lambda pk, c0=c0, n=n, dstb=dstb: P.op("act", lambda h: h.activation(dstb.t[:, 2 + c0:2 + c0 + n], pk.t[:, 0:n], AF.Identity), [pk], [dstb])
                proj_feat(wt[j], None, ev, hnT_o, lambda kt, c0=c0, n=n: hnT_o.t[:, kt, c0:c0 + n], n)
            if j in (1, 2):
                proj_feat(wt[j], None, lambda pk, dstb=dstb: P.op("act", lambda h: h.activation(dstb.t[:, 0:2], pk.t[:, 0:2], AF.Identity), [pk], [dstb]),
                          hp2, lambda kt: hp2.t[:, kt, :], 2)
        P.op("dve", lambda h: h.tensor_tensor(cb_.t[:], cb_.t[:], hb_.t[:], ALU.mult), [cb_, hb_], [cb_])
        w0 = cw.t[:, cc, 0:1]
        w1 = cw.t[:, cc, 1:2]
        w2 = cw.t[:, cc, 2:3]
        P.op("dve", lambda h, w2=w2: h.tensor_scalar(hb_.t[:, 2:1026], cb_.t[:, 2:1026], w2, None, ALU.mult), [cb_, cw], [hb_])
        P.op("dve", lambda h, w1=w1: h.scalar_tensor_tensor(hb_.t[:, 2:1026], cb_.t[:, 1:1025], w1, hb_.t[:, 2:1026], ALU.mult, ALU.add), [cb_, cw, hb_], [hb_])
        P.op("dve", lambda h, w0=w0: h.scalar_tensor_tensor(hb_.t[:, 2:1026], cb_.t[:, 0:1024], w0, hb_.t[:, 2:1026], ALU.mult, ALU.add), [cb_, cw, hb_], [hb_])
        P.op("dve", lambda h, cc=cc: h.tensor_copy(convo_p.t[:, cc, :], cb_.t[:, 1024:1026]), [cb_], [convo_p])
        P.op("dve", lambda h, cc=cc: h.tensor_copy(cb_.t[:, 1024:1026], sconv.t[:, cc, :]), [sconv], [cb_])
        P.op("dve", lambda h, w2=w2: h.tensor_scalar(hb_.t[:, 1026:1034], cb_.t[:, 1026:1034], w2, None, ALU.mult), [cb_, cw], [hb_])
        P.op("dve", lambda h, w1=w1: h.scalar_tensor_tensor(hb_.t[:, 1026:1034], cb_.t[:, 1025:1033], w1, hb_.t[:, 1026:1034], ALU.mult, ALU.add), [cb_, cw, hb_], [hb_])
        P.op("dve", lambda h, w0=w0: h.scalar_tensor_tensor(hb_.t[:, 1026:1034], cb_.t[:, 1024:1032], w0, hb_.t[:, 1026:1034], ALU.mult, ALU.add), [cb_, cw, hb_], [hb_])
        P.op("dve", lambda h, cc=cc: h.tensor_copy(convo_s.t[:, cc, :], cb_.t[:, 1032:1034]), [cb_], [convo_s])
        P.op("dve", lambda h: h.tensor_tensor(hb_.t[:, 2:1034], hb_.t[:, 2:1034], bb.t[:, 2:1034], ALU.mult), [hb_, bb], [hb_])
        P.op("dve", lambda h, cc=cc: h.tensor_tensor(ocat.t[:, 8 + cc, 0:1032], hb_.t[:, 2:1034], zb.t[:, 2:1034], ALU.mult), [hb_, zb], [ocat])
        P.op("dve", lambda h, cc=cc: h.memset(ocat.t[:, 8 + cc, 1032:1152], 0.0), [], [ocat])
    P.dma("sp", OUT["convp"].ap(), convo_p.t[:], [convo_p], [outbufs["convp"]], convo_p)
    P.dma("sp", OUT["convs"].ap(), convo_s.t[:], [convo_s], [outbufs["convs"]], convo_s)

    if STOP == 'B':
        P.barrier()
        return
    P.barrier()
    hn1T = P.sb("hn1T", [128, KT, NCH], BF16, at=hnT_o.at)
    P.off = B0
    wgb = [P.sb("wg%d" % i, [128, KT, 512], BF16) for i in range(2)]
    h1s = [P.sb("h1s%d" % i, [128, D], F32) for i in range(5)]
    xn1 = [P.sb("xn1%d" % i, [128, D], BF16) for i in range(2)]
    junk = P.sb("junk2", [128, D], F32)
    ss = P.sb("ss2", [128, 4], F32)
    tmpT = P.sb("tmpT", [128, KT, 128], BF16)
    load_g("g_ssm")
    wi = 0
    for tiles in [list(range(0, 5)), list(range(5, 9))]:
        for si, tj in enumerate(tiles):
            o0 = tj * 128
            src = IN["xh"].ap()[NHALO + o0:NHALO + o0 + 128, :] if tj < 8 else IN["xs"].ap()
            P.dma("sp", h1s[si].t[:], src, [], [h1s[si]], h1s[si])
        for g4 in range(4):
            wg = wgb[wi % 2]
            wi += 1
            P.dma("pool", wg.t[:], IN["w_out_ab"].ap()[:, g4 * 512:(g4 + 1) * 512].rearrange("(kt p) c -> p kt c", p=128), [], [wg], wg)
            for si, tj in enumerate(tiles):
                o0 = tj * 128
                ht_ = h1s[si]
                pk = next_pf()
                fns = [(lambda h, pk=pk, kt=kt, o0=o0, wg=wg: h.matmul(pk.t[:], ocat.t[:, kt, o0:o0 + 128], wg.t[:, kt, :],
                                                                         start=(kt == 0), stop=(kt == KT - 1))) for kt in range(KT)]
                P.mm(fns, [ocat, wg], [pk])
                P.op("dve", lambda h, pk=pk, ht_=ht_, g4=g4: h.tensor_tensor(ht_.t[:, g4 * 512:(g4 + 1) * 512], ht_.t[:, g4 * 512:(g4 + 1) * 512], pk.t[:], ALU.add),
                     [pk, ht_], [ht_])
        for si, tj in enumerate(tiles):
            o0 = tj * 128
            ht_ = h1s[si]
            P.dma("sp", h1_scr.t.ap()[o0:o0 + 128, :], ht_.t[:], [ht_], [h1_scr], ht_)
            if STOP == 'C1':
                if tj < 8:
                    P.dma("sp", OUT["yp"].ap()[o0:o0 + 128, :], ht_.t[:], [ht_], [outbufs["yp"]], ht_)
                else:
                    P.dma("sp", OUT["ys"].ap(), ht_.t[:], [ht_], [outbufs["ys"]], ht_)
            xn = xn1[tj % 2]
            rmsnorm_rows(ht_, xn, ss, junk)
            if tj < 8:
                transpose_rows(xn, hn1T, lambda half, o0=o0: hn1T.t[:, half * 8:(half + 1) * 8, o0:o0 + 128])
            else:
                transpose_rows(xn, tmpT, lambda half: tmpT.t[:, half * 8:(half + 1) * 8, :])
                P.op("dve", lambda h: h.tensor_copy(hn1T.t[:, :, 1024:1032], tmpT.t[:, :, 0:8]), [tmpT], [hn1T])
    for j in range(8):
        P.dma("sp", hn_src[j].t.ap().rearrange("(k p) t -> p k t", p=128), hn1T.t[:, 2 * j:2 * j + 2, :], [hn1T], [hn_src[j]], hn1T)
        P.coll(hn_src[j], hn_dst[j], GROUPS)

    if STOP == 'C1':
        P.barrier()
        return
    P.barrier()
    P.off = CONST_END
    uT = P.sb("uT", [128, 4, 4 * NCH], BF16)
    ygT = P.sb("ygT", [128, 4, 4 * NCH], BF16)
    L1 = P.off
    wu = P.sb("wu", [128, KT, 512], BF16)
    hch = [P.sb("hch%d" % i, [128, KT, 516], BF16) for i in range(2)]
    P.dma("pool", wu.t[:], IN["w_in_c"].ap()[:, 0:512].rearrange("(kt p) c -> p kt c", p=128), [], [wu], wu)
    ci = 0
    for r in range(4):
        for hf in range(2):
            hc = hch[ci % 2]
            ci += 1
            c0 = hf * 516
            for j in range(8):
                P.dma("sp", hc.t[:, 2 * j:2 * j + 2, :], hn_dst[j].t.ap()[r * 256:(r + 1) * 256, c0:c0 + 516].rearrange("(k p) t -> p k t", p=128),
                      [hn_dst[j]], [hc], hc)
            for ft in range(4):
                pk = next_pf()
                fns = [(lambda h, pk=pk, kt=kt, ft=ft, hc=hc: h.matmul(pk.t[:, 0:512], wu.t[:, kt, ft * 128:(ft + 1) * 128], hc.t[:, kt, 0:512],
                                                                      start=(kt == 0), stop=(kt == KT - 1))) for kt in range(KT)]
                P.mm(fns, [wu, hc], [pk])
                P.op("act", lambda h, pk=pk, ft=ft, r=r, c0=c0: h.activation(uT.t[:, ft, r * NCH + c0:r * NCH + c0 + 512], pk.t[:, 0:512], AF.Identity), [pk], [uT])
                pk2 = next_pf()
                fns2 = [(lambda h, pk2=pk2, kt=kt, ft=ft, hc=hc: h.matmul(pk2.t[:, 0:4], wu.t[:, kt, ft * 128:(ft + 1) * 128], hc.t[:, kt, 512:516],
                                                                         start=(kt == 0), stop=(kt == KT - 1))) for kt in range(KT)]
                P.mm(fns2, [wu, hc], [pk2])
                P.op("act", lambda h, pk2=pk2, ft=ft, r=r, c0=c0: h.activation(uT.t[:, ft, r * NCH + c0 + 512:r * NCH + c0 + 516], pk2.t[:, 0:4], AF.Identity), [pk2], [uT])

    if STOP == 'U':
        P.barrier()
        return
    P.barrier()
    P.off = L1
    def small(name, shape, dt=F32):
        return P.sb(name, shape, dt)
    lre_s = small("lre_s", [128, 16]); lim_s = small("lim_s", [128, 16]); lst_s = small("lst_s", [128, 16])
    lre_r = small("lre_r", [128, 256]); lim_r = small("lim_r", [128, 256]); lst_r = small("lst_r", [128, 256])
    bre_r = small("bre_r", [128, 256]); bim_r = small("bim_r", [128, 256])
    cre_s = small("cre_s", [128, 16, 16]); cim_s = small("cim_s", [128, 16, 16])
    rmask = small("rmask", [128, 8]); smask = small("smask", [128, 2])
    sre0 = small("sre0", [128, 4, 16]); sim0 = small("sim0", [128, 4, 16])
    iota = small("iota", [128, 1024])
    for b_, n in [(lre_s, "lre_s"), (lim_s, "lim_s"), (lst_s, "lst_s"), (cre_s, "cre_s"), (cim_s, "cim_s"),
                  (rmask, "rmask"), (smask, "smask"), (sre0, "sre0"), (sim0, "sim0"), (iota, "iota")]:
        P.dma("sp", b_.t[:], IN[n].ap(), [], [b_], b_)
    for b_, n in [(lre_r, "lre_r"), (lim_r, "lim_r"), (lst_r, "lst_r"), (bre_r, "bre_r"), (bim_r, "bim_r")]:
        P.dma("sp", b_.t[:], IN[n].ap().rearrange("p a b -> p (a b)"), [], [b_], b_)
    negpi = small("negpi", [128, 1])
    P.op("dve", lambda h: h.memset(negpi.t[:], -math.pi), [], [negpi])

    I32 = mybir.dt.int32
    tq = small("tq", [128, 1024]); tiq = small("tiq", [128, 1024], I32)
    halfpi = small("halfpi", [128, 1]); zero_t = small("zero_t", [128, 1])
    P.op("dve", lambda h: h.memset(halfpi.t[:], 0.5 * math.pi), [], [halfpi])
    P.op("dve", lambda h: h.memset(zero_t.t[:], 0.0), [], [zero_t])

    def sincos(ang_ap, n, s_ap, c_ap, rd, wr):
        for (dst, addc, bt, lo, hi) in [(s_ap, 0.0, zero_t, -math.pi, math.pi), (c_ap, 0.25, halfpi, -1.5 * math.pi, 0.5 * math.pi)]:
            P.op("dve", lambda h, addc=addc: h.tensor_scalar(tq.t[:, 0:n], ang_ap, 1.0 / TWO_PI, addc, ALU.mult, ALU.add), rd, [tq])
            P.op("dve", lambda h: h.tensor_copy(tiq.t[:, 0:n], tq.t[:, 0:n]), [tq], [tiq])
            P.op("dve", lambda h: h.tensor_copy(tq.t[:, 0:n], tiq.t[:, 0:n]), [tiq], [tq])
            P.op("dve", lambda h: h.scalar_tensor_tensor(tq.t[:, 0:n], tq.t[:, 0:n], -TWO_PI, ang_ap, ALU.mult, ALU.add), [tq] + rd, [tq])
            P.op("dve", lambda h, lo=lo, hi=hi: h.tensor_scalar(tq.t[:, 0:n], tq.t[:, 0:n], lo, hi, ALU.max, ALU.min), [tq], [tq])
            P.op("act", lambda h, dst=dst, bt=bt: h.activation(dst, tq.t[:, 0:n], AF.Sin, bias=bt.t[:, 0:1], scale=1.0), [tq, bt], wr)

    def disc(lre, lim, lst, n, pref):
        o = {}
        for nm in ["step", "mag", "th", "c", "s", "tmp", "nr", "den", "cr", "ci", "a", "b"]:
            o[nm] = small(pref + nm, [128, n])
        al = [o[k] for k in o] + [lre, lim, lst]
        P.op("act", lambda h: h.activation(o["step"].t[:], lst.t[:, 0:n], AF.Exp), al, al)
        P.op("dve", lambda h: h.tensor_tensor(o["th"].t[:], lim.t[:, 0:n], o["step"].t[:], ALU.mult), al, al)
        P.op("dve", lambda h: h.tensor_tensor(o["a"].t[:], lre.t[:, 0:n], o["step"].t[:], ALU.mult), al, al)
        P.op("act", lambda h: h.activation(o["mag"].t[:], o["a"].t[:], AF.Exp), al, al)
        sincos(o["th"].t[:], n, o["s"].t[:], o["c"].t[:], al, al)
        P.op("dve", lambda h: h.tensor_tensor(o["a"].t[:], o["mag"].t[:], o["c"].t[:], ALU.mult), al, al)
        P.op("dve", lambda h: h.tensor_scalar(o["nr"].t[:], o["a"].t[:], 1.0, -1.0, ALU.mult, ALU.add), al, al)
        P.op("dve", lambda h: h.tensor_tensor(o["b"].t[:], o["mag"].t[:], o["s"].t[:], ALU.mult), al, al)
        P.op("dve", lambda h: h.tensor_tensor(o["den"].t[:], lre.t[:, 0:n], lre.t[:, 0:n], ALU.mult), al, al)
        P.op("dve", lambda h: h.tensor_tensor(o["tmp"].t[:], lim.t[:, 0:n], lim.t[:, 0:n], ALU.mult), al, al)
        P.op("dve", lambda h: h.tensor_tensor(o["den"].t[:], o["den"].t[:], o["tmp"].t[:], ALU.add), al, al)
        P.op("dve", lambda h: h.reciprocal(o["den"].t[:], o["den"].t[:]), al, al)
        P.op("dve", lambda h: h.tensor_tensor(o["cr"].t[:], o["nr"].t[:], lre.t[:, 0:n], ALU.mult), al, al)
        P.op("dve", lambda h: h.tensor_tensor(o["tmp"].t[:], o["b"].t[:], lim.t[:, 0:n], ALU.mult), al, al)
        P.op("dve", lambda h: h.tensor_tensor(o["cr"].t[:], o["cr"].t[:], o["tmp"].t[:], ALU.add), al, al)
        P.op("dve", lambda h: h.tensor_tensor(o["cr"].t[:], o["cr"].t[:], o["den"].t[:], ALU.mult), al, al)
        P.op("dve", lambda h: h.tensor_tensor(o["ci"].t[:], o["b"].t[:], lre.t[:, 0:n], ALU.mult), al, al)
        P.op("dve", lambda h: h.tensor_tensor(o["tmp"].t[:], o["nr"].t[:], lim.t[:, 0:n], ALU.mult), al, al)
        P.op("dve", lambda h: h.tensor_tensor(o["ci"].t[:], o["ci"].t[:], o["tmp"].t[:], ALU.subtract), al, al)
        P.op("dve", lambda h: h.tensor_tensor(o["ci"].t[:], o["ci"].t[:], o["den"].t[:], ALU.mult), al, al)
        return o, al

    ds_, als = disc(lre_s, lim_s, lst_s, 16, "ds_")
    dr_, alr = disc(lre_r, lim_r, lst_r, 256, "dr_")
    bbr = small("bbr", [128, 256]); bbi = small("bbi", [128, 256]); tmpr = small("tmpr", [128, 256])
    alr2 = alr + [bbr, bbi, tmpr, bre_r, bim_r]
    P.op("dve", lambda h: h.tensor_tensor(bbr.t[:], dr_["cr"].t[:], bre_r.t[:], ALU.mult), alr2, alr2)
    P.op("dve", lambda h: h.tensor_tensor(tmpr.t[:], dr_["ci"].t[:], bim_r.t[:], ALU.mult), alr2, alr2)
    P.op("dve", lambda h: h.tensor_tensor(bbr.t[:], bbr.t[:], tmpr.t[:], ALU.subtract), alr2, alr2)
    P.op("dve", lambda h: h.tensor_tensor(bbi.t[:], dr_["cr"].t[:], bim_r.t[:], ALU.mult), alr2, alr2)
    P.op("dve", lambda h: h.tensor_tensor(tmpr.t[:], dr_["ci"].t[:], bre_r.t[:], ALU.mult), alr2, alr2)
    P.op("dve", lambda h: h.tensor_tensor(bbi.t[:], bbi.t[:], tmpr.t[:], ALU.add), alr2, alr2)
    BbT = [small("BbT%d" % ri, [128, 16, 128], BF16) for ri in range(2)]
    for ri, src in enumerate([bbr, bbi]):
        for qq in range(4):
            for g2 in range(2):
                m = rmask.t[:, qq * 2 + g2:qq * 2 + g2 + 1]
                o_ap = BbT[ri].t[:].rearrange("p (ft q) c -> p ft q c", q=4)[:, :, qq, g2 * 64:(g2 + 1) * 64]
                i_ap = src.t[:].rearrange("p (ft d) -> p ft d", ft=4)
                P.op("dve", lambda h, o_ap=o_ap, i_ap=i_ap, m=m: h.tensor_scalar(o_ap, i_ap, m, None, ALU.mult), alr2 + [rmask], [BbT[ri]])
    CT = [small("CT%d" % ri, [128, 16, 128], BF16) for ri in range(2)]
    for ri in range(2):
        P.op("dve", lambda h, ri=ri: h.memset(CT[ri].t[:], 0.0), [], [CT[ri]])
    for ri, (src, sgn) in enumerate([(cre_s, 1.0), (cim_s, -1.0)]):
        for pair in range(16):
            qq = pair % 4
            for g2 in range(2):
                m = smask.t[:, g2:g2 + 1]
                col = qq * 32 + g2 * 16
                P.op("dve", lambda h, ri=ri, pair=pair, col=col, m=m, src=src, sgn=sgn: h.tensor_scalar(
                    CT[ri].t[:, pair, col:col + 16], src.t[:, pair, :], m, sgn, ALU.mult, ALU.mult), [src, smask], [CT[ri]])
    rr = ds_["mag"]; th = ds_["th"]
    cth = small("cth", [128, 16]); sth = small("sth", [128, 16])
    P.op("dve", lambda h: h.tensor_copy(cth.t[:], ds_["c"].t[:]), als, [cth])
    P.op("dve", lambda h: h.tensor_copy(sth.t[:], ds_["s"].t[:]), als, [sth])

    tabc = [small("tabc%d" % i, [128, 1024]) for i in range(4)]
    tabs = [small("tabs%d" % i, [128, 1024]) for i in range(4)]
    WS = [dict(gr=small("gr0", [128, 1024]), gi=small("gi0", [128, 1024]), yr=small("yr0", [128, 1024]), yi=small("yi0", [128, 1024]),
               hrb=small("hrb0", [128, 1024], BF16), hib=small("hib0", [128, 1024], BF16))]
    hib1 = small("hib1", [128, 1024], BF16)
    ytmp = small("ytmp", [128, 512]); ysq = small("ysq", [128, 512])
    stp_l = [small("stp%d" % p_, [128, 2]) for p_ in range(16)]
    sts_l = [[small("sts%d_%d" % (b_, p_), [128, 2]) for p_ in range(16)] for b_ in range(4)]
    for w_ in WS:
        w_["gin"] = small("gin0", [128, 2]); w_["hend"] = small("hend0", [128, 2])
    for p_ in range(16):
        P.op("dve", lambda h, p_=p_: h.memset(stp_l[p_].t[:], 0.0), [], [stp_l[p_]])
    P.barrier()
    blkA = lre_r.at
    blkB = dr_["step"].at
    WS.append(dict(gr=P.sb("gr1", [128, 1024], F32, at=blkB), gi=P.sb("gi1", [128, 1024], F32, at=blkB + 4096),
                   yr=P.sb("yr1", [128, 1024], F32, at=blkB + 8192), yi=P.sb("yi1", [128, 1024], F32, at=blkA),
                   hrb=P.sb("hrb1", [128, 1024], BF16, at=blkB + 12288), hib=hib1,
                   gin=small("gin1", [128, 2]), hend=small("hend1", [128, 2])))
    ang = WS[0]["gr"]
    GC = 1.5957691216057308
    kcount = [0]

    def stageX(pair, qq, ft, col0, T, tc_, ts_, init_re, init_im, init_bufs, out_b):
        w = WS[kcount[0] % 2]
        kcount[0] += 1
        gr, gi, yr, yi, hrb, hib, gin, hend = w["gr"], w["gi"], w["yr"], w["yi"], w["hrb"], w["hib"], w["gin"], w["hend"]
        for h0 in range(0, T, 512):
            n = min(512, T - h0)
            pxr = next_pf(); pxi = next_pf()
            P.mm([lambda h, pxr=pxr, n=n, h0=h0: h.matmul(pxr.t[:, 0:n], BbT[0].t[:, pair, :], uT.t[:, ft, col0 + h0:col0 + h0 + n], start=True, stop=True)], [BbT[0], uT], [pxr])
            P.mm([lambda h, pxi=pxi, n=n, h0=h0: h.matmul(pxi.t[:, 0:n], BbT[1].t[:, pair, :], uT.t[:, ft, col0 + h0:col0 + h0 + n], start=True, stop=True)], [BbT[1], uT], [pxi])
            c_ = tc_.t[:, h0:h0 + n]; s_ = ts_.t[:, h0:h0 + n]
            P.op("dve", lambda h, pxr=pxr, c_=c_, n=n, h0=h0: h.tensor_tensor(yr.t[:, h0:h0 + n], pxr.t[:, 0:n], c_, ALU.mult), [pxr, tc_], [yr])
            P.op("dve", lambda h, pxi=pxi, s_=s_, n=n, h0=h0: h.tensor_tensor(gr.t[:, h0:h0 + n], pxi.t[:, 0:n], s_, ALU.mult), [pxi, ts_], [gr])
            P.op("dve", lambda h, n=n, h0=h0: h.tensor_tensor(yr.t[:, h0:h0 + n], yr.t[:, h0:h0 + n], gr.t[:, h0:h0 + n], ALU.add), [yr, gr], [yr])
            P.op("dve", lambda h, pxi=pxi, c_=c_, n=n, h0=h0: h.tensor_tensor(yi.t[:, h0:h0 + n], pxi.t[:, 0:n], c_, ALU.mult), [pxi, tc_], [yi])
            P.op("dve", lambda h, pxr=pxr, s_=s_, n=n, h0=h0: h.tensor_tensor(gi.t[:, h0:h0 + n], pxr.t[:, 0:n], s_, ALU.mult), [pxr, ts_], [gi])
            P.op("dve", lambda h, n=n, h0=h0: h.tensor_tensor(yi.t[:, h0:h0 + n], yi.t[:, h0:h0 + n], gi.t[:, h0:h0 + n], ALU.subtract), [yi, gi], [yi])
        ct = cth.t[:, pair:pair + 1]; st_ = sth.t[:, pair:pair + 1]
        P.op("dve", lambda h: h.tensor_scalar(gin.t[:, 0:1], init_re, ct, None, ALU.mult), init_bufs + [cth], [gin])
        P.op("dve", lambda h: h.scalar_tensor_tensor(gin.t[:, 0:1], init_im, st_, gin.t[:, 0:1], ALU.mult, ALU.subtract), init_bufs + [sth, gin], [gin])
        P.op("dve", lambda h: h.tensor_scalar(gin.t[:, 0:1], gin.t[:, 0:1], -1.0, None, ALU.mult), [gin], [gin])
        P.op("dve", lambda h: h.tensor_scalar(gin.t[:, 1:2], init_re, st_, None, ALU.mult), init_bufs + [sth], [gin])
        P.op("dve", lambda h: h.scalar_tensor_tensor(gin.t[:, 1:2], init_im, ct, gin.t[:, 1:2], ALU.mult, ALU.add), init_bufs + [cth, gin], [gin])
        rb = rr.t[:, pair:pair + 1].to_broadcast([128, T])
        P.op("dve", lambda h: h.tensor_tensor_scan(gr.t[:, 0:T], rb, yr.t[:, 0:T], gin.t[:, 0:1], ALU.mult, ALU.add), [yr, gin] + als, [gr])
        P.op("dve", lambda h: h.tensor_tensor_scan(gi.t[:, 0:T], rb, yi.t[:, 0:T], gin.t[:, 1:2], ALU.mult, ALU.add), [yi, gin] + als, [gi])
        c_ = tc_.t[:, 0:T]; s_ = ts_.t[:, 0:T]
        P.op("dve", lambda h: h.tensor_tensor(yr.t[:, 0:T], gr.t[:, 0:T], c_, ALU.mult), [gr, tc_], [yr])
        P.op("dve", lambda h: h.tensor_tensor(yi.t[:, 0:T], gi.t[:, 0:T], s_, ALU.mult), [gi, ts_], [yi])
        P.op("dve", lambda h: h.tensor_tensor(hrb.t[:, 0:T], yr.t[:, 0:T], yi.t[:, 0:T], ALU.subtract), [yr, yi], [hrb])
        P.op("dve", lambda h: h.tensor_tensor(hend.t[:, 0:1], yr.t[:, T - 1:T], yi.t[:, T - 1:T], ALU.subtract), [yr, yi], [hend])
        P.op("dve", lambda h: h.tensor_tensor(yr.t[:, 0:T], gr.t[:, 0:T], s_, ALU.mult), [gr, ts_, hrb, hend], [yr])
        P.op("dve", lambda h: h.tensor_tensor(yi.t[:, 0:T], gi.t[:, 0:T], c_, ALU.mult), [gi, tc_, hrb, hend], [yi])
        P.op("dve", lambda h: h.tensor_tensor(hib.t[:, 0:T], yr.t[:, 0:T], yi.t[:, 0:T], ALU.add), [yr, yi], [hib])
        P.op("dve", lambda h: h.tensor_tensor(hend.t[:, 1:2], yr.t[:, T - 1:T], yi.t[:, T - 1:T], ALU.add), [yr, yi], [hend])
        P.op("dve", lambda h: h.tensor_copy(out_b.t[:, 0:2], hend.t[:, 0:2]), [hend], [out_b])
        return dict(pair=pair, qq=qq, T=T, hrb=hrb, hib=hib, after=[])

    ypb = [pf[4], pf[5]]

    def stageY(c):
        pair, qq, T, hrb, hib = c["pair"], c["qq"], c["T"], c["hrb"], c["hib"]
        for h0 in range(0, T, 512):
            n = min(512, T - h0)
            yp_ = ypb[h0 // 512]
            P.mm([lambda h, yp_=yp_, n=n, h0=h0: h.matmul(yp_.t[:, 0:n], CT[0].t[:, pair, :], hrb.t[:, h0:h0 + n], start=(qq == 0), stop=False),
                  lambda h, yp_=yp_, n=n, h0=h0: h.matmul(yp_.t[:, 0:n], CT[1].t[:, pair, :], hib.t[:, h0:h0 + n], start=False, stop=(qq == 3))],
                 [CT[0], CT[1], hrb, hib], [yp_])
        for f in c["after"]:
            f()

    def y_evac(yp_, ft, col0, n):
        P.op("dve", lambda h: h.scalar_tensor_tensor(ytmp.t[:, 0:n], uT.t[:, ft, col0:col0 + n], dsk.t[:, ft:ft + 1], yp_.t[:, 0:n], ALU.mult, ALU.add),
             [uT, dsk, yp_], [ytmp])
        P.op("dve", lambda h: h.tensor_tensor(ysq.t[:, 0:n], ytmp.t[:, 0:n], ytmp.t[:, 0:n], ALU.mult), [ytmp], [ysq])
        P.op("dve", lambda h: h.tensor_scalar(ysq.t[:, 0:n], ysq.t[:, 0:n], 0.044715, 1.0, ALU.mult, ALU.add), [ysq], [ysq])
        P.op("dve", lambda h: h.tensor_tensor(ysq.t[:, 0:n], ysq.t[:, 0:n], ytmp.t[:, 0:n], ALU.mult), [ysq, ytmp], [ysq])
        P.op("act", lambda h: h.activation(ysq.t[:, 0:n], ysq.t[:, 0:n], AF.Sigmoid, scale=GC), [ysq], [ysq])
        P.op("dve", lambda h: h.tensor_tensor(ygT.t[:, ft, col0:col0 + n], ysq.t[:, 0:n], ytmp.t[:, 0:n], ALU.mult), [ysq, ytmp], [ygT])

    pending = [None]

    def push(ctx):
        if pending[0] is not None:
            stageY(pending[0])
        pending[0] = ctx

    for ft in range(4):
        for qq in range(4):
            pair = ft * 4 + qq
            P.op("dve", lambda h, pair=pair: h.tensor_scalar(ang.t[:], iota.t[:], th.t[:, pair:pair + 1], None, ALU.mult), [iota] + als, [ang])
            sincos(ang.t[:], 1024, tabs[qq].t[:], tabc[qq].t[:], [ang], [tabs[qq], tabc[qq]])
        for seg in range(4):
            for qq in range(4):
                pair = ft * 4 + qq
                ctx = stageX(pair, qq, ft, seg * NCH, 1024, tabc[qq], tabs[qq],
                             stp_l[pair].t[:, 0:1], stp_l[pair].t[:, 1:2], [stp_l[pair]], stp_l[pair])
                if qq == 3:
                    ctx["after"] = [(lambda ft=ft, seg=seg, hf=hf: y_evac(ypb[hf], ft, seg * NCH + hf * 512, 512)) for hf in range(2)]
                push(ctx)
        for sb_i in range(4):
            for qq in range(4):
                pair = ft * 4 + qq
                ctx = stageX(pair, qq, ft, sb_i * NCH + 1024, 8, tabc[qq], tabs[qq],
                             sre0.t[:, sb_i, pair:pair + 1], sim0.t[:, sb_i, pair:pair + 1], [sre0, sim0], sts_l[sb_i][pair])
                if qq == 3:
                    ctx["after"] = [(lambda ft=ft, sb_i=sb_i: y_evac(ypb[0], ft, sb_i * NCH + 1024, 8))]
                push(ctx)
    push(None)
    stp2 = P.sb("stp2", [128, 2, 16], F32, at=tq.at); sts2 = P.sb("sts2", [128, 4, 2, 16], F32, at=tq.at + 128)
    for p_ in range(16):
        P.op("dve", lambda h, p_=p_: h.tensor_copy(stp2.t[:, :, p_], stp_l[p_].t[:, 0:2]), [stp_l[p_]], [stp2])
        for b_ in range(4):
            P.op("dve", lambda h, p_=p_, b_=b_: h.tensor_copy(sts2.t[:, b_, :, p_], sts_l[b_][p_].t[:, 0:2]), [sts_l[b_][p_]], [sts2])
    if STOP == 'SSM':
        for r_ in range(4):
            P.dma("pool", OUT["yp"].ap().rearrange("(p f x) c -> p f (x c)", p=128, f=4)[:, :, r_ * 1024:(r_ + 1) * 1024],
                  ygT.t[:, :, r_ * NCH:r_ * NCH + 1024], [ygT], [outbufs["yp"]], ygT)
        P.dma("pool", OUT["ys"].ap()[:, 0:128].rearrange("p (f r s) -> p f r s", f=4, r=4),
              ygT.t[:].rearrange("p f (r c) -> p f r c", r=4)[:, :, :, 1024:1032], [ygT], [outbufs["ys"]], ygT)
    P.dma("sp", OUT["ssm_p"].ap(), stp2.t[:], [stp2], [outbufs["ssm_p"]], stp2)
    P.dma("sp", OUT["ssm_s"].ap(), sts2.t[:], [sts2], [outbufs["ssm_s"]], sts2)
    for j in range(8):
        P.dma("sp", y_src[j].t.ap().rearrange("(ft p) t -> p ft t", p=128), ygT.t[:, :, j * 516:(j + 1) * 516], [ygT], [y_src[j]], ygT)
        P.coll(y_src[j], y_dst[j], GROUPS)

    if STOP == 'SSM':
        P.barrier()
        return
    P.barrier()
    P.off = CONST_END
    y2T = P.sb("y2T", [128, KT, NO], BF16)
    F0 = P.off
    ygo = P.sb("ygo", [128, KT, NCH], BF16)
    hn1o = P.sb("hn1o", [128, KT, NCH], BF16)
    ych = [P.sb("ych%d" % i, [128, 4, NCH], BF16) for i in range(2)]
    wt2 = [P.sb("wt2_%d" % i, [128, KT, 128], BF16) for i in range(4)]
    gl = P.sb("gl", [128, NCH], F32); zz = P.sb("zz", [128, NCH], F32)
    for j in range(8):
        P.dma("sp", hn1o.t[:, 2 * j:2 * j + 2, :], hn_src[j].t.ap().rearrange("(k p) t -> p k t", p=128), [hn_src[j]], [hn1o], hn1o)
    P.op("pool", lambda h: h.memset(y2T.t[:, :, NCH:NO], 0.0), [], [y2T])
    ci = 0
    for rf in range(4):
        for r in range(4):
            yc = ych[ci % 2]; ci += 1
            for hh_ in range(2):
                P.dma("sp", yc.t[:, :, hh_ * 516:(hh_ + 1) * 516],
                      y_dst[2 * r + hh_].t.ap()[rf * 512:(rf + 1) * 512, :].rearrange("(ft p) t -> p ft t", p=128),
                      [y_dst[2 * r + hh_]], [yc], yc)
            dst = ygo.t[:, rf * 4:(rf + 1) * 4, :]
            if r == 0:
                P.op("dve", lambda h, yc=yc, dst=dst: h.tensor_scalar(dst, yc.t[:], sel.t[:, 0:1], None, ALU.mult), [yc, sel], [ygo])
            else:
                P.op("dve", lambda h, yc=yc, dst=dst, r=r: h.scalar_tensor_tensor(dst, yc.t[:], sel.t[:, r:r + 1], dst, ALU.mult, ALU.add), [yc, sel, ygo], [ygo])
    if STOP == 'G1':
        P.barrier()
        return
    for nt in range(16):
        wa = wt2[2 * (nt % 2)]
        wb_ = wt2[2 * (nt % 2) + 1]
        P.dma("pool", wa.t[:], IN["w_glu"].ap()[:, nt * 128:(nt + 1) * 128].rearrange("(kt p) c -> p kt c", p=128), [], [wa], wa)
        P.dma("pool", wb_.t[:], IN["w_in_c"].ap()[:, 2048 + nt * 128:2048 + (nt + 1) * 128].rearrange("(kt p) c -> p kt c", p=128), [], [wb_], wb_)
        for (c0, n) in [(0, 512), (512, 512), (1024, 8)]:
            proj_feat(wa, None, lambda pk, c0=c0, n=n, nt=nt: P.op("act", lambda h: h.activation(gl.t[:, c0:c0 + n], pk.t[:, 0:n], AF.Sigmoid, bias=bglu.t[:, nt:nt + 1], scale=1.0), [pk, bglu], [gl]),
                      ygo, lambda kt, c0=c0, n=n: ygo.t[:, kt, c0:c0 + n], n)
            def zev(pk, c0=c0, n=n):
                P.op("act", lambda h: h.activation(zz.t[:, c0:c0 + n], pk.t[:, 0:n], AF.Sigmoid), [pk], [zz])
                P.op("dve", lambda h: h.tensor_tensor(zz.t[:, c0:c0 + n], zz.t[:, c0:c0 + n], pk.t[:, 0:n], ALU.mult), [zz, pk], [zz])
            proj_feat(wb_, None, zev, hn1o, lambda kt, c0=c0, n=n: hn1o.t[:, kt, c0:c0 + n], n)
        P.op("dve", lambda h, nt=nt: h.tensor_tensor(gl.t[:], gl.t[:], ygo.t[:, nt, :], ALU.mult), [gl, ygo], [gl])
        P.op("dve", lambda h, nt=nt: h.tensor_tensor(y2T.t[:, nt, 0:NCH], gl.t[:], zz.t[:], ALU.mult), [gl, zz], [y2T])
    if STOP == 'G2':
        P.barrier()
        return
    P.barrier()
    P.off = F0
    wgb2 = [P.sb("wgc%d" % i, [128, KT, 512], BF16) for i in range(2)]
    h1s = [P.sb("h1f%d" % i, [128, D], F32) for i in range(5)]
    junk = P.sb("junk3", [128, D], F32); ss = P.sb("ss3", [128, 4], F32)
    yo = [P.sb("yo%d" % i, [128, D], F32) for i in range(2)]
    load_g("g_fin")
    wi = 0
    for tiles in [list(range(0, 5)), list(range(5, 9))]:
        for si, tj in enumerate(tiles):
            o0 = tj * 128
            P.dma("sp", h1s[si].t[:], h1_scr.t.ap()[o0:o0 + 128, :], [h1_scr], [h1s[si]], h1s[si])
        for g4 in range(4):
            wg = wgb2[wi % 2]
            wi += 1
            P.dma("pool", wg.t[:], IN["w_out_c"].ap()[:, g4 * 512:(g4 + 1) * 512].rearrange("(kt p) c -> p kt c", p=128), [], [wg], wg)
            for si, tj in enumerate(tiles):
                o0 = tj * 128
                ht_ = h1s[si]
                pk = next_pf()
                fns = [(lambda h, pk=pk, kt=kt, o0=o0, wg=wg: h.matmul(pk.t[:], y2T.t[:, kt, o0:o0 + 128], wg.t[:, kt, :],
                                                                         start=(kt == 0), stop=(kt == KT - 1))) for kt in range(KT)]
                P.mm(fns, [y2T, wg], [pk])
                P.op("dve", lambda h, pk=pk, ht_=ht_, g4=g4: h.tensor_tensor(ht_.t[:, g4 * 512:(g4 + 1) * 512], ht_.t[:, g4 * 512:(g4 + 1) * 512], pk.t[:], ALU.add),
                     [pk, ht_], [ht_])
        for si, tj in enumerate(tiles):
            o0 = tj * 128
            ht_ = h1s[si]
            yo_ = yo[tj % 2]
            rmsnorm_rows(ht_, yo_, ss, junk)
            if tj < 8:
                P.dma("sp", OUT["yp"].ap()[o0:o0 + 128, :], yo_.t[:], [yo_], [outbufs["yp"]], yo_)
            else:
                P.dma("sp", OUT["ys"].ap(), yo_.t[:], [yo_], [outbufs["ys"]], yo_)
    P.barrier()


_NC_CACHE = {}


def _rope_tables(pos):
    half = 64
    inv = (np.float32(10000.0) ** (-np.arange(half, dtype=np.float32) / np.float32(half))).astype(np.float32)
    ang = pos.astype(np.float32)[:, None] * inv[None, :]
    return np.cos(ang).astype(np.float32), np.sin(ang).astype(np.float32)


def kernel(x_prompt, x_sample, cache_win_k, cache_win_v, state_conv, state_ssm_re, state_ssm_im,
           attn_norm, w_in_ab, conv_w, w_out_ab, ssm_norm, w_in_c, lam_re, lam_im, log_step,
           b_re, b_im, c_re, c_im, d_skip, w_glu, b_glu, w_out_c, final_norm):
    f = lambda a: np.ascontiguousarray(np.asarray(a, dtype=np.float32))
    x_prompt, x_sample = f(x_prompt), f(x_sample)
    cache_win_k, cache_win_v, state_conv = f(cache_win_k), f(cache_win_v), f(state_conv)
    state_ssm_re, state_ssm_im = f(state_ssm_re), f(state_ssm_im)
    w_in_ab0, w_out_ab0, w_in_c0, w_glu0, w_out_c0 = f(w_in_ab)[0], f(w_out_ab)[0], f(w_in_c)[0], f(w_glu)[0], f(w_out_c)[0]
    lam_re, lam_im, log_step = f(lam_re)[0], f(lam_im)[0], f(log_step)[0]
    b_re, b_im, c_re, c_im = f(b_re)[0], f(b_im)[0], f(c_re)[0], f(c_im)[0]
    d_skip0, b_glu0 = f(d_skip)[0], f(b_glu)[0]
    if "nc" not in _NC_CACHE:
        _NC_CACHE["nc"] = build_nc()
    nc = _NC_CACHE["nc"]

    kk = np.arange(128)[:, None]
    qq_ = np.arange(512)[None, :]
    maskp = np.stack([mult_of(qq_ - ((i - 16) * 128 + kk)) for i in range(20)], 1)
    rows = np.arange(2176).reshape(17, 128)
    s_ = np.arange(128)[None, :]
    masks = np.zeros((128, 17, 128), np.float32)
    for i in range(17):
        row = rows[i][:, None]
        m = mult_of(2048 + s_ - row)
        m[:, 8:] = ((2048 + s_[:, 8:] - row) == 0)
        masks[:, i, :] = m
    iota = np.broadcast_to(np.arange(1024, dtype=np.float32)[None, :], (128, 1024)).copy()
    rmask = np.zeros((128, 8), np.float32)
    for p in range(128):
        rmask[p, (p // 32) * 2 + (p % 32) // 16] = 1.0
    smask = np.zeros((128, 2), np.float32)
    smask[:64, 0] = 1.0
    smask[64:, 1] = 1.0
    bc = lambda v: np.ascontiguousarray(np.broadcast_to(v[None, :], (128, v.shape[0])))

    in_maps = []
    for c in range(8):
        b, r = c // 4, c % 4
        T0 = r * NOWN
        xh = np.zeros((NTP, D), np.float32)
        lo = T0 - NHALO
        src_lo = max(lo, 0)
        xh[src_lo - lo:] = x_prompt[b, src_lo:T0 + NOWN]
        pos = np.concatenate([np.arange(lo, T0 + NOWN), PAST + np.arange(128)]).astype(np.float32)
        valid = (pos[:NTP] >= 0).astype(np.float32)
        cosv, sinv = _rope_tables(np.maximum(pos, 0))
        xs = np.zeros((128, D), np.float32)
        xs[:8] = x_sample[c]
        g0 = 32 * r
        gs = slice(g0, g0 + 32)
        st_lay = lambda a: np.ascontiguousarray(a.reshape(16, 2, 64).transpose(1, 2, 0).reshape(128, 16))
        def row_lay_rep(a):
            t = a.reshape(4, 4, 2, 64)
            t = np.broadcast_to(t[:, :, :, None, :], (4, 4, 2, 16, 64))
            return np.ascontiguousarray(t.transpose(1, 2, 3, 0, 4).reshape(128, 4, 64))
        def row_lay_b(a):
            t = a.reshape(4, 4, 2, 64, 16)
            return np.ascontiguousarray(t.transpose(1, 2, 4, 0, 3).reshape(128, 4, 64))
        def st_lay_c(a):
            t = a.reshape(16, 2, 16, 64)
            return np.ascontiguousarray(t.transpose(1, 3, 0, 2).reshape(128, 16, 16))
        lst32 = np.broadcast_to(log_step[gs][:, None], (32, 64))
        sel = np.zeros((128, 4), np.float32)
        sel[:, r] = 1.0
        sre0 = np.stack([st_lay(state_ssm_re[0, 4 * b + i, gs]) for i in range(4)], 1)
        sim0 = np.stack([st_lay(state_ssm_im[0, 4 * b + i, gs]) for i in range(4)], 1)
        w_in_c_rolled = np.concatenate([w_in_c0[:, 512 * r:512 * (r + 1)], w_in_c0[:, 512:2048], w_in_c0[:, 2048:]], 1)
        m = {
            "xh": xh, "xs": xs,
            "cs": np.ascontiguousarray(cosv.reshape(25, 128, 64).transpose(1, 0, 2)),
            "sn": np.ascontiguousarray(sinv.reshape(25, 128, 64).transpose(1, 0, 2)),
            "valid": np.ascontiguousarray(valid.reshape(24, 128).T),
            "ck": np.ascontiguousarray(cache_win_k[0, c].reshape(2048, 1024)),
            "cv": np.ascontiguousarray(cache_win_v[0, c].reshape(2048, 1024)),
            "sconv": np.ascontiguousarray(state_conv[0, c].reshape(2, 8, 128).transpose(2, 1, 0)),
            "g_attn": bc(f(attn_norm)[0]), "g_ssm": bc(f(ssm_norm)[0]), "g_fin": bc(f(final_norm)),
            "w_in_ab": w_in_ab0, "cw": np.ascontiguousarray(f(conv_w)[0].reshape(3, 8, 128).transpose(2, 1, 0)),
            "w_out_ab": w_out_ab0, "w_in_c": np.ascontiguousarray(w_in_c_rolled),
            "w_glu": w_glu0, "w_out_c": w_out_c0,
            "bglu": np.ascontiguousarray(b_glu0.reshape(16, 128).T),
            "dsk": np.ascontiguousarray(d_skip0[512 * r:512 * (r + 1)].reshape(4, 128).T),
            "maskp": maskp, "masks": masks,
            "lre_s": st_lay(lam_re[gs]), "lim_s": st_lay(lam_im[gs]), "lst_s": st_lay(lst32),
            "lre_r": row_lay_rep(lam_re[gs]), "lim_r": row_lay_rep(lam_im[gs]), "lst_r": row_lay_rep(np.ascontiguousarray(lst32)),
            "bre_r": row_lay_b(b_re[gs]), "bim_r": row_lay_b(b_im[gs]),
            "cre_s": st_lay_c(c_re[gs]), "cim_s": st_lay_c(c_im[gs]),
            "rmask": rmask, "smask": smask, "sel": sel, "sre0": sre0, "sim0": sim0, "iota": iota,
        }
        in_maps.append({k: np.ascontiguousarray(v, dtype=np.float32) for k, v in m.items()})

    res = run_bass_kernel_spmd(nc, in_maps, core_ids=list(range(8)))
    R = res.results
    _NC_CACHE['raw'] = R
    y_prompt = np.zeros((2, SEQ, D), np.float32)
    y_sample = np.zeros((8, 8, D), np.float32)
    kp = np.zeros((1, 2, 2048, 8, 128), np.float32)
    vp = np.zeros((1, 2, 2048, 8, 128), np.float32)
    convp = np.zeros((1, 2, 2, 1024), np.float32)
    srp = np.zeros((1, 2, 128, 64), np.float32)
    sip = np.zeros((1, 2, 128, 64), np.float32)
    ks = np.zeros((1, 8, 8, 8, 128), np.float32)
    vs = np.zeros((1, 8, 8, 8, 128), np.float32)
    convs = np.zeros((1, 8, 2, 1024), np.float32)
    srs = np.zeros((1, 8, 128, 64), np.float32)
    sis = np.zeros((1, 8, 128, 64), np.float32)
    unst = lambda a: a.reshape(2, 64, 16).transpose(2, 0, 1).reshape(32, 64)
    for c in range(8):
        b, r = c // 4, c % 4
        o = R[c]
        y_prompt[b, r * NOWN:(r + 1) * NOWN] = o["yp"]
        y_sample[c] = o["ys"][:8]
        if r >= 2:
            kp[0, b, (r - 2) * NOWN:(r - 1) * NOWN] = o["kp"].reshape(NOWN, 8, 128)
            vp[0, b, (r - 2) * NOWN:(r - 1) * NOWN] = o["vp"].reshape(NOWN, 8, 128)
        if r == 3:
            convp[0, b] = o["convp"].transpose(2, 1, 0).reshape(2, 1024)
        ks[0, c] = o["ks"][:8].reshape(8, 8, 128)
        vs[0, c] = o["vs"][:8].reshape(8, 8, 128)
        convs[0, c] = o["convs"].transpose(2, 1, 0).reshape(2, 1024)
        srp[0, b, 32 * r:32 * (r + 1)] = unst(o["ssm_p"][:, 0, :])
        sip[0, b, 32 * r:32 * (r + 1)] = unst(o["ssm_p"][:, 1, :])
        for i in range(4):
            srs[0, 4 * b + i, 32 * r:32 * (r + 1)] = unst(o["ssm_s"][:, i, 0, :])
            sis[0, 4 * b + i, 32 * r:32 * (r + 1)] = unst(o["ssm_s"][:, i, 1, :])
    return (y_prompt, y_sample, kp, vp, convp, srp, sip, ks, vs, convs, srs, sis)
```

```python
import math
import os
STOP = os.environ.get('MK_STOP', '')
from contextlib import ExitStack

import numpy as np
import concourse.bass as bass
import concourse.mybir as mybir
from concourse.bass_utils import run_bass_kernel_spmd

F32 = mybir.dt.float32
BF16 = mybir.dt.bfloat16
ALU = mybir.AluOpType
AF = mybir.ActivationFunctionType
AX = mybir.AxisListType

ENGS = ["pe", "act", "dve", "pool", "sp"]
D = 2048
KT = 16
NOWN = 1024
NHALO = 2048
NTP = NOWN + NHALO
NTILE_P = NTP // 128
NO = NOWN + 128
SEQ = 4096
PAST = 16384
NCH = 1032
TWO_PI = 2.0 * math.pi


class Buf:
    def __init__(self, t, name):
        self.t = t
        self.name = name
        self.w = {}
        self.r = {}
        self.dsem = None
        self.dcnt = 0


class Prog:
    def __init__(self, nc, stack):
        self.nc = nc
        self.stack = stack
        self.q = {e: [] for e in ENGS}
        self.cnt = {e: 0 for e in ENGS}
        self.seen = {e: {} for e in ENGS}
        self.sems = {}
        self.semval = {}
        for e in ["pe", "act", "dve", "pool"]:
            self.sems[e] = stack.enter_context(nc.semaphore("s_" + e))
        self.off = 16512
        self.free = []
        self.dval = {}
        self.phase_bufs = []

    def sb(self, name, shape, dt, at=None):
        nbytes = int(np.prod(shape[1:])) * (2 if dt == BF16 else 4)
        if at is None:
            at = self.off
            self.off = (at + nbytes + 63) // 64 * 64
        assert at + nbytes <= 229300, (name, at, nbytes)
        t = self.nc.alloc_sbuf_tensor_at(name, list(shape), dt, offset=at)
        b = Buf(t, name)
        b.at = at
        b.nbytes = nbytes
        return b

    def ps(self, name, shape, dt=F32):
        t = self.stack.enter_context(self.nc.psum_tensor(name, list(shape), dt))
        return Buf(t, name)

    def dram(self, name, shape, dt, kind="Internal"):
        t = self.nc.dram_tensor(name, list(shape), dt, kind=kind)
        return Buf(t, name)

    def _need(self, eng, k, v, waits):
        if self.seen[eng].get(k, 0) >= v:
            return
        waits[k] = max(waits.get(k, 0), v)

    def _deps(self, eng, reads, writes):
        waits = {}
        for b in reads:
            for k, v in b.w.items():
                self._need(eng, k, v, waits)
        for b in writes:
            for k, v in b.w.items():
                self._need(eng, k, v, waits)
            for k, v in b.r.items():
                self._need(eng, k, v, waits)
        for k, v in waits.items():
            self.seen[eng][k] = v
        return [(self.sems[k], v) for k, v in waits.items()]

    def _commit(self, k, v, reads, writes):
        self.semval[k] = v
        for b in reads:
            b.r[k] = max(b.r.get(k, 0), v)
        for b in writes:
            b.w[k] = max(b.w.get(k, 0), v)
            b.r = {}

    def op(self, eng, fn, reads=(), writes=()):
        reads = [b for b in reads if b is not None]
        writes = [b for b in writes if b is not None]
        wl = self._deps(eng, reads, writes)
        self.cnt[eng] += 1
        sem = self.sems[eng]

        def emit(h, fn=fn, wl=wl, sem=sem):
            for s, v in wl:
                h.wait_ge(s, v)
            fn(h).then_inc(sem, 1)

        self.q[eng].append(emit)
        self._commit(eng, self.cnt[eng], reads, writes)

    def mm(self, fns, reads, writes):
        eng = "pe"
        wl = self._deps(eng, reads, writes)
        self.cnt[eng] += 1
        sem = self.sems[eng]

        def emit(h, fns=fns, wl=wl, sem=sem):
            for s, v in wl:
                h.wait_ge(s, v)
            for f in fns[:-1]:
                f(h)
            fns[-1](h).then_inc(sem, 1)

        self.q[eng].append(emit)
        self._commit(eng, self.cnt[eng], reads, writes)

    def dma(self, eng, out, in_, reads, writes, semb, **kw):
        reads = [b for b in reads if b is not None]
        writes = [b for b in writes if b is not None]
        if semb.dsem is None:
            if self.free:
                key = self.free.pop()
            else:
                key = "d%d" % len(self.sems)
                self.sems[key] = self.stack.enter_context(self.nc.semaphore(key))
            semb.dsem = key
            semb.dcnt = self.dval.get(key, 0)
            self.phase_bufs.append(semb)
        wl = self._deps(eng, reads, writes)
        semb.dcnt += 16
        self.dval[semb.dsem] = semb.dcnt
        sem = self.sems[semb.dsem]

        def emit(h, wl=wl, sem=sem, out=out, in_=in_, kw=kw):
            for s, v in wl:
                h.wait_ge(s, v)
            h.dma_start(out=out, in_=in_, **kw).then_inc(sem, 16)

        self.q[eng].append(emit)
        self._commit(semb.dsem, semb.dcnt, reads, writes)

    def coll(self, src, dst, groups):
        key = "c%d" % len(self.sems)
        self.sems[key] = self.stack.enter_context(self.nc.semaphore(key))
        wl = self._deps("pool", [src], [dst])
        sem = self.sems[key]

        def emit(h, wl=wl, sem=sem):
            for s, v in wl:
                h.wait_ge(s, v)
            h.collective_compute("AllGather", ALU.bypass, replica_groups=groups,
                                 ins=[src.t.ap()], outs=[dst.t.ap()]).then_inc(sem)

        self.q["pool"].append(emit)
        self._commit(key, 1, [src], [dst])

    def barrier(self):
        for b in self.phase_bufs:
            self.free.append(b.dsem)
            b.dsem = None
        self.phase_bufs = []
        items = list(self.semval.items())
        for e in ENGS:
            wl = []
            for k, v in items:
                if self.seen[e].get(k, 0) < v:
                    self.seen[e][k] = v
                    wl.append((self.sems[k], v))

            def emit(h, wl=wl):
                for s, v in wl:
                    h.wait_ge(s, v)

            if wl:
                self.q[e].append(emit)

    def run(self):
        nc = self.nc
        with nc.Block() as block:
            @block.tensor
            def _(h):
                for f in self.q["pe"]:
                    f(h)

            @block.scalar
            def _(h):
                for f in self.q["act"]:
                    f(h)

            @block.vector
            def _(h):
                for f in self.q["dve"]:
                    f(h)

            @block.gpsimd
            def _(h):
                for f in self.q["pool"]:
                    f(h)

            @block.sync
            def _(h):
                for f in self.q["sp"]:
                    f(h)


def mult_of(d):
    d = np.asarray(d)
    m = ((d >= 0) & (d <= 128)).astype(np.float32)
    m += ((d >= 0) & (d <= 512) & (d % 4 == 0))
    m += ((d >= 0) & (d <= 2048) & (d % 16 == 0))
    return m.astype(np.float32)


IN_SPECS = [
    ("xh", [NTP, D]), ("xs", [128, D]), ("cs", [128, 25, 64]), ("sn", [128, 25, 64]),
    ("valid", [128, 24]), ("ck", [2048, 1024]), ("cv", [2048, 1024]), ("sconv", [128, 8, 2]),
    ("g_attn", [128, D]), ("g_ssm", [128, D]), ("g_fin", [128, D]),
    ("w_in_ab", [D, 8192]), ("cw", [128, 8, 3]), ("w_out_ab", [D, D]), ("w_in_c", [D, 4096]),
    ("w_glu", [D, D]), ("w_out_c", [D, D]), ("bglu", [128, 16]), ("dsk", [128, 4]),
    ("maskp", [128, 20, 512]), ("masks", [128, 17, 128]),
    ("lre_s", [128, 16]), ("lim_s", [128, 16]), ("lst_s", [128, 16]),
    ("lre_r", [128, 4, 64]), ("lim_r", [128, 4, 64]), ("lst_r", [128, 4, 64]),
    ("bre_r", [128, 4, 64]), ("bim_r", [128, 4, 64]),
    ("cre_s", [128, 16, 16]), ("cim_s", [128, 16, 16]),
    ("rmask", [128, 8]), ("smask", [128, 2]), ("sel", [128, 4]),
    ("sre0", [128, 4, 16]), ("sim0", [128, 4, 16]), ("iota", [128, 1024]),
]
OUT_SPECS = [
    ("yp", [NOWN, D]), ("ys", [128, D]), ("kp", [NOWN, 1024]), ("vp", [NOWN, 1024]),
    ("convp", [128, 8, 2]), ("ssm_p", [128, 2, 16]), ("ks", [128, 1024]), ("vs", [128, 1024]),
    ("convs", [128, 8, 2]), ("ssm_s", [128, 4, 2, 16]),
]


def build_nc():
    nc = bass.Bass("TRN2", target_bir_lowering=False)
    IN = {}
    for n, s in IN_SPECS:
        IN[n] = nc.dram_tensor(n, s, F32, kind="ExternalInput")
    OUT = {}
    for n, s in OUT_SPECS:
        OUT[n] = nc.dram_tensor(n, s, F32, kind="ExternalOutput")
    st = ExitStack()
    with st:
        P = Prog(nc, st)
        build_program(nc, P, IN, OUT)
        P.run()
    return nc


def build_program(nc, P, IN, OUT):
    GROUPS = [[0, 1, 2, 3], [4, 5, 6, 7]]
    outbufs = {n: Buf(OUT[n], n) for n in OUT}
    kT_scr = P.dram("kT_scr", [8, 128, NTP], BF16)
    v_scr = P.dram("v_scr", [NTP, 1024], BF16)
    kTs_scr = P.dram("kTs_scr", [8, 128, 2176], BF16)
    vs_scr = P.dram("vs_scr", [2176, 1024], BF16)
    qT_scr = P.dram("qT_scr", [8, 128, NO], BF16)
    h1_scr = P.dram("h1_scr", [NO, D], F32)
    hn_src = [P.dram("hn_src%d" % j, [256, NCH], BF16) for j in range(8)]
    hn_dst = [P.dram("hn_dst%d" % j, [4 * 256, NCH], BF16) for j in range(8)]
    y_src = [P.dram("y_src%d" % j, [512, 516], BF16) for j in range(8)]
    y_dst = [P.dram("y_dst%d" % j, [4 * 512, 516], BF16) for j in range(8)]

    pf = [P.ps("pf%d" % i, [128, 512], F32) for i in range(6)]
    pb = [P.ps("pb%d" % i, [128, 8, 128], BF16) for i in range(2)]
    pfi = [0]
    pbi = [0]

    def next_pf():
        pfi[0] = (pfi[0] + 1) % 4
        return pf[pfi[0]]

    def next_pb():
        pbi[0] = (pbi[0] + 1) % 2
        return pb[pbi[0]]

    ident = P.sb("ident", [128, 128], BF16)
    P.op("pool", lambda h: h.memset(ident.t[:], 1.0), [], [ident])
    P.op("pool", lambda h: h.affine_select(ident.t[:], ident.t[:], [[-1, 128]], ALU.is_equal, 0.0,
                                            base=0, channel_multiplier=1), [ident], [ident])
    ones_bf = P.sb("ones_bf", [128, 128], BF16)
    P.op("pool", lambda h: h.memset(ones_bf.t[:], 1.0), [], [ones_bf])
    gt = P.sb("gt", [128, D], F32)
    cs = P.sb("cs", [128, 25, 64], F32)
    sn = P.sb("sn", [128, 25, 64], F32)
    valid = P.sb("valid", [128, 24], F32)
    validB = P.sb("validB", [128, 24, 128], BF16)
    cw = P.sb("cw", [128, 8, 3], F32)
    bglu = P.sb("bglu", [128, 16], F32)
    dsk = P.sb("dsk", [128, 4], F32)
    sel = P.sb("sel", [128, 4], F32)
    eps_t = P.sb("eps_t", [128, 1], F32)
    P.op("pool", lambda h: h.memset(eps_t.t[:], 1e-6), [], [eps_t])
    for b_, n in [(cs, "cs"), (sn, "sn"), (valid, "valid"), (cw, "cw"), (bglu, "bglu"), (dsk, "dsk"), (sel, "sel")]:
        P.dma("sp", b_.t[:], IN[n].ap(), [], [b_], b_)
    P.op("dve", lambda h: h.tensor_copy(validB.t[:], valid.t[:].unsqueeze(2).to_broadcast([128, 24, 128])),
         [valid], [validB])
    pospi = P.sb("pospi", [128, 1], F32)
    P.op("pool", lambda h: h.memset(pospi.t[:], math.pi), [], [pospi])
    CONST_END = P.off
    hnT_o = P.sb("hnT_o", [128, KT, NO], BF16)

    def load_g(name):
        P.dma("sp", gt.t[:], IN[name].ap(), [], [gt], gt)

    def rmsnorm_rows(xt, xn, ss, junk):
        P.op("act", lambda h: h.activation(junk.t[:], xt.t[:], AF.Square, accum_out=ss.t[:, 0:1]), [xt], [junk, ss])
        P.op("act", lambda h: h.activation(ss.t[:, 1:2], ss.t[:, 0:1], AF.Sqrt, bias=eps_t.t[:, 0:1], scale=1.0 / D), [ss, eps_t], [ss])
        P.op("dve", lambda h: h.reciprocal(ss.t[:, 2:3], ss.t[:, 1:2]), [ss], [ss])
        P.op("dve", lambda h: h.scalar_tensor_tensor(xn.t[:], xt.t[:], ss.t[:, 2:3], gt.t[:], ALU.mult, ALU.mult),
             [xt, ss, gt], [xn])

    def transpose_rows(xn, dst, dst_ap_fn):
        for half in range(2):
            p = next_pb()
            fns = []
            for j in range(8):
                kt = half * 8 + j
                fns.append(lambda h, p=p, j=j, kt=kt: h.transpose(p.t[:, j, :], xn.t[:, kt * 128:(kt + 1) * 128], ident.t[:]))
            P.mm(fns, [xn, ident], [p])
            P.op("act", lambda h, p=p, half=half: h.activation(dst_ap_fn(half), p.t[:], AF.Identity), [p], [dst])

    A0 = P.off
    wkv = P.sb("wkv", [128, KT, 2048], BF16)
    xts = [P.sb("xt%d" % i, [128, D], F32) for i in range(3)]
    xns = [P.sb("xn%d" % i, [128, D], BF16) for i in range(3)]
    hts = [P.sb("ht%d" % i, [128, KT, 128], BF16) for i in range(3)]
    ss = P.sb("ss", [128, 4], F32)
    krs = [P.sb("kr%d" % i, [128, 1024], F32) for i in range(2)]
    vfs = [P.sb("vf%d" % i, [128, 1024], F32) for i in range(2)]
    t1 = P.sb("t1", [128, 256], F32)
    t2 = P.sb("t2", [128, 256], F32)
    krbs = [P.sb("krb%d" % i, [128, 1024], BF16) for i in range(2)]
    vbs = [P.sb("vb%d" % i, [128, 1024], BF16) for i in range(2)]
    kTts = [P.sb("kTt%d" % i, [128, 8, 128], BF16) for i in range(2)]
    kTt = kTts[0]
    hprev2 = P.sb("hprev2", [128, KT, 2], BF16)
    A1_END = P.off

    load_g("g_attn")
    for half in range(2):
        P.dma("pool", wkv.t[:, :, half * 1024:(half + 1) * 1024],
              IN["w_in_ab"].ap()[:, 1024 + half * 1024:2048 + half * 1024].rearrange("(kt p) c -> p kt c", p=128),
              [], [wkv], wkv)

    def rotary(pk, ti, dst, c0):
        v = pk.t[:].rearrange("p (h two d) -> p h two d", h=4, two=2)
        o = dst.t[:, c0:c0 + 512].rearrange("p (h two d) -> p h two d", h=4, two=2)
        cb = cs.t[:, ti, :].unsqueeze(1).to_broadcast([128, 4, 64])
        sb_ = sn.t[:, ti, :].unsqueeze(1).to_broadcast([128, 4, 64])
        a = t1.t[:].rearrange("p (h d) -> p h d", h=4)
        b = t2.t[:].rearrange("p (h d) -> p h d", h=4)
        P.op("dve", lambda h: h.tensor_tensor(a, v[:, :, 0, :], cb, ALU.mult), [pk, cs], [t1])
        P.op("dve", lambda h: h.tensor_tensor(b, v[:, :, 1, :], sb_, ALU.mult), [pk, sn], [t2])
        P.op("dve", lambda h: h.tensor_tensor(o[:, :, 0, :], a, b, ALU.subtract), [t1, t2], [dst])
        P.op("dve", lambda h: h.tensor_tensor(a, v[:, :, 1, :], cb, ALU.mult), [pk, cs], [t1])
        P.op("dve", lambda h: h.tensor_tensor(b, v[:, :, 0, :], sb_, ALU.mult), [pk, sn], [t2])
        P.op("dve", lambda h: h.tensor_tensor(o[:, :, 1, :], a, b, ALU.add), [t1, t2], [dst])

    def store_kT(src_bf, scr, col0, kb=None):
        p = next_pb()
        if kb is None:
            kb = kTt
        fns = [(lambda h, p=p, j=j: h.transpose(p.t[:, j, :], src_bf.t[:, j * 128:(j + 1) * 128], ident.t[:])) for j in range(8)]
        P.mm(fns, [src_bf, ident], [p])
        P.op("act", lambda h, p=p, kb=kb: h.activation(kb.t[:], p.t[:], AF.Identity), [p], [kb])
        P.dma("sp", scr.t.ap()[:, :, col0:col0 + 128].rearrange("h d t -> d h t"), kb.t[:], [kb], [scr], kb)

    def stageL(ti):
        xt = xts[ti % 3]
        src = IN["xh"].ap()[ti * 128:(ti + 1) * 128, :] if ti < 24 else IN["xs"].ap()
        P.dma("sp", xt.t[:], src, [], [xt], xt)

    def stageN(ti):
        rmsnorm_rows(xts[ti % 3], xns[ti % 3], ss, xns[ti % 3])

    def stageA2(ti):
        xn = xns[ti % 3]
        if ti < 16:
            ht = hts[ti % 3]
            transpose_rows(xn, ht, lambda half, ht=ht: ht.t[:, half * 8:(half + 1) * 8, :])
            if ti == 15:
                P.op("dve", lambda h, ht=ht: h.tensor_copy(hprev2.t[:], ht.t[:, :, 126:128]), [ht], [hprev2])
            return (lambda kt, ht=ht: ht.t[:, kt, :]), ht
        o0 = (ti - 16) * 128
        transpose_rows(xn, hnT_o, lambda half, o0=o0: hnT_o.t[:, half * 8:(half + 1) * 8, o0:o0 + 128])
        return (lambda kt, o0=o0: hnT_o.t[:, kt, o0:o0 + 128]), hnT_o

    def stageM(ti, lhs, hb):
        kr = krs[ti % 2]
        vf = vfs[ti % 2]
        for g4 in range(4):
            pk = next_pf()
            fns = [(lambda h, pk=pk, kt=kt, g4=g4, lhs=lhs: h.matmul(pk.t[:], lhs(kt), wkv.t[:, kt, g4 * 512:(g4 + 1) * 512],
                                                                    start=(kt == 0), stop=(kt == KT - 1))) for kt in range(KT)]
            P.mm(fns, [hb, wkv], [pk])
            if g4 < 2:
                rotary(pk, ti, kr, g4 * 512)
            else:
                c0 = (g4 - 2) * 512
                P.op("act", lambda h, pk=pk, c0=c0, vf=vf: h.activation(vf.t[:, c0:c0 + 512], pk.t[:], AF.Identity), [pk], [vf])

    def stageKpre(ti):
        kr = krs[ti % 2]
        vf = vfs[ti % 2]
        krb = krbs[ti % 2]
        vb = vbs[ti % 2]
        P.op("act", lambda h: h.activation(krb.t[:], kr.t[:], AF.Identity), [kr], [krb])
        if ti < 24:
            P.op("dve", lambda h: h.tensor_scalar(vb.t[:], vf.t[:], valid.t[:, ti:ti + 1], None, ALU.mult), [vf, valid], [vb])
        else:
            P.op("dve", lambda h: h.tensor_copy(vb.t[:], vf.t[:]), [vf], [vb])

    def stageKpost(ti):
        kr = krs[ti % 2]
        vf = vfs[ti % 2]
        krb = krbs[ti % 2]
        vb = vbs[ti % 2]
        kb = kTts[ti % 2]
        if ti < 24:
            store_kT(krb, kT_scr, ti * 128, kb)
            P.dma("sp", v_scr.t.ap()[ti * 128:(ti + 1) * 128, :], vb.t[:], [vb], [v_scr], vb)
            if ti >= 16:
                r0 = (ti - 16) * 128
                P.dma("sp", OUT["kp"].ap()[r0:r0 + 128, :], kr.t[:], [kr], [outbufs["kp"]], kr)
                P.dma("sp", OUT["vp"].ap()[r0:r0 + 128, :], vf.t[:], [vf], [outbufs["vp"]], vf)
        else:
            store_kT(krb, kTs_scr, 2048, kb)
            P.dma("sp", vs_scr.t.ap()[2048:2176, :], vb.t[:], [vb], [vs_scr], vb)
            P.dma("sp", OUT["ks"].ap(), kr.t[:], [kr], [outbufs["ks"]], kr)
            P.dma("sp", OUT["vs"].ap(), vf.t[:], [vf], [outbufs["vs"]], vf)

    for t_ in range(3):
        stageL(t_)
    stageN(0)
    stageN(1)
    infoA = {0: stageA2(0)}
    for ti in range(25):
        if ti + 3 < 25:
            stageL(ti + 3)
        if ti + 2 < 25:
            stageN(ti + 2)
        if ti >= 1:
            stageKpre(ti - 1)
        if ti + 1 < 25:
            infoA[ti + 1] = stageA2(ti + 1)
        stageM(ti, *infoA[ti])
        if ti >= 1:
            stageKpost(ti - 1)
    stageKpre(24)
    stageKpost(24)
    def cacheL(ti):
        xt = xts[ti % 3]
        P.dma("sp", xt.t[:, 0:1024], IN["ck"].ap()[ti * 128:(ti + 1) * 128, :], [], [xt], xt)
        P.dma("sp", xt.t[:, 1024:2048], IN["cv"].ap()[ti * 128:(ti + 1) * 128, :], [], [xt], xt)

    cacheL(0)
    cacheL(1)
    for ti in range(16):
        if ti + 2 < 16:
            cacheL(ti + 2)
        xt = xts[ti % 3]
        krb = krbs[ti % 2]
        vb = vbs[ti % 2]
        P.op("act", lambda h, xt=xt, krb=krb: h.activation(krb.t[:], xt.t[:, 0:1024], AF.Identity), [xt], [krb])
        P.op("dve", lambda h, xt=xt, vb=vb: h.tensor_copy(vb.t[:], xt.t[:, 1024:2048]), [xt], [vb])
        store_kT(krb, kTs_scr, ti * 128, kTts[ti % 2])
        P.dma("sp", vs_scr.t.ap()[ti * 128:(ti + 1) * 128, :], vb.t[:], [vb], [vs_scr], vb)

    if STOP == 'A1':
        P.barrier()
        return
    P.barrier()
    P.off = A0
    wq = P.sb("wq", [128, KT, 1024], BF16)
    hprev2b = P.sb("hprev2b", [128, KT, 2], BF16)
    qf = P.sb("qf", [128, 1024], F32)
    qb = P.sb("qb", [128, 1024], BF16)
    t1 = P.sb("t1b", [128, 256], F32)
    t2 = P.sb("t2b", [128, 256], F32)
    kTt = P.sb("kTtb", [128, 8, 128], BF16)
    hprev2k = P.sb("hprev2k", [128, KT, 2], BF16, at=hprev2.at)
    hprev2k.w = dict(hprev2.w)
    P.dma("pool", wq.t[:], IN["w_in_ab"].ap()[:, 0:1024].rearrange("(kt p) c -> p kt c", p=128), [], [wq], wq)
    for tj in range(9):
        ti = 16 + tj
        o0 = tj * 128
        for g2_ in range(2):
            pk = next_pf()
            fns = [(lambda h, pk=pk, kt=kt, g2_=g2_, o0=o0: h.matmul(pk.t[:], hnT_o.t[:, kt, o0:o0 + 128],
                                                                      wq.t[:, kt, g2_ * 512:(g2_ + 1) * 512],
                                                                      start=(kt == 0), stop=(kt == KT - 1))) for kt in range(KT)]
            P.mm(fns, [hnT_o, wq], [pk])
            rotary(pk, ti, qf, g2_ * 512)
        P.op("act", lambda h: h.activation(qb.t[:], qf.t[:], AF.Identity), [qf], [qb])
        store_kT(qb, qT_scr, o0)

    if STOP == 'A2':
        P.barrier()
        return
    P.barrier()
    P.off = A0
    ocat = P.sb("ocat", [128, KT, NO], BF16)
    hp2 = P.sb("hp2", [128, KT, 2], BF16)
    B0 = P.off
    P.op("dve", lambda h: h.tensor_copy(hp2.t[:], hprev2k.t[:]), [hprev2k], [hp2])
    P.barrier()
    maskp = P.sb("maskp", [128, 20, 512], BF16)
    masks_ = P.sb("masks_", [128, 17, 128], BF16)
    P.dma("pool", maskp.t[:], IN["maskp"].ap(), [], [maskp], maskp)
    P.dma("pool", masks_.t[:], IN["masks"].ap(), [], [masks_], masks_)
    kTh = [P.sb("kTh%d" % i, [128, NTP], BF16) for i in range(1)] * 2
    vh = [P.sb("vh%d" % i, [128, 24, 128], BF16) for i in range(1)] * 2
    kTsh = [P.sb("kTsh%d" % i, [128, 2176], BF16) for i in range(1)] * 2
    vsh = [P.sb("vsh%d" % i, [128, 17, 128], BF16) for i in range(1)] * 2
    qTh = [P.sb("qTh%d" % i, [128, NO], BF16) for i in range(1)] * 2
    wt = [P.sb("wt%d" % i, [128, KT, 128], BF16) for i in range(4)]
    pts = [P.sb("pt%d" % i, [128, 512], BF16) for i in range(4)]
    ptm = [P.sb("ptm%d" % i, [128, 512], BF16) for i in range(4)]
    za = P.sb("za", [128, NO], F32)
    rl = P.sb("rl", [128, 512], F32)
    of = P.sb("of", [128, 512], F32)
    sg = P.sb("sg", [128, 512], F32)

    def silu_evac(pk, dstb, dst_ap, n):
        P.op("act", lambda h: h.activation(sg.t[:, 0:n], pk.t[:, 0:n], AF.Exp, scale=-1.0), [pk], [sg])
        P.op("dve", lambda h: h.tensor_scalar(sg.t[:, 0:n], sg.t[:, 0:n], 1.0, None, ALU.add), [sg], [sg])
        P.op("dve", lambda h: h.reciprocal(sg.t[:, 0:n], sg.t[:, 0:n]), [sg], [sg])
        P.op("dve", lambda h: h.tensor_tensor(dst_ap, pk.t[:, 0:n], sg.t[:, 0:n], ALU.mult), [pk, sg], [dstb])
    fb = [P.sb("fb%d" % i, [128, NO + 2], F32) for i in range(4)]
    convo_p = P.sb("convo_p", [128, 8, 2], F32)
    convo_s = P.sb("convo_s", [128, 8, 2], F32)
    sconv = P.sb("sconv", [128, 8, 2], F32)
    P.dma("sp", sconv.t[:], IN["sconv"].ap(), [], [sconv], sconv)
    scale = 128.0 ** -0.5

    def load_wt(i, c0):
        P.dma("pool", wt[i].t[:], IN["w_in_ab"].ap()[:, c0:c0 + 128].rearrange("(kt p) c -> p kt c", p=128), [], [wt[i]], wt[i])

    def proj_feat(wb, dst_ap_fn, evac, rhs_buf, rhs_fn, n):
        pk = next_pf()
        fns = [(lambda h, pk=pk, kt=kt: h.matmul(pk.t[:, 0:n], wb.t[:, kt, :], rhs_fn(kt), start=(kt == 0), stop=(kt == KT - 1)))
               for kt in range(KT)]
        P.mm(fns, [wb, rhs_buf], [pk])
        evac(pk)

    def attention(hh, qT, q0, nq, kT, vt, ktiles, mask_fn, vB_fn, o_dst_fn, zcol0):
        po = pf[4]
        pl = pf[5]
        nk = len(ktiles)
        LA = 3
        pms = {}

        def issue_S(i):
            kt_ = ktiles[i]
            ps_ = next_pf()
            P.mm([lambda h, ps_=ps_, kt_=kt_: h.matmul(ps_.t[:, 0:nq], kT.t[:, kt_ * 128:(kt_ + 1) * 128], qT.t[:, q0:q0 + nq],
                                                        start=True, stop=True)], [kT, qT], [ps_])
            pe_ = pts[i % 4]
            pm_ = ptm[i % 4]
            P.op("act", lambda h, ps_=ps_, pe_=pe_: h.activation(pe_.t[:, 0:nq], ps_.t[:, 0:nq], AF.Exp, scale=scale), [ps_], [pe_])
            mk, mb = mask_fn(i)
            eng = "dve"
            P.op(eng, lambda h, pe_=pe_, pm_=pm_, mk=mk: h.tensor_tensor(pm_.t[:, 0:nq], pe_.t[:, 0:nq], mk, ALU.mult), [pe_, mb], [pm_])
            pms[i] = pm_

        def issue_PV(i):
            kt_ = ktiles[i]
            pm_ = pms[i]
            vB, vBb = vB_fn(i)
            P.mm([lambda h, pm_=pm_, kt_=kt_, i=i: h.matmul(po.t[:, 0:nq], vt.t[:, kt_, :], pm_.t[:, 0:nq], start=(i == 0), stop=(i == nk - 1)),
                  lambda h, pm_=pm_, vB=vB, i=i: h.matmul(pl.t[:, 0:nq], vB, pm_.t[:, 0:nq], start=(i == 0), stop=(i == nk - 1))],
                 [vt, pm_, vBb], [po, pl])

        for i in range(min(LA, nk)):
            issue_S(i)
        for i in range(nk):
            if i + LA < nk:
                issue_S(i + LA)
            issue_PV(i)
        P.op("dve", lambda h: h.reciprocal(rl.t[:, 0:nq], pl.t[:, 0:nq]), [pl], [rl])
        P.op("dve", lambda h: h.tensor_tensor(of.t[:, 0:nq], po.t[:, 0:nq], rl.t[:, 0:nq], ALU.mult), [po, rl], [of])
        P.op("dve", lambda h: h.tensor_tensor(o_dst_fn(), of.t[:, 0:nq], za.t[:, zcol0:zcol0 + nq], ALU.mult), [of, za], [ocat])

    for hh in range(8):
        b2 = hh % 2
        P.dma("sp", kTh[b2].t[:], kT_scr.t.ap()[hh], [kT_scr], [kTh[b2]], kTh[b2])
        P.dma("sp", vh[b2].t[:], v_scr.t.ap()[:, hh * 128:(hh + 1) * 128].rearrange("(t p) d -> p t d", p=128), [v_scr], [vh[b2]], vh[b2])
        P.dma("sp", kTsh[b2].t[:], kTs_scr.t.ap()[hh], [kTs_scr], [kTsh[b2]], kTsh[b2])
        P.dma("sp", vsh[b2].t[:], vs_scr.t.ap()[:, hh * 128:(hh + 1) * 128].rearrange("(t p) d -> p t d", p=128), [vs_scr], [vsh[b2]], vsh[b2])
        P.dma("sp", qTh[b2].t[:], qT_scr.t.ap()[hh], [qT_scr], [qTh[b2]], qTh[b2])
        load_wt(0, 3072 + hh * 128)
        for (c0, n) in [(0, 512), (512, 512), (1024, 128)]:
            proj_feat(wt[0], None, lambda pk, c0=c0, n=n: silu_evac(pk, za, za.t[:, c0:c0 + n], n),
                      hnT_o, lambda kt, c0=c0, n=n: hnT_o.t[:, kt, c0:c0 + n], n)
        for qc in range(2):
            kts = list(range(4 * qc, 4 * qc + 20))
            attention(hh, qTh[b2], qc * 512, 512, kTh[b2], vh[b2], kts,
                      lambda i: (maskp.t[:, i, :], maskp),
                      lambda i, kts=kts: (validB.t[:, kts[i], :], validB),
                      lambda qc=qc, hh=hh: ocat.t[:, hh, qc * 512:(qc + 1) * 512], qc * 512)
        attention(hh, qTh[b2], 1024, 128, kTsh[b2], vsh[b2], list(range(17)),
                  lambda i: (masks_.t[:, i, :], masks_),
                  lambda i: (ones_bf.t[:], ones_bf),
                  lambda hh=hh: ocat.t[:, hh, 1024:1152], 1024)

    for cc in range(8):
        for j, base in enumerate([4096, 5120, 6144, 7168]):
            load_wt(j, base + cc * 128)
        bb, cb_, hb_, zb = fb
        for j, dstb in enumerate(fb):
            for (c0, n) in [(0, 512), (512, 512), (1024, 128)]:
                if j == 3:
                    ev = lambda pk, c0=c0, n=n, dstb=dstb: silu_evac(pk, dstb, dstb.t[:, 2 + c0:2 + c0 + n], n)
                else:
                    ev = lambda pk, c0=c0, n=n, dstb=dstb: P.op("act", lambda h: h.activation(dstb.t[:, 2 + c0:2 + c0 + n], pk.t[:, 0:n], AF.Identity), [pk], [dstb])
                proj_feat(wt[j], None, ev, hnT_o, lambda kt, c0=c0, n=n: hnT_o.t[:, kt, c0:c0 + n], n)
            if j in (1, 2):
                proj_feat(wt[j], None, lambda pk, dstb=dstb: P.op("act", lambda h: h.activation(dstb.t[:, 0:2], pk.t[:, 0:2], AF.Identity), [pk], [dstb]),
                          hp2, lambda kt: hp2.t[:, kt, :], 2)
        P.op("dve", lambda h: h.tensor_tensor(cb_.t[:], cb_.t[:], hb_.t[:], ALU.mult), [cb_, hb_], [cb_])
        w0 = cw.t[:, cc, 0:1]
        w1 = cw.t[:, cc, 1:2]
        w2 = cw.t[:, cc, 2:3]
        P.op("dve", lambda h, w2=w2: h.tensor_scalar(hb_.t[:, 2:1026], cb_.t[:, 2:1026], w2, None, ALU.mult), [cb_, cw], [hb_])
        P.op("dve", lambda h, w1=w1: h.scalar_tensor_tensor(hb_.t[:, 2:1026], cb_.t[:, 1:1025], w1, hb_.t[:, 2:1026], ALU.mult, ALU.add), [cb_, cw, hb_], [hb_])
        P.op("dve", lambda h, w0=w0: h.scalar_tensor_tensor(hb_.t[:, 2:1026], cb_.t[:, 0:1024], w0, hb_.t[:, 2:1026], ALU.mult, ALU.add), [cb_, cw, hb_], [hb_])
        P.op("dve", lambda h, cc=cc: h.tensor_copy(convo_p.t[:, cc, :], cb_.t[:, 1024:1026]), [cb_], [convo_p])
        P.op("dve", lambda h, cc=cc: h.tensor_copy(cb_.t[:, 1024:1026], sconv.t[:, cc, :]), [sconv], [cb_])
        P.op("dve", lambda h, w2=w2: h.tensor_scalar(hb_.t[:, 1026:1034], cb_.t[:, 1026:1034], w2, None, ALU.mult), [cb_, cw], [hb_])
        P.op("dve", lambda h, w1=w1: h.scalar_tensor_tensor(hb_.t[:, 1026:1034], cb_.t[:, 1025:1033], w1, hb_.t[:, 1026:1034], ALU.mult, ALU.add), [cb_, cw, hb_], [hb_])
        P.op("dve", lambda h, w0=w0: h.scalar_tensor_tensor(hb_.t[:, 1026:1034], cb_.t[:, 1024:1032], w0, hb_.t[:, 1026:1034], ALU.mult, ALU.add), [cb_, cw, hb_], [hb_])
        P.op("dve", lambda h, cc=cc: h.tensor_copy(convo_s.t[:, cc, :], cb_.t[:, 1032:1034]), [cb_], [convo_s])
        P.op("dve", lambda h: h.tensor_tensor(hb_.t[:, 2:1034], hb_.t[:, 2:1034], bb.t[:, 2:1034], ALU.mult), [hb_, bb], [hb_])
        P.op("dve", lambda h, cc=cc: h.tensor_tensor(ocat.t[:, 8 + cc, 0:1032], hb_.t[:, 2:1034], zb.t[:, 2:1034], ALU.mult), [hb_, zb], [ocat])
        P.op("dve", lambda h, cc=cc: h.memset(ocat.t[:, 8 + cc, 1032:1152], 0.0), [], [ocat])
    P.dma("sp", OUT["convp"].ap(), convo_p.t[:], [convo_p], [outbufs["convp"]], convo_p)
    P.dma("sp", OUT["convs"].ap(), convo_s.t[:], [convo_s], [outbufs["convs"]], convo_s)

    if STOP == 'B':
        P.barrier()
        return
    P.barrier()
    hn1T = P.sb("hn1T", [128, KT, NCH], BF16, at=hnT_o.at)
    P.off = B0
    wgb = [P.sb("wg%d" % i, [128, KT, 512], BF16) for i in range(2)]
    h1s = [P.sb("h1s%d" % i, [128, D], F32) for i in range(5)]
    xn1 = [P.sb("xn1%d" % i, [128, D], BF16) for i in range(2)]
    junk = P.sb("junk2", [128, D], F32)
    ss = P.sb("ss2", [128, 4], F32)
    tmpT = P.sb("tmpT", [128, KT, 128], BF16)
    load_g("g_ssm")
    wi = 0
    for tiles in [list(range(0, 5)), list(range(5, 9))]:
        for si, tj in enumerate(tiles):
            o0 = tj * 128
            src = IN["xh"].ap()[NHALO + o0:NHALO + o0 + 128, :] if tj < 8 else IN["xs"].ap()
            P.dma("sp", h1s[si].t[:], src, [], [h1s[si]], h1s[si])
        for g4 in range(4):
            wg = wgb[wi % 2]
            wi += 1
            P.dma("pool", wg.t[:], IN["w_out_ab"].ap()[:, g4 * 512:(g4 + 1) * 512].rearrange("(kt p) c -> p kt c", p=128), [], [wg], wg)
            for si, tj in enumerate(tiles):
                o0 = tj * 128
                ht_ = h1s[si]
                pk = next_pf()
                fns = [(lambda h, pk=pk, kt=kt, o0=o0, wg=wg: h.matmul(pk.t[:], ocat.t[:, kt, o0:o0 + 128], wg.t[:, kt, :],
                                                                         start=(kt == 0), stop=(kt == KT - 1))) for kt in range(KT)]
                P.mm(fns, [ocat, wg], [pk])
                P.op("dve", lambda h, pk=pk, ht_=ht_, g4=g4: h.tensor_tensor(ht_.t[:, g4 * 512:(g4 + 1) * 512], ht_.t[:, g4 * 512:(g4 + 1) * 512], pk.t[:], ALU.add),
                     [pk, ht_], [ht_])
        for si, tj in enumerate(tiles):
            o0 = tj * 128
            ht_ = h1s[si]
            P.dma("sp", h1_scr.t.ap()[o0:o0 + 128, :], ht_.t[:], [ht_], [h1_scr], ht_)
            if STOP == 'C1':
                if tj < 8:
                    P.dma("sp", OUT["yp"].ap()[o0:o0 + 128, :], ht_.t[:], [ht_], [outbufs["yp"]], ht_)
                else:
                    P.dma("sp", OUT["ys"].ap(), ht_.t[:], [ht_], [outbufs["ys"]], ht_)
            xn = xn1[tj % 2]
            rmsnorm_rows(ht_, xn, ss, junk)
            if tj < 8:
                transpose_rows(xn, hn1T, lambda half, o0=o0: hn1T.t[:, half * 8:(half + 1) * 8, o0:o0 + 128])
            else:
                transpose_rows(xn, tmpT, lambda half: tmpT.t[:, half * 8:(half + 1) * 8, :])
                P.op("dve", lambda h: h.tensor_copy(hn1T.t[:, :, 1024:1032], tmpT.t[:, :, 0:8]), [tmpT], [hn1T])
    for j in range(8):
        P.dma("sp", hn_src[j].t.ap().rearrange("(k p) t -> p k t", p=128), hn1T.t[:, 2 * j:2 * j + 2, :], [hn1T], [hn_src[j]], hn1T)
        P.coll(hn_src[j], hn_dst[j], GROUPS)

    if STOP == 'C1':
        P.barrier()
        return
    P.barrier()
    P.off = CONST_END
    uT = P.sb("uT", [128, 4, 4 * NCH], BF16)
    ygT = P.sb("ygT", [128, 4, 4 * NCH], BF16)
    L1 = P.off
    wu = P.sb("wu", [128, KT, 512], BF16)
    hch = [P.sb("hch%d" % i, [128, KT, 516], BF16) for i in range(2)]
    P.dma("pool", wu.t[:], IN["w_in_c"].ap()[:, 0:512].rearrange("(kt p) c -> p kt c", p=128), [], [wu], wu)
    ci = 0
    for r in range(4):
        for hf in range(2):
            hc = hch[ci % 2]
            ci += 1
            c0 = hf * 516
            for j in range(8):
                P.dma("sp", hc.t[:, 2 * j:2 * j + 2, :], hn_dst[j].t.ap()[r * 256:(r + 1) * 256, c0:c0 + 516].rearrange("(k p) t -> p k t", p=128),
                      [hn_dst[j]], [hc], hc)
            for ft in range(4):
                pk = next_pf()
                fns = [(lambda h, pk=pk, kt=kt, ft=ft, hc=hc: h.matmul(pk.t[:, 0:512], wu.t[:, kt, ft * 128:(ft + 1) * 128], hc.t[:, kt, 0:512],
                                                                      start=(kt == 0), stop=(kt == KT - 1))) for kt in range(KT)]
                P.mm(fns, [wu, hc], [pk])
                P.op("act", lambda h, pk=pk, ft=ft, r=r, c0=c0: h.activation(uT.t[:, ft, r * NCH + c0:r * NCH + c0 + 512], pk.t[:, 0:512], AF.Identity), [pk], [uT])
                pk2 = next_pf()
                fns2 = [(lambda h, pk2=pk2, kt=kt, ft=ft, hc=hc: h.matmul(pk2.t[:, 0:4], wu.t[:, kt, ft * 128:(ft + 1) * 128], hc.t[:, kt, 512:516],
                                                                         start=(kt == 0), stop=(kt == KT - 1))) for kt in range(KT)]
                P.mm(fns2, [wu, hc], [pk2])
                P.op("act", lambda h, pk2=pk2, ft=ft, r=r, c0=c0: h.activation(uT.t[:, ft, r * NCH + c0 + 512:r * NCH + c0 + 516], pk2.t[:, 0:4], AF.Identity), [pk2], [uT])

    if STOP == 'U':
        P.barrier()
        return
    P.barrier()
    P.off = L1
    def small(name, shape, dt=F32):
        return P.sb(name, shape, dt)
    lre_s = small("lre_s", [128, 16]); lim_s = small("lim_s", [128, 16]); lst_s = small("lst_s", [128, 16])
    lre_r = small("lre_r", [128, 256]); lim_r = small("lim_r", [128, 256]); lst_r = small("lst_r", [128, 256])
    bre_r = small("bre_r", [128, 256]); bim_r = small("bim_r", [128, 256])
    cre_s = small("cre_s", [128, 16, 16]); cim_s = small("cim_s", [128, 16, 16])
    rmask = small("rmask", [128, 8]); smask = small("smask", [128, 2])
    sre0 = small("sre0", [128, 4, 16]); sim0 = small("sim0", [128, 4, 16])
    iota = small("iota", [128, 1024])
    for b_, n in [(lre_s, "lre_s"), (lim_s, "lim_s"), (lst_s, "lst_s"), (cre_s, "cre_s"), (cim_s, "cim_s"),
                  (rmask, "rmask"), (smask, "smask"), (sre0, "sre0"), (sim0, "sim0"), (iota, "iota")]:
        P.dma("sp", b_.t[:], IN[n].ap(), [], [b_], b_)
    for b_, n in [(lre_r, "lre_r"), (lim_r, "lim_r"), (lst_r, "lst_r"), (bre_r, "bre_r"), (bim_r, "bim_r")]:
        P.dma("sp", b_.t[:], IN[n].ap().rearrange("p a b -> p (a b)"), [], [b_], b_)
    negpi = small("negpi", [128, 1])
    P.op("dve", lambda h: h.memset(negpi.t[:], -math.pi), [], [negpi])

    I32 = mybir.dt.int32
    tq = small("tq", [128, 1024]); tiq = small("tiq", [128, 1024], I32)
    halfpi = small("halfpi", [128, 1]); zero_t = small("zero_t", [128, 1])
    P.op("dve", lambda h: h.memset(halfpi.t[:], 0.5 * math.pi), [], [halfpi])
    P.op("dve", lambda h: h.memset(zero_t.t[:], 0.0), [], [zero_t])

    def sincos(ang_ap, n, s_ap, c_ap, rd, wr):
        for (dst, addc, bt, lo, hi) in [(s_ap, 0.0, zero_t, -math.pi, math.pi), (c_ap, 0.25, halfpi, -1.5 * math.pi, 0.5 * math.pi)]:
            P.op("dve", lambda h, addc=addc: h.tensor_scalar(tq.t[:, 0:n], ang_ap, 1.0 / TWO_PI, addc, ALU.mult, ALU.add), rd, [tq])
            P.op("dve", lambda h: h.tensor_copy(tiq.t[:, 0:n], tq.t[:, 0:n]), [tq], [tiq])
            P.op("dve", lambda h: h.tensor_copy(tq.t[:, 0:n], tiq.t[:, 0:n]), [tiq], [tq])
            P.op("dve", lambda h: h.scalar_tensor_tensor(tq.t[:, 0:n], tq.t[:, 0:n], -TWO_PI, ang_ap, ALU.mult, ALU.add), [tq] + rd, [tq])
            P.op("dve", lambda h, lo=lo, hi=hi: h.tensor_scalar(tq.t[:, 0:n], tq.t[:, 0:n], lo, hi, ALU.max, ALU.min), [tq], [tq])
            P.op("act", lambda h, dst=dst, bt=bt: h.activation(dst, tq.t[:, 0:n], AF.Sin, bias=bt.t[:, 0:1], scale=1.0), [tq, bt], wr)

    def disc(lre, lim, lst, n, pref):
        o = {}
        for nm in ["step", "mag", "th", "c", "s", "tmp", "nr", "den", "cr", "ci", "a", "b"]:
            o[nm] = small(pref + nm, [128, n])
        al = [o[k] for k in o] + [lre, lim, lst]
        P.op("act", lambda h: h.activation(o["step"].t[:], lst.t[:, 0:n], AF.Exp), al, al)
        P.op("dve", lambda h: h.tensor_tensor(o["th"].t[:], lim.t[:, 0:n], o["step"].t[:], ALU.mult), al, al)
        P.op("dve", lambda h: h.tensor_tensor(o["a"].t[:], lre.t[:, 0:n], o["step"].t[:], ALU.mult), al, al)
        P.op("act", lambda h: h.activation(o["mag"].t[:], o["a"].t[:], AF.Exp), al, al)
        sincos(o["th"].t[:], n, o["s"].t[:], o["c"].t[:], al, al)
        P.op("dve", lambda h: h.tensor_tensor(o["a"].t[:], o["mag"].t[:], o["c"].t[:], ALU.mult), al, al)
        P.op("dve", lambda h: h.tensor_scalar(o["nr"].t[:], o["a"].t[:], 1.0, -1.0, ALU.mult, ALU.add), al, al)
        P.op("dve", lambda h: h.tensor_tensor(o["b"].t[:], o["mag"].t[:], o["s"].t[:], ALU.mult), al, al)
        P.op("dve", lambda h: h.tensor_tensor(o["den"].t[:], lre.t[:, 0:n], lre.t[:, 0:n], ALU.mult), al, al)
        P.op("dve", lambda h: h.tensor_tensor(o["tmp"].t[:], lim.t[:, 0:n], lim.t[:, 0:n], ALU.mult), al, al)
        P.op("dve", lambda h: h.tensor_tensor(o["den"].t[:], o["den"].t[:], o["tmp"].t[:], ALU.add), al, al)
        P.op("dve", lambda h: h.reciprocal(o["den"].t[:], o["den"].t[:]), al, al)
        P.op("dve", lambda h: h.tensor_tensor(o["cr"].t[:], o["nr"].t[:], lre.t[:, 0:n], ALU.mult), al, al)
        P.op("dve", lambda h: h.tensor_tensor(o["tmp"].t[:], o["b"].t[:], lim.t[:, 0:n], ALU.mult), al, al)
        P.op("dve", lambda h: h.tensor_tensor(o["cr"].t[:], o["cr"].t[:], o["tmp"].t[:], ALU.add), al, al)
        P.op("dve", lambda h: h.tensor_tensor(o["cr"].t[:], o["cr"].t[:], o["den"].t[:], ALU.mult), al, al)
        P.op("dve", lambda h: h.tensor_tensor(o["ci"].t[:], o["b"].t[:], lre.t[:, 0:n], ALU.mult), al, al)
        P.op("dve", lambda h: h.tensor_tensor(o["tmp"].t[:], o["nr"].t[:], lim.t[:, 0:n], ALU.mult), al, al)
        P.op("dve", lambda h: h.tensor_tensor(o["ci"].t[:], o["ci"].t[:], o["tmp"].t[:], ALU.subtract), al, al)
        P.op("dve", lambda h: h.tensor_tensor(o["ci"].t[:], o["ci"].t[:], o["den"].t[:], ALU.mult), al, al)
        return o, al

    ds_, als = disc(lre_s, lim_s, lst_s, 16, "ds_")
    dr_, alr = disc(lre_r, lim_r, lst_r, 256, "dr_")
    bbr = small("bbr", [128, 256]); bbi = small("bbi", [128, 256]); tmpr = small("tmpr", [128, 256])
    alr2 = alr + [bbr, bbi, tmpr, bre_r, bim_r]
    P.op("dve", lambda h: h.tensor_tensor(bbr.t[:], dr_["cr"].t[:], bre_r.t[:], ALU.mult), alr2, alr2)
    P.op("dve", lambda h: h.tensor_tensor(tmpr.t[:], dr_["ci"].t[:], bim_r.t[:], ALU.mult), alr2, alr2)
    P.op("dve", lambda h: h.tensor_tensor(bbr.t[:], bbr.t[:], tmpr.t[:], ALU.subtract), alr2, alr2)
    P.op("dve", lambda h: h.tensor_tensor(bbi.t[:], dr_["cr"].t[:], bim_r.t[:], ALU.mult), alr2, alr2)
    P.op("dve", lambda h: h.tensor_tensor(tmpr.t[:], dr_["ci"].t[:], bre_r.t[:], ALU.mult), alr2, alr2)
    P.op("dve", lambda h: h.tensor_tensor(bbi.t[:], bbi.t[:], tmpr.t[:], ALU.add), alr2, alr2)
    BbT = [small("BbT%d" % ri, [128, 16, 128], BF16) for ri in range(2)]
    for ri, src in enumerate([bbr, bbi]):
        for qq in range(4):
            for g2 in range(2):
                m = rmask.t[:, qq * 2 + g2:qq * 2 + g2 + 1]
                o_ap = BbT[ri].t[:].rearrange("p (ft q) c -> p ft q c", q=4)[:, :, qq, g2 * 64:(g2 + 1) * 64]
                i_ap = src.t[:].rearrange("p (ft d) -> p ft d", ft=4)
                P.op("dve", lambda h, o_ap=o_ap, i_ap=i_ap, m=m: h.tensor_scalar(o_ap, i_ap, m, None, ALU.mult), alr2 + [rmask], [BbT[ri]])
    CT = [small("CT%d" % ri, [128, 16, 128], BF16) for ri in range(2)]
    for ri in range(2):
        P.op("dve", lambda h, ri=ri: h.memset(CT[ri].t[:], 0.0), [], [CT[ri]])
    for ri, (src, sgn) in enumerate([(cre_s, 1.0), (cim_s, -1.0)]):
        for pair in range(16):
            qq = pair % 4
            for g2 in range(2):
                m = smask.t[:, g2:g2 + 1]
                col = qq * 32 + g2 * 16
                P.op("dve", lambda h, ri=ri, pair=pair, col=col, m=m, src=src, sgn=sgn: h.tensor_scalar(
                    CT[ri].t[:, pair, col:col + 16], src.t[:, pair, :], m, sgn, ALU.mult, ALU.mult), [src, smask], [CT[ri]])
    rr = ds_["mag"]; th = ds_["th"]
    cth = small("cth", [128, 16]); sth = small("sth", [128, 16])
    P.op("dve", lambda h: h.tensor_copy(cth.t[:], ds_["c"].t[:]), als, [cth])
    P.op("dve", lambda h: h.tensor_copy(sth.t[:], ds_["s"].t[:]), als, [sth])

    tabc = [small("tabc%d" % i, [128, 1024]) for i in range(4)]
    tabs = [small("tabs%d" % i, [128, 1024]) for i in range(4)]
    WS = [dict(gr=small("gr0", [128, 1024]), gi=small("gi0", [128, 1024]), yr=small("yr0", [128, 1024]), yi=small("yi0", [128, 1024]),
               hrb=small("hrb0", [128, 1024], BF16), hib=small("hib0", [128, 1024], BF16))]
    hib1 = small("hib1", [128, 1024], BF16)
    ytmp = small("ytmp", [128, 512]); ysq = small("ysq", [128, 512])
    stp_l = [small("stp%d" % p_, [128, 2]) for p_ in range(16)]
    sts_p = [small("stsp%d" % p_, [128, 4, 2]) for p_ in range(16)]
    m32 = small("m32", [128, 32])
    P.op("dve", lambda h: h.memset(m32.t[:], 1.0), [], [m32])
    P.op("dve", lambda h: h.memset(m32.t[:].rearrange("p (r c) -> p r c", r=4)[:, :, 0], 0.0), [m32], [m32])
    for w_ in WS:
        w_["gin"] = small("gin0", [128, 2]); w_["hend"] = small("hend0", [128, 2])
        w_["gin4"] = small("gin40", [128, 4, 2]); w_["d0"] = small("d00", [128, 32])
    for p_ in range(16):
        P.op("dve", lambda h, p_=p_: h.memset(stp_l[p_].t[:], 0.0), [], [stp_l[p_]])
    P.barrier()
    blkA = lre_r.at
    blkB = dr_["step"].at
    WS.append(dict(gr=P.sb("gr1", [128, 1024], F32, at=blkB), gi=P.sb("gi1", [128, 1024], F32, at=blkB + 4096),
                   yr=P.sb("yr1", [128, 1024], F32, at=blkB + 8192), yi=P.sb("yi1", [128, 1024], F32, at=blkA),
                   hrb=P.sb("hrb1", [128, 1024], BF16, at=blkB + 12288), hib=hib1,
                   gin=small("gin1", [128, 2]), hend=small("hend1", [128, 2]),
                   gin4=small("gin41", [128, 4, 2]), d0=small("d01", [128, 32])))
    ang = WS[0]["gr"]
    GC = 1.5957691216057308
    kcount = [0]

    def stageX(pair, qq, ft, col0, T, tc_, ts_, init_re, init_im, init_bufs, out_b):
        w = WS[kcount[0] % 2]
        kcount[0] += 1
        gr, gi, yr, yi, hrb, hib, gin, hend = w["gr"], w["gi"], w["yr"], w["yi"], w["hrb"], w["hib"], w["gin"], w["hend"]
        for h0 in range(0, T, 512):
            n = min(512, T - h0)
            pxr = next_pf(); pxi = next_pf()
            P.mm([lambda h, pxr=pxr, n=n, h0=h0: h.matmul(pxr.t[:, 0:n], BbT[0].t[:, pair, :], uT.t[:, ft, col0 + h0:col0 + h0 + n], start=True, stop=True)], [BbT[0], uT], [pxr])
            P.mm([lambda h, pxi=pxi, n=n, h0=h0: h.matmul(pxi.t[:, 0:n], BbT[1].t[:, pair, :], uT.t[:, ft, col0 + h0:col0 + h0 + n], start=True, stop=True)], [BbT[1], uT], [pxi])
            c_ = tc_.t[:, h0:h0 + n]; s_ = ts_.t[:, h0:h0 + n]
            P.op("dve", lambda h, pxr=pxr, c_=c_, n=n, h0=h0: h.tensor_tensor(yr.t[:, h0:h0 + n], pxr.t[:, 0:n], c_, ALU.mult), [pxr, tc_], [yr])
            P.op("dve", lambda h, pxi=pxi, s_=s_, n=n, h0=h0: h.tensor_tensor(gr.t[:, h0:h0 + n], pxi.t[:, 0:n], s_, ALU.mult), [pxi, ts_], [gr])
            P.op("dve", lambda h, n=n, h0=h0: h.tensor_tensor(yr.t[:, h0:h0 + n], yr.t[:, h0:h0 + n], gr.t[:, h0:h0 + n], ALU.add), [yr, gr], [yr])
            P.op("dve", lambda h, pxi=pxi, c_=c_, n=n, h0=h0: h.tensor_tensor(yi.t[:, h0:h0 + n], pxi.t[:, 0:n], c_, ALU.mult), [pxi, tc_], [yi])
            P.op("dve", lambda h, pxr=pxr, s_=s_, n=n, h0=h0: h.tensor_tensor(gi.t[:, h0:h0 + n], pxr.t[:, 0:n], s_, ALU.mult), [pxr, ts_], [gi])
            P.op("dve", lambda h, n=n, h0=h0: h.tensor_tensor(yi.t[:, h0:h0 + n], yi.t[:, h0:h0 + n], gi.t[:, h0:h0 + n], ALU.subtract), [yi, gi], [yi])
        ct = cth.t[:, pair:pair + 1]; st_ = sth.t[:, pair:pair + 1]
        P.op("dve", lambda h: h.tensor_scalar(gin.t[:, 0:1], init_re, ct, None, ALU.mult), init_bufs + [cth], [gin])
        P.op("dve", lambda h: h.scalar_tensor_tensor(gin.t[:, 0:1], init_im, st_, gin.t[:, 0:1], ALU.mult, ALU.subtract), init_bufs + [sth, gin], [gin])
        P.op("dve", lambda h: h.tensor_scalar(gin.t[:, 0:1], gin.t[:, 0:1], -1.0, None, ALU.mult), [gin], [gin])
        P.op("dve", lambda h: h.tensor_scalar(gin.t[:, 1:2], init_re, st_, None, ALU.mult), init_bufs + [sth], [gin])
        P.op("dve", lambda h: h.scalar_tensor_tensor(gin.t[:, 1:2], init_im, ct, gin.t[:, 1:2], ALU.mult, ALU.add), init_bufs + [cth, gin], [gin])
        rb = rr.t[:, pair:pair + 1].to_broadcast([128, T])
        P.op("dve", lambda h: h.tensor_tensor_scan(gr.t[:, 0:T], rb, yr.t[:, 0:T], gin.t[:, 0:1], ALU.mult, ALU.add), [yr, gin] + als, [gr])
        P.op("dve", lambda h: h.tensor_tensor_scan(gi.t[:, 0:T], rb, yi.t[:, 0:T], gin.t[:, 1:2], ALU.mult, ALU.add), [yi, gin] + als, [gi])
        c_ = tc_.t[:, 0:T]; s_ = ts_.t[:, 0:T]
        P.op("dve", lambda h: h.tensor_tensor(yr.t[:, 0:T], gr.t[:, 0:T], c_, ALU.mult), [gr, tc_], [yr])
        P.op("dve", lambda h: h.tensor_tensor(yi.t[:, 0:T], gi.t[:, 0:T], s_, ALU.mult), [gi, ts_], [yi])
        P.op("dve", lambda h: h.tensor_tensor(hrb.t[:, 0:T], yr.t[:, 0:T], yi.t[:, 0:T], ALU.subtract), [yr, yi], [hrb])
        P.op("dve", lambda h: h.tensor_tensor(hend.t[:, 0:1], yr.t[:, T - 1:T], yi.t[:, T - 1:T], ALU.subtract), [yr, yi], [hend])
        P.op("dve", lambda h: h.tensor_tensor(yr.t[:, 0:T], gr.t[:, 0:T], s_, ALU.mult), [gr, ts_, hrb, hend], [yr])
        P.op("dve", lambda h: h.tensor_tensor(yi.t[:, 0:T], gi.t[:, 0:T], c_, ALU.mult), [gi, tc_, hrb, hend], [yi])
        P.op("dve", lambda h: h.tensor_tensor(hib.t[:, 0:T], yr.t[:, 0:T], yi.t[:, 0:T], ALU.add), [yr, yi], [hib])
        P.op("dve", lambda h: h.tensor_tensor(hend.t[:, 1:2], yr.t[:, T - 1:T], yi.t[:, T - 1:T], ALU.add), [yr, yi], [hend])
        P.op("dve", lambda h: h.tensor_copy(out_b.t[:, 0:2], hend.t[:, 0:2]), [hend], [out_b])
        return dict(pair=pair, qq=qq, T=T, hrb=hrb, hib=hib, after=[])

    def v48(ap):
        return ap.rearrange("p (r c) -> p r c", r=4)

    def stageXs(pair, qq, ft, tc_, ts_):
        w = WS[kcount[0] % 2]
        kcount[0] += 1
        gr, gi, yr, yi, hrb, hib, gin4, d0 = w["gr"], w["gi"], w["yr"], w["yi"], w["hrb"], w["hib"], w["gin4"], w["d0"]
        ucols = v48(uT.t[:, ft, :])[:, :, 1024:1032]
        pxr = next_pf(); pxi = next_pf()
        P.mm([lambda h: h.matmul(pxr.t[:, 0:32], BbT[0].t[:, pair, :], ucols, start=True, stop=True)], [BbT[0], uT], [pxr])
        P.mm([lambda h: h.matmul(pxi.t[:, 0:32], BbT[1].t[:, pair, :], ucols, start=True, stop=True)], [BbT[1], uT], [pxi])
        cb = tc_.t[:, 0:8].unsqueeze(1).to_broadcast([128, 4, 8])
        sb_ = ts_.t[:, 0:8].unsqueeze(1).to_broadcast([128, 4, 8])
        xr = v48(pxr.t[:, 0:32]); xi = v48(pxi.t[:, 0:32])
        yrv = v48(yr.t[:, 0:32]); yiv = v48(yi.t[:, 0:32]); grv = v48(gr.t[:, 0:32]); giv = v48(gi.t[:, 0:32])
        P.op("dve", lambda h: h.tensor_tensor(yrv, xr, cb, ALU.mult), [pxr, tc_], [yr])
        P.op("dve", lambda h: h.tensor_tensor(grv, xi, sb_, ALU.mult), [pxi, ts_], [gr])
        P.op("dve", lambda h: h.tensor_tensor(yrv, yrv, grv, ALU.add), [yr, gr], [yr])
        P.op("dve", lambda h: h.tensor_tensor(yiv, xi, cb, ALU.mult), [pxi, tc_], [yi])
        P.op("dve", lambda h: h.tensor_tensor(giv, xr, sb_, ALU.mult), [pxr, ts_], [gi])
        P.op("dve", lambda h: h.tensor_tensor(yiv, yiv, giv, ALU.subtract), [yi, gi], [yi])
        ct = cth.t[:, pair:pair + 1]; st_ = sth.t[:, pair:pair + 1]; rs = rr.t[:, pair:pair + 1]
        ire = sre0.t[:, :, pair]; iim = sim0.t[:, :, pair]
        g_re = gin4.t[:, :, 0]; g_im = gin4.t[:, :, 1]
        P.op("dve", lambda h: h.tensor_scalar(g_re, ire, ct, None, ALU.mult), [sre0, cth], [gin4])
        P.op("dve", lambda h: h.scalar_tensor_tensor(g_re, iim, st_, g_re, ALU.mult, ALU.subtract), [sim0, sth, gin4], [gin4])
        P.op("dve", lambda h: h.tensor_scalar(g_re, g_re, -1.0, None, ALU.mult), [gin4], [gin4])
        P.op("dve", lambda h: h.tensor_scalar(g_im, ire, st_, None, ALU.mult), [sre0, sth, gin4], [gin4])
        P.op("dve", lambda h: h.scalar_tensor_tensor(g_im, iim, ct, g_im, ALU.mult, ALU.add), [sim0, cth, gin4], [gin4])
        P.op("dve", lambda h: h.scalar_tensor_tensor(yrv[:, :, 0], g_re, rs, yrv[:, :, 0], ALU.mult, ALU.add), [gin4, yr] + als, [yr])
        P.op("dve", lambda h: h.scalar_tensor_tensor(yiv[:, :, 0], g_im, rs, yiv[:, :, 0], ALU.mult, ALU.add), [gin4, yi] + als, [yi])
        P.op("dve", lambda h: h.tensor_scalar(d0.t[:], m32.t[:], rs, None, ALU.mult), [m32] + als, [d0])
        P.op("dve", lambda h: h.tensor_tensor_scan(gr.t[:, 0:32], d0.t[:], yr.t[:, 0:32], 0.0, ALU.mult, ALU.add), [yr, d0], [gr])
        P.op("dve", lambda h: h.tensor_tensor_scan(gi.t[:, 0:32], d0.t[:], yi.t[:, 0:32], 0.0, ALU.mult, ALU.add), [yi, d0], [gi])
        hrv = v48(hrb.t[:, 0:32]); hiv = v48(hib.t[:, 0:32])
        out_b = sts_p[pair]
        P.op("dve", lambda h: h.tensor_tensor(yrv, grv, cb, ALU.mult), [gr, tc_], [yr])
        P.op("dve", lambda h: h.tensor_tensor(yiv, giv, sb_, ALU.mult), [gi, ts_], [yi])
        P.op("dve", lambda h: h.tensor_tensor(hrv, yrv, yiv, ALU.subtract), [yr, yi], [hrb])
        P.op("dve", lambda h: h.tensor_tensor(out_b.t[:, :, 0], yrv[:, :, 7], yiv[:, :, 7], ALU.subtract), [yr, yi], [out_b])
        P.op("dve", lambda h: h.tensor_tensor(yrv, grv, sb_, ALU.mult), [gr, ts_, hrb, out_b], [yr])
        P.op("dve", lambda h: h.tensor_tensor(yiv, giv, cb, ALU.mult), [gi, tc_, hrb, out_b], [yi])
        P.op("dve", lambda h: h.tensor_tensor(hiv, yrv, yiv, ALU.add), [yr, yi], [hib])
        P.op("dve", lambda h: h.tensor_tensor(out_b.t[:, :, 1], yrv[:, :, 7], yiv[:, :, 7], ALU.add), [yr, yi], [out_b])
        return dict(pair=pair, qq=qq, T=32, hrb=hrb, hib=hib, after=[])

    def y_evac_s(ft):
        ucols = v48(uT.t[:, ft, :])[:, :, 1024:1032]
        dcols = v48(ygT.t[:, ft, :])[:, :, 1024:1032]
        yp_ = ypb[0]
        ypv = v48(yp_.t[:, 0:32]); yt = v48(ytmp.t[:, 0:32]); yq = v48(ysq.t[:, 0:32])
        P.op("dve", lambda h: h.scalar_tensor_tensor(yt, ucols, dsk.t[:, ft:ft + 1], ypv, ALU.mult, ALU.add), [uT, dsk, yp_], [ytmp])
        P.op("dve", lambda h: h.tensor_tensor(yq, yt, yt, ALU.mult), [ytmp], [ysq])
        P.op("dve", lambda h: h.tensor_scalar(yq, yq, 0.044715, 1.0, ALU.mult, ALU.add), [ysq], [ysq])
        P.op("dve", lambda h: h.tensor_tensor(yq, yq, yt, ALU.mult), [ysq, ytmp], [ysq])
        P.op("act", lambda h: h.activation(ysq.t[:, 0:32], ysq.t[:, 0:32], AF.Sigmoid, scale=GC), [ysq], [ysq])
        P.op("dve", lambda h: h.tensor_tensor(dcols, yq, yt, ALU.mult), [ysq, ytmp], [ygT])

    ypb = [pf[4], pf[5]]

    def stageY(c):
        pair, qq, T, hrb, hib = c["pair"], c["qq"], c["T"], c["hrb"], c["hib"]
        for h0 in range(0, T, 512):
            n = min(512, T - h0)
            yp_ = ypb[h0 // 512]
            P.mm([lambda h, yp_=yp_, n=n, h0=h0: h.matmul(yp_.t[:, 0:n], CT[0].t[:, pair, :], hrb.t[:, h0:h0 + n], start=(qq == 0), stop=False),
                  lambda h, yp_=yp_, n=n, h0=h0: h.matmul(yp_.t[:, 0:n], CT[1].t[:, pair, :], hib.t[:, h0:h0 + n], start=False, stop=(qq == 3))],
                 [CT[0], CT[1], hrb, hib], [yp_])
        for f in c["after"]:
            f()

    def y_evac(yp_, ft, col0, n):
        P.op("dve", lambda h: h.scalar_tensor_tensor(ytmp.t[:, 0:n], uT.t[:, ft, col0:col0 + n], dsk.t[:, ft:ft + 1], yp_.t[:, 0:n], ALU.mult, ALU.add),
             [uT, dsk, yp_], [ytmp])
        P.op("dve", lambda h: h.tensor_tensor(ysq.t[:, 0:n], ytmp.t[:, 0:n], ytmp.t[:, 0:n], ALU.mult), [ytmp], [ysq])
        P.op("dve", lambda h: h.tensor_scalar(ysq.t[:, 0:n], ysq.t[:, 0:n], 0.044715, 1.0, ALU.mult, ALU.add), [ysq], [ysq])
        P.op("dve", lambda h: h.tensor_tensor(ysq.t[:, 0:n], ysq.t[:, 0:n], ytmp.t[:, 0:n], ALU.mult), [ysq, ytmp], [ysq])
        P.op("act", lambda h: h.activation(ysq.t[:, 0:n], ysq.t[:, 0:n], AF.Sigmoid, scale=GC), [ysq], [ysq])
        P.op("dve", lambda h: h.tensor_tensor(ygT.t[:, ft, col0:col0 + n], ysq.t[:, 0:n], ytmp.t[:, 0:n], ALU.mult), [ysq, ytmp], [ygT])

    pending = [None]

    def push(ctx):
        if pending[0] is not None:
            stageY(pending[0])
        pending[0] = ctx

    for ft in range(4):
        for qq in range(4):
            pair = ft * 4 + qq
            P.op("dve", lambda h, pair=pair: h.tensor_scalar(ang.t[:], iota.t[:], th.t[:, pair:pair + 1], None, ALU.mult), [iota] + als, [ang])
            sincos(ang.t[:], 1024, tabs[qq].t[:], tabc[qq].t[:], [ang], [tabs[qq], tabc[qq]])
        for seg in range(4):
            for qq in range(4):
                pair = ft * 4 + qq
                ctx = stageX(pair, qq, ft, seg * NCH, 1024, tabc[qq], tabs[qq],
                             stp_l[pair].t[:, 0:1], stp_l[pair].t[:, 1:2], [stp_l[pair]], stp_l[pair])
                if qq == 3:
                    ctx["after"] = [(lambda ft=ft, seg=seg, hf=hf: y_evac(ypb[hf], ft, seg * NCH + hf * 512, 512)) for hf in range(2)]
                push(ctx)
        for qq in range(4):
            pair = ft * 4 + qq
            ctx = stageXs(pair, qq, ft, tabc[qq], tabs[qq])
            if qq == 3:
                ctx["after"] = [(lambda ft=ft: y_evac_s(ft))]
            push(ctx)
    push(None)
    stp2 = P.sb("stp2", [128, 2, 16], F32, at=tq.at); sts2 = P.sb("sts2", [128, 4, 2, 16], F32, at=tq.at + 128)
    for p_ in range(16):
        P.op("dve", lambda h, p_=p_: h.tensor_copy(stp2.t[:, :, p_], stp_l[p_].t[:, 0:2]), [stp_l[p_]], [stp2])
        P.op("dve", lambda h, p_=p_: h.tensor_copy(sts2.t[:, :, :, p_], sts_p[p_].t[:]), [sts_p[p_]], [sts2])
    if STOP == 'SSM':
        for r_ in range(4):
            P.dma("pool", OUT["yp"].ap().rearrange("(p f x) c -> p f (x c)", p=128, f=4)[:, :, r_ * 1024:(r_ + 1) * 1024],
                  ygT.t[:, :, r_ * NCH:r_ * NCH + 1024], [ygT], [outbufs["yp"]], ygT)
        P.dma("pool", OUT["ys"].ap()[:, 0:128].rearrange("p (f r s) -> p f r s", f=4, r=4),
              ygT.t[:].rearrange("p f (r c) -> p f r c", r=4)[:, :, :, 1024:1032], [ygT], [outbufs["ys"]], ygT)
    P.dma("sp", OUT["ssm_p"].ap(), stp2.t[:], [stp2], [outbufs["ssm_p"]], stp2)
    P.dma("sp", OUT["ssm_s"].ap(), sts2.t[:], [sts2], [outbufs["ssm_s"]], sts2)
    for j in range(8):
        P.dma("sp", y_src[j].t.ap().rearrange("(ft p) t -> p ft t", p=128), ygT.t[:, :, j * 516:(j + 1) * 516], [ygT], [y_src[j]], ygT)
        P.coll(y_src[j], y_dst[j], GROUPS)

    if STOP == 'SSM':
        P.barrier()
        return
    P.barrier()
    P.off = CONST_END
    y2T = P.sb("y2T", [128, KT, NO], BF16)
    F0 = P.off
    ygo = P.sb("ygo", [128, KT, NCH], BF16)
    hn1o = P.sb("hn1o", [128, KT, NCH], BF16)
    ych = [P.sb("ych%d" % i, [128, 4, NCH], BF16) for i in range(2)]
    wt2 = [P.sb("wt2_%d" % i, [128, KT, 128], BF16) for i in range(4)]
    gl = P.sb("gl", [128, NCH], F32); zz = P.sb("zz", [128, NCH], F32)
    for j in range(8):
        P.dma("sp", hn1o.t[:, 2 * j:2 * j + 2, :], hn_src[j].t.ap().rearrange("(k p) t -> p k t", p=128), [hn_src[j]], [hn1o], hn1o)
    P.op("pool", lambda h: h.memset(y2T.t[:, :, NCH:NO], 0.0), [], [y2T])
    ci = 0
    for rf in range(4):
        for r in range(4):
            yc = ych[ci % 2]; ci += 1
            for hh_ in range(2):
                P.dma("sp", yc.t[:, :, hh_ * 516:(hh_ + 1) * 516],
                      y_dst[2 * r + hh_].t.ap()[rf * 512:(rf + 1) * 512, :].rearrange("(ft p) t -> p ft t", p=128),
                      [y_dst[2 * r + hh_]], [yc], yc)
            dst = ygo.t[:, rf * 4:(rf + 1) * 4, :]
            if r == 0:
                P.op("dve", lambda h, yc=yc, dst=dst: h.tensor_scalar(dst, yc.t[:], sel.t[:, 0:1], None, ALU.mult), [yc, sel], [ygo])
            else:
                P.op("dve", lambda h, yc=yc, dst=dst, r=r: h.scalar_tensor_tensor(dst, yc.t[:], sel.t[:, r:r + 1], dst, ALU.mult, ALU.add), [yc, sel, ygo], [ygo])
    if STOP == 'G1':
        P.barrier()
        return
    for nt in range(16):
        wa = wt2[2 * (nt % 2)]
        wb_ = wt2[2 * (nt % 2) + 1]
        P.dma("pool", wa.t[:], IN["w_glu"].ap()[:, nt * 128:(nt + 1) * 128].rearrange("(kt p) c -> p kt c", p=128), [], [wa], wa)
        P.dma("pool", wb_.t[:], IN["w_in_c"].ap()[:, 2048 + nt * 128:2048 + (nt + 1) * 128].rearrange("(kt p) c -> p kt c", p=128), [], [wb_], wb_)
        for (c0, n) in [(0, 512), (512, 512), (1024, 8)]:
            proj_feat(wa, None, lambda pk, c0=c0, n=n, nt=nt: P.op("act", lambda h: h.activation(gl.t[:, c0:c0 + n], pk.t[:, 0:n], AF.Sigmoid, bias=bglu.t[:, nt:nt + 1], scale=1.0), [pk, bglu], [gl]),
                      ygo, lambda kt, c0=c0, n=n: ygo.t[:, kt, c0:c0 + n], n)
            def zev(pk, c0=c0, n=n):
                P.op("act", lambda h: h.activation(zz.t[:, c0:c0 + n], pk.t[:, 0:n], AF.Sigmoid), [pk], [zz])
                P.op("dve", lambda h: h.tensor_tensor(zz.t[:, c0:c0 + n], zz.t[:, c0:c0 + n], pk.t[:, 0:n], ALU.mult), [zz, pk], [zz])
            proj_feat(wb_, None, zev, hn1o, lambda kt, c0=c0, n=n: hn1o.t[:, kt, c0:c0 + n], n)
        P.op("dve", lambda h, nt=nt: h.tensor_tensor(gl.t[:], gl.t[:], ygo.t[:, nt, :], ALU.mult), [gl, ygo], [gl])
        P.op("dve", lambda h, nt=nt: h.tensor_tensor(y2T.t[:, nt, 0:NCH], gl.t[:], zz.t[:], ALU.mult), [gl, zz], [y2T])
    if STOP == 'G2':
        P.barrier()
        return
    P.barrier()
    P.off = F0
    wgb2 = [P.sb("wgc%d" % i, [128, KT, 512], BF16) for i in range(2)]
    h1s = [P.sb("h1f%d" % i, [128, D], F32) for i in range(5)]
    junk = P.sb("junk3", [128, D], F32); ss = P.sb("ss3", [128, 4], F32)
    yo = [P.sb("yo%d" % i, [128, D], F32) for i in range(2)]
    load_g("g_fin")
    wi = 0
    for tiles in [list(range(0, 5)), list(range(5, 9))]:
        for si, tj in enumerate(tiles):
            o0 = tj * 128
            P.dma("sp", h1s[si].t[:], h1_scr.t.ap()[o0:o0 + 128, :], [h1_scr], [h1s[si]], h1s[si])
        for g4 in range(4):
            wg = wgb2[wi % 2]
            wi += 1
            P.dma("pool", wg.t[:], IN["w_out_c"].ap()[:, g4 * 512:(g4 + 1) * 512].rearrange("(kt p) c -> p kt c", p=128), [], [wg], wg)
            for si, tj in enumerate(tiles):
                o0 = tj * 128
                ht_ = h1s[si]
                pk = next_pf()
                fns = [(lambda h, pk=pk, kt=kt, o0=o0, wg=wg: h.matmul(pk.t[:], y2T.t[:, kt, o0:o0 + 128], wg.t[:, kt, :],
                                                                         start=(kt == 0), stop=(kt == KT - 1))) for kt in range(KT)]
                P.mm(fns, [y2T, wg], [pk])
                P.op("dve", lambda h, pk=pk, ht_=ht_, g4=g4: h.tensor_tensor(ht_.t[:, g4 * 512:(g4 + 1) * 512], ht_.t[:, g4 * 512:(g4 + 1) * 512], pk.t[:], ALU.add),
                     [pk, ht_], [ht_])
        for si, tj in enumerate(tiles):
            o0 = tj * 128
            ht_ = h1s[si]
            yo_ = yo[tj % 2]
            rmsnorm_rows(ht_, yo_, ss, junk)
            if tj < 8:
                P.dma("sp", OUT["yp"].ap()[o0:o0 + 128, :], yo_.t[:], [yo_], [outbufs["yp"]], yo_)
            else:
                P.dma("sp", OUT["ys"].ap(), yo_.t[:], [yo_], [outbufs["ys"]], yo_)
    P.barrier()


_NC_CACHE = {}


def _rope_tables(pos):
    half = 64
    inv = (np.float32(10000.0) ** (-np.arange(half, dtype=np.float32) / np.float32(half))).astype(np.float32)
    ang = pos.astype(np.float32)[:, None] * inv[None, :]
    return np.cos(ang).astype(np.float32), np.sin(ang).astype(np.float32)


def kernel(x_prompt, x_sample, cache_win_k, cache_win_v, state_conv, state_ssm_re, state_ssm_im,
           attn_norm, w_in_ab, conv_w, w_out_ab, ssm_norm, w_in_c, lam_re, lam_im, log_step,
           b_re, b_im, c_re, c_im, d_skip, w_glu, b_glu, w_out_c, final_norm):
    f = lambda a: np.ascontiguousarray(np.asarray(a, dtype=np.float32))
    x_prompt, x_sample = f(x_prompt), f(x_sample)
    cache_win_k, cache_win_v, state_conv = f(cache_win_k), f(cache_win_v), f(state_conv)
    state_ssm_re, state_ssm_im = f(state_ssm_re), f(state_ssm_im)
    w_in_ab0, w_out_ab0, w_in_c0, w_glu0, w_out_c0 = f(w_in_ab)[0], f(w_out_ab)[0], f(w_in_c)[0], f(w_glu)[0], f(w_out_c)[0]
    lam_re, lam_im, log_step = f(lam_re)[0], f(lam_im)[0], f(log_step)[0]
    b_re, b_im, c_re, c_im = f(b_re)[0], f(b_im)[0], f(c_re)[0], f(c_im)[0]
    d_skip0, b_glu0 = f(d_skip)[0], f(b_glu)[0]
    if "nc" not in _NC_CACHE:
        _NC_CACHE["nc"] = build_nc()
    nc = _NC_CACHE["nc"]

    kk = np.arange(128)[:, None]
    qq_ = np.arange(512)[None, :]
    maskp = np.stack([mult_of(qq_ - ((i - 16) * 128 + kk)) for i in range(20)], 1)
    rows = np.arange(2176).reshape(17, 128)
    s_ = np.arange(128)[None, :]
    masks = np.zeros((128, 17, 128), np.float32)
    for i in range(17):
        row = rows[i][:, None]
        m = mult_of(2048 + s_ - row)
        m[:, 8:] = ((2048 + s_[:, 8:] - row) == 0)
        masks[:, i, :] = m
    iota = np.broadcast_to(np.arange(1024, dtype=np.float32)[None, :], (128, 1024)).copy()
    rmask = np.zeros((128, 8), np.float32)
    for p in range(128):
        rmask[p, (p // 32) * 2 + (p % 32) // 16] = 1.0
    smask = np.zeros((128, 2), np.float32)
    smask[:64, 0] = 1.0
    smask[64:, 1] = 1.0
    bc = lambda v: np.ascontiguousarray(np.broadcast_to(v[None, :], (128, v.shape[0])))

    in_maps = []
    for c in range(8):
        b, r = c // 4, c % 4
        T0 = r * NOWN
        xh = np.zeros((NTP, D), np.float32)
        lo = T0 - NHALO
        src_lo = max(lo, 0)
        xh[src_lo - lo:] = x_prompt[b, src_lo:T0 + NOWN]
        pos = np.concatenate([np.arange(lo, T0 + NOWN), PAST + np.arange(128)]).astype(np.float32)
        valid = (pos[:NTP] >= 0).astype(np.float32)
        cosv, sinv = _rope_tables(np.maximum(pos, 0))
        xs = np.zeros((128, D), np.float32)
        xs[:8] = x_sample[c]
        g0 = 32 * r
        gs = slice(g0, g0 + 32)
        st_lay = lambda a: np.ascontiguousarray(a.reshape(16, 2, 64).transpose(1, 2, 0).reshape(128, 16))
        def row_lay_rep(a):
            t = a.reshape(4, 4, 2, 64)
            t = np.broadcast_to(t[:, :, :, None, :], (4, 4, 2, 16, 64))
            return np.ascontiguousarray(t.transpose(1, 2, 3, 0, 4).reshape(128, 4, 64))
        def row_lay_b(a):
            t = a.reshape(4, 4, 2, 64, 16)
            return np.ascontiguousarray(t.transpose(1, 2, 4, 0, 3).reshape(128, 4, 64))
        def st_lay_c(a):
            t = a.reshape(16, 2, 16, 64)
            return np.ascontiguousarray(t.transpose(1, 3, 0, 2).reshape(128, 16, 16))
        lst32 = np.broadcast_to(log_step[gs][:, None], (32, 64))
        sel = np.zeros((128, 4), np.float32)
        sel[:, r] = 1.0
        sre0 = np.stack([st_lay(state_ssm_re[0, 4 * b + i, gs]) for i in range(4)], 1)
        sim0 = np.stack([st_lay(state_ssm_im[0, 4 * b + i, gs]) for i in range(4)], 1)
        w_in_c_rolled = np.concatenate([w_in_c0[:, 512 * r:512 * (r + 1)], w_in_c0[:, 512:2048], w_in_c0[:, 2048:]], 1)
        m = {
            "xh": xh, "xs": xs,
            "cs": np.ascontiguousarray(cosv.reshape(25, 128, 64).transpose(1, 0, 2)),
            "sn": np.ascontiguousarray(sinv.reshape(25, 128, 64).transpose(1, 0, 2)),
            "valid": np.ascontiguousarray(valid.reshape(24, 128).T),
            "ck": np.ascontiguousarray(cache_win_k[0, c].reshape(2048, 1024)),
            "cv": np.ascontiguousarray(cache_win_v[0, c].reshape(2048, 1024)),
            "sconv": np.ascontiguousarray(state_conv[0, c].reshape(2, 8, 128).transpose(2, 1, 0)),
            "g_attn": bc(f(attn_norm)[0]), "g_ssm": bc(f(ssm_norm)[0]), "g_fin": bc(f(final_norm)),
            "w_in_ab": w_in_ab0, "cw": np.ascontiguousarray(f(conv_w)[0].reshape(3, 8, 128).transpose(2, 1, 0)),
            "w_out_ab": w_out_ab0, "w_in_c": np.ascontiguousarray(w_in_c_rolled),
            "w_glu": w_glu0, "w_out_c": w_out_c0,
            "bglu": np.ascontiguousarray(b_glu0.reshape(16, 128).T),
            "dsk": np.ascontiguousarray(d_skip0[512 * r:512 * (r + 1)].reshape(4, 128).T),
            "maskp": maskp, "masks": masks,
            "lre_s": st_lay(lam_re[gs]), "lim_s": st_lay(lam_im[gs]), "lst_s": st_lay(lst32),
            "lre_r": row_lay_rep(lam_re[gs]), "lim_r": row_lay_rep(lam_im[gs]), "lst_r": row_lay_rep(np.ascontiguousarray(lst32)),
            "bre_r": row_lay_b(b_re[gs]), "bim_r": row_lay_b(b_im[gs]),
            "cre_s": st_lay_c(c_re[gs]), "cim_s": st_lay_c(c_im[gs]),
            "rmask": rmask, "smask": smask, "sel": sel, "sre0": sre0, "sim0": sim0, "iota": iota,
        }
        in_maps.append({k: np.ascontiguousarray(v, dtype=np.float32) for k, v in m.items()})

    res = run_bass_kernel_spmd(nc, in_maps, core_ids=list(range(8)))
    R = res.results
    _NC_CACHE['raw'] = R
    y_prompt = np.zeros((2, SEQ, D), np.float32)
    y_sample = np.zeros((8, 8, D), np.float32)
    kp = np.zeros((1, 2, 2048, 8, 128), np.float32)
    vp = np.zeros((1, 2, 2048, 8, 128), np.float32)
    convp = np.zeros((1, 2, 2, 1024), np.float32)
    srp = np.zeros((1, 2, 128, 64), np.float32)
    sip = np.zeros((1, 2, 128, 64), np.float32)
    ks = np.zeros((1, 8, 8, 8, 128), np.float32)
    vs = np.zeros((1, 8, 8, 8, 128), np.float32)
    convs = np.zeros((1, 8, 2, 1024), np.float32)
    srs = np.zeros((1, 8, 128, 64), np.float32)
    sis = np.zeros((1, 8, 128, 64), np.float32)
    unst = lambda a: a.reshape(2, 64, 16).transpose(2, 0, 1).reshape(32, 64)
    for c in range(8):
        b, r = c // 4, c % 4
        o = R[c]
        y_prompt[b, r * NOWN:(r + 1) * NOWN] = o["yp"]
        y_sample[c] = o["ys"][:8]
        if r >= 2:
            kp[0, b, (r - 2) * NOWN:(r - 1) * NOWN] = o["kp"].reshape(NOWN, 8, 128)
            vp[0, b, (r - 2) * NOWN:(r - 1) * NOWN] = o["vp"].reshape(NOWN, 8, 128)
        if r == 3:
            convp[0, b] = o["convp"].transpose(2, 1, 0).reshape(2, 1024)
        ks[0, c] = o["ks"][:8].reshape(8, 8, 128)
        vs[0, c] = o["vs"][:8].reshape(8, 8, 128)
        convs[0, c] = o["convs"].transpose(2, 1, 0).reshape(2, 1024)
        srp[0, b, 32 * r:32 * (r + 1)] = unst(o["ssm_p"][:, 0, :])
        sip[0, b, 32 * r:32 * (r + 1)] = unst(o["ssm_p"][:, 1, :])
        for i in range(4):
            srs[0, 4 * b + i, 32 * r:32 * (r + 1)] = unst(o["ssm_s"][:, i, 0, :])
            sis[0, 4 * b + i, 32 * r:32 * (r + 1)] = unst(o["ssm_s"][:, i, 1, :])
    return (y_prompt, y_sample, kp, vp, convp, srp, sip, ks, vs, convs, srs, sis)
```

```python
import math
import os
STOP = os.environ.get('MK_STOP', '')
from contextlib import ExitStack

import numpy as np
import concourse.bass as bass
import concourse.mybir as mybir
from concourse.bass_utils import run_bass_kernel_spmd

F32 = mybir.dt.float32
BF16 = mybir.dt.bfloat16
ALU = mybir.AluOpType
AF = mybir.ActivationFunctionType
AX = mybir.AxisListType

ENGS = ["pe", "act", "dve", "pool", "sp"]
D = 2048
KT = 16
NOWN = 1024
NHALO = 2048
NTP = NOWN + NHALO
NTILE_P = NTP // 128
NO = NOWN + 128
SEQ = 4096
PAST = 16384
NCH = 1032
TWO_PI = 2.0 * math.pi


class Buf:
    def __init__(self, t, name):
        self.t = t
        self.name = name
        self.w = {}
        self.r = {}
        self.dsem = None
        self.dcnt = 0


class Prog:
    def __init__(self, nc, stack):
        self.nc = nc
        self.stack = stack
        self.q = {e: [] for e in ENGS}
        self.cnt = {e: 0 for e in ENGS}
        self.seen = {e: {} for e in ENGS}
        self.sems = {}
        self.semval = {}
        for e in ["pe", "act", "dve", "pool"]:
            self.sems[e] = stack.enter_context(nc.semaphore("s_" + e))
        self.off = 16512
        self.free = []
        self.dval = {}
        self.phase_bufs = []

    def sb(self, name, shape, dt, at=None):
        nbytes = int(np.prod(shape[1:])) * (2 if dt == BF16 else 4)
        if at is None:
            at = self.off
            self.off = (at + nbytes + 63) // 64 * 64
        assert at + nbytes <= 229300, (name, at, nbytes)
        t = self.nc.alloc_sbuf_tensor_at(name, list(shape), dt, offset=at)
        b = Buf(t, name)
        b.at = at
        b.nbytes = nbytes
        return b

    def ps(self, name, shape, dt=F32):
        t = self.stack.enter_context(self.nc.psum_tensor(name, list(shape), dt))
        return Buf(t, name)

    def dram(self, name, shape, dt, kind="Internal"):
        t = self.nc.dram_tensor(name, list(shape), dt, kind=kind)
        return Buf(t, name)

    def _need(self, eng, k, v, waits):
        if self.seen[eng].get(k, 0) >= v:
            return
        waits[k] = max(waits.get(k, 0), v)

    def _deps(self, eng, reads, writes):
        waits = {}
        for b in reads:
            for k, v in b.w.items():
                self._need(eng, k, v, waits)
        for b in writes:
            for k, v in b.w.items():
                self._need(eng, k, v, waits)
            for k, v in b.r.items():
                self._need(eng, k, v, waits)
        for k, v in waits.items():
            self.seen[eng][k] = v
        return [(self.sems[k], v) for k, v in waits.items()]

    def _commit(self, k, v, reads, writes):
        self.semval[k] = v
        for b in reads:
            b.r[k] = max(b.r.get(k, 0), v)
        for b in writes:
            b.w[k] = max(b.w.get(k, 0), v)
            b.r = {}

    def op(self, eng, fn, reads=(), writes=()):
        reads = [b for b in reads if b is not None]
        writes = [b for b in writes if b is not None]
        wl = self._deps(eng, reads, writes)
        self.cnt[eng] += 1
        sem = self.sems[eng]

        def emit(h, fn=fn, wl=wl, sem=sem):
            for s, v in wl:
                h.wait_ge(s, v)
            fn(h).then_inc(sem, 1)

        self.q[eng].append(emit)
        self._commit(eng, self.cnt[eng], reads, writes)

    def mm(self, fns, reads, writes):
        eng = "pe"
        wl = self._deps(eng, reads, writes)
        self.cnt[eng] += 1
        sem = self.sems[eng]

        def emit(h, fns=fns, wl=wl, sem=sem):
            for s, v in wl:
                h.wait_ge(s, v)
            for f in fns[:-1]:
                f(h)
            fns[-1](h).then_inc(sem, 1)

        self.q[eng].append(emit)
        self._commit(eng, self.cnt[eng], reads, writes)

    def dma(self, eng, out, in_, reads, writes, semb, **kw):
        reads = [b for b in reads if b is not None]
        writes = [b for b in writes if b is not None]
        if semb.dsem is None:
            if self.free:
                key = self.free.pop()
            else:
                key = "d%d" % len(self.sems)
                self.sems[key] = self.stack.enter_context(self.nc.semaphore(key))
            semb.dsem = key
            semb.dcnt = self.dval.get(key, 0)
            self.phase_bufs.append(semb)
        wl = self._deps(eng, reads, writes)
        semb.dcnt += 16
        self.dval[semb.dsem] = semb.dcnt
        sem = self.sems[semb.dsem]

        def emit(h, wl=wl, sem=sem, out=out, in_=in_, kw=kw):
            for s, v in wl:
                h.wait_ge(s, v)
            h.dma_start(out=out, in_=in_, **kw).then_inc(sem, 16)

        self.q[eng].append(emit)
        self._commit(semb.dsem, semb.dcnt, reads, writes)

    def coll(self, src, dst, groups):
        key = "c%d" % len(self.sems)
        self.sems[key] = self.stack.enter_context(self.nc.semaphore(key))
        wl = self._deps("pool", [src], [dst])
        sem = self.sems[key]

        def emit(h, wl=wl, sem=sem):
            for s, v in wl:
                h.wait_ge(s, v)
            h.collective_compute("AllGather", ALU.bypass, replica_groups=groups,
                                 ins=[src.t.ap()], outs=[dst.t.ap()]).then_inc(sem)

        self.q["pool"].append(emit)
        self._commit(key, 1, [src], [dst])

    def barrier(self):
        for b in self.phase_bufs:
            self.free.append(b.dsem)
            b.dsem = None
        self.phase_bufs = []
        items = list(self.semval.items())
        for e in ENGS:
            wl = []
            for k, v in items:
                if self.seen[e].get(k, 0) < v:
                    self.seen[e][k] = v
                    wl.append((self.sems[k], v))

            def emit(h, wl=wl):
                for s, v in wl:
                    h.wait_ge(s, v)

            if wl:
                self.q[e].append(emit)

    def run(self):
        nc = self.nc
        with nc.Block() as block:
            @block.tensor
            def _(h):
                for f in self.q["pe"]:
                    f(h)

            @block.scalar
            def _(h):
                for f in self.q["act"]:
                    f(h)

            @block.vector
            def _(h):
                for f in self.q["dve"]:
                    f(h)

            @block.gpsimd
            def _(h):
                for f in self.q["pool"]:
                    f(h)

            @block.sync
            def _(h):
                for f in self.q["sp"]:
                    f(h)


def mult_of(d):
    d = np.asarray(d)
    m = ((d >= 0) & (d <= 128)).astype(np.float32)
    m += ((d >= 0) & (d <= 512) & (d % 4 == 0))
    m += ((d >= 0) & (d <= 2048) & (d % 16 == 0))
    return m.astype(np.float32)


IN_SPECS = [
    ("xh", [NTP, D]), ("xs", [128, D]), ("cs", [128, 25, 64]), ("sn", [128, 25, 64]),
    ("valid", [128, 24]), ("ck", [2048, 1024]), ("cv", [2048, 1024]), ("sconv", [128, 8, 2]),
    ("g_attn", [128, D]), ("g_ssm", [128, D]), ("g_fin", [128, D]),
    ("w_in_ab", [D, 8192]), ("cw", [128, 8, 3]), ("w_out_ab", [D, D]), ("w_in_c", [D, 4096]),
    ("w_glu", [D, D]), ("w_out_c", [D, D]), ("bglu", [128, 16]), ("dsk", [128, 4]),
    ("maskp", [128, 20, 512]), ("masks", [128, 17, 128]),
    ("lre_s", [128, 16]), ("lim_s", [128, 16]), ("lst_s", [128, 16]),
    ("lre_r", [128, 4, 64]), ("lim_r", [128, 4, 64]), ("lst_r", [128, 4, 64]),
    ("bre_r", [128, 4, 64]), ("bim_r", [128, 4, 64]),
    ("cre_s", [128, 16, 16]), ("cim_s", [128, 16, 16]),
    ("rmask", [128, 8]), ("smask", [128, 2]), ("sel", [128, 4]),
    ("sre0", [128, 4, 16]), ("sim0", [128, 4, 16]), ("iota", [128, 1024]),
]
OUT_SPECS = [
    ("yp", [NOWN, D]), ("ys", [128, D]), ("kp", [NOWN, 1024]), ("vp", [NOWN, 1024]),
    ("convp", [128, 8, 2]), ("ssm_p", [128, 2, 16]), ("ks", [128, 1024]), ("vs", [128, 1024]),
    ("convs", [128, 8, 2]), ("ssm_s", [128, 4, 2, 16]),
]


def build_nc():
    nc = bass.Bass("TRN2", target_bir_lowering=False)
    IN = {}
    for n, s in IN_SPECS:
        IN[n] = nc.dram_tensor(n, s, F32, kind="ExternalInput")
    OUT = {}
    for n, s in OUT_SPECS:
        OUT[n] = nc.dram_tensor(n, s, F32, kind="ExternalOutput")
    st = ExitStack()
    with st:
        P = Prog(nc, st)
        build_program(nc, P, IN, OUT)
        P.run()
    return nc


def build_program(nc, P, IN, OUT):
    GROUPS = [[0, 1, 2, 3], [4, 5, 6, 7]]
    outbufs = {n: Buf(OUT[n], n) for n in OUT}
    kT_scr = P.dram("kT_scr", [8, 128, NTP], BF16)
    v_scr = P.dram("v_scr", [NTP, 1024], BF16)
    kTs_scr = P.dram("kTs_scr", [8, 128, 2176], BF16)
    vs_scr = P.dram("vs_scr", [2176, 1024], BF16)
    qT_scr = P.dram("qT_scr", [8, 128, NO], BF16)
    h1_scr = P.dram("h1_scr", [NO, D], F32)
    hn_src = [P.dram("hn_src%d" % j, [256, NCH], BF16) for j in range(8)]
    hn_dst = [P.dram("hn_dst%d" % j, [4 * 256, NCH], BF16) for j in range(8)]
    y_src = [P.dram("y_src%d" % j, [512, 516], BF16) for j in range(8)]
    y_dst = [P.dram("y_dst%d" % j, [4 * 512, 516], BF16) for j in range(8)]

    pf = [P.ps("pf%d" % i, [128, 512], F32) for i in range(6)]
    pb = [P.ps("pb%d" % i, [128, 8, 128], BF16) for i in range(2)]
    pfi = [0]
    pbi = [0]

    def next_pf():
        pfi[0] = (pfi[0] + 1) % 4
        return pf[pfi[0]]

    def next_pb():
        pbi[0] = (pbi[0] + 1) % 2
        return pb[pbi[0]]

    ident = P.sb("ident", [128, 128], BF16)
    P.op("pool", lambda h: h.memset(ident.t[:], 1.0), [], [ident])
    P.op("pool", lambda h: h.affine_select(ident.t[:], ident.t[:], [[-1, 128]], ALU.is_equal, 0.0,
                                            base=0, channel_multiplier=1), [ident], [ident])
    ones_bf = P.sb("ones_bf", [128, 128], BF16)
    P.op("pool", lambda h: h.memset(ones_bf.t[:], 1.0), [], [ones_bf])
    gt = P.sb("gt", [128, D], F32)
    cs = P.sb("cs", [128, 25, 64], F32)
    sn = P.sb("sn", [128, 25, 64], F32)
    valid = P.sb("valid", [128, 24], F32)
    validB = P.sb("validB", [128, 24, 128], BF16)
    cw = P.sb("cw", [128, 8, 3], F32)
    bglu = P.sb("bglu", [128, 16], F32)
    dsk = P.sb("dsk", [128, 4], F32)
    sel = P.sb("sel", [128, 4], F32)
    eps_t = P.sb("eps_t", [128, 1], F32)
    P.op("pool", lambda h: h.memset(eps_t.t[:], 1e-6), [], [eps_t])
    for b_, n in [(cs, "cs"), (sn, "sn"), (valid, "valid"), (cw, "cw"), (bglu, "bglu"), (dsk, "dsk"), (sel, "sel")]:
        P.dma("sp", b_.t[:], IN[n].ap(), [], [b_], b_)
    P.op("dve", lambda h: h.tensor_copy(validB.t[:], valid.t[:].unsqueeze(2).to_broadcast([128, 24, 128])),
         [valid], [validB])
    pospi = P.sb("pospi", [128, 1], F32)
    P.op("pool", lambda h: h.memset(pospi.t[:], math.pi), [], [pospi])
    CONST_END = P.off
    hnT_o = P.sb("hnT_o", [128, KT, NO], BF16)

    def load_g(name):
        P.dma("sp", gt.t[:], IN[name].ap(), [], [gt], gt)

    def rmsnorm_rows(xt, xn, ss, junk):
        P.op("act", lambda h: h.activation(junk.t[:], xt.t[:], AF.Square, accum_out=ss.t[:, 0:1]), [xt], [junk, ss])
        P.op("act", lambda h: h.activation(ss.t[:, 1:2], ss.t[:, 0:1], AF.Sqrt, bias=eps_t.t[:, 0:1], scale=1.0 / D), [ss, eps_t], [ss])
        P.op("dve", lambda h: h.reciprocal(ss.t[:, 2:3], ss.t[:, 1:2]), [ss], [ss])
        P.op("dve", lambda h: h.scalar_tensor_tensor(xn.t[:], xt.t[:], ss.t[:, 2:3], gt.t[:], ALU.mult, ALU.mult),
             [xt, ss, gt], [xn])

    def transpose_rows(xn, dst, dst_ap_fn):
        for half in range(2):
            p = next_pb()
            fns = []
            for j in range(8):
                kt = half * 8 + j
                fns.append(lambda h, p=p, j=j, kt=kt: h.transpose(p.t[:, j, :], xn.t[:, kt * 128:(kt + 1) * 128], ident.t[:]))
            P.mm(fns, [xn, ident], [p])
            P.op("act", lambda h, p=p, half=half: h.activation(dst_ap_fn(half), p.t[:], AF.Identity), [p], [dst])

    A0 = P.off
    wkv = P.sb("wkv", [128, KT, 2048], BF16)
    xts = [P.sb("xt%d" % i, [128, D], F32) for i in range(3)]
    xns = [P.sb("xn%d" % i, [128, D], BF16) for i in range(3)]
    hts = [P.sb("ht%d" % i, [128, KT, 128], BF16) for i in range(3)]
    ss = P.sb("ss", [128, 4], F32)
    krs = [P.sb("kr%d" % i, [128, 1024], F32) for i in range(2)]
    vfs = [P.sb("vf%d" % i, [128, 1024], F32) for i in range(2)]
    t1 = P.sb("t1", [128, 256], F32)
    t2 = P.sb("t2", [128, 256], F32)
    krbs = [P.sb("krb%d" % i, [128, 1024], BF16) for i in range(2)]
    vbs = [P.sb("vb%d" % i, [128, 1024], BF16) for i in range(2)]
    kTts = [P.sb("kTt%d" % i, [128, 8, 128], BF16) for i in range(2)]
    kTt = kTts[0]
    hprev2 = P.sb("hprev2", [128, KT, 2], BF16)
    A1_END = P.off

    load_g("g_attn")
    for half in range(2):
        P.dma("pool", wkv.t[:, :, half * 1024:(half + 1) * 1024],
              IN["w_in_ab"].ap()[:, 1024 + half * 1024:2048 + half * 1024].rearrange("(kt p) c -> p kt c", p=128),
              [], [wkv], wkv)

    def rotary(pk, ti, dst, c0):
        v = pk.t[:].rearrange("p (h two d) -> p h two d", h=4, two=2)
        o = dst.t[:, c0:c0 + 512].rearrange("p (h two d) -> p h two d", h=4, two=2)
        cb = cs.t[:, ti, :].unsqueeze(1).to_broadcast([128, 4, 64])
        sb_ = sn.t[:, ti, :].unsqueeze(1).to_broadcast([128, 4, 64])
        a = t1.t[:].rearrange("p (h d) -> p h d", h=4)
        b = t2.t[:].rearrange("p (h d) -> p h d", h=4)
        P.op("dve", lambda h: h.tensor_tensor(a, v[:, :, 0, :], cb, ALU.mult), [pk, cs], [t1])
        P.op("dve", lambda h: h.tensor_tensor(b, v[:, :, 1, :], sb_, ALU.mult), [pk, sn], [t2])
        P.op("dve", lambda h: h.tensor_tensor(o[:, :, 0, :], a, b, ALU.subtract), [t1, t2], [dst])
        P.op("dve", lambda h: h.tensor_tensor(a, v[:, :, 1, :], cb, ALU.mult), [pk, cs], [t1])
        P.op("dve", lambda h: h.tensor_tensor(b, v[:, :, 0, :], sb_, ALU.mult), [pk, sn], [t2])
        P.op("dve", lambda h: h.tensor_tensor(o[:, :, 1, :], a, b, ALU.add), [t1, t2], [dst])

    def store_kT(src_bf, scr, col0, kb=None):
        p = next_pb()
        if kb is None:
            kb = kTt
        fns = [(lambda h, p=p, j=j: h.transpose(p.t[:, j, :], src_bf.t[:, j * 128:(j + 1) * 128], ident.t[:])) for j in range(8)]
        P.mm(fns, [src_bf, ident], [p])
        P.op("act", lambda h, p=p, kb=kb: h.activation(kb.t[:], p.t[:], AF.Identity), [p], [kb])
        P.dma("sp", scr.t.ap()[:, :, col0:col0 + 128].rearrange("h d t -> d h t"), kb.t[:], [kb], [scr], kb)

    def stageL(ti):
        xt = xts[ti % 3]
        src = IN["xh"].ap()[ti * 128:(ti + 1) * 128, :] if ti < 24 else IN["xs"].ap()
        P.dma("sp", xt.t[:], src, [], [xt], xt)

    def stageN(ti):
        rmsnorm_rows(xts[ti % 3], xns[ti % 3], ss, xns[ti % 3])

    def stageA2(ti):
        xn = xns[ti % 3]
        if ti < 16:
            ht = hts[ti % 3]
            transpose_rows(xn, ht, lambda half, ht=ht: ht.t[:, half * 8:(half + 1) * 8, :])
            if ti == 15:
                P.op("dve", lambda h, ht=ht: h.tensor_copy(hprev2.t[:], ht.t[:, :, 126:128]), [ht], [hprev2])
            return (lambda kt, ht=ht: ht.t[:, kt, :]), ht
        o0 = (ti - 16) * 128
        transpose_rows(xn, hnT_o, lambda half, o0=o0: hnT_o.t[:, half * 8:(half + 1) * 8, o0:o0 + 128])
        return (lambda kt, o0=o0: hnT_o.t[:, kt, o0:o0 + 128]), hnT_o

    def stageM(ti, lhs, hb):
        kr = krs[ti % 2]
        vf = vfs[ti % 2]
        for g4 in range(4):
            pk = next_pf()
            fns = [(lambda h, pk=pk, kt=kt, g4=g4, lhs=lhs: h.matmul(pk.t[:], lhs(kt), wkv.t[:, kt, g4 * 512:(g4 + 1) * 512],
                                                                    start=(kt == 0), stop=(kt == KT - 1))) for kt in range(KT)]
            P.mm(fns, [hb, wkv], [pk])
            if g4 < 2:
                rotary(pk, ti, kr, g4 * 512)
            else:
                c0 = (g4 - 2) * 512
                P.op("act", lambda h, pk=pk, c0=c0, vf=vf: h.activation(vf.t[:, c0:c0 + 512], pk.t[:], AF.Identity), [pk], [vf])

    def stageKpre(ti):
        kr = krs[ti % 2]
        vf = vfs[ti % 2]
        krb = krbs[ti % 2]
        vb = vbs[ti % 2]
        P.op("act", lambda h: h.activation(krb.t[:], kr.t[:], AF.Identity), [kr], [krb])
        if ti < 24:
            P.op("dve", lambda h: h.tensor_scalar(vb.t[:], vf.t[:], valid.t[:, ti:ti + 1], None, ALU.mult), [vf, valid], [vb])
        else:
            P.op("dve", lambda h: h.tensor_copy(vb.t[:], vf.t[:]), [vf], [vb])

    def stageKpost(ti):
        kr = krs[ti % 2]
        vf = vfs[ti % 2]
        krb = krbs[ti % 2]
        vb = vbs[ti % 2]
        kb = kTts[ti % 2]
        if ti < 24:
            store_kT(krb, kT_scr, ti * 128, kb)
            P.dma("sp", v_scr.t.ap()[ti * 128:(ti + 1) * 128, :], vb.t[:], [vb], [v_scr], vb)
            if ti >= 16:
                r0 = (ti - 16) * 128
                P.dma("sp", OUT["kp"].ap()[r0:r0 + 128, :], kr.t[:], [kr], [outbufs["kp"]], kr)
                P.dma("sp", OUT["vp"].ap()[r0:r0 + 128, :], vf.t[:], [vf], [outbufs["vp"]], vf)
        else:
            store_kT(krb, kTs_scr, 2048, kb)
            P.dma("sp", vs_scr.t.ap()[2048:2176, :], vb.t[:], [vb], [vs_scr], vb)
            P.dma("sp", OUT["ks"].ap(), kr.t[:], [kr], [outbufs["ks"]], kr)
            P.dma("sp", OUT["vs"].ap(), vf.t[:], [vf], [outbufs["vs"]], vf)

    for t_ in range(3):
        stageL(t_)
    stageN(0)
    stageN(1)
    infoA = {0: stageA2(0)}
    for ti in range(25):
        if ti + 3 < 25:
            stageL(ti + 3)
        if ti + 2 < 25:
            stageN(ti + 2)
        if ti >= 1:
            stageKpre(ti - 1)
        if ti + 1 < 25:
            infoA[ti + 1] = stageA2(ti + 1)
        stageM(ti, *infoA[ti])
        if ti >= 1:
            stageKpost(ti - 1)
    stageKpre(24)
    stageKpost(24)
    def cacheL(ti):
        xt = xts[ti % 3]
        P.dma("sp", xt.t[:, 0:1024], IN["ck"].ap()[ti * 128:(ti + 1) * 128, :], [], [xt], xt)
        P.dma("sp", xt.t[:, 1024:2048], IN["cv"].ap()[ti * 128:(ti + 1) * 128, :], [], [xt], xt)

    cacheL(0)
    cacheL(1)
    for ti in range(16):
        if ti + 2 < 16:
            cacheL(ti + 2)
        xt = xts[ti % 3]
        krb = krbs[ti % 2]
        vb = vbs[ti % 2]
        P.op("act", lambda h, xt=xt, krb=krb: h.activation(krb.t[:], xt.t[:, 0:1024], AF.Identity), [xt], [krb])
        P.op("dve", lambda h, xt=xt, vb=vb: h.tensor_copy(vb.t[:], xt.t[:, 1024:2048]), [xt], [vb])
        store_kT(krb, kTs_scr, ti * 128, kTts[ti % 2])
        P.dma("sp", vs_scr.t.ap()[ti * 128:(ti + 1) * 128, :], vb.t[:], [vb], [vs_scr], vb)

    if STOP == 'A1':
        P.barrier()
        return
    P.barrier()
    P.off = A0
    wq = P.sb("wq", [128, KT, 1024], BF16)
    hprev2b = P.sb("hprev2b", [128, KT, 2], BF16)
    qf = P.sb("qf", [128, 1024], F32)
    qb = P.sb("qb", [128, 1024], BF16)
    t1 = P.sb("t1b", [128, 256], F32)
    t2 = P.sb("t2b", [128, 256], F32)
    kTt = P.sb("kTtb", [128, 8, 128], BF16)
    hprev2k = P.sb("hprev2k", [128, KT, 2], BF16, at=hprev2.at)
    hprev2k.w = dict(hprev2.w)
    P.dma("pool", wq.t[:], IN["w_in_ab"].ap()[:, 0:1024].rearrange("(kt p) c -> p kt c", p=128), [], [wq], wq)
    for tj in range(9):
        ti = 16 + tj
        o0 = tj * 128
        for g2_ in range(2):
            pk = next_pf()
            fns = [(lambda h, pk=pk, kt=kt, g2_=g2_, o0=o0: h.matmul(pk.t[:], hnT_o.t[:, kt, o0:o0 + 128],
                                                                      wq.t[:, kt, g2_ * 512:(g2_ + 1) * 512],
                                                                      start=(kt == 0), stop=(kt == KT - 1))) for kt in range(KT)]
            P.mm(fns, [hnT_o, wq], [pk])
            rotary(pk, ti, qf, g2_ * 512)
        P.op("act", lambda h: h.activation(qb.t[:], qf.t[:], AF.Identity), [qf], [qb])
        store_kT(qb, qT_scr, o0)

    if STOP == 'A2':
        P.barrier()
        return
    P.barrier()
    P.off = A0
    ocat = P.sb("ocat", [128, KT, NO], BF16)
    hp2 = P.sb("hp2", [128, KT, 2], BF16)
    B0 = P.off
    P.op("dve", lambda h: h.tensor_copy(hp2.t[:], hprev2k.t[:]), [hprev2k], [hp2])
    P.barrier()
    maskp = P.sb("maskp", [128, 20, 512], BF16)
    masks_ = P.sb("masks_", [128, 17, 128], BF16)
    P.dma("pool", maskp.t[:], IN["maskp"].ap(), [], [maskp], maskp)
    P.dma("pool", masks_.t[:], IN["masks"].ap(), [], [masks_], masks_)
    kTh = [P.sb("kTh%d" % i, [128, NTP], BF16) for i in range(1)] * 2
    vh = [P.sb("vh%d" % i, [128, 24, 128], BF16) for i in range(1)] * 2
    kTsh = [P.sb("kTsh%d" % i, [128, 2176], BF16) for i in range(1)] * 2
    vsh = [P.sb("vsh%d" % i, [128, 17, 128], BF16) for i in range(1)] * 2
    qTh = [P.sb("qTh%d" % i, [128, NO], BF16) for i in range(1)] * 2
    wt = [P.sb("wt%d" % i, [128, KT, 128], BF16) for i in range(4)]
    pts = [P.sb("pt%d" % i, [128, 512], BF16) for i in range(4)]
    ptm = [P.sb("ptm%d" % i, [128, 512], BF16) for i in range(4)]
    za = P.sb("za", [128, NO], F32)
    rl = P.sb("rl", [128, 512], F32)
    of = P.sb("of", [128, 512], F32)
    sg = P.sb("sg", [128, 512], F32)

    def silu_evac(pk, dstb, dst_ap, n):
        P.op("act", lambda h: h.activation(sg.t[:, 0:n], pk.t[:, 0:n], AF.Exp, scale=-1.0), [pk], [sg])
        P.op("dve", lambda h: h.tensor_scalar(sg.t[:, 0:n], sg.t[:, 0:n], 1.0, None, ALU.add), [sg], [sg])
        P.op("dve", lambda h: h.reciprocal(sg.t[:, 0:n], sg.t[:, 0:n]), [sg], [sg])
        P.op("dve", lambda h: h.tensor_tensor(dst_ap, pk.t[:, 0:n], sg.t[:, 0:n], ALU.mult), [pk, sg], [dstb])
    fb = [P.sb("fb%d" % i, [128, NO + 2], F32) for i in range(4)]
    convo_p = P.sb("convo_p", [128, 8, 2], F32)
    convo_s = P.sb("convo_s", [128, 8, 2], F32)
    sconv = P.sb("sconv", [128, 8, 2], F32)
    P.dma("sp", sconv.t[:], IN["sconv"].ap(), [], [sconv], sconv)
    scale = 128.0 ** -0.5

    def load_wt(i, c0):
        P.dma("pool", wt[i].t[:], IN["w_in_ab"].ap()[:, c0:c0 + 128].rearrange("(kt p) c -> p kt c", p=128), [], [wt[i]], wt[i])

    def proj_feat(wb, dst_ap_fn, evac, rhs_buf, rhs_fn, n):
        pk = next_pf()
        fns = [(lambda h, pk=pk, kt=kt: h.matmul(pk.t[:, 0:n], wb.t[:, kt, :], rhs_fn(kt), start=(kt == 0), stop=(kt == KT - 1)))
               for kt in range(KT)]
        P.mm(fns, [wb, rhs_buf], [pk])
        evac(pk)

    def attention(hh, qT, q0, nq, kT, vt, ktiles, mask_fn, vB_fn, o_dst_fn, zcol0):
        po = pf[4]
        pl = pf[5]
        nk = len(ktiles)
        LA = 3
        pms = {}

        def issue_S(i):
            kt_ = ktiles[i]
            ps_ = next_pf()
            P.mm([lambda h, ps_=ps_, kt_=kt_: h.matmul(ps_.t[:, 0:nq], kT.t[:, kt_ * 128:(kt_ + 1) * 128], qT.t[:, q0:q0 + nq],
                                                        start=True, stop=True)], [kT, qT], [ps_])
            pe_ = pts[i % 4]
            pm_ = ptm[i % 4]
            P.op("act", lambda h, ps_=ps_, pe_=pe_: h.activation(pe_.t[:, 0:nq], ps_.t[:, 0:nq], AF.Exp, scale=scale), [ps_], [pe_])
            mk, mb = mask_fn(i)
            eng = "dve"
            P.op(eng, lambda h, pe_=pe_, pm_=pm_, mk=mk: h.tensor_tensor(pm_.t[:, 0:nq], pe_.t[:, 0:nq], mk, ALU.mult), [pe_, mb], [pm_])
            pms[i] = pm_

        def issue_PV(i):
            kt_ = ktiles[i]
            pm_ = pms[i]
            vB, vBb = vB_fn(i)
            P.mm([lambda h, pm_=pm_, kt_=kt_, i=i: h.matmul(po.t[:, 0:nq], vt.t[:, kt_, :], pm_.t[:, 0:nq], start=(i == 0), stop=(i == nk - 1)),
                  lambda h, pm_=pm_, vB=vB, i=i: h.matmul(pl.t[:, 0:nq], vB, pm_.t[:, 0:nq], start=(i == 0), stop=(i == nk - 1))],
                 [vt, pm_, vBb], [po, pl])

        for i in range(min(LA, nk)):
            issue_S(i)
        for i in range(nk):
            if i + LA < nk:
                issue_S(i + LA)
            issue_PV(i)
        P.op("dve", lambda h: h.reciprocal(rl.t[:, 0:nq], pl.t[:, 0:nq]), [pl], [rl])
        P.op("dve", lambda h: h.tensor_tensor(of.t[:, 0:nq], po.t[:, 0:nq], rl.t[:, 0:nq], ALU.mult), [po, rl], [of])
        P.op("dve", lambda h: h.tensor_tensor(o_dst_fn(), of.t[:, 0:nq], za.t[:, zcol0:zcol0 + nq], ALU.mult), [of, za], [ocat])

    for hh in range(8):
        b2 = hh % 2
        P.dma("sp", kTh[b2].t[:], kT_scr.t.ap()[hh], [kT_scr], [kTh[b2]], kTh[b2])
        P.dma("sp", vh[b2].t[:], v_scr.t.ap()[:, hh * 128:(hh + 1) * 128].rearrange("(t p) d -> p t d", p=128), [v_scr], [vh[b2]], vh[b2])
        P.dma("sp", kTsh[b2].t[:], kTs_scr.t.ap()[hh], [kTs_scr], [kTsh[b2]], kTsh[b2])
        P.dma("sp", vsh[b2].t[:], vs_scr.t.ap()[:, hh * 128:(hh + 1) * 128].rearrange("(t p) d -> p t d", p=128), [vs_scr], [vsh[b2]], vsh[b2])
        P.dma("sp", qTh[b2].t[:], qT_scr.t.ap()[hh], [qT_scr], [qTh[b2]], qTh[b2])
        load_wt(0, 3072 + hh * 128)
        for (c0, n) in [(0, 512), (512, 512), (1024, 128)]:
            proj_feat(wt[0], None, lambda pk, c0=c0, n=n: silu_evac(pk, za, za.t[:, c0:c0 + n], n),
                      hnT_o, lambda kt, c0=c0, n=n: hnT_o.t[:, kt, c0:c0 + n], n)
        for qc in range(2):
            kts = list(range(4 * qc, 4 * qc + 20))
            attention(hh, qTh[b2], qc * 512, 512, kTh[b2], vh[b2], kts,
                      lambda i: (maskp.t[:, i, :], maskp),
                      lambda i, kts=kts: (validB.t[:, kts[i], :], validB),
                      lambda qc=qc, hh=hh: ocat.t[:, hh, qc * 512:(qc + 1) * 512], qc * 512)
        attention(hh, qTh[b2], 1024, 128, kTsh[b2], vsh[b2], list(range(17)),
                  lambda i: (masks_.t[:, i, :], masks_),
                  lambda i: (ones_bf.t[:], ones_bf),
                  lambda hh=hh: ocat.t[:, hh, 1024:1152], 1024)

    for cc in range(8):
        for j, base in enumerate([4096, 5120, 6144, 7168]):
            load_wt(j, base + cc * 128)
        bb, cb_, hb_, zb = fb
        for j, dstb in enumerate(fb):
            for (c0, n) in [(0, 512), (512, 512), (1024, 128)]:
                if j == 3:
                    ev = lambda pk, c0=c0, n=n, dstb=dstb: silu_evac(pk, dstb, dstb.t[:, 2 + c0:2 + c0 + n], n)
                else:
                    ev = lambda pk, c0=c0, n=n, dstb=dstb: P.op("act", lambda h: h.activation(dstb.t[:, 2 + c0:2 + c0 + n], pk.t[:, 0:n], AF.Identity), [pk], [dstb])
                proj_feat(wt[j], None, ev, hnT_o, lambda kt, c0=c0, n=n: hnT_o.t[:, kt, c0:c0 + n], n)
            if j in (1, 2):
                proj_feat(wt[j], None, lambda pk, dstb=dstb: P.op("act", lambda h: h.activation(dstb.t[:, 0:2], pk.t[:, 0:2], AF.Identity), [pk], [dstb]),
                          hp2, lambda kt: hp2.t[:, kt, :], 2)
        P.op("dve", lambda h: h.tensor_tensor(cb_.t[:], cb_.t[:], hb_.t[:], ALU.mult), [cb_, hb_], [cb_])
        w0 = cw.t[:, cc, 0:1]
        w1 = cw.t[:, cc, 1:2]
        w2 = cw.t[:, cc, 2:3]
        P.op("dve", lambda h, w2=w2: h.tensor_scalar(hb_.t[:, 2:1026], cb_.t[:, 2:1026], w2, None, ALU.mult), [cb_, cw], [hb_])
        P.op("dve", lambda h, w1=w1: h.scalar_tensor_tensor(hb_.t[:, 2:1026], cb_.t[:, 1:1025], w1, hb_.t[:, 2:1026], ALU.mult, ALU.add), [cb_, cw, hb_], [hb_])
        P.op("dve", lambda h, w0=w0: h.scalar_tensor_tensor(hb_.t[:, 2:1026], cb_.t[:, 0:1024], w0, hb_.t[:, 2:1026], ALU.mult, ALU.add), [cb_, cw, hb_], [hb_])
        P.op("dve", lambda h, cc=cc: h.tensor_copy(convo_p.t[:, cc, :], cb_.t[:, 1024:1026]), [cb_], [convo_p])
        P.op("dve", lambda h, cc=cc: h.tensor_copy(cb_.t[:, 1024:1026], sconv.t[:, cc, :]), [sconv], [cb_])
        P.op("dve", lambda h, w2=w2: h.tensor_scalar(hb_.t[:, 1026:1034], cb_.t[:, 1026:1034], w2, None, ALU.mult), [cb_, cw], [hb_])
        P.op("dve", lambda h, w1=w1: h.scalar_tensor_tensor(hb_.t[:, 1026:1034], cb_.t[:, 1025:1033], w1, hb_.t[:, 1026:1034], ALU.mult, ALU.add), [cb_, cw, hb_], [hb_])
        P.op("dve", lambda h, w0=w0: h.scalar_tensor_tensor(hb_.t[:, 1026:1034], cb_.t[:, 1024:1032], w0, hb_.t[:, 1026:1034], ALU.mult, ALU.add), [cb_, cw, hb_], [hb_])
        P.op("dve", lambda h, cc=cc: h.tensor_copy(convo_s.t[:, cc, :], cb_.t[:, 1032:1034]), [cb_], [convo_s])
        P.op("dve", lambda h: h.tensor_tensor(hb_.t[:, 2:1034], hb_.t[:, 2:1034], bb.t[:, 2:1034], ALU.mult), [hb_, bb], [hb_])
        P.op("dve", lambda h, cc=cc: h.tensor_tensor(ocat.t[:, 8 + cc, 0:1032], hb_.t[:, 2:1034], zb.t[:, 2:1034], ALU.mult), [hb_, zb], [ocat])
        P.op("dve", lambda h, cc=cc: h.memset(ocat.t[:, 8 + cc, 1032:1152], 0.0), [], [ocat])
    P.dma("sp", OUT["convp"].ap(), convo_p.t[:], [convo_p], [outbufs["convp"]], convo_p)
    P.dma("sp", OUT["convs"].ap(), convo_s.t[:], [convo_s], [outbufs["convs"]], convo_s)

    if STOP == 'B':
        P.barrier()
        return
    P.barrier()
    hn1T = P.sb("hn1T", [128, KT, NCH], BF16, at=hnT_o.at)
    P.off = B0
    wgb = [P.sb("wg%d" % i, [128, KT, 512], BF16) for i in range(2)]
    h1s = [P.sb("h1s%d" % i, [128, D], F32) for i in range(5)]
    xn1 = [P.sb("xn1%d" % i, [128, D], BF16) for i in range(2)]
    junk = P.sb("junk2", [128, D], F32)
    ss = P.sb("ss2", [128, 4], F32)
    tmpT = P.sb("tmpT", [128, KT, 128], BF16)
    load_g("g_ssm")
    wi = 0
    for tiles in [list(range(0, 5)), list(range(5, 9))]:
        for si, tj in enumerate(tiles):
            o0 = tj * 128
            src = IN["xh"].ap()[NHALO + o0:NHALO + o0 + 128, :] if tj < 8 else IN["xs"].ap()
            P.dma("sp", h1s[si].t[:], src, [], [h1s[si]], h1s[si])
        for g4 in range(4):
            wg = wgb[wi % 2]
            wi += 1
            P.dma("pool", wg.t[:], IN["w_out_ab"].ap()[:, g4 * 512:(g4 + 1) * 512].rearrange("(kt p) c -> p kt c", p=128), [], [wg], wg)
            for si, tj in enumerate(tiles):
                o0 = tj * 128
                ht_ = h1s[si]
                pk = next_pf()
                fns = [(lambda h, pk=pk, kt=kt, o0=o0, wg=wg: h.matmul(pk.t[:], ocat.t[:, kt, o0:o0 + 128], wg.t[:, kt, :],
                                                                         start=(kt == 0), stop=(kt == KT - 1))) for kt in range(KT)]
                P.mm(fns, [ocat, wg], [pk])
                P.op("dve", lambda h, pk=pk, ht_=ht_, g4=g4: h.tensor_tensor(ht_.t[:, g4 * 512:(g4 + 1) * 512], ht_.t[:, g4 * 512:(g4 + 1) * 512], pk.t[:], ALU.add),
                     [pk, ht_], [ht_])
        for si, tj in enumerate(tiles):
            o0 = tj * 128
            ht_ = h1s[si]
            P.dma("sp", h1_scr.t.ap()[o0:o0 + 128, :], ht_.t[:], [ht_], [h1_scr], ht_)
            if STOP == 'C1':
                if tj < 8:
                    P.dma("sp", OUT["yp"].ap()[o0:o0 + 128, :], ht_.t[:], [ht_], [outbufs["yp"]], ht_)
                else:
                    P.dma("sp", OUT["ys"].ap(), ht_.t[:], [ht_], [outbufs["ys"]], ht_)
            xn = xn1[tj % 2]
            rmsnorm_rows(ht_, xn, ss, junk)
            if tj < 8:
                transpose_rows(xn, hn1T, lambda half, o0=o0: hn1T.t[:, half * 8:(half + 1) * 8, o0:o0 + 128])
            else:
                transpose_rows(xn, tmpT, lambda half: tmpT.t[:, half * 8:(half + 1) * 8, :])
                P.op("dve", lambda h: h.tensor_copy(hn1T.t[:, :, 1024:1032], tmpT.t[:, :, 0:8]), [tmpT], [hn1T])
    for j in range(8):
        P.dma("sp", hn_src[j].t.ap().rearrange("(k p) t -> p k t", p=128), hn1T.t[:, 2 * j:2 * j + 2, :], [hn1T], [hn_src[j]], hn1T)
        P.coll(hn_src[j], hn_dst[j], GROUPS)

    if STOP == 'C1':
        P.barrier()
        return
    P.barrier()
    P.off = CONST_END
    uT = P.sb("uT", [128, 4, 4 * NCH], BF16)
    ygT = P.sb("ygT", [128, 4, 4 * NCH], BF16)
    L1 = P.off
    wu = P.sb("wu", [128, KT, 512], BF16)
    hch = [P.sb("hch%d" % i, [128, KT, 516], BF16) for i in range(2)]
    P.dma("pool", wu.t[:], IN["w_in_c"].ap()[:, 0:512].rearrange("(kt p) c -> p kt c", p=128), [], [wu], wu)
    ci = 0
    for r in range(4):
        for hf in range(2):
            hc = hch[ci % 2]
            ci += 1
            c0 = hf * 516
            for j in range(8):
                P.dma("sp", hc.t[:, 2 * j:2 * j + 2, :], hn_dst[j].t.ap()[r * 256:(r + 1) * 256, c0:c0 + 516].rearrange("(k p) t -> p k t", p=128),
                      [hn_dst[j]], [hc], hc)
            for ft in range(4):
                pk = next_pf()
                fns = [(lambda h, pk=pk, kt=kt, ft=ft, hc=hc: h.matmul(pk.t[:, 0:512], wu.t[:, kt, ft * 128:(ft + 1) * 128], hc.t[:, kt, 0:512],
                                                                      start=(kt == 0), stop=(kt == KT - 1))) for kt in range(KT)]
                P.mm(fns, [wu, hc], [pk])
                P.op("act", lambda h, pk=pk, ft=ft, r=r, c0=c0: h.activation(uT.t[:, ft, r * NCH + c0:r * NCH + c0 + 512], pk.t[:, 0:512], AF.Identity), [pk], [uT])
                pk2 = next_pf()
                fns2 = [(lambda h, pk2=pk2, kt=kt, ft=ft, hc=hc: h.matmul(pk2.t[:, 0:4], wu.t[:, kt, ft * 128:(ft + 1) * 128], hc.t[:, kt, 512:516],
                                                                         start=(kt == 0), stop=(kt == KT - 1))) for kt in range(KT)]
                P.mm(fns2, [wu, hc], [pk2])
                P.op("act", lambda h, pk2=pk2, ft=ft, r=r, c0=c0: h.activation(uT.t[:, ft, r * NCH + c0 + 512:r * NCH + c0 + 516], pk2.t[:, 0:4], AF.Identity), [pk2], [uT])

    if STOP == 'U':
        P.barrier()
        return
    P.barrier()
    P.off = L1
    def small(name, shape, dt=F32):
        return P.sb(name, shape, dt)
    lre_s = small("lre_s", [128, 16]); lim_s = small("lim_s", [128, 16]); lst_s = small("lst_s", [128, 16])
    lre_r = small("lre_r", [128, 256]); lim_r = small("lim_r", [128, 256]); lst_r = small("lst_r", [128, 256])
    bre_r = small("bre_r", [128, 256]); bim_r = small("bim_r", [128, 256])
    cre_s = small("cre_s", [128, 16, 16]); cim_s = small("cim_s", [128, 16, 16])
    rmask = small("rmask", [128, 8]); smask = small("smask", [128, 2])
    sre0 = small("sre0", [128, 4, 16]); sim0 = small("sim0", [128, 4, 16])
    iota = small("iota", [128, 1024])
    for b_, n in [(lre_s, "lre_s"), (lim_s, "lim_s"), (lst_s, "lst_s"), (cre_s, "cre_s"), (cim_s, "cim_s"),
                  (rmask, "rmask"), (smask, "smask"), (sre0, "sre0"), (sim0, "sim0"), (iota, "iota")]:
        P.dma("sp", b_.t[:], IN[n].ap(), [], [b_], b_)
    for b_, n in [(lre_r, "lre_r"), (lim_r, "lim_r"), (lst_r, "lst_r"), (bre_r, "bre_r"), (bim_r, "bim_r")]:
        P.dma("sp", b_.t[:], IN[n].ap().rearrange("p a b -> p (a b)"), [], [b_], b_)
    negpi = small("negpi", [128, 1])
    P.op("dve", lambda h: h.memset(negpi.t[:], -math.pi), [], [negpi])

    I32 = mybir.dt.int32
    tq = small("tq", [128, 1024]); tiq = small("tiq", [128, 1024], I32)
    halfpi = small("halfpi", [128, 1]); zero_t = small("zero_t", [128, 1])
    P.op("dve", lambda h: h.memset(halfpi.t[:], 0.5 * math.pi), [], [halfpi])
    P.op("dve", lambda h: h.memset(zero_t.t[:], 0.0), [], [zero_t])

    def sincos(ang_ap, n, s_ap, c_ap, rd, wr):
        for (dst, addc, bt, lo, hi) in [(s_ap, 0.0, zero_t, -math.pi, math.pi), (c_ap, 0.25, halfpi, -1.5 * math.pi, 0.5 * math.pi)]:
            P.op("dve", lambda h, addc=addc: h.tensor_scalar(tq.t[:, 0:n], ang_ap, 1.0 / TWO_PI, addc, ALU.mult, ALU.add), rd, [tq])
            P.op("dve", lambda h: h.tensor_copy(tiq.t[:, 0:n], tq.t[:, 0:n]), [tq], [tiq])
            P.op("dve", lambda h: h.tensor_copy(tq.t[:, 0:n], tiq.t[:, 0:n]), [tiq], [tq])
            P.op("dve", lambda h: h.scalar_tensor_tensor(tq.t[:, 0:n], tq.t[:, 0:n], -TWO_PI, ang_ap, ALU.mult, ALU.add), [tq] + rd, [tq])
            P.op("dve", lambda h, lo=lo, hi=hi: h.tensor_scalar(tq.t[:, 0:n], tq.t[:, 0:n], lo, hi, ALU.max, ALU.min), [tq], [tq])
            P.op("act", lambda h, dst=dst, bt=bt: h.activation(dst, tq.t[:, 0:n], AF.Sin, bias=bt.t[:, 0:1], scale=1.0), [tq, bt], wr)

    def disc(lre, lim, lst, n, pref):
        o = {}
        for nm in ["step", "mag", "th", "c", "s", "tmp", "nr", "den", "cr", "ci", "a", "b"]:
            o[nm] = small(pref + nm, [128, n])
        al = [o[k] for k in o] + [lre, lim, lst]
        P.op("act", lambda h: h.activation(o["step"].t[:], lst.t[:, 0:n], AF.Exp), al, al)
        P.op("dve", lambda h: h.tensor_tensor(o["th"].t[:], lim.t[:, 0:n], o["step"].t[:], ALU.mult), al, al)
        P.op("dve", lambda h: h.tensor_tensor(o["a"].t[:], lre.t[:, 0:n], o["step"].t[:], ALU.mult), al, al)
        P.op("act", lambda h: h.activation(o["mag"].t[:], o["a"].t[:], AF.Exp), al, al)
        sincos(o["th"].t[:], n, o["s"].t[:], o["c"].t[:], al, al)
        P.op("dve", lambda h: h.tensor_tensor(o["a"].t[:], o["mag"].t[:], o["c"].t[:], ALU.mult), al, al)
        P.op("dve", lambda h: h.tensor_scalar(o["nr"].t[:], o["a"].t[:], 1.0, -1.0, ALU.mult, ALU.add), al, al)
        P.op("dve", lambda h: h.tensor_tensor(o["b"].t[:], o["mag"].t[:], o["s"].t[:], ALU.mult), al, al)
        P.op("dve", lambda h: h.tensor_tensor(o["den"].t[:], lre.t[:, 0:n], lre.t[:, 0:n], ALU.mult), al, al)
        P.op("dve", lambda h: h.tensor_tensor(o["tmp"].t[:], lim.t[:, 0:n], lim.t[:, 0:n], ALU.mult), al, al)
        P.op("dve", lambda h: h.tensor_tensor(o["den"].t[:], o["den"].t[:], o["tmp"].t[:], ALU.add), al, al)
        P.op("dve", lambda h: h.reciprocal(o["den"].t[:], o["den"].t[:]), al, al)
        P.op("dve", lambda h: h.tensor_tensor(o["cr"].t[:], o["nr"].t[:], lre.t[:, 0:n], ALU.mult), al, al)
        P.op("dve", lambda h: h.tensor_tensor(o["tmp"].t[:], o["b"].t[:], lim.t[:, 0:n], ALU.mult), al, al)
        P.op("dve", lambda h: h.tensor_tensor(o["cr"].t[:], o["cr"].t[:], o["tmp"].t[:], ALU.add), al, al)
        P.op("dve", lambda h: h.tensor_tensor(o["cr"].t[:], o["cr"].t[:], o["den"].t[:], ALU.mult), al, al)
        P.op("dve", lambda h: h.tensor_tensor(o["ci"].t[:], o["b"].t[:], lre.t[:, 0:n], ALU.mult), al, al)
        P.op("dve", lambda h: h.tensor_tensor(o["tmp"].t[:], o["nr"].t[:], lim.t[:, 0:n], ALU.mult), al, al)
        P.op("dve", lambda h: h.tensor_tensor(o["ci"].t[:], o["ci"].t[:], o["tmp"].t[:], ALU.subtract), al, al)
        P.op("dve", lambda h: h.tensor_tensor(o["ci"].t[:], o["ci"].t[:], o["den"].t[:], ALU.mult), al, al)
        return o, al

    ds_, als = disc(lre_s, lim_s, lst_s, 16, "ds_")
    dr_, alr = disc(lre_r, lim_r, lst_r, 256, "dr_")
    bbr = small("bbr", [128, 256]); bbi = small("bbi", [128, 256]); tmpr = small("tmpr", [128, 256])
    alr2 = alr + [bbr, bbi, tmpr, bre_r, bim_r]
    P.op("dve", lambda h: h.tensor_tensor(bbr.t[:], dr_["cr"].t[:], bre_r.t[:], ALU.mult), alr2, alr2)
    P.op("dve", lambda h: h.tensor_tensor(tmpr.t[:], dr_["ci"].t[:], bim_r.t[:], ALU.mult), alr2, alr2)
    P.op("dve", lambda h: h.tensor_tensor(bbr.t[:], bbr.t[:], tmpr.t[:], ALU.subtract), alr2, alr2)
    P.op("dve", lambda h: h.tensor_tensor(bbi.t[:], dr_["cr"].t[:], bim_r.t[:], ALU.mult), alr2, alr2)
    P.op("dve", lambda h: h.tensor_tensor(tmpr.t[:], dr_["ci"].t[:], bre_r.t[:], ALU.mult), alr2, alr2)
    P.op("dve", lambda h: h.tensor_tensor(bbi.t[:], bbi.t[:], tmpr.t[:], ALU.add), alr2, alr2)
    BbT = [small("BbT%d" % ri, [128, 16, 128], BF16) for ri in range(2)]
    for ri, src in enumerate([bbr, bbi]):
        for qq in range(4):
            for g2 in range(2):
                m = rmask.t[:, qq * 2 + g2:qq * 2 + g2 + 1]
                o_ap = BbT[ri].t[:].rearrange("p (ft q) c -> p ft q c", q=4)[:, :, qq, g2 * 64:(g2 + 1) * 64]
                i_ap = src.t[:].rearrange("p (ft d) -> p ft d", ft=4)
                P.op("dve", lambda h, o_ap=o_ap, i_ap=i_ap, m=m: h.tensor_scalar(o_ap, i_ap, m, None, ALU.mult), alr2 + [rmask], [BbT[ri]])
    CT = [small("CT%d" % ri, [128, 16, 128], BF16) for ri in range(2)]
    for ri in range(2):
        P.op("dve", lambda h, ri=ri: h.memset(CT[ri].t[:], 0.0), [], [CT[ri]])
    for ri, (src, sgn) in enumerate([(cre_s, 1.0), (cim_s, -1.0)]):
        for pair in range(16):
            qq = pair % 4
            for g2 in range(2):
                m = smask.t[:, g2:g2 + 1]
                col = qq * 32 + g2 * 16
                P.op("dve", lambda h, ri=ri, pair=pair, col=col, m=m, src=src, sgn=sgn: h.tensor_scalar(
                    CT[ri].t[:, pair, col:col + 16], src.t[:, pair, :], m, sgn, ALU.mult, ALU.mult), [src, smask], [CT[ri]])
    rr = ds_["mag"]; th = ds_["th"]
    cth = small("cth", [128, 16]); sth = small("sth", [128, 16])
    P.op("dve", lambda h: h.tensor_copy(cth.t[:], ds_["c"].t[:]), als, [cth])
    P.op("dve", lambda h: h.tensor_copy(sth.t[:], ds_["s"].t[:]), als, [sth])

    tabc = [small("tabc%d" % i, [128, 1024]) for i in range(4)]
    tabs = [small("tabs%d" % i, [128, 1024]) for i in range(4)]
    WS = [dict(gr=small("gr0", [128, 1024]), gi=small("gi0", [128, 1024]), yr=small("yr0", [128, 1024]), yi=small("yi0", [128, 1024]),
               hrb=small("hrb0", [128, 1024], BF16), hib=small("hib0", [128, 1024], BF16))]
    hib1 = small("hib1", [128, 1024], BF16)
    ytmp = small("ytmp", [128, 512]); ysq = small("ysq", [128, 512])
    stp_l = [small("stp%d" % p_, [128, 2]) for p_ in range(16)]
    sts_p = [small("stsp%d" % p_, [128, 4, 2]) for p_ in range(16)]
    m32 = small("m32", [128, 32])
    P.op("dve", lambda h: h.memset(m32.t[:], 1.0), [], [m32])
    P.op("dve", lambda h: h.memset(m32.t[:].rearrange("p (r c) -> p r c", r=4)[:, :, 0], 0.0), [m32], [m32])
    for w_ in WS:
        w_["gin"] = small("gin0", [128, 2]); w_["hend"] = small("hend0", [128, 2])
        w_["gin4"] = small("gin40", [128, 4, 2]); w_["d0"] = small("d00", [128, 32])
    for p_ in range(16):
        P.op("dve", lambda h, p_=p_: h.memset(stp_l[p_].t[:], 0.0), [], [stp_l[p_]])
    P.barrier()
    blkA = lre_r.at
    blkB = dr_["step"].at
    WS.append(dict(gr=P.sb("gr1", [128, 1024], F32, at=blkB), gi=P.sb("gi1", [128, 1024], F32, at=blkB + 4096),
                   yr=P.sb("yr1", [128, 1024], F32, at=blkB + 8192), yi=P.sb("yi1", [128, 1024], F32, at=blkA),
                   hrb=P.sb("hrb1", [128, 1024], BF16, at=blkB + 12288), hib=hib1,
                   gin=small("gin1", [128, 2]), hend=small("hend1", [128, 2]),
                   gin4=small("gin41", [128, 4, 2]), d0=small("d01", [128, 32])))
    ang = WS[0]["gr"]
    GC = 1.5957691216057308
    kcount = [0]

    def stageX(pair, qq, ft, col0, T, tc_, ts_, init_re, init_im, init_bufs, out_b):
        w = WS[kcount[0] % 2]
        kcount[0] += 1
        gr, gi, yr, yi, hrb, hib, gin, hend = w["gr"], w["gi"], w["yr"], w["yi"], w["hrb"], w["hib"], w["gin"], w["hend"]
        for h0 in range(0, T, 512):
            n = min(512, T - h0)
            pxr = next_pf(); pxi = next_pf()
            P.mm([lambda h, pxr=pxr, n=n, h0=h0: h.matmul(pxr.t[:, 0:n], BbT[0].t[:, pair, :], uT.t[:, ft, col0 + h0:col0 + h0 + n], start=True, stop=True)], [BbT[0], uT], [pxr])
            P.mm([lambda h, pxi=pxi, n=n, h0=h0: h.matmul(pxi.t[:, 0:n], BbT[1].t[:, pair, :], uT.t[:, ft, col0 + h0:col0 + h0 + n], start=True, stop=True)], [BbT[1], uT], [pxi])
            c_ = tc_.t[:, h0:h0 + n]; s_ = ts_.t[:, h0:h0 + n]
            P.op("dve", lambda h, pxr=pxr, c_=c_, n=n, h0=h0: h.tensor_tensor(yr.t[:, h0:h0 + n], pxr.t[:, 0:n], c_, ALU.mult), [pxr, tc_], [yr])
            P.op("dve", lambda h, pxi=pxi, s_=s_, n=n, h0=h0: h.tensor_tensor(gr.t[:, h0:h0 + n], pxi.t[:, 0:n], s_, ALU.mult), [pxi, ts_], [gr])
            P.op("dve", lambda h, n=n, h0=h0: h.tensor_tensor(yr.t[:, h0:h0 + n], yr.t[:, h0:h0 + n], gr.t[:, h0:h0 + n], ALU.add), [yr, gr], [yr])
            P.op("dve", lambda h, pxi=pxi, c_=c_, n=n, h0=h0: h.tensor_tensor(yi.t[:, h0:h0 + n], pxi.t[:, 0:n], c_, ALU.mult), [pxi, tc_], [yi])
            P.op("dve", lambda h, pxr=pxr, s_=s_, n=n, h0=h0: h.tensor_tensor(gi.t[:, h0:h0 + n], pxr.t[:, 0:n], s_, ALU.mult), [pxr, ts_], [gi])
            P.op("dve", lambda h, n=n, h0=h0: h.tensor_tensor(yi.t[:, h0:h0 + n], yi.t[:, h0:h0 + n], gi.t[:, h0:h0 + n], ALU.subtract), [yi, gi], [yi])
        ct = cth.t[:, pair:pair + 1]; st_ = sth.t[:, pair:pair + 1]
        P.op("dve", lambda h: h.tensor_scalar(gin.t[:, 0:1], init_re, ct, None, ALU.mult), init_bufs + [cth], [gin])
        P.op("dve", lambda h: h.scalar_tensor_tensor(gin.t[:, 0:1], init_im, st_, gin.t[:, 0:1], ALU.mult, ALU.subtract), init_bufs + [sth, gin], [gin])
        P.op("dve", lambda h: h.tensor_scalar(gin.t[:, 0:1], gin.t[:, 0:1], -1.0, None, ALU.mult), [gin], [gin])
        P.op("dve", lambda h: h.tensor_scalar(gin.t[:, 1:2], init_re, st_, None, ALU.mult), init_bufs + [sth], [gin])
        P.op("dve", lambda h: h.scalar_tensor_tensor(gin.t[:, 1:2], init_im, ct, gin.t[:, 1:2], ALU.mult, ALU.add), init_bufs + [cth, gin], [gin])
        rb = rr.t[:, pair:pair + 1].to_broadcast([128, T])
        P.op("dve", lambda h: h.tensor_tensor_scan(gr.t[:, 0:T], rb, yr.t[:, 0:T], gin.t[:, 0:1], ALU.mult, ALU.add), [yr, gin] + als, [gr])
        P.op("dve", lambda h: h.tensor_tensor_scan(gi.t[:, 0:T], rb, yi.t[:, 0:T], gin.t[:, 1:2], ALU.mult, ALU.add), [yi, gin] + als, [gi])
        c_ = tc_.t[:, 0:T]; s_ = ts_.t[:, 0:T]
        P.op("dve", lambda h: h.tensor_tensor(yr.t[:, 0:T], gr.t[:, 0:T], c_, ALU.mult), [gr, tc_], [yr])
        P.op("dve", lambda h: h.tensor_tensor(yi.t[:, 0:T], gi.t[:, 0:T], s_, ALU.mult), [gi, ts_], [yi])
        P.op("dve", lambda h: h.tensor_tensor(hrb.t[:, 0:T], yr.t[:, 0:T], yi.t[:, 0:T], ALU.subtract), [yr, yi], [hrb])
        P.op("dve", lambda h: h.tensor_tensor(hend.t[:, 0:1], yr.t[:, T - 1:T], yi.t[:, T - 1:T], ALU.subtract), [yr, yi], [hend])
        P.op("dve", lambda h: h.tensor_tensor(yr.t[:, 0:T], gr.t[:, 0:T], s_, ALU.mult), [gr, ts_, hrb, hend], [yr])
        P.op("dve", lambda h: h.tensor_tensor(yi.t[:, 0:T], gi.t[:, 0:T], c_, ALU.mult), [gi, tc_, hrb, hend], [yi])
        P.op("dve", lambda h: h.tensor_tensor(hib.t[:, 0:T], yr.t[:, 0:T], yi.t[:, 0:T], ALU.add), [yr, yi], [hib])
        P.op("dve", lambda h: h.tensor_tensor(hend.t[:, 1:2], yr.t[:, T - 1:T], yi.t[:, T - 1:T], ALU.add), [yr, yi], [hend])
        P.op("dve", lambda h: h.tensor_copy(out_b.t[:, 0:2], hend.t[:, 0:2]), [hend], [out_b])
        return dict(pair=pair, qq=qq, T=T, hrb=hrb, hib=hib, after=[])

    def v48(ap):
        return ap.rearrange("p (r c) -> p r c", r=4)

    def stageXs(pair, qq, ft, tc_, ts_):
        w = WS[kcount[0] % 2]
        kcount[0] += 1
        gr, gi, yr, yi, hrb, hib, gin4, d0 = w["gr"], w["gi"], w["yr"], w["yi"], w["hrb"], w["hib"], w["gin4"], w["d0"]
        ucols = v48(uT.t[:, ft, :])[:, :, 1024:1032]
        pxr = next_pf(); pxi = next_pf()
        P.mm([lambda h: h.matmul(pxr.t[:, 0:32], BbT[0].t[:, pair, :], ucols, start=True, stop=True)], [BbT[0], uT], [pxr])
        P.mm([lambda h: h.matmul(pxi.t[:, 0:32], BbT[1].t[:, pair, :], ucols, start=True, stop=True)], [BbT[1], uT], [pxi])
        cb = tc_.t[:, 0:8].unsqueeze(1).to_broadcast([128, 4, 8])
        sb_ = ts_.t[:, 0:8].unsqueeze(1).to_broadcast([128, 4, 8])
        xr = v48(pxr.t[:, 0:32]); xi = v48(pxi.t[:, 0:32])
        yrv = v48(yr.t[:, 0:32]); yiv = v48(yi.t[:, 0:32]); grv = v48(gr.t[:, 0:32]); giv = v48(gi.t[:, 0:32])
        P.op("dve", lambda h: h.tensor_tensor(yrv, xr, cb, ALU.mult), [pxr, tc_], [yr])
        P.op("dve", lambda h: h.tensor_tensor(grv, xi, sb_, ALU.mult), [pxi, ts_], [gr])
        P.op("dve", lambda h: h.tensor_tensor(yrv, yrv, grv, ALU.add), [yr, gr], [yr])
        P.op("dve", lambda h: h.tensor_tensor(yiv, xi, cb, ALU.mult), [pxi, tc_], [yi])
        P.op("dve", lambda h: h.tensor_tensor(giv, xr, sb_, ALU.mult), [pxr, ts_], [gi])
        P.op("dve", lambda h: h.tensor_tensor(yiv, yiv, giv, ALU.subtract), [yi, gi], [yi])
        ct = cth.t[:, pair:pair + 1]; st_ = sth.t[:, pair:pair + 1]; rs = rr.t[:, pair:pair + 1]
        ire = sre0.t[:, :, pair]; iim = sim0.t[:, :, pair]
        g_re = gin4.t[:, :, 0]; g_im = gin4.t[:, :, 1]
        P.op("dve", lambda h: h.tensor_scalar(g_re, ire, ct, None, ALU.mult), [sre0, cth], [gin4])
        P.op("dve", lambda h: h.scalar_tensor_tensor(g_re, iim, st_, g_re, ALU.mult, ALU.subtract), [sim0, sth, gin4], [gin4])
        P.op("dve", lambda h: h.tensor_scalar(g_re, g_re, -1.0, None, ALU.mult), [gin4], [gin4])
        P.op("dve", lambda h: h.tensor_scalar(g_im, ire, st_, None, ALU.mult), [sre0, sth, gin4], [gin4])
        P.op("dve", lambda h: h.scalar_tensor_tensor(g_im, iim, ct, g_im, ALU.mult, ALU.add), [sim0, cth, gin4], [gin4])
        P.op("dve", lambda h: h.scalar_tensor_tensor(yrv[:, :, 0], g_re, rs, yrv[:, :, 0], ALU.mult, ALU.add), [gin4, yr] + als, [yr])
        P.op("dve", lambda h: h.scalar_tensor_tensor(yiv[:, :, 0], g_im, rs, yiv[:, :, 0], ALU.mult, ALU.add), [gin4, yi] + als, [yi])
        P.op("dve", lambda h: h.tensor_scalar(d0.t[:], m32.t[:], rs, None, ALU.mult), [m32] + als, [d0])
        P.op("dve", lambda h: h.tensor_tensor_scan(gr.t[:, 0:32], d0.t[:], yr.t[:, 0:32], 0.0, ALU.mult, ALU.add), [yr, d0], [gr])
        P.op("dve", lambda h: h.tensor_tensor_scan(gi.t[:, 0:32], d0.t[:], yi.t[:, 0:32], 0.0, ALU.mult, ALU.add), [yi, d0], [gi])
        hrv = v48(hrb.t[:, 0:32]); hiv = v48(hib.t[:, 0:32])
        out_b = sts_p[pair]
        P.op("dve", lambda h: h.tensor_tensor(yrv, grv, cb, ALU.mult), [gr, tc_], [yr])
        P.op("dve", lambda h: h.tensor_tensor(yiv, giv, sb_, ALU.mult), [gi, ts_], [yi])
        P.op("dve", lambda h: h.tensor_tensor(hrv, yrv, yiv, ALU.subtract), [yr, yi], [hrb])
        P.op("dve", lambda h: h.tensor_tensor(out_b.t[:, :, 0], yrv[:, :, 7], yiv[:, :, 7], ALU.subtract), [yr, yi], [out_b])
        P.op("dve", lambda h: h.tensor_tensor(yrv, grv, sb_, ALU.mult), [gr, ts_, hrb, out_b], [yr])
        P.op("dve", lambda h: h.tensor_tensor(yiv, giv, cb, ALU.mult), [gi, tc_, hrb, out_b], [yi])
        P.op("dve", lambda h: h.tensor_tensor(hiv, yrv, yiv, ALU.add), [yr, yi], [hib])
        P.op("dve", lambda h: h.tensor_tensor(out_b.t[:, :, 1], yrv[:, :, 7], yiv[:, :, 7], ALU.add), [yr, yi], [out_b])
        return dict(pair=pair, qq=qq, T=32, hrb=hrb, hib=hib, after=[])

    def y_evac_s(ft):
        ucols = v48(uT.t[:, ft, :])[:, :, 1024:1032]
        dcols = v48(ygT.t[:, ft, :])[:, :, 1024:1032]
        yp_ = ypb[0]
        ypv = v48(yp_.t[:, 0:32]); yt = v48(ytmp.t[:, 0:32]); yq = v48(ysq.t[:, 0:32])
        P.op("dve", lambda h: h.scalar_tensor_tensor(yt, ucols, dsk.t[:, ft:ft + 1], ypv, ALU.mult, ALU.add), [uT, dsk, yp_], [ytmp])
        P.op("dve", lambda h: h.tensor_tensor(yq, yt, yt, ALU.mult), [ytmp], [ysq])
        P.op("dve", lambda h: h.tensor_scalar(yq, yq, 0.044715, 1.0, ALU.mult, ALU.add), [ysq], [ysq])
        P.op("dve", lambda h: h.tensor_tensor(yq, yq, yt, ALU.mult), [ysq, ytmp], [ysq])
        P.op("act", lambda h: h.activation(ysq.t[:, 0:32], ysq.t[:, 0:32], AF.Sigmoid, scale=GC), [ysq], [ysq])
        P.op("dve", lambda h: h.tensor_tensor(dcols, yq, yt, ALU.mult), [ysq, ytmp], [ygT])

    ypb = [pf[4], pf[5]]

    def stageY(c):
        pair, qq, T, hrb, hib = c["pair"], c["qq"], c["T"], c["hrb"], c["hib"]
        for h0 in range(0, T, 512):
            n = min(512, T - h0)
            yp_ = ypb[h0 // 512]
            P.mm([lambda h, yp_=yp_, n=n, h0=h0: h.matmul(yp_.t[:, 0:n], CT[0].t[:, pair, :], hrb.t[:, h0:h0 + n], start=(qq == 0), stop=False),
                  lambda h, yp_=yp_, n=n, h0=h0: h.matmul(yp_.t[:, 0:n], CT[1].t[:, pair, :], hib.t[:, h0:h0 + n], start=False, stop=(qq == 3))],
                 [CT[0], CT[1], hrb, hib], [yp_])
        for f in c["after"]:
            f()

    def y_evac(yp_, ft, col0, n):
        P.op("dve", lambda h: h.scalar_tensor_tensor(ytmp.t[:, 0:n], uT.t[:, ft, col0:col0 + n], dsk.t[:, ft:ft + 1], yp_.t[:, 0:n], ALU.mult, ALU.add),
             [uT, dsk, yp_], [ytmp])
        P.op("dve", lambda h: h.tensor_tensor(ysq.t[:, 0:n], ytmp.t[:, 0:n], ytmp.t[:, 0:n], ALU.mult), [ytmp], [ysq])
        P.op("dve", lambda h: h.tensor_scalar(ysq.t[:, 0:n], ysq.t[:, 0:n], 0.044715, 1.0, ALU.mult, ALU.add), [ysq], [ysq])
        P.op("dve", lambda h: h.tensor_tensor(ysq.t[:, 0:n], ysq.t[:, 0:n], ytmp.t[:, 0:n], ALU.mult), [ysq, ytmp], [ysq])
        P.op("act", lambda h: h.activation(ysq.t[:, 0:n], ysq.t[:, 0:n], AF.Sigmoid, scale=GC), [ysq], [ysq])
        P.op("dve", lambda h: h.tensor_tensor(ygT.t[:, ft, col0:col0 + n], ysq.t[:, 0:n], ytmp.t[:, 0:n], ALU.mult), [ysq, ytmp], [ygT])

    pending = [None]

    def push(ctx):
        if pending[0] is not None:
            stageY(pending[0])
        pending[0] = ctx

    for ft in range(4):
        for qq in range(4):
            pair = ft * 4 + qq
            P.op("dve", lambda h, pair=pair: h.tensor_scalar(ang.t[:], iota.t[:], th.t[:, pair:pair + 1], None, ALU.mult), [iota] + als, [ang])
            sincos(ang.t[:], 1024, tabs[qq].t[:], tabc[qq].t[:], [ang], [tabs[qq], tabc[qq]])
        for seg in range(4):
            for qq in range(4):
                pair = ft * 4 + qq
                ctx = stageX(pair, qq, ft, seg * NCH, 1024, tabc[qq], tabs[qq],
                             stp_l[pair].t[:, 0:1], stp_l[pair].t[:, 1:2], [stp_l[pair]], stp_l[pair])
                if qq == 3:
                    ctx["after"] = [(lambda ft=ft, seg=seg, hf=hf: y_evac(ypb[hf], ft, seg * NCH + hf * 512, 512)) for hf in range(2)]
                push(ctx)
        for qq in range(4):
            pair = ft * 4 + qq
            ctx = stageXs(pair, qq, ft, tabc[qq], tabs[qq])
            if qq == 3:
                ctx["after"] = [(lambda ft=ft: y_evac_s(ft))]
            push(ctx)
    push(None)
    stp2 = P.sb("stp2", [128, 2, 16], F32, at=tq.at); sts2 = P.sb("sts2", [128, 4, 2, 16], F32, at=tq.at + 128)
    for p_ in range(16):
        P.op("dve", lambda h, p_=p_: h.tensor_copy(stp2.t[:, :, p_], stp_l[p_].t[:, 0:2]), [stp_l[p_]], [stp2])
        P.op("dve", lambda h, p_=p_: h.tensor_copy(sts2.t[:, :, :, p_], sts_p[p_].t[:]), [sts_p[p_]], [sts2])
    if STOP == 'SSM':
        for r_ in range(4):
            P.dma("pool", OUT["yp"].ap().rearrange("(p f x) c -> p f (x c)", p=128, f=4)[:, :, r_ * 1024:(r_ + 1) * 1024],
                  ygT.t[:, :, r_ * NCH:r_ * NCH + 1024], [ygT], [outbufs["yp"]], ygT)
        P.dma("pool", OUT["ys"].ap()[:, 0:128].rearrange("p (f r s) -> p f r s", f=4, r=4),
              ygT.t[:].rearrange("p f (r c) -> p f r c", r=4)[:, :, :, 1024:1032], [ygT], [outbufs["ys"]], ygT)
    P.dma("sp", OUT["ssm_p"].ap(), stp2.t[:], [stp2], [outbufs["ssm_p"]], stp2)
    P.dma("sp", OUT["ssm_s"].ap(), sts2.t[:], [sts2], [outbufs["ssm_s"]], sts2)
    P.barrier()
    hn1o = P.sb("hn1o", [128, KT, NCH], BF16, at=uT.at)
    P.off = L1 + 40960
    zall = P.sb("zall", [128, KT, NCH], BF16)
    wz = [P.sb("wz%d" % i, [128, KT, 128], BF16) for i in range(2)]
    zz = P.sb("zz", [128, NCH], F32)
    for j in range(8):
        P.dma("sp", y_src[j].t.ap().rearrange("(ft p) t -> p ft t", p=128), ygT.t[:, :, j * 516:(j + 1) * 516], [ygT], [y_src[j]], ygT)
        P.coll(y_src[j], y_dst[j], GROUPS)

    for j in range(8):
        P.dma("sp", hn1o.t[:, 2 * j:2 * j + 2, :], hn_src[j].t.ap().rearrange("(k p) t -> p k t", p=128), [hn_src[j]], [hn1o], hn1o)
    for nt in range(16):
        wb_ = wz[nt % 2]
        P.dma("pool", wb_.t[:], IN["w_in_c"].ap()[:, 2048 + nt * 128:2048 + (nt + 1) * 128].rearrange("(kt p) c -> p kt c", p=128), [], [wb_], wb_)
        for (c0, n) in [(0, 512), (512, 512), (1024, 8)]:
            def zev(pk, c0=c0, n=n, nt=nt):
                P.op("act", lambda h: h.activation(zz.t[:, c0:c0 + n], pk.t[:, 0:n], AF.Sigmoid), [pk], [zz])
                P.op("dve", lambda h: h.tensor_tensor(zall.t[:, nt, c0:c0 + n], zz.t[:, c0:c0 + n], pk.t[:, 0:n], ALU.mult), [zz, pk], [zall])
            proj_feat(wb_, None, zev, hn1o, lambda kt, c0=c0, n=n: hn1o.t[:, kt, c0:c0 + n], n)

    if STOP == 'SSM':
        P.barrier()
        return
    P.barrier()
    P.off = CONST_END
    y2T = P.sb("y2T", [128, KT, NO], BF16)
    F0 = P.off
    ygo = P.sb("ygo", [128, KT, NCH], BF16)
    ych = [P.sb("ych%d" % i, [128, 4, NCH], BF16) for i in range(2)]
    wt2 = [P.sb("wt2_%d" % i, [128, KT, 128], BF16) for i in range(2)]
    gl = P.sb("gl", [128, NCH], F32)
    assert P.off <= L1 + 40960
    P.op("pool", lambda h: h.memset(y2T.t[:, :, NCH:NO], 0.0), [], [y2T])
    ci = 0
    for rf in range(4):
        for r in range(4):
            yc = ych[ci % 2]; ci += 1
            for hh_ in range(2):
                P.dma("sp", yc.t[:, :, hh_ * 516:(hh_ + 1) * 516],
                      y_dst[2 * r + hh_].t.ap()[rf * 512:(rf + 1) * 512, :].rearrange("(ft p) t -> p ft t", p=128),
                      [y_dst[2 * r + hh_]], [yc], yc)
            dst = ygo.t[:, rf * 4:(rf + 1) * 4, :]
            if r == 0:
                P.op("dve", lambda h, yc=yc, dst=dst: h.tensor_scalar(dst, yc.t[:], sel.t[:, 0:1], None, ALU.mult), [yc, sel], [ygo])
            else:
                P.op("dve", lambda h, yc=yc, dst=dst, r=r: h.scalar_tensor_tensor(dst, yc.t[:], sel.t[:, r:r + 1], dst, ALU.mult, ALU.add), [yc, sel, ygo], [ygo])
    if STOP == 'G1':
        P.barrier()
        return
    for nt in range(int(os.environ.get('MK_NT', '16'))):
        wa = wt2[nt % 2]
        P.dma("pool", wa.t[:], IN["w_glu"].ap()[:, nt * 128:(nt + 1) * 128].rearrange("(kt p) c -> p kt c", p=128), [], [wa], wa)
        for (c0, n) in [(0, 512), (512, 512), (1024, 8)]:
            proj_feat(wa, None, lambda pk, c0=c0, n=n, nt=nt: P.op("act", lambda h: h.activation(gl.t[:, c0:c0 + n], pk.t[:, 0:n], AF.Sigmoid, bias=bglu.t[:, nt:nt + 1], scale=1.0), [pk, bglu], [gl]),
                      ygo, lambda kt, c0=c0, n=n: ygo.t[:, kt, c0:c0 + n], n)
        P.op("dve", lambda h, nt=nt: h.tensor_tensor(gl.t[:], gl.t[:], ygo.t[:, nt, :], ALU.mult), [gl, ygo], [gl])
        P.op("dve", lambda h, nt=nt: h.tensor_tensor(y2T.t[:, nt, 0:NCH], gl.t[:], zall.t[:, nt, :], ALU.mult), [gl, zall], [y2T])
    if STOP == 'G2':
        P.barrier()
        return
    P.barrier()
    P.off = F0
    wgb2 = [P.sb("wgc%d" % i, [128, KT, 512], BF16) for i in range(2)]
    h1s = [P.sb("h1f%d" % i, [128, D], F32) for i in range(5)]
    junk = P.sb("junk3", [128, D], F32); ss = P.sb("ss3", [128, 4], F32)
    yo = [P.sb("yo%d" % i, [128, D], F32) for i in range(2)]
    load_g("g_fin")
    wi = 0
    for tiles in [list(range(0, 5)), list(range(5, 9))]:
        for si, tj in enumerate(tiles):
            o0 = tj * 128
            P.dma("sp", h1s[si].t[:], h1_scr.t.ap()[o0:o0 + 128, :], [h1_scr], [h1s[si]], h1s[si])
        for g4 in range(4):
            wg = wgb2[wi % 2]
            wi += 1
            P.dma("pool", wg.t[:], IN["w_out_c"].ap()[:, g4 * 512:(g4 + 1) * 512].rearrange("(kt p) c -> p kt c", p=128), [], [wg], wg)
            for si, tj in enumerate(tiles):
                o0 = tj * 128
                ht_ = h1s[si]
                pk = next_pf()
                fns = [(lambda h, pk=pk, kt=kt, o0=o0, wg=wg: h.matmul(pk.t[:], y2T.t[:, kt, o0:o0 + 128], wg.t[:, kt, :],
                                                                         start=(kt == 0), stop=(kt == KT - 1))) for kt in range(KT)]
                P.mm(fns, [y2T, wg], [pk])
                P.op("dve", lambda h, pk=pk, ht_=ht_, g4=g4: h.tensor_tensor(ht_.t[:, g4 * 512:(g4 + 1) * 512], ht_.t[:, g4 * 512:(g4 + 1) * 512], pk.t[:], ALU.add),
                     [pk, ht_], [ht_])
        for si, tj in enumerate(tiles):
            o0 = tj * 128
            ht_ = h1s[si]
            yo_ = yo[tj % 2]
            rmsnorm_rows(ht_, yo_, ss, junk)
            if tj < 8:
                P.dma("sp", OUT["yp"].ap()[o0:o0 + 128, :], yo_.t[:], [yo_], [outbufs["yp"]], yo_)
            else:
                P.dma("sp", OUT["ys"].ap(), yo_.t[:], [yo_], [outbufs["ys"]], yo_)
    P.barrier()


_NC_CACHE = {}


def _rope_tables(pos):
    half = 64
    inv = (np.float32(10000.0) ** (-np.arange(half, dtype=np.float32) / np.float32(half))).astype(np.float32)
    ang = pos.astype(np.float32)[:, None] * inv[None, :]
    return np.cos(ang).astype(np.float32), np.sin(ang).astype(np.float32)


def kernel(x_prompt, x_sample, cache_win_k, cache_win_v, state_conv, state_ssm_re, state_ssm_im,
           attn_norm, w_in_ab, conv_w, w_out_ab, ssm_norm, w_in_c, lam_re, lam_im, log_step,
           b_re, b_im, c_re, c_im, d_skip, w_glu, b_glu, w_out_c, final_norm):
    f = lambda a: np.ascontiguousarray(np.asarray(a, dtype=np.float32))
    x_prompt, x_sample = f(x_prompt), f(x_sample)
    cache_win_k, cache_win_v, state_conv = f(cache_win_k), f(cache_win_v), f(state_conv)
    state_ssm_re, state_ssm_im = f(state_ssm_re), f(state_ssm_im)
    w_in_ab0, w_out_ab0, w_in_c0, w_glu0, w_out_c0 = f(w_in_ab)[0], f(w_out_ab)[0], f(w_in_c)[0], f(w_glu)[0], f(w_out_c)[0]
    lam_re, lam_im, log_step = f(lam_re)[0], f(lam_im)[0], f(log_step)[0]
    b_re, b_im, c_re, c_im = f(b_re)[0], f(b_im)[0], f(c_re)[0], f(c_im)[0]
    d_skip0, b_glu0 = f(d_skip)[0], f(b_glu)[0]
    if "nc" not in _NC_CACHE:
        _NC_CACHE["nc"] = build_nc()
    nc = _NC_CACHE["nc"]

    kk = np.arange(128)[:, None]
    qq_ = np.arange(512)[None, :]
    maskp = np.stack([mult_of(qq_ - ((i - 16) * 128 + kk)) for i in range(20)], 1)
    rows = np.arange(2176).reshape(17, 128)
    s_ = np.arange(128)[None, :]
    masks = np.zeros((128, 17, 128), np.float32)
    for i in range(17):
        row = rows[i][:, None]
        m = mult_of(2048 + s_ - row)
        m[:, 8:] = ((2048 + s_[:, 8:] - row) == 0)
        masks[:, i, :] = m
    iota = np.broadcast_to(np.arange(1024, dtype=np.float32)[None, :], (128, 1024)).copy()
    rmask = np.zeros((128, 8), np.float32)
    for p in range(128):
        rmask[p, (p // 32) * 2 + (p % 32) // 16] = 1.0
    smask = np.zeros((128, 2), np.float32)
    smask[:64, 0] = 1.0
    smask[64:, 1] = 1.0
    bc = lambda v: np.ascontiguousarray(np.broadcast_to(v[None, :], (128, v.shape[0])))

    in_maps = []
    for c in range(8):
        b, r = c // 4, c % 4
        T0 = r * NOWN
        xh = np.zeros((NTP, D), np.float32)
        lo = T0 - NHALO
        src_lo = max(lo, 0)
        xh[src_lo - lo:] = x_prompt[b, src_lo:T0 + NOWN]
        pos = np.concatenate([np.arange(lo, T0 + NOWN), PAST + np.arange(128)]).astype(np.float32)
        valid = (pos[:NTP] >= 0).astype(np.float32)
        cosv, sinv = _rope_tables(np.maximum(pos, 0))
        xs = np.zeros((128, D), np.float32)
        xs[:8] = x_sample[c]
        g0 = 32 * r
        gs = slice(g0, g0 + 32)
        st_lay = lambda a: np.ascontiguousarray(a.reshape(16, 2, 64).transpose(1, 2, 0).reshape(128, 16))
        def row_lay_rep(a):
            t = a.reshape(4, 4, 2, 64)
            t = np.broadcast_to(t[:, :, :, None, :], (4, 4, 2, 16, 64))
            return np.ascontiguousarray(t.transpose(1, 2, 3, 0, 4).reshape(128, 4, 64))
        def row_lay_b(a):
            t = a.reshape(4, 4, 2, 64, 16)
            return np.ascontiguousarray(t.transpose(1, 2, 4, 0, 3).reshape(128, 4, 64))
        def st_lay_c(a):
            t = a.reshape(16, 2, 16, 64)
            return np.ascontiguousarray(t.transpose(1, 3, 0, 2).reshape(128, 16, 16))
        lst32 = np.broadcast_to(log_step[gs][:, None], (32, 64))
        sel = np.zeros((128, 4), np.float32)
        sel[:, r] = 1.0
        sre0 = np.stack([st_lay(state_ssm_re[0, 4 * b + i, gs]) for i in range(4)], 1)
        sim0 = np.stack([st_lay(state_ssm_im[0, 4 * b + i, gs]) for i in range(4)], 1)
        w_in_c_rolled = np.concatenate([w_in_c0[:, 512 * r:512 * (r + 1)], w_in_c0[:, 512:2048], w_in_c0[:, 2048:]], 1)
        m = {
            "xh": xh, "xs": xs,
            "cs": np.ascontiguousarray(cosv.reshape(25, 128, 64).transpose(1, 0, 2)),
            "sn": np.ascontiguousarray(sinv.reshape(25, 128, 64).transpose(1, 0, 2)),
            "valid": np.ascontiguousarray(valid.reshape(24, 128).T),
            "ck": np.ascontiguousarray(cache_win_k[0, c].reshape(2048, 1024)),
            "cv": np.ascontiguousarray(cache_win_v[0, c].reshape(2048, 1024)),
            "sconv": np.ascontiguousarray(state_conv[0, c].reshape(2, 8, 128).transpose(2, 1, 0)),
            "g_attn": bc(f(attn_norm)[0]), "g_ssm": bc(f(ssm_norm)[0]), "g_fin": bc(f(final_norm)),
            "w_in_ab": w_in_ab0, "cw": np.ascontiguousarray(f(conv_w)[0].reshape(3, 8, 128).transpose(2, 1, 0)),
            "w_out_ab": w_out_ab0, "w_in_c": np.ascontiguousarray(w_in_c_rolled),
            "w_glu": w_glu0, "w_out_c": w_out_c0,
            "bglu": np.ascontiguousarray(b_glu0.reshape(16, 128).T),
            "dsk": np.ascontiguousarray(d_skip0[512 * r:512 * (r + 1)].reshape(4, 128).T),
            "maskp": maskp, "masks": masks,
            "lre_s": st_lay(lam_re[gs]), "lim_s": st_lay(lam_im[gs]), "lst_s": st_lay(lst32),
            "lre_r": row_lay_rep(lam_re[gs]), "lim_r": row_lay_rep(lam_im[gs]), "lst_r": row_lay_rep(np.ascontiguousarray(lst32)),
            "bre_r": row_lay_b(b_re[gs]), "bim_r": row_lay_b(b_im[gs]),
            "cre_s": st_lay_c(c_re[gs]), "cim_s": st_lay_c(c_im[gs]),
            "rmask": rmask, "smask": smask, "sel": sel, "sre0": sre0, "sim0": sim0, "iota": iota,
        }
        in_maps.append({k: np.ascontiguousarray(v, dtype=np.float32) for k, v in m.items()})

    res = run_bass_kernel_spmd(nc, in_maps, core_ids=list(range(8)))
    R = res.results
    _NC_CACHE['raw'] = R
    y_prompt = np.zeros((2, SEQ, D), np.float32)
    y_sample = np.zeros((8, 8, D), np.float32)
    kp = np.zeros((1, 2, 2048, 8, 128), np.float32)
    vp = np.zeros((1, 2, 2048, 8, 128), np.float32)
    convp = np.zeros((1, 2, 2, 1024), np.float32)
    srp = np.zeros((1, 2, 128, 64), np.float32)
    sip = np.zeros((1, 2, 128, 64), np.float32)
    ks = np.zeros((1, 8, 8, 8, 128), np.float32)
    vs = np.zeros((1, 8, 8, 8, 128), np.float32)
    convs = np.zeros((1, 8, 2, 1024), np.float32)
    srs = np.zeros((1, 8, 128, 64), np.float32)
    sis = np.zeros((1, 8, 128, 64), np.float32)
    unst = lambda a: a.reshape(2, 64, 16).transpose(2, 0, 1).reshape(32, 64)
    for c in range(8):
        b, r = c // 4, c % 4
        o = R[c]
        y_prompt[b, r * NOWN:(r + 1) * NOWN] = o["yp"]
        y_sample[c] = o["ys"][:8]
        if r >= 2:
            kp[0, b, (r - 2) * NOWN:(r - 1) * NOWN] = o["kp"].reshape(NOWN, 8, 128)
            vp[0, b, (r - 2) * NOWN:(r - 1) * NOWN] = o["vp"].reshape(NOWN, 8, 128)
        if r == 3:
            convp[0, b] = o["convp"].transpose(2, 1, 0).reshape(2, 1024)
        ks[0, c] = o["ks"][:8].reshape(8, 8, 128)
        vs[0, c] = o["vs"][:8].reshape(8, 8, 128)
        convs[0, c] = o["convs"].transpose(2, 1, 0).reshape(2, 1024)
        srp[0, b, 32 * r:32 * (r + 1)] = unst(o["ssm_p"][:, 0, :])
        sip[0, b, 32 * r:32 * (r + 1)] = unst(o["ssm_p"][:, 1, :])
        for i in range(4):
            srs[0, 4 * b + i, 32 * r:32 * (r + 1)] = unst(o["ssm_s"][:, i, 0, :])
            sis[0, 4 * b + i, 32 * r:32 * (r + 1)] = unst(o["ssm_s"][:, i, 1, :])
    return (y_prompt, y_sample, kp, vp, convp, srp, sip, ks, vs, convs, srs, sis)
```

```python
import math
import os
STOP = os.environ.get('MK_STOP', '')
from contextlib import ExitStack

import numpy as np
import concourse.bass as bass
import concourse.mybir as mybir
from concourse.bass_utils import run_bass_kernel_spmd

F32 = mybir.dt.float32
BF16 = mybir.dt.bfloat16
ALU = mybir.AluOpType
AF = mybir.ActivationFunctionType
AX = mybir.AxisListType

ENGS = ["pe", "act", "dve", "pool", "sp"]
D = 2048
KT = 16
NOWN = 1024
NHALO = 2048
NTP = NOWN + NHALO
NTILE_P = NTP // 128
NO = NOWN + 128
SEQ = 4096
PAST = 16384
NCH = 1032
TWO_PI = 2.0 * math.pi


class Buf:
    def __init__(self, t, name):
        self.t = t
        self.name = name
        self.w = {}
        self.r = {}
        self.dsem = None
        self.dcnt = 0


class Prog:
    def __init__(self, nc, stack):
        self.nc = nc
        self.stack = stack
        self.q = {e: [] for e in ENGS}
        self.cnt = {e: 0 for e in ENGS}
        self.seen = {e: {} for e in ENGS}
        self.sems = {}
        self.semval = {}
        for e in ["pe", "act", "dve", "pool"]:
            self.sems[e] = stack.enter_context(nc.semaphore("s_" + e))
        self.off = 16512
        self.free = []
        self.dval = {}
        self.phase_bufs = []

    def sb(self, name, shape, dt, at=None):
        nbytes = int(np.prod(shape[1:])) * (2 if dt == BF16 else 4)
        if at is None:
            at = self.off
            self.off = (at + nbytes + 63) // 64 * 64
        assert at + nbytes <= 229300, (name, at, nbytes)
        t = self.nc.alloc_sbuf_tensor_at(name, list(shape), dt, offset=at)
        b = Buf(t, name)
        b.at = at
        b.nbytes = nbytes
        return b

    def ps(self, name, shape, dt=F32):
        t = self.stack.enter_context(self.nc.psum_tensor(name, list(shape), dt))
        return Buf(t, name)

    def dram(self, name, shape, dt, kind="Internal"):
        t = self.nc.dram_tensor(name, list(shape), dt, kind=kind)
        return Buf(t, name)

    def _need(self, eng, k, v, waits):
        if self.seen[eng].get(k, 0) >= v:
            return
        waits[k] = max(waits.get(k, 0), v)

    def _deps(self, eng, reads, writes):
        waits = {}
        for b in reads:
            for k, v in b.w.items():
                self._need(eng, k, v, waits)
        for b in writes:
            for k, v in b.w.items():
                self._need(eng, k, v, waits)
            for k, v in b.r.items():
                self._need(eng, k, v, waits)
        for k, v in waits.items():
            self.seen[eng][k] = v
        return [(self.sems[k], v) for k, v in waits.items()]

    def _commit(self, k, v, reads, writes):
        self.semval[k] = v
        for b in reads:
            b.r[k] = max(b.r.get(k, 0), v)
        for b in writes:
            b.w[k] = max(b.w.get(k, 0), v)
            b.r = {}

    def op(self, eng, fn, reads=(), writes=()):
        reads = [b for b in reads if b is not None]
        writes = [b for b in writes if b is not None]
        wl = self._deps(eng, reads, writes)
        self.cnt[eng] += 1
        sem = self.sems[eng]

        def emit(h, fn=fn, wl=wl, sem=sem):
            for s, v in wl:
                h.wait_ge(s, v)
            fn(h).then_inc(sem, 1)

        self.q[eng].append(emit)
        self._commit(eng, self.cnt[eng], reads, writes)

    def mm(self, fns, reads, writes):
        eng = "pe"
        wl = self._deps(eng, reads, writes)
        self.cnt[eng] += 1
        sem = self.sems[eng]

        def emit(h, fns=fns, wl=wl, sem=sem):
            for s, v in wl:
                h.wait_ge(s, v)
            for f in fns[:-1]:
                f(h)
            fns[-1](h).then_inc(sem, 1)

        self.q[eng].append(emit)
        self._commit(eng, self.cnt[eng], reads, writes)

    def dma(self, eng, out, in_, reads, writes, semb, **kw):
        reads = [b for b in reads if b is not None]
        writes = [b for b in writes if b is not None]
        if eng == "pool":
            if getattr(semb, "psem", None) is None:
                key = "q%d" % len(self.sems)
                self.sems[key] = self.stack.enter_context(self.nc.semaphore(key))
                semb.psem = key
                semb.pcnt = 0
            wl = self._deps(eng, reads, writes)
            semb.pcnt += 16
            sem = self.sems[semb.psem]

            def emit_p(h, wl=wl, sem=sem, out=out, in_=in_, kw=kw):
                for s, v in wl:
                    h.wait_ge(s, v)
                h.dma_start(out=out, in_=in_, **kw).then_inc(sem, 16)

            self.q[eng].append(emit_p)
            self._commit(semb.psem, semb.pcnt, reads, writes)
            return
        if semb.dsem is None:
            if self.free:
                key = self.free.pop()
            else:
                key = "d%d" % len(self.sems)
                self.sems[key] = self.stack.enter_context(self.nc.semaphore(key))
            semb.dsem = key
            semb.dcnt = self.dval.get(key, 0)
            self.phase_bufs.append(semb)
        wl = self._deps(eng, reads, writes)
        semb.dcnt += 16
        self.dval[semb.dsem] = semb.dcnt
        sem = self.sems[semb.dsem]

        def emit(h, wl=wl, sem=sem, out=out, in_=in_, kw=kw):
            for s, v in wl:
                h.wait_ge(s, v)
            h.dma_start(out=out, in_=in_, **kw).then_inc(sem, 16)

        self.q[eng].append(emit)
        self._commit(semb.dsem, semb.dcnt, reads, writes)

    def coll(self, src, dst, groups):
        key = "c%d" % len(self.sems)
        self.sems[key] = self.stack.enter_context(self.nc.semaphore(key))
        wl = self._deps("pool", [src], [dst])
        sem = self.sems[key]

        def emit(h, wl=wl, sem=sem):
            for s, v in wl:
                h.wait_ge(s, v)
            h.collective_compute("AllGather", ALU.bypass, replica_groups=groups,
                                 ins=[src.t.ap()], outs=[dst.t.ap()]).then_inc(sem)

        self.q["pool"].append(emit)
        self._commit(key, 1, [src], [dst])

    def barrier(self):
        for b in self.phase_bufs:
            self.free.append(b.dsem)
            b.dsem = None
        self.phase_bufs = []
        items = list(self.semval.items())
        for e in ENGS:
            wl = []
            for k, v in items:
                if self.seen[e].get(k, 0) < v:
                    self.seen[e][k] = v
                    wl.append((self.sems[k], v))

            def emit(h, wl=wl):
                for s, v in wl:
                    h.wait_ge(s, v)

            if wl:
                self.q[e].append(emit)

    def run(self):
        nc = self.nc
        with nc.Block() as block:
            @block.tensor
            def _(h):
                for f in self.q["pe"]:
                    f(h)

            @block.scalar
            def _(h):
                for f in self.q["act"]:
                    f(h)

            @block.vector
            def _(h):
                for f in self.q["dve"]:
                    f(h)

            @block.gpsimd
            def _(h):
                for f in self.q["pool"]:
                    f(h)

            @block.sync
            def _(h):
                for f in self.q["sp"]:
                    f(h)


def mult_of(d):
    d = np.asarray(d)
    m = ((d >= 0) & (d <= 128)).astype(np.float32)
    m += ((d >= 0) & (d <= 512) & (d % 4 == 0))
    m += ((d >= 0) & (d <= 2048) & (d % 16 == 0))
    return m.astype(np.float32)


IN_SPECS = [
    ("xh", [NTP, D]), ("xs", [128, D]), ("cs", [128, 25, 64]), ("sn", [128, 25, 64]),
    ("valid", [128, 24]), ("ck", [2048, 1024]), ("cv", [2048, 1024]), ("sconv", [128, 8, 2]),
    ("g_attn", [128, D]), ("g_ssm", [128, D]), ("g_fin", [128, D]),
    ("w_in_ab", [D, 8192]), ("cw", [128, 8, 3]), ("w_out_ab", [D, D]), ("w_in_c", [D, 4096]),
    ("w_glu", [D, D]), ("w_out_c", [D, D]), ("bglu", [128, 16]), ("dsk", [128, 4]),
    ("maskp", [128, 20, 512]), ("masks", [128, 17, 128]),
    ("lre_s", [128, 16]), ("lim_s", [128, 16]), ("lst_s", [128, 16]),
    ("lre_r", [128, 4, 64]), ("lim_r", [128, 4, 64]), ("lst_r", [128, 4, 64]),
    ("bre_r", [128, 4, 64]), ("bim_r", [128, 4, 64]),
    ("cre_s", [128, 16, 16]), ("cim_s", [128, 16, 16]),
    ("rmask", [128, 8]), ("smask", [128, 2]), ("sel", [128, 4]),
    ("sre0", [128, 4, 16]), ("sim0", [128, 4, 16]), ("iota", [128, 1024]),
]
OUT_SPECS = [
    ("yp", [NOWN, D]), ("ys", [128, D]), ("kp", [NOWN, 1024]), ("vp", [NOWN, 1024]),
    ("convp", [128, 8, 2]), ("ssm_p", [128, 2, 16]), ("ks", [128, 1024]), ("vs", [128, 1024]),
    ("convs", [128, 8, 2]), ("ssm_s", [128, 4, 2, 16]),
]


def build_nc():
    nc = bass.Bass("TRN2", target_bir_lowering=False)
    IN = {}
    for n, s in IN_SPECS:
        IN[n] = nc.dram_tensor(n, s, F32, kind="ExternalInput")
    OUT = {}
    for n, s in OUT_SPECS:
        OUT[n] = nc.dram_tensor(n, s, F32, kind="ExternalOutput")
    st = ExitStack()
    with st:
        P = Prog(nc, st)
        build_program(nc, P, IN, OUT)
        P.run()
    return nc


def build_program(nc, P, IN, OUT):
    GROUPS = [[0, 1, 2, 3], [4, 5, 6, 7]]
    outbufs = {n: Buf(OUT[n], n) for n in OUT}
    kT_scr = P.dram("kT_scr", [8, 128, NTP], BF16)
    v_scr = P.dram("v_scr", [NTP, 1024], BF16)
    kTs_scr = P.dram("kTs_scr", [8, 128, 2176], BF16)
    vs_scr = P.dram("vs_scr", [2176, 1024], BF16)
    qT_scr = P.dram("qT_scr", [8, 128, NO], BF16)
    h1_scr = P.dram("h1_scr", [NO, D], F32)
    hn_src = [P.dram("hn_src%d" % j, [256, NCH], BF16) for j in range(8)]
    hn_dst = [P.dram("hn_dst%d" % j, [4 * 256, NCH], BF16) for j in range(8)]
    y_src = [P.dram("y_src%d" % j, [512, 516], BF16) for j in range(8)]
    y_dst = [P.dram("y_dst%d" % j, [4 * 512, 516], BF16) for j in range(8)]

    pf = [P.ps("pf%d" % i, [128, 512], F32) for i in range(6)]
    pb = [P.ps("pb%d" % i, [128, 8, 128], BF16) for i in range(2)]
    pfi = [0]
    pbi = [0]

    def next_pf():
        pfi[0] = (pfi[0] + 1) % 4
        return pf[pfi[0]]

    def next_pb():
        pbi[0] = (pbi[0] + 1) % 2
        return pb[pbi[0]]

    ident = P.sb("ident", [128, 128], BF16)
    P.op("pool", lambda h: h.memset(ident.t[:], 1.0), [], [ident])
    P.op("pool", lambda h: h.affine_select(ident.t[:], ident.t[:], [[-1, 128]], ALU.is_equal, 0.0,
                                            base=0, channel_multiplier=1), [ident], [ident])
    ones_bf = P.sb("ones_bf", [128, 128], BF16)
    P.op("pool", lambda h: h.memset(ones_bf.t[:], 1.0), [], [ones_bf])
    gt = P.sb("gt", [128, D], F32)
    cs = P.sb("cs", [128, 25, 64], F32)
    sn = P.sb("sn", [128, 25, 64], F32)
    valid = P.sb("valid", [128, 24], F32)
    validB = P.sb("validB", [128, 24, 128], BF16)
    cw = P.sb("cw", [128, 8, 3], F32)
    bglu = P.sb("bglu", [128, 16], F32)
    dsk = P.sb("dsk", [128, 4], F32)
    sel = P.sb("sel", [128, 4], F32)
    eps_t = P.sb("eps_t", [128, 1], F32)
    P.op("pool", lambda h: h.memset(eps_t.t[:], 1e-6), [], [eps_t])
    for b_, n in [(cs, "cs"), (sn, "sn"), (valid, "valid"), (cw, "cw"), (bglu, "bglu"), (dsk, "dsk"), (sel, "sel")]:
        P.dma("sp", b_.t[:], IN[n].ap(), [], [b_], b_)
    P.op("dve", lambda h: h.tensor_copy(validB.t[:], valid.t[:].unsqueeze(2).to_broadcast([128, 24, 128])),
         [valid], [validB])
    pospi = P.sb("pospi", [128, 1], F32)
    P.op("pool", lambda h: h.memset(pospi.t[:], math.pi), [], [pospi])
    CONST_END = P.off
    hnT_o = P.sb("hnT_o", [128, KT, NO], BF16)

    def load_g(name):
        P.dma("sp", gt.t[:], IN[name].ap(), [], [gt], gt)

    def rmsnorm_rows(xt, xn, ss, junk):
        P.op("act", lambda h: h.activation(junk.t[:], xt.t[:], AF.Square, accum_out=ss.t[:, 0:1]), [xt], [junk, ss])
        P.op("act", lambda h: h.activation(ss.t[:, 1:2], ss.t[:, 0:1], AF.Sqrt, bias=eps_t.t[:, 0:1], scale=1.0 / D), [ss, eps_t], [ss])
        P.op("dve", lambda h: h.reciprocal(ss.t[:, 2:3], ss.t[:, 1:2]), [ss], [ss])
        P.op("dve", lambda h: h.scalar_tensor_tensor(xn.t[:], xt.t[:], ss.t[:, 2:3], gt.t[:], ALU.mult, ALU.mult),
             [xt, ss, gt], [xn])

    def transpose_rows(xn, dst, dst_ap_fn):
        for half in range(2):
            p = next_pb()
            fns = []
            for j in range(8):
                kt = half * 8 + j
                fns.append(lambda h, p=p, j=j, kt=kt: h.transpose(p.t[:, j, :], xn.t[:, kt * 128:(kt + 1) * 128], ident.t[:]))
            P.mm(fns, [xn, ident], [p])
            P.op("act", lambda h, p=p, half=half: h.activation(dst_ap_fn(half), p.t[:], AF.Identity), [p], [dst])

    A0 = P.off
    wkv = P.sb("wkv", [128, KT, 2048], BF16)
    xts = [P.sb("xt%d" % i, [128, D], F32) for i in range(3)]
    xns = [P.sb("xn%d" % i, [128, D], BF16) for i in range(3)]
    hts = [P.sb("ht%d" % i, [128, KT, 128], BF16) for i in range(3)]
    ss = P.sb("ss", [128, 4], F32)
    krs = [P.sb("kr%d" % i, [128, 1024], F32) for i in range(2)]
    vfs = [P.sb("vf%d" % i, [128, 1024], F32) for i in range(2)]
    t1 = P.sb("t1", [128, 256], F32)
    t2 = P.sb("t2", [128, 256], F32)
    krbs = [P.sb("krb%d" % i, [128, 1024], BF16) for i in range(2)]
    vbs = [P.sb("vb%d" % i, [128, 1024], BF16) for i in range(2)]
    kTts = [P.sb("kTt%d" % i, [128, 8, 128], BF16) for i in range(2)]
    kTt = kTts[0]
    hprev2 = P.sb("hprev2", [128, KT, 2], BF16)
    A1_END = P.off

    load_g("g_attn")
    for half in range(2):
        P.dma("pool", wkv.t[:, :, half * 1024:(half + 1) * 1024],
              IN["w_in_ab"].ap()[:, 1024 + half * 1024:2048 + half * 1024].rearrange("(kt p) c -> p kt c", p=128),
              [], [wkv], wkv)

    def rotary(pk, ti, dst, c0):
        v = pk.t[:].rearrange("p (h two d) -> p h two d", h=4, two=2)
        o = dst.t[:, c0:c0 + 512].rearrange("p (h two d) -> p h two d", h=4, two=2)
        cb = cs.t[:, ti, :].unsqueeze(1).to_broadcast([128, 4, 64])
        sb_ = sn.t[:, ti, :].unsqueeze(1).to_broadcast([128, 4, 64])
        a = t1.t[:].rearrange("p (h d) -> p h d", h=4)
        b = t2.t[:].rearrange("p (h d) -> p h d", h=4)
        P.op("dve", lambda h: h.tensor_tensor(a, v[:, :, 0, :], cb, ALU.mult), [pk, cs], [t1])
        P.op("dve", lambda h: h.tensor_tensor(b, v[:, :, 1, :], sb_, ALU.mult), [pk, sn], [t2])
        P.op("dve", lambda h: h.tensor_tensor(o[:, :, 0, :], a, b, ALU.subtract), [t1, t2], [dst])
        P.op("dve", lambda h: h.tensor_tensor(a, v[:, :, 1, :], cb, ALU.mult), [pk, cs], [t1])
        P.op("dve", lambda h: h.tensor_tensor(b, v[:, :, 0, :], sb_, ALU.mult), [pk, sn], [t2])
        P.op("dve", lambda h: h.tensor_tensor(o[:, :, 1, :], a, b, ALU.add), [t1, t2], [dst])

    def store_kT(src_bf, scr, col0, kb=None):
        p = next_pb()
        if kb is None:
            kb = kTt
        fns = [(lambda h, p=p, j=j: h.transpose(p.t[:, j, :], src_bf.t[:, j * 128:(j + 1) * 128], ident.t[:])) for j in range(8)]
        P.mm(fns, [src_bf, ident], [p])
        P.op("act", lambda h, p=p, kb=kb: h.activation(kb.t[:], p.t[:], AF.Identity), [p], [kb])
        P.dma("sp", scr.t.ap()[:, :, col0:col0 + 128].rearrange("h d t -> d h t"), kb.t[:], [kb], [scr], kb)

    def stageL(ti):
        xt = xts[ti % 3]
        src = IN["xh"].ap()[ti * 128:(ti + 1) * 128, :] if ti < 24 else IN["xs"].ap()
        P.dma("sp", xt.t[:], src, [], [xt], xt)

    def stageN(ti):
        rmsnorm_rows(xts[ti % 3], xns[ti % 3], ss, xns[ti % 3])

    def stageA2(ti):
        xn = xns[ti % 3]
        if ti < 16:
            ht = hts[ti % 3]
            transpose_rows(xn, ht, lambda half, ht=ht: ht.t[:, half * 8:(half + 1) * 8, :])
            if ti == 15:
                P.op("dve", lambda h, ht=ht: h.tensor_copy(hprev2.t[:], ht.t[:, :, 126:128]), [ht], [hprev2])
            return (lambda kt, ht=ht: ht.t[:, kt, :]), ht
        o0 = (ti - 16) * 128
        transpose_rows(xn, hnT_o, lambda half, o0=o0: hnT_o.t[:, half * 8:(half + 1) * 8, o0:o0 + 128])
        return (lambda kt, o0=o0: hnT_o.t[:, kt, o0:o0 + 128]), hnT_o

    def stageM(ti, lhs, hb):
        kr = krs[ti % 2]
        vf = vfs[ti % 2]
        for g4 in range(4):
            pk = next_pf()
            fns = [(lambda h, pk=pk, kt=kt, g4=g4, lhs=lhs: h.matmul(pk.t[:], lhs(kt), wkv.t[:, kt, g4 * 512:(g4 + 1) * 512],
                                                                    start=(kt == 0), stop=(kt == KT - 1))) for kt in range(KT)]
            P.mm(fns, [hb, wkv], [pk])
            if g4 < 2:
                rotary(pk, ti, kr, g4 * 512)
            else:
                c0 = (g4 - 2) * 512
                P.op("act", lambda h, pk=pk, c0=c0, vf=vf: h.activation(vf.t[:, c0:c0 + 512], pk.t[:], AF.Identity), [pk], [vf])

    def stageKpre(ti):
        kr = krs[ti % 2]
        vf = vfs[ti % 2]
        krb = krbs[ti % 2]
        vb = vbs[ti % 2]
        P.op("act", lambda h: h.activation(krb.t[:], kr.t[:], AF.Identity), [kr], [krb])
        if ti < 24:
            P.op("dve", lambda h: h.tensor_scalar(vb.t[:], vf.t[:], valid.t[:, ti:ti + 1], None, ALU.mult), [vf, valid], [vb])
        else:
            P.op("dve", lambda h: h.tensor_copy(vb.t[:], vf.t[:]), [vf], [vb])

    def stageKpost(ti):
        kr = krs[ti % 2]
        vf = vfs[ti % 2]
        krb = krbs[ti % 2]
        vb = vbs[ti % 2]
        kb = kTts[ti % 2]
        if ti < 24:
            store_kT(krb, kT_scr, ti * 128, kb)
            P.dma("sp", v_scr.t.ap()[ti * 128:(ti + 1) * 128, :], vb.t[:], [vb], [v_scr], vb)
            if ti >= 16:
                r0 = (ti - 16) * 128
                P.dma("sp", OUT["kp"].ap()[r0:r0 + 128, :], kr.t[:], [kr], [outbufs["kp"]], kr)
                P.dma("sp", OUT["vp"].ap()[r0:r0 + 128, :], vf.t[:], [vf], [outbufs["vp"]], vf)
        else:
            store_kT(krb, kTs_scr, 2048, kb)
            P.dma("sp", vs_scr.t.ap()[2048:2176, :], vb.t[:], [vb], [vs_scr], vb)
            P.dma("sp", OUT["ks"].ap(), kr.t[:], [kr], [outbufs["ks"]], kr)
            P.dma("sp", OUT["vs"].ap(), vf.t[:], [vf], [outbufs["vs"]], vf)

    for t_ in range(3):
        stageL(t_)
    stageN(0)
    stageN(1)
    infoA = {0: stageA2(0)}
    for ti in range(25):
        if ti + 3 < 25:
            stageL(ti + 3)
        if ti + 2 < 25:
            stageN(ti + 2)
        if ti >= 1:
            stageKpre(ti - 1)
        if ti + 1 < 25:
            infoA[ti + 1] = stageA2(ti + 1)
        stageM(ti, *infoA[ti])
        if ti >= 1:
            stageKpost(ti - 1)
    stageKpre(24)
    stageKpost(24)
    def cacheL(ti):
        xt = xts[ti % 3]
        P.dma("sp", xt.t[:, 0:1024], IN["ck"].ap()[ti * 128:(ti + 1) * 128, :], [], [xt], xt)
        P.dma("sp", xt.t[:, 1024:2048], IN["cv"].ap()[ti * 128:(ti + 1) * 128, :], [], [xt], xt)

    cacheL(0)
    cacheL(1)
    for ti in range(16):
        if ti + 2 < 16:
            cacheL(ti + 2)
        xt = xts[ti % 3]
        krb = krbs[ti % 2]
        vb = vbs[ti % 2]
        P.op("act", lambda h, xt=xt, krb=krb: h.activation(krb.t[:], xt.t[:, 0:1024], AF.Identity), [xt], [krb])
        P.op("dve", lambda h, xt=xt, vb=vb: h.tensor_copy(vb.t[:], xt.t[:, 1024:2048]), [xt], [vb])
        store_kT(krb, kTs_scr, ti * 128, kTts[ti % 2])
        P.dma("sp", vs_scr.t.ap()[ti * 128:(ti + 1) * 128, :], vb.t[:], [vb], [vs_scr], vb)

    if STOP == 'A1':
        P.barrier()
        return
    P.barrier()
    P.off = A0
    wq = P.sb("wq", [128, KT, 1024], BF16)
    hprev2b = P.sb("hprev2b", [128, KT, 2], BF16)
    qf = P.sb("qf", [128, 1024], F32)
    qb = P.sb("qb", [128, 1024], BF16)
    t1 = P.sb("t1b", [128, 256], F32)
    t2 = P.sb("t2b", [128, 256], F32)
    kTt = P.sb("kTtb", [128, 8, 128], BF16)
    hprev2k = P.sb("hprev2k", [128, KT, 2], BF16, at=hprev2.at)
    hprev2k.w = dict(hprev2.w)
    P.dma("pool", wq.t[:], IN["w_in_ab"].ap()[:, 0:1024].rearrange("(kt p) c -> p kt c", p=128), [], [wq], wq)
    for tj in range(9):
        ti = 16 + tj
        o0 = tj * 128
        for g2_ in range(2):
            pk = next_pf()
            fns = [(lambda h, pk=pk, kt=kt, g2_=g2_, o0=o0: h.matmul(pk.t[:], hnT_o.t[:, kt, o0:o0 + 128],
                                                                      wq.t[:, kt, g2_ * 512:(g2_ + 1) * 512],
                                                                      start=(kt == 0), stop=(kt == KT - 1))) for kt in range(KT)]
            P.mm(fns, [hnT_o, wq], [pk])
            rotary(pk, ti, qf, g2_ * 512)
        P.op("act", lambda h: h.activation(qb.t[:], qf.t[:], AF.Identity), [qf], [qb])
        store_kT(qb, qT_scr, o0)

    if STOP == 'A2':
        P.barrier()
        return
    P.barrier()
    P.off = A0
    ocat = P.sb("ocat", [128, KT, NO], BF16)
    hp2 = P.sb("hp2", [128, KT, 2], BF16)
    B0 = P.off
    P.op("dve", lambda h: h.tensor_copy(hp2.t[:], hprev2k.t[:]), [hprev2k], [hp2])
    P.barrier()
    maskp = P.sb("maskp", [128, 20, 512], BF16)
    masks_ = P.sb("masks_", [128, 17, 128], BF16)
    P.dma("pool", maskp.t[:], IN["maskp"].ap(), [], [maskp], maskp)
    P.dma("pool", masks_.t[:], IN["masks"].ap(), [], [masks_], masks_)
    kTh = [P.sb("kTh%d" % i, [128, NTP], BF16) for i in range(1)] * 2
    vh = [P.sb("vh%d" % i, [128, 24, 128], BF16) for i in range(1)] * 2
    kTsh = [P.sb("kTsh%d" % i, [128, 2176], BF16) for i in range(1)] * 2
    vsh = [P.sb("vsh%d" % i, [128, 17, 128], BF16) for i in range(1)] * 2
    qTh = [P.sb("qTh%d" % i, [128, NO], BF16) for i in range(1)] * 2
    wt = [P.sb("wt%d" % i, [128, KT, 128], BF16) for i in range(4)]
    pts = [P.sb("pt%d" % i, [128, 512], BF16) for i in range(4)]
    ptm = [P.sb("ptm%d" % i, [128, 512], BF16) for i in range(4)]
    za = P.sb("za", [128, NO], F32)
    rl = P.sb("rl", [128, 512], F32)
    of = P.sb("of", [128, 512], F32)
    sg = P.sb("sg", [128, 512], F32)

    def silu_evac(pk, dstb, dst_ap, n):
        P.op("act", lambda h: h.activation(sg.t[:, 0:n], pk.t[:, 0:n], AF.Exp, scale=-1.0), [pk], [sg])
        P.op("dve", lambda h: h.tensor_scalar(sg.t[:, 0:n], sg.t[:, 0:n], 1.0, None, ALU.add), [sg], [sg])
        P.op("dve", lambda h: h.reciprocal(sg.t[:, 0:n], sg.t[:, 0:n]), [sg], [sg])
        P.op("dve", lambda h: h.tensor_tensor(dst_ap, pk.t[:, 0:n], sg.t[:, 0:n], ALU.mult), [pk, sg], [dstb])
    fb = [P.sb("fb%d" % i, [128, NO + 2], F32) for i in range(4)]
    convo_p = P.sb("convo_p", [128, 8, 2], F32)
    convo_s = P.sb("convo_s", [128, 8, 2], F32)
    sconv = P.sb("sconv", [128, 8, 2], F32)
    P.dma("sp", sconv.t[:], IN["sconv"].ap(), [], [sconv], sconv)
    scale = 128.0 ** -0.5

    def load_wt(i, c0):
        P.dma("pool", wt[i].t[:], IN["w_in_ab"].ap()[:, c0:c0 + 128].rearrange("(kt p) c -> p kt c", p=128), [], [wt[i]], wt[i])

    def proj_feat(wb, dst_ap_fn, evac, rhs_buf, rhs_fn, n):
        pk = next_pf()
        fns = [(lambda h, pk=pk, kt=kt: h.matmul(pk.t[:, 0:n], wb.t[:, kt, :], rhs_fn(kt), start=(kt == 0), stop=(kt == KT - 1)))
               for kt in range(KT)]
        P.mm(fns, [wb, rhs_buf], [pk])
        evac(pk)

    def attention(hh, qT, q0, nq, kT, vt, ktiles, mask_fn, vB_fn, o_dst_fn, zcol0):
        po = pf[4]
        pl = pf[5]
        nk = len(ktiles)
        LA = 3
        pms = {}

        def issue_S(i):
            kt_ = ktiles[i]
            ps_ = next_pf()
            P.mm([lambda h, ps_=ps_, kt_=kt_: h.matmul(ps_.t[:, 0:nq], kT.t[:, kt_ * 128:(kt_ + 1) * 128], qT.t[:, q0:q0 + nq],
                                                        start=True, stop=True)], [kT, qT], [ps_])
            pe_ = pts[i % 4]
            pm_ = ptm[i % 4]
            P.op("act", lambda h, ps_=ps_, pe_=pe_: h.activation(pe_.t[:, 0:nq], ps_.t[:, 0:nq], AF.Exp, scale=scale), [ps_], [pe_])
            mk, mb = mask_fn(i)
            eng = "dve"
            P.op(eng, lambda h, pe_=pe_, pm_=pm_, mk=mk: h.tensor_tensor(pm_.t[:, 0:nq], pe_.t[:, 0:nq], mk, ALU.mult), [pe_, mb], [pm_])
            pms[i] = pm_

        def issue_PV(i):
            kt_ = ktiles[i]
            pm_ = pms[i]
            vB, vBb = vB_fn(i)
            P.mm([lambda h, pm_=pm_, kt_=kt_, i=i: h.matmul(po.t[:, 0:nq], vt.t[:, kt_, :], pm_.t[:, 0:nq], start=(i == 0), stop=(i == nk - 1)),
                  lambda h, pm_=pm_, vB=vB, i=i: h.matmul(pl.t[:, 0:nq], vB, pm_.t[:, 0:nq], start=(i == 0), stop=(i == nk - 1))],
                 [vt, pm_, vBb], [po, pl])

        for i in range(min(LA, nk)):
            issue_S(i)
        for i in range(nk):
            if i + LA < nk:
                issue_S(i + LA)
            issue_PV(i)
        P.op("dve", lambda h: h.reciprocal(rl.t[:, 0:nq], pl.t[:, 0:nq]), [pl], [rl])
        P.op("dve", lambda h: h.tensor_tensor(of.t[:, 0:nq], po.t[:, 0:nq], rl.t[:, 0:nq], ALU.mult), [po, rl], [of])
        P.op("dve", lambda h: h.tensor_tensor(o_dst_fn(), of.t[:, 0:nq], za.t[:, zcol0:zcol0 + nq], ALU.mult), [of, za], [ocat])

    for hh in range(8):
        b2 = hh % 2
        P.dma("sp", kTh[b2].t[:], kT_scr.t.ap()[hh], [kT_scr], [kTh[b2]], kTh[b2])
        P.dma("sp", vh[b2].t[:], v_scr.t.ap()[:, hh * 128:(hh + 1) * 128].rearrange("(t p) d -> p t d", p=128), [v_scr], [vh[b2]], vh[b2])
        P.dma("sp", kTsh[b2].t[:], kTs_scr.t.ap()[hh], [kTs_scr], [kTsh[b2]], kTsh[b2])
        P.dma("sp", vsh[b2].t[:], vs_scr.t.ap()[:, hh * 128:(hh + 1) * 128].rearrange("(t p) d -> p t d", p=128), [vs_scr], [vsh[b2]], vsh[b2])
        P.dma("sp", qTh[b2].t[:], qT_scr.t.ap()[hh], [qT_scr], [qTh[b2]], qTh[b2])
        load_wt(0, 3072 + hh * 128)
        for (c0, n) in [(0, 512), (512, 512), (1024, 128)]:
            proj_feat(wt[0], None, lambda pk, c0=c0, n=n: silu_evac(pk, za, za.t[:, c0:c0 + n], n),
                      hnT_o, lambda kt, c0=c0, n=n: hnT_o.t[:, kt, c0:c0 + n], n)
        for qc in range(2):
            kts = list(range(4 * qc, 4 * qc + 20))
            attention(hh, qTh[b2], qc * 512, 512, kTh[b2], vh[b2], kts,
                      lambda i: (maskp.t[:, i, :], maskp),
                      lambda i, kts=kts: (validB.t[:, kts[i], :], validB),
                      lambda qc=qc, hh=hh: ocat.t[:, hh, qc * 512:(qc + 1) * 512], qc * 512)
        attention(hh, qTh[b2], 1024, 128, kTsh[b2], vsh[b2], list(range(17)),
                  lambda i: (masks_.t[:, i, :], masks_),
                  lambda i: (ones_bf.t[:], ones_bf),
                  lambda hh=hh: ocat.t[:, hh, 1024:1152], 1024)

    for cc in range(8):
        for j, base in enumerate([4096, 5120, 6144, 7168]):
            load_wt(j, base + cc * 128)
        bb, cb_, hb_, zb = fb
        for j, dstb in enumerate(fb):
            for (c0, n) in [(0, 512), (512, 512), (1024, 128)]:
                if j == 3:
                    ev = lambda pk, c0=c0, n=n, dstb=dstb: silu_evac(pk, dstb, dstb.t[:, 2 + c0:2 + c0 + n], n)
                else:
                    ev = lambda pk, c0=c0, n=n, dstb=dstb: P.op("act", lambda h: h.activation(dstb.t[:, 2 + c0:2 + c0 + n], pk.t[:, 0:n], AF.Identity), [pk], [dstb])
                proj_feat(wt[j], None, ev, hnT_o, lambda kt, c0=c0, n=n: hnT_o.t[:, kt, c0:c0 + n], n)
            if j in (1, 2):
                proj_feat(wt[j], None, lambda pk, dstb=dstb: P.op("act", lambda h: h.activation(dstb.t[:, 0:2], pk.t[:, 0:2], AF.Identity), [pk], [dstb]),
                          hp2, lambda kt: hp2.t[:, kt, :], 2)
        P.op("dve", lambda h: h.tensor_tensor(cb_.t[:], cb_.t[:], hb_.t[:], ALU.mult), [cb_, hb_], [cb_])
        w0 = cw.t[:, cc, 0:1]
        w1 = cw.t[:, cc, 1:2]
        w2 = cw.t[:, cc, 2:3]
        P.op("dve", lambda h, w2=w2: h.tensor_scalar(hb_.t[:, 2:1026], cb_.t[:, 2:1026], w2, None, ALU.mult), [cb_, cw], [hb_])
        P.op("dve", lambda h, w1=w1: h.scalar_tensor_tensor(hb_.t[:, 2:1026], cb_.t[:, 1:1025], w1, hb_.t[:, 2:1026], ALU.mult, ALU.add), [cb_, cw, hb_], [hb_])
        P.op("dve", lambda h, w0=w0: h.scalar_tensor_tensor(hb_.t[:, 2:1026], cb_.t[:, 0:1024], w0, hb_.t[:, 2:1026], ALU.mult, ALU.add), [cb_, cw, hb_], [hb_])
        P.op("dve", lambda h, cc=cc: h.tensor_copy(convo_p.t[:, cc, :], cb_.t[:, 1024:1026]), [cb_], [convo_p])
        P.op("dve", lambda h, cc=cc: h.tensor_copy(cb_.t[:, 1024:1026], sconv.t[:, cc, :]), [sconv], [cb_])
        P.op("dve", lambda h, w2=w2: h.tensor_scalar(hb_.t[:, 1026:1034], cb_.t[:, 1026:1034], w2, None, ALU.mult), [cb_, cw], [hb_])
        P.op("dve", lambda h, w1=w1: h.scalar_tensor_tensor(hb_.t[:, 1026:1034], cb_.t[:, 1025:1033], w1, hb_.t[:, 1026:1034], ALU.mult, ALU.add), [cb_, cw, hb_], [hb_])
        P.op("dve", lambda h, w0=w0: h.scalar_tensor_tensor(hb_.t[:, 1026:1034], cb_.t[:, 1024:1032], w0, hb_.t[:, 1026:1034], ALU.mult, ALU.add), [cb_, cw, hb_], [hb_])
        P.op("dve", lambda h, cc=cc: h.tensor_copy(convo_s.t[:, cc, :], cb_.t[:, 1032:1034]), [cb_], [convo_s])
        P.op("dve", lambda h: h.tensor_tensor(hb_.t[:, 2:1034], hb_.t[:, 2:1034], bb.t[:, 2:1034], ALU.mult), [hb_, bb], [hb_])
        P.op("dve", lambda h, cc=cc: h.tensor_tensor(ocat.t[:, 8 + cc, 0:1032], hb_.t[:, 2:1034], zb.t[:, 2:1034], ALU.mult), [hb_, zb], [ocat])
        P.op("dve", lambda h, cc=cc: h.memset(ocat.t[:, 8 + cc, 1032:1152], 0.0), [], [ocat])
    P.dma("sp", OUT["convp"].ap(), convo_p.t[:], [convo_p], [outbufs["convp"]], convo_p)
    P.dma("sp", OUT["convs"].ap(), convo_s.t[:], [convo_s], [outbufs["convs"]], convo_s)

    if STOP == 'B':
        P.barrier()
        return
    P.barrier()
    hn1T = P.sb("hn1T", [128, KT, NCH], BF16, at=hnT_o.at)
    P.off = B0
    wgb = [P.sb("wg%d" % i, [128, KT, 512], BF16) for i in range(2)]
    h1s = [P.sb("h1s%d" % i, [128, D], F32) for i in range(5)]
    xn1 = [P.sb("xn1%d" % i, [128, D], BF16) for i in range(2)]
    junk = P.sb("junk2", [128, D], F32)
    ss = P.sb("ss2", [128, 4], F32)
    tmpT = P.sb("tmpT", [128, KT, 128], BF16)
    load_g("g_ssm")
    wi = 0
    for tiles in [list(range(0, 5)), list(range(5, 9))]:
        for si, tj in enumerate(tiles):
            o0 = tj * 128
            src = IN["xh"].ap()[NHALO + o0:NHALO + o0 + 128, :] if tj < 8 else IN["xs"].ap()
            P.dma("sp", h1s[si].t[:], src, [], [h1s[si]], h1s[si])
        for g4 in range(4):
            wg = wgb[wi % 2]
            wi += 1
            P.dma("pool", wg.t[:], IN["w_out_ab"].ap()[:, g4 * 512:(g4 + 1) * 512].rearrange("(kt p) c -> p kt c", p=128), [], [wg], wg)
            for si, tj in enumerate(tiles):
                o0 = tj * 128
                ht_ = h1s[si]
                pk = next_pf()
                fns = [(lambda h, pk=pk, kt=kt, o0=o0, wg=wg: h.matmul(pk.t[:], ocat.t[:, kt, o0:o0 + 128], wg.t[:, kt, :],
                                                                         start=(kt == 0), stop=(kt == KT - 1))) for kt in range(KT)]
                P.mm(fns, [ocat, wg], [pk])
                P.op("dve", lambda h, pk=pk, ht_=ht_, g4=g4: h.tensor_tensor(ht_.t[:, g4 * 512:(g4 + 1) * 512], ht_.t[:, g4 * 512:(g4 + 1) * 512], pk.t[:], ALU.add),
                     [pk, ht_], [ht_])
        for si, tj in enumerate(tiles):
            o0 = tj * 128
            ht_ = h1s[si]
            P.dma("sp", h1_scr.t.ap()[o0:o0 + 128, :], ht_.t[:], [ht_], [h1_scr], ht_)
            if STOP == 'C1':
                if tj < 8:
                    P.dma("sp", OUT["yp"].ap()[o0:o0 + 128, :], ht_.t[:], [ht_], [outbufs["yp"]], ht_)
                else:
                    P.dma("sp", OUT["ys"].ap(), ht_.t[:], [ht_], [outbufs["ys"]], ht_)
            xn = xn1[tj % 2]
            rmsnorm_rows(ht_, xn, ss, junk)
            if tj < 8:
                transpose_rows(xn, hn1T, lambda half, o0=o0: hn1T.t[:, half * 8:(half + 1) * 8, o0:o0 + 128])
            else:
                transpose_rows(xn, tmpT, lambda half: tmpT.t[:, half * 8:(half + 1) * 8, :])
                P.op("dve", lambda h: h.tensor_copy(hn1T.t[:, :, 1024:1032], tmpT.t[:, :, 0:8]), [tmpT], [hn1T])
    for j in range(8):
        P.dma("sp", hn_src[j].t.ap().rearrange("(k p) t -> p k t", p=128), hn1T.t[:, 2 * j:2 * j + 2, :], [hn1T], [hn_src[j]], hn1T)
        P.coll(hn_src[j], hn_dst[j], GROUPS)

    if STOP == 'C1':
        P.barrier()
        return
    P.barrier()
    P.off = CONST_END
    uT = P.sb("uT", [128, 4, 4 * NCH], BF16)
    ygT = P.sb("ygT", [128, 4, 4 * NCH], BF16)
    L1 = P.off
    wu = P.sb("wu", [128, KT, 512], BF16)
    hch = [P.sb("hch%d" % i, [128, KT, 516], BF16) for i in range(2)]
    P.dma("pool", wu.t[:], IN["w_in_c"].ap()[:, 0:512].rearrange("(kt p) c -> p kt c", p=128), [], [wu], wu)
    ci = 0
    for r in range(4):
        for hf in range(2):
            hc = hch[ci % 2]
            ci += 1
            c0 = hf * 516
            for j in range(8):
                P.dma("sp", hc.t[:, 2 * j:2 * j + 2, :], hn_dst[j].t.ap()[r * 256:(r + 1) * 256, c0:c0 + 516].rearrange("(k p) t -> p k t", p=128),
                      [hn_dst[j]], [hc], hc)
            for ft in range(4):
                pk = next_pf()
                fns = [(lambda h, pk=pk, kt=kt, ft=ft, hc=hc: h.matmul(pk.t[:, 0:512], wu.t[:, kt, ft * 128:(ft + 1) * 128], hc.t[:, kt, 0:512],
                                                                      start=(kt == 0), stop=(kt == KT - 1))) for kt in range(KT)]
                P.mm(fns, [wu, hc], [pk])
                P.op("act", lambda h, pk=pk, ft=ft, r=r, c0=c0: h.activation(uT.t[:, ft, r * NCH + c0:r * NCH + c0 + 512], pk.t[:, 0:512], AF.Identity), [pk], [uT])
                pk2 = next_pf()
                fns2 = [(lambda h, pk2=pk2, kt=kt, ft=ft, hc=hc: h.matmul(pk2.t[:, 0:4], wu.t[:, kt, ft * 128:(ft + 1) * 128], hc.t[:, kt, 512:516],
                                                                         start=(kt == 0), stop=(kt == KT - 1))) for kt in range(KT)]
                P.mm(fns2, [wu, hc], [pk2])
                P.op("act", lambda h, pk2=pk2, ft=ft, r=r, c0=c0: h.activation(uT.t[:, ft, r * NCH + c0 + 512:r * NCH + c0 + 516], pk2.t[:, 0:4], AF.Identity), [pk2], [uT])

    if STOP == 'U':
        P.barrier()
        return
    P.barrier()
    P.off = L1
    def small(name, shape, dt=F32):
        return P.sb(name, shape, dt)
    lre_s = small("lre_s", [128, 16]); lim_s = small("lim_s", [128, 16]); lst_s = small("lst_s", [128, 16])
    lre_r = small("lre_r", [128, 256]); lim_r = small("lim_r", [128, 256]); lst_r = small("lst_r", [128, 256])
    bre_r = small("bre_r", [128, 256]); bim_r = small("bim_r", [128, 256])
    cre_s = small("cre_s", [128, 16, 16]); cim_s = small("cim_s", [128, 16, 16])
    rmask = small("rmask", [128, 8]); smask = small("smask", [128, 2])
    sre0 = small("sre0", [128, 4, 16]); sim0 = small("sim0", [128, 4, 16])
    iota = small("iota", [128, 1024])
    for b_, n in [(lre_s, "lre_s"), (lim_s, "lim_s"), (lst_s, "lst_s"), (cre_s, "cre_s"), (cim_s, "cim_s"),
                  (rmask, "rmask"), (smask, "smask"), (sre0, "sre0"), (sim0, "sim0"), (iota, "iota")]:
        P.dma("sp", b_.t[:], IN[n].ap(), [], [b_], b_)
    for b_, n in [(lre_r, "lre_r"), (lim_r, "lim_r"), (lst_r, "lst_r"), (bre_r, "bre_r"), (bim_r, "bim_r")]:
        P.dma("sp", b_.t[:], IN[n].ap().rearrange("p a b -> p (a b)"), [], [b_], b_)
    negpi = small("negpi", [128, 1])
    P.op("dve", lambda h: h.memset(negpi.t[:], -math.pi), [], [negpi])

    I32 = mybir.dt.int32
    tq = small("tq", [128, 1024]); tiq = small("tiq", [128, 1024], I32)
    halfpi = small("halfpi", [128, 1]); zero_t = small("zero_t", [128, 1])
    P.op("dve", lambda h: h.memset(halfpi.t[:], 0.5 * math.pi), [], [halfpi])
    P.op("dve", lambda h: h.memset(zero_t.t[:], 0.0), [], [zero_t])

    def sincos(ang_ap, n, s_ap, c_ap, rd, wr):
        for (dst, addc, bt, lo, hi) in [(s_ap, 0.0, zero_t, -math.pi, math.pi), (c_ap, 0.25, halfpi, -1.5 * math.pi, 0.5 * math.pi)]:
            P.op("dve", lambda h, addc=addc: h.tensor_scalar(tq.t[:, 0:n], ang_ap, 1.0 / TWO_PI, addc, ALU.mult, ALU.add), rd, [tq])
            P.op("dve", lambda h: h.tensor_copy(tiq.t[:, 0:n], tq.t[:, 0:n]), [tq], [tiq])
            P.op("dve", lambda h: h.tensor_copy(tq.t[:, 0:n], tiq.t[:, 0:n]), [tiq], [tq])
            P.op("dve", lambda h: h.scalar_tensor_tensor(tq.t[:, 0:n], tq.t[:, 0:n], -TWO_PI, ang_ap, ALU.mult, ALU.add), [tq] + rd, [tq])
            P.op("dve", lambda h, lo=lo, hi=hi: h.tensor_scalar(tq.t[:, 0:n], tq.t[:, 0:n], lo, hi, ALU.max, ALU.min), [tq], [tq])
            P.op("act", lambda h, dst=dst, bt=bt: h.activation(dst, tq.t[:, 0:n], AF.Sin, bias=bt.t[:, 0:1], scale=1.0), [tq, bt], wr)

    def disc(lre, lim, lst, n, pref):
        o = {}
        for nm in ["step", "mag", "th", "c", "s", "tmp", "nr", "den", "cr", "ci", "a", "b"]:
            o[nm] = small(pref + nm, [128, n])
        al = [o[k] for k in o] + [lre, lim, lst]
        P.op("act", lambda h: h.activation(o["step"].t[:], lst.t[:, 0:n], AF.Exp), al, al)
        P.op("dve", lambda h: h.tensor_tensor(o["th"].t[:], lim.t[:, 0:n], o["step"].t[:], ALU.mult), al, al)
        P.op("dve", lambda h: h.tensor_tensor(o["a"].t[:], lre.t[:, 0:n], o["step"].t[:], ALU.mult), al, al)
        P.op("act", lambda h: h.activation(o["mag"].t[:], o["a"].t[:], AF.Exp), al, al)
        sincos(o["th"].t[:], n, o["s"].t[:], o["c"].t[:], al, al)
        P.op("dve", lambda h: h.tensor_tensor(o["a"].t[:], o["mag"].t[:], o["c"].t[:], ALU.mult), al, al)
        P.op("dve", lambda h: h.tensor_scalar(o["nr"].t[:], o["a"].t[:], 1.0, -1.0, ALU.mult, ALU.add), al, al)
        P.op("dve", lambda h: h.tensor_tensor(o["b"].t[:], o["mag"].t[:], o["s"].t[:], ALU.mult), al, al)
        P.op("dve", lambda h: h.tensor_tensor(o["den"].t[:], lre.t[:, 0:n], lre.t[:, 0:n], ALU.mult), al, al)
        P.op("dve", lambda h: h.tensor_tensor(o["tmp"].t[:], lim.t[:, 0:n], lim.t[:, 0:n], ALU.mult), al, al)
        P.op("dve", lambda h: h.tensor_tensor(o["den"].t[:], o["den"].t[:], o["tmp"].t[:], ALU.add), al, al)
        P.op("dve", lambda h: h.reciprocal(o["den"].t[:], o["den"].t[:]), al, al)
        P.op("dve", lambda h: h.tensor_tensor(o["cr"].t[:], o["nr"].t[:], lre.t[:, 0:n], ALU.mult), al, al)
        P.op("dve", lambda h: h.tensor_tensor(o["tmp"].t[:], o["b"].t[:], lim.t[:, 0:n], ALU.mult), al, al)
        P.op("dve", lambda h: h.tensor_tensor(o["cr"].t[:], o["cr"].t[:], o["tmp"].t[:], ALU.add), al, al)
        P.op("dve", lambda h: h.tensor_tensor(o["cr"].t[:], o["cr"].t[:], o["den"].t[:], ALU.mult), al, al)
        P.op("dve", lambda h: h.tensor_tensor(o["ci"].t[:], o["b"].t[:], lre.t[:, 0:n], ALU.mult), al, al)
        P.op("dve", lambda h: h.tensor_tensor(o["tmp"].t[:], o["nr"].t[:], lim.t[:, 0:n], ALU.mult), al, al)
        P.op("dve", lambda h: h.tensor_tensor(o["ci"].t[:], o["ci"].t[:], o["tmp"].t[:], ALU.subtract), al, al)
        P.op("dve", lambda h: h.tensor_tensor(o["ci"].t[:], o["ci"].t[:], o["den"].t[:], ALU.mult), al, al)
        return o, al

    ds_, als = disc(lre_s, lim_s, lst_s, 16, "ds_")
    dr_, alr = disc(lre_r, lim_r, lst_r, 256, "dr_")
    bbr = small("bbr", [128, 256]); bbi = small("bbi", [128, 256]); tmpr = small("tmpr", [128, 256])
    alr2 = alr + [bbr, bbi, tmpr, bre_r, bim_r]
    P.op("dve", lambda h: h.tensor_tensor(bbr.t[:], dr_["cr"].t[:], bre_r.t[:], ALU.mult), alr2, alr2)
    P.op("dve", lambda h: h.tensor_tensor(tmpr.t[:], dr_["ci"].t[:], bim_r.t[:], ALU.mult), alr2, alr2)
    P.op("dve", lambda h: h.tensor_tensor(bbr.t[:], bbr.t[:], tmpr.t[:], ALU.subtract), alr2, alr2)
    P.op("dve", lambda h: h.tensor_tensor(bbi.t[:], dr_["cr"].t[:], bim_r.t[:], ALU.mult), alr2, alr2)
    P.op("dve", lambda h: h.tensor_tensor(tmpr.t[:], dr_["ci"].t[:], bre_r.t[:], ALU.mult), alr2, alr2)
    P.op("dve", lambda h: h.tensor_tensor(bbi.t[:], bbi.t[:], tmpr.t[:], ALU.add), alr2, alr2)
    BbT = [small("BbT%d" % ri, [128, 16, 128], BF16) for ri in range(2)]
    for ri, src in enumerate([bbr, bbi]):
        for qq in range(4):
            for g2 in range(2):
                m = rmask.t[:, qq * 2 + g2:qq * 2 + g2 + 1]
                o_ap = BbT[ri].t[:].rearrange("p (ft q) c -> p ft q c", q=4)[:, :, qq, g2 * 64:(g2 + 1) * 64]
                i_ap = src.t[:].rearrange("p (ft d) -> p ft d", ft=4)
                P.op("dve", lambda h, o_ap=o_ap, i_ap=i_ap, m=m: h.tensor_scalar(o_ap, i_ap, m, None, ALU.mult), alr2 + [rmask], [BbT[ri]])
    CT = [small("CT%d" % ri, [128, 16, 128], BF16) for ri in range(2)]
    for ri in range(2):
        P.op("dve", lambda h, ri=ri: h.memset(CT[ri].t[:], 0.0), [], [CT[ri]])
    for ri, (src, sgn) in enumerate([(cre_s, 1.0), (cim_s, -1.0)]):
        for pair in range(16):
            qq = pair % 4
            for g2 in range(2):
                m = smask.t[:, g2:g2 + 1]
                col = qq * 32 + g2 * 16
                P.op("dve", lambda h, ri=ri, pair=pair, col=col, m=m, src=src, sgn=sgn: h.tensor_scalar(
                    CT[ri].t[:, pair, col:col + 16], src.t[:, pair, :], m, sgn, ALU.mult, ALU.mult), [src, smask], [CT[ri]])
    rr = ds_["mag"]; th = ds_["th"]
    cth = small("cth", [128, 16]); sth = small("sth", [128, 16])
    P.op("dve", lambda h: h.tensor_copy(cth.t[:], ds_["c"].t[:]), als, [cth])
    P.op("dve", lambda h: h.tensor_copy(sth.t[:], ds_["s"].t[:]), als, [sth])

    tabc = [small("tabc%d" % i, [128, 1024]) for i in range(4)]
    tabs = [small("tabs%d" % i, [128, 1024]) for i in range(4)]
    WS = [dict(gr=small("gr0", [128, 1024]), gi=small("gi0", [128, 1024]), yr=small("yr0", [128, 1024]), yi=small("yi0", [128, 1024]),
               hrb=small("hrb0", [128, 1024], BF16), hib=small("hib0", [128, 1024], BF16))]
    hib1 = small("hib1", [128, 1024], BF16)
    ytmp = small("ytmp", [128, 512]); ysq = small("ysq", [128, 512])
    stp_l = [small("stp%d" % p_, [128, 2]) for p_ in range(16)]
    sts_p = [small("stsp%d" % p_, [128, 4, 2]) for p_ in range(16)]
    m32 = small("m32", [128, 32])
    P.op("dve", lambda h: h.memset(m32.t[:], 1.0), [], [m32])
    P.op("dve", lambda h: h.memset(m32.t[:].rearrange("p (r c) -> p r c", r=4)[:, :, 0], 0.0), [m32], [m32])
    for w_ in WS:
        w_["gin"] = small("gin0", [128, 2]); w_["hend"] = small("hend0", [128, 2])
        w_["gin4"] = small("gin40", [128, 4, 2]); w_["d0"] = small("d00", [128, 32])
    for p_ in range(16):
        P.op("dve", lambda h, p_=p_: h.memset(stp_l[p_].t[:], 0.0), [], [stp_l[p_]])
    P.barrier()
    blkA = lre_r.at
    blkB = dr_["step"].at
    WS.append(dict(gr=P.sb("gr1", [128, 1024], F32, at=blkB), gi=P.sb("gi1", [128, 1024], F32, at=blkB + 4096),
                   yr=P.sb("yr1", [128, 1024], F32, at=blkB + 8192), yi=P.sb("yi1", [128, 1024], F32, at=blkA),
                   hrb=P.sb("hrb1", [128, 1024], BF16, at=blkB + 12288), hib=hib1,
                   gin=small("gin1", [128, 2]), hend=small("hend1", [128, 2]),
                   gin4=small("gin41", [128, 4, 2]), d0=small("d01", [128, 32])))
    ang = WS[0]["gr"]
    GC = 1.5957691216057308
    kcount = [0]

    def stageX(pair, qq, ft, col0, T, tc_, ts_, init_re, init_im, init_bufs, out_b):
        w = WS[kcount[0] % 2]
        kcount[0] += 1
        gr, gi, yr, yi, hrb, hib, gin, hend = w["gr"], w["gi"], w["yr"], w["yi"], w["hrb"], w["hib"], w["gin"], w["hend"]
        for h0 in range(0, T, 512):
            n = min(512, T - h0)
            pxr = next_pf(); pxi = next_pf()
            P.mm([lambda h, pxr=pxr, n=n, h0=h0: h.matmul(pxr.t[:, 0:n], BbT[0].t[:, pair, :], uT.t[:, ft, col0 + h0:col0 + h0 + n], start=True, stop=True)], [BbT[0], uT], [pxr])
            P.mm([lambda h, pxi=pxi, n=n, h0=h0: h.matmul(pxi.t[:, 0:n], BbT[1].t[:, pair, :], uT.t[:, ft, col0 + h0:col0 + h0 + n], start=True, stop=True)], [BbT[1], uT], [pxi])
            c_ = tc_.t[:, h0:h0 + n]; s_ = ts_.t[:, h0:h0 + n]
            P.op("dve", lambda h, pxr=pxr, c_=c_, n=n, h0=h0: h.tensor_tensor(yr.t[:, h0:h0 + n], pxr.t[:, 0:n], c_, ALU.mult), [pxr, tc_], [yr])
            P.op("dve", lambda h, pxi=pxi, s_=s_, n=n, h0=h0: h.tensor_tensor(gr.t[:, h0:h0 + n], pxi.t[:, 0:n], s_, ALU.mult), [pxi, ts_], [gr])
            P.op("dve", lambda h, n=n, h0=h0: h.tensor_tensor(yr.t[:, h0:h0 + n], yr.t[:, h0:h0 + n], gr.t[:, h0:h0 + n], ALU.add), [yr, gr], [yr])
            P.op("dve", lambda h, pxi=pxi, c_=c_, n=n, h0=h0: h.tensor_tensor(yi.t[:, h0:h0 + n], pxi.t[:, 0:n], c_, ALU.mult), [pxi, tc_], [yi])
            P.op("dve", lambda h, pxr=pxr, s_=s_, n=n, h0=h0: h.tensor_tensor(gi.t[:, h0:h0 + n], pxr.t[:, 0:n], s_, ALU.mult), [pxr, ts_], [gi])
            P.op("dve", lambda h, n=n, h0=h0: h.tensor_tensor(yi.t[:, h0:h0 + n], yi.t[:, h0:h0 + n], gi.t[:, h0:h0 + n], ALU.subtract), [yi, gi], [yi])
        ct = cth.t[:, pair:pair + 1]; st_ = sth.t[:, pair:pair + 1]
        P.op("dve", lambda h: h.tensor_scalar(gin.t[:, 0:1], init_re, ct, None, ALU.mult), init_bufs + [cth], [gin])
        P.op("dve", lambda h: h.scalar_tensor_tensor(gin.t[:, 0:1], init_im, st_, gin.t[:, 0:1], ALU.mult, ALU.subtract), init_bufs + [sth, gin], [gin])
        P.op("dve", lambda h: h.tensor_scalar(gin.t[:, 0:1], gin.t[:, 0:1], -1.0, None, ALU.mult), [gin], [gin])
        P.op("dve", lambda h: h.tensor_scalar(gin.t[:, 1:2], init_re, st_, None, ALU.mult), init_bufs + [sth], [gin])
        P.op("dve", lambda h: h.scalar_tensor_tensor(gin.t[:, 1:2], init_im, ct, gin.t[:, 1:2], ALU.mult, ALU.add), init_bufs + [cth, gin], [gin])
        rb = rr.t[:, pair:pair + 1].to_broadcast([128, T])
        P.op("dve", lambda h: h.tensor_tensor_scan(gr.t[:, 0:T], rb, yr.t[:, 0:T], gin.t[:, 0:1], ALU.mult, ALU.add), [yr, gin] + als, [gr])
        P.op("dve", lambda h: h.tensor_tensor_scan(gi.t[:, 0:T], rb, yi.t[:, 0:T], gin.t[:, 1:2], ALU.mult, ALU.add), [yi, gin] + als, [gi])
        c_ = tc_.t[:, 0:T]; s_ = ts_.t[:, 0:T]
        yrb = yr.t[:].bitcast(BF16)
        yib = yi.t[:].bitcast(BF16)
        p1 = yrb[:, 0:T]; p2 = yrb[:, 1024:1024 + T]; p3 = yib[:, 0:T]; p4 = yib[:, 1024:1024 + T]
        P.op("dve", lambda h: h.tensor_tensor(p1, gr.t[:, 0:T], c_, ALU.mult), [gr, tc_], [yr])
        P.op("dve", lambda h: h.scalar_tensor_tensor(p2, gi.t[:, 0:T], -1.0, s_, ALU.mult, ALU.mult), [gi, ts_], [yr])
        P.op("dve", lambda h: h.tensor_tensor(p3, gr.t[:, 0:T], s_, ALU.mult), [gr, ts_], [yi])
        P.op("dve", lambda h: h.tensor_tensor(p4, gi.t[:, 0:T], c_, ALU.mult), [gi, tc_], [yi])
        cl = tc_.t[:, T - 1:T]; sl = ts_.t[:, T - 1:T]
        P.op("dve", lambda h: h.tensor_tensor(gin.t[:, 0:1], gi.t[:, T - 1:T], sl, ALU.mult), [gi, ts_, gin], [gin])
        P.op("dve", lambda h: h.tensor_tensor(gin.t[:, 1:2], gi.t[:, T - 1:T], cl, ALU.mult), [gi, tc_, gin], [gin])
        P.op("dve", lambda h: h.scalar_tensor_tensor(out_b.t[:, 0:1], gr.t[:, T - 1:T], cl, gin.t[:, 0:1], ALU.mult, ALU.subtract), [gr, tc_, gin], [out_b])
        P.op("dve", lambda h: h.scalar_tensor_tensor(out_b.t[:, 1:2], gr.t[:, T - 1:T], sl, gin.t[:, 1:2], ALU.mult, ALU.add), [gr, ts_, gin], [out_b])
        return dict(pair=pair, qq=qq, T=T, prods=[(0, lambda h0, n: yrb[:, h0:h0 + n], yr), (0, lambda h0, n: yrb[:, 1024 + h0:1024 + h0 + n], yr),
                                                  (1, lambda h0, n: yib[:, h0:h0 + n], yi), (1, lambda h0, n: yib[:, 1024 + h0:1024 + h0 + n], yi)], after=[])

    def v48(ap):
        return ap.rearrange("p (r c) -> p r c", r=4)

    def stageXs(pair, qq, ft, tc_, ts_):
        w = WS[kcount[0] % 2]
        kcount[0] += 1
        gr, gi, yr, yi, hrb, hib, gin4, d0 = w["gr"], w["gi"], w["yr"], w["yi"], w["hrb"], w["hib"], w["gin4"], w["d0"]
        ucols = v48(uT.t[:, ft, :])[:, :, 1024:1032]
        pxr = next_pf(); pxi = next_pf()
        P.mm([lambda h: h.matmul(pxr.t[:, 0:32], BbT[0].t[:, pair, :], ucols, start=True, stop=True)], [BbT[0], uT], [pxr])
        P.mm([lambda h: h.matmul(pxi.t[:, 0:32], BbT[1].t[:, pair, :], ucols, start=True, stop=True)], [BbT[1], uT], [pxi])
        cb = tc_.t[:, 0:8].unsqueeze(1).to_broadcast([128, 4, 8])
        sb_ = ts_.t[:, 0:8].unsqueeze(1).to_broadcast([128, 4, 8])
        xr = v48(pxr.t[:, 0:32]); xi = v48(pxi.t[:, 0:32])
        yrv = v48(yr.t[:, 0:32]); yiv = v48(yi.t[:, 0:32]); grv = v48(gr.t[:, 0:32]); giv = v48(gi.t[:, 0:32])
        P.op("dve", lambda h: h.tensor_tensor(yrv, xr, cb, ALU.mult), [pxr, tc_], [yr])
        P.op("dve", lambda h: h.tensor_tensor(grv, xi, sb_, ALU.mult), [pxi, ts_], [gr])
        P.op("dve", lambda h: h.tensor_tensor(yrv, yrv, grv, ALU.add), [yr, gr], [yr])
        P.op("dve", lambda h: h.tensor_tensor(yiv, xi, cb, ALU.mult), [pxi, tc_], [yi])
        P.op("dve", lambda h: h.tensor_tensor(giv, xr, sb_, ALU.mult), [pxr, ts_], [gi])
        P.op("dve", lambda h: h.tensor_tensor(yiv, yiv, giv, ALU.subtract), [yi, gi], [yi])
        ct = cth.t[:, pair:pair + 1]; st_ = sth.t[:, pair:pair + 1]; rs = rr.t[:, pair:pair + 1]
        ire = sre0.t[:, :, pair]; iim = sim0.t[:, :, pair]
        g_re = gin4.t[:, :, 0]; g_im = gin4.t[:, :, 1]
        P.op("dve", lambda h: h.tensor_scalar(g_re, ire, ct, None, ALU.mult), [sre0, cth], [gin4])
        P.op("dve", lambda h: h.scalar_tensor_tensor(g_re, iim, st_, g_re, ALU.mult, ALU.subtract), [sim0, sth, gin4], [gin4])
        P.op("dve", lambda h: h.tensor_scalar(g_re, g_re, -1.0, None, ALU.mult), [gin4], [gin4])
        P.op("dve", lambda h: h.tensor_scalar(g_im, ire, st_, None, ALU.mult), [sre0, sth, gin4], [gin4])
        P.op("dve", lambda h: h.scalar_tensor_tensor(g_im, iim, ct, g_im, ALU.mult, ALU.add), [sim0, cth, gin4], [gin4])
        P.op("dve", lambda h: h.scalar_tensor_tensor(yrv[:, :, 0], g_re, rs, yrv[:, :, 0], ALU.mult, ALU.add), [gin4, yr] + als, [yr])
        P.op("dve", lambda h: h.scalar_tensor_tensor(yiv[:, :, 0], g_im, rs, yiv[:, :, 0], ALU.mult, ALU.add), [gin4, yi] + als, [yi])
        P.op("dve", lambda h: h.tensor_scalar(d0.t[:], m32.t[:], rs, None, ALU.mult), [m32] + als, [d0])
        P.op("dve", lambda h: h.tensor_tensor_scan(gr.t[:, 0:32], d0.t[:], yr.t[:, 0:32], 0.0, ALU.mult, ALU.add), [yr, d0], [gr])
        P.op("dve", lambda h: h.tensor_tensor_scan(gi.t[:, 0:32], d0.t[:], yi.t[:, 0:32], 0.0, ALU.mult, ALU.add), [yi, d0], [gi])
        hrv = v48(hrb.t[:, 0:32]); hiv = v48(hib.t[:, 0:32])
        out_b = sts_p[pair]
        P.op("dve", lambda h: h.tensor_tensor(yrv, grv, cb, ALU.mult), [gr, tc_], [yr])
        P.op("dve", lambda h: h.tensor_tensor(yiv, giv, sb_, ALU.mult), [gi, ts_], [yi])
        P.op("dve", lambda h: h.tensor_tensor(hrv, yrv, yiv, ALU.subtract), [yr, yi], [hrb])
        P.op("dve", lambda h: h.tensor_tensor(out_b.t[:, :, 0], yrv[:, :, 7], yiv[:, :, 7], ALU.subtract), [yr, yi], [out_b])
        P.op("dve", lambda h: h.tensor_tensor(yrv, grv, sb_, ALU.mult), [gr, ts_, hrb, out_b], [yr])
        P.op("dve", lambda h: h.tensor_tensor(yiv, giv, cb, ALU.mult), [gi, tc_, hrb, out_b], [yi])
        P.op("dve", lambda h: h.tensor_tensor(hiv, yrv, yiv, ALU.add), [yr, yi], [hib])
        P.op("dve", lambda h: h.tensor_tensor(out_b.t[:, :, 1], yrv[:, :, 7], yiv[:, :, 7], ALU.add), [yr, yi], [out_b])
        return dict(pair=pair, qq=qq, T=32, hrb=hrb, hib=hib, after=[])

    def y_evac_s(ft):
        ucols = v48(uT.t[:, ft, :])[:, :, 1024:1032]
        dcols = v48(ygT.t[:, ft, :])[:, :, 1024:1032]
        yp_ = ypb[0]
        ypv = v48(yp_.t[:, 0:32]); yt = v48(ytmp.t[:, 0:32]); yq = v48(ysq.t[:, 0:32])
        P.op("dve", lambda h: h.scalar_tensor_tensor(yt, ucols, dsk.t[:, ft:ft + 1], ypv, ALU.mult, ALU.add), [uT, dsk, yp_], [ytmp])
        P.op("dve", lambda h: h.tensor_tensor(yq, yt, yt, ALU.mult), [ytmp], [ysq])
        P.op("dve", lambda h: h.tensor_scalar(yq, yq, 0.044715, 1.0, ALU.mult, ALU.add), [ysq], [ysq])
        P.op("dve", lambda h: h.tensor_tensor(yq, yq, yt, ALU.mult), [ysq, ytmp], [ysq])
        P.op("act", lambda h: h.activation(ysq.t[:, 0:32], ysq.t[:, 0:32], AF.Sigmoid, scale=GC), [ysq], [ysq])
        P.op("dve", lambda h: h.tensor_tensor(dcols, yq, yt, ALU.mult), [ysq, ytmp], [ygT])

    ypb = [pf[4], pf[5]]

    def stageY(c):
        if "prods" in c:
            pair, qq, T = c["pair"], c["qq"], c["T"]
            for h0 in range(0, T, 512):
                n = min(512, T - h0)
                yp_ = ypb[h0 // 512]
                fns = []
                for k_, (ci_, rf_, _b) in enumerate(c["prods"]):
                    fns.append(lambda h, yp_=yp_, n=n, h0=h0, ci_=ci_, rf_=rf_, k_=k_: h.matmul(
                        yp_.t[:, 0:n], CT[ci_].t[:, pair, :], rf_(h0, n), start=(qq == 0 and k_ == 0), stop=(qq == 3 and k_ == 3)))
                P.mm(fns, [CT[0], CT[1], c["prods"][0][2], c["prods"][2][2]], [yp_])
            for f in c["after"]:
                f()
            return
        pair, qq, T, hrb, hib = c["pair"], c["qq"], c["T"], c["hrb"], c["hib"]
        for h0 in range(0, T, 512):
            n = min(512, T - h0)
            yp_ = ypb[h0 // 512]
            P.mm([lambda h, yp_=yp_, n=n, h0=h0: h.matmul(yp_.t[:, 0:n], CT[0].t[:, pair, :], hrb.t[:, h0:h0 + n], start=(qq == 0), stop=False),
                  lambda h, yp_=yp_, n=n, h0=h0: h.matmul(yp_.t[:, 0:n], CT[1].t[:, pair, :], hib.t[:, h0:h0 + n], start=False, stop=(qq == 3))],
                 [CT[0], CT[1], hrb, hib], [yp_])
        for f in c["after"]:
            f()

    def y_evac(yp_, ft, col0, n):
        P.op("dve", lambda h: h.scalar_tensor_tensor(ytmp.t[:, 0:n], uT.t[:, ft, col0:col0 + n], dsk.t[:, ft:ft + 1], yp_.t[:, 0:n], ALU.mult, ALU.add),
             [uT, dsk, yp_], [ytmp])
        P.op("dve", lambda h: h.tensor_tensor(ysq.t[:, 0:n], ytmp.t[:, 0:n], ytmp.t[:, 0:n], ALU.mult), [ytmp], [ysq])
        P.op("dve", lambda h: h.tensor_scalar(ysq.t[:, 0:n], ysq.t[:, 0:n], 0.044715, 1.0, ALU.mult, ALU.add), [ysq], [ysq])
        P.op("dve", lambda h: h.tensor_tensor(ysq.t[:, 0:n], ysq.t[:, 0:n], ytmp.t[:, 0:n], ALU.mult), [ysq, ytmp], [ysq])
        P.op("act", lambda h: h.activation(ysq.t[:, 0:n], ysq.t[:, 0:n], AF.Sigmoid, scale=GC), [ysq], [ysq])
        P.op("dve", lambda h: h.tensor_tensor(ygT.t[:, ft, col0:col0 + n], ysq.t[:, 0:n], ytmp.t[:, 0:n], ALU.mult), [ysq, ytmp], [ygT])

    pending = [None]

    def push(ctx):
        if pending[0] is not None:
            stageY(pending[0])
        pending[0] = ctx

    for ft in range(4):
        for qq in range(4):
            pair = ft * 4 + qq
            P.op("dve", lambda h, pair=pair: h.tensor_scalar(ang.t[:], iota.t[:], th.t[:, pair:pair + 1], None, ALU.mult), [iota] + als, [ang])
            sincos(ang.t[:], 1024, tabs[qq].t[:], tabc[qq].t[:], [ang], [tabs[qq], tabc[qq]])
        for seg in range(4):
            for qq in range(4):
                pair = ft * 4 + qq
                ctx = stageX(pair, qq, ft, seg * NCH, 1024, tabc[qq], tabs[qq],
                             stp_l[pair].t[:, 0:1], stp_l[pair].t[:, 1:2], [stp_l[pair]], stp_l[pair])
                if qq == 3:
                    ctx["after"] = [(lambda ft=ft, seg=seg, hf=hf: y_evac(ypb[hf], ft, seg * NCH + hf * 512, 512)) for hf in range(2)]
                push(ctx)
        for qq in range(4):
            pair = ft * 4 + qq
            ctx = stageXs(pair, qq, ft, tabc[qq], tabs[qq])
            if qq == 3:
                ctx["after"] = [(lambda ft=ft: y_evac_s(ft))]
            push(ctx)
    push(None)
    stp2 = P.sb("stp2", [128, 2, 16], F32, at=tq.at); sts2 = P.sb("sts2", [128, 4, 2, 16], F32, at=tq.at + 128)
    for p_ in range(16):
        P.op("dve", lambda h, p_=p_: h.tensor_copy(stp2.t[:, :, p_], stp_l[p_].t[:, 0:2]), [stp_l[p_]], [stp2])
        P.op("dve", lambda h, p_=p_: h.tensor_copy(sts2.t[:, :, :, p_], sts_p[p_].t[:]), [sts_p[p_]], [sts2])
    if STOP == 'SSM':
        for r_ in range(4):
            P.dma("pool", OUT["yp"].ap().rearrange("(p f x) c -> p f (x c)", p=128, f=4)[:, :, r_ * 1024:(r_ + 1) * 1024],
                  ygT.t[:, :, r_ * NCH:r_ * NCH + 1024], [ygT], [outbufs["yp"]], ygT)
        P.dma("pool", OUT["ys"].ap()[:, 0:128].rearrange("p (f r s) -> p f r s", f=4, r=4),
              ygT.t[:].rearrange("p f (r c) -> p f r c", r=4)[:, :, :, 1024:1032], [ygT], [outbufs["ys"]], ygT)
    P.dma("sp", OUT["ssm_p"].ap(), stp2.t[:], [stp2], [outbufs["ssm_p"]], stp2)
    P.dma("sp", OUT["ssm_s"].ap(), sts2.t[:], [sts2], [outbufs["ssm_s"]], sts2)
    P.barrier()
    hn1o = P.sb("hn1o", [128, KT, NCH], BF16, at=uT.at)
    P.off = L1 + 40960
    zall = P.sb("zall", [128, KT, NCH], BF16)
    wz = [P.sb("wz%d" % i, [128, KT, 128], BF16) for i in range(2)]
    zz = P.sb("zz", [128, NCH], F32)
    for j in range(8):
        P.dma("sp", y_src[j].t.ap().rearrange("(ft p) t -> p ft t", p=128), ygT.t[:, :, j * 516:(j + 1) * 516], [ygT], [y_src[j]], ygT)
        P.coll(y_src[j], y_dst[j], GROUPS)

    for j in range(8):
        P.dma("sp", hn1o.t[:, 2 * j:2 * j + 2, :], hn_src[j].t.ap().rearrange("(k p) t -> p k t", p=128), [hn_src[j]], [hn1o], hn1o)
    for nt in range(16):
        wb_ = wz[nt % 2]
        P.dma("pool", wb_.t[:], IN["w_in_c"].ap()[:, 2048 + nt * 128:2048 + (nt + 1) * 128].rearrange("(kt p) c -> p kt c", p=128), [], [wb_], wb_)
        for (c0, n) in [(0, 512), (512, 512), (1024, 8)]:
            def zev(pk, c0=c0, n=n, nt=nt):
                P.op("act", lambda h: h.activation(zz.t[:, c0:c0 + n], pk.t[:, 0:n], AF.Sigmoid), [pk], [zz])
                P.op("dve", lambda h: h.tensor_tensor(zall.t[:, nt, c0:c0 + n], zz.t[:, c0:c0 + n], pk.t[:, 0:n], ALU.mult), [zz, pk], [zall])
            proj_feat(wb_, None, zev, hn1o, lambda kt, c0=c0, n=n: hn1o.t[:, kt, c0:c0 + n], n)

    if STOP == 'SSM':
        P.barrier()
        return
    P.barrier()
    P.off = CONST_END
    y2T = P.sb("y2T", [128, KT, NO], BF16)
    F0 = P.off
    ygo = P.sb("ygo", [128, KT, NCH], BF16)
    ych = [P.sb("ych%d" % i, [128, 4, NCH], BF16) for i in range(2)]
    wt2 = [P.sb("wt2_%d" % i, [128, KT, 128], BF16) for i in range(2)]
    gl = P.sb("gl", [128, NCH], F32)
    assert P.off <= L1 + 40960
    P.op("pool", lambda h: h.memset(y2T.t[:, :, NCH:NO], 0.0), [], [y2T])
    ci = 0
    for rf in range(4):
        for r in range(4):
            yc = ych[ci % 2]; ci += 1
            for hh_ in range(2):
                P.dma("sp", yc.t[:, :, hh_ * 516:(hh_ + 1) * 516],
                      y_dst[2 * r + hh_].t.ap()[rf * 512:(rf + 1) * 512, :].rearrange("(ft p) t -> p ft t", p=128),
                      [y_dst[2 * r + hh_]], [yc], yc)
            dst = ygo.t[:, rf * 4:(rf + 1) * 4, :]
            if r == 0:
                P.op("dve", lambda h, yc=yc, dst=dst: h.tensor_scalar(dst, yc.t[:], sel.t[:, 0:1], None, ALU.mult), [yc, sel], [ygo])
            else:
                P.op("dve", lambda h, yc=yc, dst=dst, r=r: h.scalar_tensor_tensor(dst, yc.t[:], sel.t[:, r:r + 1], dst, ALU.mult, ALU.add), [yc, sel, ygo], [ygo])
    if STOP == 'G1':
        P.barrier()
        return
    for nt in range(int(os.environ.get('MK_NT', '16'))):
        wa = wt2[nt % 2]
        P.dma("pool", wa.t[:], IN["w_glu"].ap()[:, nt * 128:(nt + 1) * 128].rearrange("(kt p) c -> p kt c", p=128), [], [wa], wa)
        for (c0, n) in [(0, 512), (512, 512), (1024, 8)]:
            proj_feat(wa, None, lambda pk, c0=c0, n=n, nt=nt: P.op("act", lambda h: h.activation(gl.t[:, c0:c0 + n], pk.t[:, 0:n], AF.Sigmoid, bias=bglu.t[:, nt:nt + 1], scale=1.0), [pk, bglu], [gl]),
                      ygo, lambda kt, c0=c0, n=n: ygo.t[:, kt, c0:c0 + n], n)
        P.op("dve", lambda h, nt=nt: h.tensor_tensor(gl.t[:], gl.t[:], ygo.t[:, nt, :], ALU.mult), [gl, ygo], [gl])
        P.op("dve", lambda h, nt=nt: h.tensor_tensor(y2T.t[:, nt, 0:NCH], gl.t[:], zall.t[:, nt, :], ALU.mult), [gl, zall], [y2T])
    if STOP == 'G2':
        P.barrier()
        return
    P.barrier()
    P.off = F0
    wgb2 = [P.sb("wgc%d" % i, [128, KT, 512], BF16) for i in range(2)]
    h1s = [P.sb("h1f%d" % i, [128, D], F32) for i in range(5)]
    junk = P.sb("junk3", [128, D], F32); ss = P.sb("ss3", [128, 4], F32)
    yo = [P.sb("yo%d" % i, [128, D], F32) for i in range(2)]
    load_g("g_fin")
    wi = 0
    for tiles in [list(range(0, 5)), list(range(5, 9))]:
        for si, tj in enumerate(tiles):
            o0 = tj * 128
            P.dma("sp", h1s[si].t[:], h1_scr.t.ap()[o0:o0 + 128, :], [h1_scr], [h1s[si]], h1s[si])
        for g4 in range(4):
            wg = wgb2[wi % 2]
            wi += 1
            P.dma("pool", wg.t[:], IN["w_out_c"].ap()[:, g4 * 512:(g4 + 1) * 512].rearrange("(kt p) c -> p kt c", p=128), [], [wg], wg)
            for si, tj in enumerate(tiles):
                o0 = tj * 128
                ht_ = h1s[si]
                pk = next_pf()
                fns = [(lambda h, pk=pk, kt=kt, o0=o0, wg=wg: h.matmul(pk.t[:], y2T.t[:, kt, o0:o0 + 128], wg.t[:, kt, :],
                                                                         start=(kt == 0), stop=(kt == KT - 1))) for kt in range(KT)]
                P.mm(fns, [y2T, wg], [pk])
                P.op("dve", lambda h, pk=pk, ht_=ht_, g4=g4: h.tensor_tensor(ht_.t[:, g4 * 512:(g4 + 1) * 512], ht_.t[:, g4 * 512:(g4 + 1) * 512], pk.t[:], ALU.add),
                     [pk, ht_], [ht_])
        for si, tj in enumerate(tiles):
            o0 = tj * 128
            ht_ = h1s[si]
            yo_ = yo[tj % 2]
            rmsnorm_rows(ht_, yo_, ss, junk)
            if tj < 8:
                P.dma("sp", OUT["yp"].ap()[o0:o0 + 128, :], yo_.t[:], [yo_], [outbufs["yp"]], yo_)
            else:
                P.dma("sp", OUT["ys"].ap(), yo_.t[:], [yo_], [outbufs["ys"]], yo_)
    P.barrier()


_NC_CACHE = {}


def _rope_tables(pos):
    half = 64
    inv = (np.float32(10000.0) ** (-np.arange(half, dtype=np.float32) / np.float32(half))).astype(np.float32)
    ang = pos.astype(np.float32)[:, None] * inv[None, :]
    return np.cos(ang).astype(np.float32), np.sin(ang).astype(np.float32)


def kernel(x_prompt, x_sample, cache_win_k, cache_win_v, state_conv, state_ssm_re, state_ssm_im,
           attn_norm, w_in_ab, conv_w, w_out_ab, ssm_norm, w_in_c, lam_re, lam_im, log_step,
           b_re, b_im, c_re, c_im, d_skip, w_glu, b_glu, w_out_c, final_norm):
    f = lambda a: np.ascontiguousarray(np.asarray(a, dtype=np.float32))
    x_prompt, x_sample = f(x_prompt), f(x_sample)
    cache_win_k, cache_win_v, state_conv = f(cache_win_k), f(cache_win_v), f(state_conv)
    state_ssm_re, state_ssm_im = f(state_ssm_re), f(state_ssm_im)
    w_in_ab0, w_out_ab0, w_in_c0, w_glu0, w_out_c0 = f(w_in_ab)[0], f(w_out_ab)[0], f(w_in_c)[0], f(w_glu)[0], f(w_out_c)[0]
    lam_re, lam_im, log_step = f(lam_re)[0], f(lam_im)[0], f(log_step)[0]
    b_re, b_im, c_re, c_im = f(b_re)[0], f(b_im)[0], f(c_re)[0], f(c_im)[0]
    d_skip0, b_glu0 = f(d_skip)[0], f(b_glu)[0]
    if "nc" not in _NC_CACHE:
        _NC_CACHE["nc"] = build_nc()
    nc = _NC_CACHE["nc"]

    kk = np.arange(128)[:, None]
    qq_ = np.arange(512)[None, :]
    maskp = np.stack([mult_of(qq_ - ((i - 16) * 128 + kk)) for i in range(20)], 1)
    rows = np.arange(2176).reshape(17, 128)
    s_ = np.arange(128)[None, :]
    masks = np.zeros((128, 17, 128), np.float32)
    for i in range(17):
        row = rows[i][:, None]
        m = mult_of(2048 + s_ - row)
        m[:, 8:] = ((2048 + s_[:, 8:] - row) == 0)
        masks[:, i, :] = m
    iota = np.broadcast_to(np.arange(1024, dtype=np.float32)[None, :], (128, 1024)).copy()
    rmask = np.zeros((128, 8), np.float32)
    for p in range(128):
        rmask[p, (p // 32) * 2 + (p % 32) // 16] = 1.0
    smask = np.zeros((128, 2), np.float32)
    smask[:64, 0] = 1.0
    smask[64:, 1] = 1.0
    bc = lambda v: np.ascontiguousarray(np.broadcast_to(v[None, :], (128, v.shape[0])))

    in_maps = []
    for c in range(8):
        b, r = c // 4, c % 4
        T0 = r * NOWN
        xh = np.zeros((NTP, D), np.float32)
        lo = T0 - NHALO
        src_lo = max(lo, 0)
        xh[src_lo - lo:] = x_prompt[b, src_lo:T0 + NOWN]
        pos = np.concatenate([np.arange(lo, T0 + NOWN), PAST + np.arange(128)]).astype(np.float32)
        valid = (pos[:NTP] >= 0).astype(np.float32)
        cosv, sinv = _rope_tables(np.maximum(pos, 0))
        xs = np.zeros((128, D), np.float32)
        xs[:8] = x_sample[c]
        g0 = 32 * r
        gs = slice(g0, g0 + 32)
        st_lay = lambda a: np.ascontiguousarray(a.reshape(16, 2, 64).transpose(1, 2, 0).reshape(128, 16))
        def row_lay_rep(a):
            t = a.reshape(4, 4, 2, 64)
            t = np.broadcast_to(t[:, :, :, None, :], (4, 4, 2, 16, 64))
            return np.ascontiguousarray(t.transpose(1, 2, 3, 0, 4).reshape(128, 4, 64))
        def row_lay_b(a):
            t = a.reshape(4, 4, 2, 64, 16)
            return np.ascontiguousarray(t.transpose(1, 2, 4, 0, 3).reshape(128, 4, 64))
        def st_lay_c(a):
            t = a.reshape(16, 2, 16, 64)
            return np.ascontiguousarray(t.transpose(1, 3, 0, 2).reshape(128, 16, 16))
        lst32 = np.broadcast_to(log_step[gs][:, None], (32, 64))
        sel = np.zeros((128, 4), np.float32)
        sel[:, r] = 1.0
        sre0 = np.stack([st_lay(state_ssm_re[0, 4 * b + i, gs]) for i in range(4)], 1)
        sim0 = np.stack([st_lay(state_ssm_im[0, 4 * b + i, gs]) for i in range(4)], 1)
        w_in_c_rolled = np.concatenate([w_in_c0[:, 512 * r:512 * (r + 1)], w_in_c0[:, 512:2048], w_in_c0[:, 2048:]], 1)
        m = {
            "xh": xh, "xs": xs,
            "cs": np.ascontiguousarray(cosv.reshape(25, 128, 64).transpose(1, 0, 2)),
            "sn": np.ascontiguousarray(sinv.reshape(25, 128, 64).transpose(1, 0, 2)),
            "valid": np.ascontiguousarray(valid.reshape(24, 128).T),
            "ck": np.ascontiguousarray(cache_win_k[0, c].reshape(2048, 1024)),
            "cv": np.ascontiguousarray(cache_win_v[0, c].reshape(2048, 1024)),
            "sconv": np.ascontiguousarray(state_conv[0, c].reshape(2, 8, 128).transpose(2, 1, 0)),
            "g_attn": bc(f(attn_norm)[0]), "g_ssm": bc(f(ssm_norm)[0]), "g_fin": bc(f(final_norm)),
            "w_in_ab": w_in_ab0, "cw": np.ascontiguousarray(f(conv_w)[0].reshape(3, 8, 128).transpose(2, 1, 0)),
            "w_out_ab": w_out_ab0, "w_in_c": np.ascontiguousarray(w_in_c_rolled),
            "w_glu": w_glu0, "w_out_c": w_out_c0,
            "bglu": np.ascontiguousarray(b_glu0.reshape(16, 128).T),
            "dsk": np.ascontiguousarray(d_skip0[512 * r:512 * (r + 1)].reshape(4, 128).T),
            "maskp": maskp, "masks": masks,
            "lre_s": st_lay(lam_re[gs]), "lim_s": st_lay(lam_im[gs]), "lst_s": st_lay(lst32),
            "lre_r": row_lay_rep(lam_re[gs]), "lim_r": row_lay_rep(lam_im[gs]), "lst_r": row_lay_rep(np.ascontiguousarray(lst32)),
            "bre_r": row_lay_b(b_re[gs]), "bim_r": row_lay_b(b_im[gs]),
            "cre_s": st_lay_c(c_re[gs]), "cim_s": st_lay_c(c_im[gs]),
            "rmask": rmask, "smask": smask, "sel": sel, "sre0": sre0, "sim0": sim0, "iota": iota,
        }
        in_maps.append({k: np.ascontiguousarray(v, dtype=np.float32) for k, v in m.items()})

    res = run_bass_kernel_spmd(nc, in_maps, core_ids=list(range(8)))
    R = res.results
    _NC_CACHE['raw'] = R
    y_prompt = np.zeros((2, SEQ, D), np.float32)
    y_sample = np.zeros((8, 8, D), np.float32)
    kp = np.zeros((1, 2, 2048, 8, 128), np.float32)
    vp = np.zeros((1, 2, 2048, 8, 128), np.float32)
    convp = np.zeros((1, 2, 2, 1024), np.float32)
    srp = np.zeros((1, 2, 128, 64), np.float32)
    sip = np.zeros((1, 2, 128, 64), np.float32)
    ks = np.zeros((1, 8, 8, 8, 128), np.float32)
    vs = np.zeros((1, 8, 8, 8, 128), np.float32)
    convs = np.zeros((1, 8, 2, 1024), np.float32)
    srs = np.zeros((1, 8, 128, 64), np.float32)
    sis = np.zeros((1, 8, 128, 64), np.float32)
    unst = lambda a: a.reshape(2, 64, 16).transpose(2, 0, 1).reshape(32, 64)
    for c in range(8):
        b, r = c // 4, c % 4
        o = R[c]
        y_prompt[b, r * NOWN:(r + 1) * NOWN] = o["yp"]
        y_sample[c] = o["ys"][:8]
        if r >= 2:
            kp[0, b, (r - 2) * NOWN:(r - 1) * NOWN] = o["kp"].reshape(NOWN, 8, 128)
            vp[0, b, (r - 2) * NOWN:(r - 1) * NOWN] = o["vp"].reshape(NOWN, 8, 128)
        if r == 3:
            convp[0, b] = o["convp"].transpose(2, 1, 0).reshape(2, 1024)
        ks[0, c] = o["ks"][:8].reshape(8, 8, 128)
        vs[0, c] = o["vs"][:8].reshape(8, 8, 128)
        convs[0, c] = o["convs"].transpose(2, 1, 0).reshape(2, 1024)
        srp[0, b, 32 * r:32 * (r + 1)] = unst(o["ssm_p"][:, 0, :])
        sip[0, b, 32 * r:32 * (r + 1)] = unst(o["ssm_p"][:, 1, :])
        for i in range(4):
            srs[0, 4 * b + i, 32 * r:32 * (r + 1)] = unst(o["ssm_s"][:, i, 0, :])
            sis[0, 4 * b + i, 32 * r:32 * (r + 1)] = unst(o["ssm_s"][:, i, 1, :])
    return (y_prompt, y_sample, kp, vp, convp, srp, sip, ks, vs, convs, srs, sis)
```

```python
import math
import os
STOP = os.environ.get('MK_STOP', '')
from contextlib import ExitStack

import numpy as np
import concourse.bass as bass
import concourse.mybir as mybir
from concourse.bass_utils import run_bass_kernel_spmd

F32 = mybir.dt.float32
BF16 = mybir.dt.bfloat16
ALU = mybir.AluOpType
AF = mybir.ActivationFunctionType
AX = mybir.AxisListType

ENGS = ["pe", "act", "dve", "pool", "sp"]
D = 2048
KT = 16
NOWN = 1024
NHALO = 2048
NTP = NOWN + NHALO
NTILE_P = NTP // 128
NO = NOWN + 128
SEQ = 4096
PAST = 16384
NCH = 1032
TWO_PI = 2.0 * math.pi


class Buf:
    def __init__(self, t, name):
        self.t = t
        self.name = name
        self.w = {}
        self.r = {}
        self.dsem = None
        self.dcnt = 0


class Prog:
    def __init__(self, nc, stack):
        self.nc = nc
        self.stack = stack
        self.q = {e: [] for e in ENGS}
        self.cnt = {e: 0 for e in ENGS}
        self.seen = {e: {} for e in ENGS}
        self.sems = {}
        self.semval = {}
        for e in ["pe", "act", "dve", "pool"]:
            self.sems[e] = stack.enter_context(nc.semaphore("s_" + e))
        self.off = 16512
        self.free = []
        self.dval = {}
        self.phase_bufs = []

    def sb(self, name, shape, dt, at=None):
        nbytes = int(np.prod(shape[1:])) * (2 if dt == BF16 else 4)
        if at is None:
            at = self.off
            self.off = (at + nbytes + 63) // 64 * 64
        assert at + nbytes <= 229300, (name, at, nbytes)
        t = self.nc.alloc_sbuf_tensor_at(name, list(shape), dt, offset=at)
        b = Buf(t, name)
        b.at = at
        b.nbytes = nbytes
        return b

    def ps(self, name, shape, dt=F32):
        t = self.stack.enter_context(self.nc.psum_tensor(name, list(shape), dt))
        return Buf(t, name)

    def dram(self, name, shape, dt, kind="Internal"):
        t = self.nc.dram_tensor(name, list(shape), dt, kind=kind)
        return Buf(t, name)

    def _need(self, eng, k, v, waits):
        if self.seen[eng].get(k, 0) >= v:
            return
        waits[k] = max(waits.get(k, 0), v)

    def _deps(self, eng, reads, writes):
        waits = {}
        for b in reads:
            for k, v in b.w.items():
                self._need(eng, k, v, waits)
        for b in writes:
            for k, v in b.w.items():
                self._need(eng, k, v, waits)
            for k, v in b.r.items():
                self._need(eng, k, v, waits)
        for k, v in waits.items():
            self.seen[eng][k] = v
        return [(self.sems[k], v) for k, v in waits.items()]

    def _commit(self, k, v, reads, writes):
        self.semval[k] = v
        for b in reads:
            b.r[k] = max(b.r.get(k, 0), v)
        for b in writes:
            b.w[k] = max(b.w.get(k, 0), v)
            b.r = {}

    def op(self, eng, fn, reads=(), writes=()):
        reads = [b for b in reads if b is not None]
        writes = [b for b in writes if b is not None]
        wl = self._deps(eng, reads, writes)
        self.cnt[eng] += 1
        sem = self.sems[eng]

        def emit(h, fn=fn, wl=wl, sem=sem):
            for s, v in wl:
                h.wait_ge(s, v)
            fn(h).then_inc(sem, 1)

        self.q[eng].append(emit)
        self._commit(eng, self.cnt[eng], reads, writes)

    def mm(self, fns, reads, writes):
        eng = "pe"
        wl = self._deps(eng, reads, writes)
        self.cnt[eng] += 1
        sem = self.sems[eng]

        def emit(h, fns=fns, wl=wl, sem=sem):
            for s, v in wl:
                h.wait_ge(s, v)
            for f in fns[:-1]:
                f(h)
            fns[-1](h).then_inc(sem, 1)

        self.q[eng].append(emit)
        self._commit(eng, self.cnt[eng], reads, writes)

    def dma(self, eng, out, in_, reads, writes, semb, **kw):
        reads = [b for b in reads if b is not None]
        writes = [b for b in writes if b is not None]
        if eng == "pool":
            if getattr(semb, "psem", None) is None:
                key = "q%d" % len(self.sems)
                self.sems[key] = self.stack.enter_context(self.nc.semaphore(key))
                semb.psem = key
                semb.pcnt = 0
            wl = self._deps(eng, reads, writes)
            semb.pcnt += 16
            sem = self.sems[semb.psem]

            def emit_p(h, wl=wl, sem=sem, out=out, in_=in_, kw=kw):
                for s, v in wl:
                    h.wait_ge(s, v)
                h.dma_start(out=out, in_=in_, **kw).then_inc(sem, 16)

            self.q[eng].append(emit_p)
            self._commit(semb.psem, semb.pcnt, reads, writes)
            return
        if semb.dsem is None:
            if self.free:
                key = self.free.pop()
            else:
                key = "d%d" % len(self.sems)
                self.sems[key] = self.stack.enter_context(self.nc.semaphore(key))
            semb.dsem = key
            semb.dcnt = self.dval.get(key, 0)
            self.phase_bufs.append(semb)
        wl = self._deps(eng, reads, writes)
        semb.dcnt += 16
        self.dval[semb.dsem] = semb.dcnt
        sem = self.sems[semb.dsem]

        def emit(h, wl=wl, sem=sem, out=out, in_=in_, kw=kw):
            for s, v in wl:
                h.wait_ge(s, v)
            h.dma_start(out=out, in_=in_, **kw).then_inc(sem, 16)

        self.q[eng].append(emit)
        self._commit(semb.dsem, semb.dcnt, reads, writes)

    def coll(self, src, dst, groups):
        key = "c%d" % len(self.sems)
        self.sems[key] = self.stack.enter_context(self.nc.semaphore(key))
        wl = self._deps("pool", [src], [dst])
        sem = self.sems[key]

        def emit(h, wl=wl, sem=sem):
            for s, v in wl:
                h.wait_ge(s, v)
            h.collective_compute("AllGather", ALU.bypass, replica_groups=groups,
                                 ins=[src.t.ap()], outs=[dst.t.ap()]).then_inc(sem)

        self.q["pool"].append(emit)
        self._commit(key, 1, [src], [dst])

    def barrier(self):
        for b in self.phase_bufs:
            self.free.append(b.dsem)
            b.dsem = None
        self.phase_bufs = []
        items = list(self.semval.items())
        for e in ENGS:
            wl = []
            for k, v in items:
                if self.seen[e].get(k, 0) < v:
                    self.seen[e][k] = v
                    wl.append((self.sems[k], v))

            def emit(h, wl=wl):
                for s, v in wl:
                    h.wait_ge(s, v)

            if wl:
                self.q[e].append(emit)

    def run(self):
        nc = self.nc
        with nc.Block() as block:
            @block.tensor
            def _(h):
                for f in self.q["pe"]:
                    f(h)

            @block.scalar
            def _(h):
                for f in self.q["act"]:
                    f(h)

            @block.vector
            def _(h):
                for f in self.q["dve"]:
                    f(h)

            @block.gpsimd
            def _(h):
                for f in self.q["pool"]:
                    f(h)

            @block.sync
            def _(h):
                for f in self.q["sp"]:
                    f(h)


def mult_of(d):
    d = np.asarray(d)
    m = ((d >= 0) & (d <= 128)).astype(np.float32)
    m += ((d >= 0) & (d <= 512) & (d % 4 == 0))
    m += ((d >= 0) & (d <= 2048) & (d % 16 == 0))
    return m.astype(np.float32)


IN_SPECS = [
    ("xh", [NTP, D]), ("xs", [128, D]), ("cs", [128, 25, 64]), ("sn", [128, 25, 64]),
    ("valid", [128, 24]), ("ck", [2048, 1024]), ("cv", [2048, 1024]), ("sconv", [128, 8, 2]),
    ("g_attn", [128, D]), ("g_ssm", [128, D]), ("g_fin", [128, D]),
    ("w_in_ab", [D, 8192]), ("cw", [128, 8, 3]), ("w_out_ab", [D, D]), ("w_in_c", [D, 4096]),
    ("w_glu", [D, D]), ("w_out_c", [D, D]), ("bglu", [128, 16]), ("dsk", [128, 4]),
    ("maskp", [128, 20, 512]), ("masks", [128, 17, 128]),
    ("lre_s", [128, 16]), ("lim_s", [128, 16]), ("lst_s", [128, 16]),
    ("lre_r", [128, 4, 64]), ("lim_r", [128, 4, 64]), ("lst_r", [128, 4, 64]),
    ("bre_r", [128, 4, 64]), ("bim_r", [128, 4, 64]),
    ("cre_s", [128, 16, 16]), ("cim_s", [128, 16, 16]),
    ("rmask", [128, 8]), ("smask", [128, 2]), ("sel", [128, 4]),
    ("sre0", [128, 4, 16]), ("sim0", [128, 4, 16]), ("iota", [128, 1024]),
]
OUT_SPECS = [
    ("yp", [NOWN, D]), ("ys", [128, D]), ("kp", [NOWN, 1024]), ("vp", [NOWN, 1024]),
    ("convp", [128, 8, 2]), ("ssm_p", [128, 2, 16]), ("ks", [128, 1024]), ("vs", [128, 1024]),
    ("convs", [128, 8, 2]), ("ssm_s", [128, 4, 2, 16]),
]


def build_nc():
    nc = bass.Bass("TRN2", target_bir_lowering=False)
    IN = {}
    for n, s in IN_SPECS:
        IN[n] = nc.dram_tensor(n, s, F32, kind="ExternalInput")
    OUT = {}
    for n, s in OUT_SPECS:
        OUT[n] = nc.dram_tensor(n, s, F32, kind="ExternalOutput")
    st = ExitStack()
    with st:
        P = Prog(nc, st)
        build_program(nc, P, IN, OUT)
        P.run()
    return nc


def build_program(nc, P, IN, OUT):
    GROUPS = [[0, 1, 2, 3], [4, 5, 6, 7]]
    outbufs = {n: Buf(OUT[n], n) for n in OUT}
    kT_scr = P.dram("kT_scr", [8, 128, NTP], BF16)
    v_scr = P.dram("v_scr", [NTP, 1024], BF16)
    kTs_scr = P.dram("kTs_scr", [8, 128, 2176], BF16)
    vs_scr = P.dram("vs_scr", [2176, 1024], BF16)
    qT_scr = P.dram("qT_scr", [8, 128, NO], BF16)
    h1_scr = P.dram("h1_scr", [NO, D], F32)
    hn_src = [P.dram("hn_src%d" % j, [256, NCH], BF16) for j in range(8)]
    hn_dst = [P.dram("hn_dst%d" % j, [4 * 256, NCH], BF16) for j in range(8)]
    y_src = [P.dram("y_src%d" % j, [512, 516], BF16) for j in range(8)]
    y_dst = [P.dram("y_dst%d" % j, [4 * 512, 516], BF16) for j in range(8)]

    pf = [P.ps("pf%d" % i, [128, 512], F32) for i in range(6)]
    pb = [P.ps("pb%d" % i, [128, 8, 128], BF16) for i in range(2)]
    pfi = [0]
    pbi = [0]

    def next_pf():
        pfi[0] = (pfi[0] + 1) % 4
        return pf[pfi[0]]

    def next_pb():
        pbi[0] = (pbi[0] + 1) % 2
        return pb[pbi[0]]

    ident = P.sb("ident", [128, 128], BF16)
    P.op("pool", lambda h: h.memset(ident.t[:], 1.0), [], [ident])
    P.op("pool", lambda h: h.affine_select(ident.t[:], ident.t[:], [[-1, 128]], ALU.is_equal, 0.0,
                                            base=0, channel_multiplier=1), [ident], [ident])
    ones_bf = P.sb("ones_bf", [128, 128], BF16)
    P.op("pool", lambda h: h.memset(ones_bf.t[:], 1.0), [], [ones_bf])
    gt = P.sb("gt", [128, D], F32)
    cs = P.sb("cs", [128, 25, 64], F32)
    sn = P.sb("sn", [128, 25, 64], F32)
    valid = P.sb("valid", [128, 24], F32)
    validB = P.sb("validB", [128, 24, 128], BF16)
    cw = P.sb("cw", [128, 8, 3], F32)
    bglu = P.sb("bglu", [128, 16], F32)
    dsk = P.sb("dsk", [128, 4], F32)
    sel = P.sb("sel", [128, 4], F32)
    eps_t = P.sb("eps_t", [128, 1], F32)
    P.op("pool", lambda h: h.memset(eps_t.t[:], 1e-6), [], [eps_t])
    for b_, n in [(cs, "cs"), (sn, "sn"), (valid, "valid"), (cw, "cw"), (bglu, "bglu"), (dsk, "dsk"), (sel, "sel")]:
        P.dma("sp", b_.t[:], IN[n].ap(), [], [b_], b_)
    P.op("dve", lambda h: h.tensor_copy(validB.t[:], valid.t[:].unsqueeze(2).to_broadcast([128, 24, 128])),
         [valid], [validB])
    pospi = P.sb("pospi", [128, 1], F32)
    P.op("pool", lambda h: h.memset(pospi.t[:], math.pi), [], [pospi])
    CONST_END = P.off
    hnT_o = P.sb("hnT_o", [128, KT, NO], BF16)

    def load_g(name):
        P.dma("sp", gt.t[:], IN[name].ap(), [], [gt], gt)

    def rmsnorm_rows(xt, xn, ss, junk):
        P.op("act", lambda h: h.activation(junk.t[:], xt.t[:], AF.Square, accum_out=ss.t[:, 0:1]), [xt], [junk, ss])
        P.op("act", lambda h: h.activation(ss.t[:, 1:2], ss.t[:, 0:1], AF.Sqrt, bias=eps_t.t[:, 0:1], scale=1.0 / D), [ss, eps_t], [ss])
        P.op("dve", lambda h: h.reciprocal(ss.t[:, 2:3], ss.t[:, 1:2]), [ss], [ss])
        P.op("dve", lambda h: h.scalar_tensor_tensor(xn.t[:], xt.t[:], ss.t[:, 2:3], gt.t[:], ALU.mult, ALU.mult),
             [xt, ss, gt], [xn])

    def transpose_rows(xn, dst, dst_ap_fn):
        for half in range(2):
            p = next_pb()
            fns = []
            for j in range(8):
                kt = half * 8 + j
                fns.append(lambda h, p=p, j=j, kt=kt: h.transpose(p.t[:, j, :], xn.t[:, kt * 128:(kt + 1) * 128], ident.t[:]))
            P.mm(fns, [xn, ident], [p])
            P.op("act", lambda h, p=p, half=half: h.activation(dst_ap_fn(half), p.t[:], AF.Identity), [p], [dst])

    A0 = P.off
    wkv = P.sb("wkv", [128, KT, 2048], BF16)
    xts = [P.sb("xt%d" % i, [128, D], F32) for i in range(3)]
    xns = [P.sb("xn%d" % i, [128, D], BF16) for i in range(3)]
    hts = [P.sb("ht%d" % i, [128, KT, 128], BF16) for i in range(3)]
    ss = P.sb("ss", [128, 4], F32)
    krs = [P.sb("kr%d" % i, [128, 1024], F32) for i in range(2)]
    vfs = [P.sb("vf%d" % i, [128, 1024], F32) for i in range(2)]
    t1 = P.sb("t1", [128, 256], F32)
    t2 = P.sb("t2", [128, 256], F32)
    krbs = [P.sb("krb%d" % i, [128, 1024], BF16) for i in range(2)]
    vbs = [P.sb("vb%d" % i, [128, 1024], BF16) for i in range(2)]
    kTts = [P.sb("kTt%d" % i, [128, 8, 128], BF16) for i in range(2)]
    kTt = kTts[0]
    hprev2 = P.sb("hprev2", [128, KT, 2], BF16)
    A1_END = P.off

    load_g("g_attn")
    for half in range(2):
        P.dma("pool", wkv.t[:, :, half * 1024:(half + 1) * 1024],
              IN["w_in_ab"].ap()[:, 1024 + half * 1024:2048 + half * 1024].rearrange("(kt p) c -> p kt c", p=128),
              [], [wkv], wkv)

    def rotary(pk, ti, dst, c0):
        v = pk.t[:].rearrange("p (h two d) -> p h two d", h=4, two=2)
        o = dst.t[:, c0:c0 + 512].rearrange("p (h two d) -> p h two d", h=4, two=2)
        cb = cs.t[:, ti, :].unsqueeze(1).to_broadcast([128, 4, 64])
        sb_ = sn.t[:, ti, :].unsqueeze(1).to_broadcast([128, 4, 64])
        a = t1.t[:].rearrange("p (h d) -> p h d", h=4)
        b = t2.t[:].rearrange("p (h d) -> p h d", h=4)
        P.op("dve", lambda h: h.tensor_tensor(a, v[:, :, 0, :], cb, ALU.mult), [pk, cs], [t1])
        P.op("dve", lambda h: h.tensor_tensor(b, v[:, :, 1, :], sb_, ALU.mult), [pk, sn], [t2])
        P.op("dve", lambda h: h.tensor_tensor(o[:, :, 0, :], a, b, ALU.subtract), [t1, t2], [dst])
        P.op("dve", lambda h: h.tensor_tensor(a, v[:, :, 1, :], cb, ALU.mult), [pk, cs], [t1])
        P.op("dve", lambda h: h.tensor_tensor(b, v[:, :, 0, :], sb_, ALU.mult), [pk, sn], [t2])
        P.op("dve", lambda h: h.tensor_tensor(o[:, :, 1, :], a, b, ALU.add), [t1, t2], [dst])

    def store_kT(src_bf, scr, col0, kb=None):
        p = next_pb()
        if kb is None:
            kb = kTt
        fns = [(lambda h, p=p, j=j: h.transpose(p.t[:, j, :], src_bf.t[:, j * 128:(j + 1) * 128], ident.t[:])) for j in range(8)]
        P.mm(fns, [src_bf, ident], [p])
        P.op("act", lambda h, p=p, kb=kb: h.activation(kb.t[:], p.t[:], AF.Identity), [p], [kb])
        P.dma("sp", scr.t.ap()[:, :, col0:col0 + 128].rearrange("h d t -> d h t"), kb.t[:], [kb], [scr], kb)

    def stageL(ti):
        xt = xts[ti % 3]
        src = IN["xh"].ap()[ti * 128:(ti + 1) * 128, :] if ti < 24 else IN["xs"].ap()
        P.dma("sp", xt.t[:], src, [], [xt], xt)

    def stageN(ti):
        rmsnorm_rows(xts[ti % 3], xns[ti % 3], ss, xns[ti % 3])

    def stageA2(ti):
        xn = xns[ti % 3]
        if ti < 16:
            ht = hts[ti % 3]
            transpose_rows(xn, ht, lambda half, ht=ht: ht.t[:, half * 8:(half + 1) * 8, :])
            if ti == 15:
                P.op("dve", lambda h, ht=ht: h.tensor_copy(hprev2.t[:], ht.t[:, :, 126:128]), [ht], [hprev2])
            return (lambda kt, ht=ht: ht.t[:, kt, :]), ht
        o0 = (ti - 16) * 128
        transpose_rows(xn, hnT_o, lambda half, o0=o0: hnT_o.t[:, half * 8:(half + 1) * 8, o0:o0 + 128])
        return (lambda kt, o0=o0: hnT_o.t[:, kt, o0:o0 + 128]), hnT_o

    def stageM(ti, lhs, hb):
        kr = krs[ti % 2]
        vf = vfs[ti % 2]
        for g4 in range(4):
            pk = next_pf()
            fns = [(lambda h, pk=pk, kt=kt, g4=g4, lhs=lhs: h.matmul(pk.t[:], lhs(kt), wkv.t[:, kt, g4 * 512:(g4 + 1) * 512],
                                                                    start=(kt == 0), stop=(kt == KT - 1))) for kt in range(KT)]
            P.mm(fns, [hb, wkv], [pk])
            if g4 < 2:
                rotary(pk, ti, kr, g4 * 512)
            else:
                c0 = (g4 - 2) * 512
                P.op("act", lambda h, pk=pk, c0=c0, vf=vf: h.activation(vf.t[:, c0:c0 + 512], pk.t[:], AF.Identity), [pk], [vf])

    def stageKpre(ti):
        kr = krs[ti % 2]
        vf = vfs[ti % 2]
        krb = krbs[ti % 2]
        vb = vbs[ti % 2]
        P.op("act", lambda h: h.activation(krb.t[:], kr.t[:], AF.Identity), [kr], [krb])
        if ti < 24:
            P.op("dve", lambda h: h.tensor_scalar(vb.t[:], vf.t[:], valid.t[:, ti:ti + 1], None, ALU.mult), [vf, valid], [vb])
        else:
            P.op("dve", lambda h: h.tensor_copy(vb.t[:], vf.t[:]), [vf], [vb])

    def stageKpost(ti):
        kr = krs[ti % 2]
        vf = vfs[ti % 2]
        krb = krbs[ti % 2]
        vb = vbs[ti % 2]
        kb = kTts[ti % 2]
        if ti < 24:
            store_kT(krb, kT_scr, ti * 128, kb)
            P.dma("sp", v_scr.t.ap()[ti * 128:(ti + 1) * 128, :], vb.t[:], [vb], [v_scr], vb)
            if ti >= 16:
                r0 = (ti - 16) * 128
                P.dma("sp", OUT["kp"].ap()[r0:r0 + 128, :], kr.t[:], [kr], [outbufs["kp"]], kr)
                P.dma("sp", OUT["vp"].ap()[r0:r0 + 128, :], vf.t[:], [vf], [outbufs["vp"]], vf)
        else:
            store_kT(krb, kTs_scr, 2048, kb)
            P.dma("sp", vs_scr.t.ap()[2048:2176, :], vb.t[:], [vb], [vs_scr], vb)
            P.dma("sp", OUT["ks"].ap(), kr.t[:], [kr], [outbufs["ks"]], kr)
            P.dma("sp", OUT["vs"].ap(), vf.t[:], [vf], [outbufs["vs"]], vf)

    for t_ in range(3):
        stageL(t_)
    stageN(0)
    stageN(1)
    infoA = {0: stageA2(0)}
    for ti in range(25):
        if ti + 3 < 25:
            stageL(ti + 3)
        if ti + 2 < 25:
            stageN(ti + 2)
        if ti >= 1:
            stageKpre(ti - 1)
        if ti + 1 < 25:
            infoA[ti + 1] = stageA2(ti + 1)
        stageM(ti, *infoA[ti])
        if ti >= 1:
            stageKpost(ti - 1)
    stageKpre(24)
    stageKpost(24)
    def cacheL(ti):
        xt = xts[ti % 3]
        P.dma("sp", xt.t[:, 0:1024], IN["ck"].ap()[ti * 128:(ti + 1) * 128, :], [], [xt], xt)
        P.dma("sp", xt.t[:, 1024:2048], IN["cv"].ap()[ti * 128:(ti + 1) * 128, :], [], [xt], xt)

    cacheL(0)
    cacheL(1)
    for ti in range(16):
        if ti + 2 < 16:
            cacheL(ti + 2)
        xt = xts[ti % 3]
        krb = krbs[ti % 2]
        vb = vbs[ti % 2]
        P.op("act", lambda h, xt=xt, krb=krb: h.activation(krb.t[:], xt.t[:, 0:1024], AF.Identity), [xt], [krb])
        P.op("dve", lambda h, xt=xt, vb=vb: h.tensor_copy(vb.t[:], xt.t[:, 1024:2048]), [xt], [vb])
        store_kT(krb, kTs_scr, ti * 128, kTts[ti % 2])
        P.dma("sp", vs_scr.t.ap()[ti * 128:(ti + 1) * 128, :], vb.t[:], [vb], [vs_scr], vb)

    if STOP == 'A1':
        P.barrier()
        return
    P.barrier()
    P.off = A0
    wq = P.sb("wq", [128, KT, 1024], BF16)
    hprev2b = P.sb("hprev2b", [128, KT, 2], BF16)
    qf = P.sb("qf", [128, 1024], F32)
    qb = P.sb("qb", [128, 1024], BF16)
    t1 = P.sb("t1b", [128, 256], F32)
    t2 = P.sb("t2b", [128, 256], F32)
    kTt = P.sb("kTtb", [128, 8, 128], BF16)
    hprev2k = P.sb("hprev2k", [128, KT, 2], BF16, at=hprev2.at)
    hprev2k.w = dict(hprev2.w)
    P.dma("pool", wq.t[:], IN["w_in_ab"].ap()[:, 0:1024].rearrange("(kt p) c -> p kt c", p=128), [], [wq], wq)
    for tj in range(9):
        ti = 16 + tj
        o0 = tj * 128
        for g2_ in range(2):
            pk = next_pf()
            fns = [(lambda h, pk=pk, kt=kt, g2_=g2_, o0=o0: h.matmul(pk.t[:], hnT_o.t[:, kt, o0:o0 + 128],
                                                                      wq.t[:, kt, g2_ * 512:(g2_ + 1) * 512],
                                                                      start=(kt == 0), stop=(kt == KT - 1))) for kt in range(KT)]
            P.mm(fns, [hnT_o, wq], [pk])
            rotary(pk, ti, qf, g2_ * 512)
        P.op("act", lambda h: h.activation(qb.t[:], qf.t[:], AF.Identity), [qf], [qb])
        store_kT(qb, qT_scr, o0)

    if STOP == 'A2':
        P.barrier()
        return
    P.barrier()
    P.off = A0
    ocat = P.sb("ocat", [128, KT, NO], BF16)
    hp2 = P.sb("hp2", [128, KT, 2], BF16)
    B0 = P.off
    P.op("dve", lambda h: h.tensor_copy(hp2.t[:], hprev2k.t[:]), [hprev2k], [hp2])
    P.barrier()
    maskp = P.sb("maskp", [128, 20, 512], BF16)
    masks_ = P.sb("masks_", [128, 17, 128], BF16)
    P.dma("pool", maskp.t[:], IN["maskp"].ap(), [], [maskp], maskp)
    P.dma("pool", masks_.t[:], IN["masks"].ap(), [], [masks_], masks_)
    kTh = [P.sb("kTh%d" % i, [128, NTP], BF16) for i in range(1)] * 2
    vh = [P.sb("vh%d" % i, [128, 24, 128], BF16) for i in range(1)] * 2
    kTsh = [P.sb("kTsh%d" % i, [128, 2176], BF16) for i in range(1)] * 2
    vsh = [P.sb("vsh%d" % i, [128, 17, 128], BF16) for i in range(1)] * 2
    qTh = [P.sb("qTh%d" % i, [128, NO], BF16) for i in range(1)] * 2
    wt = [P.sb("wt%d" % i, [128, KT, 128], BF16) for i in range(4)]
    pts = [P.sb("pt%d" % i, [128, 512], BF16) for i in range(4)]
    ptm = [P.sb("ptm%d" % i, [128, 512], BF16) for i in range(4)]
    za = P.sb("za", [128, NO], F32)
    rl = P.sb("rl", [128, 512], F32)
    of = P.sb("of", [128, 512], F32)
    sg = P.sb("sg", [128, 512], F32)

    def silu_evac(pk, dstb, dst_ap, n):
        P.op("act", lambda h: h.activation(sg.t[:, 0:n], pk.t[:, 0:n], AF.Exp, scale=-1.0), [pk], [sg])
        P.op("dve", lambda h: h.tensor_scalar(sg.t[:, 0:n], sg.t[:, 0:n], 1.0, None, ALU.add), [sg], [sg])
        P.op("dve", lambda h: h.reciprocal(sg.t[:, 0:n], sg.t[:, 0:n]), [sg], [sg])
        P.op("dve", lambda h: h.tensor_tensor(dst_ap, pk.t[:, 0:n], sg.t[:, 0:n], ALU.mult), [pk, sg], [dstb])
    fb = [P.sb("fb%d" % i, [128, NO + 2], F32) for i in range(4)]
    convo_p = P.sb("convo_p", [128, 8, 2], F32)
    convo_s = P.sb("convo_s", [128, 8, 2], F32)
    sconv = P.sb("sconv", [128, 8, 2], F32)
    P.dma("sp", sconv.t[:], IN["sconv"].ap(), [], [sconv], sconv)
    scale = 128.0 ** -0.5

    def load_wt(i, c0):
        P.dma("pool", wt[i].t[:], IN["w_in_ab"].ap()[:, c0:c0 + 128].rearrange("(kt p) c -> p kt c", p=128), [], [wt[i]], wt[i])

    def proj_feat(wb, dst_ap_fn, evac, rhs_buf, rhs_fn, n):
        pk = next_pf()
        fns = [(lambda h, pk=pk, kt=kt: h.matmul(pk.t[:, 0:n], wb.t[:, kt, :], rhs_fn(kt), start=(kt == 0), stop=(kt == KT - 1)))
               for kt in range(KT)]
        P.mm(fns, [wb, rhs_buf], [pk])
        evac(pk)

    def attention(hh, qT, q0, nq, kT, vt, ktiles, mask_fn, vB_fn, o_dst_fn, zcol0):
        po = pf[4]
        pl = pf[5]
        nk = len(ktiles)
        LA = 3
        pms = {}

        def issue_S(i):
            kt_ = ktiles[i]
            ps_ = next_pf()
            P.mm([lambda h, ps_=ps_, kt_=kt_: h.matmul(ps_.t[:, 0:nq], kT.t[:, kt_ * 128:(kt_ + 1) * 128], qT.t[:, q0:q0 + nq],
                                                        start=True, stop=True)], [kT, qT], [ps_])
            pe_ = pts[i % 4]
            pm_ = ptm[i % 4]
            P.op("act", lambda h, ps_=ps_, pe_=pe_: h.activation(pe_.t[:, 0:nq], ps_.t[:, 0:nq], AF.Exp, scale=scale), [ps_], [pe_])
            mk, mb = mask_fn(i)
            eng = "dve"
            P.op(eng, lambda h, pe_=pe_, pm_=pm_, mk=mk: h.tensor_tensor(pm_.t[:, 0:nq], pe_.t[:, 0:nq], mk, ALU.mult), [pe_, mb], [pm_])
            pms[i] = pm_

        def issue_PV(i):
            kt_ = ktiles[i]
            pm_ = pms[i]
            vB, vBb = vB_fn(i)
            P.mm([lambda h, pm_=pm_, kt_=kt_, i=i: h.matmul(po.t[:, 0:nq], vt.t[:, kt_, :], pm_.t[:, 0:nq], start=(i == 0), stop=(i == nk - 1)),
                  lambda h, pm_=pm_, vB=vB, i=i: h.matmul(pl.t[:, 0:nq], vB, pm_.t[:, 0:nq], start=(i == 0), stop=(i == nk - 1))],
                 [vt, pm_, vBb], [po, pl])

        for i in range(min(LA, nk)):
            issue_S(i)
        for i in range(nk):
            if i + LA < nk:
                issue_S(i + LA)
            issue_PV(i)
        P.op("dve", lambda h: h.reciprocal(rl.t[:, 0:nq], pl.t[:, 0:nq]), [pl], [rl])
        P.op("dve", lambda h: h.tensor_tensor(of.t[:, 0:nq], po.t[:, 0:nq], rl.t[:, 0:nq], ALU.mult), [po, rl], [of])
        P.op("dve", lambda h: h.tensor_tensor(o_dst_fn(), of.t[:, 0:nq], za.t[:, zcol0:zcol0 + nq], ALU.mult), [of, za], [ocat])

    for hh in range(8):
        b2 = hh % 2
        P.dma("sp", kTh[b2].t[:], kT_scr.t.ap()[hh], [kT_scr], [kTh[b2]], kTh[b2])
        P.dma("sp", vh[b2].t[:], v_scr.t.ap()[:, hh * 128:(hh + 1) * 128].rearrange("(t p) d -> p t d", p=128), [v_scr], [vh[b2]], vh[b2])
        P.dma("sp", kTsh[b2].t[:], kTs_scr.t.ap()[hh], [kTs_scr], [kTsh[b2]], kTsh[b2])
        P.dma("sp", vsh[b2].t[:], vs_scr.t.ap()[:, hh * 128:(hh + 1) * 128].rearrange("(t p) d -> p t d", p=128), [vs_scr], [vsh[b2]], vsh[b2])
        P.dma("sp", qTh[b2].t[:], qT_scr.t.ap()[hh], [qT_scr], [qTh[b2]], qTh[b2])
        load_wt(0, 3072 + hh * 128)
        for (c0, n) in [(0, 512), (512, 512), (1024, 128)]:
            proj_feat(wt[0], None, lambda pk, c0=c0, n=n: silu_evac(pk, za, za.t[:, c0:c0 + n], n),
                      hnT_o, lambda kt, c0=c0, n=n: hnT_o.t[:, kt, c0:c0 + n], n)
        for qc in range(2):
            kts = list(range(4 * qc, 4 * qc + 20))
            attention(hh, qTh[b2], qc * 512, 512, kTh[b2], vh[b2], kts,
                      lambda i: (maskp.t[:, i, :], maskp),
                      lambda i, kts=kts: (validB.t[:, kts[i], :], validB),
                      lambda qc=qc, hh=hh: ocat.t[:, hh, qc * 512:(qc + 1) * 512], qc * 512)
        attention(hh, qTh[b2], 1024, 128, kTsh[b2], vsh[b2], list(range(17)),
                  lambda i: (masks_.t[:, i, :], masks_),
                  lambda i: (ones_bf.t[:], ones_bf),
                  lambda hh=hh: ocat.t[:, hh, 1024:1152], 1024)

    for cc in range(8):
        for j, base in enumerate([4096, 5120, 6144, 7168]):
            load_wt(j, base + cc * 128)
        bb, cb_, hb_, zb = fb
        for j, dstb in enumerate(fb):
            for (c0, n) in [(0, 512), (512, 512), (1024, 128)]:
                if j == 3:
                    ev = lambda pk, c0=c0, n=n, dstb=dstb: silu_evac(pk, dstb, dstb.t[:, 2 + c0:2 + c0 + n], n)
                else:
                    ev = lambda pk, c0=c0, n=n, dstb=dstb: P.op("act", lambda h: h.activation(dstb.t[:, 2 + c0:2 + c0 + n], pk.t[:, 0:n], AF.Identity), [pk], [dstb])
                proj_feat(wt[j], None, ev, hnT_o, lambda kt, c0=c0, n=n: hnT_o.t[:, kt, c0:c0 + n], n)
            if j in (1, 2):
                proj_feat(wt[j], None, lambda pk, dstb=dstb: P.op("act", lambda h: h.activation(dstb.t[:, 0:2], pk.t[:, 0:2], AF.Identity), [pk], [dstb]),
                          hp2, lambda kt: hp2.t[:, kt, :], 2)
        P.op("dve", lambda h: h.tensor_tensor(cb_.t[:], cb_.t[:], hb_.t[:], ALU.mult), [cb_, hb_], [cb_])
        w0 = cw.t[:, cc, 0:1]
        w1 = cw.t[:, cc, 1:2]
        w2 = cw.t[:, cc, 2:3]
        P.op("dve", lambda h, w2=w2: h.tensor_scalar(hb_.t[:, 2:1026], cb_.t[:, 2:1026], w2, None, ALU.mult), [cb_, cw], [hb_])
        P.op("dve", lambda h, w1=w1: h.scalar_tensor_tensor(hb_.t[:, 2:1026], cb_.t[:, 1:1025], w1, hb_.t[:, 2:1026], ALU.mult, ALU.add), [cb_, cw, hb_], [hb_])
        P.op("dve", lambda h, w0=w0: h.scalar_tensor_tensor(hb_.t[:, 2:1026], cb_.t[:, 0:1024], w0, hb_.t[:, 2:1026], ALU.mult, ALU.add), [cb_, cw, hb_], [hb_])
        P.op("dve", lambda h, cc=cc: h.tensor_copy(convo_p.t[:, cc, :], cb_.t[:, 1024:1026]), [cb_], [convo_p])
        P.op("dve", lambda h, cc=cc: h.tensor_copy(cb_.t[:, 1024:1026], sconv.t[:, cc, :]), [sconv], [cb_])
        P.op("dve", lambda h, w2=w2: h.tensor_scalar(hb_.t[:, 1026:1034], cb_.t[:, 1026:1034], w2, None, ALU.mult), [cb_, cw], [hb_])
        P.op("dve", lambda h, w1=w1: h.scalar_tensor_tensor(hb_.t[:, 1026:1034], cb_.t[:, 1025:1033], w1, hb_.t[:, 1026:1034], ALU.mult, ALU.add), [cb_, cw, hb_], [hb_])
        P.op("dve", lambda h, w0=w0: h.scalar_tensor_tensor(hb_.t[:, 1026:1034], cb_.t[:, 1024:1032], w0, hb_.t[:, 1026:1034], ALU.mult, ALU.add), [cb_, cw, hb_], [hb_])
        P.op("dve", lambda h, cc=cc: h.tensor_copy(convo_s.t[:, cc, :], cb_.t[:, 1032:1034]), [cb_], [convo_s])
        P.op("dve", lambda h: h.tensor_tensor(hb_.t[:, 2:1034], hb_.t[:, 2:1034], bb.t[:, 2:1034], ALU.mult), [hb_, bb], [hb_])
        P.op("dve", lambda h, cc=cc: h.tensor_tensor(ocat.t[:, 8 + cc, 0:1032], hb_.t[:, 2:1034], zb.t[:, 2:1034], ALU.mult), [hb_, zb], [ocat])
        P.op("dve", lambda h, cc=cc: h.memset(ocat.t[:, 8 + cc, 1032:1152], 0.0), [], [ocat])
    P.dma("sp", OUT["convp"].ap(), convo_p.t[:], [convo_p], [outbufs["convp"]], convo_p)
    P.dma("sp", OUT["convs"].ap(), convo_s.t[:], [convo_s], [outbufs["convs"]], convo_s)

    if STOP == 'B':
        P.barrier()
        return
    P.barrier()
    hn1T = P.sb("hn1T", [128, KT, NCH], BF16, at=hnT_o.at)
    P.off = B0
    wgb = [P.sb("wg%d" % i, [128, KT, 512], BF16) for i in range(2)]
    h1s = [P.sb("h1s%d" % i, [128, D], F32) for i in range(5)]
    xn1 = [P.sb("xn1%d" % i, [128, D], BF16) for i in range(2)]
    junk = P.sb("junk2", [128, D], F32)
    ss = P.sb("ss2", [128, 4], F32)
    tmpT = P.sb("tmpT", [128, KT, 128], BF16)
    load_g("g_ssm")
    wi = 0
    for tiles in [list(range(0, 5)), list(range(5, 9))]:
        for si, tj in enumerate(tiles):
            o0 = tj * 128
            src = IN["xh"].ap()[NHALO + o0:NHALO + o0 + 128, :] if tj < 8 else IN["xs"].ap()
            P.dma("sp", h1s[si].t[:], src, [], [h1s[si]], h1s[si])
        for g4 in range(4):
            wg = wgb[wi % 2]
            wi += 1
            P.dma("pool", wg.t[:], IN["w_out_ab"].ap()[:, g4 * 512:(g4 + 1) * 512].rearrange("(kt p) c -> p kt c", p=128), [], [wg], wg)
            for si, tj in enumerate(tiles):
                o0 = tj * 128
                ht_ = h1s[si]
                pk = next_pf()
                fns = [(lambda h, pk=pk, kt=kt, o0=o0, wg=wg: h.matmul(pk.t[:], ocat.t[:, kt, o0:o0 + 128], wg.t[:, kt, :],
                                                                         start=(kt == 0), stop=(kt == KT - 1))) for kt in range(KT)]
                P.mm(fns, [ocat, wg], [pk])
                P.op("dve", lambda h, pk=pk, ht_=ht_, g4=g4: h.tensor_tensor(ht_.t[:, g4 * 512:(g4 + 1) * 512], ht_.t[:, g4 * 512:(g4 + 1) * 512], pk.t[:], ALU.add),
                     [pk, ht_], [ht_])
        for si, tj in enumerate(tiles):
            o0 = tj * 128
            ht_ = h1s[si]
            P.dma("sp", h1_scr.t.ap()[o0:o0 + 128, :], ht_.t[:], [ht_], [h1_scr], ht_)
            if STOP == 'C1':
                if tj < 8:
                    P.dma("sp", OUT["yp"].ap()[o0:o0 + 128, :], ht_.t[:], [ht_], [outbufs["yp"]], ht_)
                else:
                    P.dma("sp", OUT["ys"].ap(), ht_.t[:], [ht_], [outbufs["ys"]], ht_)
            xn = xn1[tj % 2]
            rmsnorm_rows(ht_, xn, ss, junk)
            if tj < 8:
                transpose_rows(xn, hn1T, lambda half, o0=o0: hn1T.t[:, half * 8:(half + 1) * 8, o0:o0 + 128])
            else:
                transpose_rows(xn, tmpT, lambda half: tmpT.t[:, half * 8:(half + 1) * 8, :])
                P.op("dve", lambda h: h.tensor_copy(hn1T.t[:, :, 1024:1032], tmpT.t[:, :, 0:8]), [tmpT], [hn1T])
    for j in range(8):
        P.dma("sp", hn_src[j].t.ap().rearrange("(k p) t -> p k t", p=128), hn1T.t[:, 2 * j:2 * j + 2, :], [hn1T], [hn_src[j]], hn1T)
        P.coll(hn_src[j], hn_dst[j], GROUPS)

    if STOP == 'C1':
        P.barrier()
        return
    P.barrier()
    P.off = CONST_END
    uT = P.sb("uT", [128, 4, 4 * NCH], BF16)
    ygT = P.sb("ygT", [128, 4, 4 * NCH], BF16)
    L1 = P.off
    wu = P.sb("wu", [128, KT, 512], BF16)
    hch = [P.sb("hch%d" % i, [128, KT, 516], BF16) for i in range(2)]
    P.dma("pool", wu.t[:], IN["w_in_c"].ap()[:, 0:512].rearrange("(kt p) c -> p kt c", p=128), [], [wu], wu)
    ci = 0
    for r in range(4):
        for hf in range(2):
            hc = hch[ci % 2]
            ci += 1
            c0 = hf * 516
            for j in range(8):
                P.dma("sp", hc.t[:, 2 * j:2 * j + 2, :], hn_dst[j].t.ap()[r * 256:(r + 1) * 256, c0:c0 + 516].rearrange("(k p) t -> p k t", p=128),
                      [hn_dst[j]], [hc], hc)
            for ft in range(4):
                pk = next_pf()
                fns = [(lambda h, pk=pk, kt=kt, ft=ft, hc=hc: h.matmul(pk.t[:, 0:512], wu.t[:, kt, ft * 128:(ft + 1) * 128], hc.t[:, kt, 0:512],
                                                                      start=(kt == 0), stop=(kt == KT - 1))) for kt in range(KT)]
                P.mm(fns, [wu, hc], [pk])
                P.op("act", lambda h, pk=pk, ft=ft, r=r, c0=c0: h.activation(uT.t[:, ft, r * NCH + c0:r * NCH + c0 + 512], pk.t[:, 0:512], AF.Identity), [pk], [uT])
                pk2 = next_pf()
                fns2 = [(lambda h, pk2=pk2, kt=kt, ft=ft, hc=hc: h.matmul(pk2.t[:, 0:4], wu.t[:, kt, ft * 128:(ft + 1) * 128], hc.t[:, kt, 512:516],
                                                                         start=(kt == 0), stop=(kt == KT - 1))) for kt in range(KT)]
                P.mm(fns2, [wu, hc], [pk2])
                P.op("act", lambda h, pk2=pk2, ft=ft, r=r, c0=c0: h.activation(uT.t[:, ft, r * NCH + c0 + 512:r * NCH + c0 + 516], pk2.t[:, 0:4], AF.Identity), [pk2], [uT])

    if STOP == 'U':
        P.barrier()
        return
    P.barrier()
    P.off = L1
    def small(name, shape, dt=F32):
        return P.sb(name, shape, dt)
    lre_s = small("lre_s", [128, 16]); lim_s = small("lim_s", [128, 16]); lst_s = small("lst_s", [128, 16])
    lre_r = small("lre_r", [128, 256]); lim_r = small("lim_r", [128, 256]); lst_r = small("lst_r", [128, 256])
    bre_r = small("bre_r", [128, 256]); bim_r = small("bim_r", [128, 256])
    cre_s = small("cre_s", [128, 16, 16]); cim_s = small("cim_s", [128, 16, 16])
    rmask = small("rmask", [128, 8]); smask = small("smask", [128, 2])
    sre0 = small("sre0", [128, 4, 16]); sim0 = small("sim0", [128, 4, 16])
    iota = small("iota", [128, 1024])
    for b_, n in [(lre_s, "lre_s"), (lim_s, "lim_s"), (lst_s, "lst_s"), (cre_s, "cre_s"), (cim_s, "cim_s"),
                  (rmask, "rmask"), (smask, "smask"), (sre0, "sre0"), (sim0, "sim0"), (iota, "iota")]:
        P.dma("sp", b_.t[:], IN[n].ap(), [], [b_], b_)
    for b_, n in [(lre_r, "lre_r"), (lim_r, "lim_r"), (lst_r, "lst_r"), (bre_r, "bre_r"), (bim_r, "bim_r")]:
        P.dma("sp", b_.t[:], IN[n].ap().rearrange("p a b -> p (a b)"), [], [b_], b_)
    negpi = small("negpi", [128, 1])
    P.op("dve", lambda h: h.memset(negpi.t[:], -math.pi), [], [negpi])

    I32 = mybir.dt.int32
    tq = small("tq", [128, 1024]); tiq = small("tiq", [128, 1024], I32)
    halfpi = small("halfpi", [128, 1]); zero_t = small("zero_t", [128, 1])
    P.op("dve", lambda h: h.memset(halfpi.t[:], 0.5 * math.pi), [], [halfpi])
    P.op("dve", lambda h: h.memset(zero_t.t[:], 0.0), [], [zero_t])

    def sincos(ang_ap, n, s_ap, c_ap, rd, wr):
        tf = tiq.t[:, 0:n].bitcast(F32)
        P.op("dve", lambda h: h.tensor_scalar(tq.t[:, 0:n], ang_ap, 1.0 / TWO_PI, None, ALU.mult), rd, [tq])
        P.op("dve", lambda h: h.tensor_copy(tiq.t[:, 0:n], tq.t[:, 0:n]), [tq], [tiq])
        P.op("dve", lambda h: h.tensor_copy(tq.t[:, 0:n], tiq.t[:, 0:n]), [tiq], [tq])
        P.op("dve", lambda h: h.scalar_tensor_tensor(tq.t[:, 0:n], tq.t[:, 0:n], -TWO_PI, ang_ap, ALU.mult, ALU.add), [tq] + rd, [tq])
        P.op("dve", lambda h: h.tensor_scalar(tq.t[:, 0:n], tq.t[:, 0:n], -math.pi, math.pi, ALU.max, ALU.min), [tq], [tq])
        P.op("act", lambda h: h.activation(s_ap, tq.t[:, 0:n], AF.Sin, bias=zero_t.t[:, 0:1], scale=1.0), [tq, zero_t], wr)
        P.op("dve", lambda h: h.scalar_tensor_tensor(tf, tq.t[:, 0:n], -1.0, tq.t[:, 0:n], ALU.mult, ALU.max), [tq], [tiq])
        P.op("act", lambda h: h.activation(c_ap, tf, AF.Sin, bias=halfpi.t[:, 0:1], scale=-1.0), [tiq, halfpi], wr)

    def disc(lre, lim, lst, n, pref):
        o = {}
        for nm in ["step", "mag", "th", "c", "s", "tmp", "nr", "den", "cr", "ci", "a", "b"]:
            o[nm] = small(pref + nm, [128, n])
        al = [o[k] for k in o] + [lre, lim, lst]
        P.op("act", lambda h: h.activation(o["step"].t[:], lst.t[:, 0:n], AF.Exp), al, al)
        P.op("dve", lambda h: h.tensor_tensor(o["th"].t[:], lim.t[:, 0:n], o["step"].t[:], ALU.mult), al, al)
        P.op("dve", lambda h: h.tensor_tensor(o["a"].t[:], lre.t[:, 0:n], o["step"].t[:], ALU.mult), al, al)
        P.op("act", lambda h: h.activation(o["mag"].t[:], o["a"].t[:], AF.Exp), al, al)
        sincos(o["th"].t[:], n, o["s"].t[:], o["c"].t[:], al, al)
        P.op("dve", lambda h: h.tensor_tensor(o["a"].t[:], o["mag"].t[:], o["c"].t[:], ALU.mult), al, al)
        P.op("dve", lambda h: h.tensor_scalar(o["nr"].t[:], o["a"].t[:], 1.0, -1.0, ALU.mult, ALU.add), al, al)
        P.op("dve", lambda h: h.tensor_tensor(o["b"].t[:], o["mag"].t[:], o["s"].t[:], ALU.mult), al, al)
        P.op("dve", lambda h: h.tensor_tensor(o["den"].t[:], lre.t[:, 0:n], lre.t[:, 0:n], ALU.mult), al, al)
        P.op("dve", lambda h: h.tensor_tensor(o["tmp"].t[:], lim.t[:, 0:n], lim.t[:, 0:n], ALU.mult), al, al)
        P.op("dve", lambda h: h.tensor_tensor(o["den"].t[:], o["den"].t[:], o["tmp"].t[:], ALU.add), al, al)
        P.op("dve", lambda h: h.reciprocal(o["den"].t[:], o["den"].t[:]), al, al)
        P.op("dve", lambda h: h.tensor_tensor(o["cr"].t[:], o["nr"].t[:], lre.t[:, 0:n], ALU.mult), al, al)
        P.op("dve", lambda h: h.tensor_tensor(o["tmp"].t[:], o["b"].t[:], lim.t[:, 0:n], ALU.mult), al, al)
        P.op("dve", lambda h: h.tensor_tensor(o["cr"].t[:], o["cr"].t[:], o["tmp"].t[:], ALU.add), al, al)
        P.op("dve", lambda h: h.tensor_tensor(o["cr"].t[:], o["cr"].t[:], o["den"].t[:], ALU.mult), al, al)
        P.op("dve", lambda h: h.tensor_tensor(o["ci"].t[:], o["b"].t[:], lre.t[:, 0:n], ALU.mult), al, al)
        P.op("dve", lambda h: h.tensor_tensor(o["tmp"].t[:], o["nr"].t[:], lim.t[:, 0:n], ALU.mult), al, al)
        P.op("dve", lambda h: h.tensor_tensor(o["ci"].t[:], o["ci"].t[:], o["tmp"].t[:], ALU.subtract), al, al)
        P.op("dve", lambda h: h.tensor_tensor(o["ci"].t[:], o["ci"].t[:], o["den"].t[:], ALU.mult), al, al)
        return o, al

    ds_, als = disc(lre_s, lim_s, lst_s, 16, "ds_")
    dr_, alr = disc(lre_r, lim_r, lst_r, 256, "dr_")
    bbr = small("bbr", [128, 256]); bbi = small("bbi", [128, 256]); tmpr = small("tmpr", [128, 256])
    alr2 = alr + [bbr, bbi, tmpr, bre_r, bim_r]
    P.op("dve", lambda h: h.tensor_tensor(bbr.t[:], dr_["cr"].t[:], bre_r.t[:], ALU.mult), alr2, alr2)
    P.op("dve", lambda h: h.tensor_tensor(tmpr.t[:], dr_["ci"].t[:], bim_r.t[:], ALU.mult), alr2, alr2)
    P.op("dve", lambda h: h.tensor_tensor(bbr.t[:], bbr.t[:], tmpr.t[:], ALU.subtract), alr2, alr2)
    P.op("dve", lambda h: h.tensor_tensor(bbi.t[:], dr_["cr"].t[:], bim_r.t[:], ALU.mult), alr2, alr2)
    P.op("dve", lambda h: h.tensor_tensor(tmpr.t[:], dr_["ci"].t[:], bre_r.t[:], ALU.mult), alr2, alr2)
    P.op("dve", lambda h: h.tensor_tensor(bbi.t[:], bbi.t[:], tmpr.t[:], ALU.add), alr2, alr2)
    BbT = [small("BbT%d" % ri, [128, 16, 128], BF16) for ri in range(2)]
    for ri, src in enumerate([bbr, bbi]):
        for qq in range(4):
            for g2 in range(2):
                m = rmask.t[:, qq * 2 + g2:qq * 2 + g2 + 1]
                o_ap = BbT[ri].t[:].rearrange("p (ft q) c -> p ft q c", q=4)[:, :, qq, g2 * 64:(g2 + 1) * 64]
                i_ap = src.t[:].rearrange("p (ft d) -> p ft d", ft=4)
                P.op("dve", lambda h, o_ap=o_ap, i_ap=i_ap, m=m: h.tensor_scalar(o_ap, i_ap, m, None, ALU.mult), alr2 + [rmask], [BbT[ri]])
    CT = [small("CT%d" % ri, [128, 16, 128], BF16) for ri in range(2)]
    for ri in range(2):
        P.op("dve", lambda h, ri=ri: h.memset(CT[ri].t[:], 0.0), [], [CT[ri]])
    for ri, (src, sgn) in enumerate([(cre_s, 1.0), (cim_s, -1.0)]):
        for pair in range(16):
            qq = pair % 4
            for g2 in range(2):
                m = smask.t[:, g2:g2 + 1]
                col = qq * 32 + g2 * 16
                P.op("dve", lambda h, ri=ri, pair=pair, col=col, m=m, src=src, sgn=sgn: h.tensor_scalar(
                    CT[ri].t[:, pair, col:col + 16], src.t[:, pair, :], m, sgn, ALU.mult, ALU.mult), [src, smask], [CT[ri]])
    rr = ds_["mag"]; th = ds_["th"]
    cth = small("cth", [128, 16]); sth = small("sth", [128, 16])
    P.op("dve", lambda h: h.tensor_copy(cth.t[:], ds_["c"].t[:]), als, [cth])
    P.op("dve", lambda h: h.tensor_copy(sth.t[:], ds_["s"].t[:]), als, [sth])

    tabc = [small("tabc%d" % i, [128, 1024]) for i in range(4)]
    tabs = [small("tabs%d" % i, [128, 1024]) for i in range(4)]
    WS = [dict(gr=small("gr0", [128, 1024]), gi=small("gi0", [128, 1024]), yr=small("yr0", [128, 1024]), yi=small("yi0", [128, 1024]),
               hrb=small("hrb0", [128, 1024], BF16), hib=small("hib0", [128, 1024], BF16))]
    hib1 = small("hib1", [128, 1024], BF16)
    ytmp = small("ytmp", [128, 512]); ysq = small("ysq", [128, 512])
    stp_l = [small("stp%d" % p_, [128, 2]) for p_ in range(16)]
    sts_p = [small("stsp%d" % p_, [128, 4, 2]) for p_ in range(16)]
    m32 = small("m32", [128, 32])
    P.op("dve", lambda h: h.memset(m32.t[:], 1.0), [], [m32])
    P.op("dve", lambda h: h.memset(m32.t[:].rearrange("p (r c) -> p r c", r=4)[:, :, 0], 0.0), [m32], [m32])
    for w_ in WS:
        w_["gin"] = small("gin0", [128, 2]); w_["hend"] = small("hend0", [128, 2])
        w_["gin4"] = small("gin40", [128, 4, 2]); w_["d0"] = small("d00", [128, 32])
    for p_ in range(16):
        P.op("dve", lambda h, p_=p_: h.memset(stp_l[p_].t[:], 0.0), [], [stp_l[p_]])
    P.barrier()
    blkA = lre_r.at
    blkB = dr_["step"].at
    WS.append(dict(gr=P.sb("gr1", [128, 1024], F32, at=blkB), gi=P.sb("gi1", [128, 1024], F32, at=blkB + 4096),
                   yr=P.sb("yr1", [128, 1024], F32, at=blkB + 8192), yi=P.sb("yi1", [128, 1024], F32, at=blkA),
                   hrb=P.sb("hrb1", [128, 1024], BF16, at=blkB + 12288), hib=hib1,
                   gin=small("gin1", [128, 2]), hend=small("hend1", [128, 2]),
                   gin4=small("gin41", [128, 4, 2]), d0=small("d01", [128, 32])))
    ang = WS[0]["gr"]
    GC = 1.5957691216057308
    kcount = [0]

    def stageX(pair, qq, ft, col0, T, tc_, ts_, init_re, init_im, init_bufs, out_b):
        w = WS[kcount[0] % 2]
        kcount[0] += 1
        gr, gi, yr, yi, hrb, hib, gin, hend = w["gr"], w["gi"], w["yr"], w["yi"], w["hrb"], w["hib"], w["gin"], w["hend"]
        for h0 in range(0, T, 512):
            n = min(512, T - h0)
            pxr = next_pf(); pxi = next_pf()
            P.mm([lambda h, pxr=pxr, n=n, h0=h0: h.matmul(pxr.t[:, 0:n], BbT[0].t[:, pair, :], uT.t[:, ft, col0 + h0:col0 + h0 + n], start=True, stop=True)], [BbT[0], uT], [pxr])
            P.mm([lambda h, pxi=pxi, n=n, h0=h0: h.matmul(pxi.t[:, 0:n], BbT[1].t[:, pair, :], uT.t[:, ft, col0 + h0:col0 + h0 + n], start=True, stop=True)], [BbT[1], uT], [pxi])
            c_ = tc_.t[:, h0:h0 + n]; s_ = ts_.t[:, h0:h0 + n]
            P.op("dve", lambda h, pxr=pxr, c_=c_, n=n, h0=h0: h.tensor_tensor(yr.t[:, h0:h0 + n], pxr.t[:, 0:n], c_, ALU.mult), [pxr, tc_], [yr])
            P.op("dve", lambda h, pxi=pxi, s_=s_, n=n, h0=h0: h.tensor_tensor(gr.t[:, h0:h0 + n], pxi.t[:, 0:n], s_, ALU.mult), [pxi, ts_], [gr])
            P.op("dve", lambda h, n=n, h0=h0: h.tensor_tensor(yr.t[:, h0:h0 + n], yr.t[:, h0:h0 + n], gr.t[:, h0:h0 + n], ALU.add), [yr, gr], [yr])
            P.op("dve", lambda h, pxi=pxi, c_=c_, n=n, h0=h0: h.tensor_tensor(yi.t[:, h0:h0 + n], pxi.t[:, 0:n], c_, ALU.mult), [pxi, tc_], [yi])
            P.op("dve", lambda h, pxr=pxr, s_=s_, n=n, h0=h0: h.tensor_tensor(gi.t[:, h0:h0 + n], pxr.t[:, 0:n], s_, ALU.mult), [pxr, ts_], [gi])
            P.op("dve", lambda h, n=n, h0=h0: h.tensor_tensor(yi.t[:, h0:h0 + n], yi.t[:, h0:h0 + n], gi.t[:, h0:h0 + n], ALU.subtract), [yi, gi], [yi])
        ct = cth.t[:, pair:pair + 1]; st_ = sth.t[:, pair:pair + 1]
        P.op("dve", lambda h: h.tensor_scalar(gin.t[:, 0:1], init_re, ct, None, ALU.mult), init_bufs + [cth], [gin])
        P.op("dve", lambda h: h.scalar_tensor_tensor(gin.t[:, 0:1], init_im, st_, gin.t[:, 0:1], ALU.mult, ALU.subtract), init_bufs + [sth, gin], [gin])
        P.op("dve", lambda h: h.tensor_scalar(gin.t[:, 0:1], gin.t[:, 0:1], -1.0, None, ALU.mult), [gin], [gin])
        P.op("dve", lambda h: h.tensor_scalar(gin.t[:, 1:2], init_re, st_, None, ALU.mult), init_bufs + [sth], [gin])
        P.op("dve", lambda h: h.scalar_tensor_tensor(gin.t[:, 1:2], init_im, ct, gin.t[:, 1:2], ALU.mult, ALU.add), init_bufs + [cth, gin], [gin])
        rb = rr.t[:, pair:pair + 1].to_broadcast([128, T])
        P.op("dve", lambda h: h.tensor_tensor_scan(gr.t[:, 0:T], rb, yr.t[:, 0:T], gin.t[:, 0:1], ALU.mult, ALU.add), [yr, gin] + als, [gr])
        P.op("dve", lambda h: h.tensor_tensor_scan(gi.t[:, 0:T], rb, yi.t[:, 0:T], gin.t[:, 1:2], ALU.mult, ALU.add), [yi, gin] + als, [gi])
        c_ = tc_.t[:, 0:T]; s_ = ts_.t[:, 0:T]
        yrb = yr.t[:].bitcast(BF16)
        yib = yi.t[:].bitcast(BF16)
        p1 = yrb[:, 0:T]; p2 = yrb[:, 1024:1024 + T]; p3 = yib[:, 0:T]; p4 = yib[:, 1024:1024 + T]
        P.op("dve", lambda h: h.tensor_tensor(p1, gr.t[:, 0:T], c_, ALU.mult), [gr, tc_], [yr])
        P.op("dve", lambda h: h.scalar_tensor_tensor(p2, gi.t[:, 0:T], -1.0, s_, ALU.mult, ALU.mult), [gi, ts_], [yr])
        P.op("dve", lambda h: h.tensor_tensor(p3, gr.t[:, 0:T], s_, ALU.mult), [gr, ts_], [yi])
        P.op("dve", lambda h: h.tensor_tensor(p4, gi.t[:, 0:T], c_, ALU.mult), [gi, tc_], [yi])
        cl = tc_.t[:, T - 1:T]; sl = ts_.t[:, T - 1:T]
        P.op("dve", lambda h: h.tensor_tensor(gin.t[:, 0:1], gi.t[:, T - 1:T], sl, ALU.mult), [gi, ts_, gin], [gin])
        P.op("dve", lambda h: h.tensor_tensor(gin.t[:, 1:2], gi.t[:, T - 1:T], cl, ALU.mult), [gi, tc_, gin], [gin])
        P.op("dve", lambda h: h.scalar_tensor_tensor(out_b.t[:, 0:1], gr.t[:, T - 1:T], cl, gin.t[:, 0:1], ALU.mult, ALU.subtract), [gr, tc_, gin], [out_b])
        P.op("dve", lambda h: h.scalar_tensor_tensor(out_b.t[:, 1:2], gr.t[:, T - 1:T], sl, gin.t[:, 1:2], ALU.mult, ALU.add), [gr, ts_, gin], [out_b])
        return dict(pair=pair, qq=qq, T=T, prods=[(0, lambda h0, n: yrb[:, h0:h0 + n], yr), (0, lambda h0, n: yrb[:, 1024 + h0:1024 + h0 + n], yr),
                                                  (1, lambda h0, n: yib[:, h0:h0 + n], yi), (1, lambda h0, n: yib[:, 1024 + h0:1024 + h0 + n], yi)], after=[])

    def v48(ap):
        return ap.rearrange("p (r c) -> p r c", r=4)

    def stageXs(pair, qq, ft, tc_, ts_):
        w = WS[kcount[0] % 2]
        kcount[0] += 1
        gr, gi, yr, yi, hrb, hib, gin4, d0 = w["gr"], w["gi"], w["yr"], w["yi"], w["hrb"], w["hib"], w["gin4"], w["d0"]
        ucols = v48(uT.t[:, ft, :])[:, :, 1024:1032]
        pxr = next_pf(); pxi = next_pf()
        P.mm([lambda h: h.matmul(pxr.t[:, 0:32], BbT[0].t[:, pair, :], ucols, start=True, stop=True)], [BbT[0], uT], [pxr])
        P.mm([lambda h: h.matmul(pxi.t[:, 0:32], BbT[1].t[:, pair, :], ucols, start=True, stop=True)], [BbT[1], uT], [pxi])
        cb = tc_.t[:, 0:8].unsqueeze(1).to_broadcast([128, 4, 8])
        sb_ = ts_.t[:, 0:8].unsqueeze(1).to_broadcast([128, 4, 8])
        xr = v48(pxr.t[:, 0:32]); xi = v48(pxi.t[:, 0:32])
        yrv = v48(yr.t[:, 0:32]); yiv = v48(yi.t[:, 0:32]); grv = v48(gr.t[:, 0:32]); giv = v48(gi.t[:, 0:32])
        P.op("dve", lambda h: h.tensor_tensor(yrv, xr, cb, ALU.mult), [pxr, tc_], [yr])
        P.op("dve", lambda h: h.tensor_tensor(grv, xi, sb_, ALU.mult), [pxi, ts_], [gr])
        P.op("dve", lambda h: h.tensor_tensor(yrv, yrv, grv, ALU.add), [yr, gr], [yr])
        P.op("dve", lambda h: h.tensor_tensor(yiv, xi, cb, ALU.mult), [pxi, tc_], [yi])
        P.op("dve", lambda h: h.tensor_tensor(giv, xr, sb_, ALU.mult), [pxr, ts_], [gi])
        P.op("dve", lambda h: h.tensor_tensor(yiv, yiv, giv, ALU.subtract), [yi, gi], [yi])
        ct = cth.t[:, pair:pair + 1]; st_ = sth.t[:, pair:pair + 1]; rs = rr.t[:, pair:pair + 1]
        ire = sre0.t[:, :, pair]; iim = sim0.t[:, :, pair]
        g_re = gin4.t[:, :, 0]; g_im = gin4.t[:, :, 1]
        P.op("dve", lambda h: h.tensor_scalar(g_re, ire, ct, None, ALU.mult), [sre0, cth], [gin4])
        P.op("dve", lambda h: h.scalar_tensor_tensor(g_re, iim, st_, g_re, ALU.mult, ALU.subtract), [sim0, sth, gin4], [gin4])
        P.op("dve", lambda h: h.tensor_scalar(g_re, g_re, -1.0, None, ALU.mult), [gin4], [gin4])
        P.op("dve", lambda h: h.tensor_scalar(g_im, ire, st_, None, ALU.mult), [sre0, sth, gin4], [gin4])
        P.op("dve", lambda h: h.scalar_tensor_tensor(g_im, iim, ct, g_im, ALU.mult, ALU.add), [sim0, cth, gin4], [gin4])
        P.op("dve", lambda h: h.scalar_tensor_tensor(yrv[:, :, 0], g_re, rs, yrv[:, :, 0], ALU.mult, ALU.add), [gin4, yr] + als, [yr])
        P.op("dve", lambda h: h.scalar_tensor_tensor(yiv[:, :, 0], g_im, rs, yiv[:, :, 0], ALU.mult, ALU.add), [gin4, yi] + als, [yi])
        P.op("dve", lambda h: h.tensor_scalar(d0.t[:], m32.t[:], rs, None, ALU.mult), [m32] + als, [d0])
        P.op("dve", lambda h: h.tensor_tensor_scan(gr.t[:, 0:32], d0.t[:], yr.t[:, 0:32], 0.0, ALU.mult, ALU.add), [yr, d0], [gr])
        P.op("dve", lambda h: h.tensor_tensor_scan(gi.t[:, 0:32], d0.t[:], yi.t[:, 0:32], 0.0, ALU.mult, ALU.add), [yi, d0], [gi])
        hrv = v48(hrb.t[:, 0:32]); hiv = v48(hib.t[:, 0:32])
        out_b = sts_p[pair]
        P.op("dve", lambda h: h.tensor_tensor(yrv, grv, cb, ALU.mult), [gr, tc_], [yr])
        P.op("dve", lambda h: h.tensor_tensor(yiv, giv, sb_, ALU.mult), [gi, ts_], [yi])
        P.op("dve", lambda h: h.tensor_tensor(hrv, yrv, yiv, ALU.subtract), [yr, yi], [hrb])
        P.op("dve", lambda h: h.tensor_tensor(out_b.t[:, :, 0], yrv[:, :, 7], yiv[:, :, 7], ALU.subtract), [yr, yi], [out_b])
        P.op("dve", lambda h: h.tensor_tensor(yrv, grv, sb_, ALU.mult), [gr, ts_, hrb, out_b], [yr])
        P.op("dve", lambda h: h.tensor_tensor(yiv, giv, cb, ALU.mult), [gi, tc_, hrb, out_b], [yi])
        P.op("dve", lambda h: h.tensor_tensor(hiv, yrv, yiv, ALU.add), [yr, yi], [hib])
        P.op("dve", lambda h: h.tensor_tensor(out_b.t[:, :, 1], yrv[:, :, 7], yiv[:, :, 7], ALU.add), [yr, yi], [out_b])
        return dict(pair=pair, qq=qq, T=32, hrb=hrb, hib=hib, after=[])

    def y_evac_s(ft):
        ucols = v48(uT.t[:, ft, :])[:, :, 1024:1032]
        dcols = v48(ygT.t[:, ft, :])[:, :, 1024:1032]
        yp_ = ypb[0]
        ypv = v48(yp_.t[:, 0:32]); yt = v48(ytmp.t[:, 0:32]); yq = v48(ysq.t[:, 0:32])
        P.op("dve", lambda h: h.scalar_tensor_tensor(yt, ucols, dsk.t[:, ft:ft + 1], ypv, ALU.mult, ALU.add), [uT, dsk, yp_], [ytmp])
        P.op("dve", lambda h: h.tensor_tensor(yq, yt, yt, ALU.mult), [ytmp], [ysq])
        P.op("dve", lambda h: h.tensor_scalar(yq, yq, 0.044715, 1.0, ALU.mult, ALU.add), [ysq], [ysq])
        P.op("dve", lambda h: h.tensor_tensor(yq, yq, yt, ALU.mult), [ysq, ytmp], [ysq])
        P.op("act", lambda h: h.activation(ysq.t[:, 0:32], ysq.t[:, 0:32], AF.Sigmoid, scale=GC), [ysq], [ysq])
        P.op("dve", lambda h: h.tensor_tensor(dcols, yq, yt, ALU.mult), [ysq, ytmp], [ygT])

    ypb = [pf[4], pf[5]]

    def stageY(c):
        if "prods" in c:
            pair, qq, T = c["pair"], c["qq"], c["T"]
            for h0 in range(0, T, 512):
                n = min(512, T - h0)
                yp_ = ypb[h0 // 512]
                fns = []
                for k_, (ci_, rf_, _b) in enumerate(c["prods"]):
                    fns.append(lambda h, yp_=yp_, n=n, h0=h0, ci_=ci_, rf_=rf_, k_=k_: h.matmul(
                        yp_.t[:, 0:n], CT[ci_].t[:, pair, :], rf_(h0, n), start=(qq == 0 and k_ == 0), stop=(qq == 3 and k_ == 3)))
                P.mm(fns, [CT[0], CT[1], c["prods"][0][2], c["prods"][2][2]], [yp_])
            for f in c["after"]:
                f()
            return
        pair, qq, T, hrb, hib = c["pair"], c["qq"], c["T"], c["hrb"], c["hib"]
        for h0 in range(0, T, 512):
            n = min(512, T - h0)
            yp_ = ypb[h0 // 512]
            P.mm([lambda h, yp_=yp_, n=n, h0=h0: h.matmul(yp_.t[:, 0:n], CT[0].t[:, pair, :], hrb.t[:, h0:h0 + n], start=(qq == 0), stop=False),
                  lambda h, yp_=yp_, n=n, h0=h0: h.matmul(yp_.t[:, 0:n], CT[1].t[:, pair, :], hib.t[:, h0:h0 + n], start=False, stop=(qq == 3))],
                 [CT[0], CT[1], hrb, hib], [yp_])
        for f in c["after"]:
            f()

    def y_evac(yp_, ft, col0, n):
        P.op("dve", lambda h: h.scalar_tensor_tensor(ytmp.t[:, 0:n], uT.t[:, ft, col0:col0 + n], dsk.t[:, ft:ft + 1], yp_.t[:, 0:n], ALU.mult, ALU.add),
             [uT, dsk, yp_], [ytmp])
        P.op("dve", lambda h: h.tensor_tensor(ysq.t[:, 0:n], ytmp.t[:, 0:n], ytmp.t[:, 0:n], ALU.mult), [ytmp], [ysq])
        P.op("dve", lambda h: h.tensor_scalar(ysq.t[:, 0:n], ysq.t[:, 0:n], 0.044715, 1.0, ALU.mult, ALU.add), [ysq], [ysq])
        P.op("dve", lambda h: h.tensor_tensor(ysq.t[:, 0:n], ysq.t[:, 0:n], ytmp.t[:, 0:n], ALU.mult), [ysq, ytmp], [ysq])
        P.op("act", lambda h: h.activation(ysq.t[:, 0:n], ysq.t[:, 0:n], AF.Sigmoid, scale=GC), [ysq], [ysq])
        P.op("dve", lambda h: h.tensor_tensor(ygT.t[:, ft, col0:col0 + n], ysq.t[:, 0:n], ytmp.t[:, 0:n], ALU.mult), [ysq, ytmp], [ygT])

    pending = [None]

    def push(ctx):
        if pending[0] is not None:
            stageY(pending[0])
        pending[0] = ctx

    for ft in range(4):
        for qq in range(4):
            pair = ft * 4 + qq
            P.op("dve", lambda h, pair=pair: h.tensor_scalar(ang.t[:], iota.t[:], th.t[:, pair:pair + 1], None, ALU.mult), [iota] + als, [ang])
            sincos(ang.t[:], 1024, tabs[qq].t[:], tabc[qq].t[:], [ang], [tabs[qq], tabc[qq]])
        for seg in range(4):
            for qq in range(4):
                pair = ft * 4 + qq
                ctx = stageX(pair, qq, ft, seg * NCH, 1024, tabc[qq], tabs[qq],
                             stp_l[pair].t[:, 0:1], stp_l[pair].t[:, 1:2], [stp_l[pair]], stp_l[pair])
                if qq == 3:
                    ctx["after"] = [(lambda ft=ft, seg=seg, hf=hf: y_evac(ypb[hf], ft, seg * NCH + hf * 512, 512)) for hf in range(2)]
                push(ctx)
        for qq in range(4):
            pair = ft * 4 + qq
            ctx = stageXs(pair, qq, ft, tabc[qq], tabs[qq])
            if qq == 3:
                ctx["after"] = [(lambda ft=ft: y_evac_s(ft))]
            push(ctx)
    push(None)
    stp2 = P.sb("stp2", [128, 2, 16], F32, at=tq.at); sts2 = P.sb("sts2", [128, 4, 2, 16], F32, at=tq.at + 128)
    for p_ in range(16):
        P.op("dve", lambda h, p_=p_: h.tensor_copy(stp2.t[:, :, p_], stp_l[p_].t[:, 0:2]), [stp_l[p_]], [stp2])
        P.op("dve", lambda h, p_=p_: h.tensor_copy(sts2.t[:, :, :, p_], sts_p[p_].t[:]), [sts_p[p_]], [sts2])
    if STOP == 'SSM':
        for r_ in range(4):
            P.dma("pool", OUT["yp"].ap().rearrange("(p f x) c -> p f (x c)", p=128, f=4)[:, :, r_ * 1024:(r_ + 1) * 1024],
                  ygT.t[:, :, r_ * NCH:r_ * NCH + 1024], [ygT], [outbufs["yp"]], ygT)
        P.dma("pool", OUT["ys"].ap()[:, 0:128].rearrange("p (f r s) -> p f r s", f=4, r=4),
              ygT.t[:].rearrange("p f (r c) -> p f r c", r=4)[:, :, :, 1024:1032], [ygT], [outbufs["ys"]], ygT)
    P.dma("sp", OUT["ssm_p"].ap(), stp2.t[:], [stp2], [outbufs["ssm_p"]], stp2)
    P.dma("sp", OUT["ssm_s"].ap(), sts2.t[:], [sts2], [outbufs["ssm_s"]], sts2)
    P.barrier()
    hn1o = P.sb("hn1o", [128, KT, NCH], BF16, at=uT.at)
    P.off = L1 + 40960
    zall = P.sb("zall", [128, KT, NCH], BF16)
    wz = [P.sb("wz%d" % i, [128, KT, 128], BF16) for i in range(2)]
    zz = P.sb("zz", [128, NCH], F32)
    for j in range(8):
        P.dma("sp", y_src[j].t.ap().rearrange("(ft p) t -> p ft t", p=128), ygT.t[:, :, j * 516:(j + 1) * 516], [ygT], [y_src[j]], ygT)
        P.coll(y_src[j], y_dst[j], GROUPS)

    for j in range(8):
        P.dma("sp", hn1o.t[:, 2 * j:2 * j + 2, :], hn_src[j].t.ap().rearrange("(k p) t -> p k t", p=128), [hn_src[j]], [hn1o], hn1o)
    for nt in range(16):
        wb_ = wz[nt % 2]
        P.dma("pool", wb_.t[:], IN["w_in_c"].ap()[:, 2048 + nt * 128:2048 + (nt + 1) * 128].rearrange("(kt p) c -> p kt c", p=128), [], [wb_], wb_)
        for (c0, n) in [(0, 512), (512, 512), (1024, 8)]:
            def zev(pk, c0=c0, n=n, nt=nt):
                P.op("act", lambda h: h.activation(zz.t[:, c0:c0 + n], pk.t[:, 0:n], AF.Sigmoid), [pk], [zz])
                P.op("dve", lambda h: h.tensor_tensor(zall.t[:, nt, c0:c0 + n], zz.t[:, c0:c0 + n], pk.t[:, 0:n], ALU.mult), [zz, pk], [zall])
            proj_feat(wb_, None, zev, hn1o, lambda kt, c0=c0, n=n: hn1o.t[:, kt, c0:c0 + n], n)

    if STOP == 'SSM':
        P.barrier()
        return
    P.barrier()
    P.off = CONST_END
    y2T = P.sb("y2T", [128, KT, NO], BF16)
    F0 = P.off
    ygo = P.sb("ygo", [128, KT, NCH], BF16)
    ych = [P.sb("ych%d" % i, [128, 4, NCH], BF16) for i in range(2)]
    wt2 = [P.sb("wt2_%d" % i, [128, KT, 128], BF16) for i in range(2)]
    gl = P.sb("gl", [128, NCH], F32)
    assert P.off <= L1 + 40960
    P.op("pool", lambda h: h.memset(y2T.t[:, :, NCH:NO], 0.0), [], [y2T])
    ci = 0
    for rf in range(4):
        for r in range(4):
            yc = ych[ci % 2]; ci += 1
            for hh_ in range(2):
                P.dma("sp", yc.t[:, :, hh_ * 516:(hh_ + 1) * 516],
                      y_dst[2 * r + hh_].t.ap()[rf * 512:(rf + 1) * 512, :].rearrange("(ft p) t -> p ft t", p=128),
                      [y_dst[2 * r + hh_]], [yc], yc)
            dst = ygo.t[:, rf * 4:(rf + 1) * 4, :]
            if r == 0:
                P.op("dve", lambda h, yc=yc, dst=dst: h.tensor_scalar(dst, yc.t[:], sel.t[:, 0:1], None, ALU.mult), [yc, sel], [ygo])
            else:
                P.op("dve", lambda h, yc=yc, dst=dst, r=r: h.scalar_tensor_tensor(dst, yc.t[:], sel.t[:, r:r + 1], dst, ALU.mult, ALU.add), [yc, sel, ygo], [ygo])
    if STOP == 'G1':
        P.barrier()
        return
    for nt in range(int(os.environ.get('MK_NT', '16'))):
        wa = wt2[nt % 2]
        P.dma("pool", wa.t[:], IN["w_glu"].ap()[:, nt * 128:(nt + 1) * 128].rearrange("(kt p) c -> p kt c", p=128), [], [wa], wa)
        for (c0, n) in [(0, 512), (512, 512), (1024, 8)]:
            proj_feat(wa, None, lambda pk, c0=c0, n=n, nt=nt: P.op("act", lambda h: h.activation(gl.t[:, c0:c0 + n], pk.t[:, 0:n], AF.Sigmoid, bias=bglu.t[:, nt:nt + 1], scale=1.0), [pk, bglu], [gl]),
                      ygo, lambda kt, c0=c0, n=n: ygo.t[:, kt, c0:c0 + n], n)
        P.op("dve", lambda h, nt=nt: h.tensor_tensor(gl.t[:], gl.t[:], ygo.t[:, nt, :], ALU.mult), [gl, ygo], [gl])
        P.op("dve", lambda h, nt=nt: h.tensor_tensor(y2T.t[:, nt, 0:NCH], gl.t[:], zall.t[:, nt, :], ALU.mult), [gl, zall], [y2T])
    if STOP == 'G2':
        P.barrier()
        return
    P.barrier()
    P.off = F0
    wgb2 = [P.sb("wgc%d" % i, [128, KT, 512], BF16) for i in range(2)]
    h1s = [P.sb("h1f%d" % i, [128, D], F32) for i in range(5)]
    junk = P.sb("junk3", [128, D], F32); ss = P.sb("ss3", [128, 4], F32)
    yo = [P.sb("yo%d" % i, [128, D], F32) for i in range(2)]
    load_g("g_fin")
    wi = 0
    for tiles in [list(range(0, 5)), list(range(5, 9))]:
        for si, tj in enumerate(tiles):
            o0 = tj * 128
            P.dma("sp", h1s[si].t[:], h1_scr.t.ap()[o0:o0 + 128, :], [h1_scr], [h1s[si]], h1s[si])
        for g4 in range(4):
            wg = wgb2[wi % 2]
            wi += 1
            P.dma("pool", wg.t[:], IN["w_out_c"].ap()[:, g4 * 512:(g4 + 1) * 512].rearrange("(kt p) c -> p kt c", p=128), [], [wg], wg)
            for si, tj in enumerate(tiles):
                o0 = tj * 128
                ht_ = h1s[si]
                pk = next_pf()
                fns = [(lambda h, pk=pk, kt=kt, o0=o0, wg=wg: h.matmul(pk.t[:], y2T.t[:, kt, o0:o0 + 128], wg.t[:, kt, :],
                                                                         start=(kt == 0), stop=(kt == KT - 1))) for kt in range(KT)]
                P.mm(fns, [y2T, wg], [pk])
                P.op("dve", lambda h, pk=pk, ht_=ht_, g4=g4: h.tensor_tensor(ht_.t[:, g4 * 512:(g4 + 1) * 512], ht_.t[:, g4 * 512:(g4 + 1) * 512], pk.t[:], ALU.add),
                     [pk, ht_], [ht_])
        for si, tj in enumerate(tiles):
            o0 = tj * 128
            ht_ = h1s[si]
            yo_ = yo[tj % 2]
            rmsnorm_rows(ht_, yo_, ss, junk)
            if tj < 8:
                P.dma("sp", OUT["yp"].ap()[o0:o0 + 128, :], yo_.t[:], [yo_], [outbufs["yp"]], yo_)
            else:
                P.dma("sp", OUT["ys"].ap(), yo_.t[:], [yo_], [outbufs["ys"]], yo_)
    P.barrier()


_NC_CACHE = {}


def _rope_tables(pos):
    half = 64
    inv = (np.float32(10000.0) ** (-np.arange(half, dtype=np.float32) / np.float32(half))).astype(np.float32)
    ang = pos.astype(np.float32)[:, None] * inv[None, :]
    return np.cos(ang).astype(np.float32), np.sin(ang).astype(np.float32)


def kernel(x_prompt, x_sample, cache_win_k, cache_win_v, state_conv, state_ssm_re, state_ssm_im,
           attn_norm, w_in_ab, conv_w, w_out_ab, ssm_norm, w_in_c, lam_re, lam_im, log_step,
           b_re, b_im, c_re, c_im, d_skip, w_glu, b_glu, w_out_c, final_norm):
    f = lambda a: np.ascontiguousarray(np.asarray(a, dtype=np.float32))
    x_prompt, x_sample = f(x_prompt), f(x_sample)
    cache_win_k, cache_win_v, state_conv = f(cache_win_k), f(cache_win_v), f(state_conv)
    state_ssm_re, state_ssm_im = f(state_ssm_re), f(state_ssm_im)
    w_in_ab0, w_out_ab0, w_in_c0, w_glu0, w_out_c0 = f(w_in_ab)[0], f(w_out_ab)[0], f(w_in_c)[0], f(w_glu)[0], f(w_out_c)[0]
    lam_re, lam_im, log_step = f(lam_re)[0], f(lam_im)[0], f(log_step)[0]
    b_re, b_im, c_re, c_im = f(b_re)[0], f(b_im)[0], f(c_re)[0], f(c_im)[0]
    d_skip0, b_glu0 = f(d_skip)[0], f(b_glu)[0]
    if "nc" not in _NC_CACHE:
        _NC_CACHE["nc"] = build_nc()
    nc = _NC_CACHE["nc"]

    kk = np.arange(128)[:, None]
    qq_ = np.arange(512)[None, :]
    maskp = np.stack([mult_of(qq_ - ((i - 16) * 128 + kk)) for i in range(20)], 1)
    rows = np.arange(2176).reshape(17, 128)
    s_ = np.arange(128)[None, :]
    masks = np.zeros((128, 17, 128), np.float32)
    for i in range(17):
        row = rows[i][:, None]
        m = mult_of(2048 + s_ - row)
        m[:, 8:] = ((2048 + s_[:, 8:] - row) == 0)
        masks[:, i, :] = m
    iota = np.broadcast_to(np.arange(1024, dtype=np.float32)[None, :], (128, 1024)).copy()
    rmask = np.zeros((128, 8), np.float32)
    for p in range(128):
        rmask[p, (p // 32) * 2 + (p % 32) // 16] = 1.0
    smask = np.zeros((128, 2), np.float32)
    smask[:64, 0] = 1.0
    smask[64:, 1] = 1.0
    bc = lambda v: np.ascontiguousarray(np.broadcast_to(v[None, :], (128, v.shape[0])))

    in_maps = []
    for c in range(8):
        b, r = c // 4, c % 4
        T0 = r * NOWN
        xh = np.zeros((NTP, D), np.float32)
        lo = T0 - NHALO
        src_lo = max(lo, 0)
        xh[src_lo - lo:] = x_prompt[b, src_lo:T0 + NOWN]
        pos = np.concatenate([np.arange(lo, T0 + NOWN), PAST + np.arange(128)]).astype(np.float32)
        valid = (pos[:NTP] >= 0).astype(np.float32)
        cosv, sinv = _rope_tables(np.maximum(pos, 0))
        xs = np.zeros((128, D), np.float32)
        xs[:8] = x_sample[c]
        g0 = 32 * r
        gs = slice(g0, g0 + 32)
        st_lay = lambda a: np.ascontiguousarray(a.reshape(16, 2, 64).transpose(1, 2, 0).reshape(128, 16))
        def row_lay_rep(a):
            t = a.reshape(4, 4, 2, 64)
            t = np.broadcast_to(t[:, :, :, None, :], (4, 4, 2, 16, 64))
            return np.ascontiguousarray(t.transpose(1, 2, 3, 0, 4).reshape(128, 4, 64))
        def row_lay_b(a):
            t = a.reshape(4, 4, 2, 64, 16)
            return np.ascontiguousarray(t.transpose(1, 2, 4, 0, 3).reshape(128, 4, 64))
        def st_lay_c(a):
            t = a.reshape(16, 2, 16, 64)
            return np.ascontiguousarray(t.transpose(1, 3, 0, 2).reshape(128, 16, 16))
        lst32 = np.broadcast_to(log_step[gs][:, None], (32, 64))
        sel = np.zeros((128, 4), np.float32)
        sel[:, r] = 1.0
        sre0 = np.stack([st_lay(state_ssm_re[0, 4 * b + i, gs]) for i in range(4)], 1)
        sim0 = np.stack([st_lay(state_ssm_im[0, 4 * b + i, gs]) for i in range(4)], 1)
        w_in_c_rolled = np.concatenate([w_in_c0[:, 512 * r:512 * (r + 1)], w_in_c0[:, 512:2048], w_in_c0[:, 2048:]], 1)
        m = {
            "xh": xh, "xs": xs,
            "cs": np.ascontiguousarray(cosv.reshape(25, 128, 64).transpose(1, 0, 2)),
            "sn": np.ascontiguousarray(sinv.reshape(25, 128, 64).transpose(1, 0, 2)),
            "valid": np.ascontiguousarray(valid.reshape(24, 128).T),
            "ck": np.ascontiguousarray(cache_win_k[0, c].reshape(2048, 1024)),
            "cv": np.ascontiguousarray(cache_win_v[0, c].reshape(2048, 1024)),
            "sconv": np.ascontiguousarray(state_conv[0, c].reshape(2, 8, 128).transpose(2, 1, 0)),
            "g_attn": bc(f(attn_norm)[0]), "g_ssm": bc(f(ssm_norm)[0]), "g_fin": bc(f(final_norm)),
            "w_in_ab": w_in_ab0, "cw": np.ascontiguousarray(f(conv_w)[0].reshape(3, 8, 128).transpose(2, 1, 0)),
            "w_out_ab": w_out_ab0, "w_in_c": np.ascontiguousarray(w_in_c_rolled),
            "w_glu": w_glu0, "w_out_c": w_out_c0,
            "bglu": np.ascontiguousarray(b_glu0.reshape(16, 128).T),
            "dsk": np.ascontiguousarray(d_skip0[512 * r:512 * (r + 1)].reshape(4, 128).T),
            "maskp": maskp, "masks": masks,
            "lre_s": st_lay(lam_re[gs]), "lim_s": st_lay(lam_im[gs]), "lst_s": st_lay(lst32),
            "lre_r": row_lay_rep(lam_re[gs]), "lim_r": row_lay_rep(lam_im[gs]), "lst_r": row_lay_rep(np.ascontiguousarray(lst32)),
            "bre_r": row_lay_b(b_re[gs]), "bim_r": row_lay_b(b_im[gs]),
            "cre_s": st_lay_c(c_re[gs]), "cim_s": st_lay_c(c_im[gs]),
            "rmask": rmask, "smask": smask, "sel": sel, "sre0": sre0, "sim0": sim0, "iota": iota,
        }
        in_maps.append({k: np.ascontiguousarray(v, dtype=np.float32) for k, v in m.items()})

    res = run_bass_kernel_spmd(nc, in_maps, core_ids=list(range(8)))
    R = res.results
    _NC_CACHE['raw'] = R
    y_prompt = np.zeros((2, SEQ, D), np.float32)
    y_sample = np.zeros((8, 8, D), np.float32)
    kp = np.zeros((1, 2, 2048, 8, 128), np.float32)
    vp = np.zeros((1, 2, 2048, 8, 128), np.float32)
    convp = np.zeros((1, 2, 2, 1024), np.float32)
    srp = np.zeros((1, 2, 128, 64), np.float32)
    sip = np.zeros((1, 2, 128, 64), np.float32)
    ks = np.zeros((1, 8, 8, 8, 128), np.float32)
    vs = np.zeros((1, 8, 8, 8, 128), np.float32)
    convs = np.zeros((1, 8, 2, 1024), np.float32)
    srs = np.zeros((1, 8, 128, 64), np.float32)
    sis = np.zeros((1, 8, 128, 64), np.float32)
    unst = lambda a: a.reshape(2, 64, 16).transpose(2, 0, 1).reshape(32, 64)
    for c in range(8):
        b, r = c // 4, c % 4
        o = R[c]
        y_prompt[b, r * NOWN:(r + 1) * NOWN] = o["yp"]
        y_sample[c] = o["ys"][:8]
        if r >= 2:
            kp[0, b, (r - 2) * NOWN:(r - 1) * NOWN] = o["kp"].reshape(NOWN, 8, 128)
            vp[0, b, (r - 2) * NOWN:(r - 1) * NOWN] = o["vp"].reshape(NOWN, 8, 128)
        if r == 3:
            convp[0, b] = o["convp"].transpose(2, 1, 0).reshape(2, 1024)
        ks[0, c] = o["ks"][:8].reshape(8, 8, 128)
        vs[0, c] = o["vs"][:8].reshape(8, 8, 128)
        convs[0, c] = o["convs"].transpose(2, 1, 0).reshape(2, 1024)
        srp[0, b, 32 * r:32 * (r + 1)] = unst(o["ssm_p"][:, 0, :])
        sip[0, b, 32 * r:32 * (r + 1)] = unst(o["ssm_p"][:, 1, :])
        for i in range(4):
            srs[0, 4 * b + i, 32 * r:32 * (r + 1)] = unst(o["ssm_s"][:, i, 0, :])
            sis[0, 4 * b + i, 32 * r:32 * (r + 1)] = unst(o["ssm_s"][:, i, 1, :])
    return (y_prompt, y_sample, kp, vp, convp, srp, sip, ks, vs, convs, srs, sis)
```

```python
import math
import os
STOP = os.environ.get('MK_STOP', '')
from contextlib import ExitStack

import numpy as np
import concourse.bass as bass
import concourse.mybir as mybir
from concourse.bass_utils import run_bass_kernel_spmd

F32 = mybir.dt.float32
BF16 = mybir.dt.bfloat16
ALU = mybir.AluOpType
AF = mybir.ActivationFunctionType
AX = mybir.AxisListType

ENGS = ["pe", "act", "dve", "pool", "sp"]
D = 2048
KT = 16
NOWN = 1024
NHALO = 2048
NTP = NOWN + NHALO
NTILE_P = NTP // 128
NO = NOWN + 128
SEQ = 4096
PAST = 16384
NCH = 1032
TWO_PI = 2.0 * math.pi


class Buf:
    def __init__(self, t, name):
        self.t = t
        self.name = name
        self.w = {}
        self.r = {}
        self.dsem = None
        self.dcnt = 0


class Prog:
    def __init__(self, nc, stack):
        self.nc = nc
        self.stack = stack
        self.q = {e: [] for e in ENGS}
        self.cnt = {e: 0 for e in ENGS}
        self.seen = {e: {} for e in ENGS}
        self.sems = {}
        self.semval = {}
        for e in ["pe", "act", "dve", "pool"]:
            self.sems[e] = stack.enter_context(nc.semaphore("s_" + e))
        self.off = 16512
        self.free = []
        self.dval = {}
        self.phase_bufs = []

    def sb(self, name, shape, dt, at=None):
        nbytes = int(np.prod(shape[1:])) * (2 if dt == BF16 else 4)
        if at is None:
            at = self.off
            self.off = (at + nbytes + 63) // 64 * 64
        assert at + nbytes <= 229300, (name, at, nbytes)
        t = self.nc.alloc_sbuf_tensor_at(name, list(shape), dt, offset=at)
        b = Buf(t, name)
        b.at = at
        b.nbytes = nbytes
        return b

    def ps(self, name, shape, dt=F32):
        t = self.stack.enter_context(self.nc.psum_tensor(name, list(shape), dt))
        return Buf(t, name)

    def dram(self, name, shape, dt, kind="Internal"):
        t = self.nc.dram_tensor(name, list(shape), dt, kind=kind)
        return Buf(t, name)

    def _need(self, eng, k, v, waits):
        if self.seen[eng].get(k, 0) >= v:
            return
        waits[k] = max(waits.get(k, 0), v)

    def _deps(self, eng, reads, writes):
        waits = {}
        for b in reads:
            for k, v in b.w.items():
                self._need(eng, k, v, waits)
        for b in writes:
            for k, v in b.w.items():
                self._need(eng, k, v, waits)
            for k, v in b.r.items():
                self._need(eng, k, v, waits)
        for k, v in waits.items():
            self.seen[eng][k] = v
        return [(self.sems[k], v) for k, v in waits.items()]

    def _commit(self, k, v, reads, writes):
        self.semval[k] = v
        for b in reads:
            b.r[k] = max(b.r.get(k, 0), v)
        for b in writes:
            b.w[k] = max(b.w.get(k, 0), v)
            b.r = {}

    def op(self, eng, fn, reads=(), writes=()):
        reads = [b for b in reads if b is not None]
        writes = [b for b in writes if b is not None]
        wl = self._deps(eng, reads, writes)
        self.cnt[eng] += 1
        sem = self.sems[eng]

        def emit(h, fn=fn, wl=wl, sem=sem):
            for s, v in wl:
                h.wait_ge(s, v)
            fn(h).then_inc(sem, 1)

        self.q[eng].append(emit)
        self._commit(eng, self.cnt[eng], reads, writes)

    def mm(self, fns, reads, writes):
        eng = "pe"
        wl = self._deps(eng, reads, writes)
        self.cnt[eng] += 1
        sem = self.sems[eng]

        def emit(h, fns=fns, wl=wl, sem=sem):
            for s, v in wl:
                h.wait_ge(s, v)
            for f in fns[:-1]:
                f(h)
            fns[-1](h).then_inc(sem, 1)

        self.q[eng].append(emit)
        self._commit(eng, self.cnt[eng], reads, writes)

    def dma(self, eng, out, in_, reads, writes, semb, **kw):
        reads = [b for b in reads if b is not None]
        writes = [b for b in writes if b is not None]
        if eng == "pool":
            if getattr(semb, "psem", None) is None:
                key = "q%d" % len(self.sems)
                self.sems[key] = self.stack.enter_context(self.nc.semaphore(key))
                semb.psem = key
                semb.pcnt = 0
            wl = self._deps(eng, reads, writes)
            semb.pcnt += 16
            sem = self.sems[semb.psem]

            def emit_p(h, wl=wl, sem=sem, out=out, in_=in_, kw=kw):
                for s, v in wl:
                    h.wait_ge(s, v)
                h.dma_start(out=out, in_=in_, **kw).then_inc(sem, 16)

            self.q[eng].append(emit_p)
            self._commit(semb.psem, semb.pcnt, reads, writes)
            return
        if semb.dsem is None:
            if self.free:
                key = self.free.pop()
            else:
                key = "d%d" % len(self.sems)
                self.sems[key] = self.stack.enter_context(self.nc.semaphore(key))
            semb.dsem = key
            semb.dcnt = self.dval.get(key, 0)
            self.phase_bufs.append(semb)
        wl = self._deps(eng, reads, writes)
        semb.dcnt += 16
        self.dval[semb.dsem] = semb.dcnt
        sem = self.sems[semb.dsem]

        def emit(h, wl=wl, sem=sem, out=out, in_=in_, kw=kw):
            for s, v in wl:
                h.wait_ge(s, v)
            h.dma_start(out=out, in_=in_, **kw).then_inc(sem, 16)

        self.q[eng].append(emit)
        self._commit(semb.dsem, semb.dcnt, reads, writes)

    def coll(self, src, dst, groups):
        key = "c%d" % len(self.sems)
        self.sems[key] = self.stack.enter_context(self.nc.semaphore(key))
        wl = self._deps("pool", [src], [dst])
        sem = self.sems[key]

        def emit(h, wl=wl, sem=sem):
            for s, v in wl:
                h.wait_ge(s, v)
            h.collective_compute("AllGather", ALU.bypass, replica_groups=groups,
                                 ins=[src.t.ap()], outs=[dst.t.ap()]).then_inc(sem)

        self.q["pool"].append(emit)
        self._commit(key, 1, [src], [dst])

    def barrier(self):
        for b in self.phase_bufs:
            self.free.append(b.dsem)
            b.dsem = None
        self.phase_bufs = []
        items = list(self.semval.items())
        for e in ENGS:
            wl = []
            for k, v in items:
                if self.seen[e].get(k, 0) < v:
                    self.seen[e][k] = v
                    wl.append((self.sems[k], v))

            def emit(h, wl=wl):
                for s, v in wl:
                    h.wait_ge(s, v)

            if wl:
                self.q[e].append(emit)

    def run(self):
        nc = self.nc
        with nc.Block() as block:
            @block.tensor
            def _(h):
                for f in self.q["pe"]:
                    f(h)

            @block.scalar
            def _(h):
                for f in self.q["act"]:
                    f(h)

            @block.vector
            def _(h):
                for f in self.q["dve"]:
                    f(h)

            @block.gpsimd
            def _(h):
                for f in self.q["pool"]:
                    f(h)

            @block.sync
            def _(h):
                for f in self.q["sp"]:
                    f(h)


def mult_of(d):
    d = np.asarray(d)
    m = ((d >= 0) & (d <= 128)).astype(np.float32)
    m += ((d >= 0) & (d <= 512) & (d % 4 == 0))
    m += ((d >= 0) & (d <= 2048) & (d % 16 == 0))
    return m.astype(np.float32)


IN_SPECS = [
    ("xh", [NTP, D]), ("xs", [128, D]), ("cs", [128, 25, 64]), ("sn", [128, 25, 64]),
    ("valid", [128, 24]), ("ck", [2048, 1024]), ("cv", [2048, 1024]), ("sconv", [128, 8, 2]),
    ("g_attn", [128, D]), ("g_ssm", [128, D]), ("g_fin", [128, D]),
    ("w_in_ab", [D, 8192]), ("cw", [128, 8, 3]), ("w_out_ab", [D, D]), ("w_in_c", [D, 4096]),
    ("w_glu", [D, D]), ("w_out_c", [D, D]), ("bglu", [128, 16]), ("dsk", [128, 4]),
    ("maskp", [128, 20, 512]), ("masks", [128, 17, 128]),
    ("lre_s", [128, 16]), ("lim_s", [128, 16]), ("lst_s", [128, 16]),
    ("lre_r", [128, 4, 64]), ("lim_r", [128, 4, 64]), ("lst_r", [128, 4, 64]),
    ("bre_r", [128, 4, 64]), ("bim_r", [128, 4, 64]),
    ("cre_s", [128, 16, 16]), ("cim_s", [128, 16, 16]),
    ("rmask", [128, 8]), ("smask", [128, 2]), ("sel", [128, 4]),
    ("sre0", [128, 4, 16]), ("sim0", [128, 4, 16]), ("iota", [128, 1024]),
]
OUT_SPECS = [
    ("yp", [NOWN, D]), ("ys", [128, D]), ("kp", [NOWN, 1024]), ("vp", [NOWN, 1024]),
    ("convp", [128, 8, 2]), ("ssm_p", [128, 2, 16]), ("ks", [128, 1024]), ("vs", [128, 1024]),
    ("convs", [128, 8, 2]), ("ssm_s", [128, 4, 2, 16]),
]


def build_nc():
    nc = bass.Bass("TRN2", target_bir_lowering=False)
    IN = {}
    for n, s in IN_SPECS:
        IN[n] = nc.dram_tensor(n, s, F32, kind="ExternalInput")
    OUT = {}
    for n, s in OUT_SPECS:
        OUT[n] = nc.dram_tensor(n, s, F32, kind="ExternalOutput")
    st = ExitStack()
    with st:
        P = Prog(nc, st)
        build_program(nc, P, IN, OUT)
        P.run()
    return nc


def build_program(nc, P, IN, OUT):
    GROUPS = [[0, 1, 2, 3], [4, 5, 6, 7]]
    outbufs = {n: Buf(OUT[n], n) for n in OUT}
    kT_scr = P.dram("kT_scr", [8, 128, NTP], BF16)
    v_scr = P.dram("v_scr", [NTP, 1024], BF16)
    kTs_scr = P.dram("kTs_scr", [8, 128, 2176], BF16)
    vs_scr = P.dram("vs_scr", [2176, 1024], BF16)
    qT_scr = P.dram("qT_scr", [8, 128, NO], BF16)
    h1_scr = P.dram("h1_scr", [NO, D], F32)
    HNP = [(0, 3), (3, 3), (6, 3), (9, 3), (12, 3), (15, 1)]
    hn_src = [P.dram("hn_src%d" % j, [nk * 128, NCH], BF16) for j, (k0, nk) in enumerate(HNP)]
    hn_dst = [P.dram("hn_dst%d" % j, [4 * nk * 128, NCH], BF16) for j, (k0, nk) in enumerate(HNP)]
    y_src = [P.dram("y_src%d" % j, [512, 516], BF16) for j in range(8)]
    y_dst = [P.dram("y_dst%d" % j, [4 * 512, 516], BF16) for j in range(8)]

    pf = [P.ps("pf%d" % i, [128, 512], F32) for i in range(6)]
    pb = [P.ps("pb%d" % i, [128, 8, 128], BF16) for i in range(2)]
    pfi = [0]
    pbi = [0]

    def next_pf():
        pfi[0] = (pfi[0] + 1) % 4
        return pf[pfi[0]]

    def next_pb():
        pbi[0] = (pbi[0] + 1) % 2
        return pb[pbi[0]]

    ident = P.sb("ident", [128, 128], BF16)
    P.op("pool", lambda h: h.memset(ident.t[:], 1.0), [], [ident])
    P.op("pool", lambda h: h.affine_select(ident.t[:], ident.t[:], [[-1, 128]], ALU.is_equal, 0.0,
                                            base=0, channel_multiplier=1), [ident], [ident])
    ones_bf = P.sb("ones_bf", [128, 128], BF16)
    P.op("pool", lambda h: h.memset(ones_bf.t[:], 1.0), [], [ones_bf])
    gt = P.sb("gt", [128, D], F32)
    cs = P.sb("cs", [128, 25, 64], F32)
    sn = P.sb("sn", [128, 25, 64], F32)
    valid = P.sb("valid", [128, 24], F32)
    validB = P.sb("validB", [128, 24, 128], BF16)
    cw = P.sb("cw", [128, 8, 3], F32)
    bglu = P.sb("bglu", [128, 16], F32)
    dsk = P.sb("dsk", [128, 4], F32)
    sel = P.sb("sel", [128, 4], F32)
    eps_t = P.sb("eps_t", [128, 1], F32)
    P.op("pool", lambda h: h.memset(eps_t.t[:], 1e-6), [], [eps_t])
    for b_, n in [(cs, "cs"), (sn, "sn"), (valid, "valid"), (cw, "cw"), (bglu, "bglu"), (dsk, "dsk"), (sel, "sel")]:
        P.dma("sp", b_.t[:], IN[n].ap(), [], [b_], b_)
    P.op("dve", lambda h: h.tensor_copy(validB.t[:], valid.t[:].unsqueeze(2).to_broadcast([128, 24, 128])),
         [valid], [validB])
    pospi = P.sb("pospi", [128, 1], F32)
    P.op("pool", lambda h: h.memset(pospi.t[:], math.pi), [], [pospi])
    CONST_END = P.off
    hnT_o = P.sb("hnT_o", [128, KT, NO], BF16)

    def load_g(name):
        P.dma("sp", gt.t[:], IN[name].ap(), [], [gt], gt)

    def rmsnorm_rows(xt, xn, ss, junk):
        P.op("act", lambda h: h.activation(junk.t[:], xt.t[:], AF.Square, accum_out=ss.t[:, 0:1]), [xt], [junk, ss])
        P.op("act", lambda h: h.activation(ss.t[:, 1:2], ss.t[:, 0:1], AF.Sqrt, bias=eps_t.t[:, 0:1], scale=1.0 / D), [ss, eps_t], [ss])
        P.op("dve", lambda h: h.reciprocal(ss.t[:, 2:3], ss.t[:, 1:2]), [ss], [ss])
        P.op("dve", lambda h: h.scalar_tensor_tensor(xn.t[:], xt.t[:], ss.t[:, 2:3], gt.t[:], ALU.mult, ALU.mult),
             [xt, ss, gt], [xn])

    def transpose_rows(xn, dst, dst_ap_fn):
        for half in range(2):
            p = next_pb()
            fns = []
            for j in range(8):
                kt = half * 8 + j
                fns.append(lambda h, p=p, j=j, kt=kt: h.transpose(p.t[:, j, :], xn.t[:, kt * 128:(kt + 1) * 128], ident.t[:]))
            P.mm(fns, [xn, ident], [p])
            P.op("act", lambda h, p=p, half=half: h.activation(dst_ap_fn(half), p.t[:], AF.Identity), [p], [dst])

    A0 = P.off
    wkv = P.sb("wkv", [128, KT, 2048], BF16)
    xts = [P.sb("xt%d" % i, [128, D], F32) for i in range(3)]
    xns = [P.sb("xn%d" % i, [128, D], BF16) for i in range(3)]
    hts = [P.sb("ht%d" % i, [128, KT, 128], BF16) for i in range(2)]
    kc = P.sb("kc", [128, 1024], BF16)
    vc = P.sb("vc", [128, 1024], BF16)
    ss = P.sb("ss", [128, 4], F32)
    krs = [P.sb("kr%d" % i, [128, 1024], F32) for i in range(2)]
    vfs = [P.sb("vf%d" % i, [128, 1024], F32) for i in range(2)]
    t1 = P.sb("t1", [128, 256], F32)
    t2 = P.sb("t2", [128, 256], F32)
    krbs = [P.sb("krb%d" % i, [128, 1024], BF16) for i in range(2)]
    vbs = [P.sb("vb%d" % i, [128, 1024], BF16) for i in range(2)]
    kTts = [P.sb("kTt%d" % i, [128, 8, 128], BF16) for i in range(2)]
    kTt = kTts[0]
    hprev2 = P.sb("hprev2", [128, KT, 2], BF16)
    A1_END = P.off

    load_g("g_attn")
    for half in range(2):
        P.dma("pool", wkv.t[:, :, half * 1024:(half + 1) * 1024],
              IN["w_in_ab"].ap()[:, 1024 + half * 1024:2048 + half * 1024].rearrange("(kt p) c -> p kt c", p=128),
              [], [wkv], wkv)

    def rotary(pk, ti, dst, c0):
        v = pk.t[:].rearrange("p (h two d) -> p h two d", h=4, two=2)
        o = dst.t[:, c0:c0 + 512].rearrange("p (h two d) -> p h two d", h=4, two=2)
        cb = cs.t[:, ti, :].unsqueeze(1).to_broadcast([128, 4, 64])
        sb_ = sn.t[:, ti, :].unsqueeze(1).to_broadcast([128, 4, 64])
        a = t1.t[:].rearrange("p (h d) -> p h d", h=4)
        b = t2.t[:].rearrange("p (h d) -> p h d", h=4)
        P.op("dve", lambda h: h.tensor_tensor(a, v[:, :, 0, :], cb, ALU.mult), [pk, cs], [t1])
        P.op("dve", lambda h: h.tensor_tensor(b, v[:, :, 1, :], sb_, ALU.mult), [pk, sn], [t2])
        P.op("dve", lambda h: h.tensor_tensor(o[:, :, 0, :], a, b, ALU.subtract), [t1, t2], [dst])
        P.op("dve", lambda h: h.tensor_tensor(a, v[:, :, 1, :], cb, ALU.mult), [pk, cs], [t1])
        P.op("dve", lambda h: h.tensor_tensor(b, v[:, :, 0, :], sb_, ALU.mult), [pk, sn], [t2])
        P.op("dve", lambda h: h.tensor_tensor(o[:, :, 1, :], a, b, ALU.add), [t1, t2], [dst])

    def store_kT(src_bf, scr, col0, kb=None):
        p = next_pb()
        if kb is None:
            kb = kTt
        fns = [(lambda h, p=p, j=j: h.transpose(p.t[:, j, :], src_bf.t[:, j * 128:(j + 1) * 128], ident.t[:])) for j in range(8)]
        P.mm(fns, [src_bf, ident], [p])
        P.op("act", lambda h, p=p, kb=kb: h.activation(kb.t[:], p.t[:], AF.Identity), [p], [kb])
        P.dma("sp", scr.t.ap()[:, :, col0:col0 + 128].rearrange("h d t -> d h t"), kb.t[:], [kb], [scr], kb)

    def stageL(ti):
        xt = xts[ti % 3]
        src = IN["xh"].ap()[ti * 128:(ti + 1) * 128, :] if ti < 24 else IN["xs"].ap()
        P.dma("sp", xt.t[:], src, [], [xt], xt)

    def stageN(ti):
        rmsnorm_rows(xts[ti % 3], xns[ti % 3], ss, xns[ti % 3])

    def stageA2(ti):
        xn = xns[ti % 3]
        if ti < 16:
            ht = hts[ti % 2]
            transpose_rows(xn, ht, lambda half, ht=ht: ht.t[:, half * 8:(half + 1) * 8, :])
            if ti == 15:
                P.op("dve", lambda h, ht=ht: h.tensor_copy(hprev2.t[:], ht.t[:, :, 126:128]), [ht], [hprev2])
            return (lambda kt, ht=ht: ht.t[:, kt, :]), ht
        o0 = (ti - 16) * 128
        transpose_rows(xn, hnT_o, lambda half, o0=o0: hnT_o.t[:, half * 8:(half + 1) * 8, o0:o0 + 128])
        return (lambda kt, o0=o0: hnT_o.t[:, kt, o0:o0 + 128]), hnT_o

    def stageM(ti, lhs, hb):
        kr = krs[ti % 2]
        vf = vfs[ti % 2]
        for g4 in range(4):
            pk = next_pf()
            fns = [(lambda h, pk=pk, kt=kt, g4=g4, lhs=lhs: h.matmul(pk.t[:], lhs(kt), wkv.t[:, kt, g4 * 512:(g4 + 1) * 512],
                                                                    start=(kt == 0), stop=(kt == KT - 1))) for kt in range(KT)]
            P.mm(fns, [hb, wkv], [pk])
            if g4 < 2:
                rotary(pk, ti, kr, g4 * 512)
            else:
                c0 = (g4 - 2) * 512
                P.op("act", lambda h, pk=pk, c0=c0, vf=vf: h.activation(vf.t[:, c0:c0 + 512], pk.t[:], AF.Identity), [pk], [vf])

    def stageKpre(ti):
        kr = krs[ti % 2]
        vf = vfs[ti % 2]
        krb = krbs[ti % 2]
        vb = vbs[ti % 2]
        P.op("act", lambda h: h.activation(krb.t[:], kr.t[:], AF.Identity), [kr], [krb])
        if ti < 24:
            P.op("dve", lambda h: h.tensor_scalar(vb.t[:], vf.t[:], valid.t[:, ti:ti + 1], None, ALU.mult), [vf, valid], [vb])
        else:
            P.op("dve", lambda h: h.tensor_copy(vb.t[:], vf.t[:]), [vf], [vb])

    def stageKpost(ti):
        kr = krs[ti % 2]
        vf = vfs[ti % 2]
        krb = krbs[ti % 2]
        vb = vbs[ti % 2]
        kb = kTts[ti % 2]
        if ti < 24:
            store_kT(krb, kT_scr, ti * 128, kb)
            P.dma("sp", v_scr.t.ap()[ti * 128:(ti + 1) * 128, :], vb.t[:], [vb], [v_scr], vb)
            if ti >= 16:
                r0 = (ti - 16) * 128
                P.dma("sp", OUT["kp"].ap()[r0:r0 + 128, :], kr.t[:], [kr], [outbufs["kp"]], kr)
                P.dma("sp", OUT["vp"].ap()[r0:r0 + 128, :], vf.t[:], [vf], [outbufs["vp"]], vf)
        else:
            store_kT(krb, kTs_scr, 2048, kb)
            P.dma("sp", vs_scr.t.ap()[2048:2176, :], vb.t[:], [vb], [vs_scr], vb)
            P.dma("sp", OUT["ks"].ap(), kr.t[:], [kr], [outbufs["ks"]], kr)
            P.dma("sp", OUT["vs"].ap(), vf.t[:], [vf], [outbufs["vs"]], vf)

    def cacheL(c):
        P.dma("pool", kc.t[:], IN["ck"].ap()[c * 128:(c + 1) * 128, :], [], [kc], kc)
        P.dma("pool", vc.t[:], IN["cv"].ap()[c * 128:(c + 1) * 128, :], [], [vc], vc)

    def cacheP(c):
        store_kT(kc, kTs_scr, c * 128, kTts[c % 2])
        P.dma("sp", vs_scr.t.ap()[c * 128:(c + 1) * 128, :], vc.t[:], [vc], [vs_scr], vc)

    for t_ in range(3):
        stageL(t_)
    stageN(0)
    stageN(1)
    infoA = {0: stageA2(0)}
    cacheL(0)
    cnext = 0
    for ti in range(25):
        if ti + 3 < 25:
            stageL(ti + 3)
        if ti + 2 < 25:
            stageN(ti + 2)
        if ti >= 1:
            stageKpre(ti - 1)
        if ti + 1 < 25:
            infoA[ti + 1] = stageA2(ti + 1)
        stageM(ti, *infoA[ti])
        if ti >= 1:
            stageKpost(ti - 1)
        if ti % 3 != 2 and cnext < 16:
            cacheP(cnext)
            cnext += 1
            if cnext < 16:
                cacheL(cnext)
    stageKpre(24)
    stageKpost(24)
    while cnext < 16:
        cacheP(cnext)
        cnext += 1
        if cnext < 16:
            cacheL(cnext)
    if STOP == 'A1':
        P.barrier()
        return
    P.barrier()
    P.off = A0
    wq = P.sb("wq", [128, KT, 1024], BF16)
    hprev2b = P.sb("hprev2b", [128, KT, 2], BF16)
    qf = P.sb("qf", [128, 1024], F32)
    qb = P.sb("qb", [128, 1024], BF16)
    t1 = P.sb("t1b", [128, 256], F32)
    t2 = P.sb("t2b", [128, 256], F32)
    kTt = P.sb("kTtb", [128, 8, 128], BF16)
    hprev2k = P.sb("hprev2k", [128, KT, 2], BF16, at=hprev2.at)
    hprev2k.w = dict(hprev2.w)
    P.dma("pool", wq.t[:], IN["w_in_ab"].ap()[:, 0:1024].rearrange("(kt p) c -> p kt c", p=128), [], [wq], wq)
    for tj in range(9):
        ti = 16 + tj
        o0 = tj * 128
        for g2_ in range(2):
            pk = next_pf()
            fns = [(lambda h, pk=pk, kt=kt, g2_=g2_, o0=o0: h.matmul(pk.t[:], hnT_o.t[:, kt, o0:o0 + 128],
                                                                      wq.t[:, kt, g2_ * 512:(g2_ + 1) * 512],
                                                                      start=(kt == 0), stop=(kt == KT - 1))) for kt in range(KT)]
            P.mm(fns, [hnT_o, wq], [pk])
            rotary(pk, ti, qf, g2_ * 512)
        P.op("act", lambda h: h.activation(qb.t[:], qf.t[:], AF.Identity), [qf], [qb])
        store_kT(qb, qT_scr, o0)

    if STOP == 'A2':
        P.barrier()
        return
    P.barrier()
    P.off = A0
    ocat = P.sb("ocat", [128, KT, NO], BF16)
    hp2 = P.sb("hp2", [128, KT, 2], BF16)
    B0 = P.off
    P.op("dve", lambda h: h.tensor_copy(hp2.t[:], hprev2k.t[:]), [hprev2k], [hp2])
    P.barrier()
    maskp = P.sb("maskp", [128, 13, 512], BF16)

    def midx(i):
        return i if i < 4 else (4 if i <= 11 else i - 7)

    masks_ = P.sb("masks_", [128, 17, 128], BF16)
    P.dma("pool", maskp.t[:, 0:5, :], IN["maskp"].ap()[:, 0:5, :], [], [maskp], maskp)
    P.dma("pool", maskp.t[:, 5:13, :], IN["maskp"].ap()[:, 12:20, :], [], [maskp], maskp)
    P.dma("pool", masks_.t[:], IN["masks"].ap(), [], [masks_], masks_)
    kTh = [P.sb("kTh%d" % i, [128, NTP], BF16) for i in range(2)]
    vh = [P.sb("vh%d" % i, [128, 24, 128], BF16) for i in range(2)]
    kTsh = [P.sb("kTsh%d" % i, [128, 2176], BF16) for i in range(1)] * 2
    vsh = [P.sb("vsh%d" % i, [128, 17, 128], BF16) for i in range(1)] * 2
    qTh = [P.sb("qTh%d" % i, [128, NO], BF16) for i in range(1)] * 2
    wt = [P.sb("wt%d" % i, [128, KT, 128], BF16) for i in range(4)]
    pts = [P.sb("pt%d" % i, [128, 512], BF16) for i in range(4)]
    ptm = [P.sb("ptm%d" % i, [128, 512], BF16) for i in range(4)]
    za = P.sb("za", [128, NO], F32)
    rl = P.sb("rl", [128, 512], F32)
    of = P.sb("of", [128, 512], F32)
    sg = P.sb("sg", [128, 512], F32)

    def silu_evac(pk, dstb, dst_ap, n):
        P.op("act", lambda h: h.activation(sg.t[:, 0:n], pk.t[:, 0:n], AF.Exp, scale=-1.0), [pk], [sg])
        P.op("dve", lambda h: h.tensor_scalar(sg.t[:, 0:n], sg.t[:, 0:n], 1.0, None, ALU.add), [sg], [sg])
        P.op("dve", lambda h: h.reciprocal(sg.t[:, 0:n], sg.t[:, 0:n]), [sg], [sg])
        P.op("dve", lambda h: h.tensor_tensor(dst_ap, pk.t[:, 0:n], sg.t[:, 0:n], ALU.mult), [pk, sg], [dstb])
    fb = [P.sb("fb%d" % i, [128, NO + 2], F32) for i in range(4)]
    convo_p = P.sb("convo_p", [128, 8, 2], F32)
    convo_s = P.sb("convo_s", [128, 8, 2], F32)
    sconv = P.sb("sconv", [128, 8, 2], F32)
    P.dma("sp", sconv.t[:], IN["sconv"].ap(), [], [sconv], sconv)
    scale = 128.0 ** -0.5

    def load_wt(i, c0):
        P.dma("pool", wt[i].t[:], IN["w_in_ab"].ap()[:, c0:c0 + 128].rearrange("(kt p) c -> p kt c", p=128), [], [wt[i]], wt[i])

    def proj_feat(wb, dst_ap_fn, evac, rhs_buf, rhs_fn, n):
        pk = next_pf()
        fns = [(lambda h, pk=pk, kt=kt: h.matmul(pk.t[:, 0:n], wb.t[:, kt, :], rhs_fn(kt), start=(kt == 0), stop=(kt == KT - 1)))
               for kt in range(KT)]
        P.mm(fns, [wb, rhs_buf], [pk])
        evac(pk)

    def attention(hh, qT, q0, nq, kT, vt, ktiles, mask_fn, vB_fn, o_dst_fn, zcol0):
        po = pf[4]
        pl = pf[5]
        nk = len(ktiles)
        LA = 3
        pms = {}

        def issue_S(i):
            kt_ = ktiles[i]
            ps_ = next_pf()
            P.mm([lambda h, ps_=ps_, kt_=kt_: h.matmul(ps_.t[:, 0:nq], kT.t[:, kt_ * 128:(kt_ + 1) * 128], qT.t[:, q0:q0 + nq],
                                                        start=True, stop=True)], [kT, qT], [ps_])
            pe_ = pts[i % 4]
            pm_ = ptm[i % 4]
            P.op("act", lambda h, ps_=ps_, pe_=pe_: h.activation(pe_.t[:, 0:nq], ps_.t[:, 0:nq], AF.Exp, scale=scale), [ps_], [pe_])
            mk, mb = mask_fn(i)
            eng = "dve"
            P.op(eng, lambda h, pe_=pe_, pm_=pm_, mk=mk: h.tensor_tensor(pm_.t[:, 0:nq], pe_.t[:, 0:nq], mk, ALU.mult), [pe_, mb], [pm_])
            pms[i] = pm_

        def issue_PV(i):
            kt_ = ktiles[i]
            pm_ = pms[i]
            vB, vBb = vB_fn(i)
            P.mm([lambda h, pm_=pm_, kt_=kt_, i=i: h.matmul(po.t[:, 0:nq], vt.t[:, kt_, :], pm_.t[:, 0:nq], start=(i == 0), stop=(i == nk - 1)),
                  lambda h, pm_=pm_, vB=vB, i=i: h.matmul(pl.t[:, 0:nq], vB, pm_.t[:, 0:nq], start=(i == 0), stop=(i == nk - 1))],
                 [vt, pm_, vBb], [po, pl])

        for i in range(min(LA, nk)):
            issue_S(i)
        for i in range(nk):
            if i + LA < nk:
                issue_S(i + LA)
            issue_PV(i)
        P.op("dve", lambda h: h.reciprocal(rl.t[:, 0:nq], pl.t[:, 0:nq]), [pl], [rl])
        P.op("dve", lambda h: h.tensor_tensor(of.t[:, 0:nq], po.t[:, 0:nq], rl.t[:, 0:nq], ALU.mult), [po, rl], [of])
        P.op("dve", lambda h: h.tensor_tensor(o_dst_fn(), of.t[:, 0:nq], za.t[:, zcol0:zcol0 + nq], ALU.mult), [of, za], [ocat])

    for hh in range(8):
        b2 = hh % 2
        P.dma("sp", kTh[b2].t[:], kT_scr.t.ap()[hh], [kT_scr], [kTh[b2]], kTh[b2])
        P.dma("sp", vh[b2].t[:], v_scr.t.ap()[:, hh * 128:(hh + 1) * 128].rearrange("(t p) d -> p t d", p=128), [v_scr], [vh[b2]], vh[b2])
        P.dma("sp", kTsh[b2].t[:], kTs_scr.t.ap()[hh], [kTs_scr], [kTsh[b2]], kTsh[b2])
        P.dma("sp", vsh[b2].t[:], vs_scr.t.ap()[:, hh * 128:(hh + 1) * 128].rearrange("(t p) d -> p t d", p=128), [vs_scr], [vsh[b2]], vsh[b2])
        P.dma("sp", qTh[b2].t[:], qT_scr.t.ap()[hh], [qT_scr], [qTh[b2]], qTh[b2])
        load_wt(0, 3072 + hh * 128)
        for (c0, n) in [(0, 512), (512, 512), (1024, 128)]:
            proj_feat(wt[0], None, lambda pk, c0=c0, n=n: silu_evac(pk, za, za.t[:, c0:c0 + n], n),
                      hnT_o, lambda kt, c0=c0, n=n: hnT_o.t[:, kt, c0:c0 + n], n)
        for qc in range(2):
            kts = list(range(4 * qc, 4 * qc + 20))
            attention(hh, qTh[b2], qc * 512, 512, kTh[b2], vh[b2], kts,
                      lambda i: (maskp.t[:, midx(i), :], maskp),
                      lambda i, kts=kts: (validB.t[:, kts[i], :], validB),
                      lambda qc=qc, hh=hh: ocat.t[:, hh, qc * 512:(qc + 1) * 512], qc * 512)
        attention(hh, qTh[b2], 1024, 128, kTsh[b2], vsh[b2], list(range(17)),
                  lambda i: (masks_.t[:, i, :], masks_),
                  lambda i: (ones_bf.t[:], ones_bf),
                  lambda hh=hh: ocat.t[:, hh, 1024:1152], 1024)

    for cc in range(8):
        for j, base in enumerate([4096, 5120, 6144, 7168]):
            load_wt(j, base + cc * 128)
        bb, cb_, hb_, zb = fb
        for j, dstb in enumerate(fb):
            for (c0, n) in [(0, 512), (512, 512), (1024, 128)]:
                if j == 3:
                    ev = lambda pk, c0=c0, n=n, dstb=dstb: silu_evac(pk, dstb, dstb.t[:, 2 + c0:2 + c0 + n], n)
                else:
                    ev = lambda pk, c0=c0, n=n, dstb=dstb: P.op("act", lambda h: h.activation(dstb.t[:, 2 + c0:2 + c0 + n], pk.t[:, 0:n], AF.Identity), [pk], [dstb])
                proj_feat(wt[j], None, ev, hnT_o, lambda kt, c0=c0, n=n: hnT_o.t[:, kt, c0:c0 + n], n)
            if j in (1, 2):
                proj_feat(wt[j], None, lambda pk, dstb=dstb: P.op("act", lambda h: h.activation(dstb.t[:, 0:2], pk.t[:, 0:2], AF.Identity), [pk], [dstb]),
                          hp2, lambda kt: hp2.t[:, kt, :], 2)
        P.op("dve", lambda h: h.tensor_tensor(cb_.t[:], cb_.t[:], hb_.t[:], ALU.mult), [cb_, hb_], [cb_])
        w0 = cw.t[:, cc, 0:1]
        w1 = cw.t[:, cc, 1:2]
        w2 = cw.t[:, cc, 2:3]
        P.op("dve", lambda h, w2=w2: h.tensor_scalar(hb_.t[:, 2:1026], cb_.t[:, 2:1026], w2, None, ALU.mult), [cb_, cw], [hb_])
        P.op("dve", lambda h, w1=w1: h.scalar_tensor_tensor(hb_.t[:, 2:1026], cb_.t[:, 1:1025], w1, hb_.t[:, 2:1026], ALU.mult, ALU.add), [cb_, cw, hb_], [hb_])
        P.op("dve", lambda h, w0=w0: h.scalar_tensor_tensor(hb_.t[:, 2:1026], cb_.t[:, 0:1024], w0, hb_.t[:, 2:1026], ALU.mult, ALU.add), [cb_, cw, hb_], [hb_])
        P.op("dve", lambda h, cc=cc: h.tensor_copy(convo_p.t[:, cc, :], cb_.t[:, 1024:1026]), [cb_], [convo_p])
        P.op("dve", lambda h, cc=cc: h.tensor_copy(cb_.t[:, 1024:1026], sconv.t[:, cc, :]), [sconv], [cb_])
        P.op("dve", lambda h, w2=w2: h.tensor_scalar(hb_.t[:, 1026:1034], cb_.t[:, 1026:1034], w2, None, ALU.mult), [cb_, cw], [hb_])
        P.op("dve", lambda h, w1=w1: h.scalar_tensor_tensor(hb_.t[:, 1026:1034], cb_.t[:, 1025:1033], w1, hb_.t[:, 1026:1034], ALU.mult, ALU.add), [cb_, cw, hb_], [hb_])
        P.op("dve", lambda h, w0=w0: h.scalar_tensor_tensor(hb_.t[:, 1026:1034], cb_.t[:, 1024:1032], w0, hb_.t[:, 1026:1034], ALU.mult, ALU.add), [cb_, cw, hb_], [hb_])
        P.op("dve", lambda h, cc=cc: h.tensor_copy(convo_s.t[:, cc, :], cb_.t[:, 1032:1034]), [cb_], [convo_s])
        P.op("dve", lambda h: h.tensor_tensor(hb_.t[:, 2:1034], hb_.t[:, 2:1034], bb.t[:, 2:1034], ALU.mult), [hb_, bb], [hb_])
        P.op("dve", lambda h, cc=cc: h.tensor_tensor(ocat.t[:, 8 + cc, 0:1032], hb_.t[:, 2:1034], zb.t[:, 2:1034], ALU.mult), [hb_, zb], [ocat])
        P.op("dve", lambda h, cc=cc: h.memset(ocat.t[:, 8 + cc, 1032:1152], 0.0), [], [ocat])
    P.dma("sp", OUT["convp"].ap(), convo_p.t[:], [convo_p], [outbufs["convp"]], convo_p)
    P.dma("sp", OUT["convs"].ap(), convo_s.t[:], [convo_s], [outbufs["convs"]], convo_s)

    if STOP == 'B':
        P.barrier()
        return
    P.barrier()
    hn1T = P.sb("hn1T", [128, KT, NCH], BF16, at=hnT_o.at)
    P.off = B0
    wgb = [P.sb("wg%d" % i, [128, KT, 512], BF16) for i in range(2)]
    h1s = [P.sb("h1s%d" % i, [128, D], F32) for i in range(5)]
    xn1 = [P.sb("xn1%d" % i, [128, D], BF16) for i in range(2)]
    junk = P.sb("junk2", [128, D], F32)
    ss = P.sb("ss2", [128, 4], F32)
    tmpT = P.sb("tmpT", [128, KT, 128], BF16)
    load_g("g_ssm")
    wi = 0
    for tiles in [list(range(0, 5)), list(range(5, 9))]:
        for si, tj in enumerate(tiles):
            o0 = tj * 128
            src = IN["xh"].ap()[NHALO + o0:NHALO + o0 + 128, :] if tj < 8 else IN["xs"].ap()
            P.dma("sp", h1s[si].t[:], src, [], [h1s[si]], h1s[si])
        for g4 in range(4):
            wg = wgb[wi % 2]
            wi += 1
            P.dma("pool", wg.t[:], IN["w_out_ab"].ap()[:, g4 * 512:(g4 + 1) * 512].rearrange("(kt p) c -> p kt c", p=128), [], [wg], wg)
            for si, tj in enumerate(tiles):
                o0 = tj * 128
                ht_ = h1s[si]
                pk = next_pf()
                fns = [(lambda h, pk=pk, kt=kt, o0=o0, wg=wg: h.matmul(pk.t[:], ocat.t[:, kt, o0:o0 + 128], wg.t[:, kt, :],
                                                                         start=(kt == 0), stop=(kt == KT - 1))) for kt in range(KT)]
                P.mm(fns, [ocat, wg], [pk])
                P.op("dve", lambda h, pk=pk, ht_=ht_, g4=g4: h.tensor_tensor(ht_.t[:, g4 * 512:(g4 + 1) * 512], ht_.t[:, g4 * 512:(g4 + 1) * 512], pk.t[:], ALU.add),
                     [pk, ht_], [ht_])
        for si, tj in enumerate(tiles):
            o0 = tj * 128
            ht_ = h1s[si]
            P.dma("sp", h1_scr.t.ap()[o0:o0 + 128, :], ht_.t[:], [ht_], [h1_scr], ht_)
            if STOP == 'C1':
                if tj < 8:
                    P.dma("sp", OUT["yp"].ap()[o0:o0 + 128, :], ht_.t[:], [ht_], [outbufs["yp"]], ht_)
                else:
                    P.dma("sp", OUT["ys"].ap(), ht_.t[:], [ht_], [outbufs["ys"]], ht_)
            xn = xn1[tj % 2]
            rmsnorm_rows(ht_, xn, ss, junk)
            if tj < 8:
                transpose_rows(xn, hn1T, lambda half, o0=o0: hn1T.t[:, half * 8:(half + 1) * 8, o0:o0 + 128])
            else:
                transpose_rows(xn, tmpT, lambda half: tmpT.t[:, half * 8:(half + 1) * 8, :])
                P.op("dve", lambda h: h.tensor_copy(hn1T.t[:, :, 1024:1032], tmpT.t[:, :, 0:8]), [tmpT], [hn1T])
    for j, (k0, nk) in enumerate(HNP):
        P.dma("sp", hn_src[j].t.ap().rearrange("(k p) t -> p k t", p=128), hn1T.t[:, k0:k0 + nk, :], [hn1T], [hn_src[j]], hn1T)
        P.coll(hn_src[j], hn_dst[j], GROUPS)

    if STOP == 'C1':
        P.barrier()
        return
    P.barrier()
    P.off = CONST_END
    uT = P.sb("uT", [128, 4, 4 * NCH], BF16)
    ygT = P.sb("ygT", [128, 4, 4 * NCH], BF16)
    L1 = P.off
    wu = P.sb("wu", [128, KT, 512], BF16)
    hch = [P.sb("hch%d" % i, [128, KT, 516], BF16) for i in range(2)]
    P.dma("pool", wu.t[:], IN["w_in_c"].ap()[:, 0:512].rearrange("(kt p) c -> p kt c", p=128), [], [wu], wu)
    ci = 0
    for r in range(4):
        for hf in range(2):
            hc = hch[ci % 2]
            ci += 1
            c0 = hf * 516
            for j, (k0, nk) in enumerate(HNP):
                P.dma("sp", hc.t[:, k0:k0 + nk, :], hn_dst[j].t.ap()[r * nk * 128:(r + 1) * nk * 128, c0:c0 + 516].rearrange("(k p) t -> p k t", p=128),
                      [hn_dst[j]], [hc], hc)
            for ft in range(4):
                pk = next_pf()
                fns = [(lambda h, pk=pk, kt=kt, ft=ft, hc=hc: h.matmul(pk.t[:, 0:512], wu.t[:, kt, ft * 128:(ft + 1) * 128], hc.t[:, kt, 0:512],
                                                                      start=(kt == 0), stop=(kt == KT - 1))) for kt in range(KT)]
                P.mm(fns, [wu, hc], [pk])
                P.op("act", lambda h, pk=pk, ft=ft, r=r, c0=c0: h.activation(uT.t[:, ft, r * NCH + c0:r * NCH + c0 + 512], pk.t[:, 0:512], AF.Identity), [pk], [uT])
                pk2 = next_pf()
                fns2 = [(lambda h, pk2=pk2, kt=kt, ft=ft, hc=hc: h.matmul(pk2.t[:, 0:4], wu.t[:, kt, ft * 128:(ft + 1) * 128], hc.t[:, kt, 512:516],
                                                                         start=(kt == 0), stop=(kt == KT - 1))) for kt in range(KT)]
                P.mm(fns2, [wu, hc], [pk2])
                P.op("act", lambda h, pk2=pk2, ft=ft, r=r, c0=c0: h.activation(uT.t[:, ft, r * NCH + c0 + 512:r * NCH + c0 + 516], pk2.t[:, 0:4], AF.Identity), [pk2], [uT])

    if STOP == 'U':
        P.barrier()
        return
    P.barrier()
    P.off = L1
    def small(name, shape, dt=F32):
        return P.sb(name, shape, dt)
    lre_s = small("lre_s", [128, 16]); lim_s = small("lim_s", [128, 16]); lst_s = small("lst_s", [128, 16])
    lre_r = small("lre_r", [128, 256]); lim_r = small("lim_r", [128, 256]); lst_r = small("lst_r", [128, 256])
    bre_r = small("bre_r", [128, 256]); bim_r = small("bim_r", [128, 256])
    cre_s = small("cre_s", [128, 16, 16]); cim_s = small("cim_s", [128, 16, 16])
    rmask = small("rmask", [128, 8]); smask = small("smask", [128, 2])
    sre0 = small("sre0", [128, 4, 16]); sim0 = small("sim0", [128, 4, 16])
    iota = small("iota", [128, 1024])
    for b_, n in [(lre_s, "lre_s"), (lim_s, "lim_s"), (lst_s, "lst_s"), (cre_s, "cre_s"), (cim_s, "cim_s"),
                  (rmask, "rmask"), (smask, "smask"), (sre0, "sre0"), (sim0, "sim0"), (iota, "iota")]:
        P.dma("sp", b_.t[:], IN[n].ap(), [], [b_], b_)
    for b_, n in [(lre_r, "lre_r"), (lim_r, "lim_r"), (lst_r, "lst_r"), (bre_r, "bre_r"), (bim_r, "bim_r")]:
        P.dma("sp", b_.t[:], IN[n].ap().rearrange("p a b -> p (a b)"), [], [b_], b_)
    negpi = small("negpi", [128, 1])
    P.op("dve", lambda h: h.memset(negpi.t[:], -math.pi), [], [negpi])

    I32 = mybir.dt.int32
    tq = small("tq", [128, 1024]); tiq = small("tiq", [128, 1024], I32)
    halfpi = small("halfpi", [128, 1]); zero_t = small("zero_t", [128, 1])
    P.op("dve", lambda h: h.memset(halfpi.t[:], 0.5 * math.pi), [], [halfpi])
    P.op("dve", lambda h: h.memset(zero_t.t[:], 0.0), [], [zero_t])

    def sincos(ang_ap, n, s_ap, c_ap, rd, wr):
        tf = tiq.t[:, 0:n].bitcast(F32)
        P.op("dve", lambda h: h.tensor_scalar(tq.t[:, 0:n], ang_ap, 1.0 / TWO_PI, None, ALU.mult), rd, [tq])
        P.op("dve", lambda h: h.tensor_copy(tiq.t[:, 0:n], tq.t[:, 0:n]), [tq], [tiq])
        P.op("dve", lambda h: h.tensor_copy(tq.t[:, 0:n], tiq.t[:, 0:n]), [tiq], [tq])
        P.op("dve", lambda h: h.scalar_tensor_tensor(tq.t[:, 0:n], tq.t[:, 0:n], -TWO_PI, ang_ap, ALU.mult, ALU.add), [tq] + rd, [tq])
        P.op("dve", lambda h: h.tensor_scalar(tq.t[:, 0:n], tq.t[:, 0:n], -math.pi, math.pi, ALU.max, ALU.min), [tq], [tq])
        P.op("act", lambda h: h.activation(s_ap, tq.t[:, 0:n], AF.Sin, bias=zero_t.t[:, 0:1], scale=1.0), [tq, zero_t], wr)
        P.op("dve", lambda h: h.scalar_tensor_tensor(tf, tq.t[:, 0:n], -1.0, tq.t[:, 0:n], ALU.mult, ALU.max), [tq], [tiq])
        P.op("act", lambda h: h.activation(c_ap, tf, AF.Sin, bias=halfpi.t[:, 0:1], scale=-1.0), [tiq, halfpi], wr)

    def disc(lre, lim, lst, n, pref):
        o = {}
        for nm in ["step", "mag", "th", "c", "s", "tmp", "nr", "den", "cr", "ci", "a", "b"]:
            o[nm] = small(pref + nm, [128, n])
        al = [o[k] for k in o] + [lre, lim, lst]
        P.op("act", lambda h: h.activation(o["step"].t[:], lst.t[:, 0:n], AF.Exp), al, al)
        P.op("dve", lambda h: h.tensor_tensor(o["th"].t[:], lim.t[:, 0:n], o["step"].t[:], ALU.mult), al, al)
        P.op("dve", lambda h: h.tensor_tensor(o["a"].t[:], lre.t[:, 0:n], o["step"].t[:], ALU.mult), al, al)
        P.op("act", lambda h: h.activation(o["mag"].t[:], o["a"].t[:], AF.Exp), al, al)
        sincos(o["th"].t[:], n, o["s"].t[:], o["c"].t[:], al, al)
        P.op("dve", lambda h: h.tensor_tensor(o["a"].t[:], o["mag"].t[:], o["c"].t[:], ALU.mult), al, al)
        P.op("dve", lambda h: h.tensor_scalar(o["nr"].t[:], o["a"].t[:], 1.0, -1.0, ALU.mult, ALU.add), al, al)
        P.op("dve", lambda h: h.tensor_tensor(o["b"].t[:], o["mag"].t[:], o["s"].t[:], ALU.mult), al, al)
        P.op("dve", lambda h: h.tensor_tensor(o["den"].t[:], lre.t[:, 0:n], lre.t[:, 0:n], ALU.mult), al, al)
        P.op("dve", lambda h: h.tensor_tensor(o["tmp"].t[:], lim.t[:, 0:n], lim.t[:, 0:n], ALU.mult), al, al)
        P.op("dve", lambda h: h.tensor_tensor(o["den"].t[:], o["den"].t[:], o["tmp"].t[:], ALU.add), al, al)
        P.op("dve", lambda h: h.reciprocal(o["den"].t[:], o["den"].t[:]), al, al)
        P.op("dve", lambda h: h.tensor_tensor(o["cr"].t[:], o["nr"].t[:], lre.t[:, 0:n], ALU.mult), al, al)
        P.op("dve", lambda h: h.tensor_tensor(o["tmp"].t[:], o["b"].t[:], lim.t[:, 0:n], ALU.mult), al, al)
        P.op("dve", lambda h: h.tensor_tensor(o["cr"].t[:], o["cr"].t[:], o["tmp"].t[:], ALU.add), al, al)
        P.op("dve", lambda h: h.tensor_tensor(o["cr"].t[:], o["cr"].t[:], o["den"].t[:], ALU.mult), al, al)
        P.op("dve", lambda h: h.tensor_tensor(o["ci"].t[:], o["b"].t[:], lre.t[:, 0:n], ALU.mult), al, al)
        P.op("dve", lambda h: h.tensor_tensor(o["tmp"].t[:], o["nr"].t[:], lim.t[:, 0:n], ALU.mult), al, al)
        P.op("dve", lambda h: h.tensor_tensor(o["ci"].t[:], o["ci"].t[:], o["tmp"].t[:], ALU.subtract), al, al)
        P.op("dve", lambda h: h.tensor_tensor(o["ci"].t[:], o["ci"].t[:], o["den"].t[:], ALU.mult), al, al)
        return o, al

    ds_, als = disc(lre_s, lim_s, lst_s, 16, "ds_")
    dr_, alr = disc(lre_r, lim_r, lst_r, 256, "dr_")
    bbr = small("bbr", [128, 256]); bbi = small("bbi", [128, 256]); tmpr = small("tmpr", [128, 256])
    alr2 = alr + [bbr, bbi, tmpr, bre_r, bim_r]
    P.op("dve", lambda h: h.tensor_tensor(bbr.t[:], dr_["cr"].t[:], bre_r.t[:], ALU.mult), alr2, alr2)
    P.op("dve", lambda h: h.tensor_tensor(tmpr.t[:], dr_["ci"].t[:], bim_r.t[:], ALU.mult), alr2, alr2)
    P.op("dve", lambda h: h.tensor_tensor(bbr.t[:], bbr.t[:], tmpr.t[:], ALU.subtract), alr2, alr2)
    P.op("dve", lambda h: h.tensor_tensor(bbi.t[:], dr_["cr"].t[:], bim_r.t[:], ALU.mult), alr2, alr2)
    P.op("dve", lambda h: h.tensor_tensor(tmpr.t[:], dr_["ci"].t[:], bre_r.t[:], ALU.mult), alr2, alr2)
    P.op("dve", lambda h: h.tensor_tensor(bbi.t[:], bbi.t[:], tmpr.t[:], ALU.add), alr2, alr2)
    BbT = [small("BbT%d" % ri, [128, 16, 128], BF16) for ri in range(2)]
    for ri, src in enumerate([bbr, bbi]):
        for qq in range(4):
            for g2 in range(2):
                m = rmask.t[:, qq * 2 + g2:qq * 2 + g2 + 1]
                o_ap = BbT[ri].t[:].rearrange("p (ft q) c -> p ft q c", q=4)[:, :, qq, g2 * 64:(g2 + 1) * 64]
                i_ap = src.t[:].rearrange("p (ft d) -> p ft d", ft=4)
                P.op("dve", lambda h, o_ap=o_ap, i_ap=i_ap, m=m: h.tensor_scalar(o_ap, i_ap, m, None, ALU.mult), alr2 + [rmask], [BbT[ri]])
    CT = [small("CT%d" % ri, [128, 16, 128], BF16) for ri in range(2)]
    for ri in range(2):
        P.op("dve", lambda h, ri=ri: h.memset(CT[ri].t[:], 0.0), [], [CT[ri]])
    for ri, (src, sgn) in enumerate([(cre_s, 1.0), (cim_s, -1.0)]):
        for pair in range(16):
            qq = pair % 4
            for g2 in range(2):
                m = smask.t[:, g2:g2 + 1]
                col = qq * 32 + g2 * 16
                P.op("dve", lambda h, ri=ri, pair=pair, col=col, m=m, src=src, sgn=sgn: h.tensor_scalar(
                    CT[ri].t[:, pair, col:col + 16], src.t[:, pair, :], m, sgn, ALU.mult, ALU.mult), [src, smask], [CT[ri]])
    rr = ds_["mag"]; th = ds_["th"]
    cth = small("cth", [128, 16]); sth = small("sth", [128, 16])
    P.op("dve", lambda h: h.tensor_copy(cth.t[:], ds_["c"].t[:]), als, [cth])
    P.op("dve", lambda h: h.tensor_copy(sth.t[:], ds_["s"].t[:]), als, [sth])

    tabc = [small("tabc%d" % i, [128, 1024]) for i in range(4)]
    tabs = [small("tabs%d" % i, [128, 1024]) for i in range(4)]
    WS = [dict(gr=small("gr0", [128, 1024]), gi=small("gi0", [128, 1024]), yr=small("yr0", [128, 1024]), yi=small("yi0", [128, 1024]),
               hrb=small("hrb0", [128, 1024], BF16), hib=small("hib0", [128, 1024], BF16))]
    hib1 = small("hib1", [128, 1024], BF16)
    ytmp = small("ytmp", [128, 512]); ysq = small("ysq", [128, 512])
    stp_l = [small("stp%d" % p_, [128, 2]) for p_ in range(16)]
    sts_p = [small("stsp%d" % p_, [128, 4, 2]) for p_ in range(16)]
    m32 = small("m32", [128, 32])
    P.op("dve", lambda h: h.memset(m32.t[:], 1.0), [], [m32])
    P.op("dve", lambda h: h.memset(m32.t[:].rearrange("p (r c) -> p r c", r=4)[:, :, 0], 0.0), [m32], [m32])
    for w_ in WS:
        w_["gin"] = small("gin0", [128, 2]); w_["hend"] = small("hend0", [128, 2])
        w_["gin4"] = small("gin40", [128, 4, 2]); w_["d0"] = small("d00", [128, 32])
    for p_ in range(16):
        P.op("dve", lambda h, p_=p_: h.memset(stp_l[p_].t[:], 0.0), [], [stp_l[p_]])
    P.barrier()
    blkA = lre_r.at
    blkB = dr_["step"].at
    WS.append(dict(gr=P.sb("gr1", [128, 1024], F32, at=blkB), gi=P.sb("gi1", [128, 1024], F32, at=blkB + 4096),
                   yr=P.sb("yr1", [128, 1024], F32, at=blkB + 8192), yi=P.sb("yi1", [128, 1024], F32, at=blkA),
                   hrb=P.sb("hrb1", [128, 1024], BF16, at=blkB + 12288), hib=hib1,
                   gin=small("gin1", [128, 2]), hend=small("hend1", [128, 2]),
                   gin4=small("gin41", [128, 4, 2]), d0=small("d01", [128, 32])))
    ang = WS[0]["gr"]
    GC = 1.5957691216057308
    kcount = [0]

    def stageX(pair, qq, ft, col0, T, tc_, ts_, init_re, init_im, init_bufs, out_b):
        w = WS[kcount[0] % 2]
        kcount[0] += 1
        gr, gi, yr, yi, hrb, hib, gin, hend = w["gr"], w["gi"], w["yr"], w["yi"], w["hrb"], w["hib"], w["gin"], w["hend"]
        for h0 in range(0, T, 512):
            n = min(512, T - h0)
            pxr = next_pf(); pxi = next_pf()
            P.mm([lambda h, pxr=pxr, n=n, h0=h0: h.matmul(pxr.t[:, 0:n], BbT[0].t[:, pair, :], uT.t[:, ft, col0 + h0:col0 + h0 + n], start=True, stop=True)], [BbT[0], uT], [pxr])
            P.mm([lambda h, pxi=pxi, n=n, h0=h0: h.matmul(pxi.t[:, 0:n], BbT[1].t[:, pair, :], uT.t[:, ft, col0 + h0:col0 + h0 + n], start=True, stop=True)], [BbT[1], uT], [pxi])
            c_ = tc_.t[:, h0:h0 + n]; s_ = ts_.t[:, h0:h0 + n]
            P.op("dve", lambda h, pxr=pxr, c_=c_, n=n, h0=h0: h.tensor_tensor(yr.t[:, h0:h0 + n], pxr.t[:, 0:n], c_, ALU.mult), [pxr, tc_], [yr])
            P.op("dve", lambda h, pxi=pxi, s_=s_, n=n, h0=h0: h.tensor_tensor(gr.t[:, h0:h0 + n], pxi.t[:, 0:n], s_, ALU.mult), [pxi, ts_], [gr])
            P.op("dve", lambda h, n=n, h0=h0: h.tensor_tensor(yr.t[:, h0:h0 + n], yr.t[:, h0:h0 + n], gr.t[:, h0:h0 + n], ALU.add), [yr, gr], [yr])
            P.op("dve", lambda h, pxi=pxi, c_=c_, n=n, h0=h0: h.tensor_tensor(yi.t[:, h0:h0 + n], pxi.t[:, 0:n], c_, ALU.mult), [pxi, tc_], [yi])
            P.op("dve", lambda h, pxr=pxr, s_=s_, n=n, h0=h0: h.tensor_tensor(gi.t[:, h0:h0 + n], pxr.t[:, 0:n], s_, ALU.mult), [pxr, ts_], [gi])
            P.op("dve", lambda h, n=n, h0=h0: h.tensor_tensor(yi.t[:, h0:h0 + n], yi.t[:, h0:h0 + n], gi.t[:, h0:h0 + n], ALU.subtract), [yi, gi], [yi])
        ct = cth.t[:, pair:pair + 1]; st_ = sth.t[:, pair:pair + 1]
        P.op("dve", lambda h: h.tensor_scalar(gin.t[:, 0:1], init_re, ct, None, ALU.mult), init_bufs + [cth], [gin])
        P.op("dve", lambda h: h.scalar_tensor_tensor(gin.t[:, 0:1], init_im, st_, gin.t[:, 0:1], ALU.mult, ALU.subtract), init_bufs + [sth, gin], [gin])
        P.op("dve", lambda h: h.tensor_scalar(gin.t[:, 0:1], gin.t[:, 0:1], -1.0, None, ALU.mult), [gin], [gin])
        P.op("dve", lambda h: h.tensor_scalar(gin.t[:, 1:2], init_re, st_, None, ALU.mult), init_bufs + [sth], [gin])
        P.op("dve", lambda h: h.scalar_tensor_tensor(gin.t[:, 1:2], init_im, ct, gin.t[:, 1:2], ALU.mult, ALU.add), init_bufs + [cth, gin], [gin])
        rb = rr.t[:, pair:pair + 1].to_broadcast([128, T])
        P.op("dve", lambda h: h.tensor_tensor_scan(gr.t[:, 0:T], rb, yr.t[:, 0:T], gin.t[:, 0:1], ALU.mult, ALU.add), [yr, gin] + als, [gr])
        P.op("dve", lambda h: h.tensor_tensor_scan(gi.t[:, 0:T], rb, yi.t[:, 0:T], gin.t[:, 1:2], ALU.mult, ALU.add), [yi, gin] + als, [gi])
        c_ = tc_.t[:, 0:T]; s_ = ts_.t[:, 0:T]
        yrb = yr.t[:].bitcast(BF16)
        yib = yi.t[:].bitcast(BF16)
        p1 = yrb[:, 0:T]; p2 = yrb[:, 1024:1024 + T]; p3 = yib[:, 0:T]; p4 = yib[:, 1024:1024 + T]
        P.op("dve", lambda h: h.tensor_tensor(p1, gr.t[:, 0:T], c_, ALU.mult), [gr, tc_], [yr])
        P.op("dve", lambda h: h.scalar_tensor_tensor(p2, gi.t[:, 0:T], -1.0, s_, ALU.mult, ALU.mult), [gi, ts_], [yr])
        P.op("dve", lambda h: h.tensor_tensor(p3, gr.t[:, 0:T], s_, ALU.mult), [gr, ts_], [yi])
        P.op("dve", lambda h: h.tensor_tensor(p4, gi.t[:, 0:T], c_, ALU.mult), [gi, tc_], [yi])
        cl = tc_.t[:, T - 1:T]; sl = ts_.t[:, T - 1:T]
        P.op("dve", lambda h: h.tensor_tensor(gin.t[:, 0:1], gi.t[:, T - 1:T], sl, ALU.mult), [gi, ts_, gin], [gin])
        P.op("dve", lambda h: h.tensor_tensor(gin.t[:, 1:2], gi.t[:, T - 1:T], cl, ALU.mult), [gi, tc_, gin], [gin])
        P.op("dve", lambda h: h.scalar_tensor_tensor(out_b.t[:, 0:1], gr.t[:, T - 1:T], cl, gin.t[:, 0:1], ALU.mult, ALU.subtract), [gr, tc_, gin], [out_b])
        P.op("dve", lambda h: h.scalar_tensor_tensor(out_b.t[:, 1:2], gr.t[:, T - 1:T], sl, gin.t[:, 1:2], ALU.mult, ALU.add), [gr, ts_, gin], [out_b])
        return dict(pair=pair, qq=qq, T=T, prods=[(0, lambda h0, n: yrb[:, h0:h0 + n], yr), (0, lambda h0, n: yrb[:, 1024 + h0:1024 + h0 + n], yr),
                                                  (1, lambda h0, n: yib[:, h0:h0 + n], yi), (1, lambda h0, n: yib[:, 1024 + h0:1024 + h0 + n], yi)], after=[])

    def v48(ap):
        return ap.rearrange("p (r c) -> p r c", r=4)

    def stageXs(pair, qq, ft, tc_, ts_):
        w = WS[kcount[0] % 2]
        kcount[0] += 1
        gr, gi, yr, yi, hrb, hib, gin4, d0 = w["gr"], w["gi"], w["yr"], w["yi"], w["hrb"], w["hib"], w["gin4"], w["d0"]
        ucols = v48(uT.t[:, ft, :])[:, :, 1024:1032]
        pxr = next_pf(); pxi = next_pf()
        P.mm([lambda h: h.matmul(pxr.t[:, 0:32], BbT[0].t[:, pair, :], ucols, start=True, stop=True)], [BbT[0], uT], [pxr])
        P.mm([lambda h: h.matmul(pxi.t[:, 0:32], BbT[1].t[:, pair, :], ucols, start=True, stop=True)], [BbT[1], uT], [pxi])
        cb = tc_.t[:, 0:8].unsqueeze(1).to_broadcast([128, 4, 8])
        sb_ = ts_.t[:, 0:8].unsqueeze(1).to_broadcast([128, 4, 8])
        xr = v48(pxr.t[:, 0:32]); xi = v48(pxi.t[:, 0:32])
        yrv = v48(yr.t[:, 0:32]); yiv = v48(yi.t[:, 0:32]); grv = v48(gr.t[:, 0:32]); giv = v48(gi.t[:, 0:32])
        P.op("dve", lambda h: h.tensor_tensor(yrv, xr, cb, ALU.mult), [pxr, tc_], [yr])
        P.op("dve", lambda h: h.tensor_tensor(grv, xi, sb_, ALU.mult), [pxi, ts_], [gr])
        P.op("dve", lambda h: h.tensor_tensor(yrv, yrv, grv, ALU.add), [yr, gr], [yr])
        P.op("dve", lambda h: h.tensor_tensor(yiv, xi, cb, ALU.mult), [pxi, tc_], [yi])
        P.op("dve", lambda h: h.tensor_tensor(giv, xr, sb_, ALU.mult), [pxr, ts_], [gi])
        P.op("dve", lambda h: h.tensor_tensor(yiv, yiv, giv, ALU.subtract), [yi, gi], [yi])
        ct = cth.t[:, pair:pair + 1]; st_ = sth.t[:, pair:pair + 1]; rs = rr.t[:, pair:pair + 1]
        ire = sre0.t[:, :, pair]; iim = sim0.t[:, :, pair]
        g_re = gin4.t[:, :, 0]; g_im = gin4.t[:, :, 1]
        P.op("dve", lambda h: h.tensor_scalar(g_re, ire, ct, None, ALU.mult), [sre0, cth], [gin4])
        P.op("dve", lambda h: h.scalar_tensor_tensor(g_re, iim, st_, g_re, ALU.mult, ALU.subtract), [sim0, sth, gin4], [gin4])
        P.op("dve", lambda h: h.tensor_scalar(g_re, g_re, -1.0, None, ALU.mult), [gin4], [gin4])
        P.op("dve", lambda h: h.tensor_scalar(g_im, ire, st_, None, ALU.mult), [sre0, sth, gin4], [gin4])
        P.op("dve", lambda h: h.scalar_tensor_tensor(g_im, iim, ct, g_im, ALU.mult, ALU.add), [sim0, cth, gin4], [gin4])
        P.op("dve", lambda h: h.scalar_tensor_tensor(yrv[:, :, 0], g_re, rs, yrv[:, :, 0], ALU.mult, ALU.add), [gin4, yr] + als, [yr])
        P.op("dve", lambda h: h.scalar_tensor_tensor(yiv[:, :, 0], g_im, rs, yiv[:, :, 0], ALU.mult, ALU.add), [gin4, yi] + als, [yi])
        P.op("dve", lambda h: h.tensor_scalar(d0.t[:], m32.t[:], rs, None, ALU.mult), [m32] + als, [d0])
        P.op("dve", lambda h: h.tensor_tensor_scan(gr.t[:, 0:32], d0.t[:], yr.t[:, 0:32], 0.0, ALU.mult, ALU.add), [yr, d0], [gr])
        P.op("dve", lambda h: h.tensor_tensor_scan(gi.t[:, 0:32], d0.t[:], yi.t[:, 0:32], 0.0, ALU.mult, ALU.add), [yi, d0], [gi])
        hrv = v48(hrb.t[:, 0:32]); hiv = v48(hib.t[:, 0:32])
        out_b = sts_p[pair]
        P.op("dve", lambda h: h.tensor_tensor(yrv, grv, cb, ALU.mult), [gr, tc_], [yr])
        P.op("dve", lambda h: h.tensor_tensor(yiv, giv, sb_, ALU.mult), [gi, ts_], [yi])
        P.op("dve", lambda h: h.tensor_tensor(hrv, yrv, yiv, ALU.subtract), [yr, yi], [hrb])
        P.op("dve", lambda h: h.tensor_tensor(out_b.t[:, :, 0], yrv[:, :, 7], yiv[:, :, 7], ALU.subtract), [yr, yi], [out_b])
        P.op("dve", lambda h: h.tensor_tensor(yrv, grv, sb_, ALU.mult), [gr, ts_, hrb, out_b], [yr])
        P.op("dve", lambda h: h.tensor_tensor(yiv, giv, cb, ALU.mult), [gi, tc_, hrb, out_b], [yi])
        P.op("dve", lambda h: h.tensor_tensor(hiv, yrv, yiv, ALU.add), [yr, yi], [hib])
        P.op("dve", lambda h: h.tensor_tensor(out_b.t[:, :, 1], yrv[:, :, 7], yiv[:, :, 7], ALU.add), [yr, yi], [out_b])
        return dict(pair=pair, qq=qq, T=32, hrb=hrb, hib=hib, after=[])

    def y_evac_s(ft):
        ucols = v48(uT.t[:, ft, :])[:, :, 1024:1032]
        dcols = v48(ygT.t[:, ft, :])[:, :, 1024:1032]
        yp_ = ypb[0]
        ypv = v48(yp_.t[:, 0:32]); yt = v48(ytmp.t[:, 0:32]); yq = v48(ysq.t[:, 0:32])
        P.op("dve", lambda h: h.scalar_tensor_tensor(yt, ucols, dsk.t[:, ft:ft + 1], ypv, ALU.mult, ALU.add), [uT, dsk, yp_], [ytmp])
        P.op("dve", lambda h: h.tensor_tensor(yq, yt, yt, ALU.mult), [ytmp], [ysq])
        P.op("dve", lambda h: h.tensor_scalar(yq, yq, 0.044715, 1.0, ALU.mult, ALU.add), [ysq], [ysq])
        P.op("dve", lambda h: h.tensor_tensor(yq, yq, yt, ALU.mult), [ysq, ytmp], [ysq])
        P.op("act", lambda h: h.activation(ysq.t[:, 0:32], ysq.t[:, 0:32], AF.Sigmoid, scale=GC), [ysq], [ysq])
        P.op("dve", lambda h: h.tensor_tensor(dcols, yq, yt, ALU.mult), [ysq, ytmp], [ygT])

    ypb = [pf[4], pf[5]]

    def stageY(c):
        if "prods" in c:
            pair, qq, T = c["pair"], c["qq"], c["T"]
            for h0 in range(0, T, 512):
                n = min(512, T - h0)
                yp_ = ypb[h0 // 512]
                fns = []
                for k_, (ci_, rf_, _b) in enumerate(c["prods"]):
                    fns.append(lambda h, yp_=yp_, n=n, h0=h0, ci_=ci_, rf_=rf_, k_=k_: h.matmul(
                        yp_.t[:, 0:n], CT[ci_].t[:, pair, :], rf_(h0, n), start=(qq == 0 and k_ == 0), stop=(qq == 3 and k_ == 3)))
                P.mm(fns, [CT[0], CT[1], c["prods"][0][2], c["prods"][2][2]], [yp_])
            for f in c["after"]:
                f()
            return
        pair, qq, T, hrb, hib = c["pair"], c["qq"], c["T"], c["hrb"], c["hib"]
        for h0 in range(0, T, 512):
            n = min(512, T - h0)
            yp_ = ypb[h0 // 512]
            P.mm([lambda h, yp_=yp_, n=n, h0=h0: h.matmul(yp_.t[:, 0:n], CT[0].t[:, pair, :], hrb.t[:, h0:h0 + n], start=(qq == 0), stop=False),
                  lambda h, yp_=yp_, n=n, h0=h0: h.matmul(yp_.t[:, 0:n], CT[1].t[:, pair, :], hib.t[:, h0:h0 + n], start=False, stop=(qq == 3))],
                 [CT[0], CT[1], hrb, hib], [yp_])
        for f in c["after"]:
            f()

    def y_evac(yp_, ft, col0, n):
        P.op("dve", lambda h: h.scalar_tensor_tensor(ytmp.t[:, 0:n], uT.t[:, ft, col0:col0 + n], dsk.t[:, ft:ft + 1], yp_.t[:, 0:n], ALU.mult, ALU.add),
             [uT, dsk, yp_], [ytmp])
        P.op("dve", lambda h: h.tensor_tensor(ysq.t[:, 0:n], ytmp.t[:, 0:n], ytmp.t[:, 0:n], ALU.mult), [ytmp], [ysq])
        P.op("dve", lambda h: h.tensor_scalar(ysq.t[:, 0:n], ysq.t[:, 0:n], 0.044715, 1.0, ALU.mult, ALU.add), [ysq], [ysq])
        P.op("dve", lambda h: h.tensor_tensor(ysq.t[:, 0:n], ysq.t[:, 0:n], ytmp.t[:, 0:n], ALU.mult), [ysq, ytmp], [ysq])
        P.op("act", lambda h: h.activation(ysq.t[:, 0:n], ysq.t[:, 0:n], AF.Sigmoid, scale=GC), [ysq], [ysq])
        P.op("dve", lambda h: h.tensor_tensor(ygT.t[:, ft, col0:col0 + n], ysq.t[:, 0:n], ytmp.t[:, 0:n], ALU.mult), [ysq, ytmp], [ygT])

    pending = [None]

    def push(ctx):
        if pending[0] is not None:
            stageY(pending[0])
        pending[0] = ctx

    for ft in range(4):
        for qq in range(4):
            pair = ft * 4 + qq
            P.op("dve", lambda h, pair=pair: h.tensor_scalar(ang.t[:], iota.t[:], th.t[:, pair:pair + 1], None, ALU.mult), [iota] + als, [ang])
            sincos(ang.t[:], 1024, tabs[qq].t[:], tabc[qq].t[:], [ang], [tabs[qq], tabc[qq]])
        for seg in range(4):
            for qq in range(4):
                pair = ft * 4 + qq
                ctx = stageX(pair, qq, ft, seg * NCH, 1024, tabc[qq], tabs[qq],
                             stp_l[pair].t[:, 0:1], stp_l[pair].t[:, 1:2], [stp_l[pair]], stp_l[pair])
                if qq == 3:
                    ctx["after"] = [(lambda ft=ft, seg=seg, hf=hf: y_evac(ypb[hf], ft, seg * NCH + hf * 512, 512)) for hf in range(2)]
                push(ctx)
        for qq in range(4):
            pair = ft * 4 + qq
            ctx = stageXs(pair, qq, ft, tabc[qq], tabs[qq])
            if qq == 3:
                ctx["after"] = [(lambda ft=ft: y_evac_s(ft))]
            push(ctx)
    push(None)
    stp2 = P.sb("stp2", [128, 2, 16], F32, at=tq.at); sts2 = P.sb("sts2", [128, 4, 2, 16], F32, at=tq.at + 128)
    for p_ in range(16):
        P.op("dve", lambda h, p_=p_: h.tensor_copy(stp2.t[:, :, p_], stp_l[p_].t[:, 0:2]), [stp_l[p_]], [stp2])
        P.op("dve", lambda h, p_=p_: h.tensor_copy(sts2.t[:, :, :, p_], sts_p[p_].t[:]), [sts_p[p_]], [sts2])
    if STOP == 'SSM':
        for r_ in range(4):
            P.dma("pool", OUT["yp"].ap().rearrange("(p f x) c -> p f (x c)", p=128, f=4)[:, :, r_ * 1024:(r_ + 1) * 1024],
                  ygT.t[:, :, r_ * NCH:r_ * NCH + 1024], [ygT], [outbufs["yp"]], ygT)
        P.dma("pool", OUT["ys"].ap()[:, 0:128].rearrange("p (f r s) -> p f r s", f=4, r=4),
              ygT.t[:].rearrange("p f (r c) -> p f r c", r=4)[:, :, :, 1024:1032], [ygT], [outbufs["ys"]], ygT)
    P.dma("sp", OUT["ssm_p"].ap(), stp2.t[:], [stp2], [outbufs["ssm_p"]], stp2)
    P.dma("sp", OUT["ssm_s"].ap(), sts2.t[:], [sts2], [outbufs["ssm_s"]], sts2)
    P.barrier()
    hn1o = P.sb("hn1o", [128, KT, NCH], BF16, at=uT.at)
    P.off = L1 + 40960
    zall = P.sb("zall", [128, KT, NCH], BF16)
    wz = [P.sb("wz%d" % i, [128, KT, 128], BF16) for i in range(2)]
    zz = P.sb("zz", [128, NCH], F32)
    for j in range(8):
        P.dma("sp", y_src[j].t.ap().rearrange("(ft p) t -> p ft t", p=128), ygT.t[:, :, j * 516:(j + 1) * 516], [ygT], [y_src[j]], ygT)
        P.coll(y_src[j], y_dst[j], GROUPS)

    for j, (k0, nk) in enumerate(HNP):
        P.dma("sp", hn1o.t[:, k0:k0 + nk, :], hn_src[j].t.ap().rearrange("(k p) t -> p k t", p=128), [hn_src[j]], [hn1o], hn1o)
    for nt in range(16):
        wb_ = wz[nt % 2]
        P.dma("pool", wb_.t[:], IN["w_in_c"].ap()[:, 2048 + nt * 128:2048 + (nt + 1) * 128].rearrange("(kt p) c -> p kt c", p=128), [], [wb_], wb_)
        for (c0, n) in [(0, 512), (512, 512), (1024, 8)]:
            def zev(pk, c0=c0, n=n, nt=nt):
                P.op("act", lambda h: h.activation(zz.t[:, c0:c0 + n], pk.t[:, 0:n], AF.Sigmoid), [pk], [zz])
                P.op("dve", lambda h: h.tensor_tensor(zall.t[:, nt, c0:c0 + n], zz.t[:, c0:c0 + n], pk.t[:, 0:n], ALU.mult), [zz, pk], [zall])
            proj_feat(wb_, None, zev, hn1o, lambda kt, c0=c0, n=n: hn1o.t[:, kt, c0:c0 + n], n)

    if STOP == 'SSM':
        P.barrier()
        return
    P.barrier()
    P.off = CONST_END
    y2T = P.sb("y2T", [128, KT, NO], BF16)
    F0 = P.off
    ygo = P.sb("ygo", [128, KT, NCH], BF16)
    ych = [P.sb("ych%d" % i, [128, 4, NCH], BF16) for i in range(2)]
    wt2 = [P.sb("wt2_%d" % i, [128, KT, 128], BF16) for i in range(2)]
    gl = P.sb("gl", [128, NCH], F32)
    assert P.off <= L1 + 40960
    P.op("pool", lambda h: h.memset(y2T.t[:, :, NCH:NO], 0.0), [], [y2T])
    ci = 0
    for rf in range(4):
        for r in range(4):
            yc = ych[ci % 2]; ci += 1
            for hh_ in range(2):
                P.dma("sp", yc.t[:, :, hh_ * 516:(hh_ + 1) * 516],
                      y_dst[2 * r + hh_].t.ap()[rf * 512:(rf + 1) * 512, :].rearrange("(ft p) t -> p ft t", p=128),
                      [y_dst[2 * r + hh_]], [yc], yc)
            dst = ygo.t[:, rf * 4:(rf + 1) * 4, :]
            if r == 0:
                P.op("dve", lambda h, yc=yc, dst=dst: h.tensor_scalar(dst, yc.t[:], sel.t[:, 0:1], None, ALU.mult), [yc, sel], [ygo])
            else:
                P.op("dve", lambda h, yc=yc, dst=dst, r=r: h.scalar_tensor_tensor(dst, yc.t[:], sel.t[:, r:r + 1], dst, ALU.mult, ALU.add), [yc, sel, ygo], [ygo])
    if STOP == 'G1':
        P.barrier()
        return
    for nt in range(int(os.environ.get('MK_NT', '16'))):
        wa = wt2[nt % 2]
        P.dma("pool", wa.t[:], IN["w_glu"].ap()[:, nt * 128:(nt + 1) * 128].rearrange("(kt p) c -> p kt c", p=128), [], [wa], wa)
        for (c0, n) in [(0, 512), (512, 512), (1024, 8)]:
            proj_feat(wa, None, lambda pk, c0=c0, n=n, nt=nt: P.op("act", lambda h: h.activation(gl.t[:, c0:c0 + n], pk.t[:, 0:n], AF.Sigmoid, bias=bglu.t[:, nt:nt + 1], scale=1.0), [pk, bglu], [gl]),
                      ygo, lambda kt, c0=c0, n=n: ygo.t[:, kt, c0:c0 + n], n)
        P.op("dve", lambda h, nt=nt: h.tensor_tensor(gl.t[:], gl.t[:], ygo.t[:, nt, :], ALU.mult), [gl, ygo], [gl])
        P.op("dve", lambda h, nt=nt: h.tensor_tensor(y2T.t[:, nt, 0:NCH], gl.t[:], zall.t[:, nt, :], ALU.mult), [gl, zall], [y2T])
    if STOP == 'G2':
        P.barrier()
        return
    P.barrier()
    P.off = F0
    wgb2 = [P.sb("wgc%d" % i, [128, KT, 512], BF16) for i in range(2)]
    h1s = [P.sb("h1f%d" % i, [128, D], F32) for i in range(5)]
    junk = P.sb("junk3", [128, D], F32); ss = P.sb("ss3", [128, 4], F32)
    yo = [P.sb("yo%d" % i, [128, D], F32) for i in range(2)]
    load_g("g_fin")
    wi = 0
    for tiles in [list(range(0, 5)), list(range(5, 9))]:
        for si, tj in enumerate(tiles):
            o0 = tj * 128
            P.dma("sp", h1s[si].t[:], h1_scr.t.ap()[o0:o0 + 128, :], [h1_scr], [h1s[si]], h1s[si])
        for g4 in range(4):
            wg = wgb2[wi % 2]
            wi += 1
            P.dma("pool", wg.t[:], IN["w_out_c"].ap()[:, g4 * 512:(g4 + 1) * 512].rearrange("(kt p) c -> p kt c", p=128), [], [wg], wg)
            for si, tj in enumerate(tiles):
                o0 = tj * 128
                ht_ = h1s[si]
                pk = next_pf()
                fns = [(lambda h, pk=pk, kt=kt, o0=o0, wg=wg: h.matmul(pk.t[:], y2T.t[:, kt, o0:o0 + 128], wg.t[:, kt, :],
                                                                         start=(kt == 0), stop=(kt == KT - 1))) for kt in range(KT)]
                P.mm(fns, [y2T, wg], [pk])
                P.op("dve", lambda h, pk=pk, ht_=ht_, g4=g4: h.tensor_tensor(ht_.t[:, g4 * 512:(g4 + 1) * 512], ht_.t[:, g4 * 512:(g4 + 1) * 512], pk.t[:], ALU.add),
                     [pk, ht_], [ht_])
        for si, tj in enumerate(tiles):
            o0 = tj * 128
            ht_ = h1s[si]
            yo_ = yo[tj % 2]
            rmsnorm_rows(ht_, yo_, ss, junk)
            if tj < 8:
                P.dma("sp", OUT["yp"].ap()[o0:o0 + 128, :], yo_.t[:], [yo_], [outbufs["yp"]], yo_)
            else:
                P.dma("sp", OUT["ys"].ap(), yo_.t[:], [yo_], [outbufs["ys"]], yo_)
    P.barrier()


_NC_CACHE = {}


def _rope_tables(pos):
    half = 64
    inv = (np.float32(10000.0) ** (-np.arange(half, dtype=np.float32) / np.float32(half))).astype(np.float32)
    ang = pos.astype(np.float32)[:, None] * inv[None, :]
    return np.cos(ang).astype(np.float32), np.sin(ang).astype(np.float32)


def kernel(x_prompt, x_sample, cache_win_k, cache_win_v, state_conv, state_ssm_re, state_ssm_im,
           attn_norm, w_in_ab, conv_w, w_out_ab, ssm_norm, w_in_c, lam_re, lam_im, log_step,
           b_re, b_im, c_re, c_im, d_skip, w_glu, b_glu, w_out_c, final_norm):
    f = lambda a: np.ascontiguousarray(np.asarray(a, dtype=np.float32))
    x_prompt, x_sample = f(x_prompt), f(x_sample)
    cache_win_k, cache_win_v, state_conv = f(cache_win_k), f(cache_win_v), f(state_conv)
    state_ssm_re, state_ssm_im = f(state_ssm_re), f(state_ssm_im)
    w_in_ab0, w_out_ab0, w_in_c0, w_glu0, w_out_c0 = f(w_in_ab)[0], f(w_out_ab)[0], f(w_in_c)[0], f(w_glu)[0], f(w_out_c)[0]
    lam_re, lam_im, log_step = f(lam_re)[0], f(lam_im)[0], f(log_step)[0]
    b_re, b_im, c_re, c_im = f(b_re)[0], f(b_im)[0], f(c_re)[0], f(c_im)[0]
    d_skip0, b_glu0 = f(d_skip)[0], f(b_glu)[0]
    if "nc" not in _NC_CACHE:
        _NC_CACHE["nc"] = build_nc()
    nc = _NC_CACHE["nc"]

    kk = np.arange(128)[:, None]
    qq_ = np.arange(512)[None, :]
    maskp = np.stack([mult_of(qq_ - ((i - 16) * 128 + kk)) for i in range(20)], 1)
    rows = np.arange(2176).reshape(17, 128)
    s_ = np.arange(128)[None, :]
    masks = np.zeros((128, 17, 128), np.float32)
    for i in range(17):
        row = rows[i][:, None]
        m = mult_of(2048 + s_ - row)
        m[:, 8:] = ((2048 + s_[:, 8:] - row) == 0)
        masks[:, i, :] = m
    iota = np.broadcast_to(np.arange(1024, dtype=np.float32)[None, :], (128, 1024)).copy()
    rmask = np.zeros((128, 8), np.float32)
    for p in range(128):
        rmask[p, (p // 32) * 2 + (p % 32) // 16] = 1.0
    smask = np.zeros((128, 2), np.float32)
    smask[:64, 0] = 1.0
    smask[64:, 1] = 1.0
    bc = lambda v: np.ascontiguousarray(np.broadcast_to(v[None, :], (128, v.shape[0])))

    in_maps = []
    for c in range(8):
        b, r = c // 4, c % 4
        T0 = r * NOWN
        xh = np.zeros((NTP, D), np.float32)
        lo = T0 - NHALO
        src_lo = max(lo, 0)
        xh[src_lo - lo:] = x_prompt[b, src_lo:T0 + NOWN]
        pos = np.concatenate([np.arange(lo, T0 + NOWN), PAST + np.arange(128)]).astype(np.float32)
        valid = (pos[:NTP] >= 0).astype(np.float32)
        cosv, sinv = _rope_tables(np.maximum(pos, 0))
        xs = np.zeros((128, D), np.float32)
        xs[:8] = x_sample[c]
        g0 = 32 * r
        gs = slice(g0, g0 + 32)
        st_lay = lambda a: np.ascontiguousarray(a.reshape(16, 2, 64).transpose(1, 2, 0).reshape(128, 16))
        def row_lay_rep(a):
            t = a.reshape(4, 4, 2, 64)
            t = np.broadcast_to(t[:, :, :, None, :], (4, 4, 2, 16, 64))
            return np.ascontiguousarray(t.transpose(1, 2, 3, 0, 4).reshape(128, 4, 64))
        def row_lay_b(a):
            t = a.reshape(4, 4, 2, 64, 16)
            return np.ascontiguousarray(t.transpose(1, 2, 4, 0, 3).reshape(128, 4, 64))
        def st_lay_c(a):
            t = a.reshape(16, 2, 16, 64)
            return np.ascontiguousarray(t.transpose(1, 3, 0, 2).reshape(128, 16, 16))
        lst32 = np.broadcast_to(log_step[gs][:, None], (32, 64))
        sel = np.zeros((128, 4), np.float32)
        sel[:, r] = 1.0
        sre0 = np.stack([st_lay(state_ssm_re[0, 4 * b + i, gs]) for i in range(4)], 1)
        sim0 = np.stack([st_lay(state_ssm_im[0, 4 * b + i, gs]) for i in range(4)], 1)
        w_in_c_rolled = np.concatenate([w_in_c0[:, 512 * r:512 * (r + 1)], w_in_c0[:, 512:2048], w_in_c0[:, 2048:]], 1)
        m = {
            "xh": xh, "xs": xs,
            "cs": np.ascontiguousarray(cosv.reshape(25, 128, 64).transpose(1, 0, 2)),
            "sn": np.ascontiguousarray(sinv.reshape(25, 128, 64).transpose(1, 0, 2)),
            "valid": np.ascontiguousarray(valid.reshape(24, 128).T),
            "ck": np.ascontiguousarray(cache_win_k[0, c].reshape(2048, 1024)),
            "cv": np.ascontiguousarray(cache_win_v[0, c].reshape(2048, 1024)),
            "sconv": np.ascontiguousarray(state_conv[0, c].reshape(2, 8, 128).transpose(2, 1, 0)),
            "g_attn": bc(f(attn_norm)[0]), "g_ssm": bc(f(ssm_norm)[0]), "g_fin": bc(f(final_norm)),
            "w_in_ab": w_in_ab0, "cw": np.ascontiguousarray(f(conv_w)[0].reshape(3, 8, 128).transpose(2, 1, 0)),
            "w_out_ab": w_out_ab0, "w_in_c": np.ascontiguousarray(w_in_c_rolled),
            "w_glu": w_glu0, "w_out_c": w_out_c0,
            "bglu": np.ascontiguousarray(b_glu0.reshape(16, 128).T),
            "dsk": np.ascontiguousarray(d_skip0[512 * r:512 * (r + 1)].reshape(4, 128).T),
            "maskp": maskp, "masks": masks,
            "lre_s": st_lay(lam_re[gs]), "lim_s": st_lay(lam_im[gs]), "lst_s": st_lay(lst32),
            "lre_r": row_lay_rep(lam_re[gs]), "lim_r": row_lay_rep(lam_im[gs]), "lst_r": row_lay_rep(np.ascontiguousarray(lst32)),
            "bre_r": row_lay_b(b_re[gs]), "bim_r": row_lay_b(b_im[gs]),
            "cre_s": st_lay_c(c_re[gs]), "cim_s": st_lay_c(c_im[gs]),
            "rmask": rmask, "smask": smask, "sel": sel, "sre0": sre0, "sim0": sim0, "iota": iota,
        }
        in_maps.append({k: np.ascontiguousarray(v, dtype=np.float32) for k, v in m.items()})

    res = run_bass_kernel_spmd(nc, in_maps, core_ids=list(range(8)))
    R = res.results
    _NC_CACHE['raw'] = R
    y_prompt = np.zeros((2, SEQ, D), np.float32)
    y_sample = np.zeros((8, 8, D), np.float32)
    kp = np.zeros((1, 2, 2048, 8, 128), np.float32)
    vp = np.zeros((1, 2, 2048, 8, 128), np.float32)
    convp = np.zeros((1, 2, 2, 1024), np.float32)
    srp = np.zeros((1, 2, 128, 64), np.float32)
    sip = np.zeros((1, 2, 128, 64), np.float32)
    ks = np.zeros((1, 8, 8, 8, 128), np.float32)
    vs = np.zeros((1, 8, 8, 8, 128), np.float32)
    convs = np.zeros((1, 8, 2, 1024), np.float32)
    srs = np.zeros((1, 8, 128, 64), np.float32)
    sis = np.zeros((1, 8, 128, 64), np.float32)
    unst = lambda a: a.reshape(2, 64, 16).transpose(2, 0, 1).reshape(32, 64)
    for c in range(8):
        b, r = c // 4, c % 4
        o = R[c]
        y_prompt[b, r * NOWN:(r + 1) * NOWN] = o["yp"]
        y_sample[c] = o["ys"][:8]
        if r >= 2:
            kp[0, b, (r - 2) * NOWN:(r - 1) * NOWN] = o["kp"].reshape(NOWN, 8, 128)
            vp[0, b, (r - 2) * NOWN:(r - 1) * NOWN] = o["vp"].reshape(NOWN, 8, 128)
        if r == 3:
            convp[0, b] = o["convp"].transpose(2, 1, 0).reshape(2, 1024)
        ks[0, c] = o["ks"][:8].reshape(8, 8, 128)
        vs[0, c] = o["vs"][:8].reshape(8, 8, 128)
        convs[0, c] = o["convs"].transpose(2, 1, 0).reshape(2, 1024)
        srp[0, b, 32 * r:32 * (r + 1)] = unst(o["ssm_p"][:, 0, :])
        sip[0, b, 32 * r:32 * (r + 1)] = unst(o["ssm_p"][:, 1, :])
        for i in range(4):
            srs[0, 4 * b + i, 32 * r:32 * (r + 1)] = unst(o["ssm_s"][:, i, 0, :])
            sis[0, 4 * b + i, 32 * r:32 * (r + 1)] = unst(o["ssm_s"][:, i, 1, :])
    return (y_prompt, y_sample, kp, vp, convp, srp, sip, ks, vs, convs, srs, sis)
```

```python
import math
import os
STOP = os.environ.get('MK_STOP', '')
from contextlib import ExitStack

import numpy as np
import concourse.bass as bass
import concourse.mybir as mybir
from concourse.bass_utils import run_bass_kernel_spmd

F32 = mybir.dt.float32
BF16 = mybir.dt.bfloat16
ALU = mybir.AluOpType
AF = mybir.ActivationFunctionType
AX = mybir.AxisListType

ENGS = ["pe", "act", "dve", "pool", "sp"]
D = 2048
KT = 16
NOWN = 1024
NHALO = 2048
NTP = NOWN + NHALO
NTILE_P = NTP // 128
NO = NOWN + 128
SEQ = 4096
PAST = 16384
NCH = 1032
TWO_PI = 2.0 * math.pi


class Buf:
    def __init__(self, t, name):
        self.t = t
        self.name = name
        self.w = {}
        self.r = {}
        self.dsem = None
        self.dcnt = 0


class Prog:
    def __init__(self, nc, stack):
        self.nc = nc
        self.stack = stack
        self.q = {e: [] for e in ENGS}
        self.cnt = {e: 0 for e in ENGS}
        self.seen = {e: {} for e in ENGS}
        self.sems = {}
        self.semval = {}
        for e in ["pe", "act", "dve", "pool"]:
            self.sems[e] = stack.enter_context(nc.semaphore("s_" + e))
        self.off = 16512
        self.free = []
        self.dval = {}
        self.phase_bufs = []

    def sb(self, name, shape, dt, at=None):
        nbytes = int(np.prod(shape[1:])) * (2 if dt == BF16 else 4)
        if at is None:
            at = self.off
            self.off = (at + nbytes + 63) // 64 * 64
        assert at + nbytes <= 229300, (name, at, nbytes)
        t = self.nc.alloc_sbuf_tensor_at(name, list(shape), dt, offset=at)
        b = Buf(t, name)
        b.at = at
        b.nbytes = nbytes
        return b

    def ps(self, name, shape, dt=F32):
        t = self.stack.enter_context(self.nc.psum_tensor(name, list(shape), dt))
        return Buf(t, name)

    def dram(self, name, shape, dt, kind="Internal"):
        t = self.nc.dram_tensor(name, list(shape), dt, kind=kind)
        return Buf(t, name)

    def _need(self, eng, k, v, waits):
        if self.seen[eng].get(k, 0) >= v:
            return
        waits[k] = max(waits.get(k, 0), v)

    def _deps(self, eng, reads, writes):
        waits = {}
        for b in reads:
            for k, v in b.w.items():
                self._need(eng, k, v, waits)
        for b in writes:
            for k, v in b.w.items():
                self._need(eng, k, v, waits)
            for k, v in b.r.items():
                self._need(eng, k, v, waits)
        for k, v in waits.items():
            self.seen[eng][k] = v
        return [(self.sems[k], v) for k, v in waits.items()]

    def _commit(self, k, v, reads, writes):
        self.semval[k] = v
        for b in reads:
            b.r[k] = max(b.r.get(k, 0), v)
        for b in writes:
            b.w[k] = max(b.w.get(k, 0), v)
            b.r = {}

    def op(self, eng, fn, reads=(), writes=()):
        reads = [b for b in reads if b is not None]
        writes = [b for b in writes if b is not None]
        wl = self._deps(eng, reads, writes)
        self.cnt[eng] += 1
        sem = self.sems[eng]

        def emit(h, fn=fn, wl=wl, sem=sem):
            for s, v in wl:
                h.wait_ge(s, v)
            fn(h).then_inc(sem, 1)

        self.q[eng].append(emit)
        self._commit(eng, self.cnt[eng], reads, writes)

    def mm(self, fns, reads, writes):
        eng = "pe"
        wl = self._deps(eng, reads, writes)
        self.cnt[eng] += 1
        sem = self.sems[eng]

        def emit(h, fns=fns, wl=wl, sem=sem):
            for s, v in wl:
                h.wait_ge(s, v)
            for f in fns[:-1]:
                f(h)
            fns[-1](h).then_inc(sem, 1)

        self.q[eng].append(emit)
        self._commit(eng, self.cnt[eng], reads, writes)

    def dma(self, eng, out, in_, reads, writes, semb, **kw):
        reads = [b for b in reads if b is not None]
        writes = [b for b in writes if b is not None]
        if eng == "pool":
            if getattr(semb, "psem", None) is None:
                key = "q%d" % len(self.sems)
                self.sems[key] = self.stack.enter_context(self.nc.semaphore(key))
                semb.psem = key
                semb.pcnt = 0
            wl = self._deps(eng, reads, writes)
            semb.pcnt += 16
            sem = self.sems[semb.psem]

            def emit_p(h, wl=wl, sem=sem, out=out, in_=in_, kw=kw):
                for s, v in wl:
                    h.wait_ge(s, v)
                h.dma_start(out=out, in_=in_, **kw).then_inc(sem, 16)

            self.q[eng].append(emit_p)
            self._commit(semb.psem, semb.pcnt, reads, writes)
            return
        if semb.dsem is None:
            if self.free:
                key = self.free.pop()
            else:
                key = "d%d" % len(self.sems)
                self.sems[key] = self.stack.enter_context(self.nc.semaphore(key))
            semb.dsem = key
            semb.dcnt = self.dval.get(key, 0)
            self.phase_bufs.append(semb)
        wl = self._deps(eng, reads, writes)
        semb.dcnt += 16
        self.dval[semb.dsem] = semb.dcnt
        sem = self.sems[semb.dsem]

        def emit(h, wl=wl, sem=sem, out=out, in_=in_, kw=kw):
            for s, v in wl:
                h.wait_ge(s, v)
            h.dma_start(out=out, in_=in_, **kw).then_inc(sem, 16)

        self.q[eng].append(emit)
        self._commit(semb.dsem, semb.dcnt, reads, writes)

    def coll(self, src, dst, groups):
        key = "c%d" % len(self.sems)
        self.sems[key] = self.stack.enter_context(self.nc.semaphore(key))
        wl = self._deps("pool", [src], [dst])
        sem = self.sems[key]

        def emit(h, wl=wl, sem=sem):
            for s, v in wl:
                h.wait_ge(s, v)
            h.collective_compute("AllGather", ALU.bypass, replica_groups=groups,
                                 ins=[src.t.ap()], outs=[dst.t.ap()]).then_inc(sem)

        self.q["pool"].append(emit)
        self._commit(key, 1, [src], [dst])

    def barrier(self):
        for b in self.phase_bufs:
            self.free.append(b.dsem)
            b.dsem = None
        self.phase_bufs = []
        items = list(self.semval.items())
        for e in ENGS:
            wl = []
            for k, v in items:
                if self.seen[e].get(k, 0) < v:
                    self.seen[e][k] = v
                    wl.append((self.sems[k], v))

            def emit(h, wl=wl):
                for s, v in wl:
                    h.wait_ge(s, v)

            if wl:
                self.q[e].append(emit)

    def run(self):
        nc = self.nc
        with nc.Block() as block:
            @block.tensor
            def _(h):
                for f in self.q["pe"]:
                    f(h)

            @block.scalar
            def _(h):
                for f in self.q["act"]:
                    f(h)

            @block.vector
            def _(h):
                for f in self.q["dve"]:
                    f(h)

            @block.gpsimd
            def _(h):
                for f in self.q["pool"]:
                    f(h)

            @block.sync
            def _(h):
                for f in self.q["sp"]:
                    f(h)


def mult_of(d):
    d = np.asarray(d)
    m = ((d >= 0) & (d <= 128)).astype(np.float32)
    m += ((d >= 0) & (d <= 512) & (d % 4 == 0))
    m += ((d >= 0) & (d <= 2048) & (d % 16 == 0))
    return m.astype(np.float32)


IN_SPECS = [
    ("xh", [NTP, D]), ("xs", [128, D]), ("cs", [128, 25, 64]), ("sn", [128, 25, 64]),
    ("valid", [128, 24]), ("ck", [2048, 1024]), ("cv", [2048, 1024]), ("sconv", [128, 8, 2]),
    ("g_attn", [128, D]), ("g_ssm", [128, D]), ("g_fin", [128, D]),
    ("w_in_ab", [D, 8192]), ("cw", [128, 8, 3]), ("w_out_ab", [D, D]), ("w_in_c", [D, 4096]),
    ("w_glu", [D, D]), ("w_out_c", [D, D]), ("bglu", [128, 16]), ("dsk", [128, 4]),
    ("maskp", [128, 20, 512]), ("masks", [128, 17, 128]),
    ("lre_s", [128, 16]), ("lim_s", [128, 16]), ("lst_s", [128, 16]),
    ("lre_r", [128, 4, 64]), ("lim_r", [128, 4, 64]), ("lst_r", [128, 4, 64]),
    ("bre_r", [128, 4, 64]), ("bim_r", [128, 4, 64]),
    ("cre_s", [128, 16, 16]), ("cim_s", [128, 16, 16]),
    ("rmask", [128, 8]), ("smask", [128, 2]), ("sel", [128, 4]),
    ("sre0", [128, 4, 16]), ("sim0", [128, 4, 16]), ("iota", [128, 1024]),
]
OUT_SPECS = [
    ("yp", [NOWN, D]), ("ys", [128, D]), ("kp", [NOWN, 1024]), ("vp", [NOWN, 1024]),
    ("convp", [128, 8, 2]), ("ssm_p", [128, 2, 16]), ("ks", [128, 1024]), ("vs", [128, 1024]),
    ("convs", [128, 8, 2]), ("ssm_s", [128, 4, 2, 16]),
]


def build_nc():
    nc = bass.Bass("TRN2", target_bir_lowering=False)
    IN = {}
    for n, s in IN_SPECS:
        IN[n] = nc.dram_tensor(n, s, F32, kind="ExternalInput")
    OUT = {}
    for n, s in OUT_SPECS:
        OUT[n] = nc.dram_tensor(n, s, F32, kind="ExternalOutput")
    st = ExitStack()
    with st:
        P = Prog(nc, st)
        build_program(nc, P, IN, OUT)
        P.run()
    return nc


def build_program(nc, P, IN, OUT):
    GROUPS = [[0, 1, 2, 3], [4, 5, 6, 7]]
    outbufs = {n: Buf(OUT[n], n) for n in OUT}
    kT_scr = P.dram("kT_scr", [8, 128, NTP], BF16)
    v_scr = P.dram("v_scr", [NTP, 1024], BF16)
    kTs_scr = P.dram("kTs_scr", [8, 128, 2176], BF16)
    vs_scr = P.dram("vs_scr", [2176, 1024], BF16)
    qT_scr = P.dram("qT_scr", [8, 128, NO], BF16)
    h1_scr = P.dram("h1_scr", [NO, D], F32)
    HNP = [(0, 3), (3, 3), (6, 3), (9, 3), (12, 3), (15, 1)]
    hn_src = [P.dram("hn_src%d" % j, [nk * 128, NCH], BF16) for j, (k0, nk) in enumerate(HNP)]
    hn_dst = [P.dram("hn_dst%d" % j, [4 * nk * 128, NCH], BF16) for j, (k0, nk) in enumerate(HNP)]
    y_src = [P.dram("y_src%d" % j, [512, 516], BF16) for j in range(8)]
    y_dst = [P.dram("y_dst%d" % j, [4 * 512, 516], BF16) for j in range(8)]

    pf = [P.ps("pf%d" % i, [128, 512], F32) for i in range(6)]
    pb = [P.ps("pb%d" % i, [128, 8, 128], BF16) for i in range(2)]
    pfi = [0]
    pbi = [0]

    def next_pf():
        pfi[0] = (pfi[0] + 1) % 4
        return pf[pfi[0]]

    def next_pb():
        pbi[0] = (pbi[0] + 1) % 2
        return pb[pbi[0]]

    ident = P.sb("ident", [128, 128], BF16)
    P.op("pool", lambda h: h.memset(ident.t[:], 1.0), [], [ident])
    P.op("pool", lambda h: h.affine_select(ident.t[:], ident.t[:], [[-1, 128]], ALU.is_equal, 0.0,
                                            base=0, channel_multiplier=1), [ident], [ident])
    ones_bf = P.sb("ones_bf", [128, 128], BF16)
    P.op("pool", lambda h: h.memset(ones_bf.t[:], 1.0), [], [ones_bf])
    gt = P.sb("gt", [128, D], F32)
    cs = P.sb("cs", [128, 25, 64], F32)
    sn = P.sb("sn", [128, 25, 64], F32)
    valid = P.sb("valid", [128, 24], F32)
    validB = P.sb("validB", [128, 24, 128], BF16)
    cw = P.sb("cw", [128, 8, 3], F32)
    bglu = P.sb("bglu", [128, 16], F32)
    dsk = P.sb("dsk", [128, 4], F32)
    sel = P.sb("sel", [128, 4], F32)
    eps_t = P.sb("eps_t", [128, 1], F32)
    P.op("pool", lambda h: h.memset(eps_t.t[:], 1e-6), [], [eps_t])
    for b_, n in [(cs, "cs"), (sn, "sn"), (valid, "valid"), (cw, "cw"), (bglu, "bglu"), (dsk, "dsk"), (sel, "sel")]:
        P.dma("sp", b_.t[:], IN[n].ap(), [], [b_], b_)
    P.op("dve", lambda h: h.tensor_copy(validB.t[:], valid.t[:].unsqueeze(2).to_broadcast([128, 24, 128])),
         [valid], [validB])
    pospi = P.sb("pospi", [128, 1], F32)
    P.op("pool", lambda h: h.memset(pospi.t[:], math.pi), [], [pospi])
    CONST_END = P.off
    hnT_o = P.sb("hnT_o", [128, KT, NO], BF16)

    def load_g(name):
        P.dma("sp", gt.t[:], IN[name].ap(), [], [gt], gt)

    def rmsnorm_rows(xt, xn, ss, junk):
        P.op("act", lambda h: h.activation(junk.t[:], xt.t[:], AF.Square, accum_out=ss.t[:, 0:1]), [xt], [junk, ss])
        P.op("act", lambda h: h.activation(ss.t[:, 1:2], ss.t[:, 0:1], AF.Sqrt, bias=eps_t.t[:, 0:1], scale=1.0 / D), [ss, eps_t], [ss])
        P.op("dve", lambda h: h.reciprocal(ss.t[:, 2:3], ss.t[:, 1:2]), [ss], [ss])
        P.op("dve", lambda h: h.scalar_tensor_tensor(xn.t[:], xt.t[:], ss.t[:, 2:3], gt.t[:], ALU.mult, ALU.mult),
             [xt, ss, gt], [xn])

    def transpose_rows(xn, dst, dst_ap_fn):
        for half in range(2):
            p = next_pb()
            fns = []
            for j in range(8):
                kt = half * 8 + j
                fns.append(lambda h, p=p, j=j, kt=kt: h.transpose(p.t[:, j, :], xn.t[:, kt * 128:(kt + 1) * 128], ident.t[:]))
            P.mm(fns, [xn, ident], [p])
            P.op("act", lambda h, p=p, half=half: h.activation(dst_ap_fn(half), p.t[:], AF.Identity), [p], [dst])

    A0 = P.off
    wkv = P.sb("wkv", [128, KT, 2048], BF16)
    xts = [P.sb("xt%d" % i, [128, D], F32) for i in range(3)]
    xns = [P.sb("xn%d" % i, [128, D], BF16) for i in range(3)]
    hts = [P.sb("ht%d" % i, [128, KT, 128], BF16) for i in range(2)]
    kc = P.sb("kc", [128, 1024], BF16)
    vc = P.sb("vc", [128, 1024], BF16)
    ss = P.sb("ss", [128, 4], F32)
    krs = [P.sb("kr%d" % i, [128, 1024], F32) for i in range(2)]
    vfs = [P.sb("vf%d" % i, [128, 1024], F32) for i in range(2)]
    t1 = P.sb("t1", [128, 256], F32)
    t2 = P.sb("t2", [128, 256], F32)
    krbs = [P.sb("krb%d" % i, [128, 1024], BF16) for i in range(2)]
    vbs = [P.sb("vb%d" % i, [128, 1024], BF16) for i in range(2)]
    kTts = [P.sb("kTt%d" % i, [128, 8, 128], BF16) for i in range(2)]
    kTt = kTts[0]
    hprev2 = P.sb("hprev2", [128, KT, 2], BF16)
    A1_END = P.off

    load_g("g_attn")
    for half in range(2):
        P.dma("pool", wkv.t[:, :, half * 1024:(half + 1) * 1024],
              IN["w_in_ab"].ap()[:, 1024 + half * 1024:2048 + half * 1024].rearrange("(kt p) c -> p kt c", p=128),
              [], [wkv], wkv)

    def rotary(pk, ti, dst, c0):
        v = pk.t[:].rearrange("p (h two d) -> p h two d", h=4, two=2)
        o = dst.t[:, c0:c0 + 512].rearrange("p (h two d) -> p h two d", h=4, two=2)
        cb = cs.t[:, ti, :].unsqueeze(1).to_broadcast([128, 4, 64])
        sb_ = sn.t[:, ti, :].unsqueeze(1).to_broadcast([128, 4, 64])
        a = t1.t[:].rearrange("p (h d) -> p h d", h=4)
        b = t2.t[:].rearrange("p (h d) -> p h d", h=4)
        P.op("dve", lambda h: h.tensor_tensor(a, v[:, :, 0, :], cb, ALU.mult), [pk, cs], [t1])
        P.op("dve", lambda h: h.tensor_tensor(b, v[:, :, 1, :], sb_, ALU.mult), [pk, sn], [t2])
        P.op("dve", lambda h: h.tensor_tensor(o[:, :, 0, :], a, b, ALU.subtract), [t1, t2], [dst])
        P.op("dve", lambda h: h.tensor_tensor(a, v[:, :, 1, :], cb, ALU.mult), [pk, cs], [t1])
        P.op("dve", lambda h: h.tensor_tensor(b, v[:, :, 0, :], sb_, ALU.mult), [pk, sn], [t2])
        P.op("dve", lambda h: h.tensor_tensor(o[:, :, 1, :], a, b, ALU.add), [t1, t2], [dst])

    def store_kT(src_bf, scr, col0, kb=None):
        p = next_pb()
        if kb is None:
            kb = kTt
        fns = [(lambda h, p=p, j=j: h.transpose(p.t[:, j, :], src_bf.t[:, j * 128:(j + 1) * 128], ident.t[:])) for j in range(8)]
        P.mm(fns, [src_bf, ident], [p])
        P.op("act", lambda h, p=p, kb=kb: h.activation(kb.t[:], p.t[:], AF.Identity), [p], [kb])
        P.dma("sp", scr.t.ap()[:, :, col0:col0 + 128].rearrange("h d t -> d h t"), kb.t[:], [kb], [scr], kb)

    def stageL(ti):
        xt = xts[ti % 3]
        src = IN["xh"].ap()[ti * 128:(ti + 1) * 128, :] if ti < 24 else IN["xs"].ap()
        P.dma("sp", xt.t[:], src, [], [xt], xt)

    def stageN(ti):
        rmsnorm_rows(xts[ti % 3], xns[ti % 3], ss, xns[ti % 3])

    def stageA2(ti):
        xn = xns[ti % 3]
        if ti < 16:
            ht = hts[ti % 2]
            transpose_rows(xn, ht, lambda half, ht=ht: ht.t[:, half * 8:(half + 1) * 8, :])
            if ti == 15:
                P.op("dve", lambda h, ht=ht: h.tensor_copy(hprev2.t[:], ht.t[:, :, 126:128]), [ht], [hprev2])
            return (lambda kt, ht=ht: ht.t[:, kt, :]), ht
        o0 = (ti - 16) * 128
        transpose_rows(xn, hnT_o, lambda half, o0=o0: hnT_o.t[:, half * 8:(half + 1) * 8, o0:o0 + 128])
        return (lambda kt, o0=o0: hnT_o.t[:, kt, o0:o0 + 128]), hnT_o

    def stageM(ti, lhs, hb):
        kr = krs[ti % 2]
        vf = vfs[ti % 2]
        for g4 in range(4):
            pk = next_pf()
            fns = [(lambda h, pk=pk, kt=kt, g4=g4, lhs=lhs: h.matmul(pk.t[:], lhs(kt), wkv.t[:, kt, g4 * 512:(g4 + 1) * 512],
                                                                    start=(kt == 0), stop=(kt == KT - 1))) for kt in range(KT)]
            P.mm(fns, [hb, wkv], [pk])
            if g4 < 2:
                rotary(pk, ti, kr, g4 * 512)
            else:
                c0 = (g4 - 2) * 512
                P.op("act", lambda h, pk=pk, c0=c0, vf=vf: h.activation(vf.t[:, c0:c0 + 512], pk.t[:], AF.Identity), [pk], [vf])

    def stageKpre(ti):
        kr = krs[ti % 2]
        vf = vfs[ti % 2]
        krb = krbs[ti % 2]
        vb = vbs[ti % 2]
        P.op("act", lambda h: h.activation(krb.t[:], kr.t[:], AF.Identity), [kr], [krb])
        if ti < 24:
            P.op("dve", lambda h: h.tensor_scalar(vb.t[:], vf.t[:], valid.t[:, ti:ti + 1], None, ALU.mult), [vf, valid], [vb])
        else:
            P.op("dve", lambda h: h.tensor_copy(vb.t[:], vf.t[:]), [vf], [vb])

    def stageKpost(ti):
        kr = krs[ti % 2]
        vf = vfs[ti % 2]
        krb = krbs[ti % 2]
        vb = vbs[ti % 2]
        kb = kTts[ti % 2]
        if ti < 24:
            store_kT(krb, kT_scr, ti * 128, kb)
            P.dma("sp", v_scr.t.ap()[ti * 128:(ti + 1) * 128, :], vb.t[:], [vb], [v_scr], vb)
            if ti >= 16:
                r0 = (ti - 16) * 128
                P.dma("sp", OUT["kp"].ap()[r0:r0 + 128, :], kr.t[:], [kr], [outbufs["kp"]], kr)
                P.dma("sp", OUT["vp"].ap()[r0:r0 + 128, :], vf.t[:], [vf], [outbufs["vp"]], vf)
        else:
            store_kT(krb, kTs_scr, 2048, kb)
            P.dma("sp", vs_scr.t.ap()[2048:2176, :], vb.t[:], [vb], [vs_scr], vb)
            P.dma("sp", OUT["ks"].ap(), kr.t[:], [kr], [outbufs["ks"]], kr)
            P.dma("sp", OUT["vs"].ap(), vf.t[:], [vf], [outbufs["vs"]], vf)

    def cacheL(c):
        P.dma("pool", kc.t[:], IN["ck"].ap()[c * 128:(c + 1) * 128, :], [], [kc], kc)
        P.dma("pool", vc.t[:], IN["cv"].ap()[c * 128:(c + 1) * 128, :], [], [vc], vc)

    def cacheP(c):
        store_kT(kc, kTs_scr, c * 128, kTts[c % 2])
        P.dma("sp", vs_scr.t.ap()[c * 128:(c + 1) * 128, :], vc.t[:], [vc], [vs_scr], vc)

    for t_ in range(3):
        stageL(t_)
    stageN(0)
    stageN(1)
    infoA = {0: stageA2(0)}
    cacheL(0)
    cnext = 0
    for ti in range(25):
        if ti + 3 < 25:
            stageL(ti + 3)
        if ti + 2 < 25:
            stageN(ti + 2)
        if ti >= 1:
            stageKpre(ti - 1)
        if ti + 1 < 25:
            infoA[ti + 1] = stageA2(ti + 1)
        stageM(ti, *infoA[ti])
        if ti >= 1:
            stageKpost(ti - 1)
        if ti % 3 != 2 and cnext < 16:
            cacheP(cnext)
            cnext += 1
            if cnext < 16:
                cacheL(cnext)
    stageKpre(24)
    stageKpost(24)
    while cnext < 16:
        cacheP(cnext)
        cnext += 1
        if cnext < 16:
            cacheL(cnext)
    if STOP == 'A1':
        P.barrier()
        return
    P.barrier()
    P.off = A0
    wq = P.sb("wq", [128, KT, 1024], BF16)
    hprev2b = P.sb("hprev2b", [128, KT, 2], BF16)
    qf = P.sb("qf", [128, 1024], F32)
    qb = P.sb("qb", [128, 1024], BF16)
    t1 = P.sb("t1b", [128, 256], F32)
    t2 = P.sb("t2b", [128, 256], F32)
    kTt = P.sb("kTtb", [128, 8, 128], BF16)
    hprev2k = P.sb("hprev2k", [128, KT, 2], BF16, at=hprev2.at)
    hprev2k.w = dict(hprev2.w)
    P.dma("pool", wq.t[:], IN["w_in_ab"].ap()[:, 0:1024].rearrange("(kt p) c -> p kt c", p=128), [], [wq], wq)
    for tj in range(9):
        ti = 16 + tj
        o0 = tj * 128
        for g2_ in range(2):
            pk = next_pf()
            fns = [(lambda h, pk=pk, kt=kt, g2_=g2_, o0=o0: h.matmul(pk.t[:], hnT_o.t[:, kt, o0:o0 + 128],
                                                                      wq.t[:, kt, g2_ * 512:(g2_ + 1) * 512],
                                                                      start=(kt == 0), stop=(kt == KT - 1))) for kt in range(KT)]
            P.mm(fns, [hnT_o, wq], [pk])
            rotary(pk, ti, qf, g2_ * 512)
        P.op("act", lambda h: h.activation(qb.t[:], qf.t[:], AF.Identity), [qf], [qb])
        store_kT(qb, qT_scr, o0)

    if STOP == 'A2':
        P.barrier()
        return
    P.barrier()
    P.off = A0
    ocat = P.sb("ocat", [128, KT, NO], BF16)
    hp2 = P.sb("hp2", [128, KT, 2], BF16)
    B0 = P.off
    P.op("dve", lambda h: h.tensor_copy(hp2.t[:], hprev2k.t[:]), [hprev2k], [hp2])
    P.barrier()
    maskp = P.sb("maskp", [128, 13, 512], BF16)

    def midx(i):
        return i if i < 4 else (4 if i <= 11 else i - 7)

    masks_ = P.sb("masks_", [128, 17, 128], BF16)
    P.dma("pool", maskp.t[:, 0:5, :], IN["maskp"].ap()[:, 0:5, :], [], [maskp], maskp)
    P.dma("pool", maskp.t[:, 5:13, :], IN["maskp"].ap()[:, 12:20, :], [], [maskp], maskp)
    P.dma("pool", masks_.t[:], IN["masks"].ap(), [], [masks_], masks_)
    kTh = [P.sb("kTh%d" % i, [128, NTP], BF16) for i in range(2)]
    vh = [P.sb("vh%d" % i, [128, 24, 128], BF16) for i in range(2)]
    kTsh = [P.sb("kTsh%d" % i, [128, 2176], BF16) for i in range(1)] * 2
    vsh = [P.sb("vsh%d" % i, [128, 17, 128], BF16) for i in range(1)] * 2
    qTh = [P.sb("qTh%d" % i, [128, NO], BF16) for i in range(1)] * 2
    wt = [P.sb("wt%d" % i, [128, KT, 128], BF16) for i in range(4)]
    pts = [P.sb("pt%d" % i, [128, 512], BF16) for i in range(4)]
    ptm = [P.sb("ptm%d" % i, [128, 512], BF16) for i in range(4)]
    za = P.sb("za", [128, NO], F32)
    rl = P.sb("rl", [128, 512], F32)
    of = P.sb("of", [128, 512], F32)
    sg = P.sb("sg", [128, 512], F32)

    def silu_evac(pk, dstb, dst_ap, n):
        P.op("act", lambda h: h.activation(sg.t[:, 0:n], pk.t[:, 0:n], AF.Exp, scale=-1.0), [pk], [sg])
        P.op("dve", lambda h: h.tensor_scalar(sg.t[:, 0:n], sg.t[:, 0:n], 1.0, None, ALU.add), [sg], [sg])
        P.op("dve", lambda h: h.reciprocal(sg.t[:, 0:n], sg.t[:, 0:n]), [sg], [sg])
        P.op("dve", lambda h: h.tensor_tensor(dst_ap, pk.t[:, 0:n], sg.t[:, 0:n], ALU.mult), [pk, sg], [dstb])
    fb = [P.sb("fb%d" % i, [128, NO + 2], F32) for i in range(4)]
    convo_p = P.sb("convo_p", [128, 8, 2], F32)
    convo_s = P.sb("convo_s", [128, 8, 2], F32)
    sconv = P.sb("sconv", [128, 8, 2], F32)
    P.dma("sp", sconv.t[:], IN["sconv"].ap(), [], [sconv], sconv)
    scale = 128.0 ** -0.5

    def load_wt(i, c0):
        P.dma("pool", wt[i].t[:], IN["w_in_ab"].ap()[:, c0:c0 + 128].rearrange("(kt p) c -> p kt c", p=128), [], [wt[i]], wt[i])

    def proj_feat(wb, dst_ap_fn, evac, rhs_buf, rhs_fn, n):
        pk = next_pf()
        fns = [(lambda h, pk=pk, kt=kt: h.matmul(pk.t[:, 0:n], wb.t[:, kt, :], rhs_fn(kt), start=(kt == 0), stop=(kt == KT - 1)))
               for kt in range(KT)]
        P.mm(fns, [wb, rhs_buf], [pk])
        evac(pk)

    def attention(hh, qT, q0, nq, kT, vt, ktiles, mask_fn, vB_fn, o_dst_fn, zcol0):
        po = pf[4]
        pl = pf[5]
        nk = len(ktiles)
        LA = 3
        pms = {}

        def issue_S(i):
            kt_ = ktiles[i]
            ps_ = next_pf()
            P.mm([lambda h, ps_=ps_, kt_=kt_: h.matmul(ps_.t[:, 0:nq], kT.t[:, kt_ * 128:(kt_ + 1) * 128], qT.t[:, q0:q0 + nq],
                                                        start=True, stop=True)], [kT, qT], [ps_])
            pe_ = pts[i % 4]
            pm_ = ptm[i % 4]
            P.op("act", lambda h, ps_=ps_, pe_=pe_: h.activation(pe_.t[:, 0:nq], ps_.t[:, 0:nq], AF.Exp, scale=scale), [ps_], [pe_])
            mk, mb = mask_fn(i)
            eng = "dve"
            P.op(eng, lambda h, pe_=pe_, pm_=pm_, mk=mk: h.tensor_tensor(pm_.t[:, 0:nq], pe_.t[:, 0:nq], mk, ALU.mult), [pe_, mb], [pm_])
            pms[i] = pm_

        def issue_PV(i):
            kt_ = ktiles[i]
            pm_ = pms[i]
            vB, vBb = vB_fn(i)
            P.mm([lambda h, pm_=pm_, kt_=kt_, i=i: h.matmul(po.t[:, 0:nq], vt.t[:, kt_, :], pm_.t[:, 0:nq], start=(i == 0), stop=(i == nk - 1)),
                  lambda h, pm_=pm_, vB=vB, i=i: h.matmul(pl.t[:, 0:nq], vB, pm_.t[:, 0:nq], start=(i == 0), stop=(i == nk - 1))],
                 [vt, pm_, vBb], [po, pl])

        for i in range(min(LA, nk)):
            issue_S(i)
        for i in range(nk):
            if i + LA < nk:
                issue_S(i + LA)
            issue_PV(i)
        P.op("dve", lambda h: h.reciprocal(rl.t[:, 0:nq], pl.t[:, 0:nq]), [pl], [rl])
        P.op("dve", lambda h: h.tensor_tensor(of.t[:, 0:nq], po.t[:, 0:nq], rl.t[:, 0:nq], ALU.mult), [po, rl], [of])
        P.op("dve", lambda h: h.tensor_tensor(o_dst_fn(), of.t[:, 0:nq], za.t[:, zcol0:zcol0 + nq], ALU.mult), [of, za], [ocat])

    for hh in range(8):
        b2 = hh % 2
        P.dma("sp", kTh[b2].t[:], kT_scr.t.ap()[hh], [kT_scr], [kTh[b2]], kTh[b2])
        P.dma("sp", vh[b2].t[:], v_scr.t.ap()[:, hh * 128:(hh + 1) * 128].rearrange("(t p) d -> p t d", p=128), [v_scr], [vh[b2]], vh[b2])
        P.dma("sp", kTsh[b2].t[:], kTs_scr.t.ap()[hh], [kTs_scr], [kTsh[b2]], kTsh[b2])
        P.dma("sp", vsh[b2].t[:], vs_scr.t.ap()[:, hh * 128:(hh + 1) * 128].rearrange("(t p) d -> p t d", p=128), [vs_scr], [vsh[b2]], vsh[b2])
        P.dma("sp", qTh[b2].t[:], qT_scr.t.ap()[hh], [qT_scr], [qTh[b2]], qTh[b2])
        load_wt(0, 3072 + hh * 128)
        for (c0, n) in [(0, 512), (512, 512), (1024, 128)]:
            proj_feat(wt[0], None, lambda pk, c0=c0, n=n: silu_evac(pk, za, za.t[:, c0:c0 + n], n),
                      hnT_o, lambda kt, c0=c0, n=n: hnT_o.t[:, kt, c0:c0 + n], n)
        for qc in range(2):
            kts = list(range(4 * qc, 4 * qc + 20))
            attention(hh, qTh[b2], qc * 512, 512, kTh[b2], vh[b2], kts,
                      lambda i: (maskp.t[:, midx(i), :], maskp),
                      lambda i, kts=kts: (validB.t[:, kts[i], :], validB),
                      lambda qc=qc, hh=hh: ocat.t[:, hh, qc * 512:(qc + 1) * 512], qc * 512)
        attention(hh, qTh[b2], 1024, 128, kTsh[b2], vsh[b2], list(range(17)),
                  lambda i: (masks_.t[:, i, :], masks_),
                  lambda i: (ones_bf.t[:], ones_bf),
                  lambda hh=hh: ocat.t[:, hh, 1024:1152], 1024)

    for cc in range(8):
        for j, base in enumerate([4096, 5120, 6144, 7168]):
            load_wt(j, base + cc * 128)
        bb, cb_, hb_, zb = fb
        for j, dstb in enumerate(fb):
            for (c0, n) in [(0, 512), (512, 512), (1024, 128)]:
                if j == 3:
                    ev = lambda pk, c0=c0, n=n, dstb=dstb: silu_evac(pk, dstb, dstb.t[:, 2 + c0:2 + c0 + n], n)
                else:
                    ev = lambda pk, c0=c0, n=n, dstb=dstb: P.op("act", lambda h: h.activation(dstb.t[:, 2 + c0:2 + c0 + n], pk.t[:, 0:n], AF.Identity), [pk], [dstb])
                proj_feat(wt[j], None, ev, hnT_o, lambda kt, c0=c0, n=n: hnT_o.t[:, kt, c0:c0 + n], n)
            if j in (1, 2):
                proj_feat(wt[j], None, lambda pk, dstb=dstb: P.op("act", lambda h: h.activation(dstb.t[:, 0:2], pk.t[:, 0:2], AF.Identity), [pk], [dstb]),
                          hp2, lambda kt: hp2.t[:, kt, :], 2)
        P.op("dve", lambda h: h.tensor_tensor(cb_.t[:], cb_.t[:], hb_.t[:], ALU.mult), [cb_, hb_], [cb_])
        w0 = cw.t[:, cc, 0:1]
        w1 = cw.t[:, cc, 1:2]
        w2 = cw.t[:, cc, 2:3]
        P.op("dve", lambda h, w2=w2: h.tensor_scalar(hb_.t[:, 2:1026], cb_.t[:, 2:1026], w2, None, ALU.mult), [cb_, cw], [hb_])
        P.op("dve", lambda h, w1=w1: h.scalar_tensor_tensor(hb_.t[:, 2:1026], cb_.t[:, 1:1025], w1, hb_.t[:, 2:1026], ALU.mult, ALU.add), [cb_, cw, hb_], [hb_])
        P.op("dve", lambda h, w0=w0: h.scalar_tensor_tensor(hb_.t[:, 2:1026], cb_.t[:, 0:1024], w0, hb_.t[:, 2:1026], ALU.mult, ALU.add), [cb_, cw, hb_], [hb_])
        P.op("dve", lambda h, cc=cc: h.tensor_copy(convo_p.t[:, cc, :], cb_.t[:, 1024:1026]), [cb_], [convo_p])
        P.op("dve", lambda h, cc=cc: h.tensor_copy(cb_.t[:, 1024:1026], sconv.t[:, cc, :]), [sconv], [cb_])
        P.op("dve", lambda h, w2=w2: h.tensor_scalar(hb_.t[:, 1026:1034], cb_.t[:, 1026:1034], w2, None, ALU.mult), [cb_, cw], [hb_])
        P.op("dve", lambda h, w1=w1: h.scalar_tensor_tensor(hb_.t[:, 1026:1034], cb_.t[:, 1025:1033], w1, hb_.t[:, 1026:1034], ALU.mult, ALU.add), [cb_, cw, hb_], [hb_])
        P.op("dve", lambda h, w0=w0: h.scalar_tensor_tensor(hb_.t[:, 1026:1034], cb_.t[:, 1024:1032], w0, hb_.t[:, 1026:1034], ALU.mult, ALU.add), [cb_, cw, hb_], [hb_])
        P.op("dve", lambda h, cc=cc: h.tensor_copy(convo_s.t[:, cc, :], cb_.t[:, 1032:1034]), [cb_], [convo_s])
        P.op("dve", lambda h: h.tensor_tensor(hb_.t[:, 2:1034], hb_.t[:, 2:1034], bb.t[:, 2:1034], ALU.mult), [hb_, bb], [hb_])
        P.op("dve", lambda h, cc=cc: h.tensor_tensor(ocat.t[:, 8 + cc, 0:1032], hb_.t[:, 2:1034], zb.t[:, 2:1034], ALU.mult), [hb_, zb], [ocat])
        P.op("dve", lambda h, cc=cc: h.memset(ocat.t[:, 8 + cc, 1032:1152], 0.0), [], [ocat])
    P.dma("sp", OUT["convp"].ap(), convo_p.t[:], [convo_p], [outbufs["convp"]], convo_p)
    P.dma("sp", OUT["convs"].ap(), convo_s.t[:], [convo_s], [outbufs["convs"]], convo_s)

    if STOP == 'B':
        P.barrier()
        return
    P.barrier()
    hn1T = P.sb("hn1T", [128, KT, NCH], BF16, at=hnT_o.at)
    P.off = B0
    wgb = [P.sb("wg%d" % i, [128, KT, 512], BF16) for i in range(2)]
    h1s = [P.sb("h1s%d" % i, [128, D], F32) for i in range(5)]
    xn1 = [P.sb("xn1%d" % i, [128, D], BF16) for i in range(2)]
    junk = P.sb("junk2", [128, D], F32)
    ss = P.sb("ss2", [128, 4], F32)
    tmpT = P.sb("tmpT", [128, KT, 128], BF16)
    load_g("g_ssm")
    wi = 0
    for tiles in [list(range(0, 5)), list(range(5, 9))]:
        for si, tj in enumerate(tiles):
            o0 = tj * 128
            src = IN["xh"].ap()[NHALO + o0:NHALO + o0 + 128, :] if tj < 8 else IN["xs"].ap()
            P.dma("sp", h1s[si].t[:], src, [], [h1s[si]], h1s[si])
        for g4 in range(4):
            wg = wgb[wi % 2]
            wi += 1
            P.dma("pool", wg.t[:], IN["w_out_ab"].ap()[:, g4 * 512:(g4 + 1) * 512].rearrange("(kt p) c -> p kt c", p=128), [], [wg], wg)
            for si, tj in enumerate(tiles):
                o0 = tj * 128
                ht_ = h1s[si]
                pk = next_pf()
                fns = [(lambda h, pk=pk, kt=kt, o0=o0, wg=wg: h.matmul(pk.t[:], ocat.t[:, kt, o0:o0 + 128], wg.t[:, kt, :],
                                                                         start=(kt == 0), stop=(kt == KT - 1))) for kt in range(KT)]
                P.mm(fns, [ocat, wg], [pk])
                P.op("dve", lambda h, pk=pk, ht_=ht_, g4=g4: h.tensor_tensor(ht_.t[:, g4 * 512:(g4 + 1) * 512], ht_.t[:, g4 * 512:(g4 + 1) * 512], pk.t[:], ALU.add),
                     [pk, ht_], [ht_])
        for si, tj in enumerate(tiles):
            o0 = tj * 128
            ht_ = h1s[si]
            P.dma("sp", h1_scr.t.ap()[o0:o0 + 128, :], ht_.t[:], [ht_], [h1_scr], ht_)
            if STOP == 'C1':
                if tj < 8:
                    P.dma("sp", OUT["yp"].ap()[o0:o0 + 128, :], ht_.t[:], [ht_], [outbufs["yp"]], ht_)
                else:
                    P.dma("sp", OUT["ys"].ap(), ht_.t[:], [ht_], [outbufs["ys"]], ht_)
            xn = xn1[tj % 2]
            rmsnorm_rows(ht_, xn, ss, junk)
            if tj < 8:
                transpose_rows(xn, hn1T, lambda half, o0=o0: hn1T.t[:, half * 8:(half + 1) * 8, o0:o0 + 128])
            else:
                transpose_rows(xn, tmpT, lambda half: tmpT.t[:, half * 8:(half + 1) * 8, :])
                P.op("dve", lambda h: h.tensor_copy(hn1T.t[:, :, 1024:1032], tmpT.t[:, :, 0:8]), [tmpT], [hn1T])
    for j, (k0, nk) in enumerate(HNP):
        P.dma("sp", hn_src[j].t.ap().rearrange("(k p) t -> p k t", p=128), hn1T.t[:, k0:k0 + nk, :], [hn1T], [hn_src[j]], hn1T)
        P.coll(hn_src[j], hn_dst[j], GROUPS)

    if STOP == 'C1':
        P.barrier()
        return
    P.barrier()
    P.off = CONST_END
    uT = P.sb("uT", [128, 4, 4 * NCH], BF16)
    ygT = P.sb("ygT", [128, 4, 4 * NCH], BF16)
    L1 = P.off
    wu = P.sb("wu", [128, KT, 512], BF16)
    hch = [P.sb("hch%d" % i, [128, KT, 516], BF16) for i in range(2)]
    P.dma("pool", wu.t[:], IN["w_in_c"].ap()[:, 0:512].rearrange("(kt p) c -> p kt c", p=128), [], [wu], wu)
    ci = 0
    for r in range(4):
        for hf in range(2):
            hc = hch[ci % 2]
            ci += 1
            c0 = hf * 516
            for j, (k0, nk) in enumerate(HNP):
                P.dma("sp", hc.t[:, k0:k0 + nk, :], hn_dst[j].t.ap()[r * nk * 128:(r + 1) * nk * 128, c0:c0 + 516].rearrange("(k p) t -> p k t", p=128),
                      [hn_dst[j]], [hc], hc)
            for ft in range(4):
                pk = next_pf()
                fns = [(lambda h, pk=pk, kt=kt, ft=ft, hc=hc: h.matmul(pk.t[:, 0:512], wu.t[:, kt, ft * 128:(ft + 1) * 128], hc.t[:, kt, 0:512],
                                                                      start=(kt == 0), stop=(kt == KT - 1))) for kt in range(KT)]
                P.mm(fns, [wu, hc], [pk])
                P.op("act", lambda h, pk=pk, ft=ft, r=r, c0=c0: h.activation(uT.t[:, ft, r * NCH + c0:r * NCH + c0 + 512], pk.t[:, 0:512], AF.Identity), [pk], [uT])
                pk2 = next_pf()
                fns2 = [(lambda h, pk2=pk2, kt=kt, ft=ft, hc=hc: h.matmul(pk2.t[:, 0:4], wu.t[:, kt, ft * 128:(ft + 1) * 128], hc.t[:, kt, 512:516],
                                                                         start=(kt == 0), stop=(kt == KT - 1))) for kt in range(KT)]
                P.mm(fns2, [wu, hc], [pk2])
                P.op("act", lambda h, pk2=pk2, ft=ft, r=r, c0=c0: h.activation(uT.t[:, ft, r * NCH + c0 + 512:r * NCH + c0 + 516], pk2.t[:, 0:4], AF.Identity), [pk2], [uT])

    if STOP == 'U':
        P.barrier()
        return
    P.barrier()
    P.off = L1
    def small(name, shape, dt=F32):
        return P.sb(name, shape, dt)
    lre_s = small("lre_s", [128, 16]); lim_s = small("lim_s", [128, 16]); lst_s = small("lst_s", [128, 16])
    lre_r = small("lre_r", [128, 256]); lim_r = small("lim_r", [128, 256]); lst_r = small("lst_r", [128, 256])
    bre_r = small("bre_r", [128, 256]); bim_r = small("bim_r", [128, 256])
    cre_s = small("cre_s", [128, 16, 16]); cim_s = small("cim_s", [128, 16, 16])
    rmask = small("rmask", [128, 8]); smask = small("smask", [128, 2])
    sre0 = small("sre0", [128, 4, 16]); sim0 = small("sim0", [128, 4, 16])
    iota = small("iota", [128, 1024])
    for b_, n in [(lre_s, "lre_s"), (lim_s, "lim_s"), (lst_s, "lst_s"), (cre_s, "cre_s"), (cim_s, "cim_s"),
                  (rmask, "rmask"), (smask, "smask"), (sre0, "sre0"), (sim0, "sim0"), (iota, "iota")]:
        P.dma("sp", b_.t[:], IN[n].ap(), [], [b_], b_)
    for b_, n in [(lre_r, "lre_r"), (lim_r, "lim_r"), (lst_r, "lst_r"), (bre_r, "bre_r"), (bim_r, "bim_r")]:
        P.dma("sp", b_.t[:], IN[n].ap().rearrange("p a b -> p (a b)"), [], [b_], b_)
    negpi = small("negpi", [128, 1])
    P.op("dve", lambda h: h.memset(negpi.t[:], -math.pi), [], [negpi])

    I32 = mybir.dt.int32
    tq = small("tq", [128, 1024]); tiq = small("tiq", [128, 1024], I32)
    halfpi = small("halfpi", [128, 1]); zero_t = small("zero_t", [128, 1])
    P.op("dve", lambda h: h.memset(halfpi.t[:], 0.5 * math.pi), [], [halfpi])
    P.op("dve", lambda h: h.memset(zero_t.t[:], 0.0), [], [zero_t])

    def sincos(ang_ap, n, s_ap, c_ap, rd, wr):
        tf = tiq.t[:, 0:n].bitcast(F32)
        P.op("dve", lambda h: h.tensor_scalar(tq.t[:, 0:n], ang_ap, 1.0 / TWO_PI, None, ALU.mult), rd, [tq])
        P.op("dve", lambda h: h.tensor_copy(tiq.t[:, 0:n], tq.t[:, 0:n]), [tq], [tiq])
        P.op("dve", lambda h: h.tensor_copy(tq.t[:, 0:n], tiq.t[:, 0:n]), [tiq], [tq])
        P.op("dve", lambda h: h.scalar_tensor_tensor(tq.t[:, 0:n], tq.t[:, 0:n], -TWO_PI, ang_ap, ALU.mult, ALU.add), [tq] + rd, [tq])
        P.op("dve", lambda h: h.tensor_scalar(tq.t[:, 0:n], tq.t[:, 0:n], -math.pi, math.pi, ALU.max, ALU.min), [tq], [tq])
        P.op("act", lambda h: h.activation(s_ap, tq.t[:, 0:n], AF.Sin, bias=zero_t.t[:, 0:1], scale=1.0), [tq, zero_t], wr)
        P.op("dve", lambda h: h.scalar_tensor_tensor(tf, tq.t[:, 0:n], -1.0, tq.t[:, 0:n], ALU.mult, ALU.max), [tq], [tiq])
        P.op("act", lambda h: h.activation(c_ap, tf, AF.Sin, bias=halfpi.t[:, 0:1], scale=-1.0), [tiq, halfpi], wr)

    def disc(lre, lim, lst, n, pref):
        o = {}
        for nm in ["step", "mag", "th", "c", "s", "tmp", "nr", "den", "cr", "ci", "a", "b"]:
            o[nm] = small(pref + nm, [128, n])
        al = [o[k] for k in o] + [lre, lim, lst]
        P.op("act", lambda h: h.activation(o["step"].t[:], lst.t[:, 0:n], AF.Exp), al, al)
        P.op("dve", lambda h: h.tensor_tensor(o["th"].t[:], lim.t[:, 0:n], o["step"].t[:], ALU.mult), al, al)
        P.op("dve", lambda h: h.tensor_tensor(o["a"].t[:], lre.t[:, 0:n], o["step"].t[:], ALU.mult), al, al)
        P.op("act", lambda h: h.activation(o["mag"].t[:], o["a"].t[:], AF.Exp), al, al)
        sincos(o["th"].t[:], n, o["s"].t[:], o["c"].t[:], al, al)
        P.op("dve", lambda h: h.tensor_tensor(o["a"].t[:], o["mag"].t[:], o["c"].t[:], ALU.mult), al, al)
        P.op("dve", lambda h: h.tensor_scalar(o["nr"].t[:], o["a"].t[:], 1.0, -1.0, ALU.mult, ALU.add), al, al)
        P.op("dve", lambda h: h.tensor_tensor(o["b"].t[:], o["mag"].t[:], o["s"].t[:], ALU.mult), al, al)
        P.op("dve", lambda h: h.tensor_tensor(o["den"].t[:], lre.t[:, 0:n], lre.t[:, 0:n], ALU.mult), al, al)
        P.op("dve", lambda h: h.tensor_tensor(o["tmp"].t[:], lim.t[:, 0:n], lim.t[:, 0:n], ALU.mult), al, al)
        P.op("dve", lambda h: h.tensor_tensor(o["den"].t[:], o["den"].t[:], o["tmp"].t[:], ALU.add), al, al)
        P.op("dve", lambda h: h.reciprocal(o["den"].t[:], o["den"].t[:]), al, al)
        P.op("dve", lambda h: h.tensor_tensor(o["cr"].t[:], o["nr"].t[:], lre.t[:, 0:n], ALU.mult), al, al)
        P.op("dve", lambda h: h.tensor_tensor(o["tmp"].t[:], o["b"].t[:], lim.t[:, 0:n], ALU.mult), al, al)
        P.op("dve", lambda h: h.tensor_tensor(o["cr"].t[:], o["cr"].t[:], o["tmp"].t[:], ALU.add), al, al)
        P.op("dve", lambda h: h.tensor_tensor(o["cr"].t[:], o["cr"].t[:], o["den"].t[:], ALU.mult), al, al)
        P.op("dve", lambda h: h.tensor_tensor(o["ci"].t[:], o["b"].t[:], lre.t[:, 0:n], ALU.mult), al, al)
        P.op("dve", lambda h: h.tensor_tensor(o["tmp"].t[:], o["nr"].t[:], lim.t[:, 0:n], ALU.mult), al, al)
        P.op("dve", lambda h: h.tensor_tensor(o["ci"].t[:], o["ci"].t[:], o["tmp"].t[:], ALU.subtract), al, al)
        P.op("dve", lambda h: h.tensor_tensor(o["ci"].t[:], o["ci"].t[:], o["den"].t[:], ALU.mult), al, al)
        return o, al

    ds_, als = disc(lre_s, lim_s, lst_s, 16, "ds_")
    dr_, alr = disc(lre_r, lim_r, lst_r, 256, "dr_")
    bbr = small("bbr", [128, 256]); bbi = small("bbi", [128, 256]); tmpr = small("tmpr", [128, 256])
    alr2 = alr + [bbr, bbi, tmpr, bre_r, bim_r]
    P.op("dve", lambda h: h.tensor_tensor(bbr.t[:], dr_["cr"].t[:], bre_r.t[:], ALU.mult), alr2, alr2)
    P.op("dve", lambda h: h.tensor_tensor(tmpr.t[:], dr_["ci"].t[:], bim_r.t[:], ALU.mult), alr2, alr2)
    P.op("dve", lambda h: h.tensor_tensor(bbr.t[:], bbr.t[:], tmpr.t[:], ALU.subtract), alr2, alr2)
    P.op("dve", lambda h: h.tensor_tensor(bbi.t[:], dr_["cr"].t[:], bim_r.t[:], ALU.mult), alr2, alr2)
    P.op("dve", lambda h: h.tensor_tensor(tmpr.t[:], dr_["ci"].t[:], bre_r.t[:], ALU.mult), alr2, alr2)
    P.op("dve", lambda h: h.tensor_tensor(bbi.t[:], bbi.t[:], tmpr.t[:], ALU.add), alr2, alr2)
    BbT = [small("BbT%d" % ri, [128, 16, 128], BF16) for ri in range(2)]
    for ri, src in enumerate([bbr, bbi]):
        for qq in range(4):
            for g2 in range(2):
                m = rmask.t[:, qq * 2 + g2:qq * 2 + g2 + 1]
                o_ap = BbT[ri].t[:].rearrange("p (ft q) c -> p ft q c", q=4)[:, :, qq, g2 * 64:(g2 + 1) * 64]
                i_ap = src.t[:].rearrange("p (ft d) -> p ft d", ft=4)
                P.op("dve", lambda h, o_ap=o_ap, i_ap=i_ap, m=m: h.tensor_scalar(o_ap, i_ap, m, None, ALU.mult), alr2 + [rmask], [BbT[ri]])
    CT = [small("CT%d" % ri, [128, 16, 128], BF16) for ri in range(2)]
    for ri in range(2):
        P.op("dve", lambda h, ri=ri: h.memset(CT[ri].t[:], 0.0), [], [CT[ri]])
    for ri, (src, sgn) in enumerate([(cre_s, 1.0), (cim_s, -1.0)]):
        for pair in range(16):
            qq = pair % 4
            for g2 in range(2):
                m = smask.t[:, g2:g2 + 1]
                col = qq * 32 + g2 * 16
                P.op("dve", lambda h, ri=ri, pair=pair, col=col, m=m, src=src, sgn=sgn: h.tensor_scalar(
                    CT[ri].t[:, pair, col:col + 16], src.t[:, pair, :], m, sgn, ALU.mult, ALU.mult), [src, smask], [CT[ri]])
    rr = ds_["mag"]; th = ds_["th"]
    cth = small("cth", [128, 16]); sth = small("sth", [128, 16])
    P.op("dve", lambda h: h.tensor_copy(cth.t[:], ds_["c"].t[:]), als, [cth])
    P.op("dve", lambda h: h.tensor_copy(sth.t[:], ds_["s"].t[:]), als, [sth])

    tabc = [small("tabc%d" % i, [128, 1024]) for i in range(4)]
    tabs = [small("tabs%d" % i, [128, 1024]) for i in range(4)]
    WS = [dict(gr=small("gr0", [128, 1024]), gi=small("gi0", [128, 1024]), yr=small("yr0", [128, 1024]), yi=small("yi0", [128, 1024]),
               hrb=small("hrb0", [128, 1024], BF16), hib=small("hib0", [128, 1024], BF16))]
    hib1 = small("hib1", [128, 1024], BF16)
    ytmp = small("ytmp", [128, 512]); ysq = small("ysq", [128, 512])
    stp_ft = [small("stpf%d" % f_, [128, 4, 2]) for f_ in range(4)]
    g4s = [small("g4_%d" % i_, [128, 4, 2]) for i_ in range(2)]
    tmp4 = small("tmp4", [128, 4])
    sts_p = [small("stsp%d" % p_, [128, 4, 2]) for p_ in range(16)]
    m32 = small("m32", [128, 32])
    P.op("dve", lambda h: h.memset(m32.t[:], 1.0), [], [m32])
    P.op("dve", lambda h: h.memset(m32.t[:].rearrange("p (r c) -> p r c", r=4)[:, :, 0], 0.0), [m32], [m32])
    for w_ in WS:
        w_["gin"] = small("gin0", [128, 2]); w_["hend"] = small("hend0", [128, 2])
        w_["gin4"] = small("gin40", [128, 4, 2]); w_["d0"] = small("d00", [128, 32])
    for f_ in range(4):
        P.op("dve", lambda h, f_=f_: h.memset(stp_ft[f_].t[:], 0.0), [], [stp_ft[f_]])
    P.barrier()
    blkA = lre_r.at
    blkB = dr_["step"].at
    WS.append(dict(gr=P.sb("gr1", [128, 1024], F32, at=blkB), gi=P.sb("gi1", [128, 1024], F32, at=blkB + 4096),
                   yr=P.sb("yr1", [128, 1024], F32, at=blkB + 8192), yi=P.sb("yi1", [128, 1024], F32, at=blkA),
                   hrb=P.sb("hrb1", [128, 1024], BF16, at=blkB + 12288), hib=hib1,
                   gin=small("gin1", [128, 2]), hend=small("hend1", [128, 2]),
                   gin4=small("gin41", [128, 4, 2]), d0=small("d01", [128, 32])))
    ang = WS[0]["gr"]
    GC = 1.5957691216057308
    kcount = [0]

    def stageX(pair, qq, ft, col0, T, tc_, ts_, g4, out_re, out_im, out_b):
        w = WS[kcount[0] % 2]
        kcount[0] += 1
        gr, gi, yr, yi, hrb, hib, gin, hend = w["gr"], w["gi"], w["yr"], w["yi"], w["hrb"], w["hib"], w["gin"], w["hend"]
        for h0 in range(0, T, 512):
            n = min(512, T - h0)
            pxr = next_pf(); pxi = next_pf()
            P.mm([lambda h, pxr=pxr, n=n, h0=h0: h.matmul(pxr.t[:, 0:n], BbT[0].t[:, pair, :], uT.t[:, ft, col0 + h0:col0 + h0 + n], start=True, stop=True)], [BbT[0], uT], [pxr])
            P.mm([lambda h, pxi=pxi, n=n, h0=h0: h.matmul(pxi.t[:, 0:n], BbT[1].t[:, pair, :], uT.t[:, ft, col0 + h0:col0 + h0 + n], start=True, stop=True)], [BbT[1], uT], [pxi])
            c_ = tc_.t[:, h0:h0 + n]; s_ = ts_.t[:, h0:h0 + n]
            P.op("dve", lambda h, pxr=pxr, c_=c_, n=n, h0=h0: h.tensor_tensor(yr.t[:, h0:h0 + n], pxr.t[:, 0:n], c_, ALU.mult), [pxr, tc_], [yr])
            P.op("dve", lambda h, pxi=pxi, s_=s_, n=n, h0=h0: h.tensor_tensor(gr.t[:, h0:h0 + n], pxi.t[:, 0:n], s_, ALU.mult), [pxi, ts_], [gr])
            P.op("dve", lambda h, n=n, h0=h0: h.tensor_tensor(yr.t[:, h0:h0 + n], yr.t[:, h0:h0 + n], gr.t[:, h0:h0 + n], ALU.add), [yr, gr], [yr])
            P.op("dve", lambda h, pxi=pxi, c_=c_, n=n, h0=h0: h.tensor_tensor(yi.t[:, h0:h0 + n], pxi.t[:, 0:n], c_, ALU.mult), [pxi, tc_], [yi])
            P.op("dve", lambda h, pxr=pxr, s_=s_, n=n, h0=h0: h.tensor_tensor(gi.t[:, h0:h0 + n], pxr.t[:, 0:n], s_, ALU.mult), [pxr, ts_], [gi])
            P.op("dve", lambda h, n=n, h0=h0: h.tensor_tensor(yi.t[:, h0:h0 + n], yi.t[:, h0:h0 + n], gi.t[:, h0:h0 + n], ALU.subtract), [yi, gi], [yi])
        rb = rr.t[:, pair:pair + 1].to_broadcast([128, T])
        P.op("dve", lambda h: h.tensor_tensor_scan(gr.t[:, 0:T], rb, yr.t[:, 0:T], g4.t[:, qq, 0:1], ALU.mult, ALU.add), [yr, g4] + als, [gr])
        P.op("dve", lambda h: h.tensor_tensor_scan(gi.t[:, 0:T], rb, yi.t[:, 0:T], g4.t[:, qq, 1:2], ALU.mult, ALU.add), [yi, g4] + als, [gi])
        c_ = tc_.t[:, 0:T]; s_ = ts_.t[:, 0:T]
        yrb = yr.t[:].bitcast(BF16)
        yib = yi.t[:].bitcast(BF16)
        p1 = yrb[:, 0:T]; p2 = yrb[:, 1024:1024 + T]; p3 = yib[:, 0:T]; p4 = yib[:, 1024:1024 + T]
        P.op("dve", lambda h: h.tensor_tensor(p1, gr.t[:, 0:T], c_, ALU.mult), [gr, tc_], [yr])
        P.op("dve", lambda h: h.scalar_tensor_tensor(p2, gi.t[:, 0:T], -1.0, s_, ALU.mult, ALU.mult), [gi, ts_], [yr])
        P.op("dve", lambda h: h.tensor_tensor(p3, gr.t[:, 0:T], s_, ALU.mult), [gr, ts_], [yi])
        P.op("dve", lambda h: h.tensor_tensor(p4, gi.t[:, 0:T], c_, ALU.mult), [gi, tc_], [yi])
        cl = tc_.t[:, T - 1:T]; sl = ts_.t[:, T - 1:T]
        P.op("dve", lambda h: h.tensor_tensor(gin.t[:, 0:1], gi.t[:, T - 1:T], sl, ALU.mult), [gi, ts_, gin], [gin])
        P.op("dve", lambda h: h.tensor_tensor(gin.t[:, 1:2], gi.t[:, T - 1:T], cl, ALU.mult), [gi, tc_, gin], [gin])
        P.op("dve", lambda h: h.scalar_tensor_tensor(out_re, gr.t[:, T - 1:T], cl, gin.t[:, 0:1], ALU.mult, ALU.subtract), [gr, tc_, gin], [out_b])
        P.op("dve", lambda h: h.scalar_tensor_tensor(out_im, gr.t[:, T - 1:T], sl, gin.t[:, 1:2], ALU.mult, ALU.add), [gr, ts_, gin], [out_b])
        return dict(pair=pair, qq=qq, T=T, prods=[(0, lambda h0, n: yrb[:, h0:h0 + n], yr), (0, lambda h0, n: yrb[:, 1024 + h0:1024 + h0 + n], yr),
                                                  (1, lambda h0, n: yib[:, h0:h0 + n], yi), (1, lambda h0, n: yib[:, 1024 + h0:1024 + h0 + n], yi)], after=[])

    def v48(ap):
        return ap.rearrange("p (r c) -> p r c", r=4)

    def stageXs(pair, qq, ft, tc_, ts_):
        w = WS[kcount[0] % 2]
        kcount[0] += 1
        gr, gi, yr, yi, hrb, hib, gin4, d0 = w["gr"], w["gi"], w["yr"], w["yi"], w["hrb"], w["hib"], w["gin4"], w["d0"]
        ucols = v48(uT.t[:, ft, :])[:, :, 1024:1032]
        pxr = next_pf(); pxi = next_pf()
        P.mm([lambda h: h.matmul(pxr.t[:, 0:32], BbT[0].t[:, pair, :], ucols, start=True, stop=True)], [BbT[0], uT], [pxr])
        P.mm([lambda h: h.matmul(pxi.t[:, 0:32], BbT[1].t[:, pair, :], ucols, start=True, stop=True)], [BbT[1], uT], [pxi])
        cb = tc_.t[:, 0:8].unsqueeze(1).to_broadcast([128, 4, 8])
        sb_ = ts_.t[:, 0:8].unsqueeze(1).to_broadcast([128, 4, 8])
        xr = v48(pxr.t[:, 0:32]); xi = v48(pxi.t[:, 0:32])
        yrv = v48(yr.t[:, 0:32]); yiv = v48(yi.t[:, 0:32]); grv = v48(gr.t[:, 0:32]); giv = v48(gi.t[:, 0:32])
        P.op("dve", lambda h: h.tensor_tensor(yrv, xr, cb, ALU.mult), [pxr, tc_], [yr])
        P.op("dve", lambda h: h.tensor_tensor(grv, xi, sb_, ALU.mult), [pxi, ts_], [gr])
        P.op("dve", lambda h: h.tensor_tensor(yrv, yrv, grv, ALU.add), [yr, gr], [yr])
        P.op("dve", lambda h: h.tensor_tensor(yiv, xi, cb, ALU.mult), [pxi, tc_], [yi])
        P.op("dve", lambda h: h.tensor_tensor(giv, xr, sb_, ALU.mult), [pxr, ts_], [gi])
        P.op("dve", lambda h: h.tensor_tensor(yiv, yiv, giv, ALU.subtract), [yi, gi], [yi])
        ct = cth.t[:, pair:pair + 1]; st_ = sth.t[:, pair:pair + 1]; rs = rr.t[:, pair:pair + 1]
        ire = sre0.t[:, :, pair]; iim = sim0.t[:, :, pair]
        g_re = gin4.t[:, :, 0]; g_im = gin4.t[:, :, 1]
        P.op("dve", lambda h: h.tensor_scalar(g_re, ire, ct, None, ALU.mult), [sre0, cth], [gin4])
        P.op("dve", lambda h: h.scalar_tensor_tensor(g_re, iim, st_, g_re, ALU.mult, ALU.subtract), [sim0, sth, gin4], [gin4])
        P.op("dve", lambda h: h.tensor_scalar(g_re, g_re, -1.0, None, ALU.mult), [gin4], [gin4])
        P.op("dve", lambda h: h.tensor_scalar(g_im, ire, st_, None, ALU.mult), [sre0, sth, gin4], [gin4])
        P.op("dve", lambda h: h.scalar_tensor_tensor(g_im, iim, ct, g_im, ALU.mult, ALU.add), [sim0, cth, gin4], [gin4])
        P.op("dve", lambda h: h.scalar_tensor_tensor(yrv[:, :, 0], g_re, rs, yrv[:, :, 0], ALU.mult, ALU.add), [gin4, yr] + als, [yr])
        P.op("dve", lambda h: h.scalar_tensor_tensor(yiv[:, :, 0], g_im, rs, yiv[:, :, 0], ALU.mult, ALU.add), [gin4, yi] + als, [yi])
        P.op("dve", lambda h: h.tensor_scalar(d0.t[:], m32.t[:], rs, None, ALU.mult), [m32] + als, [d0])
        P.op("dve", lambda h: h.tensor_tensor_scan(gr.t[:, 0:32], d0.t[:], yr.t[:, 0:32], 0.0, ALU.mult, ALU.add), [yr, d0], [gr])
        P.op("dve", lambda h: h.tensor_tensor_scan(gi.t[:, 0:32], d0.t[:], yi.t[:, 0:32], 0.0, ALU.mult, ALU.add), [yi, d0], [gi])
        hrv = v48(hrb.t[:, 0:32]); hiv = v48(hib.t[:, 0:32])
        out_b = sts_p[pair]
        P.op("dve", lambda h: h.tensor_tensor(yrv, grv, cb, ALU.mult), [gr, tc_], [yr])
        P.op("dve", lambda h: h.tensor_tensor(yiv, giv, sb_, ALU.mult), [gi, ts_], [yi])
        P.op("dve", lambda h: h.tensor_tensor(hrv, yrv, yiv, ALU.subtract), [yr, yi], [hrb])
        P.op("dve", lambda h: h.tensor_tensor(out_b.t[:, :, 0], yrv[:, :, 7], yiv[:, :, 7], ALU.subtract), [yr, yi], [out_b])
        P.op("dve", lambda h: h.tensor_tensor(yrv, grv, sb_, ALU.mult), [gr, ts_, hrb, out_b], [yr])
        P.op("dve", lambda h: h.tensor_tensor(yiv, giv, cb, ALU.mult), [gi, tc_, hrb, out_b], [yi])
        P.op("dve", lambda h: h.tensor_tensor(hiv, yrv, yiv, ALU.add), [yr, yi], [hib])
        P.op("dve", lambda h: h.tensor_tensor(out_b.t[:, :, 1], yrv[:, :, 7], yiv[:, :, 7], ALU.add), [yr, yi], [out_b])
        return dict(pair=pair, qq=qq, T=32, hrb=hrb, hib=hib, after=[])

    def y_evac_s(ft):
        ucols = v48(uT.t[:, ft, :])[:, :, 1024:1032]
        dcols = v48(ygT.t[:, ft, :])[:, :, 1024:1032]
        yp_ = ypb[0]
        ypv = v48(yp_.t[:, 0:32]); yt = v48(ytmp.t[:, 0:32]); yq = v48(ysq.t[:, 0:32])
        P.op("dve", lambda h: h.scalar_tensor_tensor(yt, ucols, dsk.t[:, ft:ft + 1], ypv, ALU.mult, ALU.add), [uT, dsk, yp_], [ytmp])
        P.op("dve", lambda h: h.tensor_tensor(yq, yt, yt, ALU.mult), [ytmp], [ysq])
        P.op("dve", lambda h: h.tensor_scalar(yq, yq, 0.044715, 1.0, ALU.mult, ALU.add), [ysq], [ysq])
        P.op("dve", lambda h: h.tensor_tensor(yq, yq, yt, ALU.mult), [ysq, ytmp], [ysq])
        P.op("act", lambda h: h.activation(ysq.t[:, 0:32], ysq.t[:, 0:32], AF.Sigmoid, scale=GC), [ysq], [ysq])
        P.op("dve", lambda h: h.tensor_tensor(dcols, yq, yt, ALU.mult), [ysq, ytmp], [ygT])

    ypb = [pf[4], pf[5]]

    def stageY(c):
        if "prods" in c:
            pair, qq, T = c["pair"], c["qq"], c["T"]
            for h0 in range(0, T, 512):
                n = min(512, T - h0)
                yp_ = ypb[h0 // 512]
                fns = []
                for k_, (ci_, rf_, _b) in enumerate(c["prods"]):
                    fns.append(lambda h, yp_=yp_, n=n, h0=h0, ci_=ci_, rf_=rf_, k_=k_: h.matmul(
                        yp_.t[:, 0:n], CT[ci_].t[:, pair, :], rf_(h0, n), start=(qq == 0 and k_ == 0), stop=(qq == 3 and k_ == 3)))
                P.mm(fns, [CT[0], CT[1], c["prods"][0][2], c["prods"][2][2]], [yp_])
            for f in c["after"]:
                f()
            return
        pair, qq, T, hrb, hib = c["pair"], c["qq"], c["T"], c["hrb"], c["hib"]
        for h0 in range(0, T, 512):
            n = min(512, T - h0)
            yp_ = ypb[h0 // 512]
            P.mm([lambda h, yp_=yp_, n=n, h0=h0: h.matmul(yp_.t[:, 0:n], CT[0].t[:, pair, :], hrb.t[:, h0:h0 + n], start=(qq == 0), stop=False),
                  lambda h, yp_=yp_, n=n, h0=h0: h.matmul(yp_.t[:, 0:n], CT[1].t[:, pair, :], hib.t[:, h0:h0 + n], start=False, stop=(qq == 3))],
                 [CT[0], CT[1], hrb, hib], [yp_])
        for f in c["after"]:
            f()

    def y_evac(yp_, ft, col0, n):
        P.op("dve", lambda h: h.scalar_tensor_tensor(ytmp.t[:, 0:n], uT.t[:, ft, col0:col0 + n], dsk.t[:, ft:ft + 1], yp_.t[:, 0:n], ALU.mult, ALU.add),
             [uT, dsk, yp_], [ytmp])
        P.op("dve", lambda h: h.tensor_tensor(ysq.t[:, 0:n], ytmp.t[:, 0:n], ytmp.t[:, 0:n], ALU.mult), [ytmp], [ysq])
        P.op("dve", lambda h: h.tensor_scalar(ysq.t[:, 0:n], ysq.t[:, 0:n], 0.044715, 1.0, ALU.mult, ALU.add), [ysq], [ysq])
        P.op("dve", lambda h: h.tensor_tensor(ysq.t[:, 0:n], ysq.t[:, 0:n], ytmp.t[:, 0:n], ALU.mult), [ysq, ytmp], [ysq])
        P.op("act", lambda h: h.activation(ysq.t[:, 0:n], ysq.t[:, 0:n], AF.Sigmoid, scale=GC), [ysq], [ysq])
        P.op("dve", lambda h: h.tensor_tensor(ygT.t[:, ft, col0:col0 + n], ysq.t[:, 0:n], ytmp.t[:, 0:n], ALU.mult), [ysq, ytmp], [ygT])

    pending = [None]

    def push(ctx):
        if pending[0] is not None:
            stageY(pending[0])
        pending[0] = ctx

    for ft in range(4):
        for qq in range(4):
            pair = ft * 4 + qq
            P.op("dve", lambda h, pair=pair: h.tensor_scalar(ang.t[:], iota.t[:], th.t[:, pair:pair + 1], None, ALU.mult), [iota] + als, [ang])
            sincos(ang.t[:], 1024, tabs[qq].t[:], tabc[qq].t[:], [ang], [tabs[qq], tabc[qq]])
        for seg in range(4):
            sf = stp_ft[ft]
            g4 = g4s[seg % 2]
            re4 = sf.t[:, :, 0]; im4 = sf.t[:, :, 1]
            c4 = cth.t[:, ft * 4:(ft + 1) * 4]; s4 = sth.t[:, ft * 4:(ft + 1) * 4]
            P.op("dve", lambda h, g4=g4, re4=re4, c4=c4: h.tensor_tensor(g4.t[:, :, 0], re4, c4, ALU.mult), [sf, cth], [g4])
            P.op("dve", lambda h, im4=im4, s4=s4: h.tensor_tensor(tmp4.t[:], im4, s4, ALU.mult), [sf, sth], [tmp4])
            P.op("dve", lambda h, g4=g4: h.tensor_tensor(g4.t[:, :, 0], g4.t[:, :, 0], tmp4.t[:], ALU.subtract), [g4, tmp4], [g4])
            P.op("dve", lambda h, g4=g4, re4=re4, s4=s4: h.tensor_tensor(g4.t[:, :, 1], re4, s4, ALU.mult), [sf, sth, tmp4], [g4])
            P.op("dve", lambda h, im4=im4, c4=c4: h.tensor_tensor(tmp4.t[:], im4, c4, ALU.mult), [sf, cth, g4], [tmp4])
            P.op("dve", lambda h, g4=g4: h.tensor_tensor(g4.t[:, :, 1], g4.t[:, :, 1], tmp4.t[:], ALU.add), [g4, tmp4], [g4])
            for qq in range(4):
                pair = ft * 4 + qq
                ctx = stageX(pair, qq, ft, seg * NCH, 1024, tabc[qq], tabs[qq],
                             g4, sf.t[:, qq, 0:1], sf.t[:, qq, 1:2], sf)
                if qq == 3:
                    ctx["after"] = [(lambda ft=ft, seg=seg, hf=hf: y_evac(ypb[hf], ft, seg * NCH + hf * 512, 512)) for hf in range(2)]
                push(ctx)
        for qq in range(4):
            pair = ft * 4 + qq
            ctx = stageXs(pair, qq, ft, tabc[qq], tabs[qq])
            if qq == 3:
                ctx["after"] = [(lambda ft=ft: y_evac_s(ft))]
            push(ctx)
    push(None)
    stp2 = P.sb("stp2", [128, 2, 16], F32, at=tq.at); sts2 = P.sb("sts2", [128, 4, 2, 16], F32, at=tq.at + 128)
    for p_ in range(16):
        if p_ % 4 == 0:
            P.op("dve", lambda h, p_=p_: h.tensor_copy(stp2.t[:, :, p_:p_ + 4], stp_ft[p_ // 4].t[:].rearrange("p q r -> p r q")), [stp_ft[p_ // 4]], [stp2])
        P.op("dve", lambda h, p_=p_: h.tensor_copy(sts2.t[:, :, :, p_], sts_p[p_].t[:]), [sts_p[p_]], [sts2])
    if STOP == 'SSM':
        for r_ in range(4):
            P.dma("pool", OUT["yp"].ap().rearrange("(p f x) c -> p f (x c)", p=128, f=4)[:, :, r_ * 1024:(r_ + 1) * 1024],
                  ygT.t[:, :, r_ * NCH:r_ * NCH + 1024], [ygT], [outbufs["yp"]], ygT)
        P.dma("pool", OUT["ys"].ap()[:, 0:128].rearrange("p (f r s) -> p f r s", f=4, r=4),
              ygT.t[:].rearrange("p f (r c) -> p f r c", r=4)[:, :, :, 1024:1032], [ygT], [outbufs["ys"]], ygT)
    P.dma("sp", OUT["ssm_p"].ap(), stp2.t[:], [stp2], [outbufs["ssm_p"]], stp2)
    P.dma("sp", OUT["ssm_s"].ap(), sts2.t[:], [sts2], [outbufs["ssm_s"]], sts2)
    P.barrier()
    hn1o = P.sb("hn1o", [128, KT, NCH], BF16, at=uT.at)
    P.off = L1 + 40960
    zall = P.sb("zall", [128, KT, NCH], BF16)
    wz = [P.sb("wz%d" % i, [128, KT, 128], BF16) for i in range(2)]
    zz = P.sb("zz", [128, NCH], F32)
    for j in range(8):
        P.dma("sp", y_src[j].t.ap().rearrange("(ft p) t -> p ft t", p=128), ygT.t[:, :, j * 516:(j + 1) * 516], [ygT], [y_src[j]], ygT)
        P.coll(y_src[j], y_dst[j], GROUPS)

    for j, (k0, nk) in enumerate(HNP):
        P.dma("sp", hn1o.t[:, k0:k0 + nk, :], hn_src[j].t.ap().rearrange("(k p) t -> p k t", p=128), [hn_src[j]], [hn1o], hn1o)
    for nt in range(16):
        wb_ = wz[nt % 2]
        P.dma("pool", wb_.t[:], IN["w_in_c"].ap()[:, 2048 + nt * 128:2048 + (nt + 1) * 128].rearrange("(kt p) c -> p kt c", p=128), [], [wb_], wb_)
        for (c0, n) in [(0, 512), (512, 512), (1024, 8)]:
            def zev(pk, c0=c0, n=n, nt=nt):
                P.op("act", lambda h: h.activation(zz.t[:, c0:c0 + n], pk.t[:, 0:n], AF.Sigmoid), [pk], [zz])
                P.op("dve", lambda h: h.tensor_tensor(zall.t[:, nt, c0:c0 + n], zz.t[:, c0:c0 + n], pk.t[:, 0:n], ALU.mult), [zz, pk], [zall])
            proj_feat(wb_, None, zev, hn1o, lambda kt, c0=c0, n=n: hn1o.t[:, kt, c0:c0 + n], n)

    if STOP == 'SSM':
        P.barrier()
        return
    P.barrier()
    P.off = CONST_END
    y2T = P.sb("y2T", [128, KT, NO], BF16)
    F0 = P.off
    ygo = P.sb("ygo", [128, KT, NCH], BF16)
    ych = [P.sb("ych%d" % i, [128, 4, NCH], BF16) for i in range(2)]
    wt2 = [P.sb("wt2_%d" % i, [128, KT, 128], BF16) for i in range(2)]
    gl = P.sb("gl", [128, NCH], F32)
    assert P.off <= L1 + 40960
    P.op("pool", lambda h: h.memset(y2T.t[:, :, NCH:NO], 0.0), [], [y2T])
    ci = 0
    for rf in range(4):
        for r in range(4):
            yc = ych[ci % 2]; ci += 1
            for hh_ in range(2):
                P.dma("sp", yc.t[:, :, hh_ * 516:(hh_ + 1) * 516],
                      y_dst[2 * r + hh_].t.ap()[rf * 512:(rf + 1) * 512, :].rearrange("(ft p) t -> p ft t", p=128),
                      [y_dst[2 * r + hh_]], [yc], yc)
            dst = ygo.t[:, rf * 4:(rf + 1) * 4, :]
            if r == 0:
                P.op("dve", lambda h, yc=yc, dst=dst: h.tensor_scalar(dst, yc.t[:], sel.t[:, 0:1], None, ALU.mult), [yc, sel], [ygo])
            else:
                P.op("dve", lambda h, yc=yc, dst=dst, r=r: h.scalar_tensor_tensor(dst, yc.t[:], sel.t[:, r:r + 1], dst, ALU.mult, ALU.add), [yc, sel, ygo], [ygo])
    if STOP == 'G1':
        P.barrier()
        return
    for nt in range(int(os.environ.get('MK_NT', '16'))):
        wa = wt2[nt % 2]
        P.dma("pool", wa.t[:], IN["w_glu"].ap()[:, nt * 128:(nt + 1) * 128].rearrange("(kt p) c -> p kt c", p=128), [], [wa], wa)
        for (c0, n) in [(0, 512), (512, 512), (1024, 8)]:
            proj_feat(wa, None, lambda pk, c0=c0, n=n, nt=nt: P.op("act", lambda h: h.activation(gl.t[:, c0:c0 + n], pk.t[:, 0:n], AF.Sigmoid, bias=bglu.t[:, nt:nt + 1], scale=1.0), [pk, bglu], [gl]),
                      ygo, lambda kt, c0=c0, n=n: ygo.t[:, kt, c0:c0 + n], n)
        P.op("dve", lambda h, nt=nt: h.tensor_tensor(gl.t[:], gl.t[:], ygo.t[:, nt, :], ALU.mult), [gl, ygo], [gl])
        P.op("dve", lambda h, nt=nt: h.tensor_tensor(y2T.t[:, nt, 0:NCH], gl.t[:], zall.t[:, nt, :], ALU.mult), [gl, zall], [y2T])
    if STOP == 'G2':
        P.barrier()
        return
    P.barrier()
    P.off = F0
    wgb2 = [P.sb("wgc%d" % i, [128, KT, 512], BF16) for i in range(2)]
    h1s = [P.sb("h1f%d" % i, [128, D], F32) for i in range(5)]
    junk = P.sb("junk3", [128, D], F32); ss = P.sb("ss3", [128, 4], F32)
    yo = [P.sb("yo%d" % i, [128, D], F32) for i in range(2)]
    load_g("g_fin")
    wi = 0
    for tiles in [list(range(0, 5)), list(range(5, 9))]:
        for si, tj in enumerate(tiles):
            o0 = tj * 128
            P.dma("sp", h1s[si].t[:], h1_scr.t.ap()[o0:o0 + 128, :], [h1_scr], [h1s[si]], h1s[si])
        for g4 in range(4):
            wg = wgb2[wi % 2]
            wi += 1
            P.dma("pool", wg.t[:], IN["w_out_c"].ap()[:, g4 * 512:(g4 + 1) * 512].rearrange("(kt p) c -> p kt c", p=128), [], [wg], wg)
            for si, tj in enumerate(tiles):
                o0 = tj * 128
                ht_ = h1s[si]
                pk = next_pf()
                fns = [(lambda h, pk=pk, kt=kt, o0=o0, wg=wg: h.matmul(pk.t[:], y2T.t[:, kt, o0:o0 + 128], wg.t[:, kt, :],
                                                                         start=(kt == 0), stop=(kt == KT - 1))) for kt in range(KT)]
                P.mm(fns, [y2T, wg], [pk])
                P.op("dve", lambda h, pk=pk, ht_=ht_, g4=g4: h.tensor_tensor(ht_.t[:, g4 * 512:(g4 + 1) * 512], ht_.t[:, g4 * 512:(g4 + 1) * 512], pk.t[:], ALU.add),
                     [pk, ht_], [ht_])
        for si, tj in enumerate(tiles):
            o0 = tj * 128
            ht_ = h1s[si]
            yo_ = yo[tj % 2]
            rmsnorm_rows(ht_, yo_, ss, junk)
            if tj < 8:
                P.dma("sp", OUT["yp"].ap()[o0:o0 + 128, :], yo_.t[:], [yo_], [outbufs["yp"]], yo_)
            else:
                P.dma("sp", OUT["ys"].ap(), yo_.t[:], [yo_], [outbufs["ys"]], yo_)
    P.barrier()


_NC_CACHE = {}


def _rope_tables(pos):
    half = 64
    inv = (np.float32(10000.0) ** (-np.arange(half, dtype=np.float32) / np.float32(half))).astype(np.float32)
    ang = pos.astype(np.float32)[:, None] * inv[None, :]
    return np.cos(ang).astype(np.float32), np.sin(ang).astype(np.float32)


def kernel(x_prompt, x_sample, cache_win_k, cache_win_v, state_conv, state_ssm_re, state_ssm_im,
           attn_norm, w_in_ab, conv_w, w_out_ab, ssm_norm, w_in_c, lam_re, lam_im, log_step,
           b_re, b_im, c_re, c_im, d_skip, w_glu, b_glu, w_out_c, final_norm):
    f = lambda a: np.ascontiguousarray(np.asarray(a, dtype=np.float32))
    x_prompt, x_sample = f(x_prompt), f(x_sample)
    cache_win_k, cache_win_v, state_conv = f(cache_win_k), f(cache_win_v), f(state_conv)
    state_ssm_re, state_ssm_im = f(state_ssm_re), f(state_ssm_im)
    w_in_ab0, w_out_ab0, w_in_c0, w_glu0, w_out_c0 = f(w_in_ab)[0], f(w_out_ab)[0], f(w_in_c)[0], f(w_glu)[0], f(w_out_c)[0]
    lam_re, lam_im, log_step = f(lam_re)[0], f(lam_im)[0], f(log_step)[0]
    b_re, b_im, c_re, c_im = f(b_re)[0], f(b_im)[0], f(c_re)[0], f(c_im)[0]
    d_skip0, b_glu0 = f(d_skip)[0], f(b_glu)[0]
    if "nc" not in _NC_CACHE:
        _NC_CACHE["nc"] = build_nc()
    nc = _NC_CACHE["nc"]

    kk = np.arange(128)[:, None]
    qq_ = np.arange(512)[None, :]
    maskp = np.stack([mult_of(qq_ - ((i - 16) * 128 + kk)) for i in range(20)], 1)
    rows = np.arange(2176).reshape(17, 128)
    s_ = np.arange(128)[None, :]
    masks = np.zeros((128, 17, 128), np.float32)
    for i in range(17):
        row = rows[i][:, None]
        m = mult_of(2048 + s_ - row)
        m[:, 8:] = ((2048 + s_[:, 8:] - row) == 0)
        masks[:, i, :] = m
    iota = np.broadcast_to(np.arange(1024, dtype=np.float32)[None, :], (128, 1024)).copy()
    rmask = np.zeros((128, 8), np.float32)
    for p in range(128):
        rmask[p, (p // 32) * 2 + (p % 32) // 16] = 1.0
    smask = np.zeros((128, 2), np.float32)
    smask[:64, 0] = 1.0
    smask[64:, 1] = 1.0
    bc = lambda v: np.ascontiguousarray(np.broadcast_to(v[None, :], (128, v.shape[0])))

    in_maps = []
    for c in range(8):
        b, r = c // 4, c % 4
        T0 = r * NOWN
        xh = np.zeros((NTP, D), np.float32)
        lo = T0 - NHALO
        src_lo = max(lo, 0)
        xh[src_lo - lo:] = x_prompt[b, src_lo:T0 + NOWN]
        pos = np.concatenate([np.arange(lo, T0 + NOWN), PAST + np.arange(128)]).astype(np.float32)
        valid = (pos[:NTP] >= 0).astype(np.float32)
        cosv, sinv = _rope_tables(np.maximum(pos, 0))
        xs = np.zeros((128, D), np.float32)
        xs[:8] = x_sample[c]
        g0 = 32 * r
        gs = slice(g0, g0 + 32)
        st_lay = lambda a: np.ascontiguousarray(a.reshape(16, 2, 64).transpose(1, 2, 0).reshape(128, 16))
        def row_lay_rep(a):
            t = a.reshape(4, 4, 2, 64)
            t = np.broadcast_to(t[:, :, :, None, :], (4, 4, 2, 16, 64))
            return np.ascontiguousarray(t.transpose(1, 2, 3, 0, 4).reshape(128, 4, 64))
        def row_lay_b(a):
            t = a.reshape(4, 4, 2, 64, 16)
            return np.ascontiguousarray(t.transpose(1, 2, 4, 0, 3).reshape(128, 4, 64))
        def st_lay_c(a):
            t = a.reshape(16, 2, 16, 64)
            return np.ascontiguousarray(t.transpose(1, 3, 0, 2).reshape(128, 16, 16))
        lst32 = np.broadcast_to(log_step[gs][:, None], (32, 64))
        sel = np.zeros((128, 4), np.float32)
        sel[:, r] = 1.0
        sre0 = np.stack([st_lay(state_ssm_re[0, 4 * b + i, gs]) for i in range(4)], 1)
        sim0 = np.stack([st_lay(state_ssm_im[0, 4 * b + i, gs]) for i in range(4)], 1)
        w_in_c_rolled = np.concatenate([w_in_c0[:, 512 * r:512 * (r + 1)], w_in_c0[:, 512:2048], w_in_c0[:, 2048:]], 1)
        m = {
            "xh": xh, "xs": xs,
            "cs": np.ascontiguousarray(cosv.reshape(25, 128, 64).transpose(1, 0, 2)),
            "sn": np.ascontiguousarray(sinv.reshape(25, 128, 64).transpose(1, 0, 2)),
            "valid": np.ascontiguousarray(valid.reshape(24, 128).T),
            "ck": np.ascontiguousarray(cache_win_k[0, c].reshape(2048, 1024)),
            "cv": np.ascontiguousarray(cache_win_v[0, c].reshape(2048, 1024)),
            "sconv": np.ascontiguousarray(state_conv[0, c].reshape(2, 8, 128).transpose(2, 1, 0)),
            "g_attn": bc(f(attn_norm)[0]), "g_ssm": bc(f(ssm_norm)[0]), "g_fin": bc(f(final_norm)),
            "w_in_ab": w_in_ab0, "cw": np.ascontiguousarray(f(conv_w)[0].reshape(3, 8, 128).transpose(2, 1, 0)),
            "w_out_ab": w_out_ab0, "w_in_c": np.ascontiguousarray(w_in_c_rolled),
            "w_glu": w_glu0, "w_out_c": w_out_c0,
            "bglu": np.ascontiguousarray(b_glu0.reshape(16, 128).T),
            "dsk": np.ascontiguousarray(d_skip0[512 * r:512 * (r + 1)].reshape(4, 128).T),
            "maskp": maskp, "masks": masks,
            "lre_s": st_lay(lam_re[gs]), "lim_s": st_lay(lam_im[gs]), "lst_s": st_lay(lst32),
            "lre_r": row_lay_rep(lam_re[gs]), "lim_r": row_lay_rep(lam_im[gs]), "lst_r": row_lay_rep(np.ascontiguousarray(lst32)),
            "bre_r": row_lay_b(b_re[gs]), "bim_r": row_lay_b(b_im[gs]),
            "cre_s": st_lay_c(c_re[gs]), "cim_s": st_lay_c(c_im[gs]),
            "rmask": rmask, "smask": smask, "sel": sel, "sre0": sre0, "sim0": sim0, "iota": iota,
        }
        in_maps.append({k: np.ascontiguousarray(v, dtype=np.float32) for k, v in m.items()})

    res = run_bass_kernel_spmd(nc, in_maps, core_ids=list(range(8)))
    R = res.results
    _NC_CACHE['raw'] = R
    y_prompt = np.zeros((2, SEQ, D), np.float32)
    y_sample = np.zeros((8, 8, D), np.float32)
    kp = np.zeros((1, 2, 2048, 8, 128), np.float32)
    vp = np.zeros((1, 2, 2048, 8, 128), np.float32)
    convp = np.zeros((1, 2, 2, 1024), np.float32)
    srp = np.zeros((1, 2, 128, 64), np.float32)
    sip = np.zeros((1, 2, 128, 64), np.float32)
    ks = np.zeros((1, 8, 8, 8, 128), np.float32)
    vs = np.zeros((1, 8, 8, 8, 128), np.float32)
    convs = np.zeros((1, 8, 2, 1024), np.float32)
    srs = np.zeros((1, 8, 128, 64), np.float32)
    sis = np.zeros((1, 8, 128, 64), np.float32)
    unst = lambda a: a.reshape(2, 64, 16).transpose(2, 0, 1).reshape(32, 64)
    for c in range(8):
        b, r = c // 4, c % 4
        o = R[c]
        y_prompt[b, r * NOWN:(r + 1) * NOWN] = o["yp"]
        y_sample[c] = o["ys"][:8]
        if r >= 2:
            kp[0, b, (r - 2) * NOWN:(r - 1) * NOWN] = o["kp"].reshape(NOWN, 8, 128)
            vp[0, b, (r - 2) * NOWN:(r - 1) * NOWN] = o["vp"].reshape(NOWN, 8, 128)
        if r == 3:
            convp[0, b] = o["convp"].transpose(2, 1, 0).reshape(2, 1024)
        ks[0, c] = o["ks"][:8].reshape(8, 8, 128)
        vs[0, c] = o["vs"][:8].reshape(8, 8, 128)
        convs[0, c] = o["convs"].transpose(2, 1, 0).reshape(2, 1024)
        srp[0, b, 32 * r:32 * (r + 1)] = unst(o["ssm_p"][:, 0, :])
        sip[0, b, 32 * r:32 * (r + 1)] = unst(o["ssm_p"][:, 1, :])
        for i in range(4):
            srs[0, 4 * b + i, 32 * r:32 * (r + 1)] = unst(o["ssm_s"][:, i, 0, :])
            sis[0, 4 * b + i, 32 * r:32 * (r + 1)] = unst(o["ssm_s"][:, i, 1, :])
    return (y_prompt, y_sample, kp, vp, convp, srp, sip, ks, vs, convs, srs, sis)
```
